# Optimizing a Trainium2 kernel written in Bass

```python
import jax, jax.numpy as jnp
from jax import lax
import numpy as np

D_MODEL = 1024
BATCH = 8
SEQ = 2048
DEPTH = 1
DEC_BATCH = 128
DEC_SEQ = 8
PAST_LEN = 16384
PAGE_SIZE = 128

POOL_WIDTH = D_MODEL // 2
POOL_GROUPS = 4
POOL_GROUP_DIM = POOL_WIDTH // POOL_GROUPS
POOL_WINDOWS = (2, 4, 8, 16)
POOL_BUF = 16 - 1
HGRN_WIDTH = D_MODEL // 2
HGRN_EXPAND = 128
HGRN_HEADS = HGRN_WIDTH // HGRN_EXPAND
HGRN_DK = HGRN_EXPAND
HGRN_DV = HGRN_WIDTH // HGRN_HEADS
CHUNK = 16
D_FF = -(-8 * D_MODEL // (3 * 256)) * 256
P_DIM = 256
EPS = 1e-6
IN_WIDTH = POOL_WIDTH + 4 * HGRN_WIDTH + 2 * D_MODEL

kernel_name = 'pool_hgrn2_gated_hybrid_step'


def rms_norm(x, gain):
    xf = x.astype(jnp.float32)
    y = xf * lax.rsqrt(jnp.mean(xf * xf, axis=-1, keepdims=True) + EPS)
    return (y * gain.astype(jnp.float32)).astype(x.dtype)


def multiscale_pool(u):
    B, T, C = u.shape
    cs = jnp.concatenate([jnp.zeros((B, 1, C), u.dtype), jnp.cumsum(u, axis=1)], axis=1)
    idx = jnp.arange(T)
    means = []
    for g, w in enumerate(POOL_WINDOWS):
        c0 = g * POOL_GROUP_DIM
        csg = cs[:, :, c0:c0 + POOL_GROUP_DIM]
        lo = jnp.maximum(idx + 1 - w, 0)
        cnt = jnp.minimum(idx + 1, w).astype(u.dtype)
        means.append((csg[:, idx + 1] - csg[:, lo]) / cnt[None, :, None])
    return jnp.concatenate(means, axis=-1) - u


def hgrn2_recurrence(q, k, v, logf, s0):
    B, T, H, _ = q.shape
    pad = (-T) % CHUNK
    if pad:
        pw = ((0, 0), (0, pad), (0, 0), (0, 0))
        q, k, v, logf = [jnp.pad(a, pw) for a in (q, k, v, logf)]
    n = (T + pad) // CHUNK

    def blocks(a):
        return a.reshape(B, n, CHUNK, H, a.shape[-1])

    q, k, v, logf = blocks(q), blocks(k), blocks(v), blocks(logf)
    G = jnp.cumsum(logf, axis=2)
    G_last = G[:, :, -1:]
    qe = q * jnp.exp(G)
    ke = k * jnp.exp(-G)
    kd = k * jnp.exp(G_last - G)
    causal = jnp.tril(jnp.ones((CHUNK, CHUNK), dtype=bool))
    A = jnp.einsum('bnthk,bnshk->bnhts', qe, ke)
    A = jnp.where(causal, A, 0.0)
    o_intra = jnp.einsum('bnhts,bnshv->bnthv', A, v)
    dS = jnp.einsum('bnshk,bnshv->bnhkv', kd, v)
    decay = jnp.exp(G_last[:, :, 0])

    def step(S, inp):
        qe_c, dec_c, dS_c = inp
        o_c = jnp.einsum('bthk,bhkv->bthv', qe_c, S)
        return dec_c[..., None] * S + dS_c, o_c

    S, o_inter = lax.scan(step, s0, (jnp.moveaxis(qe, 1, 0), jnp.moveaxis(decay, 1, 0), jnp.moveaxis(dS, 1, 0)))
    o = o_intra + jnp.moveaxis(o_inter, 0, 1)
    o = o.reshape(B, n * CHUNK, H, HGRN_DV)[:, :T]
    return o, S


def run_trunk(x, p, state_pool, state_hgrn, w):
    f32 = jnp.float32
    B, T, _ = x.shape
    dt = x.dtype
    out_dt = dt if state_pool is None else state_pool.dtype
    splits = [POOL_WIDTH, POOL_WIDTH + HGRN_WIDTH, POOL_WIDTH + 2 * HGRN_WIDTH,
              POOL_WIDTH + 3 * HGRN_WIDTH, POOL_WIDTH + 4 * HGRN_WIDTH,
              POOL_WIDTH + 4 * HGRN_WIDTH + D_MODEL]
    lb_all = jnp.cumsum(jax.nn.softmax(w['hgrn_lb'].astype(f32), axis=0), axis=0)
    new_pool, new_hgrn = [], []
    for l in range(DEPTH):
        h = rms_norm(x, w['g_mix'][l])
        z = (h @ w['w_in'][l]).astype(f32)
        u, zq, zf, zi, zg, ga, gb = jnp.split(z, splits, axis=-1)

        u_ext = u if state_pool is None else jnp.concatenate([state_pool[l].astype(f32), u], axis=1)
        new_pool.append(u_ext[:, -POOL_BUF:].astype(out_dt))
        pooled = multiscale_pool(u_ext)[:, -T:].reshape(B, T, POOL_GROUPS, POOL_GROUP_DIM)
        pool_out = jnp.einsum('btgc,gcd->btgd', pooled, w['w_pool_mix'][l].astype(f32))
        pool_out = pool_out.reshape(B, T, POOL_WIDTH) * w['pool_scale'][l].astype(f32)

        lb = lb_all[l]
        fg = lb + (1.0 - lb) * jax.nn.sigmoid(zf)
        heads = lambda a: a.reshape(B, T, HGRN_HEADS, -1)
        q = heads(jax.nn.silu(zq))
        k = heads(1.0 - fg)
        logf = heads(jnp.log(fg))
        v = heads(zi)
        s0 = jnp.zeros((B, HGRN_HEADS, HGRN_DK, HGRN_DV), f32) if state_hgrn is None else state_hgrn[l].astype(f32)
        o, s_new = hgrn2_recurrence(q, k, v, logf, s0)
        new_hgrn.append(s_new.astype(out_dt))
        o = o * lax.rsqrt(jnp.mean(o * o, axis=-1, keepdims=True) + EPS)
        o = o * w['hgrn_norm'][l].astype(f32).reshape(HGRN_HEADS, HGRN_DV)
        o = o.reshape(B, T, HGRN_WIDTH) * jax.nn.silu(zg)

        ya = pool_out.astype(dt) @ w['w_pool_up'][l]
        yb = o.astype(dt) @ w['w_hgrn_up'][l]
        merged = (jax.nn.sigmoid(ga) * ya.astype(f32) + jax.nn.sigmoid(gb) * yb.astype(f32)).astype(dt)
        x = x + merged @ w['w_out'][l]

        h2 = rms_norm(x, w['g_ffn'][l])
        x = x + (jax.nn.silu(h2 @ w['w_ffn_gate'][l]) * (h2 @ w['w_ffn_up'][l])) @ w['w_ffn_down'][l]

        h3 = rms_norm(x, w['g_ple'][l])
        gate = jax.nn.sigmoid((h3 @ w['w_ple_gate'][l]).astype(f32))
        emb = (p[l].astype(dt) @ w['w_ple_proj'][l]).astype(f32)
        x = x + (gate * emb).astype(dt)

    y = rms_norm(x, w['g_final'])
    return y, jnp.stack(new_pool, axis=0), jnp.stack(new_hgrn, axis=0)


def setup_inputs(seed: int = 0) -> dict:
    key = jax.random.key(seed)
    ks = jax.random.split(key, 24)
    f32 = jnp.float32

    def nrm(k, shape, scale):
        return jax.random.normal(k, shape, f32) * scale

    def gain(k, shape):
        return 1.0 + 0.05 * jax.random.normal(k, shape, f32)

    return {
        'x_prompt': nrm(ks[0], (BATCH, SEQ, D_MODEL), 1.0),
        'x_sample': nrm(ks[1], (DEC_BATCH, DEC_SEQ, D_MODEL), 1.0),
        'state_pool': nrm(ks[2], (DEPTH, DEC_BATCH, POOL_BUF, POOL_WIDTH), 1.0),
        'state_hgrn': nrm(ks[3], (DEPTH, DEC_BATCH, HGRN_HEADS, HGRN_DK, HGRN_DV), 0.5),
        'p_prompt': nrm(ks[4], (DEPTH, BATCH, SEQ, P_DIM), 1.0),
        'p_sample': nrm(ks[5], (DEPTH, DEC_BATCH, DEC_SEQ, P_DIM), 1.0),
        'g_mix': gain(ks[6], (DEPTH, D_MODEL)),
        'w_in': nrm(ks[7], (DEPTH, D_MODEL, IN_WIDTH), D_MODEL ** -0.5),
        'w_pool_mix': nrm(ks[8], (DEPTH, POOL_GROUPS, POOL_GROUP_DIM, POOL_GROUP_DIM), POOL_GROUP_DIM ** -0.5),
        'pool_scale': gain(ks[9], (DEPTH, POOL_WIDTH)),
        'hgrn_lb': nrm(ks[10], (DEPTH + 1, HGRN_WIDTH), 0.1),
        'hgrn_norm': gain(ks[11], (DEPTH, HGRN_WIDTH)),
        'w_pool_up': nrm(ks[12], (DEPTH, POOL_WIDTH, D_MODEL), POOL_WIDTH ** -0.5),
        'w_hgrn_up': nrm(ks[13], (DEPTH, HGRN_WIDTH, D_MODEL), HGRN_WIDTH ** -0.5),
        'w_out': nrm(ks[14], (DEPTH, D_MODEL, D_MODEL), D_MODEL ** -0.5),
        'g_ffn': gain(ks[15], (DEPTH, D_MODEL)),
        'w_ffn_gate': nrm(ks[16], (DEPTH, D_MODEL, D_FF), D_MODEL ** -0.5),
        'w_ffn_up': nrm(ks[17], (DEPTH, D_MODEL, D_FF), D_MODEL ** -0.5),
        'w_ffn_down': nrm(ks[18], (DEPTH, D_FF, D_MODEL), D_FF ** -0.5),
        'g_ple': gain(ks[19], (DEPTH, D_MODEL)),
        'w_ple_gate': nrm(ks[20], (DEPTH, D_MODEL, D_MODEL), D_MODEL ** -0.5),
        'w_ple_proj': nrm(ks[21], (DEPTH, P_DIM, D_MODEL), P_DIM ** -0.5),
        'g_final': gain(ks[22], (D_MODEL,)),
    }


def reference(x_prompt, x_sample, state_pool, state_hgrn, p_prompt, p_sample,
              g_mix, w_in, w_pool_mix, pool_scale, hgrn_lb, hgrn_norm,
              w_pool_up, w_hgrn_up, w_out, g_ffn, w_ffn_gate, w_ffn_up, w_ffn_down,
              g_ple, w_ple_gate, w_ple_proj, g_final):
    w = {
        'g_mix': g_mix, 'w_in': w_in, 'w_pool_mix': w_pool_mix, 'pool_scale': pool_scale,
        'hgrn_lb': hgrn_lb, 'hgrn_norm': hgrn_norm, 'w_pool_up': w_pool_up,
        'w_hgrn_up': w_hgrn_up, 'w_out': w_out, 'g_ffn': g_ffn, 'w_ffn_gate': w_ffn_gate,
        'w_ffn_up': w_ffn_up, 'w_ffn_down': w_ffn_down, 'g_ple': g_ple,
        'w_ple_gate': w_ple_gate, 'w_ple_proj': w_ple_proj, 'g_final': g_final,
    }
    y_prompt, new_pool_prompt, new_hgrn_prompt = run_trunk(x_prompt, p_prompt, None, None, w)
    y_sample, new_pool_sample, new_hgrn_sample = run_trunk(x_sample, p_sample, state_pool, state_hgrn, w)
    return (y_prompt, y_sample, new_pool_prompt, new_hgrn_prompt, new_pool_sample, new_hgrn_sample)
```

```python
import numpy as np
from contextlib import ExitStack
import concourse.bass as bass
import concourse.mybir as mybir
from concourse.bass_utils import run_bass_kernel_spmd

F32 = mybir.dt.float32
BF16 = mybir.dt.bfloat16
AF = mybir.ActivationFunctionType
ALU = mybir.AluOpType

ENGS = ("pe", "act", "dve", "pool", "sp")
EPS = 1e-6
NCORES = 8
NRING = 5
NTILES = 4
import os as _os
SAME_DIST = int(_os.environ.get("SAME_DIST", "2"))


class Prog:
    def __init__(self, nc, stack, same_engine_wait=True):
        self.nc = nc
        self.stack = stack
        self.streams = {e: [] for e in ENGS}
        self.count = {e: 0 for e in ENGS}
        self.known = {e: {} for e in ENGS}
        self.sems = {}
        self.dma_count = {}
        self.lastw = {}
        self.readers = {}
        self.same_engine_wait = same_engine_wait

    def _collect(self, eng, reads, writes):
        deps = []
        for k in reads:
            ev = self.lastw.get(k)
            if ev is not None:
                deps.append(ev)
        for k in writes:
            ev = self.lastw.get(k)
            if ev is not None:
                deps.append(ev)
            rd = self.readers.get(k)
            if rd:
                deps.extend(rd.values())
        kn = self.known[eng]
        need = {}
        for (s, v, vc) in deps:
            if s == eng and (eng == "pe" or not self.same_engine_wait or self.count[eng] - v >= SAME_DIST):
                continue
            if kn.get(s, 0) >= v:
                continue
            if need.get(s, 0) < v:
                need[s] = v
            for s2, v2 in vc.items():
                if s2 == eng:
                    continue
                if kn.get(s2, 0) < v2:
                    kn[s2] = v2
        waits = []
        for s, v in need.items():
            waits.append((s, v))
            if kn.get(s, 0) < v:
                kn[s] = v
        return waits

    def _record(self, ev, reads, writes):
        s = ev[0]
        for k in reads:
            self.readers.setdefault(k, {})[s] = ev
        for k in writes:
            self.lastw[k] = ev
            self.readers[k] = {}

    def op(self, eng, fn, reads=(), writes=()):
        waits = self._collect(eng, reads, writes)
        self.count[eng] += 1
        n = self.count[eng]
        vc = dict(self.known[eng])
        vc[eng] = n
        ev = (eng, n, vc)
        self.streams[eng].append((waits, fn, (eng, 1)))
        self._record(ev, reads, writes)
        return ev

    def dma(self, q, semname, fn, reads=(), writes=()):
        waits = self._collect(q, reads, writes)
        self.dma_count[semname] = self.dma_count.get(semname, 0) + 1
        v = 16 * self.dma_count[semname]
        vc = dict(self.known[q])
        vc[semname] = v
        ev = (semname, v, vc)
        self.streams[q].append((waits, fn, (semname, 16)))
        self._record(ev, reads, writes)
        return ev

    def barrier(self, engines=("act", "dve", "sp")):
        for e in engines:
            kn = self.known[e]
            waits = []
            for e2 in ("pe", "act", "dve"):
                c = self.count[e2]
                if c and kn.get(e2, 0) < c:
                    waits.append((e2, c))
                    kn[e2] = c
            for s, c in self.dma_count.items():
                if s.startswith("ring") or s.startswith("d_y") or s.startswith("d_x"):
                    continue
                if kn.get(s, 0) < 16 * c:
                    waits.append((s, 16 * c))
                    kn[s] = 16 * c
            if waits:
                self.streams[e].append((waits, None, None))

    def finish(self, eng="sp"):
        kn = self.known[eng]
        waits = []
        for s, c in self.dma_count.items():
            if kn.get(s, 0) < 16 * c:
                waits.append((s, 16 * c))
                kn[s] = 16 * c
        for e in ("pe", "act", "dve"):
            if self.count[e] and kn.get(e, 0) < self.count[e]:
                waits.append((e, self.count[e]))
        self.streams[eng].append((waits, None, None))

    def emit(self):
        nc = self.nc
        for s in list(ENGS) + list(self.dma_count):
            if s not in self.sems:
                self.sems[s] = self.stack.enter_context(nc.semaphore(s))
        with nc.Block() as block:
            def run(engine, items):
                for waits, fn, inc in items:
                    for s, v in waits:
                        engine.wait_ge(self.sems[s], v)
                    if fn is not None:
                        ins = fn(engine)
                        ins.then_inc(self.sems[inc[0]], inc[1])

            @block.tensor
            def _(eng):
                run(eng, self.streams["pe"])

            @block.scalar
            def _(eng):
                run(eng, self.streams["act"])

            @block.vector
            def _(eng):
                run(eng, self.streams["dve"])

            @block.gpsimd
            def _(eng):
                run(eng, self.streams["pool"])

            @block.sync
            def _(eng):
                run(eng, self.streams["sp"])


NBLK = 33


def _kc(W, nk):
    C = W.shape[1]
    return np.ascontiguousarray(W.reshape(nk, 128, C).transpose(1, 0, 2)).reshape(128, nk * C)


def _pad(a):
    out = np.zeros((128, 4096), np.float32)
    out[:, : a.shape[1]] = a
    return out


def build_wall(w_in, w_pool_up, w_hgrn_up, w_out, w_g, w_u, w_d, w_pg, w_pp):
    blocks = []
    blocks.append(_kc(w_in[:, 0:512], 8))
    zz = w_in[:, 512:2560].reshape(8, 128, 4, 4, 128)
    ga = w_in[:, 2560:3584].reshape(8, 128, 8, 128)
    gb = w_in[:, 3584:4608].reshape(8, 128, 8, 128)
    for h in range(4):
        blocks.append(np.ascontiguousarray(zz[:, :, :, h, :].transpose(1, 0, 2, 3)).reshape(128, 4096))
        gg = np.stack([ga[:, :, 2 * h], gb[:, :, 2 * h], ga[:, :, 2 * h + 1], gb[:, :, 2 * h + 1]], axis=2)
        blocks.append(np.ascontiguousarray(gg.transpose(1, 0, 2, 3)).reshape(128, 4096))
    for half in range(2):
        yy = np.stack([w_pool_up[:, half * 512:(half + 1) * 512].reshape(4, 128, 512),
                       w_hgrn_up[:, half * 512:(half + 1) * 512].reshape(4, 128, 512)], axis=2)
        blocks.append(np.ascontiguousarray(yy.transpose(1, 0, 2, 3)).reshape(128, 4096))
    blocks.append(_kc(w_out[:, 0:512], 8))
    blocks.append(_kc(w_out[:, 512:1024], 8))
    for jb in range(11):
        gu = np.stack([w_g[:, jb * 256:(jb + 1) * 256], w_u[:, jb * 256:(jb + 1) * 256]], axis=1)
        blocks.append(_kc(gu.reshape(1024, 512), 8))
    for half in range(2):
        for kb in range(3):
            nk = 8 if kb < 2 else 6
            blocks.append(_pad(_kc(w_d[kb * 1024: kb * 1024 + nk * 128, half * 512:(half + 1) * 512], nk)))
    blocks.append(_kc(w_pg[:, 0:512], 8))
    blocks.append(_pad(_kc(w_pp, 2)))
    blocks.append(_kc(w_pg[:, 512:1024], 8))
    assert len(blocks) == NBLK
    return np.ascontiguousarray(np.stack(blocks, axis=0).astype(np.float32))


def build_consts():
    c = {}
    c["ident"] = np.eye(128, dtype=np.float32)
    s = np.arange(128)[:, None]
    t = np.arange(128)[None, :]
    c["maskP"] = ((s // 64 == t // 64) & (s <= t)).astype(np.float32)
    c["maskS"] = ((s // 8 == t // 8) & (s <= t)).astype(np.float32)
    c["seqm"] = (s // 8 == np.arange(16)[None, :]).astype(np.float32)
    r = np.ones(640, np.float32)
    r[0:512:64] = 0.0
    r[512:640:8] = 0.0
    c["resetm"] = np.broadcast_to(r, (128, 640)).copy()
    rc = np.zeros((4, 16), np.float32)
    for g, w in enumerate((2, 4, 8, 16)):
        rc[g] = 1.0 / np.minimum(np.arange(16) + 1, w)
    c["rc"] = np.broadcast_to(rc.reshape(1, 64), (128, 64)).copy()
    order = ["ident", "seqm", "rc", "maskP", "maskS", "resetm"]
    offs = {}
    o = 0
    for k in order:
        offs[k] = (o, c[k].shape[1])
        o += c[k].shape[1]
    return np.ascontiguousarray(np.concatenate([c[k] for k in order], axis=1)), offs


CONST_ARR, COFF = build_consts()
CW = CONST_ARR.shape[1]
CW_KEEP = COFF["maskP"][0]


class _Stop(Exception):
    pass


STOP = None


def ck(name):
    if STOP == name:
        raise _Stop()


def build_program(ntiles=NTILES):
    nc = bass.Bass("TRN2", target_bir_lowering=False)

    def din(name, shape):
        return nc.dram_tensor(name, shape, F32, kind="ExternalInput").ap()

    def dout(name, shape):
        return nc.dram_tensor(name, shape, F32, kind="ExternalOutput").ap()

    xp = din("xp", [2048, 1024])
    xs = din("xs", [128, 1024])
    ppd = din("pp", [2048, 256])
    psd = din("ps", [128, 256])
    spool = din("spool", [240, 512])
    shg = din("shg", [16, 4, 128, 128])
    wall = din("wall", [NBLK, 128, 4096])
    cst = din("cst", [128, CW])
    gvec = din("gvec", [4, 1024])
    smalld = din("small", [128, 16])
    pmixd = din("pmix", [128, 512])
    yp = dout("yp", [2048, 1024])
    ys = dout("ys", [128, 1024])
    npp = dout("npp", [15, 512])
    nhp = dout("nhp", [4, 128, 128])
    nps = dout("nps", [240, 512])
    nhs = dout("nhs", [16, 4, 128, 128])

    with ExitStack() as st:
        P = Prog(nc, st)

        def sb(name, shape, dt):
            return st.enter_context(nc.sbuf_tensor("sb_" + name, shape, dt))

        xres = sb("xres", [128, 5, 1024], F32)
        hT = sb("hT", [128, 8, 640], BF16)
        hn = [sb(f"hn{i}", [128, 1024], BF16) for i in range(2)]
        junk = sb("junk", [128, 512], BF16)
        ystage = [sb(f"ystage{i}", [128, 512], F32) for i in range(4)]
        resetmb = sb("resetmb", [128, 640], BF16)
        ring = [sb(f"ring{i}", [128, 4096], BF16) for i in range(NRING)]
        gB = sb("gB", [128, 4, 1024], F32)
        cs = sb("cs", [128, CW_KEEP], F32)
        identb = sb("identb", [128, 128], BF16)
        onesd = sb("onesd", [128, 128], BF16)
        maskPb = sb("maskPb", [128, 128], BF16)
        maskSb = sb("maskSb", [128, 128], BF16)
        pmixf = sb("pmixf", [128, 512], F32)
        pmix = sb("pmix", [128, 4, 128], BF16)
        small = sb("small", [128, 16], F32)
        lbv = sb("lbv", [128, 8], F32)
        carry = sb("carry", [128, 4, 128], F32)
        ucarry = sb("ucarry", [128, 4, 16], F32)
        ssq = sb("ssq", [128, 16], F32)
        rst = sb("rst", [128, 8], F32)
        identf = cs[:, COFF["ident"][0]:COFF["ident"][0] + 128]
        seqm = cs[:, COFF["seqm"][0]:COFF["seqm"][0] + 16]
        rcv = cs[:, COFF["rc"][0]:COFF["rc"][0] + 64]
        resetm = resetmb

        MIX_BYTES = 0
        scr_plan = {}

        def plan(phase, name, nelem, dt):
            nonlocal MIX_BYTES
            nbytes = nelem * (4 if dt == F32 else 2)
            nbytes = (nbytes + 31) // 32 * 32
            off = scr_plan.setdefault(("off", phase), 0)
            scr_plan[name] = (off, nelem, dt)
            scr_plan[("off", phase)] = off + nbytes

        U_NAMES = ["ug", "ue", "pa", "pb_", "sa", "sb_", "tmp16", "pooled", "npsb", "stg", "npp_sb"]
        for name, n, dt in [
            ("ug", 528, F32), ("ue", 16 * 24, F32), ("pa", 528, F32), ("pb_", 528, F32),
            ("sa", 16 * 24, F32), ("sb_", 16 * 24, F32), ("tmp16", 16, F32),
            ("pooled", 4 * 640, BF16), ("npsb", 4 * 240, F32),
            ("stg", 2 * 512, F32), ("npp_sb", 512, F32), ("pool_out", 4 * 640, BF16),
            ("qf", 640, F32), ("t1", 640, F32), ("t2", 640, F32), ("t3", 640, F32), ("t4", 640, F32),
            ("qe", 640, BF16), ("keb", 640, BF16), ("kdb", 640, BF16), ("vb", 640, BF16), ("gbf", 640, BF16), ("gbf2", 640, BF16),
            ("kdT", 5 * 128, BF16), ("vT", 5 * 128, BF16), ("Sall", 9 * 128, F32), ("Sbf", 8 * 128, BF16),
            ("dec", 24, F32), ("Am", 5 * 128, BF16), ("osq", 640, BF16), ("o_fin", 4 * 640, BF16),
            ("S0", 16 * 128, F32), ("S0bf", 16 * 128, BF16), ("Vm", 16 * 128, BF16),
            ("merged", 8 * 640, BF16), ("s1", 640, F32), ("s2", 640, F32),
        ]:
            plan("mix", name, n, dt)
        for name, n, dt in [
            ("hidden", 22 * 640, BF16), ("ftmp", 640, F32), ("sgt0", 512, F32), ("sgt1", 512, F32),
            ("pT", 2 * 640, BF16), ("pstage", 5 * 256, F32), ("pbf", 256, BF16),
        ]:
            plan("ffn", name, n, dt)
        u_end = scr_plan["pool_out"][0]
        assert 2 * 8 * 640 * 2 <= u_end, u_end
        scr_plan["sgA"] = (0, 8 * 640, BF16)
        scr_plan["sgB"] = (8 * 640 * 2, 8 * 640, BF16)
        scr_bytes = max(scr_plan[("off", "mix")], scr_plan[("off", "ffn")])
        scr = sb("scr", [128, scr_bytes // 4], F32)

        def sv(name):
            off, n, dt = scr_plan[name]
            if dt == F32:
                return scr[:, off // 4: off // 4 + n]
            return scr[:, off // 4: off // 4 + n // 2].bitcast(BF16)

        ug = sv("ug"); ue = sv("ue").rearrange("p (i r) -> p i r", r=24)
        pa = sv("pa"); pb_ = sv("pb_")
        sa = sv("sa").rearrange("p (i r) -> p i r", r=24); sb_ = sv("sb_").rearrange("p (i r) -> p i r", r=24)
        tmp16 = sv("tmp16")
        pooled = sv("pooled").rearrange("p (g t) -> p g t", g=4)
        pool_out = sv("pool_out").rearrange("p (g t) -> p g t", g=4)
        npsb = sv("npsb").rearrange("p (g r) -> p g r", g=4)
        stg = sv("stg").rearrange("p (h c) -> p h c", h=2)
        nps_sb = stg
        npp_sb = sv("npp_sb")
        qf = sv("qf"); t1 = sv("t1"); t2 = sv("t2"); t3 = sv("t3"); t4 = sv("t4")
        qe = sv("qe"); keb = sv("keb"); kdb = sv("kdb"); vb = sv("vb"); gbf = sv("gbf"); gbf2 = sv("gbf2")
        kdT = sv("kdT").rearrange("p (b k) -> p b k", b=5); vT = sv("vT").rearrange("p (b k) -> p b k", b=5)
        Sall = sv("Sall").rearrange("p (c v) -> p c v", c=9); Sbf = sv("Sbf").rearrange("p (c v) -> p c v", c=8)
        dec = sv("dec"); Am = sv("Am").rearrange("p (b k) -> p b k", b=5); osq = sv("osq")
        o_fin = sv("o_fin").rearrange("p (h t) -> p h t", h=4)
        S0 = sv("S0").rearrange("p (i v) -> p i v", i=16); S0bf = sv("S0bf").rearrange("p (i v) -> p i v", i=16)
        Vm = sv("Vm").rearrange("p (i v) -> p i v", i=16)
        merged = sv("merged").rearrange("p (k t) -> p k t", k=8); s1 = sv("s1"); s2 = sv("s2")
        sgA = sv("sgA").rearrange("p (k t) -> p k t", k=8); sgB = sv("sgB").rearrange("p (k t) -> p k t", k=8)
        hidden = sv("hidden").rearrange("p (f t) -> p f t", f=22); ftmp = sv("ftmp")
        sgt = [sv("sgt0"), sv("sgt1")]
        pT = sv("pT").rearrange("p (k t) -> p k t", k=2); pstage = sv("pstage").rearrange("p (b c) -> p b c", b=5); pbf = sv("pbf")

        pbs = [st.enter_context(nc.psum_tensor(f"pb{i}", [128, 512], F32)) for i in range(8)]
        bank_ctr = [0]

        def nb():
            b = bank_ctr[0] % 8
            bank_ctr[0] += 1
            return b

        def PB(b):
            return ("pb", b)

        wstate = {"seq": 0, "issued": 0, "total": NBLK * ntiles}

        def w_issue(extra_reads=()):
            n = wstate["issued"]
            if n >= wstate["total"]:
                return
            slot = n % NRING
            blk = n % NBLK
            P.dma("pool", f"ring{slot}",
                  lambda e, slot=slot, blk=blk: e.dma_start(out=ring[slot][:], in_=wall[blk]),
                  reads=list(extra_reads), writes=[("ring", slot)])
            wstate["issued"] += 1

        wstate["rel"] = 0

        def w_get(expect):
            n = wstate["seq"]
            assert n % NBLK == expect, (n % NBLK, expect)
            assert wstate["issued"] > n, "ring too small"
            slot = n % NRING
            wstate["seq"] += 1
            return ring[slot], ("ring", slot)

        def w_release(expect):
            assert wstate["rel"] % NBLK == expect, (wstate["rel"] % NBLK, expect)
            assert wstate["rel"] < wstate["seq"]
            wstate["rel"] += 1
            w_issue()

        def A(eng, fn, reads, writes):
            return P.op(eng, fn, reads=reads, writes=writes)

        def mm(out, lhsT, rhs, start, stop, reads, bank):
            A("pe", lambda e: e.matmul(out, lhsT=lhsT, rhs=rhs, start=start, stop=stop), reads, [PB(bank)])

        def tp(out, in_, ident, reads, bank):
            A("pe", lambda e: e.transpose(out=out, in_=in_, identity=ident), reads, [PB(bank)])

        def act(out, in_, func, reads, writes, **kw):
            A("act", lambda e: e.activation(out=out, in_=in_, func=func, **kw), reads, writes)

        def tt(out, in0, in1, op, reads, writes, eng="dve"):
            A(eng, lambda e: e.tensor_tensor(out=out, in0=in0, in1=in1, op=op), reads, writes)

        def ts(out, in0, s1_, s2_, op0, op1, reads, writes):
            A("dve", lambda e: e.tensor_scalar(out=out, in0=in0, scalar1=s1_, scalar2=s2_, op0=op0, op1=op1), reads, writes)

        def stt(out, in0, scalar, in1, op0, op1, reads, writes):
            A("dve", lambda e: e.scalar_tensor_tensor(out=out, in0=in0, scalar=scalar, in1=in1, op0=op0, op1=op1), reads, writes)

        def cp(out, in_, reads, writes, eng="dve"):
            if eng == "act":
                act(out, in_, AF.Copy, reads, writes)
            else:
                A(eng, lambda e: e.tensor_copy(out=out, in_=in_), reads, writes)

        P.dma("sp", "d_cs", lambda e: e.dma_start(out=cs[:], in_=cst[:, 0:CW_KEEP]), writes=["cs"])
        cstage = scr[:, 0:CW - CW_KEEP]
        P.dma("sp", "d_cs2", lambda e: e.dma_start(out=cstage, in_=cst[:, CW_KEEP:CW]), writes=["cstage"])
        maskPf = cstage[:, 0:128]
        maskSf = cstage[:, 128:256]
        resetmf = cstage[:, 256:896]
        P.dma("sp", "d_small", lambda e: e.dma_start(out=small[:], in_=smalld), writes=["small"])
        P.dma("sp", "d_pmix", lambda e: e.dma_start(out=pmixf[:], in_=pmixd), writes=["pmixf"])
        for gi in range(4):
            P.dma("sp", f"d_g{gi}",
                  lambda e, gi=gi: e.dma_start(out=gB[:, gi, :], in_=gvec[gi:gi + 1, :].broadcast_to([128, 1024])),
                  writes=[("gB", gi)])
        cp(identb[:], identf, ["cs"], ["identb"])
        cp(maskPb[:], maskPf, ["cstage"], ["maskPb"])
        cp(maskSb[:], maskSf, ["cstage"], ["maskSb"])
        cp(resetmb[:], resetmf, ["cstage"], ["resetmb"])
        cp(pmix[:].rearrange("p g c -> p (g c)"), pmixf[:], ["pmixf"], ["pmix"])
        A("dve", lambda e: e.memset(onesd[:], 1.0 / 128.0), [], ["onesd"])
        A("dve", lambda e: e.memset(carry[:].rearrange("p h v -> p (h v)"), 0.0), [], ["carry"])
        A("dve", lambda e: e.memset(ucarry[:].rearrange("p g t -> p (g t)"), 0.0), [], ["ucarry"])
        tt(lbv[:, 0:4], small[:, 8:12], small[:, 12:16], ALU.subtract, ["small"], ["lbv"])
        act(lbv[:, 0:4], lbv[:, 0:4], AF.Sigmoid, ["lbv"], ["lbv"])
        ts(lbv[:, 4:8], lbv[:, 0:4], -1.0, 1.0, ALU.mult, ALU.add, ["lbv"], ["lbv"])

        def run_tile(ti):
            has_s = ti == 0
            last = ti == 3
            NT = 640 if has_s else 512
            blocks = [0, 1, 2, 3] + ([4] if has_s else [])
            subs = [(0, 512, "p")] + ([(512, 128, "s")] if has_s else [])
            hTk_p = [("hT", b) for b in range(4)]
            hTk = {"p": hTk_p, "s": [("hT", 4)]}

            def K(name):
                return [(name, "p")] + ([(name, "s")] if has_s else [])

            def load_x(tj, b):
                src = xp[tj * 512 + b * 128: tj * 512 + (b + 1) * 128, :] if b < 4 else xs
                P.dma("sp", f"d_x{b}", lambda e, b=b, src=src: e.dma_start(out=xres[:, b, :], in_=src),
                      writes=[("xres", b)])

            if ti == 0:
                for b in blocks:
                    load_x(0, b)
            if ti == 0:
                for _ in range(NRING):
                    w_issue(extra_reads=[("xres", b) for b in blocks])
            if has_s:
                for half in range(2):
                    P.dma("sp", f"d_stg{half}",
                          lambda e, half=half: e.dma_start(out=stg[0:120, half, :], in_=spool[half * 120:(half + 1) * 120, :]),
                          writes=[("stg", half)])

            def half_stats(b, half):
                act(junk[:, 0:512], xres[:, b, half * 512:(half + 1) * 512], AF.Square, [("xres", b)], ["junk", ("ssq", b, half)],
                    accum_out=ssq[:, 2 * b + half:2 * b + half + 1])

            def norm_stats(have_partials):
                if not have_partials:
                    for b in blocks:
                        for half in range(2):
                            half_stats(b, half)
                nbk = len(blocks)
                tt(rst[:, 0:nbk], ssq[:, 0:2 * nbk:2], ssq[:, 1:2 * nbk:2], ALU.add,
                   [("ssq", b, hf) for b in blocks for hf in range(2)], ["rst"])
                act(rst[:, 0:nbk], rst[:, 0:nbk], AF.Ln, ["rst"], ["rst"], scale=1.0 / 1024.0, bias=EPS)
                act(rst[:, 0:nbk], rst[:, 0:nbk], AF.Exp, ["rst"], ["rst"], scale=-0.5)

            def norm_T(gi, have_partials):
                norm_stats(have_partials)
                for b in blocks:
                    hb = hn[b % 2]
                    hk = f"hn{b % 2}"
                    stt(hb[:], xres[:, b, :], rst[:, b:b + 1], gB[:, gi, :], ALU.mult, ALU.mult,
                        [("xres", b), "rst", ("gB", gi)], [hk])
                    bank = nb()
                    bv = pbs[bank][:].bitcast(BF16)
                    for k in range(8):
                        tp(bv[:, k * 128:(k + 1) * 128], hb[:, k * 128:(k + 1) * 128], identb[:], [hk, "identb"], bank)
                    c0 = b * 128
                    cp(hT[:, :, c0:c0 + 128], bv.rearrange("p (k t) -> p k t", k=8), [PB(bank)], [("hT", b)], eng="act")

            ck("load")
            norm_T(0, False)
            ck("norm1")

            wv, wk = w_get(0)
            wu = wv[:].rearrange("p (k c) -> p k c", k=8)
            ubanks = []
            for g in range(4):
                bp = nb()
                bs = nb() if has_s else None
                for k in range(8):
                    mm(pbs[bp][:, 0:512], wu[:, k, g * 128:(g + 1) * 128], hT[:, k, 0:512], k == 0, k == 7, [wk] + hTk_p, bp)
                    if has_s:
                        mm(pbs[bs][:, 0:128], wu[:, k, g * 128:(g + 1) * 128], hT[:, k, 512:640], k == 0, k == 7, [wk, ("hT", 4)], bs)
                ubanks.append((bp, bs))
                if g == 3:
                    w_release(0)
                w = 2 << g
                cp(ug[:, 0:16], ucarry[:, g, :], ["ucarry"], ["ug"])
                cp(ug[:, 16:528], pbs[bp][:, 0:512], [PB(bp)], ["ug"], eng="act")
                tt(pa[:, 1:528], ug[:, 1:528], ug[:, 0:527], ALU.add, ["ug"], ["pa"])
                sw = pa
                swk = "pa"
                if w >= 4:
                    tt(pb_[:, 3:528], pa[:, 3:528], pa[:, 1:526], ALU.add, ["pa"], ["pb_"])
                    sw, swk = pb_, "pb_"
                if w >= 8:
                    tt(pa[:, 7:528], pb_[:, 7:528], pb_[:, 3:524], ALU.add, ["pb_"], ["pa"])
                    sw, swk = pa, "pa"
                if w >= 16:
                    tt(pb_[:, 15:528], pa[:, 15:528], pa[:, 7:520], ALU.add, ["pa"], ["pb_"])
                    sw, swk = pb_, "pb_"
                stt(pooled[:, g, 0:512], sw[:, 16:528], 1.0 / w, ug[:, 16:528], ALU.mult, ALU.subtract,
                    [swk, "ug"], [("pooled", g, "p")])
                if ti == 0:
                    tt(tmp16[:], sw[:, 16:32], rcv[:, g * 16:(g + 1) * 16], ALU.mult, [swk, "cs"], ["tmp16"])
                    tt(pooled[:, g, 0:16], tmp16[:], ug[:, 16:32], ALU.subtract, ["tmp16", "ug"], [("pooled", g, "p")])
                cp(ucarry[:, g, :], ug[:, 512:528], ["ug"], ["ucarry"])
                if last:
                    bt = nb()
                    tp(pbs[bt][0:15, 0:128], ug[:, 513:528], identf, ["ug", "cs"], bt)
                    cp(npp_sb[0:15, g * 128:(g + 1) * 128], pbs[bt][0:15, 0:128], [PB(bt)], ["npp_sb"], eng="act")
                if has_s:
                    bt = nb()
                    for half in range(2):
                        tp(pbs[bt][:, half * 120:(half + 1) * 120], stg[0:120, half, g * 128:(g + 1) * 128],
                           identf[0:120, 0:120], [("stg", half), "cs"], bt)
                    cp(ue[:, :, 0:15], pbs[bt][:, 0:240].rearrange("p (i r) -> p i r", r=15), [PB(bt)], ["ue"], eng="act")
                    cp(ue[:, :, 15:23], pbs[bs][:, 0:128].rearrange("p (i t) -> p i t", t=8), [PB(bs)], ["ue"], eng="act")
                    tt(sa[:, :, 1:23], ue[:, :, 1:23], ue[:, :, 0:22], ALU.add, ["ue"], ["sa"])
                    ssw, sswk = sa, "sa"
                    if w >= 4:
                        tt(sb_[:, :, 3:23], sa[:, :, 3:23], sa[:, :, 1:21], ALU.add, ["sa"], ["sb_"])
                        ssw, sswk = sb_, "sb_"
                    if w >= 8:
                        tt(sa[:, :, 7:23], sb_[:, :, 7:23], sb_[:, :, 3:19], ALU.add, ["sb_"], ["sa"])
                        ssw, sswk = sa, "sa"
                    if w >= 16:
                        tt(sb_[:, :, 15:23], sa[:, :, 15:23], sa[:, :, 7:15], ALU.add, ["sa"], ["sb_"])
                        ssw, sswk = sb_, "sb_"
                    stt(pooled[:, g, 512:640].rearrange("p (i t) -> p i t", t=8), ssw[:, :, 15:23], 1.0 / w,
                        ue[:, :, 15:23], ALU.mult, ALU.subtract, [sswk, "ue"], [("pooled", g, "s")])
                    cp(npsb[:, g, :].rearrange("p (i r) -> p i r", r=15), ue[:, :, 8:23], ["ue"], [("npsb", g)])
            if last:
                P.dma("sp", "d_npp", lambda e: e.dma_start(out=npp, in_=npp_sb[0:15, :]), reads=["npp_sb"])
            if has_s:
                for half in range(2):
                    bt = nb()
                    for g in range(4):
                        tp(pbs[bt][0:120, g * 128:(g + 1) * 128], npsb[:, g, half * 120:(half + 1) * 120], identf,
                           [("npsb", g), "cs"], bt)
                    cp(nps_sb[0:120, half, :], pbs[bt][0:120, 0:512], [PB(bt)], [("stg", half)], eng="act")
                    P.dma("sp", f"d_nps{half}",
                          lambda e, half=half: e.dma_start(out=nps[half * 120:(half + 1) * 120, :], in_=nps_sb[0:120, half, :]),
                          reads=[("stg", half)])
            ck("u")
            PIPE = not has_s
            if PIPE:
                zc, rcn = [0], [0]

                def nbZ():
                    b = zc[0] % 4
                    zc[0] += 1
                    return b

                def nbR():
                    b = 6 + rcn[0] % 2
                    rcn[0] += 1
                    return b

                gcn = [0]

                def nbG():
                    b = 4 + gcn[0] % 2
                    gcn[0] += 1
                    return b
            else:
                nbZ = nbR = nbG = nb
            gb2 = [gbf, gbf2]
            HS = {h: {} for h in range(4)}

            def poolmix():
                for g in range(4):
                    for (c0, n, sk) in subs:
                        bk = nbR()
                        mm(pbs[bk][:, 0:n], pmix[:, g, :], pooled[:, g, c0:c0 + n], True, True, ["pmix", ("pooled", g, sk)], bk)
                        act(pool_out[:, g, c0:c0 + n], pbs[bk][:, 0:n], AF.Copy, [PB(bk), "small"], [("pool_out", g, sk)],
                            scale=small[:, g:g + 1])

            def z_group(h, j):
                S_ = HS[h]
                if j == 0:
                    wv, wk = w_get(1 + 2 * h)
                    S_["wh"] = wv[:].rearrange("p (k j c) -> p k j c", k=8, j=4)
                    S_["wk"] = wk
                    S_["bs"] = nbZ() if has_s else None
                    S_["zb"] = []
                wh, wk, bs = S_["wh"], S_["wk"], S_["bs"]
                bp = nbZ()
                for k in range(8):
                    mm(pbs[bp][:, 0:512], wh[:, k, j, :], hT[:, k, 0:512], k == 0, k == 7, [wk] + hTk_p, bp)
                    if has_s:
                        mm(pbs[bs][:, j * 128:(j + 1) * 128], wh[:, k, j, :], hT[:, k, 512:640], k == 0, k == 7,
                           [wk, ("hT", 4)], bs)
                S_["zb"].append(bp)
                if j == 3:
                    w_release(1 + 2 * h)

            def z_evac(h):
                S_ = HS[h]
                zb, bs = S_["zb"], S_["bs"]
                gbh = gb2[h % 2]
                gbk = f"gbf{h % 2}"

                def zsrc(j, sk):
                    return (pbs[zb[j]][:, 0:512], PB(zb[j])) if sk == "p" else (pbs[bs][:, j * 128:(j + 1) * 128], PB(bs))

                for (c0, n, sk) in subs:
                    src, key = zsrc(1, sk)
                    act(t1[:, c0:c0 + n], src, AF.Sigmoid, [key], [("t1", sk)])
                    src, key = zsrc(0, sk)
                    act(qf[:, c0:c0 + n], src, AF.Sigmoid, [key], [("qf", sk)])
                    tt(qf[:, c0:c0 + n], qf[:, c0:c0 + n], src, ALU.mult, [("qf", sk), key], [("qf", sk)])
                    src, key = zsrc(3, sk)
                    act(t4[:, c0:c0 + n], src, AF.Sigmoid, [key], [("t4", sk)])
                    tt(gbh[:, c0:c0 + n], t4[:, c0:c0 + n], src, ALU.mult, [("t4", sk), key], [(gbk, sk)])
                    src, key = zsrc(2, sk)
                    cp(vb[:, c0:c0 + n], src, [key], [("vb", sk)])

            def g_mm(h, jj):
                S_ = HS[h]
                if jj == 0:
                    wgv_, wgk_ = w_get(2 + 2 * h)
                    S_["wgv"] = wgv_[:].rearrange("p (k t c) -> p k t c", k=8, t=4)
                    S_["wgk"] = wgk_
                    S_["gate_ev"] = []
                wgv, wgk_ = S_["wgv"], S_["wgk"]
                j = 2 * h + jj
                bsg = nbG() if has_s else None
                for t_ in range(2):
                    bpg = nbG()
                    for k in range(8):
                        mm(pbs[bpg][:, 0:512], wgv[:, k, 2 * jj + t_, :], hT[:, k, 0:512], k == 0, k == 7, [wgk_] + hTk_p, bpg)
                        if has_s:
                            mm(pbs[bsg][:, t_ * 128:(t_ + 1) * 128], wgv[:, k, 2 * jj + t_, :], hT[:, k, 512:640], k == 0, k == 7,
                               [wgk_, ("hT", 4)], bsg)
                    S_["gate_ev"].append((j, t_, bpg, bsg))
                if jj == 1:
                    w_release(2 + 2 * h)

            def g_evac(h, jj):
                for (j, t_, bpg, bsg) in HS[h]["gate_ev"][2 * jj:2 * jj + 2]:
                    dst = sgA if t_ == 0 else sgB
                    dk = "sgA" if t_ == 0 else "sgB"
                    cp(dst[:, j, 0:512], pbs[bpg][:, 0:512], [PB(bpg)], [(dk, j, "p")], eng="act")
                    if has_s:
                        cp(dst[:, j, 512:640], pbs[bsg][:, t_ * 128:(t_ + 1) * 128], [PB(bsg)], [(dk, j, "s")], eng="act")

            def chain_a(h):
                ts(t1[:, 0:NT], t1[:, 0:NT], lbv[:, 4 + h:5 + h], lbv[:, h:h + 1], ALU.mult, ALU.add, K("t1") + ["lbv"], K("t1"))
                ts(t2[:, 0:NT], t1[:, 0:NT], -1.0, 1.0, ALU.mult, ALU.add, K("t1"), K("t2"))
                act(t1[:, 0:NT], t1[:, 0:NT], AF.Ln, K("t1"), K("t1"))

            def chain_b(h):
                A("dve", lambda e: e.tensor_tensor_scan(out=t3[:, 0:NT], data0=resetm[:, 0:NT], data1=t1[:, 0:NT],
                                                        initial=0.0, op0=ALU.mult, op1=ALU.add),
                  K("t1") + ["resetmb"], K("t3"))
                act(t1[:, 0:NT], t3[:, 0:NT], AF.Exp, K("t3"), K("t1"))
                act(t4[:, 0:NT], t3[:, 0:NT], AF.Exp, K("t3"), K("t4"), scale=-1.0)
                tt(qe[:, 0:NT], qf[:, 0:NT], t1[:, 0:NT], ALU.mult, K("qf") + K("t1"), K("qe"))
                tt(t2[:, 0:NT], t2[:, 0:NT], t4[:, 0:NT], ALU.mult, K("t2") + K("t4"), K("t2"))
                cp(keb[:, 0:NT], t2[:, 0:NT], K("t2"), K("keb"), eng="act")
                cp(dec[:, 0:8], t1[:, 63:512:64], [("t1", "p")], [("dec", "p")])
                tt(kdb[:, 0:512].rearrange("p (c t) -> p c t", t=64), t2[:, 0:512].rearrange("p (c t) -> p c t", t=64),
                   dec[:, 0:8].unsqueeze(2).broadcast_to([128, 8, 64]), ALU.mult, [("t2", "p"), ("dec", "p")], [("kdb", "p")])
                if has_s:
                    cp(dec[:, 8:24], t1[:, 519:640:8], [("t1", "s")], [("dec", "s")])
                    tt(kdb[:, 512:640].rearrange("p (c t) -> p c t", t=8), t2[:, 512:640].rearrange("p (c t) -> p c t", t=8),
                       dec[:, 8:24].unsqueeze(2).broadcast_to([128, 16, 8]), ALU.mult, [("t2", "s"), ("dec", "s")], [("kdb", "s")])

            def R1(h):
                for (src_, srck, dst, dstk) in ((kdb, "kdb", kdT, "kdT"), (vb, "vb", vT, "vT")):
                    bt = nbR()
                    bv = pbs[bt][:].bitcast(BF16)
                    for b in blocks:
                        sk = "p" if b < 4 else "s"
                        tp(bv[:, b * 128:(b + 1) * 128], src_[:, b * 128:(b + 1) * 128], identb[:], [(srck, sk), "identb"], bt)
                    nbk = len(blocks)
                    cp(dst[:, 0:nbk, :], bv[:, 0:nbk * 128].rearrange("p (b k) -> p b k", k=128), [PB(bt)], [dstk], eng="act")
                ba = nbR()
                for b in range(4):
                    mm(pbs[ba][:, b * 128:(b + 1) * 128], keb[:, b * 128:(b + 1) * 128], qe[:, b * 128:(b + 1) * 128],
                       True, True, [("keb", "p"), ("qe", "p")], ba)
                tt(Am[:, 0:4, :], pbs[ba][:, 0:512].rearrange("p (b t) -> p b t", b=4),
                   maskPb[:].unsqueeze(1).broadcast_to([128, 4, 128]), ALU.mult, [PB(ba), "maskPb"], [("Am", "p")])
                if has_s:
                    bas = nbR()
                    mm(pbs[bas][:, 0:128], keb[:, 512:640], qe[:, 512:640], True, True, [("keb", "s"), ("qe", "s")], bas)
                    tt(Am[:, 4, :], pbs[bas][:, 0:128], maskSb[:], ALU.mult, [PB(bas), "maskSb"], [("Am", "s")])
                    tt(Vm[:, :, :], vT[:, 4, :].unsqueeze(1).broadcast_to([128, 16, 128]),
                       seqm.unsqueeze(2).broadcast_to([128, 16, 128]), ALU.mult, ["vT", "cs"], ["Vm"])

            def sample_prefetch(h):
                P.dma("sp", "d_S0", lambda e, h=h: e.dma_start(out=S0[:, :, :], in_=shg[:, h, :, :].rearrange("i k v -> k i v")),
                      writes=["S0"])

            def sample_bf(h):
                cp(S0bf[:, :, :].rearrange("p i v -> p (i v)"), S0[:, :, :].rearrange("p i v -> p (i v)"), ["S0"], ["S0bf"], eng="act")

            def sample_state_update(h):
                tt(S0[:, :, :], S0[:, :, :], dec[:, 8:24].unsqueeze(2).broadcast_to([128, 16, 128]), ALU.mult,
                   ["S0", ("dec", "s")], ["S0"])
                for q4 in range(4):
                    bd = nbR()
                    mm(pbs[bd][:, 0:512], kdT[:, 4, :], Vm[:, 4 * q4:4 * q4 + 4, :].rearrange("p i v -> p (i v)"), True, True,
                       ["kdT", "Vm"], bd)
                    tt(S0[:, 4 * q4:4 * q4 + 4, :].rearrange("p i v -> p (i v)"),
                       S0[:, 4 * q4:4 * q4 + 4, :].rearrange("p i v -> p (i v)"), pbs[bd][:, 0:512], ALU.add,
                       ["S0", PB(bd)], ["S0"])
                P.dma("sp", "d_nhs", lambda e, h=h: e.dma_start(out=nhs[:, h, :, :].rearrange("i k v -> k i v"), in_=S0[:, :, :]),
                      reads=["S0"])

            def R2(h):
                dsb = [nbR(), nbR()]
                for c in range(8):
                    blk, half = c // 2, c % 2
                    bk = dsb[half]
                    mm(pbs[bk][:, blk * 128:(blk + 1) * 128], kdT[half * 64:(half + 1) * 64, blk, :],
                       vT[half * 64:(half + 1) * 64, blk, :], True, True, ["kdT", "vT"], bk)
                cp(Sall[:, 0, :], carry[:, h, :], ["carry"], [("Sall", 0)])
                for c in range(8):
                    blk, half = c // 2, c % 2
                    bk = dsb[half]
                    stt(Sall[:, c + 1, :], Sall[:, c, :], dec[:, c:c + 1], pbs[bk][:, blk * 128:(blk + 1) * 128],
                        ALU.mult, ALU.add, [("Sall", c), ("dec", "p"), PB(bk)], [("Sall", c + 1)])
                cp(Sbf[:, :, :].rearrange("p c v -> p (c v)"), Sall[:, 0:8, :].rearrange("p c v -> p (c v)"),
                   [("Sall", c) for c in range(8)], ["Sbf"], eng="act")
                cp(carry[:, h, :], Sall[:, 8, :], [("Sall", 8)], ["carry"])
                if last:
                    P.dma("sp", f"d_nhp{h}", lambda e, h=h: e.dma_start(out=nhp[h], in_=Sall[:, 8, :]), reads=[("Sall", 8)])

            def R3(h):
                S_ = HS[h]
                bo = nbR()
                for b in range(4):
                    mm(pbs[bo][:, b * 128:(b + 1) * 128], vT[:, b, :], Am[:, b, :], True, False, ["vT", ("Am", "p")], bo)
                    for half in range(2):
                        c = 2 * b + half
                        mm(pbs[bo][:, c * 64:(c + 1) * 64], Sbf[:, c, :], qe[:, c * 64:(c + 1) * 64], False, half == 1,
                           ["Sbf", ("qe", "p")], bo)
                bos = None
                if has_s:
                    sample_bf(h)
                    bos = nbR()
                    mm(pbs[bos][:, 0:128], vT[:, 4, :], Am[:, 4, :], True, False, ["vT", ("Am", "s")], bos)
                    for i in range(16):
                        mm(pbs[bos][:, i * 8:(i + 1) * 8], S0bf[:, i, :], qe[:, 512 + i * 8:512 + (i + 1) * 8], False, i == 15,
                           ["S0bf", ("qe", "s")], bos)
                S_["obanks"] = [(0, 512, "p", bo)] + ([(512, 128, "s", bos)] if has_s else [])
                for (c0, n, sk, bk) in S_["obanks"]:
                    act(osq[:, c0:c0 + n], pbs[bk][:, 0:n], AF.Square, [PB(bk)], [("osq", sk)])

            def R4(h):
                gbh = gb2[h % 2]
                gbk = f"gbf{h % 2}"
                for (c0, n, sk, bk) in HS[h]["obanks"]:
                    bn = nbR()
                    mm(pbs[bn][:, 0:n], onesd[:], osq[:, c0:c0 + n], True, True, ["onesd", ("osq", sk)], bn)
                    act(s1[:, c0:c0 + n], pbs[bn][:, 0:n], AF.Ln, [PB(bn)], [("s1", sk)], bias=EPS)
                    act(s1[:, c0:c0 + n], s1[:, c0:c0 + n], AF.Exp, [("s1", sk)], [("s1", sk)], scale=-0.5)
                    stt(s2[:, c0:c0 + n], pbs[bk][:, 0:n], small[:, 4 + h:5 + h], s1[:, c0:c0 + n], ALU.mult, ALU.mult,
                        [PB(bk), "small", ("s1", sk)], [("s2", sk)])
                tt(o_fin[:, h, 0:NT], s2[:, 0:NT], gbh[:, 0:NT], ALU.mult, K("s2") + [(gbk, "p"), (gbk, "s")],
                   [("o_fin", h, s_) for s_ in ("p", "s")])

            if PIPE:
                for j in range(4):
                    z_group(0, j)
            poolmix()
            ck("poolmix")
            P.barrier()
            ck("bar")
            if PIPE:
                for h in range(4):
                    if h > 0:
                        z_group(h, 0)
                        R1(h - 1)
                        z_group(h, 1)
                        R2(h - 1)
                        z_group(h, 2)
                        z_group(h, 3)
                    z_evac(h)
                    g_mm(h, 0)
                    g_evac(h, 0)
                    chain_a(h)
                    if h > 0:
                        R3(h - 1)
                    g_mm(h, 1)
                    chain_b(h)
                    g_evac(h, 1)
                    if h > 0:
                        R4(h - 1)
                R1(3); R2(3); R3(3); R4(3)
            else:
                for h in range(4):
                    sample_prefetch(h)
                    for j in range(4):
                        z_group(h, j)
                    z_evac(h)
                    g_mm(h, 0)
                    g_mm(h, 1)
                    chain_a(h)
                    chain_b(h)
                    g_evac(h, 0)
                    g_evac(h, 1)
                    R1(h); R2(h); R3(h); R4(h)
                    sample_state_update(h)

            ck("hgrn")
            for jh in range(2):
                wy, wyk = w_get(9 + jh)
                wyv = wy[:].rearrange("p (k t c) -> p k t c", k=4, t=2)
                for jj in range(4):
                    j = jh * 4 + jj
                    bs = nb() if has_s else None
                    b_ya, b_yb = nb(), nb()
                    for t_, bk, srcb, srck in ((0, b_ya, pool_out, "pool_out"), (1, b_yb, o_fin, "o_fin")):
                        for kk in range(4):
                            for (c0, n, sk) in subs:
                                o = pbs[bk][:, 0:512] if sk == "p" else pbs[bs][:, t_ * 128:(t_ + 1) * 128]
                                mm(o, wyv[:, kk, t_, jj * 128:(jj + 1) * 128], srcb[:, kk, c0:c0 + n], kk == 0, kk == 3,
                                   [wyk, (srck, kk, sk)], bk if sk == "p" else bs)
                    if jj == 3:
                        w_release(9 + jh)
                    for (c0, n, sk) in subs:
                        oa = pbs[b_ya][:, 0:512] if sk == "p" else pbs[bs][:, 0:128]
                        ob = pbs[b_yb][:, 0:512] if sk == "p" else pbs[bs][:, 128:256]
                        ka = PB(b_ya) if sk == "p" else PB(bs)
                        kb_ = PB(b_yb) if sk == "p" else PB(bs)
                        act(s1[:, c0:c0 + n], sgA[:, j, c0:c0 + n], AF.Sigmoid, [("sgA", j, sk)], [("s1", sk)])
                        act(s2[:, c0:c0 + n], sgB[:, j, c0:c0 + n], AF.Sigmoid, [("sgB", j, sk)], [("s2", sk)])
                        tt(s1[:, c0:c0 + n], s1[:, c0:c0 + n], oa, ALU.mult, [("s1", sk), ka], [("s1", sk)])
                        tt(s2[:, c0:c0 + n], s2[:, c0:c0 + n], ob, ALU.mult, [("s2", sk), kb_], [("s2", sk)])
                    tt(merged[:, j, 0:NT], s1[:, 0:NT], s2[:, 0:NT], ALU.add, K("s1") + K("s2"),
                       [("merged", j, s_) for s_ in ("p", "s")])

            ck("merge")
            for half in range(2):
                wo, wok = w_get(11 + half)
                wov = wo[:].rearrange("p (k c) -> p k c", k=8)
                bks = {b: nb() for b in blocks}
                for k in range(8):
                    for b in blocks:
                        sk = "p" if b < 4 else "s"
                        mm(pbs[bks[b]][:, 0:512], merged[:, k, b * 128:(b + 1) * 128], wov[:, k, :], k == 0, k == 7,
                           [wok, ("merged", k, sk)], bks[b])
                w_release(11 + half)
                for b in blocks:
                    tt(xres[:, b, half * 512:(half + 1) * 512], xres[:, b, half * 512:(half + 1) * 512], pbs[bks[b]][:, 0:512],
                       ALU.add, [("xres", b), PB(bks[b])], [("xres", b)])
                    half_stats(b, half)

            ck("wout")
            norm_T(1, True)
            P.barrier()
            for b in blocks:
                src = ppd[ti * 512 + b * 128: ti * 512 + (b + 1) * 128, :] if b < 4 else psd
                P.dma("sp", f"d_p{b}", lambda e, src=src, b=b: e.dma_start(out=pstage[:, b, :], in_=src), writes=[("pstage", b)])
            for jb in range(11):
                wf, wfk = w_get(13 + jb)
                wfv = wf[:].rearrange("p (k j c) -> p k j c", k=8, j=2)
                for cc in range(2):
                    f = 2 * jb + cc
                    bs = nb() if has_s else None
                    b_g, b_u = nb(), nb()
                    for jx, bk in ((0, b_g), (1, b_u)):
                        for k in range(8):
                            for (c0, n, sk) in subs:
                                o = pbs[bk][:, 0:512] if sk == "p" else pbs[bs][:, jx * 128:(jx + 1) * 128]
                                mm(o, wfv[:, k, jx, cc * 128:(cc + 1) * 128], hT[:, k, c0:c0 + n], k == 0, k == 7,
                                   [wfk] + hTk[sk], bk if sk == "p" else bs)
                    if cc == 1:
                        w_release(13 + jb)
                    for (c0, n, sk) in subs:
                        og = pbs[b_g][:, 0:512] if sk == "p" else pbs[bs][:, 0:128]
                        ou = pbs[b_u][:, 0:512] if sk == "p" else pbs[bs][:, 128:256]
                        kg = PB(b_g) if sk == "p" else PB(bs)
                        ku = PB(b_u) if sk == "p" else PB(bs)
                        act(ftmp[:, c0:c0 + n], og, AF.Sigmoid, [kg], [("ftmp", sk)])
                        tt(ftmp[:, c0:c0 + n], ftmp[:, c0:c0 + n], og, ALU.mult, [("ftmp", sk), kg], [("ftmp", sk)])
                        tt(hidden[:, f, c0:c0 + n], ftmp[:, c0:c0 + n], ou, ALU.mult, [("ftmp", sk), ku], [("hidden", f, sk)])
            for b in blocks:
                cp(pbf[:], pstage[:, b, :], [("pstage", b)], ["pbf"], eng="act")
                bt = nb()
                bv = pbs[bt][:].bitcast(BF16)
                for k in range(2):
                    tp(bv[:, k * 128:(k + 1) * 128], pbf[:, k * 128:(k + 1) * 128], identb[:], ["pbf", "identb"], bt)
                cp(pT[:, :, b * 128:(b + 1) * 128], bv[:, 0:256].rearrange("p (k t) -> p k t", k=2), [PB(bt)], [("pT", b)])
            for half in range(2):
                bks = {b: nb() for b in blocks}
                for kb in range(3):
                    wd, wdk = w_get(24 + half * 3 + kb)
                    wdv = wd[:].rearrange("p (k c) -> p k c", k=8)
                    nk = 8 if kb < 2 else 6
                    for kk in range(nk):
                        f = kb * 8 + kk
                        for b in blocks:
                            sk = "p" if b < 4 else "s"
                            mm(pbs[bks[b]][:, 0:512], hidden[:, f, b * 128:(b + 1) * 128], wdv[:, kk, :], f == 0, f == 21,
                               [wdk, ("hidden", f, sk)], bks[b])
                    w_release(24 + half * 3 + kb)
                for b in blocks:
                    tt(xres[:, b, half * 512:(half + 1) * 512], xres[:, b, half * 512:(half + 1) * 512], pbs[bks[b]][:, 0:512],
                       ALU.add, [("xres", b), PB(bks[b])], [("xres", b)])
                    half_stats(b, half)

            ck("ffn")
            norm_T(2, True)
            wg0, wg0k = w_get(30)
            wpp_, wppk = w_get(31)
            wg1, wg1k = w_get(32)
            wppv = wpp_[:, 0:2048].rearrange("p (k c) -> p k c", k=2)
            for half in range(2):
                wg_, wgk = (wg0, wg0k) if half == 0 else (wg1, wg1k)
                wgv = wg_[:].rearrange("p (k c) -> p k c", k=8)
                bks = {b: nb() for b in blocks}
                for k in range(8):
                    for b in blocks:
                        mm(pbs[bks[b]][:, 0:512], hT[:, k, b * 128:(b + 1) * 128], wgv[:, k, :], k == 0, k == 7,
                           [wgk, ("hT", b)], bks[b])
                if half == 0:
                    w_release(30)
                for b in blocks:
                    sg = sgt[b % 2]
                    sgk = f"sgt{b % 2}"
                    act(sg[:], pbs[bks[b]][:, 0:512], AF.Sigmoid, [PB(bks[b])], [sgk])
                    be = nb()
                    for k in range(2):
                        mm(pbs[be][:, 0:512], pT[:, k, b * 128:(b + 1) * 128], wppv[:, k, half * 512:(half + 1) * 512], k == 0, k == 1,
                           [wppk, ("pT", b)], be)
                    tt(sg[:], sg[:], pbs[be][:, 0:512], ALU.mult, [sgk, PB(be)], [sgk])
                    tt(xres[:, b, half * 512:(half + 1) * 512], xres[:, b, half * 512:(half + 1) * 512], sg[:], ALU.add,
                       [("xres", b), sgk], [("xres", b)])
                for b in blocks:
                    half_stats(b, half)
                if half == 1:
                    w_release(31); w_release(32)

            ck("ple")
            P.barrier()
            norm_stats(True)
            for b in blocks:
                dst = yp[ti * 512 + b * 128: ti * 512 + (b + 1) * 128, :] if b < 4 else ys
                for half in range(2):
                    q_ = (2 * b + half) % 4
                    yst = ystage[q_]
                    ysk = f"ystage{q_}"
                    hs_ = slice(half * 512, (half + 1) * 512)
                    stt(yst[:], xres[:, b, hs_], rst[:, b:b + 1], gB[:, 3, hs_], ALU.mult, ALU.mult,
                        [("xres", b), "rst", ("gB", 3)], [ysk])
                    P.dma("sp", f"d_y{q_}", lambda e, yst=yst, dst=dst, hs_=hs_: e.dma_start(out=dst[:, hs_], in_=yst[:]),
                          reads=[ysk])
                if ti + 1 < ntiles and b < 4:
                    load_x(ti + 1, b)

        P.barrier()
        try:
            ck("setup")
            for ti in range(ntiles):
                run_tile(ti)
        except _Stop:
            pass
        P.finish("sp")
        P.emit()
    return nc


_PROG = {}


def _prep_inputs(inp):
    f = lambda a: np.ascontiguousarray(np.asarray(a, dtype=np.float32))
    w_in = f(inp["w_in"][0])
    wallv = build_wall(w_in, f(inp["w_pool_up"][0]), f(inp["w_hgrn_up"][0]), f(inp["w_out"][0]),
                       f(inp["w_ffn_gate"][0]), f(inp["w_ffn_up"][0]), f(inp["w_ffn_down"][0]),
                       f(inp["w_ple_gate"][0]), f(inp["w_ple_proj"][0]))
    gvec = np.ascontiguousarray(np.stack([f(inp["g_mix"][0]), f(inp["g_ffn"][0]), f(inp["g_ple"][0]), f(inp["g_final"])], 0))
    small = np.zeros((128, 16), np.float32)
    small[:, 0:4] = f(inp["pool_scale"][0]).reshape(4, 128).T
    small[:, 4:8] = f(inp["hgrn_norm"][0]).reshape(4, 128).T
    small[:, 8:12] = f(inp["hgrn_lb"][0]).reshape(4, 128).T
    small[:, 12:16] = f(inp["hgrn_lb"][1]).reshape(4, 128).T
    pmix = np.ascontiguousarray(f(inp["w_pool_mix"][0]).transpose(1, 0, 2)).reshape(128, 512)
    xp = f(inp["x_prompt"]); xsm = f(inp["x_sample"])
    ppr = f(inp["p_prompt"][0]); psm = f(inp["p_sample"][0])
    spl = f(inp["state_pool"][0]); shg = f(inp["state_hgrn"][0])
    maps = []
    for c in range(NCORES):
        maps.append({
            "xp": xp[c], "xs": xsm[16 * c:16 * c + 16].reshape(128, 1024),
            "pp": ppr[c], "ps": psm[16 * c:16 * c + 16].reshape(128, 256),
            "spool": spl[16 * c:16 * c + 16].reshape(240, 512),
            "shg": shg[16 * c:16 * c + 16],
            "wall": wallv, "cst": CONST_ARR, "gvec": gvec, "small": small, "pmix": pmix,
        })
    return maps


def kernel(**inputs):
    if "nc" not in _PROG:
        _PROG["nc"] = build_program()
    nc = _PROG["nc"]
    maps = _prep_inputs(inputs)
    res = run_bass_kernel_spmd(nc, maps, core_ids=list(range(NCORES)))
    R = res.results
    y_p = np.stack([R[c]["yp"] for c in range(NCORES)], 0).astype(np.float32)
    y_s = np.concatenate([R[c]["ys"].reshape(16, 8, 1024) for c in range(NCORES)], 0).astype(np.float32)
    npp = np.stack([R[c]["npp"] for c in range(NCORES)], 0)[None].astype(np.float32)
    nhp = np.stack([R[c]["nhp"] for c in range(NCORES)], 0)[None].astype(np.float32)
    nps = np.concatenate([R[c]["nps"].reshape(16, 15, 512) for c in range(NCORES)], 0)[None].astype(np.float32)
    nhs = np.concatenate([R[c]["nhs"] for c in range(NCORES)], 0)[None].astype(np.float32)
    return (y_p, y_s, npp, nhp, nps, nhs)
```

```python
import numpy as np
from contextlib import ExitStack
import concourse.bass as bass
import concourse.mybir as mybir
from concourse.bass_utils import run_bass_kernel_spmd

F32 = mybir.dt.float32
BF16 = mybir.dt.bfloat16
AF = mybir.ActivationFunctionType
ALU = mybir.AluOpType

ENGS = ("pe", "act", "dve", "pool", "sp")
EPS = 1e-6
NCORES = 8
NRING = 5
NTILES = 4
import os as _os
SAME_DIST = int(_os.environ.get("SAME_DIST", "2"))


class Prog:
    def __init__(self, nc, stack, same_engine_wait=True):
        self.nc = nc
        self.stack = stack
        self.streams = {e: [] for e in ENGS}
        self.count = {e: 0 for e in ENGS}
        self.known = {e: {} for e in ENGS}
        self.sems = {}
        self.dma_count = {}
        self.lastw = {}
        self.readers = {}
        self.same_engine_wait = same_engine_wait

    def _collect(self, eng, reads, writes):
        deps = []
        for k in reads:
            ev = self.lastw.get(k)
            if ev is not None:
                deps.append(ev)
        for k in writes:
            ev = self.lastw.get(k)
            if ev is not None:
                deps.append(ev)
            rd = self.readers.get(k)
            if rd:
                deps.extend(rd.values())
        kn = self.known[eng]
        need = {}
        for (s, v, vc) in deps:
            if s == eng and (eng == "pe" or not self.same_engine_wait or self.count[eng] - v >= SAME_DIST):
                continue
            if kn.get(s, 0) >= v:
                continue
            if need.get(s, 0) < v:
                need[s] = v
            for s2, v2 in vc.items():
                if s2 == eng:
                    continue
                if kn.get(s2, 0) < v2:
                    kn[s2] = v2
        waits = []
        for s, v in need.items():
            waits.append((s, v))
            if kn.get(s, 0) < v:
                kn[s] = v
        return waits

    def _record(self, ev, reads, writes):
        s = ev[0]
        for k in reads:
            self.readers.setdefault(k, {})[s] = ev
        for k in writes:
            self.lastw[k] = ev
            self.readers[k] = {}

    def op(self, eng, fn, reads=(), writes=()):
        waits = self._collect(eng, reads, writes)
        self.count[eng] += 1
        n = self.count[eng]
        vc = dict(self.known[eng])
        vc[eng] = n
        ev = (eng, n, vc)
        self.streams[eng].append((waits, fn, (eng, 1)))
        self._record(ev, reads, writes)
        return ev

    def dma(self, q, semname, fn, reads=(), writes=()):
        waits = self._collect(q, reads, writes)
        self.dma_count[semname] = self.dma_count.get(semname, 0) + 1
        v = 16 * self.dma_count[semname]
        vc = dict(self.known[q])
        vc[semname] = v
        ev = (semname, v, vc)
        self.streams[q].append((waits, fn, (semname, 16)))
        self._record(ev, reads, writes)
        return ev

    def barrier(self, engines=("act", "dve", "sp")):
        for e in engines:
            kn = self.known[e]
            waits = []
            for e2 in ("pe", "act", "dve"):
                c = self.count[e2]
                if c and kn.get(e2, 0) < c:
                    waits.append((e2, c))
                    kn[e2] = c
            for s, c in self.dma_count.items():
                if s.startswith("ring") or s.startswith("d_y") or s.startswith("d_x"):
                    continue
                if kn.get(s, 0) < 16 * c:
                    waits.append((s, 16 * c))
                    kn[s] = 16 * c
            if waits:
                self.streams[e].append((waits, None, None))

    def finish(self, eng="sp"):
        kn = self.known[eng]
        waits = []
        for s, c in self.dma_count.items():
            if kn.get(s, 0) < 16 * c:
                waits.append((s, 16 * c))
                kn[s] = 16 * c
        for e in ("pe", "act", "dve"):
            if self.count[e] and kn.get(e, 0) < self.count[e]:
                waits.append((e, self.count[e]))
        self.streams[eng].append((waits, None, None))

    def emit(self):
        nc = self.nc
        for s in list(ENGS) + list(self.dma_count):
            if s not in self.sems:
                self.sems[s] = self.stack.enter_context(nc.semaphore(s))
        with nc.Block() as block:
            def run(engine, items):
                for waits, fn, inc in items:
                    for s, v in waits:
                        engine.wait_ge(self.sems[s], v)
                    if fn is not None:
                        ins = fn(engine)
                        ins.then_inc(self.sems[inc[0]], inc[1])

            @block.tensor
            def _(eng):
                run(eng, self.streams["pe"])

            @block.scalar
            def _(eng):
                run(eng, self.streams["act"])

            @block.vector
            def _(eng):
                run(eng, self.streams["dve"])

            @block.gpsimd
            def _(eng):
                run(eng, self.streams["pool"])

            @block.sync
            def _(eng):
                run(eng, self.streams["sp"])


NBLK = 33


def _kc(W, nk):
    C = W.shape[1]
    return np.ascontiguousarray(W.reshape(nk, 128, C).transpose(1, 0, 2)).reshape(128, nk * C)


def _pad(a):
    out = np.zeros((128, 4096), np.float32)
    out[:, : a.shape[1]] = a
    return out


def build_wall(w_in, w_pool_up, w_hgrn_up, w_out, w_g, w_u, w_d, w_pg, w_pp):
    blocks = []
    blocks.append(_kc(w_in[:, 0:512], 8))
    zz = w_in[:, 512:2560].reshape(8, 128, 4, 4, 128)
    ga = w_in[:, 2560:3584].reshape(8, 128, 8, 128)
    gb = w_in[:, 3584:4608].reshape(8, 128, 8, 128)
    for h in range(4):
        blocks.append(np.ascontiguousarray(zz[:, :, :, h, :].transpose(1, 0, 2, 3)).reshape(128, 4096))
        gg = np.stack([ga[:, :, 2 * h], gb[:, :, 2 * h], ga[:, :, 2 * h + 1], gb[:, :, 2 * h + 1]], axis=2)
        blocks.append(np.ascontiguousarray(gg.transpose(1, 0, 2, 3)).reshape(128, 4096))
    for half in range(2):
        yy = np.stack([w_pool_up[:, half * 512:(half + 1) * 512].reshape(4, 128, 512),
                       w_hgrn_up[:, half * 512:(half + 1) * 512].reshape(4, 128, 512)], axis=2)
        blocks.append(np.ascontiguousarray(yy.transpose(1, 0, 2, 3)).reshape(128, 4096))
    blocks.append(_kc(w_out[:, 0:512], 8))
    blocks.append(_kc(w_out[:, 512:1024], 8))
    for jb in range(11):
        gu = np.stack([w_g[:, jb * 256:(jb + 1) * 256], w_u[:, jb * 256:(jb + 1) * 256]], axis=1)
        blocks.append(_kc(gu.reshape(1024, 512), 8))
    for half in range(2):
        for kb in range(3):
            nk = 8 if kb < 2 else 6
            blocks.append(_pad(_kc(w_d[kb * 1024: kb * 1024 + nk * 128, half * 512:(half + 1) * 512], nk)))
    blocks.append(_kc(w_pg[:, 0:512], 8))
    blocks.append(_pad(_kc(w_pp, 2)))
    blocks.append(_kc(w_pg[:, 512:1024], 8))
    assert len(blocks) == NBLK
    return np.ascontiguousarray(np.stack(blocks, axis=0).astype(np.float32))


def build_consts():
    c = {}
    c["ident"] = np.eye(128, dtype=np.float32)
    s = np.arange(128)[:, None]
    t = np.arange(128)[None, :]
    c["maskP"] = ((s // 64 == t // 64) & (s <= t)).astype(np.float32)
    c["maskS"] = ((s // 8 == t // 8) & (s <= t)).astype(np.float32)
    c["seqm"] = (s // 8 == np.arange(16)[None, :]).astype(np.float32)
    r = np.ones(640, np.float32)
    r[0:512:64] = 0.0
    r[512:640:8] = 0.0
    c["resetm"] = np.broadcast_to(r, (128, 640)).copy()
    rc = np.zeros((4, 16), np.float32)
    for g, w in enumerate((2, 4, 8, 16)):
        rc[g] = 1.0 / np.minimum(np.arange(16) + 1, w)
    c["rc"] = np.broadcast_to(rc.reshape(1, 64), (128, 64)).copy()
    order = ["ident", "seqm", "rc", "maskP", "maskS", "resetm"]
    offs = {}
    o = 0
    for k in order:
        offs[k] = (o, c[k].shape[1])
        o += c[k].shape[1]
    return np.ascontiguousarray(np.concatenate([c[k] for k in order], axis=1)), offs


CONST_ARR, COFF = build_consts()
CW = CONST_ARR.shape[1]
CW_KEEP = COFF["maskP"][0]


class _Stop(Exception):
    pass


STOP = None


def ck(name):
    if STOP == name:
        raise _Stop()


def build_program(ntiles=NTILES):
    nc = bass.Bass("TRN2", target_bir_lowering=False)

    def din(name, shape):
        return nc.dram_tensor(name, shape, F32, kind="ExternalInput").ap()

    def dout(name, shape):
        return nc.dram_tensor(name, shape, F32, kind="ExternalOutput").ap()

    xp = din("xp", [2048, 1024])
    xs = din("xs", [128, 1024])
    ppd = din("pp", [2048, 256])
    psd = din("ps", [128, 256])
    spool = din("spool", [240, 512])
    shg = din("shg", [16, 4, 128, 128])
    wall = din("wall", [NBLK, 128, 4096])
    cst = din("cst", [128, CW])
    gvec = din("gvec", [4, 1024])
    smalld = din("small", [128, 16])
    pmixd = din("pmix", [128, 512])
    yp = dout("yp", [2048, 1024])
    ys = dout("ys", [128, 1024])
    npp = dout("npp", [15, 512])
    nhp = dout("nhp", [4, 128, 128])
    nps = dout("nps", [240, 512])
    nhs = dout("nhs", [16, 4, 128, 128])

    with ExitStack() as st:
        P = Prog(nc, st)

        def sb(name, shape, dt):
            return st.enter_context(nc.sbuf_tensor("sb_" + name, shape, dt))

        xres = sb("xres", [128, 5, 1024], F32)
        hT = sb("hT", [128, 8, 640], BF16)
        hn = [sb(f"hn{i}", [128, 1024], BF16) for i in range(2)]
        junk = sb("junk", [128, 512], BF16)
        ystage = [sb(f"ystage{i}", [128, 512], F32) for i in range(4)]
        resetmb = sb("resetmb", [128, 640], BF16)
        ring = [sb(f"ring{i}", [128, 4096], BF16) for i in range(NRING)]
        gB = sb("gB", [128, 4, 1024], F32)
        cs = sb("cs", [128, CW_KEEP], F32)
        identb = sb("identb", [128, 128], BF16)
        onesd = sb("onesd", [128, 128], BF16)
        maskPb = sb("maskPb", [128, 128], BF16)
        maskSb = sb("maskSb", [128, 128], BF16)
        pmixf = sb("pmixf", [128, 512], F32)
        pmix = sb("pmix", [128, 4, 128], BF16)
        small = sb("small", [128, 16], F32)
        lbv = sb("lbv", [128, 8], F32)
        carry = sb("carry", [128, 4, 128], F32)
        ucarry = sb("ucarry", [128, 4, 16], F32)
        ssq = sb("ssq", [128, 16], F32)
        rst = sb("rst", [128, 8], F32)
        identf = cs[:, COFF["ident"][0]:COFF["ident"][0] + 128]
        seqm = cs[:, COFF["seqm"][0]:COFF["seqm"][0] + 16]
        rcv = cs[:, COFF["rc"][0]:COFF["rc"][0] + 64]
        resetm = resetmb

        MIX_BYTES = 0
        scr_plan = {}

        def plan(phase, name, nelem, dt):
            nonlocal MIX_BYTES
            nbytes = nelem * (4 if dt == F32 else 2)
            nbytes = (nbytes + 31) // 32 * 32
            off = scr_plan.setdefault(("off", phase), 0)
            scr_plan[name] = (off, nelem, dt)
            scr_plan[("off", phase)] = off + nbytes

        U_NAMES = ["ug", "ue", "pa", "pb_", "sa", "sb_", "tmp16", "pooled", "npsb", "stg", "npp_sb"]
        for name, n, dt in [
            ("ug", 528, F32), ("ue", 16 * 24, F32), ("pa", 528, F32), ("pb_", 528, F32),
            ("sa", 16 * 24, F32), ("sb_", 16 * 24, F32), ("tmp16", 16, F32),
            ("pooled", 4 * 640, BF16), ("npsb", 4 * 240, F32),
            ("stg", 2 * 512, F32), ("npp_sb", 512, F32), ("pool_out", 4 * 640, BF16),
            ("qf", 640, F32), ("t1", 640, F32), ("t2", 640, F32), ("t3", 640, F32), ("t4", 640, F32),
            ("qe", 640, BF16), ("qe2", 640, BF16), ("keb", 640, BF16), ("kdb", 640, BF16), ("vb", 640, BF16), ("gbf", 640, BF16), ("gbf2", 640, BF16),
            ("kdT", 5 * 128, BF16), ("vT", 5 * 128, BF16), ("Sall", 9 * 128, F32), ("Sbf", 8 * 128, BF16),
            ("dec", 24, F32), ("dec2", 24, F32), ("Am", 5 * 128, BF16), ("osq", 640, BF16), ("o_fin", 4 * 640, BF16),
            ("S0", 16 * 128, F32), ("S0bf", 16 * 128, BF16), ("Vm", 16 * 128, BF16),
            ("merged", 8 * 640, BF16), ("s1", 640, F32), ("s2", 640, F32),
        ]:
            plan("mix", name, n, dt)
        for name, n, dt in [
            ("hidden", 22 * 640, BF16), ("ftmp", 640, F32), ("sgt0", 512, F32), ("sgt1", 512, F32),
            ("pT", 2 * 640, BF16), ("pstage", 5 * 256, F32), ("pbf", 256, BF16),
        ]:
            plan("ffn", name, n, dt)
        u_end = scr_plan["pool_out"][0]
        assert 2 * 8 * 640 * 2 <= u_end, u_end
        scr_plan["sgA"] = (0, 8 * 640, BF16)
        scr_plan["sgB"] = (8 * 640 * 2, 8 * 640, BF16)
        scr_bytes = max(scr_plan[("off", "mix")], scr_plan[("off", "ffn")])
        scr = sb("scr", [128, scr_bytes // 4], F32)

        def sv(name):
            off, n, dt = scr_plan[name]
            if dt == F32:
                return scr[:, off // 4: off // 4 + n]
            return scr[:, off // 4: off // 4 + n // 2].bitcast(BF16)

        ug = sv("ug"); ue = sv("ue").rearrange("p (i r) -> p i r", r=24)
        pa = sv("pa"); pb_ = sv("pb_")
        sa = sv("sa").rearrange("p (i r) -> p i r", r=24); sb_ = sv("sb_").rearrange("p (i r) -> p i r", r=24)
        tmp16 = sv("tmp16")
        pooled = sv("pooled").rearrange("p (g t) -> p g t", g=4)
        pool_out = sv("pool_out").rearrange("p (g t) -> p g t", g=4)
        npsb = sv("npsb").rearrange("p (g r) -> p g r", g=4)
        stg = sv("stg").rearrange("p (h c) -> p h c", h=2)
        nps_sb = stg
        npp_sb = sv("npp_sb")
        qf = sv("qf"); t1 = sv("t1"); t2 = sv("t2"); t3 = sv("t3"); t4 = sv("t4")
        qe = sv("qe"); qe2 = sv("qe2"); keb = sv("keb"); kdb = sv("kdb"); vb = sv("vb"); gbf = sv("gbf"); gbf2 = sv("gbf2")
        kdT = sv("kdT").rearrange("p (b k) -> p b k", b=5); vT = sv("vT").rearrange("p (b k) -> p b k", b=5)
        Sall = sv("Sall").rearrange("p (c v) -> p c v", c=9); Sbf = sv("Sbf").rearrange("p (c v) -> p c v", c=8)
        dec = sv("dec"); dec2 = sv("dec2"); Am = sv("Am").rearrange("p (b k) -> p b k", b=5); osq = sv("osq")
        o_fin = sv("o_fin").rearrange("p (h t) -> p h t", h=4)
        S0 = sv("S0").rearrange("p (i v) -> p i v", i=16); S0bf = sv("S0bf").rearrange("p (i v) -> p i v", i=16)
        Vm = sv("Vm").rearrange("p (i v) -> p i v", i=16)
        merged = sv("merged").rearrange("p (k t) -> p k t", k=8); s1 = sv("s1"); s2 = sv("s2")
        sgA = sv("sgA").rearrange("p (k t) -> p k t", k=8); sgB = sv("sgB").rearrange("p (k t) -> p k t", k=8)
        hidden = sv("hidden").rearrange("p (f t) -> p f t", f=22); ftmp = sv("ftmp")
        sgt = [sv("sgt0"), sv("sgt1")]
        pT = sv("pT").rearrange("p (k t) -> p k t", k=2); pstage = sv("pstage").rearrange("p (b c) -> p b c", b=5); pbf = sv("pbf")

        pbs = [st.enter_context(nc.psum_tensor(f"pb{i}", [128, 512], F32)) for i in range(8)]
        bank_ctr = [0]

        def nb():
            b = bank_ctr[0] % 8
            bank_ctr[0] += 1
            return b

        def PB(b):
            return ("pb", b)

        wstate = {"tile": 0, "issued": set(), "released": set(), "total": NBLK * ntiles}

        def w_issue_n(n, extra_reads=()):
            if n >= wstate["total"] or n in wstate["issued"]:
                return
            slot = n % NRING
            blk = n % NBLK
            P.dma("pool", f"ring{slot}",
                  lambda e, slot=slot, blk=blk: e.dma_start(out=ring[slot][:], in_=wall[blk]),
                  reads=list(extra_reads), writes=[("ring", slot)])
            wstate["issued"].add(n)

        def w_issue(extra_reads=()):
            w_issue_n(len(wstate["issued"]), extra_reads)

        def w_get(expect):
            n = wstate["tile"] * NBLK + expect
            assert n in wstate["issued"], ("ring too small / block not prefetched", n)
            assert n - NRING < 0 or (n - NRING) in wstate["released"]
            slot = n % NRING
            return ring[slot], ("ring", slot)

        def w_release(expect):
            n = wstate["tile"] * NBLK + expect
            assert n in wstate["issued"] and n not in wstate["released"]
            wstate["released"].add(n)
            w_issue_n(n + NRING)

        def A(eng, fn, reads, writes):
            return P.op(eng, fn, reads=reads, writes=writes)

        def mm(out, lhsT, rhs, start, stop, reads, bank):
            A("pe", lambda e: e.matmul(out, lhsT=lhsT, rhs=rhs, start=start, stop=stop), reads, [PB(bank)])

        def tp(out, in_, ident, reads, bank):
            A("pe", lambda e: e.transpose(out=out, in_=in_, identity=ident), reads, [PB(bank)])

        def act(out, in_, func, reads, writes, **kw):
            A("act", lambda e: e.activation(out=out, in_=in_, func=func, **kw), reads, writes)

        def tt(out, in0, in1, op, reads, writes, eng="dve"):
            A(eng, lambda e: e.tensor_tensor(out=out, in0=in0, in1=in1, op=op), reads, writes)

        def ts(out, in0, s1_, s2_, op0, op1, reads, writes):
            A("dve", lambda e: e.tensor_scalar(out=out, in0=in0, scalar1=s1_, scalar2=s2_, op0=op0, op1=op1), reads, writes)

        def stt(out, in0, scalar, in1, op0, op1, reads, writes):
            A("dve", lambda e: e.scalar_tensor_tensor(out=out, in0=in0, scalar=scalar, in1=in1, op0=op0, op1=op1), reads, writes)

        def cp(out, in_, reads, writes, eng="dve"):
            if eng == "act":
                act(out, in_, AF.Copy, reads, writes)
            else:
                A(eng, lambda e: e.tensor_copy(out=out, in_=in_), reads, writes)

        P.dma("sp", "d_cs", lambda e: e.dma_start(out=cs[:], in_=cst[:, 0:CW_KEEP]), writes=["cs"])
        cstage = scr[:, 0:CW - CW_KEEP]
        P.dma("sp", "d_cs2", lambda e: e.dma_start(out=cstage, in_=cst[:, CW_KEEP:CW]), writes=["cstage"])
        maskPf = cstage[:, 0:128]
        maskSf = cstage[:, 128:256]
        resetmf = cstage[:, 256:896]
        P.dma("sp", "d_small", lambda e: e.dma_start(out=small[:], in_=smalld), writes=["small"])
        P.dma("sp", "d_pmix", lambda e: e.dma_start(out=pmixf[:], in_=pmixd), writes=["pmixf"])
        for gi in range(4):
            P.dma("sp", f"d_g{gi}",
                  lambda e, gi=gi: e.dma_start(out=gB[:, gi, :], in_=gvec[gi:gi + 1, :].broadcast_to([128, 1024])),
                  writes=[("gB", gi)])
        cp(identb[:], identf, ["cs"], ["identb"])
        cp(maskPb[:], maskPf, ["cstage"], ["maskPb"])
        cp(maskSb[:], maskSf, ["cstage"], ["maskSb"])
        cp(resetmb[:], resetmf, ["cstage"], ["resetmb"])
        cp(pmix[:].rearrange("p g c -> p (g c)"), pmixf[:], ["pmixf"], ["pmix"])
        A("dve", lambda e: e.memset(onesd[:], 1.0 / 128.0), [], ["onesd"])
        A("dve", lambda e: e.memset(carry[:].rearrange("p h v -> p (h v)"), 0.0), [], ["carry"])
        A("dve", lambda e: e.memset(ucarry[:].rearrange("p g t -> p (g t)"), 0.0), [], ["ucarry"])
        tt(lbv[:, 0:4], small[:, 8:12], small[:, 12:16], ALU.subtract, ["small"], ["lbv"])
        act(lbv[:, 0:4], lbv[:, 0:4], AF.Sigmoid, ["lbv"], ["lbv"])
        ts(lbv[:, 4:8], lbv[:, 0:4], -1.0, 1.0, ALU.mult, ALU.add, ["lbv"], ["lbv"])

        def run_tile(ti):
            wstate["tile"] = ti
            has_s = ti == 0
            last = ti == 3
            NT = 640 if has_s else 512
            blocks = [0, 1, 2, 3] + ([4] if has_s else [])
            subs = [(0, 512, "p")] + ([(512, 128, "s")] if has_s else [])
            hTk_p = [("hT", b) for b in range(4)]
            hTk = {"p": hTk_p, "s": [("hT", 4)]}

            def K(name):
                return [(name, "p")] + ([(name, "s")] if has_s else [])

            def load_x(tj, b):
                src = xp[tj * 512 + b * 128: tj * 512 + (b + 1) * 128, :] if b < 4 else xs
                P.dma("sp", f"d_x{b}", lambda e, b=b, src=src: e.dma_start(out=xres[:, b, :], in_=src),
                      writes=[("xres", b)])

            if ti == 0:
                for b in blocks:
                    load_x(0, b)
            if ti == 0:
                for _ in range(NRING):
                    w_issue(extra_reads=[("xres", b) for b in blocks])
            if has_s:
                for half in range(2):
                    P.dma("sp", f"d_stg{half}",
                          lambda e, half=half: e.dma_start(out=stg[0:120, half, :], in_=spool[half * 120:(half + 1) * 120, :]),
                          writes=[("stg", half)])

            def half_stats(b, half):
                act(junk[:, 0:512], xres[:, b, half * 512:(half + 1) * 512], AF.Square, [("xres", b)], ["junk", ("ssq", b, half)],
                    accum_out=ssq[:, 2 * b + half:2 * b + half + 1])

            def norm_stats(have_partials):
                if not have_partials:
                    for b in blocks:
                        for half in range(2):
                            half_stats(b, half)
                nbk = len(blocks)
                tt(rst[:, 0:nbk], ssq[:, 0:2 * nbk:2], ssq[:, 1:2 * nbk:2], ALU.add,
                   [("ssq", b, hf) for b in blocks for hf in range(2)], ["rst"])
                act(rst[:, 0:nbk], rst[:, 0:nbk], AF.Ln, ["rst"], ["rst"], scale=1.0 / 1024.0, bias=EPS)
                act(rst[:, 0:nbk], rst[:, 0:nbk], AF.Exp, ["rst"], ["rst"], scale=-0.5)

            def norm_T(gi, have_partials):
                norm_stats(have_partials)
                for b in blocks:
                    hb = hn[b % 2]
                    hk = f"hn{b % 2}"
                    stt(hb[:], xres[:, b, :], rst[:, b:b + 1], gB[:, gi, :], ALU.mult, ALU.mult,
                        [("xres", b), "rst", ("gB", gi)], [hk])
                    bank = nb()
                    bv = pbs[bank][:].bitcast(BF16)
                    for k in range(8):
                        tp(bv[:, k * 128:(k + 1) * 128], hb[:, k * 128:(k + 1) * 128], identb[:], [hk, "identb"], bank)
                    c0 = b * 128
                    cp(hT[:, :, c0:c0 + 128], bv.rearrange("p (k t) -> p k t", k=8), [PB(bank)], [("hT", b)], eng="act")

            ck("load")
            norm_T(0, False)
            ck("norm1")

            wv, wk = w_get(0)
            wu = wv[:].rearrange("p (k c) -> p k c", k=8)
            ubanks = []
            for g in range(4):
                bp = nb()
                bs = nb() if has_s else None
                for k in range(8):
                    mm(pbs[bp][:, 0:512], wu[:, k, g * 128:(g + 1) * 128], hT[:, k, 0:512], k == 0, k == 7, [wk] + hTk_p, bp)
                    if has_s:
                        mm(pbs[bs][:, 0:128], wu[:, k, g * 128:(g + 1) * 128], hT[:, k, 512:640], k == 0, k == 7, [wk, ("hT", 4)], bs)
                ubanks.append((bp, bs))
                if g == 3:
                    w_release(0)
                w = 2 << g
                cp(ug[:, 0:16], ucarry[:, g, :], ["ucarry"], ["ug"])
                cp(ug[:, 16:528], pbs[bp][:, 0:512], [PB(bp)], ["ug"], eng="act")
                tt(pa[:, 1:528], ug[:, 1:528], ug[:, 0:527], ALU.add, ["ug"], ["pa"])
                sw = pa
                swk = "pa"
                if w >= 4:
                    tt(pb_[:, 3:528], pa[:, 3:528], pa[:, 1:526], ALU.add, ["pa"], ["pb_"])
                    sw, swk = pb_, "pb_"
                if w >= 8:
                    tt(pa[:, 7:528], pb_[:, 7:528], pb_[:, 3:524], ALU.add, ["pb_"], ["pa"])
                    sw, swk = pa, "pa"
                if w >= 16:
                    tt(pb_[:, 15:528], pa[:, 15:528], pa[:, 7:520], ALU.add, ["pa"], ["pb_"])
                    sw, swk = pb_, "pb_"
                stt(pooled[:, g, 0:512], sw[:, 16:528], 1.0 / w, ug[:, 16:528], ALU.mult, ALU.subtract,
                    [swk, "ug"], [("pooled", g, "p")])
                if ti == 0:
                    tt(tmp16[:], sw[:, 16:32], rcv[:, g * 16:(g + 1) * 16], ALU.mult, [swk, "cs"], ["tmp16"])
                    tt(pooled[:, g, 0:16], tmp16[:], ug[:, 16:32], ALU.subtract, ["tmp16", "ug"], [("pooled", g, "p")])
                cp(ucarry[:, g, :], ug[:, 512:528], ["ug"], ["ucarry"])
                if last:
                    bt = nb()
                    tp(pbs[bt][0:15, 0:128], ug[:, 513:528], identf, ["ug", "cs"], bt)
                    cp(npp_sb[0:15, g * 128:(g + 1) * 128], pbs[bt][0:15, 0:128], [PB(bt)], ["npp_sb"], eng="act")
                if has_s:
                    bt = nb()
                    for half in range(2):
                        tp(pbs[bt][:, half * 120:(half + 1) * 120], stg[0:120, half, g * 128:(g + 1) * 128],
                           identf[0:120, 0:120], [("stg", half), "cs"], bt)
                    cp(ue[:, :, 0:15], pbs[bt][:, 0:240].rearrange("p (i r) -> p i r", r=15), [PB(bt)], ["ue"], eng="act")
                    cp(ue[:, :, 15:23], pbs[bs][:, 0:128].rearrange("p (i t) -> p i t", t=8), [PB(bs)], ["ue"], eng="act")
                    tt(sa[:, :, 1:23], ue[:, :, 1:23], ue[:, :, 0:22], ALU.add, ["ue"], ["sa"])
                    ssw, sswk = sa, "sa"
                    if w >= 4:
                        tt(sb_[:, :, 3:23], sa[:, :, 3:23], sa[:, :, 1:21], ALU.add, ["sa"], ["sb_"])
                        ssw, sswk = sb_, "sb_"
                    if w >= 8:
                        tt(sa[:, :, 7:23], sb_[:, :, 7:23], sb_[:, :, 3:19], ALU.add, ["sb_"], ["sa"])
                        ssw, sswk = sa, "sa"
                    if w >= 16:
                        tt(sb_[:, :, 15:23], sa[:, :, 15:23], sa[:, :, 7:15], ALU.add, ["sa"], ["sb_"])
                        ssw, sswk = sb_, "sb_"
                    stt(pooled[:, g, 512:640].rearrange("p (i t) -> p i t", t=8), ssw[:, :, 15:23], 1.0 / w,
                        ue[:, :, 15:23], ALU.mult, ALU.subtract, [sswk, "ue"], [("pooled", g, "s")])
                    cp(npsb[:, g, :].rearrange("p (i r) -> p i r", r=15), ue[:, :, 8:23], ["ue"], [("npsb", g)])
            if last:
                P.dma("sp", "d_npp", lambda e: e.dma_start(out=npp, in_=npp_sb[0:15, :]), reads=["npp_sb"])
            if has_s:
                for half in range(2):
                    bt = nb()
                    for g in range(4):
                        tp(pbs[bt][0:120, g * 128:(g + 1) * 128], npsb[:, g, half * 120:(half + 1) * 120], identf,
                           [("npsb", g), "cs"], bt)
                    cp(nps_sb[0:120, half, :], pbs[bt][0:120, 0:512], [PB(bt)], [("stg", half)], eng="act")
                    P.dma("sp", f"d_nps{half}",
                          lambda e, half=half: e.dma_start(out=nps[half * 120:(half + 1) * 120, :], in_=nps_sb[0:120, half, :]),
                          reads=[("stg", half)])
            ck("u")
            PIPE = not has_s
            if PIPE:
                zc, rcn = [0], [0]

                def nbZ():
                    b = zc[0] % 4
                    zc[0] += 1
                    return b

                def nbR():
                    b = 6 + rcn[0] % 2
                    rcn[0] += 1
                    return b

                gcn = [0]

                def nbG():
                    b = 4 + gcn[0] % 2
                    gcn[0] += 1
                    return b
            else:
                nbZ = nbR = nbG = nb
            gb2 = [gbf, gbf2]
            QE = [qe, qe2]
            DEC = [dec, dec2]
            HS = {h: {} for h in range(4)}

            def poolmix():
                for g in range(4):
                    for (c0, n, sk) in subs:
                        bk = nbR()
                        mm(pbs[bk][:, 0:n], pmix[:, g, :], pooled[:, g, c0:c0 + n], True, True, ["pmix", ("pooled", g, sk)], bk)
                        act(pool_out[:, g, c0:c0 + n], pbs[bk][:, 0:n], AF.Copy, [PB(bk), "small"], [("pool_out", g, sk)],
                            scale=small[:, g:g + 1])

            def z_group(h, j):
                S_ = HS[h]
                if j == 0:
                    wv, wk = w_get(1 + 2 * h)
                    S_["wh"] = wv[:].rearrange("p (k j c) -> p k j c", k=8, j=4)
                    S_["wk"] = wk
                    S_["bs"] = nbZ() if has_s else None
                    S_["zb"] = []
                wh, wk, bs = S_["wh"], S_["wk"], S_["bs"]
                bp = nbZ()
                for k in range(8):
                    mm(pbs[bp][:, 0:512], wh[:, k, j, :], hT[:, k, 0:512], k == 0, k == 7, [wk] + hTk_p, bp)
                    if has_s:
                        mm(pbs[bs][:, j * 128:(j + 1) * 128], wh[:, k, j, :], hT[:, k, 512:640], k == 0, k == 7,
                           [wk, ("hT", 4)], bs)
                S_["zb"].append(bp)
                if j == 3:
                    w_release(1 + 2 * h)

            def z_evac(h):
                S_ = HS[h]
                zb, bs = S_["zb"], S_["bs"]
                gbh = gb2[h % 2]
                gbk = f"gbf{h % 2}"

                def zsrc(j, sk):
                    return (pbs[zb[j]][:, 0:512], PB(zb[j])) if sk == "p" else (pbs[bs][:, j * 128:(j + 1) * 128], PB(bs))

                for (c0, n, sk) in subs:
                    src, key = zsrc(1, sk)
                    act(t1[:, c0:c0 + n], src, AF.Sigmoid, [key], [("t1", sk)])
                    src, key = zsrc(0, sk)
                    act(qf[:, c0:c0 + n], src, AF.Sigmoid, [key], [("qf", sk)])
                    tt(qf[:, c0:c0 + n], qf[:, c0:c0 + n], src, ALU.mult, [("qf", sk), key], [("qf", sk)])
                    src, key = zsrc(3, sk)
                    act(t4[:, c0:c0 + n], src, AF.Sigmoid, [key], [("t4", sk)])
                    tt(gbh[:, c0:c0 + n], t4[:, c0:c0 + n], src, ALU.mult, [("t4", sk), key], [(gbk, sk)])
                    src, key = zsrc(2, sk)
                    cp(vb[:, c0:c0 + n], src, [key], [("vb", sk)])

            def g_mm(h, jj):
                S_ = HS[h]
                if jj == 0:
                    wgv_, wgk_ = w_get(2 + 2 * h)
                    S_["wgv"] = wgv_[:].rearrange("p (k t c) -> p k t c", k=8, t=4)
                    S_["wgk"] = wgk_
                    S_["gate_ev"] = []
                wgv, wgk_ = S_["wgv"], S_["wgk"]
                j = 2 * h + jj
                bsg = nbG() if has_s else None
                for t_ in range(2):
                    bpg = nbG()
                    for k in range(8):
                        mm(pbs[bpg][:, 0:512], wgv[:, k, 2 * jj + t_, :], hT[:, k, 0:512], k == 0, k == 7, [wgk_] + hTk_p, bpg)
                        if has_s:
                            mm(pbs[bsg][:, t_ * 128:(t_ + 1) * 128], wgv[:, k, 2 * jj + t_, :], hT[:, k, 512:640], k == 0, k == 7,
                               [wgk_, ("hT", 4)], bsg)
                    S_["gate_ev"].append((j, t_, bpg, bsg))
                if jj == 1:
                    w_release(2 + 2 * h)

            def g_evac(h, jj):
                for (j, t_, bpg, bsg) in HS[h]["gate_ev"][2 * jj:2 * jj + 2]:
                    dst = sgA if t_ == 0 else sgB
                    dk = "sgA" if t_ == 0 else "sgB"
                    cp(dst[:, j, 0:512], pbs[bpg][:, 0:512], [PB(bpg)], [(dk, j, "p")], eng="act")
                    if has_s:
                        cp(dst[:, j, 512:640], pbs[bsg][:, t_ * 128:(t_ + 1) * 128], [PB(bsg)], [(dk, j, "s")], eng="act")

            def chain_a(h):
                ts(t1[:, 0:NT], t1[:, 0:NT], lbv[:, 4 + h:5 + h], lbv[:, h:h + 1], ALU.mult, ALU.add, K("t1") + ["lbv"], K("t1"))
                ts(t2[:, 0:NT], t1[:, 0:NT], -1.0, 1.0, ALU.mult, ALU.add, K("t1"), K("t2"))
                act(t1[:, 0:NT], t1[:, 0:NT], AF.Ln, K("t1"), K("t1"))

            def cb_scan(h):
                A("dve", lambda e: e.tensor_tensor_scan(out=t3[:, 0:NT], data0=resetm[:, 0:NT], data1=t1[:, 0:NT],
                                                        initial=0.0, op0=ALU.mult, op1=ALU.add),
                  K("t1") + ["resetmb"], K("t3"))

            def cb_exp(h):
                act(t1[:, 0:NT], t3[:, 0:NT], AF.Exp, K("t3"), K("t1"))
                act(t4[:, 0:NT], t3[:, 0:NT], AF.Exp, K("t3"), K("t4"), scale=-1.0)

            def cb_mul(h):
                qe_ = QE[h % 2]
                qk = f"qe{h % 2}"
                tt(qe_[:, 0:NT], qf[:, 0:NT], t1[:, 0:NT], ALU.mult, K("qf") + K("t1"), [(qk, "p")] + ([(qk, "s")] if has_s else []))
                tt(t2[:, 0:NT], t2[:, 0:NT], t4[:, 0:NT], ALU.mult, K("t2") + K("t4"), K("t2"))

            def cb_keb(h):
                cp(keb[:, 0:NT], t2[:, 0:NT], K("t2"), K("keb"), eng="act")

            def cb_dec(h):
                dec_ = DEC[h % 2]
                dk_ = f"dec{h % 2}"
                cp(dec_[:, 0:8], t1[:, 63:512:64], [("t1", "p")], [(dk_, "p")])
                tt(kdb[:, 0:512].rearrange("p (c t) -> p c t", t=64), t2[:, 0:512].rearrange("p (c t) -> p c t", t=64),
                   dec_[:, 0:8].unsqueeze(2).broadcast_to([128, 8, 64]), ALU.mult, [("t2", "p"), (dk_, "p")], [("kdb", "p")])
                if has_s:
                    cp(dec_[:, 8:24], t1[:, 519:640:8], [("t1", "s")], [(dk_, "s")])
                    tt(kdb[:, 512:640].rearrange("p (c t) -> p c t", t=8), t2[:, 512:640].rearrange("p (c t) -> p c t", t=8),
                       dec_[:, 8:24].unsqueeze(2).broadcast_to([128, 16, 8]), ALU.mult, [("t2", "s"), (dk_, "s")], [("kdb", "s")])

            def chain_b(h):
                cb_scan(h); cb_exp(h); cb_mul(h); cb_keb(h); cb_dec(h)

            def R1(h):
                qe = QE[h % 2]
                qk = f"qe{h % 2}"
                for (src_, srck, dst, dstk) in ((kdb, "kdb", kdT, "kdT"), (vb, "vb", vT, "vT")):
                    bt = nbR()
                    bv = pbs[bt][:].bitcast(BF16)
                    for b in blocks:
                        sk = "p" if b < 4 else "s"
                        tp(bv[:, b * 128:(b + 1) * 128], src_[:, b * 128:(b + 1) * 128], identb[:], [(srck, sk), "identb"], bt)
                    nbk = len(blocks)
                    cp(dst[:, 0:nbk, :], bv[:, 0:nbk * 128].rearrange("p (b k) -> p b k", k=128), [PB(bt)], [dstk], eng="act")
                ba = nbR()
                for b in range(4):
                    mm(pbs[ba][:, b * 128:(b + 1) * 128], keb[:, b * 128:(b + 1) * 128], qe[:, b * 128:(b + 1) * 128],
                       True, True, [("keb", "p"), (qk, "p")], ba)
                tt(Am[:, 0:4, :], pbs[ba][:, 0:512].rearrange("p (b t) -> p b t", b=4),
                   maskPb[:].unsqueeze(1).broadcast_to([128, 4, 128]), ALU.mult, [PB(ba), "maskPb"], [("Am", "p")])
                if has_s:
                    bas = nbR()
                    mm(pbs[bas][:, 0:128], keb[:, 512:640], qe[:, 512:640], True, True, [("keb", "s"), (qk, "s")], bas)
                    tt(Am[:, 4, :], pbs[bas][:, 0:128], maskSb[:], ALU.mult, [PB(bas), "maskSb"], [("Am", "s")])
                    tt(Vm[:, :, :], vT[:, 4, :].unsqueeze(1).broadcast_to([128, 16, 128]),
                       seqm.unsqueeze(2).broadcast_to([128, 16, 128]), ALU.mult, ["vT", "cs"], ["Vm"])

            def sample_prefetch(h):
                P.dma("sp", "d_S0", lambda e, h=h: e.dma_start(out=S0[:, :, :], in_=shg[:, h, :, :].rearrange("i k v -> k i v")),
                      writes=["S0"])

            def sample_bf(h):
                cp(S0bf[:, :, :].rearrange("p i v -> p (i v)"), S0[:, :, :].rearrange("p i v -> p (i v)"), ["S0"], ["S0bf"], eng="act")

            def sample_state_update(h):
                dec = DEC[h % 2]
                dk_ = f"dec{h % 2}"
                tt(S0[:, :, :], S0[:, :, :], dec[:, 8:24].unsqueeze(2).broadcast_to([128, 16, 128]), ALU.mult,
                   ["S0", (dk_, "s")], ["S0"])
                for q4 in range(4):
                    bd = nbR()
                    mm(pbs[bd][:, 0:512], kdT[:, 4, :], Vm[:, 4 * q4:4 * q4 + 4, :].rearrange("p i v -> p (i v)"), True, True,
                       ["kdT", "Vm"], bd)
                    tt(S0[:, 4 * q4:4 * q4 + 4, :].rearrange("p i v -> p (i v)"),
                       S0[:, 4 * q4:4 * q4 + 4, :].rearrange("p i v -> p (i v)"), pbs[bd][:, 0:512], ALU.add,
                       ["S0", PB(bd)], ["S0"])
                P.dma("sp", "d_nhs", lambda e, h=h: e.dma_start(out=nhs[:, h, :, :].rearrange("i k v -> k i v"), in_=S0[:, :, :]),
                      reads=["S0"])

            def R2_mm(h):
                dsb = [nbR(), nbR()]
                HS[h]["dsb"] = dsb
                for c in range(8):
                    blk, half = c // 2, c % 2
                    bk = dsb[half]
                    mm(pbs[bk][:, blk * 128:(blk + 1) * 128], kdT[half * 64:(half + 1) * 64, blk, :],
                       vT[half * 64:(half + 1) * 64, blk, :], True, True, ["kdT", "vT"], bk)
                cp(Sall[:, 0, :], carry[:, h, :], ["carry"], [("Sall", 0)])

            def R2_steps(h, c0, c1):
                dec_ = DEC[h % 2]
                dk_ = f"dec{h % 2}"
                dsb = HS[h]["dsb"]
                for c in range(c0, c1):
                    blk, half = c // 2, c % 2
                    bk = dsb[half]
                    stt(Sall[:, c + 1, :], Sall[:, c, :], dec_[:, c:c + 1], pbs[bk][:, blk * 128:(blk + 1) * 128],
                        ALU.mult, ALU.add, [("Sall", c), (dk_, "p"), PB(bk)], [("Sall", c + 1)])

            def R2_fin(h):
                cp(Sbf[:, :, :].rearrange("p c v -> p (c v)"), Sall[:, 0:8, :].rearrange("p c v -> p (c v)"),
                   [("Sall", c) for c in range(8)], ["Sbf"], eng="act")
                cp(carry[:, h, :], Sall[:, 8, :], [("Sall", 8)], ["carry"])
                if last:
                    P.dma("sp", f"d_nhp{h}", lambda e, h=h: e.dma_start(out=nhp[h], in_=Sall[:, 8, :]), reads=[("Sall", 8)])

            def R2(h):
                R2_mm(h); R2_steps(h, 0, 8); R2_fin(h)

            def R3(h):
                qe = QE[h % 2]
                qk = f"qe{h % 2}"
                S_ = HS[h]
                bo = nbR()
                for b in range(4):
                    mm(pbs[bo][:, b * 128:(b + 1) * 128], vT[:, b, :], Am[:, b, :], True, False, ["vT", ("Am", "p")], bo)
                    for half in range(2):
                        c = 2 * b + half
                        mm(pbs[bo][:, c * 64:(c + 1) * 64], Sbf[:, c, :], qe[:, c * 64:(c + 1) * 64], False, half == 1,
                           ["Sbf", (qk, "p")], bo)
                bos = None
                if has_s:
                    sample_bf(h)
                    bos = nbR()
                    mm(pbs[bos][:, 0:128], vT[:, 4, :], Am[:, 4, :], True, False, ["vT", ("Am", "s")], bos)
                    for i in range(16):
                        mm(pbs[bos][:, i * 8:(i + 1) * 8], S0bf[:, i, :], qe[:, 512 + i * 8:512 + (i + 1) * 8], False, i == 15,
                           ["S0bf", (qk, "s")], bos)
                S_["obanks"] = [(0, 512, "p", bo)] + ([(512, 128, "s", bos)] if has_s else [])
                for (c0, n, sk, bk) in S_["obanks"]:
                    act(osq[:, c0:c0 + n], pbs[bk][:, 0:n], AF.Square, [PB(bk)], [("osq", sk)])

            def R4(h):
                gbh = gb2[h % 2]
                gbk = f"gbf{h % 2}"
                for (c0, n, sk, bk) in HS[h]["obanks"]:
                    bn = nbR()
                    mm(pbs[bn][:, 0:n], onesd[:], osq[:, c0:c0 + n], True, True, ["onesd", ("osq", sk)], bn)
                    act(s1[:, c0:c0 + n], pbs[bn][:, 0:n], AF.Ln, [PB(bn)], [("s1", sk)], bias=EPS)
                    act(s1[:, c0:c0 + n], s1[:, c0:c0 + n], AF.Exp, [("s1", sk)], [("s1", sk)], scale=-0.5)
                    stt(s2[:, c0:c0 + n], pbs[bk][:, 0:n], small[:, 4 + h:5 + h], s1[:, c0:c0 + n], ALU.mult, ALU.mult,
                        [PB(bk), "small", ("s1", sk)], [("s2", sk)])
                tt(o_fin[:, h, 0:NT], s2[:, 0:NT], gbh[:, 0:NT], ALU.mult, K("s2") + [(gbk, "p"), (gbk, "s")],
                   [("o_fin", h, s_) for s_ in ("p", "s")])

            if PIPE:
                for j in range(4):
                    z_group(0, j)
            poolmix()
            ck("poolmix")
            P.barrier()
            ck("bar")
            if PIPE:
                for h in range(4):
                    p = h - 1
                    if h > 0:
                        R1(p)
                    z_evac(h)
                    g_mm(h, 0)
                    if h > 0:
                        R2_mm(p)
                    chain_a(h)
                    if h > 0:
                        R2_steps(p, 0, 4)
                    g_evac(h, 0)
                    if h < 3:
                        for j in range(4):
                            z_group(h + 1, j)
                    cb_scan(h)
                    if h > 0:
                        R2_steps(p, 4, 7)
                    cb_exp(h)
                    cb_mul(h)
                    cb_dec(h)
                    cb_keb(h)
                    g_mm(h, 1)
                    if h > 0:
                        R2_steps(p, 7, 8)
                        R2_fin(p)
                        R3(p)
                    g_evac(h, 1)
                    if h > 0:
                        R4(p)
                R1(3); R2(3); R3(3); R4(3)
            else:
                for h in range(4):
                    sample_prefetch(h)
                    for j in range(4):
                        z_group(h, j)
                    z_evac(h)
                    g_mm(h, 0)
                    g_mm(h, 1)
                    chain_a(h)
                    chain_b(h)
                    g_evac(h, 0)
                    g_evac(h, 1)
                    R1(h); R2(h); R3(h); R4(h)
                    sample_state_update(h)

            ck("hgrn")
            for jh in range(2):
                wy, wyk = w_get(9 + jh)
                wyv = wy[:].rearrange("p (k t c) -> p k t c", k=4, t=2)
                for jj in range(4):
                    j = jh * 4 + jj
                    bs = nb() if has_s else None
                    b_ya, b_yb = nb(), nb()
                    for t_, bk, srcb, srck in ((0, b_ya, pool_out, "pool_out"), (1, b_yb, o_fin, "o_fin")):
                        for kk in range(4):
                            for (c0, n, sk) in subs:
                                o = pbs[bk][:, 0:512] if sk == "p" else pbs[bs][:, t_ * 128:(t_ + 1) * 128]
                                mm(o, wyv[:, kk, t_, jj * 128:(jj + 1) * 128], srcb[:, kk, c0:c0 + n], kk == 0, kk == 3,
                                   [wyk, (srck, kk, sk)], bk if sk == "p" else bs)
                    if jj == 3:
                        w_release(9 + jh)
                    for (c0, n, sk) in subs:
                        oa = pbs[b_ya][:, 0:512] if sk == "p" else pbs[bs][:, 0:128]
                        ob = pbs[b_yb][:, 0:512] if sk == "p" else pbs[bs][:, 128:256]
                        ka = PB(b_ya) if sk == "p" else PB(bs)
                        kb_ = PB(b_yb) if sk == "p" else PB(bs)
                        act(s1[:, c0:c0 + n], sgA[:, j, c0:c0 + n], AF.Sigmoid, [("sgA", j, sk)], [("s1", sk)])
                        act(s2[:, c0:c0 + n], sgB[:, j, c0:c0 + n], AF.Sigmoid, [("sgB", j, sk)], [("s2", sk)])
                        tt(s1[:, c0:c0 + n], s1[:, c0:c0 + n], oa, ALU.mult, [("s1", sk), ka], [("s1", sk)])
                        tt(s2[:, c0:c0 + n], s2[:, c0:c0 + n], ob, ALU.mult, [("s2", sk), kb_], [("s2", sk)])
                    tt(merged[:, j, 0:NT], s1[:, 0:NT], s2[:, 0:NT], ALU.add, K("s1") + K("s2"),
                       [("merged", j, s_) for s_ in ("p", "s")])

            ck("merge")
            for half in range(2):
                wo, wok = w_get(11 + half)
                wov = wo[:].rearrange("p (k c) -> p k c", k=8)
                bks = {b: nb() for b in blocks}
                for k in range(8):
                    for b in blocks:
                        sk = "p" if b < 4 else "s"
                        mm(pbs[bks[b]][:, 0:512], merged[:, k, b * 128:(b + 1) * 128], wov[:, k, :], k == 0, k == 7,
                           [wok, ("merged", k, sk)], bks[b])
                w_release(11 + half)
                for b in blocks:
                    tt(xres[:, b, half * 512:(half + 1) * 512], xres[:, b, half * 512:(half + 1) * 512], pbs[bks[b]][:, 0:512],
                       ALU.add, [("xres", b), PB(bks[b])], [("xres", b)])
                    half_stats(b, half)

            ck("wout")
            norm_T(1, True)
            P.barrier()
            for b in blocks:
                src = ppd[ti * 512 + b * 128: ti * 512 + (b + 1) * 128, :] if b < 4 else psd
                P.dma("sp", f"d_p{b}", lambda e, src=src, b=b: e.dma_start(out=pstage[:, b, :], in_=src), writes=[("pstage", b)])
            for jb in range(11):
                wf, wfk = w_get(13 + jb)
                wfv = wf[:].rearrange("p (k j c) -> p k j c", k=8, j=2)
                for cc in range(2):
                    f = 2 * jb + cc
                    bs = nb() if has_s else None
                    b_g, b_u = nb(), nb()
                    for jx, bk in ((0, b_g), (1, b_u)):
                        for k in range(8):
                            for (c0, n, sk) in subs:
                                o = pbs[bk][:, 0:512] if sk == "p" else pbs[bs][:, jx * 128:(jx + 1) * 128]
                                mm(o, wfv[:, k, jx, cc * 128:(cc + 1) * 128], hT[:, k, c0:c0 + n], k == 0, k == 7,
                                   [wfk] + hTk[sk], bk if sk == "p" else bs)
                    if cc == 1:
                        w_release(13 + jb)
                    for (c0, n, sk) in subs:
                        og = pbs[b_g][:, 0:512] if sk == "p" else pbs[bs][:, 0:128]
                        ou = pbs[b_u][:, 0:512] if sk == "p" else pbs[bs][:, 128:256]
                        kg = PB(b_g) if sk == "p" else PB(bs)
                        ku = PB(b_u) if sk == "p" else PB(bs)
                        act(ftmp[:, c0:c0 + n], og, AF.Sigmoid, [kg], [("ftmp", sk)])
                        tt(ftmp[:, c0:c0 + n], ftmp[:, c0:c0 + n], og, ALU.mult, [("ftmp", sk), kg], [("ftmp", sk)])
                        tt(hidden[:, f, c0:c0 + n], ftmp[:, c0:c0 + n], ou, ALU.mult, [("ftmp", sk), ku], [("hidden", f, sk)])
            for b in blocks:
                cp(pbf[:], pstage[:, b, :], [("pstage", b)], ["pbf"], eng="act")
                bt = nb()
                bv = pbs[bt][:].bitcast(BF16)
                for k in range(2):
                    tp(bv[:, k * 128:(k + 1) * 128], pbf[:, k * 128:(k + 1) * 128], identb[:], ["pbf", "identb"], bt)
                cp(pT[:, :, b * 128:(b + 1) * 128], bv[:, 0:256].rearrange("p (k t) -> p k t", k=2), [PB(bt)], [("pT", b)])
            for half in range(2):
                bks = {b: nb() for b in blocks}
                for kb in range(3):
                    wd, wdk = w_get(24 + half * 3 + kb)
                    wdv = wd[:].rearrange("p (k c) -> p k c", k=8)
                    nk = 8 if kb < 2 else 6
                    for kk in range(nk):
                        f = kb * 8 + kk
                        for b in blocks:
                            sk = "p" if b < 4 else "s"
                            mm(pbs[bks[b]][:, 0:512], hidden[:, f, b * 128:(b + 1) * 128], wdv[:, kk, :], f == 0, f == 21,
                               [wdk, ("hidden", f, sk)], bks[b])
                    w_release(24 + half * 3 + kb)
                for b in blocks:
                    tt(xres[:, b, half * 512:(half + 1) * 512], xres[:, b, half * 512:(half + 1) * 512], pbs[bks[b]][:, 0:512],
                       ALU.add, [("xres", b), PB(bks[b])], [("xres", b)])
                    half_stats(b, half)

            ck("ffn")
            norm_T(2, True)
            wg0, wg0k = w_get(30)
            wpp_, wppk = w_get(31)
            wg1, wg1k = w_get(32)
            wppv = wpp_[:, 0:2048].rearrange("p (k c) -> p k c", k=2)
            for half in range(2):
                wg_, wgk = (wg0, wg0k) if half == 0 else (wg1, wg1k)
                wgv = wg_[:].rearrange("p (k c) -> p k c", k=8)
                bks = {b: nb() for b in blocks}
                for k in range(8):
                    for b in blocks:
                        mm(pbs[bks[b]][:, 0:512], hT[:, k, b * 128:(b + 1) * 128], wgv[:, k, :], k == 0, k == 7,
                           [wgk, ("hT", b)], bks[b])
                if half == 0:
                    w_release(30)
                for b in blocks:
                    sg = sgt[b % 2]
                    sgk = f"sgt{b % 2}"
                    act(sg[:], pbs[bks[b]][:, 0:512], AF.Sigmoid, [PB(bks[b])], [sgk])
                    be = nb()
                    for k in range(2):
                        mm(pbs[be][:, 0:512], pT[:, k, b * 128:(b + 1) * 128], wppv[:, k, half * 512:(half + 1) * 512], k == 0, k == 1,
                           [wppk, ("pT", b)], be)
                    tt(sg[:], sg[:], pbs[be][:, 0:512], ALU.mult, [sgk, PB(be)], [sgk])
                    tt(xres[:, b, half * 512:(half + 1) * 512], xres[:, b, half * 512:(half + 1) * 512], sg[:], ALU.add,
                       [("xres", b), sgk], [("xres", b)])
                for b in blocks:
                    half_stats(b, half)
                if half == 1:
                    w_release(31); w_release(32)

            ck("ple")
            P.barrier()
            norm_stats(True)
            for b in blocks:
                dst = yp[ti * 512 + b * 128: ti * 512 + (b + 1) * 128, :] if b < 4 else ys
                for half in range(2):
                    q_ = (2 * b + half) % 4
                    yst = ystage[q_]
                    ysk = f"ystage{q_}"
                    hs_ = slice(half * 512, (half + 1) * 512)
                    stt(yst[:], xres[:, b, hs_], rst[:, b:b + 1], gB[:, 3, hs_], ALU.mult, ALU.mult,
                        [("xres", b), "rst", ("gB", 3)], [ysk])
                    P.dma("sp", f"d_y{q_}", lambda e, yst=yst, dst=dst, hs_=hs_: e.dma_start(out=dst[:, hs_], in_=yst[:]),
                          reads=[ysk])
                if ti + 1 < ntiles and b < 4:
                    load_x(ti + 1, b)

        P.barrier()
        try:
            ck("setup")
            for ti in range(ntiles):
                run_tile(ti)
        except _Stop:
            pass
        P.finish("sp")
        P.emit()
    return nc


_PROG = {}


def _prep_inputs(inp):
    f = lambda a: np.ascontiguousarray(np.asarray(a, dtype=np.float32))
    w_in = f(inp["w_in"][0])
    wallv = build_wall(w_in, f(inp["w_pool_up"][0]), f(inp["w_hgrn_up"][0]), f(inp["w_out"][0]),
                       f(inp["w_ffn_gate"][0]), f(inp["w_ffn_up"][0]), f(inp["w_ffn_down"][0]),
                       f(inp["w_ple_gate"][0]), f(inp["w_ple_proj"][0]))
    gvec = np.ascontiguousarray(np.stack([f(inp["g_mix"][0]), f(inp["g_ffn"][0]), f(inp["g_ple"][0]), f(inp["g_final"])], 0))
    small = np.zeros((128, 16), np.float32)
    small[:, 0:4] = f(inp["pool_scale"][0]).reshape(4, 128).T
    small[:, 4:8] = f(inp["hgrn_norm"][0]).reshape(4, 128).T
    small[:, 8:12] = f(inp["hgrn_lb"][0]).reshape(4, 128).T
    small[:, 12:16] = f(inp["hgrn_lb"][1]).reshape(4, 128).T
    pmix = np.ascontiguousarray(f(inp["w_pool_mix"][0]).transpose(1, 0, 2)).reshape(128, 512)
    xp = f(inp["x_prompt"]); xsm = f(inp["x_sample"])
    ppr = f(inp["p_prompt"][0]); psm = f(inp["p_sample"][0])
    spl = f(inp["state_pool"][0]); shg = f(inp["state_hgrn"][0])
    maps = []
    for c in range(NCORES):
        maps.append({
            "xp": xp[c], "xs": xsm[16 * c:16 * c + 16].reshape(128, 1024),
            "pp": ppr[c], "ps": psm[16 * c:16 * c + 16].reshape(128, 256),
            "spool": spl[16 * c:16 * c + 16].reshape(240, 512),
            "shg": shg[16 * c:16 * c + 16],
            "wall": wallv, "cst": CONST_ARR, "gvec": gvec, "small": small, "pmix": pmix,
        })
    return maps


def kernel(**inputs):
    if "nc" not in _PROG:
        _PROG["nc"] = build_program()
    nc = _PROG["nc"]
    maps = _prep_inputs(inputs)
    res = run_bass_kernel_spmd(nc, maps, core_ids=list(range(NCORES)))
    R = res.results
    y_p = np.stack([R[c]["yp"] for c in range(NCORES)], 0).astype(np.float32)
    y_s = np.concatenate([R[c]["ys"].reshape(16, 8, 1024) for c in range(NCORES)], 0).astype(np.float32)
    npp = np.stack([R[c]["npp"] for c in range(NCORES)], 0)[None].astype(np.float32)
    nhp = np.stack([R[c]["nhp"] for c in range(NCORES)], 0)[None].astype(np.float32)
    nps = np.concatenate([R[c]["nps"].reshape(16, 15, 512) for c in range(NCORES)], 0)[None].astype(np.float32)
    nhs = np.concatenate([R[c]["nhs"] for c in range(NCORES)], 0)[None].astype(np.float32)
    return (y_p, y_s, npp, nhp, nps, nhs)
```

```python
import numpy as np
from contextlib import ExitStack
import concourse.bass as bass
import concourse.mybir as mybir
from concourse.bass_utils import run_bass_kernel_spmd

F32 = mybir.dt.float32
BF16 = mybir.dt.bfloat16
AF = mybir.ActivationFunctionType
ALU = mybir.AluOpType

ENGS = ("pe", "act", "dve", "pool", "sp")
EPS = 1e-6
NCORES = 8
NRING = 5
NTILES = 4
import os as _os
SAME_DIST = int(_os.environ.get("SAME_DIST", "2"))


class Prog:
    def __init__(self, nc, stack, same_engine_wait=True):
        self.nc = nc
        self.stack = stack
        self.streams = {e: [] for e in ENGS}
        self.count = {e: 0 for e in ENGS}
        self.known = {e: {} for e in ENGS}
        self.sems = {}
        self.dma_count = {}
        self.lastw = {}
        self.readers = {}
        self.same_engine_wait = same_engine_wait

    def _collect(self, eng, reads, writes):
        deps = []
        for k in reads:
            ev = self.lastw.get(k)
            if ev is not None:
                deps.append(ev)
        for k in writes:
            ev = self.lastw.get(k)
            if ev is not None:
                deps.append(ev)
            rd = self.readers.get(k)
            if rd:
                deps.extend(rd.values())
        kn = self.known[eng]
        need = {}
        for (s, v, vc) in deps:
            if s == eng and (eng == "pe" or not self.same_engine_wait or self.count[eng] - v >= SAME_DIST):
                continue
            if kn.get(s, 0) >= v:
                continue
            if need.get(s, 0) < v:
                need[s] = v
            for s2, v2 in vc.items():
                if s2 == eng:
                    continue
                if kn.get(s2, 0) < v2:
                    kn[s2] = v2
        waits = []
        for s, v in need.items():
            waits.append((s, v))
            if kn.get(s, 0) < v:
                kn[s] = v
        return waits

    def _record(self, ev, reads, writes):
        s = ev[0]
        for k in reads:
            self.readers.setdefault(k, {})[s] = ev
        for k in writes:
            self.lastw[k] = ev
            self.readers[k] = {}

    def op(self, eng, fn, reads=(), writes=()):
        waits = self._collect(eng, reads, writes)
        self.count[eng] += 1
        n = self.count[eng]
        vc = dict(self.known[eng])
        vc[eng] = n
        ev = (eng, n, vc)
        self.streams[eng].append((waits, fn, (eng, 1)))
        self._record(ev, reads, writes)
        return ev

    def dma(self, q, semname, fn, reads=(), writes=()):
        waits = self._collect(q, reads, writes)
        self.dma_count[semname] = self.dma_count.get(semname, 0) + 1
        v = 16 * self.dma_count[semname]
        vc = dict(self.known[q])
        vc[semname] = v
        ev = (semname, v, vc)
        self.streams[q].append((waits, fn, (semname, 16)))
        self._record(ev, reads, writes)
        return ev

    def barrier(self, engines=("act", "dve", "sp")):
        for e in engines:
            kn = self.known[e]
            waits = []
            for e2 in ("pe", "act", "dve"):
                c = self.count[e2]
                if c and kn.get(e2, 0) < c:
                    waits.append((e2, c))
                    kn[e2] = c
            for s, c in self.dma_count.items():
                if s.startswith("ring") or s.startswith("d_y") or s.startswith("d_x"):
                    continue
                if kn.get(s, 0) < 16 * c:
                    waits.append((s, 16 * c))
                    kn[s] = 16 * c
            if waits:
                self.streams[e].append((waits, None, None))

    def finish(self, eng="sp"):
        kn = self.known[eng]
        waits = []
        for s, c in self.dma_count.items():
            if kn.get(s, 0) < 16 * c:
                waits.append((s, 16 * c))
                kn[s] = 16 * c
        for e in ("pe", "act", "dve"):
            if self.count[e] and kn.get(e, 0) < self.count[e]:
                waits.append((e, self.count[e]))
        self.streams[eng].append((waits, None, None))

    def emit(self):
        nc = self.nc
        for s in list(ENGS) + list(self.dma_count):
            if s not in self.sems:
                self.sems[s] = self.stack.enter_context(nc.semaphore(s))
        with nc.Block() as block:
            def run(engine, items):
                for waits, fn, inc in items:
                    for s, v in waits:
                        engine.wait_ge(self.sems[s], v)
                    if fn is not None:
                        ins = fn(engine)
                        ins.then_inc(self.sems[inc[0]], inc[1])

            @block.tensor
            def _(eng):
                run(eng, self.streams["pe"])

            @block.scalar
            def _(eng):
                run(eng, self.streams["act"])

            @block.vector
            def _(eng):
                run(eng, self.streams["dve"])

            @block.gpsimd
            def _(eng):
                run(eng, self.streams["pool"])

            @block.sync
            def _(eng):
                run(eng, self.streams["sp"])


NBLK = 33


def _kc(W, nk):
    C = W.shape[1]
    return np.ascontiguousarray(W.reshape(nk, 128, C).transpose(1, 0, 2)).reshape(128, nk * C)


def _pad(a):
    out = np.zeros((128, 4096), np.float32)
    out[:, : a.shape[1]] = a
    return out


def build_wall(w_in, w_pool_up, w_hgrn_up, w_out, w_g, w_u, w_d, w_pg, w_pp):
    blocks = []
    blocks.append(_kc(w_in[:, 0:512], 8))
    zz = w_in[:, 512:2560].reshape(8, 128, 4, 4, 128)
    ga = w_in[:, 2560:3584].reshape(8, 128, 8, 128)
    gb = w_in[:, 3584:4608].reshape(8, 128, 8, 128)
    for h in range(4):
        blocks.append(np.ascontiguousarray(zz[:, :, :, h, :].transpose(1, 0, 2, 3)).reshape(128, 4096))
        gg = np.stack([ga[:, :, 2 * h], gb[:, :, 2 * h], ga[:, :, 2 * h + 1], gb[:, :, 2 * h + 1]], axis=2)
        blocks.append(np.ascontiguousarray(gg.transpose(1, 0, 2, 3)).reshape(128, 4096))
    for half in range(2):
        yy = np.stack([w_pool_up[:, half * 512:(half + 1) * 512].reshape(4, 128, 512),
                       w_hgrn_up[:, half * 512:(half + 1) * 512].reshape(4, 128, 512)], axis=2)
        blocks.append(np.ascontiguousarray(yy.transpose(1, 0, 2, 3)).reshape(128, 4096))
    blocks.append(_kc(w_out[:, 0:512], 8))
    blocks.append(_kc(w_out[:, 512:1024], 8))
    for jb in range(11):
        gu = np.stack([w_g[:, jb * 256:(jb + 1) * 256], w_u[:, jb * 256:(jb + 1) * 256]], axis=1)
        blocks.append(_kc(gu.reshape(1024, 512), 8))
    for half in range(2):
        for kb in range(3):
            nk = 8 if kb < 2 else 6
            blocks.append(_pad(_kc(w_d[kb * 1024: kb * 1024 + nk * 128, half * 512:(half + 1) * 512], nk)))
    blocks.append(_kc(w_pg[:, 0:512], 8))
    blocks.append(_pad(_kc(w_pp, 2)))
    blocks.append(_kc(w_pg[:, 512:1024], 8))
    assert len(blocks) == NBLK
    return np.ascontiguousarray(np.stack(blocks, axis=0).astype(np.float32))


def build_consts():
    c = {}
    c["ident"] = np.eye(128, dtype=np.float32)
    s = np.arange(128)[:, None]
    t = np.arange(128)[None, :]
    c["maskP"] = ((s // 64 == t // 64) & (s <= t)).astype(np.float32)
    c["maskS"] = ((s // 8 == t // 8) & (s <= t)).astype(np.float32)
    c["seqm"] = (s // 8 == np.arange(16)[None, :]).astype(np.float32)
    r = np.ones(640, np.float32)
    r[0:512:64] = 0.0
    r[512:640:8] = 0.0
    c["resetm"] = np.broadcast_to(r, (128, 640)).copy()
    rc = np.zeros((4, 16), np.float32)
    for g, w in enumerate((2, 4, 8, 16)):
        rc[g] = 1.0 / np.minimum(np.arange(16) + 1, w)
    c["rc"] = np.broadcast_to(rc.reshape(1, 64), (128, 64)).copy()
    order = ["ident", "seqm", "rc", "maskP", "maskS", "resetm"]
    offs = {}
    o = 0
    for k in order:
        offs[k] = (o, c[k].shape[1])
        o += c[k].shape[1]
    return np.ascontiguousarray(np.concatenate([c[k] for k in order], axis=1)), offs


CONST_ARR, COFF = build_consts()
CW = CONST_ARR.shape[1]
CW_KEEP = COFF["maskP"][0]


class _Stop(Exception):
    pass


STOP = None


def ck(name):
    if STOP == name:
        raise _Stop()


def build_program(ntiles=NTILES):
    nc = bass.Bass("TRN2", target_bir_lowering=False)

    def din(name, shape):
        return nc.dram_tensor(name, shape, F32, kind="ExternalInput").ap()

    def dout(name, shape):
        return nc.dram_tensor(name, shape, F32, kind="ExternalOutput").ap()

    xp = din("xp", [2048, 1024])
    xs = din("xs", [128, 1024])
    ppd = din("pp", [2048, 256])
    psd = din("ps", [128, 256])
    spool = din("spool", [240, 512])
    shg = din("shg", [16, 4, 128, 128])
    wall = din("wall", [NBLK, 128, 4096])
    cst = din("cst", [128, CW])
    gvec = din("gvec", [4, 1024])
    smalld = din("small", [128, 16])
    pmixd = din("pmix", [128, 512])
    yp = dout("yp", [2048, 1024])
    ys = dout("ys", [128, 1024])
    npp = dout("npp", [15, 512])
    nhp = dout("nhp", [4, 128, 128])
    nps = dout("nps", [240, 512])
    nhs = dout("nhs", [16, 4, 128, 128])

    with ExitStack() as st:
        P = Prog(nc, st)

        def sb(name, shape, dt):
            return st.enter_context(nc.sbuf_tensor("sb_" + name, shape, dt))

        xres = sb("xres", [128, 5, 1024], F32)
        hT = sb("hT", [128, 8, 640], BF16)
        hn = [sb(f"hn{i}", [128, 1024], BF16) for i in range(2)]
        junk = sb("junk", [128, 512], BF16)
        ystage = [sb(f"ystage{i}", [128, 512], F32) for i in range(4)]
        resetmb = sb("resetmb", [128, 640], BF16)
        ring = [sb(f"ring{i}", [128, 4096], BF16) for i in range(NRING)]
        gB = sb("gB", [128, 4, 1024], F32)
        cs = sb("cs", [128, CW_KEEP], F32)
        identb = sb("identb", [128, 128], BF16)
        onesd = sb("onesd", [128, 128], BF16)
        maskPb = sb("maskPb", [128, 128], BF16)
        maskSb = sb("maskSb", [128, 128], BF16)
        pmixf = sb("pmixf", [128, 512], F32)
        pmix = sb("pmix", [128, 4, 128], BF16)
        small = sb("small", [128, 16], F32)
        lbv = sb("lbv", [128, 8], F32)
        carry = sb("carry", [128, 4, 128], F32)
        ucarry = sb("ucarry", [128, 4, 16], F32)
        ssq = sb("ssq", [128, 16], F32)
        ssq1 = sb("ssq1", [128, 8], F32)
        rst1 = sb("rst1", [128, 4], F32)
        rst = sb("rst", [128, 8], F32)
        identf = cs[:, COFF["ident"][0]:COFF["ident"][0] + 128]
        seqm = cs[:, COFF["seqm"][0]:COFF["seqm"][0] + 16]
        rcv = cs[:, COFF["rc"][0]:COFF["rc"][0] + 64]
        resetm = resetmb

        MIX_BYTES = 0
        scr_plan = {}

        def plan(phase, name, nelem, dt):
            nonlocal MIX_BYTES
            nbytes = nelem * (4 if dt == F32 else 2)
            nbytes = (nbytes + 31) // 32 * 32
            off = scr_plan.setdefault(("off", phase), 0)
            scr_plan[name] = (off, nelem, dt)
            scr_plan[("off", phase)] = off + nbytes

        U_NAMES = ["ug", "ue", "pa", "pb_", "sa", "sb_", "tmp16", "pooled", "npsb", "stg", "npp_sb"]
        for name, n, dt in [
            ("ug", 528, F32), ("ue", 16 * 24, F32), ("pa", 528, F32), ("pb_", 528, F32),
            ("sa", 16 * 24, F32), ("sb_", 16 * 24, F32), ("tmp16", 16, F32),
            ("pooled", 4 * 640, BF16), ("npsb", 4 * 240, F32),
            ("stg", 2 * 512, F32), ("npp_sb", 512, F32), ("pool_out", 4 * 640, BF16),
            ("qf", 640, F32), ("t1", 640, F32), ("t2", 640, F32), ("t3", 640, F32), ("t4", 640, F32),
            ("qe", 640, BF16), ("qe2", 640, BF16), ("keb", 640, BF16), ("kdb", 640, BF16), ("vb", 640, BF16), ("gbf", 640, BF16), ("gbf2", 640, BF16),
            ("kdT", 5 * 128, BF16), ("vT", 5 * 128, BF16), ("Sall", 9 * 128, F32), ("Sbf", 8 * 128, BF16),
            ("dec", 24, F32), ("dec2", 24, F32), ("Am", 5 * 128, BF16), ("osq", 640, BF16), ("o_fin", 4 * 640, BF16),
            ("S0", 16 * 128, F32), ("S0bf", 16 * 128, BF16), ("Vm", 16 * 128, BF16),
            ("merged", 8 * 640, BF16), ("s1", 640, F32), ("s2", 640, F32),
        ]:
            plan("mix", name, n, dt)
        for name, n, dt in [
            ("hidden", 22 * 640, BF16), ("ftmp", 640, F32), ("sgt0", 512, F32), ("sgt1", 512, F32),
            ("pT", 2 * 640, BF16), ("pstage", 5 * 256, F32), ("pbf", 256, BF16),
        ]:
            plan("ffn", name, n, dt)
        u_end = scr_plan["pool_out"][0]
        assert 2 * 8 * 640 * 2 <= u_end, u_end
        scr_plan["sgA"] = (0, 8 * 640, BF16)
        scr_plan["sgB"] = (8 * 640 * 2, 8 * 640, BF16)
        scr_bytes = max(scr_plan[("off", "mix")], scr_plan[("off", "ffn")])
        scr = sb("scr", [128, scr_bytes // 4], F32)

        def sv(name):
            off, n, dt = scr_plan[name]
            if dt == F32:
                return scr[:, off // 4: off // 4 + n]
            return scr[:, off // 4: off // 4 + n // 2].bitcast(BF16)

        ug = sv("ug"); ue = sv("ue").rearrange("p (i r) -> p i r", r=24)
        pa = sv("pa"); pb_ = sv("pb_")
        sa = sv("sa").rearrange("p (i r) -> p i r", r=24); sb_ = sv("sb_").rearrange("p (i r) -> p i r", r=24)
        tmp16 = sv("tmp16")
        pooled = sv("pooled").rearrange("p (g t) -> p g t", g=4)
        pool_out = sv("pool_out").rearrange("p (g t) -> p g t", g=4)
        npsb = sv("npsb").rearrange("p (g r) -> p g r", g=4)
        stg = sv("stg").rearrange("p (h c) -> p h c", h=2)
        nps_sb = stg
        npp_sb = sv("npp_sb")
        qf = sv("qf"); t1 = sv("t1"); t2 = sv("t2"); t3 = sv("t3"); t4 = sv("t4")
        qe = sv("qe"); qe2 = sv("qe2"); keb = sv("keb"); kdb = sv("kdb"); vb = sv("vb"); gbf = sv("gbf"); gbf2 = sv("gbf2")
        kdT = sv("kdT").rearrange("p (b k) -> p b k", b=5); vT = sv("vT").rearrange("p (b k) -> p b k", b=5)
        Sall = sv("Sall").rearrange("p (c v) -> p c v", c=9); Sbf = sv("Sbf").rearrange("p (c v) -> p c v", c=8)
        dec = sv("dec"); dec2 = sv("dec2"); Am = sv("Am").rearrange("p (b k) -> p b k", b=5); osq = sv("osq")
        o_fin = sv("o_fin").rearrange("p (h t) -> p h t", h=4)
        S0 = sv("S0").rearrange("p (i v) -> p i v", i=16); S0bf = sv("S0bf").rearrange("p (i v) -> p i v", i=16)
        Vm = sv("Vm").rearrange("p (i v) -> p i v", i=16)
        merged = sv("merged").rearrange("p (k t) -> p k t", k=8); s1 = sv("s1"); s2 = sv("s2")
        sgA = sv("sgA").rearrange("p (k t) -> p k t", k=8); sgB = sv("sgB").rearrange("p (k t) -> p k t", k=8)
        hidden = sv("hidden").rearrange("p (f t) -> p f t", f=22); ftmp = sv("ftmp")
        assert scr_plan["S0bf"][0] == scr_plan["S0"][0] + 8192 and scr_plan["Vm"][0] == scr_plan["S0"][0] + 12288
        assert scr_plan["S0"][0] >= scr_plan[("off", "ffn")]
        xo = scr_plan["S0"][0] // 4
        xalt = scr[:, xo:xo + 4096].rearrange("p (b d) -> p b d", b=4)
        sgt = [sv("sgt0"), sv("sgt1")]
        pT = sv("pT").rearrange("p (k t) -> p k t", k=2); pstage = sv("pstage").rearrange("p (b c) -> p b c", b=5); pbf = sv("pbf")

        pbs = [st.enter_context(nc.psum_tensor(f"pb{i}", [128, 512], F32)) for i in range(8)]
        bank_ctr = [0]

        def nb():
            b = bank_ctr[0] % 8
            bank_ctr[0] += 1
            return b

        def PB(b):
            return ("pb", b)

        wstate = {"tile": 0, "issued": set(), "released": set(), "total": NBLK * ntiles}

        def w_issue_n(n, extra_reads=()):
            if n >= wstate["total"] or n in wstate["issued"]:
                return
            slot = n % NRING
            blk = n % NBLK
            P.dma("pool", f"ring{slot}",
                  lambda e, slot=slot, blk=blk: e.dma_start(out=ring[slot][:], in_=wall[blk]),
                  reads=list(extra_reads), writes=[("ring", slot)])
            wstate["issued"].add(n)

        def w_issue(extra_reads=()):
            w_issue_n(len(wstate["issued"]), extra_reads)

        def w_get(expect):
            n = wstate["tile"] * NBLK + expect
            assert n in wstate["issued"], ("ring too small / block not prefetched", n)
            assert n - NRING < 0 or (n - NRING) in wstate["released"]
            slot = n % NRING
            return ring[slot], ("ring", slot)

        def w_release(expect):
            n = wstate["tile"] * NBLK + expect
            assert n in wstate["issued"] and n not in wstate["released"]
            wstate["released"].add(n)
            w_issue_n(n + NRING)

        def A(eng, fn, reads, writes):
            return P.op(eng, fn, reads=reads, writes=writes)

        def mm(out, lhsT, rhs, start, stop, reads, bank):
            A("pe", lambda e: e.matmul(out, lhsT=lhsT, rhs=rhs, start=start, stop=stop), reads, [PB(bank)])

        def tp(out, in_, ident, reads, bank):
            A("pe", lambda e: e.transpose(out=out, in_=in_, identity=ident), reads, [PB(bank)])

        def act(out, in_, func, reads, writes, **kw):
            A("act", lambda e: e.activation(out=out, in_=in_, func=func, **kw), reads, writes)

        def tt(out, in0, in1, op, reads, writes, eng="dve"):
            A(eng, lambda e: e.tensor_tensor(out=out, in0=in0, in1=in1, op=op), reads, writes)

        def ts(out, in0, s1_, s2_, op0, op1, reads, writes):
            A("dve", lambda e: e.tensor_scalar(out=out, in0=in0, scalar1=s1_, scalar2=s2_, op0=op0, op1=op1), reads, writes)

        def stt(out, in0, scalar, in1, op0, op1, reads, writes):
            A("dve", lambda e: e.scalar_tensor_tensor(out=out, in0=in0, scalar=scalar, in1=in1, op0=op0, op1=op1), reads, writes)

        def cp(out, in_, reads, writes, eng="dve"):
            if eng == "act":
                act(out, in_, AF.Copy, reads, writes)
            else:
                A(eng, lambda e: e.tensor_copy(out=out, in_=in_), reads, writes)

        P.dma("sp", "d_cs", lambda e: e.dma_start(out=cs[:], in_=cst[:, 0:CW_KEEP]), writes=["cs"])
        cstage = scr[:, 0:CW - CW_KEEP]
        P.dma("sp", "d_cs2", lambda e: e.dma_start(out=cstage, in_=cst[:, CW_KEEP:CW]), writes=["cstage"])
        maskPf = cstage[:, 0:128]
        maskSf = cstage[:, 128:256]
        resetmf = cstage[:, 256:896]
        P.dma("sp", "d_small", lambda e: e.dma_start(out=small[:], in_=smalld), writes=["small"])
        P.dma("sp", "d_pmix", lambda e: e.dma_start(out=pmixf[:], in_=pmixd), writes=["pmixf"])
        for gi in range(4):
            P.dma("sp", f"d_g{gi}",
                  lambda e, gi=gi: e.dma_start(out=gB[:, gi, :], in_=gvec[gi:gi + 1, :].broadcast_to([128, 1024])),
                  writes=[("gB", gi)])
        cp(identb[:], identf, ["cs"], ["identb"])
        cp(maskPb[:], maskPf, ["cstage"], ["maskPb"])
        cp(maskSb[:], maskSf, ["cstage"], ["maskSb"])
        cp(resetmb[:], resetmf, ["cstage"], ["resetmb"])
        cp(pmix[:].rearrange("p g c -> p (g c)"), pmixf[:], ["pmixf"], ["pmix"])
        A("dve", lambda e: e.memset(onesd[:], 1.0 / 128.0), [], ["onesd"])
        A("dve", lambda e: e.memset(carry[:].rearrange("p h v -> p (h v)"), 0.0), [], ["carry"])
        A("dve", lambda e: e.memset(ucarry[:].rearrange("p g t -> p (g t)"), 0.0), [], ["ucarry"])
        tt(lbv[:, 0:4], small[:, 8:12], small[:, 12:16], ALU.subtract, ["small"], ["lbv"])
        act(lbv[:, 0:4], lbv[:, 0:4], AF.Sigmoid, ["lbv"], ["lbv"])
        ts(lbv[:, 4:8], lbv[:, 0:4], -1.0, 1.0, ALU.mult, ALU.add, ["lbv"], ["lbv"])

        def run_tile(ti):
            wstate["tile"] = ti
            has_s = ti == 0
            last = ti == 3
            NT = 640 if has_s else 512
            blocks = [0, 1, 2, 3] + ([4] if has_s else [])
            subs = [(0, 512, "p")] + ([(512, 128, "s")] if has_s else [])
            hTk_p = [("hT", b) for b in range(4)]
            hTk = {"p": hTk_p, "s": [("hT", 4)]}

            def K(name):
                return [(name, "p")] + ([(name, "s")] if has_s else [])

            def xbuf(tj):
                return (xres, "xres") if tj % 2 == 0 else (xalt, "xalt")

            X, xk = xbuf(ti)

            def load_x(tj, b):
                src = xp[tj * 512 + b * 128: tj * 512 + (b + 1) * 128, :] if b < 4 else xs
                Xn, xkn = xbuf(tj)
                P.dma("sp", f"d_x{tj % 2}_{b}", lambda e, b=b, src=src, Xn=Xn: e.dma_start(out=Xn[:, b, :], in_=src),
                      writes=[(xkn, b)])

            if ti == 0:
                for b in blocks:
                    load_x(0, b)
            if ti == 0:
                for _ in range(NRING):
                    w_issue(extra_reads=[(xk, b) for b in blocks])
            if has_s:
                for half in range(2):
                    P.dma("sp", f"d_stg{half}",
                          lambda e, half=half: e.dma_start(out=stg[0:120, half, :], in_=spool[half * 120:(half + 1) * 120, :]),
                          writes=[("stg", half)])

            def half_stats(b, half, X_=None, xk_=None, ssq_=None, tag="ssq"):
                X_ = X if X_ is None else X_
                xk_ = xk if xk_ is None else xk_
                ssq_ = ssq if ssq_ is None else ssq_
                act(junk[:, 0:512], X_[:, b, half * 512:(half + 1) * 512], AF.Square, [(xk_, b)], ["junk", (tag, b, half)],
                    accum_out=ssq_[:, 2 * b + half:2 * b + half + 1])

            def norm_stats(have_partials, blks=None, X_=None, xk_=None, ssq_=None, rst_=None, tag="ssq", rtag="rst"):
                blks = blocks if blks is None else blks
                ssq_ = ssq if ssq_ is None else ssq_
                rst_ = rst if rst_ is None else rst_
                if not have_partials:
                    for b in blks:
                        for half in range(2):
                            half_stats(b, half, X_, xk_, ssq_, tag)
                nbk = len(blks)
                tt(rst_[:, 0:nbk], ssq_[:, 0:2 * nbk:2], ssq_[:, 1:2 * nbk:2], ALU.add,
                   [(tag, b, hf) for b in blks for hf in range(2)], [rtag])
                act(rst_[:, 0:nbk], rst_[:, 0:nbk], AF.Ln, [rtag], [rtag], scale=1.0 / 1024.0, bias=EPS)
                act(rst_[:, 0:nbk], rst_[:, 0:nbk], AF.Exp, [rtag], [rtag], scale=-0.5)

            def norm_apply(gi, blks=None, X_=None, xk_=None, rst_=None, rtag="rst"):
                blks = blocks if blks is None else blks
                X_ = X if X_ is None else X_
                xk_ = xk if xk_ is None else xk_
                rst_ = rst if rst_ is None else rst_
                for b in blks:
                    hb = hn[b % 2]
                    hk = f"hn{b % 2}"
                    stt(hb[:], X_[:, b, :], rst_[:, b:b + 1], gB[:, gi, :], ALU.mult, ALU.mult,
                        [(xk_, b), rtag, ("gB", gi)], [hk])
                    bank = nb()
                    bv = pbs[bank][:].bitcast(BF16)
                    for k in range(8):
                        tp(bv[:, k * 128:(k + 1) * 128], hb[:, k * 128:(k + 1) * 128], identb[:], [hk, "identb"], bank)
                    c0 = b * 128
                    cp(hT[:, :, c0:c0 + 128], bv.rearrange("p (k t) -> p k t", k=8), [PB(bank)], [("hT", b)], eng="act")

            def norm_T(gi, have_partials):
                norm_stats(have_partials)
                norm_apply(gi)

            ck("load")
            if ti == 0:
                norm_T(0, False)
            ck("norm1")

            wv, wk = w_get(0)
            wu = wv[:].rearrange("p (k c) -> p k c", k=8)
            ubanks = []
            for g in range(4):
                bp = nb()
                bs = nb() if has_s else None
                for k in range(8):
                    mm(pbs[bp][:, 0:512], wu[:, k, g * 128:(g + 1) * 128], hT[:, k, 0:512], k == 0, k == 7, [wk] + hTk_p, bp)
                    if has_s:
                        mm(pbs[bs][:, 0:128], wu[:, k, g * 128:(g + 1) * 128], hT[:, k, 512:640], k == 0, k == 7, [wk, ("hT", 4)], bs)
                ubanks.append((bp, bs))
                if g == 3:
                    w_release(0)
                w = 2 << g
                cp(ug[:, 0:16], ucarry[:, g, :], ["ucarry"], ["ug"])
                cp(ug[:, 16:528], pbs[bp][:, 0:512], [PB(bp)], ["ug"], eng="act")
                tt(pa[:, 1:528], ug[:, 1:528], ug[:, 0:527], ALU.add, ["ug"], ["pa"])
                sw = pa
                swk = "pa"
                if w >= 4:
                    tt(pb_[:, 3:528], pa[:, 3:528], pa[:, 1:526], ALU.add, ["pa"], ["pb_"])
                    sw, swk = pb_, "pb_"
                if w >= 8:
                    tt(pa[:, 7:528], pb_[:, 7:528], pb_[:, 3:524], ALU.add, ["pb_"], ["pa"])
                    sw, swk = pa, "pa"
                if w >= 16:
                    tt(pb_[:, 15:528], pa[:, 15:528], pa[:, 7:520], ALU.add, ["pa"], ["pb_"])
                    sw, swk = pb_, "pb_"
                stt(pooled[:, g, 0:512], sw[:, 16:528], 1.0 / w, ug[:, 16:528], ALU.mult, ALU.subtract,
                    [swk, "ug"], [("pooled", g, "p")])
                if ti == 0:
                    tt(tmp16[:], sw[:, 16:32], rcv[:, g * 16:(g + 1) * 16], ALU.mult, [swk, "cs"], ["tmp16"])
                    tt(pooled[:, g, 0:16], tmp16[:], ug[:, 16:32], ALU.subtract, ["tmp16", "ug"], [("pooled", g, "p")])
                cp(ucarry[:, g, :], ug[:, 512:528], ["ug"], ["ucarry"])
                if last:
                    bt = nb()
                    tp(pbs[bt][0:15, 0:128], ug[:, 513:528], identf, ["ug", "cs"], bt)
                    cp(npp_sb[0:15, g * 128:(g + 1) * 128], pbs[bt][0:15, 0:128], [PB(bt)], ["npp_sb"], eng="act")
                if has_s:
                    bt = nb()
                    for half in range(2):
                        tp(pbs[bt][:, half * 120:(half + 1) * 120], stg[0:120, half, g * 128:(g + 1) * 128],
                           identf[0:120, 0:120], [("stg", half), "cs"], bt)
                    cp(ue[:, :, 0:15], pbs[bt][:, 0:240].rearrange("p (i r) -> p i r", r=15), [PB(bt)], ["ue"], eng="act")
                    cp(ue[:, :, 15:23], pbs[bs][:, 0:128].rearrange("p (i t) -> p i t", t=8), [PB(bs)], ["ue"], eng="act")
                    tt(sa[:, :, 1:23], ue[:, :, 1:23], ue[:, :, 0:22], ALU.add, ["ue"], ["sa"])
                    ssw, sswk = sa, "sa"
                    if w >= 4:
                        tt(sb_[:, :, 3:23], sa[:, :, 3:23], sa[:, :, 1:21], ALU.add, ["sa"], ["sb_"])
                        ssw, sswk = sb_, "sb_"
                    if w >= 8:
                        tt(sa[:, :, 7:23], sb_[:, :, 7:23], sb_[:, :, 3:19], ALU.add, ["sb_"], ["sa"])
                        ssw, sswk = sa, "sa"
                    if w >= 16:
                        tt(sb_[:, :, 15:23], sa[:, :, 15:23], sa[:, :, 7:15], ALU.add, ["sa"], ["sb_"])
                        ssw, sswk = sb_, "sb_"
                    stt(pooled[:, g, 512:640].rearrange("p (i t) -> p i t", t=8), ssw[:, :, 15:23], 1.0 / w,
                        ue[:, :, 15:23], ALU.mult, ALU.subtract, [sswk, "ue"], [("pooled", g, "s")])
                    cp(npsb[:, g, :].rearrange("p (i r) -> p i r", r=15), ue[:, :, 8:23], ["ue"], [("npsb", g)])
            if last:
                P.dma("sp", "d_npp", lambda e: e.dma_start(out=npp, in_=npp_sb[0:15, :]), reads=["npp_sb"])
            if has_s:
                for half in range(2):
                    bt = nb()
                    for g in range(4):
                        tp(pbs[bt][0:120, g * 128:(g + 1) * 128], npsb[:, g, half * 120:(half + 1) * 120], identf,
                           [("npsb", g), "cs"], bt)
                    cp(nps_sb[0:120, half, :], pbs[bt][0:120, 0:512], [PB(bt)], [("stg", half)], eng="act")
                    P.dma("sp", f"d_nps{half}",
                          lambda e, half=half: e.dma_start(out=nps[half * 120:(half + 1) * 120, :], in_=nps_sb[0:120, half, :]),
                          reads=[("stg", half)])
            ck("u")
            PIPE = not has_s
            if PIPE:
                zc, rcn = [0], [0]

                def nbZ():
                    b = zc[0] % 4
                    zc[0] += 1
                    return b

                def nbR():
                    b = 6 + rcn[0] % 2
                    rcn[0] += 1
                    return b

                gcn = [0]

                def nbG():
                    b = 4 + gcn[0] % 2
                    gcn[0] += 1
                    return b
            else:
                nbZ = nbR = nbG = nb
            gb2 = [gbf, gbf2]
            QE = [qe, qe2]
            DEC = [dec, dec2]
            HS = {h: {} for h in range(4)}

            def poolmix():
                for g in range(4):
                    for (c0, n, sk) in subs:
                        bk = nbR()
                        mm(pbs[bk][:, 0:n], pmix[:, g, :], pooled[:, g, c0:c0 + n], True, True, ["pmix", ("pooled", g, sk)], bk)
                        act(pool_out[:, g, c0:c0 + n], pbs[bk][:, 0:n], AF.Copy, [PB(bk), "small"], [("pool_out", g, sk)],
                            scale=small[:, g:g + 1])

            def z_group(h, j):
                S_ = HS[h]
                if j == 0:
                    wv, wk = w_get(1 + 2 * h)
                    S_["wh"] = wv[:].rearrange("p (k j c) -> p k j c", k=8, j=4)
                    S_["wk"] = wk
                    S_["bs"] = nbZ() if has_s else None
                    S_["zb"] = []
                wh, wk, bs = S_["wh"], S_["wk"], S_["bs"]
                bp = nbZ()
                for k in range(8):
                    mm(pbs[bp][:, 0:512], wh[:, k, j, :], hT[:, k, 0:512], k == 0, k == 7, [wk] + hTk_p, bp)
                    if has_s:
                        mm(pbs[bs][:, j * 128:(j + 1) * 128], wh[:, k, j, :], hT[:, k, 512:640], k == 0, k == 7,
                           [wk, ("hT", 4)], bs)
                S_["zb"].append(bp)
                if j == 3:
                    w_release(1 + 2 * h)

            def z_evac(h):
                S_ = HS[h]
                zb, bs = S_["zb"], S_["bs"]
                gbh = gb2[h % 2]
                gbk = f"gbf{h % 2}"

                def zsrc(j, sk):
                    return (pbs[zb[j]][:, 0:512], PB(zb[j])) if sk == "p" else (pbs[bs][:, j * 128:(j + 1) * 128], PB(bs))

                for (c0, n, sk) in subs:
                    src, key = zsrc(1, sk)
                    act(t1[:, c0:c0 + n], src, AF.Sigmoid, [key], [("t1", sk)])
                    src, key = zsrc(0, sk)
                    act(qf[:, c0:c0 + n], src, AF.Sigmoid, [key], [("qf", sk)])
                    tt(qf[:, c0:c0 + n], qf[:, c0:c0 + n], src, ALU.mult, [("qf", sk), key], [("qf", sk)])
                    src, key = zsrc(3, sk)
                    act(t4[:, c0:c0 + n], src, AF.Sigmoid, [key], [("t4", sk)])
                    tt(gbh[:, c0:c0 + n], t4[:, c0:c0 + n], src, ALU.mult, [("t4", sk), key], [(gbk, sk)])
                    src, key = zsrc(2, sk)
                    cp(vb[:, c0:c0 + n], src, [key], [("vb", sk)])

            def g_mm(h, jj):
                S_ = HS[h]
                if jj == 0:
                    wgv_, wgk_ = w_get(2 + 2 * h)
                    S_["wgv"] = wgv_[:].rearrange("p (k t c) -> p k t c", k=8, t=4)
                    S_["wgk"] = wgk_
                    S_["gate_ev"] = []
                wgv, wgk_ = S_["wgv"], S_["wgk"]
                j = 2 * h + jj
                bsg = nbG() if has_s else None
                for t_ in range(2):
                    bpg = nbG()
                    for k in range(8):
                        mm(pbs[bpg][:, 0:512], wgv[:, k, 2 * jj + t_, :], hT[:, k, 0:512], k == 0, k == 7, [wgk_] + hTk_p, bpg)
                        if has_s:
                            mm(pbs[bsg][:, t_ * 128:(t_ + 1) * 128], wgv[:, k, 2 * jj + t_, :], hT[:, k, 512:640], k == 0, k == 7,
                               [wgk_, ("hT", 4)], bsg)
                    S_["gate_ev"].append((j, t_, bpg, bsg))
                if jj == 1:
                    w_release(2 + 2 * h)

            def g_evac(h, jj):
                for (j, t_, bpg, bsg) in HS[h]["gate_ev"][2 * jj:2 * jj + 2]:
                    dst = sgA if t_ == 0 else sgB
                    dk = "sgA" if t_ == 0 else "sgB"
                    cp(dst[:, j, 0:512], pbs[bpg][:, 0:512], [PB(bpg)], [(dk, j, "p")], eng="act")
                    if has_s:
                        cp(dst[:, j, 512:640], pbs[bsg][:, t_ * 128:(t_ + 1) * 128], [PB(bsg)], [(dk, j, "s")], eng="act")

            def chain_a(h):
                ts(t1[:, 0:NT], t1[:, 0:NT], lbv[:, 4 + h:5 + h], lbv[:, h:h + 1], ALU.mult, ALU.add, K("t1") + ["lbv"], K("t1"))
                ts(t2[:, 0:NT], t1[:, 0:NT], -1.0, 1.0, ALU.mult, ALU.add, K("t1"), K("t2"))
                act(t1[:, 0:NT], t1[:, 0:NT], AF.Ln, K("t1"), K("t1"))

            def cb_scan(h):
                A("dve", lambda e: e.tensor_tensor_scan(out=t3[:, 0:NT], data0=resetm[:, 0:NT], data1=t1[:, 0:NT],
                                                        initial=0.0, op0=ALU.mult, op1=ALU.add),
                  K("t1") + ["resetmb"], K("t3"))

            def cb_exp(h):
                act(t1[:, 0:NT], t3[:, 0:NT], AF.Exp, K("t3"), K("t1"))
                act(t4[:, 0:NT], t3[:, 0:NT], AF.Exp, K("t3"), K("t4"), scale=-1.0)

            def cb_mul(h):
                qe_ = QE[h % 2]
                qk = f"qe{h % 2}"
                tt(qe_[:, 0:NT], qf[:, 0:NT], t1[:, 0:NT], ALU.mult, K("qf") + K("t1"), [(qk, "p")] + ([(qk, "s")] if has_s else []))
                tt(t2[:, 0:NT], t2[:, 0:NT], t4[:, 0:NT], ALU.mult, K("t2") + K("t4"), K("t2"))

            def cb_keb(h):
                cp(keb[:, 0:NT], t2[:, 0:NT], K("t2"), K("keb"), eng="act")

            def cb_dec(h):
                dec_ = DEC[h % 2]
                dk_ = f"dec{h % 2}"
                cp(dec_[:, 0:8], t1[:, 63:512:64], [("t1", "p")], [(dk_, "p")])
                tt(kdb[:, 0:512].rearrange("p (c t) -> p c t", t=64), t2[:, 0:512].rearrange("p (c t) -> p c t", t=64),
                   dec_[:, 0:8].unsqueeze(2).broadcast_to([128, 8, 64]), ALU.mult, [("t2", "p"), (dk_, "p")], [("kdb", "p")])
                if has_s:
                    cp(dec_[:, 8:24], t1[:, 519:640:8], [("t1", "s")], [(dk_, "s")])
                    tt(kdb[:, 512:640].rearrange("p (c t) -> p c t", t=8), t2[:, 512:640].rearrange("p (c t) -> p c t", t=8),
                       dec_[:, 8:24].unsqueeze(2).broadcast_to([128, 16, 8]), ALU.mult, [("t2", "s"), (dk_, "s")], [("kdb", "s")])

            def chain_b(h):
                cb_scan(h); cb_exp(h); cb_mul(h); cb_keb(h); cb_dec(h)

            def R1(h):
                qe = QE[h % 2]
                qk = f"qe{h % 2}"
                for (src_, srck, dst, dstk) in ((kdb, "kdb", kdT, "kdT"), (vb, "vb", vT, "vT")):
                    bt = nbR()
                    bv = pbs[bt][:].bitcast(BF16)
                    for b in blocks:
                        sk = "p" if b < 4 else "s"
                        tp(bv[:, b * 128:(b + 1) * 128], src_[:, b * 128:(b + 1) * 128], identb[:], [(srck, sk), "identb"], bt)
                    nbk = len(blocks)
                    cp(dst[:, 0:nbk, :], bv[:, 0:nbk * 128].rearrange("p (b k) -> p b k", k=128), [PB(bt)], [dstk], eng="act")
                ba = nbR()
                for b in range(4):
                    mm(pbs[ba][:, b * 128:(b + 1) * 128], keb[:, b * 128:(b + 1) * 128], qe[:, b * 128:(b + 1) * 128],
                       True, True, [("keb", "p"), (qk, "p")], ba)
                tt(Am[:, 0:4, :], pbs[ba][:, 0:512].rearrange("p (b t) -> p b t", b=4),
                   maskPb[:].unsqueeze(1).broadcast_to([128, 4, 128]), ALU.mult, [PB(ba), "maskPb"], [("Am", "p")])
                if has_s:
                    bas = nbR()
                    mm(pbs[bas][:, 0:128], keb[:, 512:640], qe[:, 512:640], True, True, [("keb", "s"), (qk, "s")], bas)
                    tt(Am[:, 4, :], pbs[bas][:, 0:128], maskSb[:], ALU.mult, [PB(bas), "maskSb"], [("Am", "s")])
                    tt(Vm[:, :, :], vT[:, 4, :].unsqueeze(1).broadcast_to([128, 16, 128]),
                       seqm.unsqueeze(2).broadcast_to([128, 16, 128]), ALU.mult, ["vT", "cs"], ["Vm"])

            def sample_prefetch(h):
                P.dma("sp", "d_S0", lambda e, h=h: e.dma_start(out=S0[:, :, :], in_=shg[:, h, :, :].rearrange("i k v -> k i v")),
                      writes=["S0"])

            def sample_bf(h):
                cp(S0bf[:, :, :].rearrange("p i v -> p (i v)"), S0[:, :, :].rearrange("p i v -> p (i v)"), ["S0"], ["S0bf"], eng="act")

            def sample_state_update(h):
                dec = DEC[h % 2]
                dk_ = f"dec{h % 2}"
                tt(S0[:, :, :], S0[:, :, :], dec[:, 8:24].unsqueeze(2).broadcast_to([128, 16, 128]), ALU.mult,
                   ["S0", (dk_, "s")], ["S0"])
                for q4 in range(4):
                    bd = nbR()
                    mm(pbs[bd][:, 0:512], kdT[:, 4, :], Vm[:, 4 * q4:4 * q4 + 4, :].rearrange("p i v -> p (i v)"), True, True,
                       ["kdT", "Vm"], bd)
                    tt(S0[:, 4 * q4:4 * q4 + 4, :].rearrange("p i v -> p (i v)"),
                       S0[:, 4 * q4:4 * q4 + 4, :].rearrange("p i v -> p (i v)"), pbs[bd][:, 0:512], ALU.add,
                       ["S0", PB(bd)], ["S0"])
                P.dma("sp", "d_nhs", lambda e, h=h: e.dma_start(out=nhs[:, h, :, :].rearrange("i k v -> k i v"), in_=S0[:, :, :]),
                      reads=["S0"])

            def R2_mm(h):
                dsb = [nbR(), nbR()]
                HS[h]["dsb"] = dsb
                for c in range(8):
                    blk, half = c // 2, c % 2
                    bk = dsb[half]
                    mm(pbs[bk][:, blk * 128:(blk + 1) * 128], kdT[half * 64:(half + 1) * 64, blk, :],
                       vT[half * 64:(half + 1) * 64, blk, :], True, True, ["kdT", "vT"], bk)
                cp(Sall[:, 0, :], carry[:, h, :], ["carry"], [("Sall", 0)])

            def R2_steps(h, c0, c1):
                dec_ = DEC[h % 2]
                dk_ = f"dec{h % 2}"
                dsb = HS[h]["dsb"]
                for c in range(c0, c1):
                    blk, half = c // 2, c % 2
                    bk = dsb[half]
                    stt(Sall[:, c + 1, :], Sall[:, c, :], dec_[:, c:c + 1], pbs[bk][:, blk * 128:(blk + 1) * 128],
                        ALU.mult, ALU.add, [("Sall", c), (dk_, "p"), PB(bk)], [("Sall", c + 1)])

            def R2_fin(h):
                cp(Sbf[:, :, :].rearrange("p c v -> p (c v)"), Sall[:, 0:8, :].rearrange("p c v -> p (c v)"),
                   [("Sall", c) for c in range(8)], ["Sbf"], eng="act")
                cp(carry[:, h, :], Sall[:, 8, :], [("Sall", 8)], ["carry"])
                if last:
                    P.dma("sp", f"d_nhp{h}", lambda e, h=h: e.dma_start(out=nhp[h], in_=Sall[:, 8, :]), reads=[("Sall", 8)])

            def R2(h):
                R2_mm(h); R2_steps(h, 0, 8); R2_fin(h)

            def R3(h):
                qe = QE[h % 2]
                qk = f"qe{h % 2}"
                S_ = HS[h]
                bo = nbR()
                for b in range(4):
                    mm(pbs[bo][:, b * 128:(b + 1) * 128], vT[:, b, :], Am[:, b, :], True, False, ["vT", ("Am", "p")], bo)
                    for half in range(2):
                        c = 2 * b + half
                        mm(pbs[bo][:, c * 64:(c + 1) * 64], Sbf[:, c, :], qe[:, c * 64:(c + 1) * 64], False, half == 1,
                           ["Sbf", (qk, "p")], bo)
                bos = None
                if has_s:
                    sample_bf(h)
                    bos = nbR()
                    mm(pbs[bos][:, 0:128], vT[:, 4, :], Am[:, 4, :], True, False, ["vT", ("Am", "s")], bos)
                    for i in range(16):
                        mm(pbs[bos][:, i * 8:(i + 1) * 8], S0bf[:, i, :], qe[:, 512 + i * 8:512 + (i + 1) * 8], False, i == 15,
                           ["S0bf", (qk, "s")], bos)
                S_["obanks"] = [(0, 512, "p", bo)] + ([(512, 128, "s", bos)] if has_s else [])
                for (c0, n, sk, bk) in S_["obanks"]:
                    act(osq[:, c0:c0 + n], pbs[bk][:, 0:n], AF.Square, [PB(bk)], [("osq", sk)])

            def R4(h):
                gbh = gb2[h % 2]
                gbk = f"gbf{h % 2}"
                for (c0, n, sk, bk) in HS[h]["obanks"]:
                    bn = nbR()
                    mm(pbs[bn][:, 0:n], onesd[:], osq[:, c0:c0 + n], True, True, ["onesd", ("osq", sk)], bn)
                    act(s1[:, c0:c0 + n], pbs[bn][:, 0:n], AF.Ln, [PB(bn)], [("s1", sk)], bias=EPS)
                    act(s1[:, c0:c0 + n], s1[:, c0:c0 + n], AF.Exp, [("s1", sk)], [("s1", sk)], scale=-0.5)
                    stt(s2[:, c0:c0 + n], pbs[bk][:, 0:n], small[:, 4 + h:5 + h], s1[:, c0:c0 + n], ALU.mult, ALU.mult,
                        [PB(bk), "small", ("s1", sk)], [("s2", sk)])
                tt(o_fin[:, h, 0:NT], s2[:, 0:NT], gbh[:, 0:NT], ALU.mult, K("s2") + [(gbk, "p"), (gbk, "s")],
                   [("o_fin", h, s_) for s_ in ("p", "s")])

            if PIPE:
                for j in range(4):
                    z_group(0, j)
            poolmix()
            ck("poolmix")
            P.barrier()
            ck("bar")
            if PIPE:
                for h in range(4):
                    p = h - 1
                    if h > 0:
                        R1(p)
                    z_evac(h)
                    g_mm(h, 0)
                    if h > 0:
                        R2_mm(p)
                    chain_a(h)
                    if h > 0:
                        R2_steps(p, 0, 4)
                    g_evac(h, 0)
                    if h < 3:
                        for j in range(4):
                            z_group(h + 1, j)
                    cb_scan(h)
                    if h > 0:
                        R2_steps(p, 4, 7)
                    cb_exp(h)
                    cb_mul(h)
                    cb_dec(h)
                    cb_keb(h)
                    g_mm(h, 1)
                    if h > 0:
                        R2_steps(p, 7, 8)
                        R2_fin(p)
                        R3(p)
                    g_evac(h, 1)
                    if h > 0:
                        R4(p)
                R1(3); R2(3); R3(3); R4(3)
            else:
                for h in range(4):
                    sample_prefetch(h)
                    for j in range(4):
                        z_group(h, j)
                    z_evac(h)
                    g_mm(h, 0)
                    g_mm(h, 1)
                    chain_a(h)
                    chain_b(h)
                    g_evac(h, 0)
                    g_evac(h, 1)
                    R1(h); R2(h); R3(h); R4(h)
                    sample_state_update(h)

            ck("hgrn")
            for jh in range(2):
                wy, wyk = w_get(9 + jh)
                wyv = wy[:].rearrange("p (k t c) -> p k t c", k=4, t=2)
                for jj in range(4):
                    j = jh * 4 + jj
                    bs = nb() if has_s else None
                    b_ya, b_yb = nb(), nb()
                    for t_, bk, srcb, srck in ((0, b_ya, pool_out, "pool_out"), (1, b_yb, o_fin, "o_fin")):
                        for kk in range(4):
                            for (c0, n, sk) in subs:
                                o = pbs[bk][:, 0:512] if sk == "p" else pbs[bs][:, t_ * 128:(t_ + 1) * 128]
                                mm(o, wyv[:, kk, t_, jj * 128:(jj + 1) * 128], srcb[:, kk, c0:c0 + n], kk == 0, kk == 3,
                                   [wyk, (srck, kk, sk)], bk if sk == "p" else bs)
                    if jj == 3:
                        w_release(9 + jh)
                    for (c0, n, sk) in subs:
                        oa = pbs[b_ya][:, 0:512] if sk == "p" else pbs[bs][:, 0:128]
                        ob = pbs[b_yb][:, 0:512] if sk == "p" else pbs[bs][:, 128:256]
                        ka = PB(b_ya) if sk == "p" else PB(bs)
                        kb_ = PB(b_yb) if sk == "p" else PB(bs)
                        act(s1[:, c0:c0 + n], sgA[:, j, c0:c0 + n], AF.Sigmoid, [("sgA", j, sk)], [("s1", sk)])
                        act(s2[:, c0:c0 + n], sgB[:, j, c0:c0 + n], AF.Sigmoid, [("sgB", j, sk)], [("s2", sk)])
                        tt(s1[:, c0:c0 + n], s1[:, c0:c0 + n], oa, ALU.mult, [("s1", sk), ka], [("s1", sk)])
                        tt(s2[:, c0:c0 + n], s2[:, c0:c0 + n], ob, ALU.mult, [("s2", sk), kb_], [("s2", sk)])
                    tt(merged[:, j, 0:NT], s1[:, 0:NT], s2[:, 0:NT], ALU.add, K("s1") + K("s2"),
                       [("merged", j, s_) for s_ in ("p", "s")])

            ck("merge")
            for half in range(2):
                wo, wok = w_get(11 + half)
                wov = wo[:].rearrange("p (k c) -> p k c", k=8)
                bks = {b: nb() for b in blocks}
                for k in range(8):
                    for b in blocks:
                        sk = "p" if b < 4 else "s"
                        mm(pbs[bks[b]][:, 0:512], merged[:, k, b * 128:(b + 1) * 128], wov[:, k, :], k == 0, k == 7,
                           [wok, ("merged", k, sk)], bks[b])
                w_release(11 + half)
                for b in blocks:
                    tt(X[:, b, half * 512:(half + 1) * 512], X[:, b, half * 512:(half + 1) * 512], pbs[bks[b]][:, 0:512],
                       ALU.add, [(xk, b), PB(bks[b])], [(xk, b)])
                    half_stats(b, half)

            ck("wout")
            norm_T(1, True)
            P.barrier()
            if ti + 1 < ntiles:
                for b in range(4):
                    load_x(ti + 1, b)
            for b in blocks:
                src = ppd[ti * 512 + b * 128: ti * 512 + (b + 1) * 128, :] if b < 4 else psd
                P.dma("sp", f"d_p{b}", lambda e, src=src, b=b: e.dma_start(out=pstage[:, b, :], in_=src), writes=[("pstage", b)])
            for jb in range(11):
                wf, wfk = w_get(13 + jb)
                wfv = wf[:].rearrange("p (k j c) -> p k j c", k=8, j=2)
                for cc in range(2):
                    f = 2 * jb + cc
                    bs = nb() if has_s else None
                    b_g, b_u = nb(), nb()
                    for jx, bk in ((0, b_g), (1, b_u)):
                        for k in range(8):
                            for (c0, n, sk) in subs:
                                o = pbs[bk][:, 0:512] if sk == "p" else pbs[bs][:, jx * 128:(jx + 1) * 128]
                                mm(o, wfv[:, k, jx, cc * 128:(cc + 1) * 128], hT[:, k, c0:c0 + n], k == 0, k == 7,
                                   [wfk] + hTk[sk], bk if sk == "p" else bs)
                    if cc == 1:
                        w_release(13 + jb)
                    for (c0, n, sk) in subs:
                        og = pbs[b_g][:, 0:512] if sk == "p" else pbs[bs][:, 0:128]
                        ou = pbs[b_u][:, 0:512] if sk == "p" else pbs[bs][:, 128:256]
                        kg = PB(b_g) if sk == "p" else PB(bs)
                        ku = PB(b_u) if sk == "p" else PB(bs)
                        act(ftmp[:, c0:c0 + n], og, AF.Sigmoid, [kg], [("ftmp", sk)])
                        tt(ftmp[:, c0:c0 + n], ftmp[:, c0:c0 + n], og, ALU.mult, [("ftmp", sk), kg], [("ftmp", sk)])
                        tt(hidden[:, f, c0:c0 + n], ftmp[:, c0:c0 + n], ou, ALU.mult, [("ftmp", sk), ku], [("hidden", f, sk)])
            for b in blocks:
                cp(pbf[:], pstage[:, b, :], [("pstage", b)], ["pbf"], eng="act")
                bt = nb()
                bv = pbs[bt][:].bitcast(BF16)
                for k in range(2):
                    tp(bv[:, k * 128:(k + 1) * 128], pbf[:, k * 128:(k + 1) * 128], identb[:], ["pbf", "identb"], bt)
                cp(pT[:, :, b * 128:(b + 1) * 128], bv[:, 0:256].rearrange("p (k t) -> p k t", k=2), [PB(bt)], [("pT", b)])
            for half in range(2):
                bks = {b: nb() for b in blocks}
                for kb in range(3):
                    wd, wdk = w_get(24 + half * 3 + kb)
                    wdv = wd[:].rearrange("p (k c) -> p k c", k=8)
                    nk = 8 if kb < 2 else 6
                    for kk in range(nk):
                        f = kb * 8 + kk
                        for b in blocks:
                            sk = "p" if b < 4 else "s"
                            mm(pbs[bks[b]][:, 0:512], hidden[:, f, b * 128:(b + 1) * 128], wdv[:, kk, :], f == 0, f == 21,
                               [wdk, ("hidden", f, sk)], bks[b])
                    w_release(24 + half * 3 + kb)
                for b in blocks:
                    tt(X[:, b, half * 512:(half + 1) * 512], X[:, b, half * 512:(half + 1) * 512], pbs[bks[b]][:, 0:512],
                       ALU.add, [(xk, b), PB(bks[b])], [(xk, b)])
                    half_stats(b, half)

            ck("ffn")
            norm_T(2, True)
            if ti + 1 < ntiles:
                Xn, xkn = xbuf(ti + 1)
                norm_stats(False, blks=[0, 1, 2, 3], X_=Xn, xk_=xkn, ssq_=ssq1, rst_=rst1, tag="ssq1", rtag="rst1")
            wg0, wg0k = w_get(30)
            wpp_, wppk = w_get(31)
            wg1, wg1k = w_get(32)
            wppv = wpp_[:, 0:2048].rearrange("p (k c) -> p k c", k=2)
            for half in range(2):
                wg_, wgk = (wg0, wg0k) if half == 0 else (wg1, wg1k)
                wgv = wg_[:].rearrange("p (k c) -> p k c", k=8)
                bks = {b: nb() for b in blocks}
                for k in range(8):
                    for b in blocks:
                        mm(pbs[bks[b]][:, 0:512], hT[:, k, b * 128:(b + 1) * 128], wgv[:, k, :], k == 0, k == 7,
                           [wgk, ("hT", b)], bks[b])
                if half == 0:
                    w_release(30)
                for b in blocks:
                    sg = sgt[b % 2]
                    sgk = f"sgt{b % 2}"
                    act(sg[:], pbs[bks[b]][:, 0:512], AF.Sigmoid, [PB(bks[b])], [sgk])
                    be = nb()
                    for k in range(2):
                        mm(pbs[be][:, 0:512], pT[:, k, b * 128:(b + 1) * 128], wppv[:, k, half * 512:(half + 1) * 512], k == 0, k == 1,
                           [wppk, ("pT", b)], be)
                    tt(sg[:], sg[:], pbs[be][:, 0:512], ALU.mult, [sgk, PB(be)], [sgk])
                    tt(X[:, b, half * 512:(half + 1) * 512], X[:, b, half * 512:(half + 1) * 512], sg[:], ALU.add,
                       [(xk, b), sgk], [(xk, b)])
                for b in blocks:
                    half_stats(b, half)
                if half == 1:
                    w_release(31); w_release(32)

            ck("ple")
            if ti + 1 < ntiles:
                Xn, xkn = xbuf(ti + 1)
                norm_apply(0, blks=[0, 1, 2, 3], X_=Xn, xk_=xkn, rst_=rst1, rtag="rst1")
            P.barrier()
            norm_stats(True)
            for b in blocks:
                dst = yp[ti * 512 + b * 128: ti * 512 + (b + 1) * 128, :] if b < 4 else ys
                for half in range(2):
                    q_ = (2 * b + half) % 4
                    yst = ystage[q_]
                    ysk = f"ystage{q_}"
                    hs_ = slice(half * 512, (half + 1) * 512)
                    stt(yst[:], X[:, b, hs_], rst[:, b:b + 1], gB[:, 3, hs_], ALU.mult, ALU.mult,
                        [(xk, b), "rst", ("gB", 3)], [ysk])
                    P.dma("sp", f"d_y{q_}", lambda e, yst=yst, dst=dst, hs_=hs_: e.dma_start(out=dst[:, hs_], in_=yst[:]),
                          reads=[ysk])

        P.barrier()
        try:
            ck("setup")
            for ti in range(ntiles):
                run_tile(ti)
        except _Stop:
            pass
        P.finish("sp")
        P.emit()
    return nc


_PROG = {}


def _prep_inputs(inp):
    f = lambda a: np.ascontiguousarray(np.asarray(a, dtype=np.float32))
    w_in = f(inp["w_in"][0])
    wallv = build_wall(w_in, f(inp["w_pool_up"][0]), f(inp["w_hgrn_up"][0]), f(inp["w_out"][0]),
                       f(inp["w_ffn_gate"][0]), f(inp["w_ffn_up"][0]), f(inp["w_ffn_down"][0]),
                       f(inp["w_ple_gate"][0]), f(inp["w_ple_proj"][0]))
    gvec = np.ascontiguousarray(np.stack([f(inp["g_mix"][0]), f(inp["g_ffn"][0]), f(inp["g_ple"][0]), f(inp["g_final"])], 0))
    small = np.zeros((128, 16), np.float32)
    small[:, 0:4] = f(inp["pool_scale"][0]).reshape(4, 128).T
    small[:, 4:8] = f(inp["hgrn_norm"][0]).reshape(4, 128).T
    small[:, 8:12] = f(inp["hgrn_lb"][0]).reshape(4, 128).T
    small[:, 12:16] = f(inp["hgrn_lb"][1]).reshape(4, 128).T
    pmix = np.ascontiguousarray(f(inp["w_pool_mix"][0]).transpose(1, 0, 2)).reshape(128, 512)
    xp = f(inp["x_prompt"]); xsm = f(inp["x_sample"])
    ppr = f(inp["p_prompt"][0]); psm = f(inp["p_sample"][0])
    spl = f(inp["state_pool"][0]); shg = f(inp["state_hgrn"][0])
    maps = []
    for c in range(NCORES):
        maps.append({
            "xp": xp[c], "xs": xsm[16 * c:16 * c + 16].reshape(128, 1024),
            "pp": ppr[c], "ps": psm[16 * c:16 * c + 16].reshape(128, 256),
            "spool": spl[16 * c:16 * c + 16].reshape(240, 512),
            "shg": shg[16 * c:16 * c + 16],
            "wall": wallv, "cst": CONST_ARR, "gvec": gvec, "small": small, "pmix": pmix,
        })
    return maps


def kernel(**inputs):
    if "nc" not in _PROG:
        _PROG["nc"] = build_program()
    nc = _PROG["nc"]
    maps = _prep_inputs(inputs)
    res = run_bass_kernel_spmd(nc, maps, core_ids=list(range(NCORES)))
    R = res.results
    y_p = np.stack([R[c]["yp"] for c in range(NCORES)], 0).astype(np.float32)
    y_s = np.concatenate([R[c]["ys"].reshape(16, 8, 1024) for c in range(NCORES)], 0).astype(np.float32)
    npp = np.stack([R[c]["npp"] for c in range(NCORES)], 0)[None].astype(np.float32)
    nhp = np.stack([R[c]["nhp"] for c in range(NCORES)], 0)[None].astype(np.float32)
    nps = np.concatenate([R[c]["nps"].reshape(16, 15, 512) for c in range(NCORES)], 0)[None].astype(np.float32)
    nhs = np.concatenate([R[c]["nhs"] for c in range(NCORES)], 0)[None].astype(np.float32)
    return (y_p, y_s, npp, nhp, nps, nhs)
```

```python
import numpy as np
from contextlib import ExitStack
import concourse.bass as bass
import concourse.mybir as mybir
from concourse.bass_utils import run_bass_kernel_spmd

F32 = mybir.dt.float32
BF16 = mybir.dt.bfloat16
AF = mybir.ActivationFunctionType
ALU = mybir.AluOpType

ENGS = ("pe", "act", "dve", "pool", "sp")
EPS = 1e-6
NCORES = 8
NRING = 5
NTILES = 4
import os as _os
SAME_DIST = int(_os.environ.get("SAME_DIST", str(1 << 30)))


class Prog:
    def __init__(self, nc, stack, same_engine_wait=True):
        self.nc = nc
        self.stack = stack
        self.streams = {e: [] for e in ENGS}
        self.count = {e: 0 for e in ENGS}
        self.known = {e: {} for e in ENGS}
        self.sems = {}
        self.dma_count = {}
        self.lastw = {}
        self.readers = {}
        self.same_engine_wait = same_engine_wait

    def _collect(self, eng, reads, writes):
        deps = []
        for k in reads:
            ev = self.lastw.get(k)
            if ev is not None:
                deps.append(ev)
        for k in writes:
            ev = self.lastw.get(k)
            if ev is not None:
                deps.append(ev)
            rd = self.readers.get(k)
            if rd:
                deps.extend(rd.values())
        kn = self.known[eng]
        need = {}
        for (s, v, vc) in deps:
            if s == eng and (eng == "pe" or not self.same_engine_wait or self.count[eng] - v >= SAME_DIST):
                continue
            if kn.get(s, 0) >= v:
                continue
            if need.get(s, 0) < v:
                need[s] = v
            for s2, v2 in vc.items():
                if s2 == eng:
                    continue
                if kn.get(s2, 0) < v2:
                    kn[s2] = v2
        waits = []
        for s, v in need.items():
            waits.append((s, v))
            if kn.get(s, 0) < v:
                kn[s] = v
        return waits

    def _record(self, ev, reads, writes):
        s = ev[0]
        for k in reads:
            self.readers.setdefault(k, {})[s] = ev
        for k in writes:
            self.lastw[k] = ev
            self.readers[k] = {}

    def op(self, eng, fn, reads=(), writes=()):
        waits = self._collect(eng, reads, writes)
        self.count[eng] += 1
        n = self.count[eng]
        vc = dict(self.known[eng])
        vc[eng] = n
        ev = (eng, n, vc)
        self.streams[eng].append((waits, fn, (eng, 1)))
        self._record(ev, reads, writes)
        return ev

    def dma(self, q, semname, fn, reads=(), writes=()):
        waits = self._collect(q, reads, writes)
        self.dma_count[semname] = self.dma_count.get(semname, 0) + 1
        v = 16 * self.dma_count[semname]
        vc = dict(self.known[q])
        vc[semname] = v
        ev = (semname, v, vc)
        self.streams[q].append((waits, fn, (semname, 16)))
        self._record(ev, reads, writes)
        return ev

    def barrier(self, engines=("act", "dve", "sp")):
        for e in engines:
            kn = self.known[e]
            waits = []
            for e2 in ("pe", "act", "dve"):
                c = self.count[e2]
                if c and kn.get(e2, 0) < c:
                    waits.append((e2, c))
                    kn[e2] = c
            for s, c in self.dma_count.items():
                if s.startswith("ring") or s.startswith("d_y") or s.startswith("d_x"):
                    continue
                if kn.get(s, 0) < 16 * c:
                    waits.append((s, 16 * c))
                    kn[s] = 16 * c
            if waits:
                self.streams[e].append((waits, None, None))

    def finish(self, eng="sp"):
        kn = self.known[eng]
        waits = []
        for s, c in self.dma_count.items():
            if kn.get(s, 0) < 16 * c:
                waits.append((s, 16 * c))
                kn[s] = 16 * c
        for e in ("pe", "act", "dve"):
            if self.count[e] and kn.get(e, 0) < self.count[e]:
                waits.append((e, self.count[e]))
        self.streams[eng].append((waits, None, None))

    def emit(self):
        nc = self.nc
        for s in list(ENGS) + list(self.dma_count):
            if s not in self.sems:
                self.sems[s] = self.stack.enter_context(nc.semaphore(s))
        with nc.Block() as block:
            def run(engine, items):
                for waits, fn, inc in items:
                    for s, v in waits:
                        engine.wait_ge(self.sems[s], v)
                    if fn is not None:
                        ins = fn(engine)
                        ins.then_inc(self.sems[inc[0]], inc[1])

            @block.tensor
            def _(eng):
                run(eng, self.streams["pe"])

            @block.scalar
            def _(eng):
                run(eng, self.streams["act"])

            @block.vector
            def _(eng):
                run(eng, self.streams["dve"])

            @block.gpsimd
            def _(eng):
                run(eng, self.streams["pool"])

            @block.sync
            def _(eng):
                run(eng, self.streams["sp"])


NBLK = 33


def _kc(W, nk):
    C = W.shape[1]
    return np.ascontiguousarray(W.reshape(nk, 128, C).transpose(1, 0, 2)).reshape(128, nk * C)


def _pad(a):
    out = np.zeros((128, 4096), np.float32)
    out[:, : a.shape[1]] = a
    return out


def build_wall(w_in, w_pool_up, w_hgrn_up, w_out, w_g, w_u, w_d, w_pg, w_pp):
    blocks = []
    blocks.append(_kc(w_in[:, 0:512], 8))
    zz = w_in[:, 512:2560].reshape(8, 128, 4, 4, 128)
    ga = w_in[:, 2560:3584].reshape(8, 128, 8, 128)
    gb = w_in[:, 3584:4608].reshape(8, 128, 8, 128)
    for h in range(4):
        blocks.append(np.ascontiguousarray(zz[:, :, :, h, :].transpose(1, 0, 2, 3)).reshape(128, 4096))
        gg = np.stack([ga[:, :, 2 * h], gb[:, :, 2 * h], ga[:, :, 2 * h + 1], gb[:, :, 2 * h + 1]], axis=2)
        blocks.append(np.ascontiguousarray(gg.transpose(1, 0, 2, 3)).reshape(128, 4096))
    for half in range(2):
        yy = np.stack([w_pool_up[:, half * 512:(half + 1) * 512].reshape(4, 128, 512),
                       w_hgrn_up[:, half * 512:(half + 1) * 512].reshape(4, 128, 512)], axis=2)
        blocks.append(np.ascontiguousarray(yy.transpose(1, 0, 2, 3)).reshape(128, 4096))
    blocks.append(_kc(w_out[:, 0:512], 8))
    blocks.append(_kc(w_out[:, 512:1024], 8))
    for jb in range(11):
        gu = np.stack([w_g[:, jb * 256:(jb + 1) * 256], w_u[:, jb * 256:(jb + 1) * 256]], axis=1)
        blocks.append(_kc(gu.reshape(1024, 512), 8))
    for half in range(2):
        for kb in range(3):
            nk = 8 if kb < 2 else 6
            blocks.append(_pad(_kc(w_d[kb * 1024: kb * 1024 + nk * 128, half * 512:(half + 1) * 512], nk)))
    blocks.append(_kc(w_pg[:, 0:512], 8))
    blocks.append(_pad(_kc(w_pp, 2)))
    blocks.append(_kc(w_pg[:, 512:1024], 8))
    assert len(blocks) == NBLK
    return np.ascontiguousarray(np.stack(blocks, axis=0).astype(np.float32))


def build_consts():
    c = {}
    c["ident"] = np.eye(128, dtype=np.float32)
    s = np.arange(128)[:, None]
    t = np.arange(128)[None, :]
    c["maskP"] = ((s // 64 == t // 64) & (s <= t)).astype(np.float32)
    c["maskS"] = ((s // 8 == t // 8) & (s <= t)).astype(np.float32)
    c["seqm"] = (s // 8 == np.arange(16)[None, :]).astype(np.float32)
    r = np.ones(640, np.float32)
    r[0:512:64] = 0.0
    r[512:640:8] = 0.0
    c["resetm"] = np.broadcast_to(r, (128, 640)).copy()
    rc = np.zeros((4, 16), np.float32)
    for g, w in enumerate((2, 4, 8, 16)):
        rc[g] = 1.0 / np.minimum(np.arange(16) + 1, w)
    c["rc"] = np.broadcast_to(rc.reshape(1, 64), (128, 64)).copy()
    order = ["ident", "seqm", "rc", "maskP", "maskS", "resetm"]
    offs = {}
    o = 0
    for k in order:
        offs[k] = (o, c[k].shape[1])
        o += c[k].shape[1]
    return np.ascontiguousarray(np.concatenate([c[k] for k in order], axis=1)), offs


CONST_ARR, COFF = build_consts()
CW = CONST_ARR.shape[1]
CW_KEEP = COFF["maskP"][0]


class _Stop(Exception):
    pass


STOP = None


def ck(name):
    if STOP == name:
        raise _Stop()


def build_program(ntiles=NTILES):
    nc = bass.Bass("TRN2", target_bir_lowering=False)

    def din(name, shape):
        return nc.dram_tensor(name, shape, F32, kind="ExternalInput").ap()

    def dout(name, shape):
        return nc.dram_tensor(name, shape, F32, kind="ExternalOutput").ap()

    xp = din("xp", [2048, 1024])
    xs = din("xs", [128, 1024])
    ppd = din("pp", [2048, 256])
    psd = din("ps", [128, 256])
    spool = din("spool", [240, 512])
    shg = din("shg", [16, 4, 128, 128])
    wall = din("wall", [NBLK, 128, 4096])
    cst = din("cst", [128, CW])
    gvec = din("gvec", [4, 1024])
    smalld = din("small", [128, 16])
    pmixd = din("pmix", [128, 512])
    yp = dout("yp", [2048, 1024])
    ys = dout("ys", [128, 1024])
    npp = dout("npp", [15, 512])
    nhp = dout("nhp", [4, 128, 128])
    nps = dout("nps", [240, 512])
    nhs = dout("nhs", [16, 4, 128, 128])

    with ExitStack() as st:
        P = Prog(nc, st)

        def sb(name, shape, dt):
            return st.enter_context(nc.sbuf_tensor("sb_" + name, shape, dt))

        xres = sb("xres", [128, 5, 1024], F32)
        hT = sb("hT", [128, 8, 640], BF16)
        hn = [sb(f"hn{i}", [128, 1024], BF16) for i in range(2)]
        junk = sb("junk", [128, 512], BF16)
        ystage = [sb(f"ystage{i}", [128, 512], F32) for i in range(4)]
        resetmb = sb("resetmb", [128, 640], BF16)
        ring = [sb(f"ring{i}", [128, 4096], BF16) for i in range(NRING)]
        gB = sb("gB", [128, 4, 1024], F32)
        cs = sb("cs", [128, CW_KEEP], F32)
        identb = sb("identb", [128, 128], BF16)
        onesd = sb("onesd", [128, 128], BF16)
        maskPb = sb("maskPb", [128, 128], BF16)
        maskSb = sb("maskSb", [128, 128], BF16)
        pmixf = sb("pmixf", [128, 512], F32)
        pmix = sb("pmix", [128, 4, 128], BF16)
        small = sb("small", [128, 16], F32)
        lbv = sb("lbv", [128, 8], F32)
        carry = sb("carry", [128, 4, 128], F32)
        ucarry = sb("ucarry", [128, 4, 16], F32)
        ssq = sb("ssq", [128, 16], F32)
        ssq1 = sb("ssq1", [128, 8], F32)
        rst1 = sb("rst1", [128, 4], F32)
        rst = sb("rst", [128, 8], F32)
        identf = cs[:, COFF["ident"][0]:COFF["ident"][0] + 128]
        seqm = cs[:, COFF["seqm"][0]:COFF["seqm"][0] + 16]
        rcv = cs[:, COFF["rc"][0]:COFF["rc"][0] + 64]
        resetm = resetmb

        MIX_BYTES = 0
        scr_plan = {}

        def plan(phase, name, nelem, dt):
            nonlocal MIX_BYTES
            nbytes = nelem * (4 if dt == F32 else 2)
            nbytes = (nbytes + 31) // 32 * 32
            off = scr_plan.setdefault(("off", phase), 0)
            scr_plan[name] = (off, nelem, dt)
            scr_plan[("off", phase)] = off + nbytes

        U_NAMES = ["ug", "ue", "pa", "pb_", "sa", "sb_", "tmp16", "pooled", "npsb", "stg", "npp_sb"]
        for name, n, dt in [
            ("ug", 528, F32), ("ue", 16 * 24, F32), ("pa", 528, F32), ("pb_", 528, F32),
            ("sa", 16 * 24, F32), ("sb_", 16 * 24, F32), ("tmp16", 16, F32),
            ("pooled", 4 * 640, BF16), ("npsb", 4 * 240, F32),
            ("stg", 2 * 512, F32), ("npp_sb", 512, F32), ("pool_out", 4 * 640, BF16),
            ("qf", 640, F32), ("t1", 640, F32), ("t2", 640, F32), ("t3", 640, F32), ("t4", 640, F32),
            ("qe", 640, BF16), ("qe2", 640, BF16), ("keb", 640, BF16), ("kdb", 640, BF16), ("vb", 640, BF16), ("gbf", 640, BF16), ("gbf2", 640, BF16),
            ("kdT", 5 * 128, BF16), ("vT", 5 * 128, BF16), ("Sall", 9 * 128, F32), ("Sbf", 8 * 128, BF16),
            ("dec", 24, F32), ("dec2", 24, F32), ("Am", 5 * 128, BF16), ("osq", 640, BF16), ("o_fin", 4 * 640, BF16),
            ("S0", 16 * 128, F32), ("S0bf", 16 * 128, BF16), ("Vm", 16 * 128, BF16),
            ("merged", 8 * 640, BF16), ("s1", 640, F32), ("s2", 640, F32),
        ]:
            plan("mix", name, n, dt)
        for name, n, dt in [
            ("hidden", 22 * 640, BF16), ("ftmp", 640, F32), ("sgt0", 512, F32), ("sgt1", 512, F32),
            ("pT", 2 * 640, BF16), ("pstage", 5 * 256, F32), ("pbf", 256, BF16),
        ]:
            plan("ffn", name, n, dt)
        u_end = scr_plan["pool_out"][0]
        assert 2 * 8 * 640 * 2 <= u_end, u_end
        scr_plan["sgA"] = (0, 8 * 640, BF16)
        scr_plan["sgB"] = (8 * 640 * 2, 8 * 640, BF16)
        scr_bytes = max(scr_plan[("off", "mix")], scr_plan[("off", "ffn")])
        scr = sb("scr", [128, scr_bytes // 4], F32)

        def sv(name):
            off, n, dt = scr_plan[name]
            if dt == F32:
                return scr[:, off // 4: off // 4 + n]
            return scr[:, off // 4: off // 4 + n // 2].bitcast(BF16)

        ug = sv("ug"); ue = sv("ue").rearrange("p (i r) -> p i r", r=24)
        pa = sv("pa"); pb_ = sv("pb_")
        sa = sv("sa").rearrange("p (i r) -> p i r", r=24); sb_ = sv("sb_").rearrange("p (i r) -> p i r", r=24)
        tmp16 = sv("tmp16")
        pooled = sv("pooled").rearrange("p (g t) -> p g t", g=4)
        pool_out = sv("pool_out").rearrange("p (g t) -> p g t", g=4)
        npsb = sv("npsb").rearrange("p (g r) -> p g r", g=4)
        stg = sv("stg").rearrange("p (h c) -> p h c", h=2)
        nps_sb = stg
        npp_sb = sv("npp_sb")
        qf = sv("qf"); t1 = sv("t1"); t2 = sv("t2"); t3 = sv("t3"); t4 = sv("t4")
        qe = sv("qe"); qe2 = sv("qe2"); keb = sv("keb"); kdb = sv("kdb"); vb = sv("vb"); gbf = sv("gbf"); gbf2 = sv("gbf2")
        kdT = sv("kdT").rearrange("p (b k) -> p b k", b=5); vT = sv("vT").rearrange("p (b k) -> p b k", b=5)
        Sall = sv("Sall").rearrange("p (c v) -> p c v", c=9); Sbf = sv("Sbf").rearrange("p (c v) -> p c v", c=8)
        dec = sv("dec"); dec2 = sv("dec2"); Am = sv("Am").rearrange("p (b k) -> p b k", b=5); osq = sv("osq")
        o_fin = sv("o_fin").rearrange("p (h t) -> p h t", h=4)
        S0 = sv("S0").rearrange("p (i v) -> p i v", i=16); S0bf = sv("S0bf").rearrange("p (i v) -> p i v", i=16)
        Vm = sv("Vm").rearrange("p (i v) -> p i v", i=16)
        merged = sv("merged").rearrange("p (k t) -> p k t", k=8); s1 = sv("s1"); s2 = sv("s2")
        sgA = sv("sgA").rearrange("p (k t) -> p k t", k=8); sgB = sv("sgB").rearrange("p (k t) -> p k t", k=8)
        hidden = sv("hidden").rearrange("p (f t) -> p f t", f=22); ftmp = sv("ftmp")
        assert scr_plan["S0bf"][0] == scr_plan["S0"][0] + 8192 and scr_plan["Vm"][0] == scr_plan["S0"][0] + 12288
        assert scr_plan["S0"][0] >= scr_plan[("off", "ffn")]
        xo = scr_plan["S0"][0] // 4
        xalt = scr[:, xo:xo + 4096].rearrange("p (b d) -> p b d", b=4)
        sgt = [sv("sgt0"), sv("sgt1")]
        pT = sv("pT").rearrange("p (k t) -> p k t", k=2); pstage = sv("pstage").rearrange("p (b c) -> p b c", b=5); pbf = sv("pbf")

        pbs = [st.enter_context(nc.psum_tensor(f"pb{i}", [128, 512], F32)) for i in range(8)]
        bank_ctr = [0]

        def nb():
            b = bank_ctr[0] % 8
            bank_ctr[0] += 1
            return b

        def PB(b):
            return ("pb", b)

        wstate = {"tile": 0, "issued": set(), "released": set(), "total": NBLK * ntiles}

        def w_issue_n(n, extra_reads=()):
            if n >= wstate["total"] or n in wstate["issued"]:
                return
            slot = n % NRING
            blk = n % NBLK
            P.dma("pool", f"ring{slot}",
                  lambda e, slot=slot, blk=blk: e.dma_start(out=ring[slot][:], in_=wall[blk]),
                  reads=list(extra_reads), writes=[("ring", slot)])
            wstate["issued"].add(n)

        def w_issue(extra_reads=()):
            w_issue_n(len(wstate["issued"]), extra_reads)

        def w_get(expect):
            n = wstate["tile"] * NBLK + expect
            assert n in wstate["issued"], ("ring too small / block not prefetched", n)
            assert n - NRING < 0 or (n - NRING) in wstate["released"]
            slot = n % NRING
            return ring[slot], ("ring", slot)

        def w_release(expect):
            n = wstate["tile"] * NBLK + expect
            assert n in wstate["issued"] and n not in wstate["released"]
            wstate["released"].add(n)
            w_issue_n(n + NRING)

        def A(eng, fn, reads, writes):
            return P.op(eng, fn, reads=reads, writes=writes)

        def mm(out, lhsT, rhs, start, stop, reads, bank):
            A("pe", lambda e: e.matmul(out, lhsT=lhsT, rhs=rhs, start=start, stop=stop), reads, [PB(bank)])

        def tp(out, in_, ident, reads, bank):
            A("pe", lambda e: e.transpose(out=out, in_=in_, identity=ident), reads, [PB(bank)])

        def act(out, in_, func, reads, writes, **kw):
            A("act", lambda e: e.activation(out=out, in_=in_, func=func, **kw), reads, writes)

        def tt(out, in0, in1, op, reads, writes, eng="dve"):
            A(eng, lambda e: e.tensor_tensor(out=out, in0=in0, in1=in1, op=op), reads, writes)

        def ts(out, in0, s1_, s2_, op0, op1, reads, writes):
            A("dve", lambda e: e.tensor_scalar(out=out, in0=in0, scalar1=s1_, scalar2=s2_, op0=op0, op1=op1), reads, writes)

        def stt(out, in0, scalar, in1, op0, op1, reads, writes):
            A("dve", lambda e: e.scalar_tensor_tensor(out=out, in0=in0, scalar=scalar, in1=in1, op0=op0, op1=op1), reads, writes)

        def cp(out, in_, reads, writes, eng="dve"):
            if eng == "act":
                act(out, in_, AF.Copy, reads, writes)
            else:
                A(eng, lambda e: e.tensor_copy(out=out, in_=in_), reads, writes)

        P.dma("sp", "d_cs", lambda e: e.dma_start(out=cs[:], in_=cst[:, 0:CW_KEEP]), writes=["cs"])
        cstage = scr[:, 0:CW - CW_KEEP]
        P.dma("sp", "d_cs2", lambda e: e.dma_start(out=cstage, in_=cst[:, CW_KEEP:CW]), writes=["cstage"])
        maskPf = cstage[:, 0:128]
        maskSf = cstage[:, 128:256]
        resetmf = cstage[:, 256:896]
        P.dma("sp", "d_small", lambda e: e.dma_start(out=small[:], in_=smalld), writes=["small"])
        P.dma("sp", "d_pmix", lambda e: e.dma_start(out=pmixf[:], in_=pmixd), writes=["pmixf"])
        for gi in range(4):
            P.dma("sp", f"d_g{gi}",
                  lambda e, gi=gi: e.dma_start(out=gB[:, gi, :], in_=gvec[gi:gi + 1, :].broadcast_to([128, 1024])),
                  writes=[("gB", gi)])
        cp(identb[:], identf, ["cs"], ["identb"])
        cp(maskPb[:], maskPf, ["cstage"], ["maskPb"])
        cp(maskSb[:], maskSf, ["cstage"], ["maskSb"])
        cp(resetmb[:], resetmf, ["cstage"], ["resetmb"])
        cp(pmix[:].rearrange("p g c -> p (g c)"), pmixf[:], ["pmixf"], ["pmix"])
        A("dve", lambda e: e.memset(onesd[:], 1.0 / 128.0), [], ["onesd"])
        A("dve", lambda e: e.memset(carry[:].rearrange("p h v -> p (h v)"), 0.0), [], ["carry"])
        A("dve", lambda e: e.memset(ucarry[:].rearrange("p g t -> p (g t)"), 0.0), [], ["ucarry"])
        tt(lbv[:, 0:4], small[:, 8:12], small[:, 12:16], ALU.subtract, ["small"], ["lbv"])
        act(lbv[:, 0:4], lbv[:, 0:4], AF.Sigmoid, ["lbv"], ["lbv"])
        ts(lbv[:, 4:8], lbv[:, 0:4], -1.0, 1.0, ALU.mult, ALU.add, ["lbv"], ["lbv"])

        def run_tile(ti):
            wstate["tile"] = ti
            has_s = ti == 0
            last = ti == 3
            NT = 640 if has_s else 512
            blocks = [0, 1, 2, 3] + ([4] if has_s else [])
            subs = [(0, 512, "p")] + ([(512, 128, "s")] if has_s else [])
            hTk_p = [("hT", b) for b in range(4)]
            hTk = {"p": hTk_p, "s": [("hT", 4)]}

            def K(name):
                return [(name, "p")] + ([(name, "s")] if has_s else [])

            def xbuf(tj):
                return (xres, "xres") if tj % 2 == 0 else (xalt, "xalt")

            X, xk = xbuf(ti)

            def load_x(tj, b):
                src = xp[tj * 512 + b * 128: tj * 512 + (b + 1) * 128, :] if b < 4 else xs
                Xn, xkn = xbuf(tj)
                P.dma("sp", f"d_x{tj % 2}_{b}", lambda e, b=b, src=src, Xn=Xn: e.dma_start(out=Xn[:, b, :], in_=src),
                      writes=[(xkn, b)])

            if ti == 0:
                for b in blocks:
                    load_x(0, b)
            if ti == 0:
                for _ in range(NRING):
                    w_issue(extra_reads=[(xk, b) for b in blocks])
            if has_s:
                for half in range(2):
                    P.dma("sp", f"d_stg{half}",
                          lambda e, half=half: e.dma_start(out=stg[0:120, half, :], in_=spool[half * 120:(half + 1) * 120, :]),
                          writes=[("stg", half)])

            def half_stats(b, half, X_=None, xk_=None, ssq_=None, tag="ssq"):
                X_ = X if X_ is None else X_
                xk_ = xk if xk_ is None else xk_
                ssq_ = ssq if ssq_ is None else ssq_
                act(junk[:, 0:512], X_[:, b, half * 512:(half + 1) * 512], AF.Square, [(xk_, b)], ["junk", (tag, b, half)],
                    accum_out=ssq_[:, 2 * b + half:2 * b + half + 1])

            def norm_stats(have_partials, blks=None, X_=None, xk_=None, ssq_=None, rst_=None, tag="ssq", rtag="rst"):
                blks = blocks if blks is None else blks
                ssq_ = ssq if ssq_ is None else ssq_
                rst_ = rst if rst_ is None else rst_
                if not have_partials:
                    for b in blks:
                        for half in range(2):
                            half_stats(b, half, X_, xk_, ssq_, tag)
                nbk = len(blks)
                tt(rst_[:, 0:nbk], ssq_[:, 0:2 * nbk:2], ssq_[:, 1:2 * nbk:2], ALU.add,
                   [(tag, b, hf) for b in blks for hf in range(2)], [rtag])
                act(rst_[:, 0:nbk], rst_[:, 0:nbk], AF.Ln, [rtag], [rtag], scale=1.0 / 1024.0, bias=EPS)
                act(rst_[:, 0:nbk], rst_[:, 0:nbk], AF.Exp, [rtag], [rtag], scale=-0.5)

            def norm_apply(gi, blks=None, X_=None, xk_=None, rst_=None, rtag="rst"):
                blks = blocks if blks is None else blks
                X_ = X if X_ is None else X_
                xk_ = xk if xk_ is None else xk_
                rst_ = rst if rst_ is None else rst_
                for b in blks:
                    hb = hn[b % 2]
                    hk = f"hn{b % 2}"
                    stt(hb[:], X_[:, b, :], rst_[:, b:b + 1], gB[:, gi, :], ALU.mult, ALU.mult,
                        [(xk_, b), rtag, ("gB", gi)], [hk])
                    bank = nb()
                    bv = pbs[bank][:].bitcast(BF16)
                    for k in range(8):
                        tp(bv[:, k * 128:(k + 1) * 128], hb[:, k * 128:(k + 1) * 128], identb[:], [hk, "identb"], bank)
                    c0 = b * 128
                    cp(hT[:, :, c0:c0 + 128], bv.rearrange("p (k t) -> p k t", k=8), [PB(bank)], [("hT", b)], eng="act")

            def norm_T(gi, have_partials):
                norm_stats(have_partials)
                norm_apply(gi)

            ck("load")
            if ti == 0:
                norm_T(0, False)
            ck("norm1")

            wv, wk = w_get(0)
            wu = wv[:].rearrange("p (k c) -> p k c", k=8)
            ubanks = []
            for g in range(4):
                bp = nb()
                bs = nb() if has_s else None
                for k in range(8):
                    mm(pbs[bp][:, 0:512], wu[:, k, g * 128:(g + 1) * 128], hT[:, k, 0:512], k == 0, k == 7, [wk] + hTk_p, bp)
                    if has_s:
                        mm(pbs[bs][:, 0:128], wu[:, k, g * 128:(g + 1) * 128], hT[:, k, 512:640], k == 0, k == 7, [wk, ("hT", 4)], bs)
                ubanks.append((bp, bs))
                if g == 3:
                    w_release(0)
                w = 2 << g
                cp(ug[:, 0:16], ucarry[:, g, :], ["ucarry"], ["ug"])
                cp(ug[:, 16:528], pbs[bp][:, 0:512], [PB(bp)], ["ug"], eng="act")
                tt(pa[:, 1:528], ug[:, 1:528], ug[:, 0:527], ALU.add, ["ug"], ["pa"])
                sw = pa
                swk = "pa"
                if w >= 4:
                    tt(pb_[:, 3:528], pa[:, 3:528], pa[:, 1:526], ALU.add, ["pa"], ["pb_"])
                    sw, swk = pb_, "pb_"
                if w >= 8:
                    tt(pa[:, 7:528], pb_[:, 7:528], pb_[:, 3:524], ALU.add, ["pb_"], ["pa"])
                    sw, swk = pa, "pa"
                if w >= 16:
                    tt(pb_[:, 15:528], pa[:, 15:528], pa[:, 7:520], ALU.add, ["pa"], ["pb_"])
                    sw, swk = pb_, "pb_"
                stt(pooled[:, g, 0:512], sw[:, 16:528], 1.0 / w, ug[:, 16:528], ALU.mult, ALU.subtract,
                    [swk, "ug"], [("pooled", g, "p")])
                if ti == 0:
                    tt(tmp16[:], sw[:, 16:32], rcv[:, g * 16:(g + 1) * 16], ALU.mult, [swk, "cs"], ["tmp16"])
                    tt(pooled[:, g, 0:16], tmp16[:], ug[:, 16:32], ALU.subtract, ["tmp16", "ug"], [("pooled", g, "p")])
                cp(ucarry[:, g, :], ug[:, 512:528], ["ug"], ["ucarry"])
                if last:
                    bt = nb()
                    tp(pbs[bt][0:15, 0:128], ug[:, 513:528], identf, ["ug", "cs"], bt)
                    cp(npp_sb[0:15, g * 128:(g + 1) * 128], pbs[bt][0:15, 0:128], [PB(bt)], ["npp_sb"], eng="act")
                if has_s:
                    bt = nb()
                    for half in range(2):
                        tp(pbs[bt][:, half * 120:(half + 1) * 120], stg[0:120, half, g * 128:(g + 1) * 128],
                           identf[0:120, 0:120], [("stg", half), "cs"], bt)
                    cp(ue[:, :, 0:15], pbs[bt][:, 0:240].rearrange("p (i r) -> p i r", r=15), [PB(bt)], ["ue"], eng="act")
                    cp(ue[:, :, 15:23], pbs[bs][:, 0:128].rearrange("p (i t) -> p i t", t=8), [PB(bs)], ["ue"], eng="act")
                    tt(sa[:, :, 1:23], ue[:, :, 1:23], ue[:, :, 0:22], ALU.add, ["ue"], ["sa"])
                    ssw, sswk = sa, "sa"
                    if w >= 4:
                        tt(sb_[:, :, 3:23], sa[:, :, 3:23], sa[:, :, 1:21], ALU.add, ["sa"], ["sb_"])
                        ssw, sswk = sb_, "sb_"
                    if w >= 8:
                        tt(sa[:, :, 7:23], sb_[:, :, 7:23], sb_[:, :, 3:19], ALU.add, ["sb_"], ["sa"])
                        ssw, sswk = sa, "sa"
                    if w >= 16:
                        tt(sb_[:, :, 15:23], sa[:, :, 15:23], sa[:, :, 7:15], ALU.add, ["sa"], ["sb_"])
                        ssw, sswk = sb_, "sb_"
                    stt(pooled[:, g, 512:640].rearrange("p (i t) -> p i t", t=8), ssw[:, :, 15:23], 1.0 / w,
                        ue[:, :, 15:23], ALU.mult, ALU.subtract, [sswk, "ue"], [("pooled", g, "s")])
                    cp(npsb[:, g, :].rearrange("p (i r) -> p i r", r=15), ue[:, :, 8:23], ["ue"], [("npsb", g)])
            if last:
                P.dma("sp", "d_npp", lambda e: e.dma_start(out=npp, in_=npp_sb[0:15, :]), reads=["npp_sb"])
            if has_s:
                for half in range(2):
                    bt = nb()
                    for g in range(4):
                        tp(pbs[bt][0:120, g * 128:(g + 1) * 128], npsb[:, g, half * 120:(half + 1) * 120], identf,
                           [("npsb", g), "cs"], bt)
                    cp(nps_sb[0:120, half, :], pbs[bt][0:120, 0:512], [PB(bt)], [("stg", half)], eng="act")
                    P.dma("sp", f"d_nps{half}",
                          lambda e, half=half: e.dma_start(out=nps[half * 120:(half + 1) * 120, :], in_=nps_sb[0:120, half, :]),
                          reads=[("stg", half)])
            ck("u")
            PIPE = not has_s
            if PIPE:
                zc, rcn = [0], [0]

                def nbZ():
                    b = zc[0] % 4
                    zc[0] += 1
                    return b

                def nbR():
                    b = 6 + rcn[0] % 2
                    rcn[0] += 1
                    return b

                gcn = [0]

                def nbG():
                    b = 4 + gcn[0] % 2
                    gcn[0] += 1
                    return b
            else:
                nbZ = nbR = nbG = nb
            gb2 = [gbf, gbf2]
            QE = [qe, qe2]
            DEC = [dec, dec2]
            HS = {h: {} for h in range(4)}

            def poolmix():
                for g in range(4):
                    for (c0, n, sk) in subs:
                        bk = nbR()
                        mm(pbs[bk][:, 0:n], pmix[:, g, :], pooled[:, g, c0:c0 + n], True, True, ["pmix", ("pooled", g, sk)], bk)
                        act(pool_out[:, g, c0:c0 + n], pbs[bk][:, 0:n], AF.Copy, [PB(bk), "small"], [("pool_out", g, sk)],
                            scale=small[:, g:g + 1])

            def z_group(h, j):
                S_ = HS[h]
                if j == 0:
                    wv, wk = w_get(1 + 2 * h)
                    S_["wh"] = wv[:].rearrange("p (k j c) -> p k j c", k=8, j=4)
                    S_["wk"] = wk
                    S_["bs"] = nbZ() if has_s else None
                    S_["zb"] = []
                wh, wk, bs = S_["wh"], S_["wk"], S_["bs"]
                bp = nbZ()
                for k in range(8):
                    mm(pbs[bp][:, 0:512], wh[:, k, j, :], hT[:, k, 0:512], k == 0, k == 7, [wk] + hTk_p, bp)
                    if has_s:
                        mm(pbs[bs][:, j * 128:(j + 1) * 128], wh[:, k, j, :], hT[:, k, 512:640], k == 0, k == 7,
                           [wk, ("hT", 4)], bs)
                S_["zb"].append(bp)
                if j == 3:
                    w_release(1 + 2 * h)

            def z_evac(h):
                S_ = HS[h]
                zb, bs = S_["zb"], S_["bs"]
                gbh = gb2[h % 2]
                gbk = f"gbf{h % 2}"

                def zsrc(j, sk):
                    return (pbs[zb[j]][:, 0:512], PB(zb[j])) if sk == "p" else (pbs[bs][:, j * 128:(j + 1) * 128], PB(bs))

                for (c0, n, sk) in subs:
                    src, key = zsrc(1, sk)
                    act(t1[:, c0:c0 + n], src, AF.Sigmoid, [key], [("t1", sk)])
                    src, key = zsrc(0, sk)
                    act(qf[:, c0:c0 + n], src, AF.Sigmoid, [key], [("qf", sk)])
                    tt(qf[:, c0:c0 + n], qf[:, c0:c0 + n], src, ALU.mult, [("qf", sk), key], [("qf", sk)])
                    src, key = zsrc(3, sk)
                    act(t4[:, c0:c0 + n], src, AF.Sigmoid, [key], [("t4", sk)])
                    tt(gbh[:, c0:c0 + n], t4[:, c0:c0 + n], src, ALU.mult, [("t4", sk), key], [(gbk, sk)])
                    src, key = zsrc(2, sk)
                    cp(vb[:, c0:c0 + n], src, [key], [("vb", sk)])

            def g_mm(h, jj):
                S_ = HS[h]
                if jj == 0:
                    wgv_, wgk_ = w_get(2 + 2 * h)
                    S_["wgv"] = wgv_[:].rearrange("p (k t c) -> p k t c", k=8, t=4)
                    S_["wgk"] = wgk_
                    S_["gate_ev"] = []
                wgv, wgk_ = S_["wgv"], S_["wgk"]
                j = 2 * h + jj
                bsg = nbG() if has_s else None
                for t_ in range(2):
                    bpg = nbG()
                    for k in range(8):
                        mm(pbs[bpg][:, 0:512], wgv[:, k, 2 * jj + t_, :], hT[:, k, 0:512], k == 0, k == 7, [wgk_] + hTk_p, bpg)
                        if has_s:
                            mm(pbs[bsg][:, t_ * 128:(t_ + 1) * 128], wgv[:, k, 2 * jj + t_, :], hT[:, k, 512:640], k == 0, k == 7,
                               [wgk_, ("hT", 4)], bsg)
                    S_["gate_ev"].append((j, t_, bpg, bsg))
                if jj == 1:
                    w_release(2 + 2 * h)

            def g_evac(h, jj):
                for (j, t_, bpg, bsg) in HS[h]["gate_ev"][2 * jj:2 * jj + 2]:
                    dst = sgA if t_ == 0 else sgB
                    dk = "sgA" if t_ == 0 else "sgB"
                    cp(dst[:, j, 0:512], pbs[bpg][:, 0:512], [PB(bpg)], [(dk, j, "p")], eng="act")
                    if has_s:
                        cp(dst[:, j, 512:640], pbs[bsg][:, t_ * 128:(t_ + 1) * 128], [PB(bsg)], [(dk, j, "s")], eng="act")

            def chain_a(h):
                ts(t1[:, 0:NT], t1[:, 0:NT], lbv[:, 4 + h:5 + h], lbv[:, h:h + 1], ALU.mult, ALU.add, K("t1") + ["lbv"], K("t1"))
                ts(t2[:, 0:NT], t1[:, 0:NT], -1.0, 1.0, ALU.mult, ALU.add, K("t1"), K("t2"))
                act(t1[:, 0:NT], t1[:, 0:NT], AF.Ln, K("t1"), K("t1"))

            def cb_scan(h):
                A("dve", lambda e: e.tensor_tensor_scan(out=t3[:, 0:NT], data0=resetm[:, 0:NT], data1=t1[:, 0:NT],
                                                        initial=0.0, op0=ALU.mult, op1=ALU.add),
                  K("t1") + ["resetmb"], K("t3"))

            def cb_exp(h):
                act(t1[:, 0:NT], t3[:, 0:NT], AF.Exp, K("t3"), K("t1"))
                act(t4[:, 0:NT], t3[:, 0:NT], AF.Exp, K("t3"), K("t4"), scale=-1.0)

            def cb_mul(h):
                qe_ = QE[h % 2]
                qk = f"qe{h % 2}"
                tt(qe_[:, 0:NT], qf[:, 0:NT], t1[:, 0:NT], ALU.mult, K("qf") + K("t1"), [(qk, "p")] + ([(qk, "s")] if has_s else []))
                tt(t2[:, 0:NT], t2[:, 0:NT], t4[:, 0:NT], ALU.mult, K("t2") + K("t4"), K("t2"))

            def cb_keb(h):
                cp(keb[:, 0:NT], t2[:, 0:NT], K("t2"), K("keb"), eng="act")

            def cb_dec(h):
                dec_ = DEC[h % 2]
                dk_ = f"dec{h % 2}"
                cp(dec_[:, 0:8], t1[:, 63:512:64], [("t1", "p")], [(dk_, "p")])
                tt(kdb[:, 0:512].rearrange("p (c t) -> p c t", t=64), t2[:, 0:512].rearrange("p (c t) -> p c t", t=64),
                   dec_[:, 0:8].unsqueeze(2).broadcast_to([128, 8, 64]), ALU.mult, [("t2", "p"), (dk_, "p")], [("kdb", "p")])
                if has_s:
                    cp(dec_[:, 8:24], t1[:, 519:640:8], [("t1", "s")], [(dk_, "s")])
                    tt(kdb[:, 512:640].rearrange("p (c t) -> p c t", t=8), t2[:, 512:640].rearrange("p (c t) -> p c t", t=8),
                       dec_[:, 8:24].unsqueeze(2).broadcast_to([128, 16, 8]), ALU.mult, [("t2", "s"), (dk_, "s")], [("kdb", "s")])

            def chain_b(h):
                cb_scan(h); cb_exp(h); cb_mul(h); cb_keb(h); cb_dec(h)

            def R1(h):
                qe = QE[h % 2]
                qk = f"qe{h % 2}"
                for (src_, srck, dst, dstk) in ((kdb, "kdb", kdT, "kdT"), (vb, "vb", vT, "vT")):
                    bt = nbR()
                    bv = pbs[bt][:].bitcast(BF16)
                    for b in blocks:
                        sk = "p" if b < 4 else "s"
                        tp(bv[:, b * 128:(b + 1) * 128], src_[:, b * 128:(b + 1) * 128], identb[:], [(srck, sk), "identb"], bt)
                    nbk = len(blocks)
                    cp(dst[:, 0:nbk, :], bv[:, 0:nbk * 128].rearrange("p (b k) -> p b k", k=128), [PB(bt)], [dstk], eng="act")
                ba = nbR()
                for b in range(4):
                    mm(pbs[ba][:, b * 128:(b + 1) * 128], keb[:, b * 128:(b + 1) * 128], qe[:, b * 128:(b + 1) * 128],
                       True, True, [("keb", "p"), (qk, "p")], ba)
                tt(Am[:, 0:4, :], pbs[ba][:, 0:512].rearrange("p (b t) -> p b t", b=4),
                   maskPb[:].unsqueeze(1).broadcast_to([128, 4, 128]), ALU.mult, [PB(ba), "maskPb"], [("Am", "p")])
                if has_s:
                    bas = nbR()
                    mm(pbs[bas][:, 0:128], keb[:, 512:640], qe[:, 512:640], True, True, [("keb", "s"), (qk, "s")], bas)
                    tt(Am[:, 4, :], pbs[bas][:, 0:128], maskSb[:], ALU.mult, [PB(bas), "maskSb"], [("Am", "s")])
                    tt(Vm[:, :, :], vT[:, 4, :].unsqueeze(1).broadcast_to([128, 16, 128]),
                       seqm.unsqueeze(2).broadcast_to([128, 16, 128]), ALU.mult, ["vT", "cs"], ["Vm"])

            def sample_prefetch(h):
                P.dma("sp", "d_S0", lambda e, h=h: e.dma_start(out=S0[:, :, :], in_=shg[:, h, :, :].rearrange("i k v -> k i v")),
                      writes=["S0"])

            def sample_bf(h):
                cp(S0bf[:, :, :].rearrange("p i v -> p (i v)"), S0[:, :, :].rearrange("p i v -> p (i v)"), ["S0"], ["S0bf"], eng="act")

            def sample_state_update(h):
                dec = DEC[h % 2]
                dk_ = f"dec{h % 2}"
                tt(S0[:, :, :], S0[:, :, :], dec[:, 8:24].unsqueeze(2).broadcast_to([128, 16, 128]), ALU.mult,
                   ["S0", (dk_, "s")], ["S0"])
                for q4 in range(4):
                    bd = nbR()
                    mm(pbs[bd][:, 0:512], kdT[:, 4, :], Vm[:, 4 * q4:4 * q4 + 4, :].rearrange("p i v -> p (i v)"), True, True,
                       ["kdT", "Vm"], bd)
                    tt(S0[:, 4 * q4:4 * q4 + 4, :].rearrange("p i v -> p (i v)"),
                       S0[:, 4 * q4:4 * q4 + 4, :].rearrange("p i v -> p (i v)"), pbs[bd][:, 0:512], ALU.add,
                       ["S0", PB(bd)], ["S0"])
                P.dma("sp", "d_nhs", lambda e, h=h: e.dma_start(out=nhs[:, h, :, :].rearrange("i k v -> k i v"), in_=S0[:, :, :]),
                      reads=["S0"])

            def R2_mm(h):
                dsb = [nbR(), nbR()]
                HS[h]["dsb"] = dsb
                for c in range(8):
                    blk, half = c // 2, c % 2
                    bk = dsb[half]
                    mm(pbs[bk][:, blk * 128:(blk + 1) * 128], kdT[half * 64:(half + 1) * 64, blk, :],
                       vT[half * 64:(half + 1) * 64, blk, :], True, True, ["kdT", "vT"], bk)
                cp(Sall[:, 0, :], carry[:, h, :], ["carry"], [("Sall", 0)])

            def R2_steps(h, c0, c1):
                dec_ = DEC[h % 2]
                dk_ = f"dec{h % 2}"
                dsb = HS[h]["dsb"]
                for c in range(c0, c1):
                    blk, half = c // 2, c % 2
                    bk = dsb[half]
                    stt(Sall[:, c + 1, :], Sall[:, c, :], dec_[:, c:c + 1], pbs[bk][:, blk * 128:(blk + 1) * 128],
                        ALU.mult, ALU.add, [("Sall", c), (dk_, "p"), PB(bk)], [("Sall", c + 1)])

            def R2_fin(h):
                cp(Sbf[:, :, :].rearrange("p c v -> p (c v)"), Sall[:, 0:8, :].rearrange("p c v -> p (c v)"),
                   [("Sall", c) for c in range(8)], ["Sbf"], eng="act")
                cp(carry[:, h, :], Sall[:, 8, :], [("Sall", 8)], ["carry"])
                if last:
                    P.dma("sp", f"d_nhp{h}", lambda e, h=h: e.dma_start(out=nhp[h], in_=Sall[:, 8, :]), reads=[("Sall", 8)])

            def R2(h):
                R2_mm(h); R2_steps(h, 0, 8); R2_fin(h)

            def R3(h):
                qe = QE[h % 2]
                qk = f"qe{h % 2}"
                S_ = HS[h]
                bo = nbR()
                for b in range(4):
                    mm(pbs[bo][:, b * 128:(b + 1) * 128], vT[:, b, :], Am[:, b, :], True, False, ["vT", ("Am", "p")], bo)
                    for half in range(2):
                        c = 2 * b + half
                        mm(pbs[bo][:, c * 64:(c + 1) * 64], Sbf[:, c, :], qe[:, c * 64:(c + 1) * 64], False, half == 1,
                           ["Sbf", (qk, "p")], bo)
                bos = None
                if has_s:
                    sample_bf(h)
                    bos = nbR()
                    mm(pbs[bos][:, 0:128], vT[:, 4, :], Am[:, 4, :], True, False, ["vT", ("Am", "s")], bos)
                    for i in range(16):
                        mm(pbs[bos][:, i * 8:(i + 1) * 8], S0bf[:, i, :], qe[:, 512 + i * 8:512 + (i + 1) * 8], False, i == 15,
                           ["S0bf", (qk, "s")], bos)
                S_["obanks"] = [(0, 512, "p", bo)] + ([(512, 128, "s", bos)] if has_s else [])
                for (c0, n, sk, bk) in S_["obanks"]:
                    act(osq[:, c0:c0 + n], pbs[bk][:, 0:n], AF.Square, [PB(bk)], [("osq", sk)])

            def R4(h):
                gbh = gb2[h % 2]
                gbk = f"gbf{h % 2}"
                for (c0, n, sk, bk) in HS[h]["obanks"]:
                    bn = nbR()
                    mm(pbs[bn][:, 0:n], onesd[:], osq[:, c0:c0 + n], True, True, ["onesd", ("osq", sk)], bn)
                    act(s1[:, c0:c0 + n], pbs[bn][:, 0:n], AF.Ln, [PB(bn)], [("s1", sk)], bias=EPS)
                    act(s1[:, c0:c0 + n], s1[:, c0:c0 + n], AF.Exp, [("s1", sk)], [("s1", sk)], scale=-0.5)
                    stt(s2[:, c0:c0 + n], pbs[bk][:, 0:n], small[:, 4 + h:5 + h], s1[:, c0:c0 + n], ALU.mult, ALU.mult,
                        [PB(bk), "small", ("s1", sk)], [("s2", sk)])
                tt(o_fin[:, h, 0:NT], s2[:, 0:NT], gbh[:, 0:NT], ALU.mult, K("s2") + [(gbk, "p"), (gbk, "s")],
                   [("o_fin", h, s_) for s_ in ("p", "s")])

            if PIPE:
                for j in range(4):
                    z_group(0, j)
            poolmix()
            ck("poolmix")
            P.barrier()
            ck("bar")
            if PIPE:
                for h in range(4):
                    p = h - 1
                    if h > 0:
                        R1(p)
                    z_evac(h)
                    g_mm(h, 0)
                    if h > 0:
                        R2_mm(p)
                    chain_a(h)
                    if h > 0:
                        R2_steps(p, 0, 4)
                    g_evac(h, 0)
                    if h < 3:
                        for j in range(4):
                            z_group(h + 1, j)
                    cb_scan(h)
                    if h > 0:
                        R2_steps(p, 4, 7)
                    cb_exp(h)
                    cb_mul(h)
                    cb_dec(h)
                    cb_keb(h)
                    g_mm(h, 1)
                    if h > 0:
                        R2_steps(p, 7, 8)
                        R2_fin(p)
                        R3(p)
                    g_evac(h, 1)
                    if h > 0:
                        R4(p)
                R1(3); R2(3); R3(3); R4(3)
            else:
                for h in range(4):
                    sample_prefetch(h)
                    for j in range(4):
                        z_group(h, j)
                    z_evac(h)
                    g_mm(h, 0)
                    g_mm(h, 1)
                    chain_a(h)
                    chain_b(h)
                    g_evac(h, 0)
                    g_evac(h, 1)
                    R1(h); R2(h); R3(h); R4(h)
                    sample_state_update(h)

            ck("hgrn")
            for jh in range(2):
                wy, wyk = w_get(9 + jh)
                wyv = wy[:].rearrange("p (k t c) -> p k t c", k=4, t=2)
                for jj in range(4):
                    j = jh * 4 + jj
                    bs = nb() if has_s else None
                    b_ya, b_yb = nb(), nb()
                    for t_, bk, srcb, srck in ((0, b_ya, pool_out, "pool_out"), (1, b_yb, o_fin, "o_fin")):
                        for kk in range(4):
                            for (c0, n, sk) in subs:
                                o = pbs[bk][:, 0:512] if sk == "p" else pbs[bs][:, t_ * 128:(t_ + 1) * 128]
                                mm(o, wyv[:, kk, t_, jj * 128:(jj + 1) * 128], srcb[:, kk, c0:c0 + n], kk == 0, kk == 3,
                                   [wyk, (srck, kk, sk)], bk if sk == "p" else bs)
                    if jj == 3:
                        w_release(9 + jh)
                    for (c0, n, sk) in subs:
                        oa = pbs[b_ya][:, 0:512] if sk == "p" else pbs[bs][:, 0:128]
                        ob = pbs[b_yb][:, 0:512] if sk == "p" else pbs[bs][:, 128:256]
                        ka = PB(b_ya) if sk == "p" else PB(bs)
                        kb_ = PB(b_yb) if sk == "p" else PB(bs)
                        act(s1[:, c0:c0 + n], sgA[:, j, c0:c0 + n], AF.Sigmoid, [("sgA", j, sk)], [("s1", sk)])
                        act(s2[:, c0:c0 + n], sgB[:, j, c0:c0 + n], AF.Sigmoid, [("sgB", j, sk)], [("s2", sk)])
                        tt(s1[:, c0:c0 + n], s1[:, c0:c0 + n], oa, ALU.mult, [("s1", sk), ka], [("s1", sk)])
                        tt(s2[:, c0:c0 + n], s2[:, c0:c0 + n], ob, ALU.mult, [("s2", sk), kb_], [("s2", sk)])
                    tt(merged[:, j, 0:NT], s1[:, 0:NT], s2[:, 0:NT], ALU.add, K("s1") + K("s2"),
                       [("merged", j, s_) for s_ in ("p", "s")])

            ck("merge")
            for half in range(2):
                wo, wok = w_get(11 + half)
                wov = wo[:].rearrange("p (k c) -> p k c", k=8)
                bks = {b: nb() for b in blocks}
                for k in range(8):
                    for b in blocks:
                        sk = "p" if b < 4 else "s"
                        mm(pbs[bks[b]][:, 0:512], merged[:, k, b * 128:(b + 1) * 128], wov[:, k, :], k == 0, k == 7,
                           [wok, ("merged", k, sk)], bks[b])
                w_release(11 + half)
                for b in blocks:
                    tt(X[:, b, half * 512:(half + 1) * 512], X[:, b, half * 512:(half + 1) * 512], pbs[bks[b]][:, 0:512],
                       ALU.add, [(xk, b), PB(bks[b])], [(xk, b)])
                    half_stats(b, half)

            ck("wout")
            norm_T(1, True)
            P.barrier()
            if ti + 1 < ntiles:
                for b in range(4):
                    load_x(ti + 1, b)
            for b in blocks:
                src = ppd[ti * 512 + b * 128: ti * 512 + (b + 1) * 128, :] if b < 4 else psd
                P.dma("sp", f"d_p{b}", lambda e, src=src, b=b: e.dma_start(out=pstage[:, b, :], in_=src), writes=[("pstage", b)])
            for jb in range(11):
                wf, wfk = w_get(13 + jb)
                wfv = wf[:].rearrange("p (k j c) -> p k j c", k=8, j=2)
                for cc in range(2):
                    f = 2 * jb + cc
                    bs = nb() if has_s else None
                    b_g, b_u = nb(), nb()
                    for jx, bk in ((0, b_g), (1, b_u)):
                        for k in range(8):
                            for (c0, n, sk) in subs:
                                o = pbs[bk][:, 0:512] if sk == "p" else pbs[bs][:, jx * 128:(jx + 1) * 128]
                                mm(o, wfv[:, k, jx, cc * 128:(cc + 1) * 128], hT[:, k, c0:c0 + n], k == 0, k == 7,
                                   [wfk] + hTk[sk], bk if sk == "p" else bs)
                    if cc == 1:
                        w_release(13 + jb)
                    for (c0, n, sk) in subs:
                        og = pbs[b_g][:, 0:512] if sk == "p" else pbs[bs][:, 0:128]
                        ou = pbs[b_u][:, 0:512] if sk == "p" else pbs[bs][:, 128:256]
                        kg = PB(b_g) if sk == "p" else PB(bs)
                        ku = PB(b_u) if sk == "p" else PB(bs)
                        act(ftmp[:, c0:c0 + n], og, AF.Sigmoid, [kg], [("ftmp", sk)])
                        tt(ftmp[:, c0:c0 + n], ftmp[:, c0:c0 + n], og, ALU.mult, [("ftmp", sk), kg], [("ftmp", sk)])
                        tt(hidden[:, f, c0:c0 + n], ftmp[:, c0:c0 + n], ou, ALU.mult, [("ftmp", sk), ku], [("hidden", f, sk)])
            for b in blocks:
                cp(pbf[:], pstage[:, b, :], [("pstage", b)], ["pbf"], eng="act")
                bt = nb()
                bv = pbs[bt][:].bitcast(BF16)
                for k in range(2):
                    tp(bv[:, k * 128:(k + 1) * 128], pbf[:, k * 128:(k + 1) * 128], identb[:], ["pbf", "identb"], bt)
                cp(pT[:, :, b * 128:(b + 1) * 128], bv[:, 0:256].rearrange("p (k t) -> p k t", k=2), [PB(bt)], [("pT", b)])
            for half in range(2):
                bks = {b: nb() for b in blocks}
                for kb in range(3):
                    wd, wdk = w_get(24 + half * 3 + kb)
                    wdv = wd[:].rearrange("p (k c) -> p k c", k=8)
                    nk = 8 if kb < 2 else 6
                    for kk in range(nk):
                        f = kb * 8 + kk
                        for b in blocks:
                            sk = "p" if b < 4 else "s"
                            mm(pbs[bks[b]][:, 0:512], hidden[:, f, b * 128:(b + 1) * 128], wdv[:, kk, :], f == 0, f == 21,
                               [wdk, ("hidden", f, sk)], bks[b])
                    w_release(24 + half * 3 + kb)
                for b in blocks:
                    tt(X[:, b, half * 512:(half + 1) * 512], X[:, b, half * 512:(half + 1) * 512], pbs[bks[b]][:, 0:512],
                       ALU.add, [(xk, b), PB(bks[b])], [(xk, b)])
                    half_stats(b, half)

            ck("ffn")
            norm_T(2, True)
            if ti + 1 < ntiles:
                Xn, xkn = xbuf(ti + 1)
                norm_stats(False, blks=[0, 1, 2, 3], X_=Xn, xk_=xkn, ssq_=ssq1, rst_=rst1, tag="ssq1", rtag="rst1")
            wg0, wg0k = w_get(30)
            wpp_, wppk = w_get(31)
            wg1, wg1k = w_get(32)
            wppv = wpp_[:, 0:2048].rearrange("p (k c) -> p k c", k=2)
            for half in range(2):
                wg_, wgk = (wg0, wg0k) if half == 0 else (wg1, wg1k)
                wgv = wg_[:].rearrange("p (k c) -> p k c", k=8)
                bks = {b: nb() for b in blocks}
                for k in range(8):
                    for b in blocks:
                        mm(pbs[bks[b]][:, 0:512], hT[:, k, b * 128:(b + 1) * 128], wgv[:, k, :], k == 0, k == 7,
                           [wgk, ("hT", b)], bks[b])
                if half == 0:
                    w_release(30)
                for b in blocks:
                    sg = sgt[b % 2]
                    sgk = f"sgt{b % 2}"
                    act(sg[:], pbs[bks[b]][:, 0:512], AF.Sigmoid, [PB(bks[b])], [sgk])
                    be = nb()
                    for k in range(2):
                        mm(pbs[be][:, 0:512], pT[:, k, b * 128:(b + 1) * 128], wppv[:, k, half * 512:(half + 1) * 512], k == 0, k == 1,
                           [wppk, ("pT", b)], be)
                    tt(sg[:], sg[:], pbs[be][:, 0:512], ALU.mult, [sgk, PB(be)], [sgk])
                    tt(X[:, b, half * 512:(half + 1) * 512], X[:, b, half * 512:(half + 1) * 512], sg[:], ALU.add,
                       [(xk, b), sgk], [(xk, b)])
                for b in blocks:
                    half_stats(b, half)
                if half == 1:
                    w_release(31); w_release(32)

            ck("ple")
            if ti + 1 < ntiles:
                Xn, xkn = xbuf(ti + 1)
                norm_apply(0, blks=[0, 1, 2, 3], X_=Xn, xk_=xkn, rst_=rst1, rtag="rst1")
            P.barrier()
            norm_stats(True)
            for b in blocks:
                dst = yp[ti * 512 + b * 128: ti * 512 + (b + 1) * 128, :] if b < 4 else ys
                for half in range(2):
                    q_ = (2 * b + half) % 4
                    yst = ystage[q_]
                    ysk = f"ystage{q_}"
                    hs_ = slice(half * 512, (half + 1) * 512)
                    stt(yst[:], X[:, b, hs_], rst[:, b:b + 1], gB[:, 3, hs_], ALU.mult, ALU.mult,
                        [(xk, b), "rst", ("gB", 3)], [ysk])
                    P.dma("sp", f"d_y{q_}", lambda e, yst=yst, dst=dst, hs_=hs_: e.dma_start(out=dst[:, hs_], in_=yst[:]),
                          reads=[ysk])

        P.barrier()
        try:
            ck("setup")
            for ti in range(ntiles):
                run_tile(ti)
        except _Stop:
            pass
        P.finish("sp")
        P.emit()
    return nc


_PROG = {}


def _prep_inputs(inp):
    f = lambda a: np.ascontiguousarray(np.asarray(a, dtype=np.float32))
    w_in = f(inp["w_in"][0])
    wallv = build_wall(w_in, f(inp["w_pool_up"][0]), f(inp["w_hgrn_up"][0]), f(inp["w_out"][0]),
                       f(inp["w_ffn_gate"][0]), f(inp["w_ffn_up"][0]), f(inp["w_ffn_down"][0]),
                       f(inp["w_ple_gate"][0]), f(inp["w_ple_proj"][0]))
    gvec = np.ascontiguousarray(np.stack([f(inp["g_mix"][0]), f(inp["g_ffn"][0]), f(inp["g_ple"][0]), f(inp["g_final"])], 0))
    small = np.zeros((128, 16), np.float32)
    small[:, 0:4] = f(inp["pool_scale"][0]).reshape(4, 128).T
    small[:, 4:8] = f(inp["hgrn_norm"][0]).reshape(4, 128).T
    small[:, 8:12] = f(inp["hgrn_lb"][0]).reshape(4, 128).T
    small[:, 12:16] = f(inp["hgrn_lb"][1]).reshape(4, 128).T
    pmix = np.ascontiguousarray(f(inp["w_pool_mix"][0]).transpose(1, 0, 2)).reshape(128, 512)
    xp = f(inp["x_prompt"]); xsm = f(inp["x_sample"])
    ppr = f(inp["p_prompt"][0]); psm = f(inp["p_sample"][0])
    spl = f(inp["state_pool"][0]); shg = f(inp["state_hgrn"][0])
    maps = []
    for c in range(NCORES):
        maps.append({
            "xp": xp[c], "xs": xsm[16 * c:16 * c + 16].reshape(128, 1024),
            "pp": ppr[c], "ps": psm[16 * c:16 * c + 16].reshape(128, 256),
            "spool": spl[16 * c:16 * c + 16].reshape(240, 512),
            "shg": shg[16 * c:16 * c + 16],
            "wall": wallv, "cst": CONST_ARR, "gvec": gvec, "small": small, "pmix": pmix,
        })
    return maps


def kernel(**inputs):
    if "nc" not in _PROG:
        _PROG["nc"] = build_program()
    nc = _PROG["nc"]
    maps = _prep_inputs(inputs)
    res = run_bass_kernel_spmd(nc, maps, core_ids=list(range(NCORES)))
    R = res.results
    y_p = np.stack([R[c]["yp"] for c in range(NCORES)], 0).astype(np.float32)
    y_s = np.concatenate([R[c]["ys"].reshape(16, 8, 1024) for c in range(NCORES)], 0).astype(np.float32)
    npp = np.stack([R[c]["npp"] for c in range(NCORES)], 0)[None].astype(np.float32)
    nhp = np.stack([R[c]["nhp"] for c in range(NCORES)], 0)[None].astype(np.float32)
    nps = np.concatenate([R[c]["nps"].reshape(16, 15, 512) for c in range(NCORES)], 0)[None].astype(np.float32)
    nhs = np.concatenate([R[c]["nhs"] for c in range(NCORES)], 0)[None].astype(np.float32)
    return (y_p, y_s, npp, nhp, nps, nhs)
```

```python
import numpy as np
from contextlib import ExitStack
import concourse.bass as bass
import concourse.mybir as mybir
from concourse.bass_utils import run_bass_kernel_spmd

F32 = mybir.dt.float32
BF16 = mybir.dt.bfloat16
AF = mybir.ActivationFunctionType
ALU = mybir.AluOpType

ENGS = ("pe", "act", "dve", "pool", "sp")
EPS = 1e-6
NCORES = 8
NRING = 5
NTILES = 4
import os as _os
SAME_DIST = int(_os.environ.get("SAME_DIST", str(1 << 30)))


class Prog:
    def __init__(self, nc, stack, same_engine_wait=True):
        self.nc = nc
        self.stack = stack
        self.streams = {e: [] for e in ENGS}
        self.count = {e: 0 for e in ENGS}
        self.known = {e: {} for e in ENGS}
        self.sems = {}
        self.dma_count = {}
        self.lastw = {}
        self.readers = {}
        self.same_engine_wait = same_engine_wait

    def _collect(self, eng, reads, writes):
        deps = []
        for k in reads:
            ev = self.lastw.get(k)
            if ev is not None:
                deps.append(ev)
        for k in writes:
            ev = self.lastw.get(k)
            if ev is not None:
                deps.append(ev)
            rd = self.readers.get(k)
            if rd:
                deps.extend(rd.values())
        kn = self.known[eng]
        need = {}
        for (s, v, vc) in deps:
            if s == eng and (eng == "pe" or not self.same_engine_wait or self.count[eng] - v >= SAME_DIST):
                continue
            if kn.get(s, 0) >= v:
                continue
            if need.get(s, 0) < v:
                need[s] = v
            for s2, v2 in vc.items():
                if s2 == eng:
                    continue
                if kn.get(s2, 0) < v2:
                    kn[s2] = v2
        waits = []
        for s, v in need.items():
            waits.append((s, v))
            if kn.get(s, 0) < v:
                kn[s] = v
        return waits

    def _record(self, ev, reads, writes):
        s = ev[0]
        for k in reads:
            self.readers.setdefault(k, {})[s] = ev
        for k in writes:
            self.lastw[k] = ev
            self.readers[k] = {}

    def op(self, eng, fn, reads=(), writes=()):
        waits = self._collect(eng, reads, writes)
        self.count[eng] += 1
        n = self.count[eng]
        vc = dict(self.known[eng])
        vc[eng] = n
        ev = (eng, n, vc)
        self.streams[eng].append((waits, fn, (eng, 1)))
        self._record(ev, reads, writes)
        return ev

    def dma(self, q, semname, fn, reads=(), writes=()):
        waits = self._collect(q, reads, writes)
        self.dma_count[semname] = self.dma_count.get(semname, 0) + 1
        v = 16 * self.dma_count[semname]
        vc = dict(self.known[q])
        vc[semname] = v
        ev = (semname, v, vc)
        self.streams[q].append((waits, fn, (semname, 16)))
        self._record(ev, reads, writes)
        return ev

    def barrier(self, engines=("act", "dve", "sp")):
        for e in engines:
            kn = self.known[e]
            waits = []
            for e2 in ("pe", "act", "dve"):
                c = self.count[e2]
                if c and kn.get(e2, 0) < c:
                    waits.append((e2, c))
                    kn[e2] = c
            for s, c in self.dma_count.items():
                if s.startswith("ring") or s.startswith("d_y") or s.startswith("d_x"):
                    continue
                if kn.get(s, 0) < 16 * c:
                    waits.append((s, 16 * c))
                    kn[s] = 16 * c
            if waits:
                self.streams[e].append((waits, None, None))

    def finish(self, eng="sp"):
        kn = self.known[eng]
        waits = []
        for s, c in self.dma_count.items():
            if kn.get(s, 0) < 16 * c:
                waits.append((s, 16 * c))
                kn[s] = 16 * c
        for e in ("pe", "act", "dve"):
            if self.count[e] and kn.get(e, 0) < self.count[e]:
                waits.append((e, self.count[e]))
        self.streams[eng].append((waits, None, None))

    def emit(self):
        nc = self.nc
        for s in list(ENGS) + list(self.dma_count):
            if s not in self.sems:
                self.sems[s] = self.stack.enter_context(nc.semaphore(s))
        with nc.Block() as block:
            def run(engine, items):
                for waits, fn, inc in items:
                    for s, v in waits:
                        engine.wait_ge(self.sems[s], v)
                    if fn is not None:
                        ins = fn(engine)
                        ins.then_inc(self.sems[inc[0]], inc[1])

            @block.tensor
            def _(eng):
                run(eng, self.streams["pe"])

            @block.scalar
            def _(eng):
                run(eng, self.streams["act"])

            @block.vector
            def _(eng):
                run(eng, self.streams["dve"])

            @block.gpsimd
            def _(eng):
                run(eng, self.streams["pool"])

            @block.sync
            def _(eng):
                run(eng, self.streams["sp"])


NBLK = 33


def _kc(W, nk):
    C = W.shape[1]
    return np.ascontiguousarray(W.reshape(nk, 128, C).transpose(1, 0, 2)).reshape(128, nk * C)


def _pad(a):
    out = np.zeros((128, 4096), np.float32)
    out[:, : a.shape[1]] = a
    return out


def build_wall(w_in, w_pool_up, w_hgrn_up, w_out, w_g, w_u, w_d, w_pg, w_pp):
    blocks = []
    blocks.append(_kc(w_in[:, 0:512], 8))
    zz = w_in[:, 512:2560].reshape(8, 128, 4, 4, 128)
    ga = w_in[:, 2560:3584].reshape(8, 128, 8, 128)
    gb = w_in[:, 3584:4608].reshape(8, 128, 8, 128)
    for h in range(4):
        blocks.append(np.ascontiguousarray(zz[:, :, :, h, :].transpose(1, 0, 2, 3)).reshape(128, 4096))
        gg = np.stack([ga[:, :, 2 * h], gb[:, :, 2 * h], ga[:, :, 2 * h + 1], gb[:, :, 2 * h + 1]], axis=2)
        blocks.append(np.ascontiguousarray(gg.transpose(1, 0, 2, 3)).reshape(128, 4096))
    for half in range(2):
        yy = np.stack([w_pool_up[:, half * 512:(half + 1) * 512].reshape(4, 128, 512),
                       w_hgrn_up[:, half * 512:(half + 1) * 512].reshape(4, 128, 512)], axis=2)
        blocks.append(np.ascontiguousarray(yy.transpose(1, 0, 2, 3)).reshape(128, 4096))
    blocks.append(_kc(w_out[:, 0:512], 8))
    blocks.append(_kc(w_out[:, 512:1024], 8))
    for jb in range(11):
        gu = np.stack([w_g[:, jb * 256:(jb + 1) * 256], w_u[:, jb * 256:(jb + 1) * 256]], axis=1)
        blocks.append(_kc(gu.reshape(1024, 512), 8))
    for half in range(2):
        for kb in range(3):
            nk = 8 if kb < 2 else 6
            blocks.append(_pad(_kc(w_d[kb * 1024: kb * 1024 + nk * 128, half * 512:(half + 1) * 512], nk)))
    blocks.append(_kc(w_pg[:, 0:512], 8))
    blocks.append(_pad(_kc(w_pp, 2)))
    blocks.append(_kc(w_pg[:, 512:1024], 8))
    assert len(blocks) == NBLK
    return np.ascontiguousarray(np.stack(blocks, axis=0).astype(np.float32))


def build_consts():
    c = {}
    c["ident"] = np.eye(128, dtype=np.float32)
    s = np.arange(128)[:, None]
    t = np.arange(128)[None, :]
    c["maskP"] = ((s // 64 == t // 64) & (s <= t)).astype(np.float32)
    c["maskS"] = ((s // 8 == t // 8) & (s <= t)).astype(np.float32)
    c["seqm"] = (s // 8 == np.arange(16)[None, :]).astype(np.float32)
    r = np.ones(640, np.float32)
    r[0:512:64] = 0.0
    r[512:640:8] = 0.0
    c["resetm"] = np.broadcast_to(r, (128, 640)).copy()
    rc = np.zeros((4, 16), np.float32)
    for g, w in enumerate((2, 4, 8, 16)):
        rc[g] = 1.0 / np.minimum(np.arange(16) + 1, w)
    c["rc"] = np.broadcast_to(rc.reshape(1, 64), (128, 64)).copy()
    order = ["ident", "seqm", "rc", "maskP", "maskS", "resetm"]
    offs = {}
    o = 0
    for k in order:
        offs[k] = (o, c[k].shape[1])
        o += c[k].shape[1]
    return np.ascontiguousarray(np.concatenate([c[k] for k in order], axis=1)), offs


CONST_ARR, COFF = build_consts()
CW = CONST_ARR.shape[1]
CW_KEEP = COFF["maskP"][0]


class _Stop(Exception):
    pass


STOP = None


def ck(name):
    if STOP == name:
        raise _Stop()


def build_program(ntiles=NTILES):
    nc = bass.Bass("TRN2", target_bir_lowering=False)

    def din(name, shape):
        return nc.dram_tensor(name, shape, F32, kind="ExternalInput").ap()

    def dout(name, shape):
        return nc.dram_tensor(name, shape, F32, kind="ExternalOutput").ap()

    xp = din("xp", [2048, 1024])
    xs = din("xs", [128, 1024])
    ppd = din("pp", [2048, 256])
    psd = din("ps", [128, 256])
    spool = din("spool", [240, 512])
    shg = din("shg", [16, 4, 128, 128])
    wall = din("wall", [NBLK, 128, 4096])
    cst = din("cst", [128, CW])
    gvec = din("gvec", [4, 1024])
    smalld = din("small", [128, 16])
    pmixd = din("pmix", [128, 512])
    yp = dout("yp", [2048, 1024])
    ys = dout("ys", [128, 1024])
    npp = dout("npp", [15, 512])
    nhp = dout("nhp", [4, 128, 128])
    nps = dout("nps", [240, 512])
    nhs = dout("nhs", [16, 4, 128, 128])

    with ExitStack() as st:
        P = Prog(nc, st)

        def sb(name, shape, dt):
            return st.enter_context(nc.sbuf_tensor("sb_" + name, shape, dt))

        xres = sb("xres", [128, 5, 1024], F32)
        hT = sb("hT", [128, 8, 640], BF16)
        hn = [sb(f"hn{i}", [128, 1024], BF16) for i in range(2)]
        junk = sb("junk", [128, 512], BF16)
        ystage = [sb(f"ystage{i}", [128, 512], F32) for i in range(4)]
        resetmb = sb("resetmb", [128, 640], BF16)
        ring = [sb(f"ring{i}", [128, 4096], BF16) for i in range(NRING)]
        gB = sb("gB", [128, 4, 1024], F32)
        cs = sb("cs", [128, CW_KEEP], F32)
        identb = sb("identb", [128, 128], BF16)
        onesd = sb("onesd", [128, 128], BF16)
        maskPb = sb("maskPb", [128, 128], BF16)
        maskSb = sb("maskSb", [128, 128], BF16)
        pmixf = sb("pmixf", [128, 512], F32)
        pmix = sb("pmix", [128, 4, 128], BF16)
        small = sb("small", [128, 16], F32)
        lbv = sb("lbv", [128, 8], F32)
        carry = sb("carry", [128, 4, 128], F32)
        ucarry = sb("ucarry", [128, 4, 16], F32)
        ssq = sb("ssq", [128, 16], F32)
        ssq1 = sb("ssq1", [128, 8], F32)
        dmy = sb("dmy", [128, 2], F32)
        rst1 = sb("rst1", [128, 4], F32)
        rst = sb("rst", [128, 8], F32)
        identf = cs[:, COFF["ident"][0]:COFF["ident"][0] + 128]
        seqm = cs[:, COFF["seqm"][0]:COFF["seqm"][0] + 16]
        rcv = cs[:, COFF["rc"][0]:COFF["rc"][0] + 64]
        resetm = resetmb

        MIX_BYTES = 0
        scr_plan = {}

        def plan(phase, name, nelem, dt):
            nonlocal MIX_BYTES
            nbytes = nelem * (4 if dt == F32 else 2)
            nbytes = (nbytes + 31) // 32 * 32
            off = scr_plan.setdefault(("off", phase), 0)
            scr_plan[name] = (off, nelem, dt)
            scr_plan[("off", phase)] = off + nbytes

        U_NAMES = ["ug", "ue", "pa", "pb_", "sa", "sb_", "tmp16", "pooled", "npsb", "stg", "npp_sb"]
        for name, n, dt in [
            ("ug", 528, F32), ("ue", 16 * 24, F32), ("pa", 528, F32), ("pb_", 528, F32),
            ("sa", 16 * 24, F32), ("sb_", 16 * 24, F32), ("tmp16", 16, F32),
            ("pooled", 4 * 640, BF16), ("npsb", 4 * 240, F32),
            ("stg", 2 * 512, F32), ("npp_sb", 512, F32), ("pool_out", 4 * 640, BF16),
            ("qf", 640, F32), ("t1", 640, F32), ("t2", 640, F32), ("t3", 640, F32), ("t4", 640, F32),
            ("qe", 640, BF16), ("qe2", 640, BF16), ("keb", 640, BF16), ("kdb", 640, BF16), ("vb", 640, BF16), ("gbf", 640, BF16), ("gbf2", 640, BF16),
            ("kdT", 5 * 128, BF16), ("vT", 5 * 128, BF16), ("Sall", 9 * 128, F32), ("Sbf", 8 * 128, BF16),
            ("dec", 24, F32), ("dec2", 24, F32), ("Am", 5 * 128, BF16), ("osq", 640, BF16), ("o_fin", 4 * 640, BF16),
            ("S0", 16 * 128, F32), ("S0bf", 16 * 128, BF16), ("Vm", 16 * 128, BF16),
            ("merged", 8 * 640, BF16), ("s1", 640, F32), ("s2", 640, F32),
        ]:
            plan("mix", name, n, dt)
        for name, n, dt in [
            ("hidden", 22 * 640, BF16), ("ftmp", 640, F32), ("sgt0", 512, F32), ("sgt1", 512, F32),
            ("pT", 2 * 640, BF16), ("pstage", 5 * 256, F32), ("pbf", 256, BF16),
        ]:
            plan("ffn", name, n, dt)
        u_end = scr_plan["pool_out"][0]
        assert 2 * 8 * 640 * 2 <= u_end, u_end
        scr_plan["sgA"] = (0, 8 * 640, BF16)
        scr_plan["sgB"] = (8 * 640 * 2, 8 * 640, BF16)
        scr_bytes = max(scr_plan[("off", "mix")], scr_plan[("off", "ffn")])
        scr = sb("scr", [128, scr_bytes // 4], F32)

        def sv(name):
            off, n, dt = scr_plan[name]
            if dt == F32:
                return scr[:, off // 4: off // 4 + n]
            return scr[:, off // 4: off // 4 + n // 2].bitcast(BF16)

        ug = sv("ug"); ue = sv("ue").rearrange("p (i r) -> p i r", r=24)
        pa = sv("pa"); pb_ = sv("pb_")
        sa = sv("sa").rearrange("p (i r) -> p i r", r=24); sb_ = sv("sb_").rearrange("p (i r) -> p i r", r=24)
        tmp16 = sv("tmp16")
        pooled = sv("pooled").rearrange("p (g t) -> p g t", g=4)
        pool_out = sv("pool_out").rearrange("p (g t) -> p g t", g=4)
        npsb = sv("npsb").rearrange("p (g r) -> p g r", g=4)
        stg = sv("stg").rearrange("p (h c) -> p h c", h=2)
        nps_sb = stg
        npp_sb = sv("npp_sb")
        qf = sv("qf"); t1 = sv("t1"); t2 = sv("t2"); t3 = sv("t3"); t4 = sv("t4")
        qe = sv("qe"); qe2 = sv("qe2"); keb = sv("keb"); kdb = sv("kdb"); vb = sv("vb"); gbf = sv("gbf"); gbf2 = sv("gbf2")
        kdT = sv("kdT").rearrange("p (b k) -> p b k", b=5); vT = sv("vT").rearrange("p (b k) -> p b k", b=5)
        Sall = sv("Sall").rearrange("p (c v) -> p c v", c=9); Sbf = sv("Sbf").rearrange("p (c v) -> p c v", c=8)
        dec = sv("dec"); dec2 = sv("dec2"); Am = sv("Am").rearrange("p (b k) -> p b k", b=5); osq = sv("osq")
        o_fin = sv("o_fin").rearrange("p (h t) -> p h t", h=4)
        S0 = sv("S0").rearrange("p (i v) -> p i v", i=16); S0bf = sv("S0bf").rearrange("p (i v) -> p i v", i=16)
        Vm = sv("Vm").rearrange("p (i v) -> p i v", i=16)
        merged = sv("merged").rearrange("p (k t) -> p k t", k=8); s1 = sv("s1"); s2 = sv("s2")
        sgA = sv("sgA").rearrange("p (k t) -> p k t", k=8); sgB = sv("sgB").rearrange("p (k t) -> p k t", k=8)
        hidden = sv("hidden").rearrange("p (f t) -> p f t", f=22); ftmp = sv("ftmp")
        assert scr_plan["S0bf"][0] == scr_plan["S0"][0] + 8192 and scr_plan["Vm"][0] == scr_plan["S0"][0] + 12288
        assert scr_plan["S0"][0] >= scr_plan[("off", "ffn")]
        xo = scr_plan["S0"][0] // 4
        xalt = scr[:, xo:xo + 4096].rearrange("p (b d) -> p b d", b=4)
        sgt = [sv("sgt0"), sv("sgt1")]
        pT = sv("pT").rearrange("p (k t) -> p k t", k=2); pstage = sv("pstage").rearrange("p (b c) -> p b c", b=5); pbf = sv("pbf")

        pbs = [st.enter_context(nc.psum_tensor(f"pb{i}", [128, 512], F32)) for i in range(8)]
        bank_ctr = [0]

        def nb():
            b = bank_ctr[0] % 8
            bank_ctr[0] += 1
            return b

        def PB(b):
            return ("pb", b)

        wstate = {"tile": 0, "issued": set(), "released": set(), "total": NBLK * ntiles}

        def w_issue_n(n, extra_reads=()):
            if n >= wstate["total"] or n in wstate["issued"]:
                return
            slot = n % NRING
            blk = n % NBLK
            P.dma("pool", f"ring{slot}",
                  lambda e, slot=slot, blk=blk: e.dma_start(out=ring[slot][:], in_=wall[blk]),
                  reads=list(extra_reads), writes=[("ring", slot)])
            wstate["issued"].add(n)

        def w_issue(extra_reads=()):
            w_issue_n(len(wstate["issued"]), extra_reads)

        def w_get(expect):
            n = wstate["tile"] * NBLK + expect
            assert n in wstate["issued"], ("ring too small / block not prefetched", n)
            assert n - NRING < 0 or (n - NRING) in wstate["released"]
            slot = n % NRING
            return ring[slot], ("ring", slot)

        def w_release(expect):
            n = wstate["tile"] * NBLK + expect
            assert n in wstate["issued"] and n not in wstate["released"]
            wstate["released"].add(n)
            w_issue_n(n + NRING)

        def A(eng, fn, reads, writes):
            return P.op(eng, fn, reads=reads, writes=writes)

        def mm(out, lhsT, rhs, start, stop, reads, bank):
            A("pe", lambda e: e.matmul(out, lhsT=lhsT, rhs=rhs, start=start, stop=stop), reads, [PB(bank)])

        def tp(out, in_, ident, reads, bank):
            A("pe", lambda e: e.transpose(out=out, in_=in_, identity=ident), reads, [PB(bank)])

        def act(out, in_, func, reads, writes, **kw):
            A("act", lambda e: e.activation(out=out, in_=in_, func=func, **kw), reads, writes)

        def tt(out, in0, in1, op, reads, writes, eng="dve"):
            A(eng, lambda e: e.tensor_tensor(out=out, in0=in0, in1=in1, op=op), reads, writes)

        def ts(out, in0, s1_, s2_, op0, op1, reads, writes):
            A("dve", lambda e: e.tensor_scalar(out=out, in0=in0, scalar1=s1_, scalar2=s2_, op0=op0, op1=op1), reads, writes)

        def stt(out, in0, scalar, in1, op0, op1, reads, writes):
            A("dve", lambda e: e.scalar_tensor_tensor(out=out, in0=in0, scalar=scalar, in1=in1, op0=op0, op1=op1), reads, writes)

        def cp(out, in_, reads, writes, eng="dve"):
            if eng == "act":
                act(out, in_, AF.Copy, reads, writes)
            else:
                A(eng, lambda e: e.tensor_copy(out=out, in_=in_), reads, writes)

        def warm(func):
            act(dmy[:, 1:2], dmy[:, 0:1], func, ["dmy0"], ["dmy1"])

        A("dve", lambda e: e.memset(dmy[:], 1.0), [], ["dmy0", "dmy1"])
        P.dma("sp", "d_cs", lambda e: e.dma_start(out=cs[:], in_=cst[:, 0:CW_KEEP]), writes=["cs"])
        cstage = scr[:, 0:CW - CW_KEEP]
        P.dma("sp", "d_cs2", lambda e: e.dma_start(out=cstage, in_=cst[:, CW_KEEP:CW]), writes=["cstage"])
        maskPf = cstage[:, 0:128]
        maskSf = cstage[:, 128:256]
        resetmf = cstage[:, 256:896]
        P.dma("sp", "d_small", lambda e: e.dma_start(out=small[:], in_=smalld), writes=["small"])
        P.dma("sp", "d_pmix", lambda e: e.dma_start(out=pmixf[:], in_=pmixd), writes=["pmixf"])
        for gi in range(4):
            P.dma("sp", f"d_g{gi}",
                  lambda e, gi=gi: e.dma_start(out=gB[:, gi, :], in_=gvec[gi:gi + 1, :].broadcast_to([128, 1024])),
                  writes=[("gB", gi)])
        cp(identb[:], identf, ["cs"], ["identb"])
        cp(maskPb[:], maskPf, ["cstage"], ["maskPb"])
        cp(maskSb[:], maskSf, ["cstage"], ["maskSb"])
        cp(resetmb[:], resetmf, ["cstage"], ["resetmb"])
        cp(pmix[:].rearrange("p g c -> p (g c)"), pmixf[:], ["pmixf"], ["pmix"])
        A("dve", lambda e: e.memset(onesd[:], 1.0 / 128.0), [], ["onesd"])
        A("dve", lambda e: e.memset(carry[:].rearrange("p h v -> p (h v)"), 0.0), [], ["carry"])
        A("dve", lambda e: e.memset(ucarry[:].rearrange("p g t -> p (g t)"), 0.0), [], ["ucarry"])
        tt(lbv[:, 0:4], small[:, 8:12], small[:, 12:16], ALU.subtract, ["small"], ["lbv"])
        act(lbv[:, 0:4], lbv[:, 0:4], AF.Sigmoid, ["lbv"], ["lbv"])
        ts(lbv[:, 4:8], lbv[:, 0:4], -1.0, 1.0, ALU.mult, ALU.add, ["lbv"], ["lbv"])

        def run_tile(ti):
            wstate["tile"] = ti
            has_s = ti == 0
            last = ti == 3
            NT = 640 if has_s else 512
            blocks = [0, 1, 2, 3] + ([4] if has_s else [])
            subs = [(0, 512, "p")] + ([(512, 128, "s")] if has_s else [])
            hTk_p = [("hT", b) for b in range(4)]
            hTk = {"p": hTk_p, "s": [("hT", 4)]}

            def K(name):
                return [(name, "p")] + ([(name, "s")] if has_s else [])

            def xbuf(tj):
                return (xres, "xres") if tj % 2 == 0 else (xalt, "xalt")

            X, xk = xbuf(ti)

            def load_x(tj, b):
                src = xp[tj * 512 + b * 128: tj * 512 + (b + 1) * 128, :] if b < 4 else xs
                Xn, xkn = xbuf(tj)
                P.dma("sp", f"d_x{tj % 2}_{b}", lambda e, b=b, src=src, Xn=Xn: e.dma_start(out=Xn[:, b, :], in_=src),
                      writes=[(xkn, b)])

            if ti == 0:
                for b in blocks:
                    load_x(0, b)
            if ti == 0:
                for _ in range(NRING):
                    w_issue(extra_reads=[(xk, b) for b in blocks])
            if has_s:
                for half in range(2):
                    P.dma("sp", f"d_stg{half}",
                          lambda e, half=half: e.dma_start(out=stg[0:120, half, :], in_=spool[half * 120:(half + 1) * 120, :]),
                          writes=[("stg", half)])

            def half_stats(b, half, X_=None, xk_=None, ssq_=None, tag="ssq"):
                X_ = X if X_ is None else X_
                xk_ = xk if xk_ is None else xk_
                ssq_ = ssq if ssq_ is None else ssq_
                act(junk[:, 0:512], X_[:, b, half * 512:(half + 1) * 512], AF.Square, [(xk_, b)], ["junk", (tag, b, half)],
                    accum_out=ssq_[:, 2 * b + half:2 * b + half + 1])

            def norm_stats(have_partials, blks=None, X_=None, xk_=None, ssq_=None, rst_=None, tag="ssq", rtag="rst"):
                blks = blocks if blks is None else blks
                ssq_ = ssq if ssq_ is None else ssq_
                rst_ = rst if rst_ is None else rst_
                if not have_partials:
                    for b in blks:
                        for half in range(2):
                            half_stats(b, half, X_, xk_, ssq_, tag)
                nbk = len(blks)
                tt(rst_[:, 0:nbk], ssq_[:, 0:2 * nbk:2], ssq_[:, 1:2 * nbk:2], ALU.add,
                   [(tag, b, hf) for b in blks for hf in range(2)], [rtag])
                act(rst_[:, 0:nbk], rst_[:, 0:nbk], AF.Ln, [rtag], [rtag], scale=1.0 / 1024.0, bias=EPS)
                act(rst_[:, 0:nbk], rst_[:, 0:nbk], AF.Exp, [rtag], [rtag], scale=-0.5)

            def norm_apply(gi, blks=None, X_=None, xk_=None, rst_=None, rtag="rst"):
                blks = blocks if blks is None else blks
                X_ = X if X_ is None else X_
                xk_ = xk if xk_ is None else xk_
                rst_ = rst if rst_ is None else rst_
                for b in blks:
                    hb = hn[b % 2]
                    hk = f"hn{b % 2}"
                    stt(hb[:], X_[:, b, :], rst_[:, b:b + 1], gB[:, gi, :], ALU.mult, ALU.mult,
                        [(xk_, b), rtag, ("gB", gi)], [hk])
                    bank = nb()
                    bv = pbs[bank][:].bitcast(BF16)
                    for k in range(8):
                        tp(bv[:, k * 128:(k + 1) * 128], hb[:, k * 128:(k + 1) * 128], identb[:], [hk, "identb"], bank)
                    c0 = b * 128
                    cp(hT[:, :, c0:c0 + 128], bv.rearrange("p (k t) -> p k t", k=8), [PB(bank)], [("hT", b)], eng="act")

            def norm_T(gi, have_partials):
                norm_stats(have_partials)
                norm_apply(gi)

            ck("load")
            if ti == 0:
                norm_T(0, False)
            ck("norm1")

            wv, wk = w_get(0)
            wu = wv[:].rearrange("p (k c) -> p k c", k=8)
            ubanks = []
            for g in range(4):
                bp = nb()
                bs = nb() if has_s else None
                for k in range(8):
                    mm(pbs[bp][:, 0:512], wu[:, k, g * 128:(g + 1) * 128], hT[:, k, 0:512], k == 0, k == 7, [wk] + hTk_p, bp)
                    if has_s:
                        mm(pbs[bs][:, 0:128], wu[:, k, g * 128:(g + 1) * 128], hT[:, k, 512:640], k == 0, k == 7, [wk, ("hT", 4)], bs)
                ubanks.append((bp, bs))
                if g == 3:
                    w_release(0)
                w = 2 << g
                cp(ug[:, 0:16], ucarry[:, g, :], ["ucarry"], ["ug"])
                cp(ug[:, 16:528], pbs[bp][:, 0:512], [PB(bp)], ["ug"], eng="act")
                tt(pa[:, 1:528], ug[:, 1:528], ug[:, 0:527], ALU.add, ["ug"], ["pa"])
                sw = pa
                swk = "pa"
                if w >= 4:
                    tt(pb_[:, 3:528], pa[:, 3:528], pa[:, 1:526], ALU.add, ["pa"], ["pb_"])
                    sw, swk = pb_, "pb_"
                if w >= 8:
                    tt(pa[:, 7:528], pb_[:, 7:528], pb_[:, 3:524], ALU.add, ["pb_"], ["pa"])
                    sw, swk = pa, "pa"
                if w >= 16:
                    tt(pb_[:, 15:528], pa[:, 15:528], pa[:, 7:520], ALU.add, ["pa"], ["pb_"])
                    sw, swk = pb_, "pb_"
                stt(pooled[:, g, 0:512], sw[:, 16:528], 1.0 / w, ug[:, 16:528], ALU.mult, ALU.subtract,
                    [swk, "ug"], [("pooled", g, "p")])
                if ti == 0:
                    tt(tmp16[:], sw[:, 16:32], rcv[:, g * 16:(g + 1) * 16], ALU.mult, [swk, "cs"], ["tmp16"])
                    tt(pooled[:, g, 0:16], tmp16[:], ug[:, 16:32], ALU.subtract, ["tmp16", "ug"], [("pooled", g, "p")])
                cp(ucarry[:, g, :], ug[:, 512:528], ["ug"], ["ucarry"])
                if last:
                    bt = nb()
                    tp(pbs[bt][0:15, 0:128], ug[:, 513:528], identf, ["ug", "cs"], bt)
                    cp(npp_sb[0:15, g * 128:(g + 1) * 128], pbs[bt][0:15, 0:128], [PB(bt)], ["npp_sb"], eng="act")
                if has_s:
                    bt = nb()
                    for half in range(2):
                        tp(pbs[bt][:, half * 120:(half + 1) * 120], stg[0:120, half, g * 128:(g + 1) * 128],
                           identf[0:120, 0:120], [("stg", half), "cs"], bt)
                    cp(ue[:, :, 0:15], pbs[bt][:, 0:240].rearrange("p (i r) -> p i r", r=15), [PB(bt)], ["ue"], eng="act")
                    cp(ue[:, :, 15:23], pbs[bs][:, 0:128].rearrange("p (i t) -> p i t", t=8), [PB(bs)], ["ue"], eng="act")
                    tt(sa[:, :, 1:23], ue[:, :, 1:23], ue[:, :, 0:22], ALU.add, ["ue"], ["sa"])
                    ssw, sswk = sa, "sa"
                    if w >= 4:
                        tt(sb_[:, :, 3:23], sa[:, :, 3:23], sa[:, :, 1:21], ALU.add, ["sa"], ["sb_"])
                        ssw, sswk = sb_, "sb_"
                    if w >= 8:
                        tt(sa[:, :, 7:23], sb_[:, :, 7:23], sb_[:, :, 3:19], ALU.add, ["sb_"], ["sa"])
                        ssw, sswk = sa, "sa"
                    if w >= 16:
                        tt(sb_[:, :, 15:23], sa[:, :, 15:23], sa[:, :, 7:15], ALU.add, ["sa"], ["sb_"])
                        ssw, sswk = sb_, "sb_"
                    stt(pooled[:, g, 512:640].rearrange("p (i t) -> p i t", t=8), ssw[:, :, 15:23], 1.0 / w,
                        ue[:, :, 15:23], ALU.mult, ALU.subtract, [sswk, "ue"], [("pooled", g, "s")])
                    cp(npsb[:, g, :].rearrange("p (i r) -> p i r", r=15), ue[:, :, 8:23], ["ue"], [("npsb", g)])
            if last:
                P.dma("sp", "d_npp", lambda e: e.dma_start(out=npp, in_=npp_sb[0:15, :]), reads=["npp_sb"])
            if has_s:
                for half in range(2):
                    bt = nb()
                    for g in range(4):
                        tp(pbs[bt][0:120, g * 128:(g + 1) * 128], npsb[:, g, half * 120:(half + 1) * 120], identf,
                           [("npsb", g), "cs"], bt)
                    cp(nps_sb[0:120, half, :], pbs[bt][0:120, 0:512], [PB(bt)], [("stg", half)], eng="act")
                    P.dma("sp", f"d_nps{half}",
                          lambda e, half=half: e.dma_start(out=nps[half * 120:(half + 1) * 120, :], in_=nps_sb[0:120, half, :]),
                          reads=[("stg", half)])
            ck("u")
            PIPE = not has_s
            if PIPE:
                zc, rcn = [0], [0]

                def nbZ():
                    b = zc[0] % 4
                    zc[0] += 1
                    return b

                def nbR():
                    b = 6 + rcn[0] % 2
                    rcn[0] += 1
                    return b

                gcn = [0]

                def nbG():
                    b = 4 + gcn[0] % 2
                    gcn[0] += 1
                    return b
            else:
                nbZ = nbR = nbG = nb
            gb2 = [gbf, gbf2]
            QE = [qe, qe2]
            DEC = [dec, dec2]
            HS = {h: {} for h in range(4)}

            def poolmix():
                for g in range(4):
                    for (c0, n, sk) in subs:
                        bk = nbR()
                        mm(pbs[bk][:, 0:n], pmix[:, g, :], pooled[:, g, c0:c0 + n], True, True, ["pmix", ("pooled", g, sk)], bk)
                        act(pool_out[:, g, c0:c0 + n], pbs[bk][:, 0:n], AF.Copy, [PB(bk), "small"], [("pool_out", g, sk)],
                            scale=small[:, g:g + 1])

            def z_group(h, j):
                S_ = HS[h]
                if j == 0:
                    wv, wk = w_get(1 + 2 * h)
                    S_["wh"] = wv[:].rearrange("p (k j c) -> p k j c", k=8, j=4)
                    S_["wk"] = wk
                    S_["bs"] = nbZ() if has_s else None
                    S_["zb"] = []
                wh, wk, bs = S_["wh"], S_["wk"], S_["bs"]
                bp = nbZ()
                for k in range(8):
                    mm(pbs[bp][:, 0:512], wh[:, k, j, :], hT[:, k, 0:512], k == 0, k == 7, [wk] + hTk_p, bp)
                    if has_s:
                        mm(pbs[bs][:, j * 128:(j + 1) * 128], wh[:, k, j, :], hT[:, k, 512:640], k == 0, k == 7,
                           [wk, ("hT", 4)], bs)
                S_["zb"].append(bp)
                if j == 3:
                    w_release(1 + 2 * h)

            def z_evac(h):
                S_ = HS[h]
                zb, bs = S_["zb"], S_["bs"]
                gbh = gb2[h % 2]
                gbk = f"gbf{h % 2}"

                def zsrc(j, sk):
                    return (pbs[zb[j]][:, 0:512], PB(zb[j])) if sk == "p" else (pbs[bs][:, j * 128:(j + 1) * 128], PB(bs))

                for (c0, n, sk) in subs:
                    src, key = zsrc(1, sk)
                    act(t1[:, c0:c0 + n], src, AF.Sigmoid, [key], [("t1", sk)])
                    src, key = zsrc(0, sk)
                    act(qf[:, c0:c0 + n], src, AF.Sigmoid, [key], [("qf", sk)])
                    tt(qf[:, c0:c0 + n], qf[:, c0:c0 + n], src, ALU.mult, [("qf", sk), key], [("qf", sk)])
                    src, key = zsrc(3, sk)
                    act(t4[:, c0:c0 + n], src, AF.Sigmoid, [key], [("t4", sk)])
                    tt(gbh[:, c0:c0 + n], t4[:, c0:c0 + n], src, ALU.mult, [("t4", sk), key], [(gbk, sk)])
                    src, key = zsrc(2, sk)
                    cp(vb[:, c0:c0 + n], src, [key], [("vb", sk)])
                warm(AF.Ln)

            def g_mm(h, jj):
                S_ = HS[h]
                if jj == 0:
                    wgv_, wgk_ = w_get(2 + 2 * h)
                    S_["wgv"] = wgv_[:].rearrange("p (k t c) -> p k t c", k=8, t=4)
                    S_["wgk"] = wgk_
                    S_["gate_ev"] = []
                wgv, wgk_ = S_["wgv"], S_["wgk"]
                j = 2 * h + jj
                bsg = nbG() if has_s else None
                for t_ in range(2):
                    bpg = nbG()
                    for k in range(8):
                        mm(pbs[bpg][:, 0:512], wgv[:, k, 2 * jj + t_, :], hT[:, k, 0:512], k == 0, k == 7, [wgk_] + hTk_p, bpg)
                        if has_s:
                            mm(pbs[bsg][:, t_ * 128:(t_ + 1) * 128], wgv[:, k, 2 * jj + t_, :], hT[:, k, 512:640], k == 0, k == 7,
                               [wgk_, ("hT", 4)], bsg)
                    S_["gate_ev"].append((j, t_, bpg, bsg))
                if jj == 1:
                    w_release(2 + 2 * h)

            def g_evac(h, jj):
                for (j, t_, bpg, bsg) in HS[h]["gate_ev"][2 * jj:2 * jj + 2]:
                    dst = sgA if t_ == 0 else sgB
                    dk = "sgA" if t_ == 0 else "sgB"
                    cp(dst[:, j, 0:512], pbs[bpg][:, 0:512], [PB(bpg)], [(dk, j, "p")], eng="act")
                    if has_s:
                        cp(dst[:, j, 512:640], pbs[bsg][:, t_ * 128:(t_ + 1) * 128], [PB(bsg)], [(dk, j, "s")], eng="act")

            def chain_a(h):
                ts(t1[:, 0:NT], t1[:, 0:NT], lbv[:, 4 + h:5 + h], lbv[:, h:h + 1], ALU.mult, ALU.add, K("t1") + ["lbv"], K("t1"))
                ts(t2[:, 0:NT], t1[:, 0:NT], -1.0, 1.0, ALU.mult, ALU.add, K("t1"), K("t2"))
                act(t1[:, 0:NT], t1[:, 0:NT], AF.Ln, K("t1"), K("t1"))

            def cb_scan(h):
                A("dve", lambda e: e.tensor_tensor_scan(out=t3[:, 0:NT], data0=resetm[:, 0:NT], data1=t1[:, 0:NT],
                                                        initial=0.0, op0=ALU.mult, op1=ALU.add),
                  K("t1") + ["resetmb"], K("t3"))

            def cb_exp(h):
                act(t1[:, 0:NT], t3[:, 0:NT], AF.Exp, K("t3"), K("t1"))
                act(t4[:, 0:NT], t3[:, 0:NT], AF.Exp, K("t3"), K("t4"), scale=-1.0)

            def cb_mul(h):
                qe_ = QE[h % 2]
                qk = f"qe{h % 2}"
                tt(qe_[:, 0:NT], qf[:, 0:NT], t1[:, 0:NT], ALU.mult, K("qf") + K("t1"), [(qk, "p")] + ([(qk, "s")] if has_s else []))
                tt(t2[:, 0:NT], t2[:, 0:NT], t4[:, 0:NT], ALU.mult, K("t2") + K("t4"), K("t2"))

            def cb_keb(h):
                cp(keb[:, 0:NT], t2[:, 0:NT], K("t2"), K("keb"), eng="act")

            def cb_dec(h):
                dec_ = DEC[h % 2]
                dk_ = f"dec{h % 2}"
                cp(dec_[:, 0:8], t1[:, 63:512:64], [("t1", "p")], [(dk_, "p")])
                tt(kdb[:, 0:512].rearrange("p (c t) -> p c t", t=64), t2[:, 0:512].rearrange("p (c t) -> p c t", t=64),
                   dec_[:, 0:8].unsqueeze(2).broadcast_to([128, 8, 64]), ALU.mult, [("t2", "p"), (dk_, "p")], [("kdb", "p")])
                if has_s:
                    cp(dec_[:, 8:24], t1[:, 519:640:8], [("t1", "s")], [(dk_, "s")])
                    tt(kdb[:, 512:640].rearrange("p (c t) -> p c t", t=8), t2[:, 512:640].rearrange("p (c t) -> p c t", t=8),
                       dec_[:, 8:24].unsqueeze(2).broadcast_to([128, 16, 8]), ALU.mult, [("t2", "s"), (dk_, "s")], [("kdb", "s")])

            def chain_b(h):
                cb_scan(h); cb_exp(h); cb_mul(h); cb_keb(h); cb_dec(h)

            def R1(h):
                qe = QE[h % 2]
                qk = f"qe{h % 2}"
                for (src_, srck, dst, dstk) in ((kdb, "kdb", kdT, "kdT"), (vb, "vb", vT, "vT")):
                    bt = nbR()
                    bv = pbs[bt][:].bitcast(BF16)
                    for b in blocks:
                        sk = "p" if b < 4 else "s"
                        tp(bv[:, b * 128:(b + 1) * 128], src_[:, b * 128:(b + 1) * 128], identb[:], [(srck, sk), "identb"], bt)
                    nbk = len(blocks)
                    cp(dst[:, 0:nbk, :], bv[:, 0:nbk * 128].rearrange("p (b k) -> p b k", k=128), [PB(bt)], [dstk], eng="act")
                ba = nbR()
                for b in range(4):
                    mm(pbs[ba][:, b * 128:(b + 1) * 128], keb[:, b * 128:(b + 1) * 128], qe[:, b * 128:(b + 1) * 128],
                       True, True, [("keb", "p"), (qk, "p")], ba)
                tt(Am[:, 0:4, :], pbs[ba][:, 0:512].rearrange("p (b t) -> p b t", b=4),
                   maskPb[:].unsqueeze(1).broadcast_to([128, 4, 128]), ALU.mult, [PB(ba), "maskPb"], [("Am", "p")])
                if has_s:
                    bas = nbR()
                    mm(pbs[bas][:, 0:128], keb[:, 512:640], qe[:, 512:640], True, True, [("keb", "s"), (qk, "s")], bas)
                    tt(Am[:, 4, :], pbs[bas][:, 0:128], maskSb[:], ALU.mult, [PB(bas), "maskSb"], [("Am", "s")])
                    tt(Vm[:, :, :], vT[:, 4, :].unsqueeze(1).broadcast_to([128, 16, 128]),
                       seqm.unsqueeze(2).broadcast_to([128, 16, 128]), ALU.mult, ["vT", "cs"], ["Vm"])

            def sample_prefetch(h):
                P.dma("sp", "d_S0", lambda e, h=h: e.dma_start(out=S0[:, :, :], in_=shg[:, h, :, :].rearrange("i k v -> k i v")),
                      writes=["S0"])

            def sample_bf(h):
                cp(S0bf[:, :, :].rearrange("p i v -> p (i v)"), S0[:, :, :].rearrange("p i v -> p (i v)"), ["S0"], ["S0bf"], eng="act")

            def sample_state_update(h):
                dec = DEC[h % 2]
                dk_ = f"dec{h % 2}"
                tt(S0[:, :, :], S0[:, :, :], dec[:, 8:24].unsqueeze(2).broadcast_to([128, 16, 128]), ALU.mult,
                   ["S0", (dk_, "s")], ["S0"])
                for q4 in range(4):
                    bd = nbR()
                    mm(pbs[bd][:, 0:512], kdT[:, 4, :], Vm[:, 4 * q4:4 * q4 + 4, :].rearrange("p i v -> p (i v)"), True, True,
                       ["kdT", "Vm"], bd)
                    tt(S0[:, 4 * q4:4 * q4 + 4, :].rearrange("p i v -> p (i v)"),
                       S0[:, 4 * q4:4 * q4 + 4, :].rearrange("p i v -> p (i v)"), pbs[bd][:, 0:512], ALU.add,
                       ["S0", PB(bd)], ["S0"])
                P.dma("sp", "d_nhs", lambda e, h=h: e.dma_start(out=nhs[:, h, :, :].rearrange("i k v -> k i v"), in_=S0[:, :, :]),
                      reads=["S0"])

            def R2_mm(h):
                dsb = [nbR(), nbR()]
                HS[h]["dsb"] = dsb
                for c in range(8):
                    blk, half = c // 2, c % 2
                    bk = dsb[half]
                    mm(pbs[bk][:, blk * 128:(blk + 1) * 128], kdT[half * 64:(half + 1) * 64, blk, :],
                       vT[half * 64:(half + 1) * 64, blk, :], True, True, ["kdT", "vT"], bk)
                cp(Sall[:, 0, :], carry[:, h, :], ["carry"], [("Sall", 0)])

            def R2_steps(h, c0, c1):
                dec_ = DEC[h % 2]
                dk_ = f"dec{h % 2}"
                dsb = HS[h]["dsb"]
                for c in range(c0, c1):
                    blk, half = c // 2, c % 2
                    bk = dsb[half]
                    stt(Sall[:, c + 1, :], Sall[:, c, :], dec_[:, c:c + 1], pbs[bk][:, blk * 128:(blk + 1) * 128],
                        ALU.mult, ALU.add, [("Sall", c), (dk_, "p"), PB(bk)], [("Sall", c + 1)])

            def R2_fin(h):
                cp(Sbf[:, :, :].rearrange("p c v -> p (c v)"), Sall[:, 0:8, :].rearrange("p c v -> p (c v)"),
                   [("Sall", c) for c in range(8)], ["Sbf"], eng="act")
                cp(carry[:, h, :], Sall[:, 8, :], [("Sall", 8)], ["carry"])
                if last:
                    P.dma("sp", f"d_nhp{h}", lambda e, h=h: e.dma_start(out=nhp[h], in_=Sall[:, 8, :]), reads=[("Sall", 8)])

            def R2(h):
                R2_mm(h); R2_steps(h, 0, 8); R2_fin(h)

            def R3(h):
                qe = QE[h % 2]
                qk = f"qe{h % 2}"
                S_ = HS[h]
                bo = nbR()
                for b in range(4):
                    mm(pbs[bo][:, b * 128:(b + 1) * 128], vT[:, b, :], Am[:, b, :], True, False, ["vT", ("Am", "p")], bo)
                    for half in range(2):
                        c = 2 * b + half
                        mm(pbs[bo][:, c * 64:(c + 1) * 64], Sbf[:, c, :], qe[:, c * 64:(c + 1) * 64], False, half == 1,
                           ["Sbf", (qk, "p")], bo)
                bos = None
                if has_s:
                    sample_bf(h)
                    bos = nbR()
                    mm(pbs[bos][:, 0:128], vT[:, 4, :], Am[:, 4, :], True, False, ["vT", ("Am", "s")], bos)
                    for i in range(16):
                        mm(pbs[bos][:, i * 8:(i + 1) * 8], S0bf[:, i, :], qe[:, 512 + i * 8:512 + (i + 1) * 8], False, i == 15,
                           ["S0bf", (qk, "s")], bos)
                S_["obanks"] = [(0, 512, "p", bo)] + ([(512, 128, "s", bos)] if has_s else [])
                for (c0, n, sk, bk) in S_["obanks"]:
                    act(osq[:, c0:c0 + n], pbs[bk][:, 0:n], AF.Square, [PB(bk)], [("osq", sk)])

            def R4(h):
                gbh = gb2[h % 2]
                gbk = f"gbf{h % 2}"
                for (c0, n, sk, bk) in HS[h]["obanks"]:
                    bn = nbR()
                    mm(pbs[bn][:, 0:n], onesd[:], osq[:, c0:c0 + n], True, True, ["onesd", ("osq", sk)], bn)
                    act(s1[:, c0:c0 + n], pbs[bn][:, 0:n], AF.Ln, [PB(bn)], [("s1", sk)], bias=EPS)
                    act(s1[:, c0:c0 + n], s1[:, c0:c0 + n], AF.Exp, [("s1", sk)], [("s1", sk)], scale=-0.5)
                    stt(s2[:, c0:c0 + n], pbs[bk][:, 0:n], small[:, 4 + h:5 + h], s1[:, c0:c0 + n], ALU.mult, ALU.mult,
                        [PB(bk), "small", ("s1", sk)], [("s2", sk)])
                tt(o_fin[:, h, 0:NT], s2[:, 0:NT], gbh[:, 0:NT], ALU.mult, K("s2") + [(gbk, "p"), (gbk, "s")],
                   [("o_fin", h, s_) for s_ in ("p", "s")])

            if PIPE:
                for j in range(4):
                    z_group(0, j)
            poolmix()
            ck("poolmix")
            P.barrier()
            ck("bar")
            if PIPE:
                for h in range(4):
                    p = h - 1
                    if h > 0:
                        R1(p)
                    z_evac(h)
                    g_mm(h, 0)
                    if h > 0:
                        R2_mm(p)
                    chain_a(h)
                    if h > 0:
                        R2_steps(p, 0, 4)
                    g_evac(h, 0)
                    if h < 3:
                        for j in range(4):
                            z_group(h + 1, j)
                    cb_scan(h)
                    if h > 0:
                        R2_steps(p, 4, 7)
                    cb_exp(h)
                    cb_mul(h)
                    cb_dec(h)
                    cb_keb(h)
                    g_mm(h, 1)
                    if h > 0:
                        R2_steps(p, 7, 8)
                        R2_fin(p)
                        R3(p)
                    g_evac(h, 1)
                    if h > 0:
                        R4(p)
                    warm(AF.Sigmoid)
                R1(3); R2(3); R3(3); R4(3)
                warm(AF.Sigmoid)
            else:
                for h in range(4):
                    sample_prefetch(h)
                    for j in range(4):
                        z_group(h, j)
                    z_evac(h)
                    g_mm(h, 0)
                    g_mm(h, 1)
                    chain_a(h)
                    chain_b(h)
                    g_evac(h, 0)
                    g_evac(h, 1)
                    R1(h); R2(h); R3(h); R4(h)
                    warm(AF.Sigmoid)
                    sample_state_update(h)

            ck("hgrn")
            for jh in range(2):
                wy, wyk = w_get(9 + jh)
                wyv = wy[:].rearrange("p (k t c) -> p k t c", k=4, t=2)
                for jj in range(4):
                    j = jh * 4 + jj
                    bs = nb() if has_s else None
                    b_ya, b_yb = nb(), nb()
                    for t_, bk, srcb, srck in ((0, b_ya, pool_out, "pool_out"), (1, b_yb, o_fin, "o_fin")):
                        for kk in range(4):
                            for (c0, n, sk) in subs:
                                o = pbs[bk][:, 0:512] if sk == "p" else pbs[bs][:, t_ * 128:(t_ + 1) * 128]
                                mm(o, wyv[:, kk, t_, jj * 128:(jj + 1) * 128], srcb[:, kk, c0:c0 + n], kk == 0, kk == 3,
                                   [wyk, (srck, kk, sk)], bk if sk == "p" else bs)
                    if jj == 3:
                        w_release(9 + jh)
                    for (c0, n, sk) in subs:
                        oa = pbs[b_ya][:, 0:512] if sk == "p" else pbs[bs][:, 0:128]
                        ob = pbs[b_yb][:, 0:512] if sk == "p" else pbs[bs][:, 128:256]
                        ka = PB(b_ya) if sk == "p" else PB(bs)
                        kb_ = PB(b_yb) if sk == "p" else PB(bs)
                        act(s1[:, c0:c0 + n], sgA[:, j, c0:c0 + n], AF.Sigmoid, [("sgA", j, sk)], [("s1", sk)])
                        act(s2[:, c0:c0 + n], sgB[:, j, c0:c0 + n], AF.Sigmoid, [("sgB", j, sk)], [("s2", sk)])
                        tt(s1[:, c0:c0 + n], s1[:, c0:c0 + n], oa, ALU.mult, [("s1", sk), ka], [("s1", sk)])
                        tt(s2[:, c0:c0 + n], s2[:, c0:c0 + n], ob, ALU.mult, [("s2", sk), kb_], [("s2", sk)])
                    tt(merged[:, j, 0:NT], s1[:, 0:NT], s2[:, 0:NT], ALU.add, K("s1") + K("s2"),
                       [("merged", j, s_) for s_ in ("p", "s")])

            ck("merge")
            warm(AF.Ln)
            for half in range(2):
                wo, wok = w_get(11 + half)
                wov = wo[:].rearrange("p (k c) -> p k c", k=8)
                bks = {b: nb() for b in blocks}
                for k in range(8):
                    for b in blocks:
                        sk = "p" if b < 4 else "s"
                        mm(pbs[bks[b]][:, 0:512], merged[:, k, b * 128:(b + 1) * 128], wov[:, k, :], k == 0, k == 7,
                           [wok, ("merged", k, sk)], bks[b])
                w_release(11 + half)
                for b in blocks:
                    tt(X[:, b, half * 512:(half + 1) * 512], X[:, b, half * 512:(half + 1) * 512], pbs[bks[b]][:, 0:512],
                       ALU.add, [(xk, b), PB(bks[b])], [(xk, b)])
                    half_stats(b, half)

            ck("wout")
            norm_T(1, True)
            warm(AF.Sigmoid)
            P.barrier()
            if ti + 1 < ntiles:
                for b in range(4):
                    load_x(ti + 1, b)
            for b in blocks:
                src = ppd[ti * 512 + b * 128: ti * 512 + (b + 1) * 128, :] if b < 4 else psd
                P.dma("sp", f"d_p{b}", lambda e, src=src, b=b: e.dma_start(out=pstage[:, b, :], in_=src), writes=[("pstage", b)])
            for jb in range(11):
                wf, wfk = w_get(13 + jb)
                wfv = wf[:].rearrange("p (k j c) -> p k j c", k=8, j=2)
                for cc in range(2):
                    f = 2 * jb + cc
                    bs = nb() if has_s else None
                    b_g, b_u = nb(), nb()
                    for jx, bk in ((0, b_g), (1, b_u)):
                        for k in range(8):
                            for (c0, n, sk) in subs:
                                o = pbs[bk][:, 0:512] if sk == "p" else pbs[bs][:, jx * 128:(jx + 1) * 128]
                                mm(o, wfv[:, k, jx, cc * 128:(cc + 1) * 128], hT[:, k, c0:c0 + n], k == 0, k == 7,
                                   [wfk] + hTk[sk], bk if sk == "p" else bs)
                    if cc == 1:
                        w_release(13 + jb)
                    for (c0, n, sk) in subs:
                        og = pbs[b_g][:, 0:512] if sk == "p" else pbs[bs][:, 0:128]
                        ou = pbs[b_u][:, 0:512] if sk == "p" else pbs[bs][:, 128:256]
                        kg = PB(b_g) if sk == "p" else PB(bs)
                        ku = PB(b_u) if sk == "p" else PB(bs)
                        act(ftmp[:, c0:c0 + n], og, AF.Sigmoid, [kg], [("ftmp", sk)])
                        tt(ftmp[:, c0:c0 + n], ftmp[:, c0:c0 + n], og, ALU.mult, [("ftmp", sk), kg], [("ftmp", sk)])
                        tt(hidden[:, f, c0:c0 + n], ftmp[:, c0:c0 + n], ou, ALU.mult, [("ftmp", sk), ku], [("hidden", f, sk)])
            for b in blocks:
                cp(pbf[:], pstage[:, b, :], [("pstage", b)], ["pbf"], eng="act")
                bt = nb()
                bv = pbs[bt][:].bitcast(BF16)
                for k in range(2):
                    tp(bv[:, k * 128:(k + 1) * 128], pbf[:, k * 128:(k + 1) * 128], identb[:], ["pbf", "identb"], bt)
                cp(pT[:, :, b * 128:(b + 1) * 128], bv[:, 0:256].rearrange("p (k t) -> p k t", k=2), [PB(bt)], [("pT", b)])
            warm(AF.Ln)
            for half in range(2):
                bks = {b: nb() for b in blocks}
                for kb in range(3):
                    wd, wdk = w_get(24 + half * 3 + kb)
                    wdv = wd[:].rearrange("p (k c) -> p k c", k=8)
                    nk = 8 if kb < 2 else 6
                    for kk in range(nk):
                        f = kb * 8 + kk
                        for b in blocks:
                            sk = "p" if b < 4 else "s"
                            mm(pbs[bks[b]][:, 0:512], hidden[:, f, b * 128:(b + 1) * 128], wdv[:, kk, :], f == 0, f == 21,
                               [wdk, ("hidden", f, sk)], bks[b])
                    w_release(24 + half * 3 + kb)
                for b in blocks:
                    tt(X[:, b, half * 512:(half + 1) * 512], X[:, b, half * 512:(half + 1) * 512], pbs[bks[b]][:, 0:512],
                       ALU.add, [(xk, b), PB(bks[b])], [(xk, b)])
                    half_stats(b, half)

            ck("ffn")
            norm_T(2, True)
            if ti + 1 < ntiles:
                Xn, xkn = xbuf(ti + 1)
                norm_stats(False, blks=[0, 1, 2, 3], X_=Xn, xk_=xkn, ssq_=ssq1, rst_=rst1, tag="ssq1", rtag="rst1")
            warm(AF.Sigmoid)
            wg0, wg0k = w_get(30)
            wpp_, wppk = w_get(31)
            wg1, wg1k = w_get(32)
            wppv = wpp_[:, 0:2048].rearrange("p (k c) -> p k c", k=2)
            for half in range(2):
                wg_, wgk = (wg0, wg0k) if half == 0 else (wg1, wg1k)
                wgv = wg_[:].rearrange("p (k c) -> p k c", k=8)
                bks = {b: nb() for b in blocks}
                for k in range(8):
                    for b in blocks:
                        mm(pbs[bks[b]][:, 0:512], hT[:, k, b * 128:(b + 1) * 128], wgv[:, k, :], k == 0, k == 7,
                           [wgk, ("hT", b)], bks[b])
                if half == 0:
                    w_release(30)
                for b in blocks:
                    sg = sgt[b % 2]
                    sgk = f"sgt{b % 2}"
                    act(sg[:], pbs[bks[b]][:, 0:512], AF.Sigmoid, [PB(bks[b])], [sgk])
                    be = nb()
                    for k in range(2):
                        mm(pbs[be][:, 0:512], pT[:, k, b * 128:(b + 1) * 128], wppv[:, k, half * 512:(half + 1) * 512], k == 0, k == 1,
                           [wppk, ("pT", b)], be)
                    tt(sg[:], sg[:], pbs[be][:, 0:512], ALU.mult, [sgk, PB(be)], [sgk])
                    tt(X[:, b, half * 512:(half + 1) * 512], X[:, b, half * 512:(half + 1) * 512], sg[:], ALU.add,
                       [(xk, b), sgk], [(xk, b)])
                for b in blocks:
                    half_stats(b, half)
                if half == 1:
                    w_release(31); w_release(32)

            ck("ple")
            if ti + 1 < ntiles:
                Xn, xkn = xbuf(ti + 1)
                norm_apply(0, blks=[0, 1, 2, 3], X_=Xn, xk_=xkn, rst_=rst1, rtag="rst1")
            P.barrier()
            norm_stats(True)
            for b in blocks:
                dst = yp[ti * 512 + b * 128: ti * 512 + (b + 1) * 128, :] if b < 4 else ys
                for half in range(2):
                    q_ = (2 * b + half) % 4
                    yst = ystage[q_]
                    ysk = f"ystage{q_}"
                    hs_ = slice(half * 512, (half + 1) * 512)
                    stt(yst[:], X[:, b, hs_], rst[:, b:b + 1], gB[:, 3, hs_], ALU.mult, ALU.mult,
                        [(xk, b), "rst", ("gB", 3)], [ysk])
                    P.dma("sp", f"d_y{q_}", lambda e, yst=yst, dst=dst, hs_=hs_: e.dma_start(out=dst[:, hs_], in_=yst[:]),
                          reads=[ysk])

        P.barrier()
        try:
            ck("setup")
            for ti in range(ntiles):
                run_tile(ti)
        except _Stop:
            pass
        P.finish("sp")
        P.emit()
    return nc


_PROG = {}


def _prep_inputs(inp):
    f = lambda a: np.ascontiguousarray(np.asarray(a, dtype=np.float32))
    w_in = f(inp["w_in"][0])
    wallv = build_wall(w_in, f(inp["w_pool_up"][0]), f(inp["w_hgrn_up"][0]), f(inp["w_out"][0]),
                       f(inp["w_ffn_gate"][0]), f(inp["w_ffn_up"][0]), f(inp["w_ffn_down"][0]),
                       f(inp["w_ple_gate"][0]), f(inp["w_ple_proj"][0]))
    gvec = np.ascontiguousarray(np.stack([f(inp["g_mix"][0]), f(inp["g_ffn"][0]), f(inp["g_ple"][0]), f(inp["g_final"])], 0))
    small = np.zeros((128, 16), np.float32)
    small[:, 0:4] = f(inp["pool_scale"][0]).reshape(4, 128).T
    small[:, 4:8] = f(inp["hgrn_norm"][0]).reshape(4, 128).T
    small[:, 8:12] = f(inp["hgrn_lb"][0]).reshape(4, 128).T
    small[:, 12:16] = f(inp["hgrn_lb"][1]).reshape(4, 128).T
    pmix = np.ascontiguousarray(f(inp["w_pool_mix"][0]).transpose(1, 0, 2)).reshape(128, 512)
    xp = f(inp["x_prompt"]); xsm = f(inp["x_sample"])
    ppr = f(inp["p_prompt"][0]); psm = f(inp["p_sample"][0])
    spl = f(inp["state_pool"][0]); shg = f(inp["state_hgrn"][0])
    maps = []
    for c in range(NCORES):
        maps.append({
            "xp": xp[c], "xs": xsm[16 * c:16 * c + 16].reshape(128, 1024),
            "pp": ppr[c], "ps": psm[16 * c:16 * c + 16].reshape(128, 256),
            "spool": spl[16 * c:16 * c + 16].reshape(240, 512),
            "shg": shg[16 * c:16 * c + 16],
            "wall": wallv, "cst": CONST_ARR, "gvec": gvec, "small": small, "pmix": pmix,
        })
    return maps


def kernel(**inputs):
    if "nc" not in _PROG:
        _PROG["nc"] = build_program()
    nc = _PROG["nc"]
    maps = _prep_inputs(inputs)
    res = run_bass_kernel_spmd(nc, maps, core_ids=list(range(NCORES)))
    R = res.results
    y_p = np.stack([R[c]["yp"] for c in range(NCORES)], 0).astype(np.float32)
    y_s = np.concatenate([R[c]["ys"].reshape(16, 8, 1024) for c in range(NCORES)], 0).astype(np.float32)
    npp = np.stack([R[c]["npp"] for c in range(NCORES)], 0)[None].astype(np.float32)
    nhp = np.stack([R[c]["nhp"] for c in range(NCORES)], 0)[None].astype(np.float32)
    nps = np.concatenate([R[c]["nps"].reshape(16, 15, 512) for c in range(NCORES)], 0)[None].astype(np.float32)
    nhs = np.concatenate([R[c]["nhs"] for c in range(NCORES)], 0)[None].astype(np.float32)
    return (y_p, y_s, npp, nhp, nps, nhs)
```

```python
import numpy as np
from contextlib import ExitStack
import concourse.bass as bass
import concourse.mybir as mybir
from concourse.bass_utils import run_bass_kernel_spmd

F32 = mybir.dt.float32
BF16 = mybir.dt.bfloat16
AF = mybir.ActivationFunctionType
ALU = mybir.AluOpType

ENGS = ("pe", "act", "dve", "pool", "sp")
EPS = 1e-6
NCORES = 8
NRING = 5
NTILES = 4
import os as _os
SAME_DIST = int(_os.environ.get("SAME_DIST", str(1 << 30)))


class Prog:
    def __init__(self, nc, stack, same_engine_wait=True):
        self.nc = nc
        self.stack = stack
        self.streams = {e: [] for e in ENGS}
        self.count = {e: 0 for e in ENGS}
        self.known = {e: {} for e in ENGS}
        self.sems = {}
        self.dma_count = {}
        self.lastw = {}
        self.readers = {}
        self.same_engine_wait = same_engine_wait

    def _collect(self, eng, reads, writes):
        deps = []
        for k in reads:
            ev = self.lastw.get(k)
            if ev is not None:
                deps.append(ev)
        for k in writes:
            ev = self.lastw.get(k)
            if ev is not None:
                deps.append(ev)
            rd = self.readers.get(k)
            if rd:
                deps.extend(rd.values())
        kn = self.known[eng]
        need = {}
        for (s, v, vc) in deps:
            if s == eng and (eng == "pe" or not self.same_engine_wait or self.count[eng] - v >= SAME_DIST):
                continue
            if kn.get(s, 0) >= v:
                continue
            if need.get(s, 0) < v:
                need[s] = v
            for s2, v2 in vc.items():
                if s2 == eng:
                    continue
                if kn.get(s2, 0) < v2:
                    kn[s2] = v2
        waits = []
        for s, v in need.items():
            waits.append((s, v))
            if kn.get(s, 0) < v:
                kn[s] = v
        return waits

    def _record(self, ev, reads, writes):
        s = ev[0]
        for k in reads:
            self.readers.setdefault(k, {})[s] = ev
        for k in writes:
            self.lastw[k] = ev
            self.readers[k] = {}

    def op(self, eng, fn, reads=(), writes=()):
        waits = self._collect(eng, reads, writes)
        self.count[eng] += 1
        n = self.count[eng]
        vc = dict(self.known[eng])
        vc[eng] = n
        ev = (eng, n, vc)
        self.streams[eng].append((waits, fn, (eng, 1)))
        self._record(ev, reads, writes)
        return ev

    def dma(self, q, semname, fn, reads=(), writes=()):
        waits = self._collect(q, reads, writes)
        self.dma_count[semname] = self.dma_count.get(semname, 0) + 1
        v = 16 * self.dma_count[semname]
        vc = dict(self.known[q])
        vc[semname] = v
        ev = (semname, v, vc)
        self.streams[q].append((waits, fn, (semname, 16)))
        self._record(ev, reads, writes)
        return ev

    def barrier(self, engines=("act", "dve", "sp")):
        for e in engines:
            kn = self.known[e]
            waits = []
            for e2 in ("pe", "act", "dve"):
                c = self.count[e2]
                if c and kn.get(e2, 0) < c:
                    waits.append((e2, c))
                    kn[e2] = c
            for s, c in self.dma_count.items():
                if s.startswith("ring") or s.startswith("d_y") or s.startswith("d_x"):
                    continue
                if kn.get(s, 0) < 16 * c:
                    waits.append((s, 16 * c))
                    kn[s] = 16 * c
            if waits:
                self.streams[e].append((waits, None, None))

    def finish(self, eng="sp"):
        kn = self.known[eng]
        waits = []
        for s, c in self.dma_count.items():
            if kn.get(s, 0) < 16 * c:
                waits.append((s, 16 * c))
                kn[s] = 16 * c
        for e in ("pe", "act", "dve"):
            if self.count[e] and kn.get(e, 0) < self.count[e]:
                waits.append((e, self.count[e]))
        self.streams[eng].append((waits, None, None))

    def emit(self):
        nc = self.nc
        for s in list(ENGS) + list(self.dma_count):
            if s not in self.sems:
                self.sems[s] = self.stack.enter_context(nc.semaphore(s))
        with nc.Block() as block:
            def run(engine, items):
                for waits, fn, inc in items:
                    for s, v in waits:
                        engine.wait_ge(self.sems[s], v)
                    if fn is not None:
                        ins = fn(engine)
                        ins.then_inc(self.sems[inc[0]], inc[1])

            @block.tensor
            def _(eng):
                run(eng, self.streams["pe"])

            @block.scalar
            def _(eng):
                run(eng, self.streams["act"])

            @block.vector
            def _(eng):
                run(eng, self.streams["dve"])

            @block.gpsimd
            def _(eng):
                run(eng, self.streams["pool"])

            @block.sync
            def _(eng):
                run(eng, self.streams["sp"])


NBLK = 33


def _kc(W, nk):
    C = W.shape[1]
    return np.ascontiguousarray(W.reshape(nk, 128, C).transpose(1, 0, 2)).reshape(128, nk * C)


def _pad(a):
    out = np.zeros((128, 4096), np.float32)
    out[:, : a.shape[1]] = a
    return out


def build_wall(w_in, w_pool_up, w_hgrn_up, w_out, w_g, w_u, w_d, w_pg, w_pp):
    blocks = []
    blocks.append(_kc(w_in[:, 0:512], 8))
    zz = w_in[:, 512:2560].reshape(8, 128, 4, 4, 128)
    ga = w_in[:, 2560:3584].reshape(8, 128, 8, 128)
    gb = w_in[:, 3584:4608].reshape(8, 128, 8, 128)
    for h in range(4):
        blocks.append(np.ascontiguousarray(zz[:, :, :, h, :].transpose(1, 0, 2, 3)).reshape(128, 4096))
        gg = np.stack([ga[:, :, 2 * h], gb[:, :, 2 * h], ga[:, :, 2 * h + 1], gb[:, :, 2 * h + 1]], axis=2)
        blocks.append(np.ascontiguousarray(gg.transpose(1, 0, 2, 3)).reshape(128, 4096))
    for half in range(2):
        yy = np.stack([w_pool_up[:, half * 512:(half + 1) * 512].reshape(4, 128, 512),
                       w_hgrn_up[:, half * 512:(half + 1) * 512].reshape(4, 128, 512)], axis=2)
        blocks.append(np.ascontiguousarray(yy.transpose(1, 0, 2, 3)).reshape(128, 4096))
    blocks.append(_kc(w_out[:, 0:512], 8))
    blocks.append(_kc(w_out[:, 512:1024], 8))
    for jb in range(11):
        gu = np.stack([w_g[:, jb * 256:(jb + 1) * 256], w_u[:, jb * 256:(jb + 1) * 256]], axis=1)
        blocks.append(_kc(gu.reshape(1024, 512), 8))
    for half in range(2):
        for kb in range(3):
            nk = 8 if kb < 2 else 6
            blocks.append(_pad(_kc(w_d[kb * 1024: kb * 1024 + nk * 128, half * 512:(half + 1) * 512], nk)))
    blocks.append(_kc(w_pg[:, 0:512], 8))
    blocks.append(_pad(_kc(w_pp, 2)))
    blocks.append(_kc(w_pg[:, 512:1024], 8))
    assert len(blocks) == NBLK
    return np.ascontiguousarray(np.stack(blocks, axis=0).astype(np.float32))


def build_consts():
    c = {}
    c["ident"] = np.eye(128, dtype=np.float32)
    s = np.arange(128)[:, None]
    t = np.arange(128)[None, :]
    c["maskP"] = ((s // 64 == t // 64) & (s <= t)).astype(np.float32)
    c["maskS"] = ((s // 8 == t // 8) & (s <= t)).astype(np.float32)
    c["seqm"] = (s // 8 == np.arange(16)[None, :]).astype(np.float32)
    r = np.ones(640, np.float32)
    r[0:512:64] = 0.0
    r[512:640:8] = 0.0
    c["resetm"] = np.broadcast_to(r, (128, 640)).copy()
    rc = np.zeros((4, 16), np.float32)
    for g, w in enumerate((2, 4, 8, 16)):
        rc[g] = 1.0 / np.minimum(np.arange(16) + 1, w)
    c["rc"] = np.broadcast_to(rc.reshape(1, 64), (128, 64)).copy()
    order = ["ident", "seqm", "rc", "maskP", "maskS", "resetm"]
    offs = {}
    o = 0
    for k in order:
        offs[k] = (o, c[k].shape[1])
        o += c[k].shape[1]
    return np.ascontiguousarray(np.concatenate([c[k] for k in order], axis=1)), offs


CONST_ARR, COFF = build_consts()
CW = CONST_ARR.shape[1]
CW_KEEP = COFF["maskP"][0]


class _Stop(Exception):
    pass


STOP = None


def ck(name):
    if STOP == name:
        raise _Stop()


def build_program(ntiles=NTILES):
    nc = bass.Bass("TRN2", target_bir_lowering=False)

    def din(name, shape):
        return nc.dram_tensor(name, shape, F32, kind="ExternalInput").ap()

    def dout(name, shape):
        return nc.dram_tensor(name, shape, F32, kind="ExternalOutput").ap()

    xp = din("xp", [2048, 1024])
    xs = din("xs", [128, 1024])
    ppd = din("pp", [2048, 256])
    psd = din("ps", [128, 256])
    spool = din("spool", [240, 512])
    shg = din("shg", [16, 4, 128, 128])
    wall = din("wall", [NBLK, 128, 4096])
    cst = din("cst", [128, CW])
    gvec = din("gvec", [4, 1024])
    smalld = din("small", [128, 16])
    pmixd = din("pmix", [128, 512])
    yp = dout("yp", [2048, 1024])
    ys = dout("ys", [128, 1024])
    npp = dout("npp", [15, 512])
    nhp = dout("nhp", [4, 128, 128])
    nps = dout("nps", [240, 512])
    nhs = dout("nhs", [16, 4, 128, 128])

    with ExitStack() as st:
        P = Prog(nc, st)

        def sb(name, shape, dt):
            return st.enter_context(nc.sbuf_tensor("sb_" + name, shape, dt))

        xres = sb("xres", [128, 5, 1024], F32)
        hT = sb("hT", [128, 8, 640], BF16)
        hn = [sb(f"hn{i}", [128, 1024], BF16) for i in range(2)]
        junk = sb("junk", [128, 512], BF16)
        ystage = [sb(f"ystage{i}", [128, 512], F32) for i in range(4)]
        resetmb = sb("resetmb", [128, 640], BF16)
        ring = [sb(f"ring{i}", [128, 4096], BF16) for i in range(NRING)]
        gB = sb("gB", [128, 4, 1024], F32)
        cs = sb("cs", [128, CW_KEEP], F32)
        identb = sb("identb", [128, 128], BF16)
        onesd = sb("onesd", [128, 128], BF16)
        maskPb = sb("maskPb", [128, 128], BF16)
        maskSb = sb("maskSb", [128, 128], BF16)
        pmixf = sb("pmixf", [128, 512], F32)
        pmix = sb("pmix", [128, 4, 128], BF16)
        small = sb("small", [128, 16], F32)
        lbv = sb("lbv", [128, 8], F32)
        carry = sb("carry", [128, 4, 128], F32)
        ucarry = sb("ucarry", [128, 4, 16], F32)
        ssq = sb("ssq", [128, 16], F32)
        ssq1 = sb("ssq1", [128, 8], F32)
        dmy = sb("dmy", [128, 2], F32)
        rst1 = sb("rst1", [128, 4], F32)
        rst = sb("rst", [128, 8], F32)
        identf = cs[:, COFF["ident"][0]:COFF["ident"][0] + 128]
        seqm = cs[:, COFF["seqm"][0]:COFF["seqm"][0] + 16]
        rcv = cs[:, COFF["rc"][0]:COFF["rc"][0] + 64]
        resetm = resetmb

        MIX_BYTES = 0
        scr_plan = {}

        def plan(phase, name, nelem, dt):
            nonlocal MIX_BYTES
            nbytes = nelem * (4 if dt == F32 else 2)
            nbytes = (nbytes + 31) // 32 * 32
            off = scr_plan.setdefault(("off", phase), 0)
            scr_plan[name] = (off, nelem, dt)
            scr_plan[("off", phase)] = off + nbytes

        U_NAMES = ["ug", "ue", "pa", "pb_", "sa", "sb_", "tmp16", "pooled", "npsb", "stg", "npp_sb"]
        for name, n, dt in [
            ("ug", 528, F32), ("ue", 16 * 24, F32), ("pa", 528, F32), ("pb_", 528, F32),
            ("sa", 16 * 24, F32), ("sb_", 16 * 24, F32), ("tmp16", 16, F32),
            ("pooled", 4 * 640, BF16), ("npsb", 4 * 240, F32),
            ("stg", 2 * 512, F32), ("npp_sb", 512, F32), ("pool_out", 4 * 640, BF16),
            ("qf", 640, F32), ("t1", 640, F32), ("t2", 640, F32), ("t3", 640, F32), ("t4", 640, F32),
            ("qe", 640, BF16), ("qe2", 640, BF16), ("keb", 640, BF16), ("kdb", 640, BF16), ("vb", 640, BF16), ("gbf", 640, BF16), ("gbf2", 640, BF16),
            ("kdT", 5 * 128, BF16), ("vT", 5 * 128, BF16), ("Sall", 9 * 128, F32), ("Sbf", 8 * 128, BF16),
            ("dec", 24, F32), ("dec2", 24, F32), ("Am", 5 * 128, BF16), ("osq", 640, BF16), ("o_fin", 4 * 640, BF16),
            ("S0", 16 * 128, F32), ("S0bf", 16 * 128, BF16), ("Vm", 16 * 128, BF16),
            ("merged", 8 * 640, BF16), ("s1", 640, F32), ("s2", 640, F32),
        ]:
            plan("mix", name, n, dt)
        for name, n, dt in [
            ("hidden", 22 * 640, BF16), ("ftmp", 640, F32), ("sgt0", 512, F32), ("sgt1", 512, F32),
            ("pT", 2 * 640, BF16), ("pstage", 5 * 256, F32), ("pbf", 256, BF16),
        ]:
            plan("ffn", name, n, dt)
        u_end = scr_plan["pool_out"][0]
        assert 2 * 8 * 640 * 2 <= u_end, u_end
        scr_plan["sgA"] = (0, 8 * 640, BF16)
        scr_plan["sgB"] = (8 * 640 * 2, 8 * 640, BF16)
        scr_bytes = max(scr_plan[("off", "mix")], scr_plan[("off", "ffn")])
        scr = sb("scr", [128, scr_bytes // 4], F32)

        def sv(name):
            off, n, dt = scr_plan[name]
            if dt == F32:
                return scr[:, off // 4: off // 4 + n]
            return scr[:, off // 4: off // 4 + n // 2].bitcast(BF16)

        ug = sv("ug"); ue = sv("ue").rearrange("p (i r) -> p i r", r=24)
        pa = sv("pa"); pb_ = sv("pb_")
        sa = sv("sa").rearrange("p (i r) -> p i r", r=24); sb_ = sv("sb_").rearrange("p (i r) -> p i r", r=24)
        tmp16 = sv("tmp16")
        pooled = sv("pooled").rearrange("p (g t) -> p g t", g=4)
        pool_out = sv("pool_out").rearrange("p (g t) -> p g t", g=4)
        npsb = sv("npsb").rearrange("p (g r) -> p g r", g=4)
        stg = sv("stg").rearrange("p (h c) -> p h c", h=2)
        nps_sb = stg
        npp_sb = sv("npp_sb")
        qf = sv("qf"); t1 = sv("t1"); t2 = sv("t2"); t3 = sv("t3"); t4 = sv("t4")
        qe = sv("qe"); qe2 = sv("qe2"); keb = sv("keb"); kdb = sv("kdb"); vb = sv("vb"); gbf = sv("gbf"); gbf2 = sv("gbf2")
        kdT = sv("kdT").rearrange("p (b k) -> p b k", b=5); vT = sv("vT").rearrange("p (b k) -> p b k", b=5)
        Sall = sv("Sall").rearrange("p (c v) -> p c v", c=9); Sbf = sv("Sbf").rearrange("p (c v) -> p c v", c=8)
        dec = sv("dec"); dec2 = sv("dec2"); Am = sv("Am").rearrange("p (b k) -> p b k", b=5); osq = sv("osq")
        o_fin = sv("o_fin").rearrange("p (h t) -> p h t", h=4)
        S0 = sv("S0").rearrange("p (i v) -> p i v", i=16); S0bf = sv("S0bf").rearrange("p (i v) -> p i v", i=16)
        Vm = sv("Vm").rearrange("p (i v) -> p i v", i=16)
        merged = sv("merged").rearrange("p (k t) -> p k t", k=8); s1 = sv("s1"); s2 = sv("s2")
        sgA = sv("sgA").rearrange("p (k t) -> p k t", k=8); sgB = sv("sgB").rearrange("p (k t) -> p k t", k=8)
        hidden = sv("hidden").rearrange("p (f t) -> p f t", f=22); ftmp = sv("ftmp")
        assert scr_plan["S0bf"][0] == scr_plan["S0"][0] + 8192 and scr_plan["Vm"][0] == scr_plan["S0"][0] + 12288
        assert scr_plan["S0"][0] >= scr_plan[("off", "ffn")]
        xo = scr_plan["S0"][0] // 4
        xalt = scr[:, xo:xo + 4096].rearrange("p (b d) -> p b d", b=4)
        sgt = [sv("sgt0"), sv("sgt1")]
        pT = sv("pT").rearrange("p (k t) -> p k t", k=2); pstage = sv("pstage").rearrange("p (b c) -> p b c", b=5); pbf = sv("pbf")

        pbs = [st.enter_context(nc.psum_tensor(f"pb{i}", [128, 512], F32)) for i in range(8)]
        bank_ctr = [0]

        def nb():
            b = bank_ctr[0] % 8
            bank_ctr[0] += 1
            return b

        def PB(b):
            return ("pb", b)

        wstate = {"tile": 0, "issued": set(), "released": set(), "total": NBLK * ntiles}

        def w_issue_n(n, extra_reads=()):
            if n >= wstate["total"] or n in wstate["issued"]:
                return
            slot = n % NRING
            blk = n % NBLK
            P.dma("pool", f"ring{slot}",
                  lambda e, slot=slot, blk=blk: e.dma_start(out=ring[slot][:], in_=wall[blk]),
                  reads=list(extra_reads), writes=[("ring", slot)])
            wstate["issued"].add(n)

        def w_issue(extra_reads=()):
            w_issue_n(len(wstate["issued"]), extra_reads)

        def w_get(expect):
            n = wstate["tile"] * NBLK + expect
            assert n in wstate["issued"], ("ring too small / block not prefetched", n)
            assert n - NRING < 0 or (n - NRING) in wstate["released"]
            slot = n % NRING
            return ring[slot], ("ring", slot)

        def w_release(expect):
            n = wstate["tile"] * NBLK + expect
            assert n in wstate["issued"] and n not in wstate["released"]
            wstate["released"].add(n)
            w_issue_n(n + NRING)

        def A(eng, fn, reads, writes):
            return P.op(eng, fn, reads=reads, writes=writes)

        def mm(out, lhsT, rhs, start, stop, reads, bank):
            A("pe", lambda e: e.matmul(out, lhsT=lhsT, rhs=rhs, start=start, stop=stop), reads, [PB(bank)])

        def tp(out, in_, ident, reads, bank):
            A("pe", lambda e: e.transpose(out=out, in_=in_, identity=ident), reads, [PB(bank)])

        def act(out, in_, func, reads, writes, **kw):
            A("act", lambda e: e.activation(out=out, in_=in_, func=func, **kw), reads, writes)

        def tt(out, in0, in1, op, reads, writes, eng="dve"):
            A(eng, lambda e: e.tensor_tensor(out=out, in0=in0, in1=in1, op=op), reads, writes)

        def ts(out, in0, s1_, s2_, op0, op1, reads, writes):
            A("dve", lambda e: e.tensor_scalar(out=out, in0=in0, scalar1=s1_, scalar2=s2_, op0=op0, op1=op1), reads, writes)

        def stt(out, in0, scalar, in1, op0, op1, reads, writes):
            A("dve", lambda e: e.scalar_tensor_tensor(out=out, in0=in0, scalar=scalar, in1=in1, op0=op0, op1=op1), reads, writes)

        def cp(out, in_, reads, writes, eng="dve"):
            if eng == "act":
                act(out, in_, AF.Copy, reads, writes)
            else:
                A(eng, lambda e: e.tensor_copy(out=out, in_=in_), reads, writes)

        def warm(func):
            act(dmy[:, 1:2], dmy[:, 0:1], func, ["dmy0"], ["dmy1"])

        A("dve", lambda e: e.memset(dmy[:], 1.0), [], ["dmy0", "dmy1"])
        P.dma("sp", "d_cs", lambda e: e.dma_start(out=cs[:], in_=cst[:, 0:CW_KEEP]), writes=["cs"])
        cstage = scr[:, 0:CW - CW_KEEP]
        P.dma("sp", "d_cs2", lambda e: e.dma_start(out=cstage, in_=cst[:, CW_KEEP:CW]), writes=["cstage"])
        maskPf = cstage[:, 0:128]
        maskSf = cstage[:, 128:256]
        resetmf = cstage[:, 256:896]
        P.dma("sp", "d_small", lambda e: e.dma_start(out=small[:], in_=smalld), writes=["small"])
        P.dma("sp", "d_pmix", lambda e: e.dma_start(out=pmixf[:], in_=pmixd), writes=["pmixf"])
        for gi in range(4):
            P.dma("sp", f"d_g{gi}",
                  lambda e, gi=gi: e.dma_start(out=gB[:, gi, :], in_=gvec[gi:gi + 1, :].broadcast_to([128, 1024])),
                  writes=[("gB", gi)])
        cp(identb[:], identf, ["cs"], ["identb"])
        cp(maskPb[:], maskPf, ["cstage"], ["maskPb"])
        cp(maskSb[:], maskSf, ["cstage"], ["maskSb"])
        cp(resetmb[:], resetmf, ["cstage"], ["resetmb"])
        cp(pmix[:].rearrange("p g c -> p (g c)"), pmixf[:], ["pmixf"], ["pmix"])
        A("dve", lambda e: e.memset(onesd[:], 1.0 / 128.0), [], ["onesd"])
        A("dve", lambda e: e.memset(carry[:].rearrange("p h v -> p (h v)"), 0.0), [], ["carry"])
        A("dve", lambda e: e.memset(ucarry[:].rearrange("p g t -> p (g t)"), 0.0), [], ["ucarry"])
        tt(lbv[:, 0:4], small[:, 8:12], small[:, 12:16], ALU.subtract, ["small"], ["lbv"])
        act(lbv[:, 0:4], lbv[:, 0:4], AF.Sigmoid, ["lbv"], ["lbv"])
        ts(lbv[:, 4:8], lbv[:, 0:4], -1.0, 1.0, ALU.mult, ALU.add, ["lbv"], ["lbv"])

        def run_tile(ti):
            wstate["tile"] = ti
            has_s = ti == 0
            last = ti == 3
            NT = 640 if has_s else 512
            blocks = [0, 1, 2, 3] + ([4] if has_s else [])
            subs = [(0, 512, "p")] + ([(512, 128, "s")] if has_s else [])
            hTk_p = [("hT", b) for b in range(4)]
            hTk = {"p": hTk_p, "s": [("hT", 4)]}

            def K(name):
                return [(name, "p")] + ([(name, "s")] if has_s else [])

            def xbuf(tj):
                return (xres, "xres") if tj % 2 == 0 else (xalt, "xalt")

            X, xk = xbuf(ti)

            def load_x(tj, b):
                src = xp[tj * 512 + b * 128: tj * 512 + (b + 1) * 128, :] if b < 4 else xs
                Xn, xkn = xbuf(tj)
                P.dma("sp", f"d_x{tj % 2}_{b}", lambda e, b=b, src=src, Xn=Xn: e.dma_start(out=Xn[:, b, :], in_=src),
                      writes=[(xkn, b)])

            if ti == 0:
                for b in blocks:
                    load_x(0, b)
            if ti == 0:
                for _ in range(NRING):
                    w_issue(extra_reads=[(xk, b) for b in blocks])
            if has_s:
                for half in range(2):
                    P.dma("sp", f"d_stg{half}",
                          lambda e, half=half: e.dma_start(out=stg[0:120, half, :], in_=spool[half * 120:(half + 1) * 120, :]),
                          writes=[("stg", half)])

            def half_stats(b, half, X_=None, xk_=None, ssq_=None, tag="ssq"):
                X_ = X if X_ is None else X_
                xk_ = xk if xk_ is None else xk_
                ssq_ = ssq if ssq_ is None else ssq_
                act(junk[:, 0:512], X_[:, b, half * 512:(half + 1) * 512], AF.Square, [(xk_, b)], ["junk", (tag, b, half)],
                    accum_out=ssq_[:, 2 * b + half:2 * b + half + 1])

            def norm_stats(have_partials, blks=None, X_=None, xk_=None, ssq_=None, rst_=None, tag="ssq", rtag="rst"):
                blks = blocks if blks is None else blks
                ssq_ = ssq if ssq_ is None else ssq_
                rst_ = rst if rst_ is None else rst_
                if not have_partials:
                    for b in blks:
                        for half in range(2):
                            half_stats(b, half, X_, xk_, ssq_, tag)
                nbk = len(blks)
                tt(rst_[:, 0:nbk], ssq_[:, 0:2 * nbk:2], ssq_[:, 1:2 * nbk:2], ALU.add,
                   [(tag, b, hf) for b in blks for hf in range(2)], [rtag])
                act(rst_[:, 0:nbk], rst_[:, 0:nbk], AF.Ln, [rtag], [rtag], scale=1.0 / 1024.0, bias=EPS)
                act(rst_[:, 0:nbk], rst_[:, 0:nbk], AF.Exp, [rtag], [rtag], scale=-0.5)

            def norm_apply(gi, blks=None, X_=None, xk_=None, rst_=None, rtag="rst"):
                blks = blocks if blks is None else blks
                X_ = X if X_ is None else X_
                xk_ = xk if xk_ is None else xk_
                rst_ = rst if rst_ is None else rst_
                for b in blks:
                    hb = hn[b % 2]
                    hk = f"hn{b % 2}"
                    stt(hb[:], X_[:, b, :], rst_[:, b:b + 1], gB[:, gi, :], ALU.mult, ALU.mult,
                        [(xk_, b), rtag, ("gB", gi)], [hk])
                    bank = nb()
                    bv = pbs[bank][:].bitcast(BF16)
                    for k in range(8):
                        tp(bv[:, k * 128:(k + 1) * 128], hb[:, k * 128:(k + 1) * 128], identb[:], [hk, "identb"], bank)
                    c0 = b * 128
                    cp(hT[:, :, c0:c0 + 128], bv.rearrange("p (k t) -> p k t", k=8), [PB(bank)], [("hT", b)], eng="act")

            def norm_T(gi, have_partials):
                norm_stats(have_partials)
                norm_apply(gi)

            ck("load")
            if ti == 0:
                norm_T(0, False)
            ck("norm1")

            wv, wk = w_get(0)
            wu = wv[:].rearrange("p (k c) -> p k c", k=8)
            ubanks = []
            for g in range(4):
                bp = nb()
                bs = nb() if has_s else None
                if g == 0:
                    for hc in range(2):
                        for k in range(8):
                            mm(pbs[bp][:, hc * 256:(hc + 1) * 256], wu[:, k, g * 128:(g + 1) * 128], hT[:, k, hc * 256:(hc + 1) * 256],
                               k == 0, k == 7, [wk, ("hT", 2 * hc), ("hT", 2 * hc + 1)], bp)
                    if has_s:
                        for k in range(8):
                            mm(pbs[bs][:, 0:128], wu[:, k, g * 128:(g + 1) * 128], hT[:, k, 512:640], k == 0, k == 7, [wk, ("hT", 4)], bs)
                else:
                    for k in range(8):
                        mm(pbs[bp][:, 0:512], wu[:, k, g * 128:(g + 1) * 128], hT[:, k, 0:512], k == 0, k == 7, [wk] + hTk_p, bp)
                        if has_s:
                            mm(pbs[bs][:, 0:128], wu[:, k, g * 128:(g + 1) * 128], hT[:, k, 512:640], k == 0, k == 7, [wk, ("hT", 4)], bs)
                ubanks.append((bp, bs))
                if g == 3:
                    w_release(0)
                w = 2 << g
                cp(ug[:, 0:16], ucarry[:, g, :], ["ucarry"], ["ug"])
                cp(ug[:, 16:528], pbs[bp][:, 0:512], [PB(bp)], ["ug"], eng="act")
                tt(pa[:, 1:528], ug[:, 1:528], ug[:, 0:527], ALU.add, ["ug"], ["pa"])
                sw = pa
                swk = "pa"
                if w >= 4:
                    tt(pb_[:, 3:528], pa[:, 3:528], pa[:, 1:526], ALU.add, ["pa"], ["pb_"])
                    sw, swk = pb_, "pb_"
                if w >= 8:
                    tt(pa[:, 7:528], pb_[:, 7:528], pb_[:, 3:524], ALU.add, ["pb_"], ["pa"])
                    sw, swk = pa, "pa"
                if w >= 16:
                    tt(pb_[:, 15:528], pa[:, 15:528], pa[:, 7:520], ALU.add, ["pa"], ["pb_"])
                    sw, swk = pb_, "pb_"
                stt(pooled[:, g, 0:512], sw[:, 16:528], 1.0 / w, ug[:, 16:528], ALU.mult, ALU.subtract,
                    [swk, "ug"], [("pooled", g, "p")])
                if ti == 0:
                    tt(tmp16[:], sw[:, 16:32], rcv[:, g * 16:(g + 1) * 16], ALU.mult, [swk, "cs"], ["tmp16"])
                    tt(pooled[:, g, 0:16], tmp16[:], ug[:, 16:32], ALU.subtract, ["tmp16", "ug"], [("pooled", g, "p")])
                cp(ucarry[:, g, :], ug[:, 512:528], ["ug"], ["ucarry"])
                if last:
                    bt = nb()
                    tp(pbs[bt][0:15, 0:128], ug[:, 513:528], identf, ["ug", "cs"], bt)
                    cp(npp_sb[0:15, g * 128:(g + 1) * 128], pbs[bt][0:15, 0:128], [PB(bt)], ["npp_sb"], eng="act")
                if has_s:
                    bt = nb()
                    for half in range(2):
                        tp(pbs[bt][:, half * 120:(half + 1) * 120], stg[0:120, half, g * 128:(g + 1) * 128],
                           identf[0:120, 0:120], [("stg", half), "cs"], bt)
                    cp(ue[:, :, 0:15], pbs[bt][:, 0:240].rearrange("p (i r) -> p i r", r=15), [PB(bt)], ["ue"], eng="act")
                    cp(ue[:, :, 15:23], pbs[bs][:, 0:128].rearrange("p (i t) -> p i t", t=8), [PB(bs)], ["ue"], eng="act")
                    tt(sa[:, :, 1:23], ue[:, :, 1:23], ue[:, :, 0:22], ALU.add, ["ue"], ["sa"])
                    ssw, sswk = sa, "sa"
                    if w >= 4:
                        tt(sb_[:, :, 3:23], sa[:, :, 3:23], sa[:, :, 1:21], ALU.add, ["sa"], ["sb_"])
                        ssw, sswk = sb_, "sb_"
                    if w >= 8:
                        tt(sa[:, :, 7:23], sb_[:, :, 7:23], sb_[:, :, 3:19], ALU.add, ["sb_"], ["sa"])
                        ssw, sswk = sa, "sa"
                    if w >= 16:
                        tt(sb_[:, :, 15:23], sa[:, :, 15:23], sa[:, :, 7:15], ALU.add, ["sa"], ["sb_"])
                        ssw, sswk = sb_, "sb_"
                    stt(pooled[:, g, 512:640].rearrange("p (i t) -> p i t", t=8), ssw[:, :, 15:23], 1.0 / w,
                        ue[:, :, 15:23], ALU.mult, ALU.subtract, [sswk, "ue"], [("pooled", g, "s")])
                    cp(npsb[:, g, :].rearrange("p (i r) -> p i r", r=15), ue[:, :, 8:23], ["ue"], [("npsb", g)])
            if last:
                P.dma("sp", "d_npp", lambda e: e.dma_start(out=npp, in_=npp_sb[0:15, :]), reads=["npp_sb"])
            if has_s:
                for half in range(2):
                    bt = nb()
                    for g in range(4):
                        tp(pbs[bt][0:120, g * 128:(g + 1) * 128], npsb[:, g, half * 120:(half + 1) * 120], identf,
                           [("npsb", g), "cs"], bt)
                    cp(nps_sb[0:120, half, :], pbs[bt][0:120, 0:512], [PB(bt)], [("stg", half)], eng="act")
                    P.dma("sp", f"d_nps{half}",
                          lambda e, half=half: e.dma_start(out=nps[half * 120:(half + 1) * 120, :], in_=nps_sb[0:120, half, :]),
                          reads=[("stg", half)])
            ck("u")
            PIPE = not has_s
            if PIPE:
                zc, rcn = [0], [0]

                def nbZ():
                    b = zc[0] % 4
                    zc[0] += 1
                    return b

                def nbR():
                    b = 6 + rcn[0] % 2
                    rcn[0] += 1
                    return b

                gcn = [0]

                def nbG():
                    b = 4 + gcn[0] % 2
                    gcn[0] += 1
                    return b
            else:
                nbZ = nbR = nbG = nb
            gb2 = [gbf, gbf2]
            QE = [qe, qe2]
            DEC = [dec, dec2]
            HS = {h: {} for h in range(4)}

            def poolmix():
                for g in range(4):
                    for (c0, n, sk) in subs:
                        bk = nbR()
                        mm(pbs[bk][:, 0:n], pmix[:, g, :], pooled[:, g, c0:c0 + n], True, True, ["pmix", ("pooled", g, sk)], bk)
                        act(pool_out[:, g, c0:c0 + n], pbs[bk][:, 0:n], AF.Copy, [PB(bk), "small"], [("pool_out", g, sk)],
                            scale=small[:, g:g + 1])

            def z_group(h, j):
                S_ = HS[h]
                if j == 0:
                    wv, wk = w_get(1 + 2 * h)
                    S_["wh"] = wv[:].rearrange("p (k j c) -> p k j c", k=8, j=4)
                    S_["wk"] = wk
                    S_["bs"] = nbZ() if has_s else None
                    S_["zb"] = []
                wh, wk, bs = S_["wh"], S_["wk"], S_["bs"]
                bp = nbZ()
                for k in range(8):
                    mm(pbs[bp][:, 0:512], wh[:, k, j, :], hT[:, k, 0:512], k == 0, k == 7, [wk] + hTk_p, bp)
                    if has_s:
                        mm(pbs[bs][:, j * 128:(j + 1) * 128], wh[:, k, j, :], hT[:, k, 512:640], k == 0, k == 7,
                           [wk, ("hT", 4)], bs)
                S_["zb"].append(bp)
                if j == 3:
                    w_release(1 + 2 * h)

            def z_evac(h):
                S_ = HS[h]
                zb, bs = S_["zb"], S_["bs"]
                gbh = gb2[h % 2]
                gbk = f"gbf{h % 2}"

                def zsrc(j, sk):
                    return (pbs[zb[j]][:, 0:512], PB(zb[j])) if sk == "p" else (pbs[bs][:, j * 128:(j + 1) * 128], PB(bs))

                for (c0, n, sk) in subs:
                    src, key = zsrc(1, sk)
                    act(t1[:, c0:c0 + n], src, AF.Sigmoid, [key], [("t1", sk)])
                    src, key = zsrc(0, sk)
                    act(qf[:, c0:c0 + n], src, AF.Sigmoid, [key], [("qf", sk)])
                    tt(qf[:, c0:c0 + n], qf[:, c0:c0 + n], src, ALU.mult, [("qf", sk), key], [("qf", sk)])
                    src, key = zsrc(3, sk)
                    act(t4[:, c0:c0 + n], src, AF.Sigmoid, [key], [("t4", sk)])
                    tt(gbh[:, c0:c0 + n], t4[:, c0:c0 + n], src, ALU.mult, [("t4", sk), key], [(gbk, sk)])
                    src, key = zsrc(2, sk)
                    cp(vb[:, c0:c0 + n], src, [key], [("vb", sk)])
                warm(AF.Ln)

            def g_mm(h, jj):
                S_ = HS[h]
                if jj == 0:
                    wgv_, wgk_ = w_get(2 + 2 * h)
                    S_["wgv"] = wgv_[:].rearrange("p (k t c) -> p k t c", k=8, t=4)
                    S_["wgk"] = wgk_
                    S_["gate_ev"] = []
                wgv, wgk_ = S_["wgv"], S_["wgk"]
                j = 2 * h + jj
                bsg = nbG() if has_s else None
                for t_ in range(2):
                    bpg = nbG()
                    for k in range(8):
                        mm(pbs[bpg][:, 0:512], wgv[:, k, 2 * jj + t_, :], hT[:, k, 0:512], k == 0, k == 7, [wgk_] + hTk_p, bpg)
                        if has_s:
                            mm(pbs[bsg][:, t_ * 128:(t_ + 1) * 128], wgv[:, k, 2 * jj + t_, :], hT[:, k, 512:640], k == 0, k == 7,
                               [wgk_, ("hT", 4)], bsg)
                    S_["gate_ev"].append((j, t_, bpg, bsg))
                if jj == 1:
                    w_release(2 + 2 * h)

            def g_evac(h, jj):
                for (j, t_, bpg, bsg) in HS[h]["gate_ev"][2 * jj:2 * jj + 2]:
                    dst = sgA if t_ == 0 else sgB
                    dk = "sgA" if t_ == 0 else "sgB"
                    cp(dst[:, j, 0:512], pbs[bpg][:, 0:512], [PB(bpg)], [(dk, j, "p")], eng="act")
                    if has_s:
                        cp(dst[:, j, 512:640], pbs[bsg][:, t_ * 128:(t_ + 1) * 128], [PB(bsg)], [(dk, j, "s")], eng="act")

            def chain_a(h):
                ts(t1[:, 0:NT], t1[:, 0:NT], lbv[:, 4 + h:5 + h], lbv[:, h:h + 1], ALU.mult, ALU.add, K("t1") + ["lbv"], K("t1"))
                ts(t2[:, 0:NT], t1[:, 0:NT], -1.0, 1.0, ALU.mult, ALU.add, K("t1"), K("t2"))
                act(t1[:, 0:NT], t1[:, 0:NT], AF.Ln, K("t1"), K("t1"))

            def cb_scan(h):
                A("dve", lambda e: e.tensor_tensor_scan(out=t3[:, 0:NT], data0=resetm[:, 0:NT], data1=t1[:, 0:NT],
                                                        initial=0.0, op0=ALU.mult, op1=ALU.add),
                  K("t1") + ["resetmb"], K("t3"))

            def cb_exp(h):
                act(t1[:, 0:NT], t3[:, 0:NT], AF.Exp, K("t3"), K("t1"))
                act(t4[:, 0:NT], t3[:, 0:NT], AF.Exp, K("t3"), K("t4"), scale=-1.0)

            def cb_mul(h):
                qe_ = QE[h % 2]
                qk = f"qe{h % 2}"
                tt(qe_[:, 0:NT], qf[:, 0:NT], t1[:, 0:NT], ALU.mult, K("qf") + K("t1"), [(qk, "p")] + ([(qk, "s")] if has_s else []))
                tt(t2[:, 0:NT], t2[:, 0:NT], t4[:, 0:NT], ALU.mult, K("t2") + K("t4"), K("t2"))

            def cb_keb(h):
                cp(keb[:, 0:NT], t2[:, 0:NT], K("t2"), K("keb"), eng="act")

            def cb_dec(h):
                dec_ = DEC[h % 2]
                dk_ = f"dec{h % 2}"
                cp(dec_[:, 0:8], t1[:, 63:512:64], [("t1", "p")], [(dk_, "p")])
                tt(kdb[:, 0:512].rearrange("p (c t) -> p c t", t=64), t2[:, 0:512].rearrange("p (c t) -> p c t", t=64),
                   dec_[:, 0:8].unsqueeze(2).broadcast_to([128, 8, 64]), ALU.mult, [("t2", "p"), (dk_, "p")], [("kdb", "p")])
                if has_s:
                    cp(dec_[:, 8:24], t1[:, 519:640:8], [("t1", "s")], [(dk_, "s")])
                    tt(kdb[:, 512:640].rearrange("p (c t) -> p c t", t=8), t2[:, 512:640].rearrange("p (c t) -> p c t", t=8),
                       dec_[:, 8:24].unsqueeze(2).broadcast_to([128, 16, 8]), ALU.mult, [("t2", "s"), (dk_, "s")], [("kdb", "s")])

            def chain_b(h):
                cb_scan(h); cb_exp(h); cb_mul(h); cb_keb(h); cb_dec(h)

            def R1(h):
                qe = QE[h % 2]
                qk = f"qe{h % 2}"
                for (src_, srck, dst, dstk) in ((kdb, "kdb", kdT, "kdT"), (vb, "vb", vT, "vT")):
                    bt = nbR()
                    bv = pbs[bt][:].bitcast(BF16)
                    for b in blocks:
                        sk = "p" if b < 4 else "s"
                        tp(bv[:, b * 128:(b + 1) * 128], src_[:, b * 128:(b + 1) * 128], identb[:], [(srck, sk), "identb"], bt)
                    nbk = len(blocks)
                    cp(dst[:, 0:nbk, :], bv[:, 0:nbk * 128].rearrange("p (b k) -> p b k", k=128), [PB(bt)], [dstk], eng="act")
                ba = nbR()
                for b in range(4):
                    mm(pbs[ba][:, b * 128:(b + 1) * 128], keb[:, b * 128:(b + 1) * 128], qe[:, b * 128:(b + 1) * 128],
                       True, True, [("keb", "p"), (qk, "p")], ba)
                tt(Am[:, 0:4, :], pbs[ba][:, 0:512].rearrange("p (b t) -> p b t", b=4),
                   maskPb[:].unsqueeze(1).broadcast_to([128, 4, 128]), ALU.mult, [PB(ba), "maskPb"], [("Am", "p")])
                if has_s:
                    bas = nbR()
                    mm(pbs[bas][:, 0:128], keb[:, 512:640], qe[:, 512:640], True, True, [("keb", "s"), (qk, "s")], bas)
                    tt(Am[:, 4, :], pbs[bas][:, 0:128], maskSb[:], ALU.mult, [PB(bas), "maskSb"], [("Am", "s")])
                    tt(Vm[:, :, :], vT[:, 4, :].unsqueeze(1).broadcast_to([128, 16, 128]),
                       seqm.unsqueeze(2).broadcast_to([128, 16, 128]), ALU.mult, ["vT", "cs"], ["Vm"])

            def sample_prefetch(h):
                P.dma("sp", "d_S0", lambda e, h=h: e.dma_start(out=S0[:, :, :], in_=shg[:, h, :, :].rearrange("i k v -> k i v")),
                      writes=["S0"])

            def sample_bf(h):
                cp(S0bf[:, :, :].rearrange("p i v -> p (i v)"), S0[:, :, :].rearrange("p i v -> p (i v)"), ["S0"], ["S0bf"], eng="act")

            def sample_state_update(h):
                dec = DEC[h % 2]
                dk_ = f"dec{h % 2}"
                tt(S0[:, :, :], S0[:, :, :], dec[:, 8:24].unsqueeze(2).broadcast_to([128, 16, 128]), ALU.mult,
                   ["S0", (dk_, "s")], ["S0"])
                for q4 in range(4):
                    bd = nbR()
                    mm(pbs[bd][:, 0:512], kdT[:, 4, :], Vm[:, 4 * q4:4 * q4 + 4, :].rearrange("p i v -> p (i v)"), True, True,
                       ["kdT", "Vm"], bd)
                    tt(S0[:, 4 * q4:4 * q4 + 4, :].rearrange("p i v -> p (i v)"),
                       S0[:, 4 * q4:4 * q4 + 4, :].rearrange("p i v -> p (i v)"), pbs[bd][:, 0:512], ALU.add,
                       ["S0", PB(bd)], ["S0"])
                P.dma("sp", "d_nhs", lambda e, h=h: e.dma_start(out=nhs[:, h, :, :].rearrange("i k v -> k i v"), in_=S0[:, :, :]),
                      reads=["S0"])

            def R2_mm(h):
                dsb = [nbR(), nbR()]
                HS[h]["dsb"] = dsb
                for c in range(8):
                    blk, half = c // 2, c % 2
                    bk = dsb[half]
                    mm(pbs[bk][:, blk * 128:(blk + 1) * 128], kdT[half * 64:(half + 1) * 64, blk, :],
                       vT[half * 64:(half + 1) * 64, blk, :], True, True, ["kdT", "vT"], bk)
                cp(Sall[:, 0, :], carry[:, h, :], ["carry"], [("Sall", 0)])

            def R2_steps(h, c0, c1):
                dec_ = DEC[h % 2]
                dk_ = f"dec{h % 2}"
                dsb = HS[h]["dsb"]
                for c in range(c0, c1):
                    blk, half = c // 2, c % 2
                    bk = dsb[half]
                    stt(Sall[:, c + 1, :], Sall[:, c, :], dec_[:, c:c + 1], pbs[bk][:, blk * 128:(blk + 1) * 128],
                        ALU.mult, ALU.add, [("Sall", c), (dk_, "p"), PB(bk)], [("Sall", c + 1)])

            def R2_fin(h):
                cp(Sbf[:, :, :].rearrange("p c v -> p (c v)"), Sall[:, 0:8, :].rearrange("p c v -> p (c v)"),
                   [("Sall", c) for c in range(8)], ["Sbf"], eng="act")
                cp(carry[:, h, :], Sall[:, 8, :], [("Sall", 8)], ["carry"])
                if last:
                    P.dma("sp", f"d_nhp{h}", lambda e, h=h: e.dma_start(out=nhp[h], in_=Sall[:, 8, :]), reads=[("Sall", 8)])

            def R2(h):
                R2_mm(h); R2_steps(h, 0, 8); R2_fin(h)

            def R3(h):
                qe = QE[h % 2]
                qk = f"qe{h % 2}"
                S_ = HS[h]
                bo = nbR()
                for b in range(4):
                    mm(pbs[bo][:, b * 128:(b + 1) * 128], vT[:, b, :], Am[:, b, :], True, False, ["vT", ("Am", "p")], bo)
                    for half in range(2):
                        c = 2 * b + half
                        mm(pbs[bo][:, c * 64:(c + 1) * 64], Sbf[:, c, :], qe[:, c * 64:(c + 1) * 64], False, half == 1,
                           ["Sbf", (qk, "p")], bo)
                bos = None
                if has_s:
                    sample_bf(h)
                    bos = nbR()
                    mm(pbs[bos][:, 0:128], vT[:, 4, :], Am[:, 4, :], True, False, ["vT", ("Am", "s")], bos)
                    for i in range(16):
                        mm(pbs[bos][:, i * 8:(i + 1) * 8], S0bf[:, i, :], qe[:, 512 + i * 8:512 + (i + 1) * 8], False, i == 15,
                           ["S0bf", (qk, "s")], bos)
                S_["obanks"] = [(0, 512, "p", bo)] + ([(512, 128, "s", bos)] if has_s else [])
                for (c0, n, sk, bk) in S_["obanks"]:
                    act(osq[:, c0:c0 + n], pbs[bk][:, 0:n], AF.Square, [PB(bk)], [("osq", sk)])

            def R4(h):
                gbh = gb2[h % 2]
                gbk = f"gbf{h % 2}"
                for (c0, n, sk, bk) in HS[h]["obanks"]:
                    bn = nbR()
                    mm(pbs[bn][:, 0:n], onesd[:], osq[:, c0:c0 + n], True, True, ["onesd", ("osq", sk)], bn)
                    act(s1[:, c0:c0 + n], pbs[bn][:, 0:n], AF.Ln, [PB(bn)], [("s1", sk)], bias=EPS)
                    act(s1[:, c0:c0 + n], s1[:, c0:c0 + n], AF.Exp, [("s1", sk)], [("s1", sk)], scale=-0.5)
                    stt(s2[:, c0:c0 + n], pbs[bk][:, 0:n], small[:, 4 + h:5 + h], s1[:, c0:c0 + n], ALU.mult, ALU.mult,
                        [PB(bk), "small", ("s1", sk)], [("s2", sk)])
                tt(o_fin[:, h, 0:NT], s2[:, 0:NT], gbh[:, 0:NT], ALU.mult, K("s2") + [(gbk, "p"), (gbk, "s")],
                   [("o_fin", h, s_) for s_ in ("p", "s")])

            if PIPE:
                for j in range(4):
                    z_group(0, j)
            poolmix()
            ck("poolmix")
            P.barrier()
            ck("bar")
            if PIPE:
                for h in range(4):
                    p = h - 1
                    if h > 0:
                        R1(p)
                    z_evac(h)
                    g_mm(h, 0)
                    if h > 0:
                        R2_mm(p)
                    chain_a(h)
                    if h > 0:
                        R2_steps(p, 0, 4)
                    g_evac(h, 0)
                    if h < 3:
                        for j in range(4):
                            z_group(h + 1, j)
                    cb_scan(h)
                    if h > 0:
                        R2_steps(p, 4, 7)
                    cb_exp(h)
                    cb_mul(h)
                    cb_dec(h)
                    cb_keb(h)
                    g_mm(h, 1)
                    if h > 0:
                        R2_steps(p, 7, 8)
                        R2_fin(p)
                        R3(p)
                    g_evac(h, 1)
                    if h > 0:
                        R4(p)
                    warm(AF.Sigmoid)
                R1(3); R2(3); R3(3); R4(3)
                warm(AF.Sigmoid)
            else:
                for h in range(4):
                    sample_prefetch(h)
                    for j in range(4):
                        z_group(h, j)
                    z_evac(h)
                    g_mm(h, 0)
                    g_mm(h, 1)
                    chain_a(h)
                    chain_b(h)
                    g_evac(h, 0)
                    g_evac(h, 1)
                    R1(h); R2(h); R3(h); R4(h)
                    warm(AF.Sigmoid)
                    sample_state_update(h)

            ck("hgrn")
            for jh in range(2):
                wy, wyk = w_get(9 + jh)
                wyv = wy[:].rearrange("p (k t c) -> p k t c", k=4, t=2)
                for jj in range(4):
                    j = jh * 4 + jj
                    bs = nb() if has_s else None
                    b_ya, b_yb = nb(), nb()
                    for t_, bk, srcb, srck in ((0, b_ya, pool_out, "pool_out"), (1, b_yb, o_fin, "o_fin")):
                        for kk in range(4):
                            for (c0, n, sk) in subs:
                                o = pbs[bk][:, 0:512] if sk == "p" else pbs[bs][:, t_ * 128:(t_ + 1) * 128]
                                mm(o, wyv[:, kk, t_, jj * 128:(jj + 1) * 128], srcb[:, kk, c0:c0 + n], kk == 0, kk == 3,
                                   [wyk, (srck, kk, sk)], bk if sk == "p" else bs)
                    if jj == 3:
                        w_release(9 + jh)
                    for (c0, n, sk) in subs:
                        oa = pbs[b_ya][:, 0:512] if sk == "p" else pbs[bs][:, 0:128]
                        ob = pbs[b_yb][:, 0:512] if sk == "p" else pbs[bs][:, 128:256]
                        ka = PB(b_ya) if sk == "p" else PB(bs)
                        kb_ = PB(b_yb) if sk == "p" else PB(bs)
                        act(s1[:, c0:c0 + n], sgA[:, j, c0:c0 + n], AF.Sigmoid, [("sgA", j, sk)], [("s1", sk)])
                        act(s2[:, c0:c0 + n], sgB[:, j, c0:c0 + n], AF.Sigmoid, [("sgB", j, sk)], [("s2", sk)])
                        tt(s1[:, c0:c0 + n], s1[:, c0:c0 + n], oa, ALU.mult, [("s1", sk), ka], [("s1", sk)])
                        tt(s2[:, c0:c0 + n], s2[:, c0:c0 + n], ob, ALU.mult, [("s2", sk), kb_], [("s2", sk)])
                    tt(merged[:, j, 0:NT], s1[:, 0:NT], s2[:, 0:NT], ALU.add, K("s1") + K("s2"),
                       [("merged", j, s_) for s_ in ("p", "s")])

            ck("merge")
            warm(AF.Ln)
            for half in range(2):
                wo, wok = w_get(11 + half)
                wov = wo[:].rearrange("p (k c) -> p k c", k=8)
                bks = {b: nb() for b in blocks}
                for k in range(8):
                    for b in blocks:
                        sk = "p" if b < 4 else "s"
                        mm(pbs[bks[b]][:, 0:512], merged[:, k, b * 128:(b + 1) * 128], wov[:, k, :], k == 0, k == 7,
                           [wok, ("merged", k, sk)], bks[b])
                w_release(11 + half)
                for b in blocks:
                    tt(X[:, b, half * 512:(half + 1) * 512], X[:, b, half * 512:(half + 1) * 512], pbs[bks[b]][:, 0:512],
                       ALU.add, [(xk, b), PB(bks[b])], [(xk, b)])
                    half_stats(b, half)

            ck("wout")
            norm_T(1, True)
            warm(AF.Sigmoid)
            P.barrier()
            if ti + 1 < ntiles:
                for b in range(4):
                    load_x(ti + 1, b)
            for b in blocks:
                src = ppd[ti * 512 + b * 128: ti * 512 + (b + 1) * 128, :] if b < 4 else psd
                P.dma("sp", f"d_p{b}", lambda e, src=src, b=b: e.dma_start(out=pstage[:, b, :], in_=src), writes=[("pstage", b)])
            for jb in range(11):
                wf, wfk = w_get(13 + jb)
                wfv = wf[:].rearrange("p (k j c) -> p k j c", k=8, j=2)
                for cc in range(2):
                    f = 2 * jb + cc
                    bs = nb() if has_s else None
                    b_g, b_u = nb(), nb()
                    if f == 0:
                        for hc in range(2):
                            for jx, bk in ((0, b_g), (1, b_u)):
                                for k in range(8):
                                    mm(pbs[bk][:, hc * 256:(hc + 1) * 256], wfv[:, k, jx, cc * 128:(cc + 1) * 128],
                                       hT[:, k, hc * 256:(hc + 1) * 256], k == 0, k == 7,
                                       [wfk, ("hT", 2 * hc), ("hT", 2 * hc + 1)], bk)
                        if has_s:
                            for jx, bk in ((0, b_g), (1, b_u)):
                                for k in range(8):
                                    mm(pbs[bs][:, jx * 128:(jx + 1) * 128], wfv[:, k, jx, cc * 128:(cc + 1) * 128], hT[:, k, 512:640],
                                       k == 0, k == 7, [wfk] + hTk["s"], bs)
                    else:
                        for jx, bk in ((0, b_g), (1, b_u)):
                            for k in range(8):
                                for (c0, n, sk) in subs:
                                    o = pbs[bk][:, 0:512] if sk == "p" else pbs[bs][:, jx * 128:(jx + 1) * 128]
                                    mm(o, wfv[:, k, jx, cc * 128:(cc + 1) * 128], hT[:, k, c0:c0 + n], k == 0, k == 7,
                                       [wfk] + hTk[sk], bk if sk == "p" else bs)
                    if cc == 1:
                        w_release(13 + jb)
                    for (c0, n, sk) in subs:
                        og = pbs[b_g][:, 0:512] if sk == "p" else pbs[bs][:, 0:128]
                        ou = pbs[b_u][:, 0:512] if sk == "p" else pbs[bs][:, 128:256]
                        kg = PB(b_g) if sk == "p" else PB(bs)
                        ku = PB(b_u) if sk == "p" else PB(bs)
                        act(ftmp[:, c0:c0 + n], og, AF.Sigmoid, [kg], [("ftmp", sk)])
                        tt(ftmp[:, c0:c0 + n], ftmp[:, c0:c0 + n], og, ALU.mult, [("ftmp", sk), kg], [("ftmp", sk)])
                        tt(hidden[:, f, c0:c0 + n], ftmp[:, c0:c0 + n], ou, ALU.mult, [("ftmp", sk), ku], [("hidden", f, sk)])
            for b in blocks:
                cp(pbf[:], pstage[:, b, :], [("pstage", b)], ["pbf"], eng="act")
                bt = nb()
                bv = pbs[bt][:].bitcast(BF16)
                for k in range(2):
                    tp(bv[:, k * 128:(k + 1) * 128], pbf[:, k * 128:(k + 1) * 128], identb[:], ["pbf", "identb"], bt)
                cp(pT[:, :, b * 128:(b + 1) * 128], bv[:, 0:256].rearrange("p (k t) -> p k t", k=2), [PB(bt)], [("pT", b)])
            warm(AF.Ln)
            for half in range(2):
                bks = {b: nb() for b in blocks}
                for kb in range(3):
                    wd, wdk = w_get(24 + half * 3 + kb)
                    wdv = wd[:].rearrange("p (k c) -> p k c", k=8)
                    nk = 8 if kb < 2 else 6
                    for kk in range(nk):
                        f = kb * 8 + kk
                        for b in blocks:
                            sk = "p" if b < 4 else "s"
                            mm(pbs[bks[b]][:, 0:512], hidden[:, f, b * 128:(b + 1) * 128], wdv[:, kk, :], f == 0, f == 21,
                               [wdk, ("hidden", f, sk)], bks[b])
                    w_release(24 + half * 3 + kb)
                for b in blocks:
                    tt(X[:, b, half * 512:(half + 1) * 512], X[:, b, half * 512:(half + 1) * 512], pbs[bks[b]][:, 0:512],
                       ALU.add, [(xk, b), PB(bks[b])], [(xk, b)])
                    half_stats(b, half)

            ck("ffn")
            norm_T(2, True)
            if ti + 1 < ntiles:
                Xn, xkn = xbuf(ti + 1)
                norm_stats(False, blks=[0, 1, 2, 3], X_=Xn, xk_=xkn, ssq_=ssq1, rst_=rst1, tag="ssq1", rtag="rst1")
            warm(AF.Sigmoid)
            wg0, wg0k = w_get(30)
            wpp_, wppk = w_get(31)
            wg1, wg1k = w_get(32)
            wppv = wpp_[:, 0:2048].rearrange("p (k c) -> p k c", k=2)
            for half in range(2):
                wg_, wgk = (wg0, wg0k) if half == 0 else (wg1, wg1k)
                wgv = wg_[:].rearrange("p (k c) -> p k c", k=8)
                bks = {b: nb() for b in blocks}
                if half == 0:
                    for b in blocks:
                        for k in range(8):
                            mm(pbs[bks[b]][:, 0:512], hT[:, k, b * 128:(b + 1) * 128], wgv[:, k, :], k == 0, k == 7,
                               [wgk, ("hT", b)], bks[b])
                else:
                    for k in range(8):
                        for b in blocks:
                            mm(pbs[bks[b]][:, 0:512], hT[:, k, b * 128:(b + 1) * 128], wgv[:, k, :], k == 0, k == 7,
                               [wgk, ("hT", b)], bks[b])
                if half == 0:
                    w_release(30)
                for b in blocks:
                    sg = sgt[b % 2]
                    sgk = f"sgt{b % 2}"
                    act(sg[:], pbs[bks[b]][:, 0:512], AF.Sigmoid, [PB(bks[b])], [sgk])
                    be = nb()
                    for k in range(2):
                        mm(pbs[be][:, 0:512], pT[:, k, b * 128:(b + 1) * 128], wppv[:, k, half * 512:(half + 1) * 512], k == 0, k == 1,
                           [wppk, ("pT", b)], be)
                    tt(sg[:], sg[:], pbs[be][:, 0:512], ALU.mult, [sgk, PB(be)], [sgk])
                    tt(X[:, b, half * 512:(half + 1) * 512], X[:, b, half * 512:(half + 1) * 512], sg[:], ALU.add,
                       [(xk, b), sgk], [(xk, b)])
                for b in blocks:
                    half_stats(b, half)
                if half == 1:
                    w_release(31); w_release(32)

            ck("ple")
            if ti + 1 < ntiles:
                Xn, xkn = xbuf(ti + 1)
                norm_apply(0, blks=[0, 1, 2, 3], X_=Xn, xk_=xkn, rst_=rst1, rtag="rst1")
            P.barrier()
            norm_stats(True)
            for b in blocks:
                dst = yp[ti * 512 + b * 128: ti * 512 + (b + 1) * 128, :] if b < 4 else ys
                for half in range(2):
                    q_ = (2 * b + half) % 4
                    yst = ystage[q_]
                    ysk = f"ystage{q_}"
                    hs_ = slice(half * 512, (half + 1) * 512)
                    stt(yst[:], X[:, b, hs_], rst[:, b:b + 1], gB[:, 3, hs_], ALU.mult, ALU.mult,
                        [(xk, b), "rst", ("gB", 3)], [ysk])
                    P.dma("sp", f"d_y{q_}", lambda e, yst=yst, dst=dst, hs_=hs_: e.dma_start(out=dst[:, hs_], in_=yst[:]),
                          reads=[ysk])

        P.barrier()
        try:
            ck("setup")
            for ti in range(ntiles):
                run_tile(ti)
        except _Stop:
            pass
        P.finish("sp")
        P.emit()
    return nc


_PROG = {}


def _prep_inputs(inp):
    f = lambda a: np.ascontiguousarray(np.asarray(a, dtype=np.float32))
    w_in = f(inp["w_in"][0])
    wallv = build_wall(w_in, f(inp["w_pool_up"][0]), f(inp["w_hgrn_up"][0]), f(inp["w_out"][0]),
                       f(inp["w_ffn_gate"][0]), f(inp["w_ffn_up"][0]), f(inp["w_ffn_down"][0]),
                       f(inp["w_ple_gate"][0]), f(inp["w_ple_proj"][0]))
    gvec = np.ascontiguousarray(np.stack([f(inp["g_mix"][0]), f(inp["g_ffn"][0]), f(inp["g_ple"][0]), f(inp["g_final"])], 0))
    small = np.zeros((128, 16), np.float32)
    small[:, 0:4] = f(inp["pool_scale"][0]).reshape(4, 128).T
    small[:, 4:8] = f(inp["hgrn_norm"][0]).reshape(4, 128).T
    small[:, 8:12] = f(inp["hgrn_lb"][0]).reshape(4, 128).T
    small[:, 12:16] = f(inp["hgrn_lb"][1]).reshape(4, 128).T
    pmix = np.ascontiguousarray(f(inp["w_pool_mix"][0]).transpose(1, 0, 2)).reshape(128, 512)
    xp = f(inp["x_prompt"]); xsm = f(inp["x_sample"])
    ppr = f(inp["p_prompt"][0]); psm = f(inp["p_sample"][0])
    spl = f(inp["state_pool"][0]); shg = f(inp["state_hgrn"][0])
    maps = []
    for c in range(NCORES):
        maps.append({
            "xp": xp[c], "xs": xsm[16 * c:16 * c + 16].reshape(128, 1024),
            "pp": ppr[c], "ps": psm[16 * c:16 * c + 16].reshape(128, 256),
            "spool": spl[16 * c:16 * c + 16].reshape(240, 512),
            "shg": shg[16 * c:16 * c + 16],
            "wall": wallv, "cst": CONST_ARR, "gvec": gvec, "small": small, "pmix": pmix,
        })
    return maps


def kernel(**inputs):
    if "nc" not in _PROG:
        _PROG["nc"] = build_program()
    nc = _PROG["nc"]
    maps = _prep_inputs(inputs)
    res = run_bass_kernel_spmd(nc, maps, core_ids=list(range(NCORES)))
    R = res.results
    y_p = np.stack([R[c]["yp"] for c in range(NCORES)], 0).astype(np.float32)
    y_s = np.concatenate([R[c]["ys"].reshape(16, 8, 1024) for c in range(NCORES)], 0).astype(np.float32)
    npp = np.stack([R[c]["npp"] for c in range(NCORES)], 0)[None].astype(np.float32)
    nhp = np.stack([R[c]["nhp"] for c in range(NCORES)], 0)[None].astype(np.float32)
    nps = np.concatenate([R[c]["nps"].reshape(16, 15, 512) for c in range(NCORES)], 0)[None].astype(np.float32)
    nhs = np.concatenate([R[c]["nhs"] for c in range(NCORES)], 0)[None].astype(np.float32)
    return (y_p, y_s, npp, nhp, nps, nhs)
```

```python
import numpy as np
from contextlib import ExitStack
import concourse.bass as bass
import concourse.mybir as mybir
from concourse.bass_utils import run_bass_kernel_spmd

F32 = mybir.dt.float32
BF16 = mybir.dt.bfloat16
AF = mybir.ActivationFunctionType
ALU = mybir.AluOpType

ENGS = ("pe", "act", "dve", "pool", "sp")
EPS = 1e-6
NCORES = 8
NRING = 5
NTILES = 4
import os as _os
SAME_DIST = int(_os.environ.get("SAME_DIST", str(1 << 30)))


class Prog:
    def __init__(self, nc, stack, same_engine_wait=True):
        self.nc = nc
        self.stack = stack
        self.streams = {e: [] for e in ENGS}
        self.count = {e: 0 for e in ENGS}
        self.known = {e: {} for e in ENGS}
        self.sems = {}
        self.dma_count = {}
        self.lastw = {}
        self.readers = {}
        self.same_engine_wait = same_engine_wait

    def _collect(self, eng, reads, writes):
        deps = []
        for k in reads:
            ev = self.lastw.get(k)
            if ev is not None:
                deps.append(ev)
        for k in writes:
            ev = self.lastw.get(k)
            if ev is not None:
                deps.append(ev)
            rd = self.readers.get(k)
            if rd:
                deps.extend(rd.values())
        kn = self.known[eng]
        need = {}
        for (s, v, vc) in deps:
            if s == eng and (eng == "pe" or not self.same_engine_wait or self.count[eng] - v >= SAME_DIST):
                continue
            if kn.get(s, 0) >= v:
                continue
            if need.get(s, 0) < v:
                need[s] = v
            for s2, v2 in vc.items():
                if s2 == eng:
                    continue
                if kn.get(s2, 0) < v2:
                    kn[s2] = v2
        waits = []
        for s, v in need.items():
            waits.append((s, v))
            if kn.get(s, 0) < v:
                kn[s] = v
        return waits

    def _record(self, ev, reads, writes):
        s = ev[0]
        for k in reads:
            self.readers.setdefault(k, {})[s] = ev
        for k in writes:
            self.lastw[k] = ev
            self.readers[k] = {}

    def op(self, eng, fn, reads=(), writes=()):
        waits = self._collect(eng, reads, writes)
        self.count[eng] += 1
        n = self.count[eng]
        vc = dict(self.known[eng])
        vc[eng] = n
        ev = (eng, n, vc)
        self.streams[eng].append((waits, fn, (eng, 1)))
        self._record(ev, reads, writes)
        return ev

    def dma(self, q, semname, fn, reads=(), writes=()):
        waits = self._collect(q, reads, writes)
        self.dma_count[semname] = self.dma_count.get(semname, 0) + 1
        v = 16 * self.dma_count[semname]
        vc = dict(self.known[q])
        vc[semname] = v
        ev = (semname, v, vc)
        self.streams[q].append((waits, fn, (semname, 16)))
        self._record(ev, reads, writes)
        return ev

    def barrier(self, engines=("act", "dve", "sp")):
        for e in engines:
            kn = self.known[e]
            waits = []
            for e2 in ("pe", "act", "dve"):
                c = self.count[e2]
                if c and kn.get(e2, 0) < c:
                    waits.append((e2, c))
                    kn[e2] = c
            for s, c in self.dma_count.items():
                if s.startswith("ring") or s.startswith("d_y") or s.startswith("d_x"):
                    continue
                if kn.get(s, 0) < 16 * c:
                    waits.append((s, 16 * c))
                    kn[s] = 16 * c
            if waits:
                self.streams[e].append((waits, None, None))

    def finish(self, eng="sp"):
        kn = self.known[eng]
        waits = []
        for s, c in self.dma_count.items():
            if kn.get(s, 0) < 16 * c:
                waits.append((s, 16 * c))
                kn[s] = 16 * c
        for e in ("pe", "act", "dve"):
            if self.count[e] and kn.get(e, 0) < self.count[e]:
                waits.append((e, self.count[e]))
        self.streams[eng].append((waits, None, None))

    def emit(self):
        nc = self.nc
        for s in list(ENGS) + list(self.dma_count):
            if s not in self.sems:
                self.sems[s] = self.stack.enter_context(nc.semaphore(s))
        with nc.Block() as block:
            def run(engine, items):
                for waits, fn, inc in items:
                    for s, v in waits:
                        engine.wait_ge(self.sems[s], v)
                    if fn is not None:
                        ins = fn(engine)
                        ins.then_inc(self.sems[inc[0]], inc[1])

            @block.tensor
            def _(eng):
                run(eng, self.streams["pe"])

            @block.scalar
            def _(eng):
                run(eng, self.streams["act"])

            @block.vector
            def _(eng):
                run(eng, self.streams["dve"])

            @block.gpsimd
            def _(eng):
                run(eng, self.streams["pool"])

            @block.sync
            def _(eng):
                run(eng, self.streams["sp"])


NBLK = 33


def _kc(W, nk):
    C = W.shape[1]
    return np.ascontiguousarray(W.reshape(nk, 128, C).transpose(1, 0, 2)).reshape(128, nk * C)


def _pad(a):
    out = np.zeros((128, 4096), np.float32)
    out[:, : a.shape[1]] = a
    return out


def build_wall(w_in, w_pool_up, w_hgrn_up, w_out, w_g, w_u, w_d, w_pg, w_pp):
    blocks = []
    blocks.append(_kc(w_in[:, 0:512], 8))
    zz = w_in[:, 512:2560].reshape(8, 128, 4, 4, 128)
    ga = w_in[:, 2560:3584].reshape(8, 128, 8, 128)
    gb = w_in[:, 3584:4608].reshape(8, 128, 8, 128)
    for h in range(4):
        blocks.append(np.ascontiguousarray(zz[:, :, :, h, :].transpose(1, 0, 2, 3)).reshape(128, 4096))
        gg = np.stack([ga[:, :, 2 * h], gb[:, :, 2 * h], ga[:, :, 2 * h + 1], gb[:, :, 2 * h + 1]], axis=2)
        blocks.append(np.ascontiguousarray(gg.transpose(1, 0, 2, 3)).reshape(128, 4096))
    for half in range(2):
        yy = np.stack([w_pool_up[:, half * 512:(half + 1) * 512].reshape(4, 128, 512),
                       w_hgrn_up[:, half * 512:(half + 1) * 512].reshape(4, 128, 512)], axis=2)
        blocks.append(np.ascontiguousarray(yy.transpose(1, 0, 2, 3)).reshape(128, 4096))
    blocks.append(_kc(w_out[:, 0:512], 8))
    blocks.append(_kc(w_out[:, 512:1024], 8))
    for jb in range(11):
        gu = np.stack([w_g[:, jb * 256:(jb + 1) * 256], w_u[:, jb * 256:(jb + 1) * 256]], axis=1)
        blocks.append(_kc(gu.reshape(1024, 512), 8))
    for half in range(2):
        for kb in range(3):
            nk = 8 if kb < 2 else 6
            blocks.append(_pad(_kc(w_d[kb * 1024: kb * 1024 + nk * 128, half * 512:(half + 1) * 512], nk)))
    blocks.append(_kc(w_pg[:, 0:512], 8))
    blocks.append(_pad(_kc(w_pp, 2)))
    blocks.append(_kc(w_pg[:, 512:1024], 8))
    assert len(blocks) == NBLK
    return np.ascontiguousarray(np.stack(blocks, axis=0).astype(np.float32))


def build_consts():
    c = {}
    c["ident"] = np.eye(128, dtype=np.float32)
    s = np.arange(128)[:, None]
    t = np.arange(128)[None, :]
    c["maskP"] = ((s // 64 == t // 64) & (s <= t)).astype(np.float32)
    c["maskS"] = ((s // 8 == t // 8) & (s <= t)).astype(np.float32)
    c["seqm"] = (s // 8 == np.arange(16)[None, :]).astype(np.float32)
    r = np.ones(640, np.float32)
    r[0:512:64] = 0.0
    r[512:640:8] = 0.0
    c["resetm"] = np.broadcast_to(r, (128, 640)).copy()
    rc = np.zeros((4, 16), np.float32)
    for g, w in enumerate((2, 4, 8, 16)):
        rc[g] = 1.0 / np.minimum(np.arange(16) + 1, w)
    c["rc"] = np.broadcast_to(rc.reshape(1, 64), (128, 64)).copy()
    order = ["ident", "seqm", "rc", "maskP", "maskS", "resetm"]
    offs = {}
    o = 0
    for k in order:
        offs[k] = (o, c[k].shape[1])
        o += c[k].shape[1]
    return np.ascontiguousarray(np.concatenate([c[k] for k in order], axis=1)), offs


CONST_ARR, COFF = build_consts()
CW = CONST_ARR.shape[1]
CW_KEEP = COFF["maskP"][0]


class _Stop(Exception):
    pass


STOP = None


def ck(name):
    if STOP == name:
        raise _Stop()


def build_program(ntiles=NTILES):
    nc = bass.Bass("TRN2", target_bir_lowering=False)

    def din(name, shape):
        return nc.dram_tensor(name, shape, F32, kind="ExternalInput").ap()

    def dout(name, shape):
        return nc.dram_tensor(name, shape, F32, kind="ExternalOutput").ap()

    xp = din("xp", [2048, 1024])
    xs = din("xs", [128, 1024])
    ppd = din("pp", [2048, 256])
    psd = din("ps", [128, 256])
    spool = din("spool", [240, 512])
    shg = din("shg", [16, 4, 128, 128])
    wall = din("wall", [NBLK, 128, 4096])
    cst = din("cst", [128, CW])
    gvec = din("gvec", [4, 1024])
    smalld = din("small", [128, 16])
    pmixd = din("pmix", [128, 512])
    yp = dout("yp", [2048, 1024])
    ys = dout("ys", [128, 1024])
    npp = dout("npp", [15, 512])
    nhp = dout("nhp", [4, 128, 128])
    nps = dout("nps", [240, 512])
    nhs = dout("nhs", [16, 4, 128, 128])

    with ExitStack() as st:
        P = Prog(nc, st)

        def sb(name, shape, dt):
            return st.enter_context(nc.sbuf_tensor("sb_" + name, shape, dt))

        xres = sb("xres", [128, 5, 1024], F32)
        hT = sb("hT", [128, 8, 640], BF16)
        hn = [sb(f"hn{i}", [128, 1024], BF16) for i in range(2)]
        junk = sb("junk", [128, 512], BF16)
        ystage = [sb(f"ystage{i}", [128, 512], F32) for i in range(4)]
        resetmb = sb("resetmb", [128, 640], BF16)
        ring = [sb(f"ring{i}", [128, 4096], BF16) for i in range(NRING)]
        gB = sb("gB", [128, 4, 1024], F32)
        cs = sb("cs", [128, CW_KEEP], F32)
        identb = sb("identb", [128, 128], BF16)
        onesd = sb("onesd", [128, 128], BF16)
        maskPb = sb("maskPb", [128, 128], BF16)
        maskSb = sb("maskSb", [128, 128], BF16)
        pmixf = sb("pmixf", [128, 512], F32)
        pmix = sb("pmix", [128, 4, 128], BF16)
        small = sb("small", [128, 16], F32)
        lbv = sb("lbv", [128, 8], F32)
        carry = sb("carry", [128, 4, 128], F32)
        ucarry = sb("ucarry", [128, 4, 16], F32)
        ssq = sb("ssq", [128, 16], F32)
        ssq1 = sb("ssq1", [128, 8], F32)
        dmy = sb("dmy", [128, 2], F32)
        rst1 = sb("rst1", [128, 4], F32)
        rst = sb("rst", [128, 8], F32)
        identf = cs[:, COFF["ident"][0]:COFF["ident"][0] + 128]
        seqm = cs[:, COFF["seqm"][0]:COFF["seqm"][0] + 16]
        rcv = cs[:, COFF["rc"][0]:COFF["rc"][0] + 64]
        resetm = resetmb

        MIX_BYTES = 0
        scr_plan = {}

        def plan(phase, name, nelem, dt):
            nonlocal MIX_BYTES
            nbytes = nelem * (4 if dt == F32 else 2)
            nbytes = (nbytes + 31) // 32 * 32
            off = scr_plan.setdefault(("off", phase), 0)
            scr_plan[name] = (off, nelem, dt)
            scr_plan[("off", phase)] = off + nbytes

        U_NAMES = ["ug", "ue", "pa", "pb_", "sa", "sb_", "tmp16", "pooled", "npsb", "stg", "npp_sb"]
        for name, n, dt in [
            ("ug", 528, F32), ("ue", 16 * 24, F32), ("pa", 528, F32), ("pb_", 528, F32),
            ("sa", 16 * 24, F32), ("sb_", 16 * 24, F32), ("tmp16", 16, F32),
            ("pooled", 4 * 640, BF16), ("npsb", 4 * 240, F32),
            ("stg", 2 * 512, F32), ("npp_sb", 512, F32), ("pool_out", 4 * 640, BF16),
            ("qf", 640, F32), ("t1", 640, F32), ("t2", 640, F32), ("t3", 640, F32), ("t4", 640, F32),
            ("qe", 640, BF16), ("qe2", 640, BF16), ("keb", 640, BF16), ("kdb", 640, BF16), ("vb", 640, BF16), ("gbf", 640, BF16), ("gbf2", 640, BF16),
            ("kdT", 5 * 128, BF16), ("vT", 5 * 128, BF16), ("Sall", 9 * 128, F32), ("Sbf", 8 * 128, BF16),
            ("dec", 24, F32), ("dec2", 24, F32), ("Am", 5 * 128, BF16), ("osq", 640, BF16), ("o_fin", 4 * 640, BF16),
            ("S0", 16 * 128, F32), ("S0bf", 16 * 128, BF16), ("Vm", 16 * 128, BF16),
            ("merged", 8 * 640, BF16), ("s1", 640, F32), ("s2", 640, F32),
        ]:
            plan("mix", name, n, dt)
        for name, n, dt in [
            ("hidden", 22 * 640, BF16), ("ftmp", 640, F32), ("sgt0", 512, F32), ("sgt1", 512, F32),
            ("pT", 2 * 640, BF16), ("pstage", 5 * 256, F32), ("pbf", 256, BF16),
        ]:
            plan("ffn", name, n, dt)
        u_end = scr_plan["pool_out"][0]
        assert 2 * 8 * 640 * 2 <= u_end, u_end
        scr_plan["sgA"] = (0, 8 * 640, BF16)
        scr_plan["sgB"] = (8 * 640 * 2, 8 * 640, BF16)
        scr_bytes = max(scr_plan[("off", "mix")], scr_plan[("off", "ffn")])
        scr = sb("scr", [128, scr_bytes // 4], F32)

        def sv(name):
            off, n, dt = scr_plan[name]
            if dt == F32:
                return scr[:, off // 4: off // 4 + n]
            return scr[:, off // 4: off // 4 + n // 2].bitcast(BF16)

        ug = sv("ug"); ue = sv("ue").rearrange("p (i r) -> p i r", r=24)
        pa = sv("pa"); pb_ = sv("pb_")
        sa = sv("sa").rearrange("p (i r) -> p i r", r=24); sb_ = sv("sb_").rearrange("p (i r) -> p i r", r=24)
        tmp16 = sv("tmp16")
        pooled = sv("pooled").rearrange("p (g t) -> p g t", g=4)
        pool_out = sv("pool_out").rearrange("p (g t) -> p g t", g=4)
        npsb = sv("npsb").rearrange("p (g r) -> p g r", g=4)
        stg = sv("stg").rearrange("p (h c) -> p h c", h=2)
        nps_sb = stg
        npp_sb = sv("npp_sb")
        qf = sv("qf"); t1 = sv("t1"); t2 = sv("t2"); t3 = sv("t3"); t4 = sv("t4")
        qe = sv("qe"); qe2 = sv("qe2"); keb = sv("keb"); kdb = sv("kdb"); vb = sv("vb"); gbf = sv("gbf"); gbf2 = sv("gbf2")
        kdT = sv("kdT").rearrange("p (b k) -> p b k", b=5); vT = sv("vT").rearrange("p (b k) -> p b k", b=5)
        Sall = sv("Sall").rearrange("p (c v) -> p c v", c=9); Sbf = sv("Sbf").rearrange("p (c v) -> p c v", c=8)
        dec = sv("dec"); dec2 = sv("dec2"); Am = sv("Am").rearrange("p (b k) -> p b k", b=5); osq = sv("osq")
        o_fin = sv("o_fin").rearrange("p (h t) -> p h t", h=4)
        S0 = sv("S0").rearrange("p (i v) -> p i v", i=16); S0bf = sv("S0bf").rearrange("p (i v) -> p i v", i=16)
        Vm = sv("Vm").rearrange("p (i v) -> p i v", i=16)
        merged = sv("merged").rearrange("p (k t) -> p k t", k=8); s1 = sv("s1"); s2 = sv("s2")
        sgA = sv("sgA").rearrange("p (k t) -> p k t", k=8); sgB = sv("sgB").rearrange("p (k t) -> p k t", k=8)
        hidden = sv("hidden").rearrange("p (f t) -> p f t", f=22); ftmp = sv("ftmp")
        assert scr_plan["S0bf"][0] == scr_plan["S0"][0] + 8192 and scr_plan["Vm"][0] == scr_plan["S0"][0] + 12288
        assert scr_plan["S0"][0] >= scr_plan[("off", "ffn")]
        xo = scr_plan["S0"][0] // 4
        xalt = scr[:, xo:xo + 4096].rearrange("p (b d) -> p b d", b=4)
        sgt = [sv("sgt0"), sv("sgt1")]
        pT = sv("pT").rearrange("p (k t) -> p k t", k=2); pstage = sv("pstage").rearrange("p (b c) -> p b c", b=5); pbf = sv("pbf")

        pbs = [st.enter_context(nc.psum_tensor(f"pb{i}", [128, 512], F32)) for i in range(8)]
        bank_ctr = [0]

        def nb():
            b = bank_ctr[0] % 8
            bank_ctr[0] += 1
            return b

        def PB(b):
            return ("pb", b)

        wstate = {"tile": 0, "issued": set(), "released": set(), "total": NBLK * ntiles}

        def w_issue_n(n, extra_reads=()):
            if n >= wstate["total"] or n in wstate["issued"]:
                return
            slot = n % NRING
            blk = n % NBLK
            P.dma("pool", f"ring{slot}",
                  lambda e, slot=slot, blk=blk: e.dma_start(out=ring[slot][:], in_=wall[blk]),
                  reads=list(extra_reads), writes=[("ring", slot)])
            wstate["issued"].add(n)

        def w_issue(extra_reads=()):
            w_issue_n(len(wstate["issued"]), extra_reads)

        def w_get(expect):
            n = wstate["tile"] * NBLK + expect
            assert n in wstate["issued"], ("ring too small / block not prefetched", n)
            assert n - NRING < 0 or (n - NRING) in wstate["released"]
            slot = n % NRING
            return ring[slot], ("ring", slot)

        def w_release(expect):
            n = wstate["tile"] * NBLK + expect
            assert n in wstate["issued"] and n not in wstate["released"]
            wstate["released"].add(n)
            w_issue_n(n + NRING)

        def A(eng, fn, reads, writes):
            return P.op(eng, fn, reads=reads, writes=writes)

        def mm(out, lhsT, rhs, start, stop, reads, bank):
            A("pe", lambda e: e.matmul(out, lhsT=lhsT, rhs=rhs, start=start, stop=stop), reads, [PB(bank)])

        def tp(out, in_, ident, reads, bank):
            A("pe", lambda e: e.transpose(out=out, in_=in_, identity=ident), reads, [PB(bank)])

        def act(out, in_, func, reads, writes, **kw):
            A("act", lambda e: e.activation(out=out, in_=in_, func=func, **kw), reads, writes)

        def tt(out, in0, in1, op, reads, writes, eng="dve"):
            A(eng, lambda e: e.tensor_tensor(out=out, in0=in0, in1=in1, op=op), reads, writes)

        def ts(out, in0, s1_, s2_, op0, op1, reads, writes):
            A("dve", lambda e: e.tensor_scalar(out=out, in0=in0, scalar1=s1_, scalar2=s2_, op0=op0, op1=op1), reads, writes)

        def stt(out, in0, scalar, in1, op0, op1, reads, writes):
            A("dve", lambda e: e.scalar_tensor_tensor(out=out, in0=in0, scalar=scalar, in1=in1, op0=op0, op1=op1), reads, writes)

        def cp(out, in_, reads, writes, eng="dve"):
            if eng == "act":
                act(out, in_, AF.Copy, reads, writes)
            else:
                A(eng, lambda e: e.tensor_copy(out=out, in_=in_), reads, writes)

        def warm(func):
            act(dmy[:, 1:2], dmy[:, 0:1], func, ["dmy0"], ["dmy1"])

        A("dve", lambda e: e.memset(dmy[:], 1.0), [], ["dmy0", "dmy1"])
        P.dma("sp", "d_cs", lambda e: e.dma_start(out=cs[:], in_=cst[:, 0:CW_KEEP]), writes=["cs"])
        cstage = scr[:, 0:CW - CW_KEEP]
        P.dma("sp", "d_cs2", lambda e: e.dma_start(out=cstage, in_=cst[:, CW_KEEP:CW]), writes=["cstage"])
        maskPf = cstage[:, 0:128]
        maskSf = cstage[:, 128:256]
        resetmf = cstage[:, 256:896]
        P.dma("sp", "d_small", lambda e: e.dma_start(out=small[:], in_=smalld), writes=["small"])
        P.dma("sp", "d_pmix", lambda e: e.dma_start(out=pmixf[:], in_=pmixd), writes=["pmixf"])
        for gi in range(4):
            P.dma("sp", f"d_g{gi}",
                  lambda e, gi=gi: e.dma_start(out=gB[:, gi, :], in_=gvec[gi:gi + 1, :].broadcast_to([128, 1024])),
                  writes=[("gB", gi)])
        cp(identb[:], identf, ["cs"], ["identb"])
        cp(maskPb[:], maskPf, ["cstage"], ["maskPb"])
        cp(maskSb[:], maskSf, ["cstage"], ["maskSb"])
        cp(resetmb[:], resetmf, ["cstage"], ["resetmb"])
        cp(pmix[:].rearrange("p g c -> p (g c)"), pmixf[:], ["pmixf"], ["pmix"])
        A("dve", lambda e: e.memset(onesd[:], 1.0 / 128.0), [], ["onesd"])
        A("dve", lambda e: e.memset(carry[:].rearrange("p h v -> p (h v)"), 0.0), [], ["carry"])
        A("dve", lambda e: e.memset(ucarry[:].rearrange("p g t -> p (g t)"), 0.0), [], ["ucarry"])
        tt(lbv[:, 0:4], small[:, 8:12], small[:, 12:16], ALU.subtract, ["small"], ["lbv"])
        act(lbv[:, 0:4], lbv[:, 0:4], AF.Sigmoid, ["lbv"], ["lbv"])
        ts(lbv[:, 4:8], lbv[:, 0:4], -1.0, 1.0, ALU.mult, ALU.add, ["lbv"], ["lbv"])

        def run_tile(ti):
            wstate["tile"] = ti
            has_s = ti == 0
            last = ti == 3
            NT = 640 if has_s else 512
            blocks = [0, 1, 2, 3] + ([4] if has_s else [])
            subs = [(0, 512, "p")] + ([(512, 128, "s")] if has_s else [])
            hTk_p = [("hT", b) for b in range(4)]
            hTk = {"p": hTk_p, "s": [("hT", 4)]}

            def K(name):
                return [(name, "p")] + ([(name, "s")] if has_s else [])

            def xbuf(tj):
                return (xres, "xres") if tj % 2 == 0 else (xalt, "xalt")

            X, xk = xbuf(ti)

            def load_x(tj, b):
                src = xp[tj * 512 + b * 128: tj * 512 + (b + 1) * 128, :] if b < 4 else xs
                Xn, xkn = xbuf(tj)
                P.dma("sp", f"d_x{tj % 2}_{b}", lambda e, b=b, src=src, Xn=Xn: e.dma_start(out=Xn[:, b, :], in_=src),
                      writes=[(xkn, b)])

            if ti == 0:
                for b in blocks:
                    load_x(0, b)
            if ti == 0:
                for _ in range(NRING):
                    w_issue(extra_reads=[(xk, b) for b in blocks])
            if has_s:
                for half in range(2):
                    P.dma("sp", f"d_stg{half}",
                          lambda e, half=half: e.dma_start(out=stg[0:120, half, :], in_=spool[half * 120:(half + 1) * 120, :]),
                          writes=[("stg", half)])

            def half_stats(b, half, X_=None, xk_=None, ssq_=None, tag="ssq"):
                X_ = X if X_ is None else X_
                xk_ = xk if xk_ is None else xk_
                ssq_ = ssq if ssq_ is None else ssq_
                act(junk[:, 0:512], X_[:, b, half * 512:(half + 1) * 512], AF.Square, [(xk_, b)], ["junk", (tag, b, half)],
                    accum_out=ssq_[:, 2 * b + half:2 * b + half + 1])

            def norm_stats(have_partials, blks=None, X_=None, xk_=None, ssq_=None, rst_=None, tag="ssq", rtag="rst"):
                blks = blocks if blks is None else blks
                ssq_ = ssq if ssq_ is None else ssq_
                rst_ = rst if rst_ is None else rst_
                if not have_partials:
                    for b in blks:
                        for half in range(2):
                            half_stats(b, half, X_, xk_, ssq_, tag)
                nbk = len(blks)
                tt(rst_[:, 0:nbk], ssq_[:, 0:2 * nbk:2], ssq_[:, 1:2 * nbk:2], ALU.add,
                   [(tag, b, hf) for b in blks for hf in range(2)], [rtag])
                act(rst_[:, 0:nbk], rst_[:, 0:nbk], AF.Ln, [rtag], [rtag], scale=1.0 / 1024.0, bias=EPS)
                act(rst_[:, 0:nbk], rst_[:, 0:nbk], AF.Exp, [rtag], [rtag], scale=-0.5)

            def norm_apply(gi, blks=None, X_=None, xk_=None, rst_=None, rtag="rst"):
                blks = blocks if blks is None else blks
                X_ = X if X_ is None else X_
                xk_ = xk if xk_ is None else xk_
                rst_ = rst if rst_ is None else rst_
                for b in blks:
                    hb = hn[b % 2]
                    hk = f"hn{b % 2}"
                    stt(hb[:], X_[:, b, :], rst_[:, b:b + 1], gB[:, gi, :], ALU.mult, ALU.mult,
                        [(xk_, b), rtag, ("gB", gi)], [hk])
                    bank = nb()
                    bv = pbs[bank][:].bitcast(BF16)
                    for k in range(8):
                        tp(bv[:, k * 128:(k + 1) * 128], hb[:, k * 128:(k + 1) * 128], identb[:], [hk, "identb"], bank)
                    c0 = b * 128
                    cp(hT[:, :, c0:c0 + 128], bv.rearrange("p (k t) -> p k t", k=8), [PB(bank)], [("hT", b)], eng="act")

            def norm_T(gi, have_partials):
                norm_stats(have_partials)
                norm_apply(gi)

            ck("load")
            if ti == 0:
                norm_T(0, False)
            ck("norm1")

            wv, wk = w_get(0)
            wu = wv[:].rearrange("p (k c) -> p k c", k=8)
            ubanks = []
            for g in range(4):
                bp = nb()
                bs = nb() if has_s else None
                for k in range(8):
                    mm(pbs[bp][:, 0:512], wu[:, k, g * 128:(g + 1) * 128], hT[:, k, 0:512], k == 0, k == 7, [wk] + hTk_p, bp)
                    if has_s:
                        mm(pbs[bs][:, 0:128], wu[:, k, g * 128:(g + 1) * 128], hT[:, k, 512:640], k == 0, k == 7, [wk, ("hT", 4)], bs)
                ubanks.append((bp, bs))
                if g == 3:
                    w_release(0)
                w = 2 << g
                cp(ug[:, 0:16], ucarry[:, g, :], ["ucarry"], ["ug"])
                cp(ug[:, 16:528], pbs[bp][:, 0:512], [PB(bp)], ["ug"], eng="act")
                tt(pa[:, 1:528], ug[:, 1:528], ug[:, 0:527], ALU.add, ["ug"], ["pa"])
                sw = pa
                swk = "pa"
                if w >= 4:
                    tt(pb_[:, 3:528], pa[:, 3:528], pa[:, 1:526], ALU.add, ["pa"], ["pb_"])
                    sw, swk = pb_, "pb_"
                if w >= 8:
                    tt(pa[:, 7:528], pb_[:, 7:528], pb_[:, 3:524], ALU.add, ["pb_"], ["pa"])
                    sw, swk = pa, "pa"
                if w >= 16:
                    tt(pb_[:, 15:528], pa[:, 15:528], pa[:, 7:520], ALU.add, ["pa"], ["pb_"])
                    sw, swk = pb_, "pb_"
                stt(pooled[:, g, 0:512], sw[:, 16:528], 1.0 / w, ug[:, 16:528], ALU.mult, ALU.subtract,
                    [swk, "ug"], [("pooled", g, "p")])
                if ti == 0:
                    tt(tmp16[:], sw[:, 16:32], rcv[:, g * 16:(g + 1) * 16], ALU.mult, [swk, "cs"], ["tmp16"])
                    tt(pooled[:, g, 0:16], tmp16[:], ug[:, 16:32], ALU.subtract, ["tmp16", "ug"], [("pooled", g, "p")])
                cp(ucarry[:, g, :], ug[:, 512:528], ["ug"], ["ucarry"])
                if last:
                    bt = nb()
                    tp(pbs[bt][0:15, 0:128], ug[:, 513:528], identf, ["ug", "cs"], bt)
                    cp(npp_sb[0:15, g * 128:(g + 1) * 128], pbs[bt][0:15, 0:128], [PB(bt)], ["npp_sb"], eng="act")
                if has_s:
                    bt = nb()
                    for half in range(2):
                        tp(pbs[bt][:, half * 120:(half + 1) * 120], stg[0:120, half, g * 128:(g + 1) * 128],
                           identf[0:120, 0:120], [("stg", half), "cs"], bt)
                    cp(ue[:, :, 0:15], pbs[bt][:, 0:240].rearrange("p (i r) -> p i r", r=15), [PB(bt)], ["ue"], eng="act")
                    cp(ue[:, :, 15:23], pbs[bs][:, 0:128].rearrange("p (i t) -> p i t", t=8), [PB(bs)], ["ue"], eng="act")
                    tt(sa[:, :, 1:23], ue[:, :, 1:23], ue[:, :, 0:22], ALU.add, ["ue"], ["sa"])
                    ssw, sswk = sa, "sa"
                    if w >= 4:
                        tt(sb_[:, :, 3:23], sa[:, :, 3:23], sa[:, :, 1:21], ALU.add, ["sa"], ["sb_"])
                        ssw, sswk = sb_, "sb_"
                    if w >= 8:
                        tt(sa[:, :, 7:23], sb_[:, :, 7:23], sb_[:, :, 3:19], ALU.add, ["sb_"], ["sa"])
                        ssw, sswk = sa, "sa"
                    if w >= 16:
                        tt(sb_[:, :, 15:23], sa[:, :, 15:23], sa[:, :, 7:15], ALU.add, ["sa"], ["sb_"])
                        ssw, sswk = sb_, "sb_"
                    stt(pooled[:, g, 512:640].rearrange("p (i t) -> p i t", t=8), ssw[:, :, 15:23], 1.0 / w,
                        ue[:, :, 15:23], ALU.mult, ALU.subtract, [sswk, "ue"], [("pooled", g, "s")])
                    cp(npsb[:, g, :].rearrange("p (i r) -> p i r", r=15), ue[:, :, 8:23], ["ue"], [("npsb", g)])
            if last:
                P.dma("sp", "d_npp", lambda e: e.dma_start(out=npp, in_=npp_sb[0:15, :]), reads=["npp_sb"])
            if has_s:
                for half in range(2):
                    bt = nb()
                    for g in range(4):
                        tp(pbs[bt][0:120, g * 128:(g + 1) * 128], npsb[:, g, half * 120:(half + 1) * 120], identf,
                           [("npsb", g), "cs"], bt)
                    cp(nps_sb[0:120, half, :], pbs[bt][0:120, 0:512], [PB(bt)], [("stg", half)], eng="act")
                    P.dma("sp", f"d_nps{half}",
                          lambda e, half=half: e.dma_start(out=nps[half * 120:(half + 1) * 120, :], in_=nps_sb[0:120, half, :]),
                          reads=[("stg", half)])
            ck("u")
            PIPE = not has_s
            if PIPE:
                zc, rcn = [0], [0]

                def nbZ():
                    b = zc[0] % 4
                    zc[0] += 1
                    return b

                def nbR():
                    b = 6 + rcn[0] % 2
                    rcn[0] += 1
                    return b

                gcn = [0]

                def nbG():
                    b = 4 + gcn[0] % 2
                    gcn[0] += 1
                    return b
            else:
                nbZ = nbR = nbG = nb
            gb2 = [gbf, gbf2]
            QE = [qe, qe2]
            DEC = [dec, dec2]
            HS = {h: {} for h in range(4)}

            def poolmix():
                for g in range(4):
                    for (c0, n, sk) in subs:
                        bk = nbR()
                        mm(pbs[bk][:, 0:n], pmix[:, g, :], pooled[:, g, c0:c0 + n], True, True, ["pmix", ("pooled", g, sk)], bk)
                        act(pool_out[:, g, c0:c0 + n], pbs[bk][:, 0:n], AF.Copy, [PB(bk), "small"], [("pool_out", g, sk)],
                            scale=small[:, g:g + 1])

            def z_group(h, j):
                S_ = HS[h]
                if j == 0:
                    wv, wk = w_get(1 + 2 * h)
                    S_["wh"] = wv[:].rearrange("p (k j c) -> p k j c", k=8, j=4)
                    S_["wk"] = wk
                    S_["bs"] = nbZ() if has_s else None
                    S_["zb"] = []
                wh, wk, bs = S_["wh"], S_["wk"], S_["bs"]
                bp = nbZ()
                for k in range(8):
                    mm(pbs[bp][:, 0:512], wh[:, k, j, :], hT[:, k, 0:512], k == 0, k == 7, [wk] + hTk_p, bp)
                    if has_s:
                        mm(pbs[bs][:, j * 128:(j + 1) * 128], wh[:, k, j, :], hT[:, k, 512:640], k == 0, k == 7,
                           [wk, ("hT", 4)], bs)
                S_["zb"].append(bp)
                if j == 3:
                    w_release(1 + 2 * h)

            def z_evac(h):
                S_ = HS[h]
                zb, bs = S_["zb"], S_["bs"]
                gbh = gb2[h % 2]
                gbk = f"gbf{h % 2}"

                def zsrc(j, sk):
                    return (pbs[zb[j]][:, 0:512], PB(zb[j])) if sk == "p" else (pbs[bs][:, j * 128:(j + 1) * 128], PB(bs))

                for (c0, n, sk) in subs:
                    src, key = zsrc(1, sk)
                    act(t1[:, c0:c0 + n], src, AF.Sigmoid, [key], [("t1", sk)])
                    src, key = zsrc(0, sk)
                    act(qf[:, c0:c0 + n], src, AF.Sigmoid, [key], [("qf", sk)])
                    tt(qf[:, c0:c0 + n], qf[:, c0:c0 + n], src, ALU.mult, [("qf", sk), key], [("qf", sk)])
                    src, key = zsrc(3, sk)
                    act(t4[:, c0:c0 + n], src, AF.Sigmoid, [key], [("t4", sk)])
                    tt(gbh[:, c0:c0 + n], t4[:, c0:c0 + n], src, ALU.mult, [("t4", sk), key], [(gbk, sk)])
                    src, key = zsrc(2, sk)
                    cp(vb[:, c0:c0 + n], src, [key], [("vb", sk)])
                warm(AF.Ln)

            def g_mm(h, jj):
                S_ = HS[h]
                if jj == 0:
                    wgv_, wgk_ = w_get(2 + 2 * h)
                    S_["wgv"] = wgv_[:].rearrange("p (k t c) -> p k t c", k=8, t=4)
                    S_["wgk"] = wgk_
                    S_["gate_ev"] = []
                wgv, wgk_ = S_["wgv"], S_["wgk"]
                j = 2 * h + jj
                bsg = nbG() if has_s else None
                for t_ in range(2):
                    bpg = nbG()
                    for k in range(8):
                        mm(pbs[bpg][:, 0:512], wgv[:, k, 2 * jj + t_, :], hT[:, k, 0:512], k == 0, k == 7, [wgk_] + hTk_p, bpg)
                        if has_s:
                            mm(pbs[bsg][:, t_ * 128:(t_ + 1) * 128], wgv[:, k, 2 * jj + t_, :], hT[:, k, 512:640], k == 0, k == 7,
                               [wgk_, ("hT", 4)], bsg)
                    S_["gate_ev"].append((j, t_, bpg, bsg))
                if jj == 1:
                    w_release(2 + 2 * h)

            def g_evac(h, jj):
                for (j, t_, bpg, bsg) in HS[h]["gate_ev"][2 * jj:2 * jj + 2]:
                    dst = sgA if t_ == 0 else sgB
                    dk = "sgA" if t_ == 0 else "sgB"
                    cp(dst[:, j, 0:512], pbs[bpg][:, 0:512], [PB(bpg)], [(dk, j, "p")], eng="act")
                    if has_s:
                        cp(dst[:, j, 512:640], pbs[bsg][:, t_ * 128:(t_ + 1) * 128], [PB(bsg)], [(dk, j, "s")], eng="act")

            def chain_a(h):
                ts(t1[:, 0:NT], t1[:, 0:NT], lbv[:, 4 + h:5 + h], lbv[:, h:h + 1], ALU.mult, ALU.add, K("t1") + ["lbv"], K("t1"))
                ts(t2[:, 0:NT], t1[:, 0:NT], -1.0, 1.0, ALU.mult, ALU.add, K("t1"), K("t2"))
                act(t1[:, 0:NT], t1[:, 0:NT], AF.Ln, K("t1"), K("t1"))

            def cb_scan(h):
                A("dve", lambda e: e.tensor_tensor_scan(out=t3[:, 0:NT], data0=resetm[:, 0:NT], data1=t1[:, 0:NT],
                                                        initial=0.0, op0=ALU.mult, op1=ALU.add),
                  K("t1") + ["resetmb"], K("t3"))

            def cb_exp(h):
                act(t1[:, 0:NT], t3[:, 0:NT], AF.Exp, K("t3"), K("t1"))
                act(t4[:, 0:NT], t3[:, 0:NT], AF.Exp, K("t3"), K("t4"), scale=-1.0)

            def cb_mul(h):
                qe_ = QE[h % 2]
                qk = f"qe{h % 2}"
                tt(qe_[:, 0:NT], qf[:, 0:NT], t1[:, 0:NT], ALU.mult, K("qf") + K("t1"), [(qk, "p")] + ([(qk, "s")] if has_s else []))
                tt(t2[:, 0:NT], t2[:, 0:NT], t4[:, 0:NT], ALU.mult, K("t2") + K("t4"), K("t2"))

            def cb_keb(h):
                cp(keb[:, 0:NT], t2[:, 0:NT], K("t2"), K("keb"), eng="act")

            def cb_dec(h):
                dec_ = DEC[h % 2]
                dk_ = f"dec{h % 2}"
                cp(dec_[:, 0:8], t1[:, 63:512:64], [("t1", "p")], [(dk_, "p")])
                tt(kdb[:, 0:512].rearrange("p (c t) -> p c t", t=64), t2[:, 0:512].rearrange("p (c t) -> p c t", t=64),
                   dec_[:, 0:8].unsqueeze(2).broadcast_to([128, 8, 64]), ALU.mult, [("t2", "p"), (dk_, "p")], [("kdb", "p")])
                if has_s:
                    cp(dec_[:, 8:24], t1[:, 519:640:8], [("t1", "s")], [(dk_, "s")])
                    tt(kdb[:, 512:640].rearrange("p (c t) -> p c t", t=8), t2[:, 512:640].rearrange("p (c t) -> p c t", t=8),
                       dec_[:, 8:24].unsqueeze(2).broadcast_to([128, 16, 8]), ALU.mult, [("t2", "s"), (dk_, "s")], [("kdb", "s")])

            def chain_b(h):
                cb_scan(h); cb_exp(h); cb_mul(h); cb_keb(h); cb_dec(h)

            def R1(h):
                qe = QE[h % 2]
                qk = f"qe{h % 2}"
                for (src_, srck, dst, dstk) in ((kdb, "kdb", kdT, "kdT"), (vb, "vb", vT, "vT")):
                    bt = nbR()
                    bv = pbs[bt][:].bitcast(BF16)
                    for b in blocks:
                        sk = "p" if b < 4 else "s"
                        tp(bv[:, b * 128:(b + 1) * 128], src_[:, b * 128:(b + 1) * 128], identb[:], [(srck, sk), "identb"], bt)
                    nbk = len(blocks)
                    cp(dst[:, 0:nbk, :], bv[:, 0:nbk * 128].rearrange("p (b k) -> p b k", k=128), [PB(bt)], [dstk], eng="act")
                ba = nbR()
                for b in range(4):
                    mm(pbs[ba][:, b * 128:(b + 1) * 128], keb[:, b * 128:(b + 1) * 128], qe[:, b * 128:(b + 1) * 128],
                       True, True, [("keb", "p"), (qk, "p")], ba)
                tt(Am[:, 0:4, :], pbs[ba][:, 0:512].rearrange("p (b t) -> p b t", b=4),
                   maskPb[:].unsqueeze(1).broadcast_to([128, 4, 128]), ALU.mult, [PB(ba), "maskPb"], [("Am", "p")])
                if has_s:
                    bas = nbR()
                    mm(pbs[bas][:, 0:128], keb[:, 512:640], qe[:, 512:640], True, True, [("keb", "s"), (qk, "s")], bas)
                    tt(Am[:, 4, :], pbs[bas][:, 0:128], maskSb[:], ALU.mult, [PB(bas), "maskSb"], [("Am", "s")])

            def build_vm():
                tt(Vm[:, :, :], vT[:, 4, :].unsqueeze(1).broadcast_to([128, 16, 128]),
                   seqm.unsqueeze(2).broadcast_to([128, 16, 128]), ALU.mult, ["vT", "cs"], ["Vm"])

            def sample_prefetch(h):
                P.dma("sp", "d_S0", lambda e, h=h: e.dma_start(out=S0[:, :, :], in_=shg[:, h, :, :].rearrange("i k v -> k i v")),
                      writes=["S0"])

            def sample_bf(h):
                cp(S0bf[:, :, :].rearrange("p i v -> p (i v)"), S0[:, :, :].rearrange("p i v -> p (i v)"), ["S0"], ["S0bf"], eng="act")

            def sample_state_update(h):
                dec = DEC[h % 2]
                dk_ = f"dec{h % 2}"
                tt(S0[:, :, :], S0[:, :, :], dec[:, 8:24].unsqueeze(2).broadcast_to([128, 16, 128]), ALU.mult,
                   ["S0", (dk_, "s")], ["S0"])
                for q4 in range(4):
                    bd = nbR()
                    mm(pbs[bd][:, 0:512], kdT[:, 4, :], Vm[:, 4 * q4:4 * q4 + 4, :].rearrange("p i v -> p (i v)"), True, True,
                       ["kdT", "Vm"], bd)
                    tt(S0[:, 4 * q4:4 * q4 + 4, :].rearrange("p i v -> p (i v)"),
                       S0[:, 4 * q4:4 * q4 + 4, :].rearrange("p i v -> p (i v)"), pbs[bd][:, 0:512], ALU.add,
                       ["S0", PB(bd)], ["S0"])
                P.dma("sp", "d_nhs", lambda e, h=h: e.dma_start(out=nhs[:, h, :, :].rearrange("i k v -> k i v"), in_=S0[:, :, :]),
                      reads=["S0"])

            def R2_mm(h):
                dsb = [nbR(), nbR()]
                HS[h]["dsb"] = dsb
                for c in range(8):
                    blk, half = c // 2, c % 2
                    bk = dsb[half]
                    mm(pbs[bk][:, blk * 128:(blk + 1) * 128], kdT[half * 64:(half + 1) * 64, blk, :],
                       vT[half * 64:(half + 1) * 64, blk, :], True, True, ["kdT", "vT"], bk)
                cp(Sall[:, 0, :], carry[:, h, :], ["carry"], [("Sall", 0)])

            def R2_steps(h, c0, c1):
                dec_ = DEC[h % 2]
                dk_ = f"dec{h % 2}"
                dsb = HS[h]["dsb"]
                for c in range(c0, c1):
                    blk, half = c // 2, c % 2
                    bk = dsb[half]
                    stt(Sall[:, c + 1, :], Sall[:, c, :], dec_[:, c:c + 1], pbs[bk][:, blk * 128:(blk + 1) * 128],
                        ALU.mult, ALU.add, [("Sall", c), (dk_, "p"), PB(bk)], [("Sall", c + 1)])

            def R2_fin(h):
                cp(Sbf[:, :, :].rearrange("p c v -> p (c v)"), Sall[:, 0:8, :].rearrange("p c v -> p (c v)"),
                   [("Sall", c) for c in range(8)], ["Sbf"], eng="act")
                cp(carry[:, h, :], Sall[:, 8, :], [("Sall", 8)], ["carry"])
                if last:
                    P.dma("sp", f"d_nhp{h}", lambda e, h=h: e.dma_start(out=nhp[h], in_=Sall[:, 8, :]), reads=[("Sall", 8)])

            def R2(h):
                R2_mm(h); R2_steps(h, 0, 8); R2_fin(h)

            def R3(h):
                qe = QE[h % 2]
                qk = f"qe{h % 2}"
                S_ = HS[h]
                bo = nbR()
                for b in range(4):
                    mm(pbs[bo][:, b * 128:(b + 1) * 128], vT[:, b, :], Am[:, b, :], True, False, ["vT", ("Am", "p")], bo)
                    for half in range(2):
                        c = 2 * b + half
                        mm(pbs[bo][:, c * 64:(c + 1) * 64], Sbf[:, c, :], qe[:, c * 64:(c + 1) * 64], False, half == 1,
                           ["Sbf", (qk, "p")], bo)
                bos = None
                if has_s:
                    bos = nbR()
                    mm(pbs[bos][:, 0:128], vT[:, 4, :], Am[:, 4, :], True, False, ["vT", ("Am", "s")], bos)
                    for i in range(16):
                        mm(pbs[bos][:, i * 8:(i + 1) * 8], S0bf[:, i, :], qe[:, 512 + i * 8:512 + (i + 1) * 8], False, i == 15,
                           ["S0bf", (qk, "s")], bos)
                S_["obanks"] = [(0, 512, "p", bo)] + ([(512, 128, "s", bos)] if has_s else [])
                for (c0, n, sk, bk) in S_["obanks"]:
                    act(osq[:, c0:c0 + n], pbs[bk][:, 0:n], AF.Square, [PB(bk)], [("osq", sk)])

            def R4(h):
                gbh = gb2[h % 2]
                gbk = f"gbf{h % 2}"
                for (c0, n, sk, bk) in HS[h]["obanks"]:
                    bn = nbR()
                    mm(pbs[bn][:, 0:n], onesd[:], osq[:, c0:c0 + n], True, True, ["onesd", ("osq", sk)], bn)
                    act(s1[:, c0:c0 + n], pbs[bn][:, 0:n], AF.Ln, [PB(bn)], [("s1", sk)], bias=EPS)
                    act(s1[:, c0:c0 + n], s1[:, c0:c0 + n], AF.Exp, [("s1", sk)], [("s1", sk)], scale=-0.5)
                    stt(s2[:, c0:c0 + n], pbs[bk][:, 0:n], small[:, 4 + h:5 + h], s1[:, c0:c0 + n], ALU.mult, ALU.mult,
                        [PB(bk), "small", ("s1", sk)], [("s2", sk)])
                tt(o_fin[:, h, 0:NT], s2[:, 0:NT], gbh[:, 0:NT], ALU.mult, K("s2") + [(gbk, "p"), (gbk, "s")],
                   [("o_fin", h, s_) for s_ in ("p", "s")])

            if PIPE:
                for j in range(4):
                    z_group(0, j)
            poolmix()
            ck("poolmix")
            P.barrier()
            ck("bar")
            if PIPE:
                for h in range(4):
                    p = h - 1
                    if h > 0:
                        R1(p)
                    z_evac(h)
                    g_mm(h, 0)
                    if h > 0:
                        R2_mm(p)
                    chain_a(h)
                    if h > 0:
                        R2_steps(p, 0, 4)
                    g_evac(h, 0)
                    if h < 3:
                        for j in range(4):
                            z_group(h + 1, j)
                    cb_scan(h)
                    if h > 0:
                        R2_steps(p, 4, 7)
                    cb_exp(h)
                    cb_mul(h)
                    cb_dec(h)
                    cb_keb(h)
                    g_mm(h, 1)
                    if h > 0:
                        R2_steps(p, 7, 8)
                        R2_fin(p)
                        R3(p)
                    g_evac(h, 1)
                    if h > 0:
                        R4(p)
                    warm(AF.Sigmoid)
                R1(3); R2(3); R3(3); R4(3)
                warm(AF.Sigmoid)
            else:
                for h in range(4):
                    sample_prefetch(h)
                    for j in range(4):
                        z_group(h, j)
                    z_evac(h)
                    g_mm(h, 0)
                    g_mm(h, 1)
                    chain_a(h)
                    g_evac(h, 0)
                    chain_b(h)
                    R1(h)
                    g_evac(h, 1)
                    sample_bf(h)
                    R2_mm(h)
                    R2_steps(h, 0, 8)
                    build_vm()
                    R2_fin(h)
                    R3(h); R4(h)
                    warm(AF.Sigmoid)
                    sample_state_update(h)

            ck("hgrn")
            for jh in range(2):
                wy, wyk = w_get(9 + jh)
                wyv = wy[:].rearrange("p (k t c) -> p k t c", k=4, t=2)
                for jj in range(4):
                    j = jh * 4 + jj
                    bs = nb() if has_s else None
                    b_ya, b_yb = nb(), nb()
                    for t_, bk, srcb, srck in ((0, b_ya, pool_out, "pool_out"), (1, b_yb, o_fin, "o_fin")):
                        for kk in range(4):
                            for (c0, n, sk) in subs:
                                o = pbs[bk][:, 0:512] if sk == "p" else pbs[bs][:, t_ * 128:(t_ + 1) * 128]
                                mm(o, wyv[:, kk, t_, jj * 128:(jj + 1) * 128], srcb[:, kk, c0:c0 + n], kk == 0, kk == 3,
                                   [wyk, (srck, kk, sk)], bk if sk == "p" else bs)
                    if jj == 3:
                        w_release(9 + jh)
                    for (c0, n, sk) in subs:
                        oa = pbs[b_ya][:, 0:512] if sk == "p" else pbs[bs][:, 0:128]
                        ob = pbs[b_yb][:, 0:512] if sk == "p" else pbs[bs][:, 128:256]
                        ka = PB(b_ya) if sk == "p" else PB(bs)
                        kb_ = PB(b_yb) if sk == "p" else PB(bs)
                        act(s1[:, c0:c0 + n], sgA[:, j, c0:c0 + n], AF.Sigmoid, [("sgA", j, sk)], [("s1", sk)])
                        act(s2[:, c0:c0 + n], sgB[:, j, c0:c0 + n], AF.Sigmoid, [("sgB", j, sk)], [("s2", sk)])
                        tt(s1[:, c0:c0 + n], s1[:, c0:c0 + n], oa, ALU.mult, [("s1", sk), ka], [("s1", sk)])
                        tt(s2[:, c0:c0 + n], s2[:, c0:c0 + n], ob, ALU.mult, [("s2", sk), kb_], [("s2", sk)])
                    tt(merged[:, j, 0:NT], s1[:, 0:NT], s2[:, 0:NT], ALU.add, K("s1") + K("s2"),
                       [("merged", j, s_) for s_ in ("p", "s")])

            ck("merge")
            warm(AF.Ln)
            for half in range(2):
                wo, wok = w_get(11 + half)
                wov = wo[:].rearrange("p (k c) -> p k c", k=8)
                bks = {b: nb() for b in blocks}
                for k in range(8):
                    for b in blocks:
                        sk = "p" if b < 4 else "s"
                        mm(pbs[bks[b]][:, 0:512], merged[:, k, b * 128:(b + 1) * 128], wov[:, k, :], k == 0, k == 7,
                           [wok, ("merged", k, sk)], bks[b])
                w_release(11 + half)
                for b in blocks:
                    tt(X[:, b, half * 512:(half + 1) * 512], X[:, b, half * 512:(half + 1) * 512], pbs[bks[b]][:, 0:512],
                       ALU.add, [(xk, b), PB(bks[b])], [(xk, b)])
                    half_stats(b, half)

            ck("wout")
            norm_T(1, True)
            warm(AF.Sigmoid)
            P.barrier()
            if ti + 1 < ntiles:
                for b in range(4):
                    load_x(ti + 1, b)
            for b in blocks:
                src = ppd[ti * 512 + b * 128: ti * 512 + (b + 1) * 128, :] if b < 4 else psd
                P.dma("sp", f"d_p{b}", lambda e, src=src, b=b: e.dma_start(out=pstage[:, b, :], in_=src), writes=[("pstage", b)])
            for jb in range(11):
                wf, wfk = w_get(13 + jb)
                wfv = wf[:].rearrange("p (k j c) -> p k j c", k=8, j=2)
                for cc in range(2):
                    f = 2 * jb + cc
                    bs = nb() if has_s else None
                    b_g, b_u = nb(), nb()
                    for jx, bk in ((0, b_g), (1, b_u)):
                        for k in range(8):
                            for (c0, n, sk) in subs:
                                o = pbs[bk][:, 0:512] if sk == "p" else pbs[bs][:, jx * 128:(jx + 1) * 128]
                                mm(o, wfv[:, k, jx, cc * 128:(cc + 1) * 128], hT[:, k, c0:c0 + n], k == 0, k == 7,
                                   [wfk] + hTk[sk], bk if sk == "p" else bs)
                    if cc == 1:
                        w_release(13 + jb)
                    for (c0, n, sk) in subs:
                        og = pbs[b_g][:, 0:512] if sk == "p" else pbs[bs][:, 0:128]
                        ou = pbs[b_u][:, 0:512] if sk == "p" else pbs[bs][:, 128:256]
                        kg = PB(b_g) if sk == "p" else PB(bs)
                        ku = PB(b_u) if sk == "p" else PB(bs)
                        act(ftmp[:, c0:c0 + n], og, AF.Sigmoid, [kg], [("ftmp", sk)])
                        tt(ftmp[:, c0:c0 + n], ftmp[:, c0:c0 + n], og, ALU.mult, [("ftmp", sk), kg], [("ftmp", sk)])
                        tt(hidden[:, f, c0:c0 + n], ftmp[:, c0:c0 + n], ou, ALU.mult, [("ftmp", sk), ku], [("hidden", f, sk)])
            for b in blocks:
                cp(pbf[:], pstage[:, b, :], [("pstage", b)], ["pbf"], eng="act")
                bt = nb()
                bv = pbs[bt][:].bitcast(BF16)
                for k in range(2):
                    tp(bv[:, k * 128:(k + 1) * 128], pbf[:, k * 128:(k + 1) * 128], identb[:], ["pbf", "identb"], bt)
                cp(pT[:, :, b * 128:(b + 1) * 128], bv[:, 0:256].rearrange("p (k t) -> p k t", k=2), [PB(bt)], [("pT", b)])
            warm(AF.Ln)
            for half in range(2):
                bks = {b: nb() for b in blocks}
                for kb in range(3):
                    wd, wdk = w_get(24 + half * 3 + kb)
                    wdv = wd[:].rearrange("p (k c) -> p k c", k=8)
                    nk = 8 if kb < 2 else 6
                    for kk in range(nk):
                        f = kb * 8 + kk
                        for b in blocks:
                            sk = "p" if b < 4 else "s"
                            mm(pbs[bks[b]][:, 0:512], hidden[:, f, b * 128:(b + 1) * 128], wdv[:, kk, :], f == 0, f == 21,
                               [wdk, ("hidden", f, sk)], bks[b])
                    w_release(24 + half * 3 + kb)
                for b in blocks:
                    tt(X[:, b, half * 512:(half + 1) * 512], X[:, b, half * 512:(half + 1) * 512], pbs[bks[b]][:, 0:512],
                       ALU.add, [(xk, b), PB(bks[b])], [(xk, b)])
                    half_stats(b, half)

            ck("ffn")
            norm_T(2, True)
            if ti + 1 < ntiles:
                Xn, xkn = xbuf(ti + 1)
                norm_stats(False, blks=[0, 1, 2, 3], X_=Xn, xk_=xkn, ssq_=ssq1, rst_=rst1, tag="ssq1", rtag="rst1")
            warm(AF.Sigmoid)
            wg0, wg0k = w_get(30)
            wpp_, wppk = w_get(31)
            wg1, wg1k = w_get(32)
            wppv = wpp_[:, 0:2048].rearrange("p (k c) -> p k c", k=2)
            for half in range(2):
                wg_, wgk = (wg0, wg0k) if half == 0 else (wg1, wg1k)
                wgv = wg_[:].rearrange("p (k c) -> p k c", k=8)
                bks = {b: nb() for b in blocks}
                for k in range(8):
                    for b in blocks:
                        mm(pbs[bks[b]][:, 0:512], hT[:, k, b * 128:(b + 1) * 128], wgv[:, k, :], k == 0, k == 7,
                           [wgk, ("hT", b)], bks[b])
                if half == 0:
                    w_release(30)
                for b in blocks:
                    sg = sgt[b % 2]
                    sgk = f"sgt{b % 2}"
                    act(sg[:], pbs[bks[b]][:, 0:512], AF.Sigmoid, [PB(bks[b])], [sgk])
                    be = nb()
                    for k in range(2):
                        mm(pbs[be][:, 0:512], pT[:, k, b * 128:(b + 1) * 128], wppv[:, k, half * 512:(half + 1) * 512], k == 0, k == 1,
                           [wppk, ("pT", b)], be)
                    tt(sg[:], sg[:], pbs[be][:, 0:512], ALU.mult, [sgk, PB(be)], [sgk])
                    tt(X[:, b, half * 512:(half + 1) * 512], X[:, b, half * 512:(half + 1) * 512], sg[:], ALU.add,
                       [(xk, b), sgk], [(xk, b)])
                for b in blocks:
                    half_stats(b, half)
                if half == 1:
                    w_release(31); w_release(32)

            ck("ple")
            if ti + 1 < ntiles:
                Xn, xkn = xbuf(ti + 1)
                norm_apply(0, blks=[0, 1, 2, 3], X_=Xn, xk_=xkn, rst_=rst1, rtag="rst1")
            P.barrier()
            norm_stats(True)
            for b in blocks:
                dst = yp[ti * 512 + b * 128: ti * 512 + (b + 1) * 128, :] if b < 4 else ys
                for half in range(2):
                    q_ = (2 * b + half) % 4
                    yst = ystage[q_]
                    ysk = f"ystage{q_}"
                    hs_ = slice(half * 512, (half + 1) * 512)
                    stt(yst[:], X[:, b, hs_], rst[:, b:b + 1], gB[:, 3, hs_], ALU.mult, ALU.mult,
                        [(xk, b), "rst", ("gB", 3)], [ysk])
                    P.dma("sp", f"d_y{q_}", lambda e, yst=yst, dst=dst, hs_=hs_: e.dma_start(out=dst[:, hs_], in_=yst[:]),
                          reads=[ysk])

        P.barrier()
        try:
            ck("setup")
            for ti in range(ntiles):
                run_tile(ti)
        except _Stop:
            pass
        P.finish("sp")
        P.emit()
    return nc


_PROG = {}


def _prep_inputs(inp):
    f = lambda a: np.ascontiguousarray(np.asarray(a, dtype=np.float32))
    w_in = f(inp["w_in"][0])
    wallv = build_wall(w_in, f(inp["w_pool_up"][0]), f(inp["w_hgrn_up"][0]), f(inp["w_out"][0]),
                       f(inp["w_ffn_gate"][0]), f(inp["w_ffn_up"][0]), f(inp["w_ffn_down"][0]),
                       f(inp["w_ple_gate"][0]), f(inp["w_ple_proj"][0]))
    gvec = np.ascontiguousarray(np.stack([f(inp["g_mix"][0]), f(inp["g_ffn"][0]), f(inp["g_ple"][0]), f(inp["g_final"])], 0))
    small = np.zeros((128, 16), np.float32)
    small[:, 0:4] = f(inp["pool_scale"][0]).reshape(4, 128).T
    small[:, 4:8] = f(inp["hgrn_norm"][0]).reshape(4, 128).T
    small[:, 8:12] = f(inp["hgrn_lb"][0]).reshape(4, 128).T
    small[:, 12:16] = f(inp["hgrn_lb"][1]).reshape(4, 128).T
    pmix = np.ascontiguousarray(f(inp["w_pool_mix"][0]).transpose(1, 0, 2)).reshape(128, 512)
    xp = f(inp["x_prompt"]); xsm = f(inp["x_sample"])
    ppr = f(inp["p_prompt"][0]); psm = f(inp["p_sample"][0])
    spl = f(inp["state_pool"][0]); shg = f(inp["state_hgrn"][0])
    maps = []
    for c in range(NCORES):
        maps.append({
            "xp": xp[c], "xs": xsm[16 * c:16 * c + 16].reshape(128, 1024),
            "pp": ppr[c], "ps": psm[16 * c:16 * c + 16].reshape(128, 256),
            "spool": spl[16 * c:16 * c + 16].reshape(240, 512),
            "shg": shg[16 * c:16 * c + 16],
            "wall": wallv, "cst": CONST_ARR, "gvec": gvec, "small": small, "pmix": pmix,
        })
    return maps


def kernel(**inputs):
    if "nc" not in _PROG:
        _PROG["nc"] = build_program()
    nc = _PROG["nc"]
    maps = _prep_inputs(inputs)
    res = run_bass_kernel_spmd(nc, maps, core_ids=list(range(NCORES)))
    R = res.results
    y_p = np.stack([R[c]["yp"] for c in range(NCORES)], 0).astype(np.float32)
    y_s = np.concatenate([R[c]["ys"].reshape(16, 8, 1024) for c in range(NCORES)], 0).astype(np.float32)
    npp = np.stack([R[c]["npp"] for c in range(NCORES)], 0)[None].astype(np.float32)
    nhp = np.stack([R[c]["nhp"] for c in range(NCORES)], 0)[None].astype(np.float32)
    nps = np.concatenate([R[c]["nps"].reshape(16, 15, 512) for c in range(NCORES)], 0)[None].astype(np.float32)
    nhs = np.concatenate([R[c]["nhs"] for c in range(NCORES)], 0)[None].astype(np.float32)
    return (y_p, y_s, npp, nhp, nps, nhs)
```

```python
import numpy as np
from contextlib import ExitStack
import concourse.bass as bass
import concourse.mybir as mybir
from concourse.bass_utils import run_bass_kernel_spmd

F32 = mybir.dt.float32
BF16 = mybir.dt.bfloat16
AF = mybir.ActivationFunctionType
ALU = mybir.AluOpType

ENGS = ("pe", "act", "dve", "pool", "sp")
EPS = 1e-6
NCORES = 8
NRING = 5
NTILES = 4
import os as _os
SAME_DIST = int(_os.environ.get("SAME_DIST", str(1 << 30)))


class Prog:
    def __init__(self, nc, stack, same_engine_wait=True):
        self.nc = nc
        self.stack = stack
        self.streams = {e: [] for e in ENGS}
        self.count = {e: 0 for e in ENGS}
        self.known = {e: {} for e in ENGS}
        self.sems = {}
        self.dma_count = {}
        self.lastw = {}
        self.readers = {}
        self.same_engine_wait = same_engine_wait

    def _collect(self, eng, reads, writes):
        deps = []
        for k in reads:
            ev = self.lastw.get(k)
            if ev is not None:
                deps.append(ev)
        for k in writes:
            ev = self.lastw.get(k)
            if ev is not None:
                deps.append(ev)
            rd = self.readers.get(k)
            if rd:
                deps.extend(rd.values())
        kn = self.known[eng]
        need = {}
        for (s, v, vc) in deps:
            if s == eng and (eng == "pe" or not self.same_engine_wait or self.count[eng] - v >= SAME_DIST):
                continue
            if kn.get(s, 0) >= v:
                continue
            if need.get(s, 0) < v:
                need[s] = v
            for s2, v2 in vc.items():
                if s2 == eng:
                    continue
                if kn.get(s2, 0) < v2:
                    kn[s2] = v2
        waits = []
        for s, v in need.items():
            waits.append((s, v))
            if kn.get(s, 0) < v:
                kn[s] = v
        return waits

    def _record(self, ev, reads, writes):
        s = ev[0]
        for k in reads:
            self.readers.setdefault(k, {})[s] = ev
        for k in writes:
            self.lastw[k] = ev
            self.readers[k] = {}

    def op(self, eng, fn, reads=(), writes=()):
        waits = self._collect(eng, reads, writes)
        self.count[eng] += 1
        n = self.count[eng]
        vc = dict(self.known[eng])
        vc[eng] = n
        ev = (eng, n, vc)
        self.streams[eng].append((waits, fn, (eng, 1)))
        self._record(ev, reads, writes)
        return ev

    def dma(self, q, semname, fn, reads=(), writes=()):
        waits = self._collect(q, reads, writes)
        self.dma_count[semname] = self.dma_count.get(semname, 0) + 1
        v = 16 * self.dma_count[semname]
        vc = dict(self.known[q])
        vc[semname] = v
        ev = (semname, v, vc)
        self.streams[q].append((waits, fn, (semname, 16)))
        self._record(ev, reads, writes)
        return ev

    def barrier(self, engines=("act", "dve", "sp")):
        for e in engines:
            kn = self.known[e]
            waits = []
            for e2 in ("pe", "act", "dve"):
                c = self.count[e2]
                if c and kn.get(e2, 0) < c:
                    waits.append((e2, c))
                    kn[e2] = c
            for s, c in self.dma_count.items():
                if s.startswith("ring") or s.startswith("d_y") or s.startswith("d_x"):
                    continue
                if kn.get(s, 0) < 16 * c:
                    waits.append((s, 16 * c))
                    kn[s] = 16 * c
            if waits:
                self.streams[e].append((waits, None, None))

    def finish(self, eng="sp"):
        kn = self.known[eng]
        waits = []
        for s, c in self.dma_count.items():
            if kn.get(s, 0) < 16 * c:
                waits.append((s, 16 * c))
                kn[s] = 16 * c
        for e in ("pe", "act", "dve"):
            if self.count[e] and kn.get(e, 0) < self.count[e]:
                waits.append((e, self.count[e]))
        self.streams[eng].append((waits, None, None))

    def emit(self):
        nc = self.nc
        for s in list(ENGS) + list(self.dma_count):
            if s not in self.sems:
                self.sems[s] = self.stack.enter_context(nc.semaphore(s))
        with nc.Block() as block:
            def run(engine, items):
                for waits, fn, inc in items:
                    for s, v in waits:
                        engine.wait_ge(self.sems[s], v)
                    if fn is not None:
                        ins = fn(engine)
                        ins.then_inc(self.sems[inc[0]], inc[1])

            @block.tensor
            def _(eng):
                run(eng, self.streams["pe"])

            @block.scalar
            def _(eng):
                run(eng, self.streams["act"])

            @block.vector
            def _(eng):
                run(eng, self.streams["dve"])

            @block.gpsimd
            def _(eng):
                run(eng, self.streams["pool"])

            @block.sync
            def _(eng):
                run(eng, self.streams["sp"])


NBLK = 33


def _kc(W, nk):
    C = W.shape[1]
    return np.ascontiguousarray(W.reshape(nk, 128, C).transpose(1, 0, 2)).reshape(128, nk * C)


def _pad(a):
    out = np.zeros((128, 4096), np.float32)
    out[:, : a.shape[1]] = a
    return out


def build_wall(w_in, w_pool_up, w_hgrn_up, w_out, w_g, w_u, w_d, w_pg, w_pp):
    blocks = []
    blocks.append(_kc(w_in[:, 0:512], 8))
    zz = w_in[:, 512:2560].reshape(8, 128, 4, 4, 128)
    ga = w_in[:, 2560:3584].reshape(8, 128, 8, 128)
    gb = w_in[:, 3584:4608].reshape(8, 128, 8, 128)
    for h in range(4):
        blocks.append(np.ascontiguousarray(zz[:, :, :, h, :].transpose(1, 0, 2, 3)).reshape(128, 4096))
        gg = np.stack([ga[:, :, 2 * h], gb[:, :, 2 * h], ga[:, :, 2 * h + 1], gb[:, :, 2 * h + 1]], axis=2)
        blocks.append(np.ascontiguousarray(gg.transpose(1, 0, 2, 3)).reshape(128, 4096))
    for half in range(2):
        yy = np.stack([w_pool_up[:, half * 512:(half + 1) * 512].reshape(4, 128, 512),
                       w_hgrn_up[:, half * 512:(half + 1) * 512].reshape(4, 128, 512)], axis=2)
        blocks.append(np.ascontiguousarray(yy.transpose(1, 0, 2, 3)).reshape(128, 4096))
    blocks.append(_kc(w_out[:, 0:512], 8))
    blocks.append(_kc(w_out[:, 512:1024], 8))
    for jb in range(11):
        gu = np.stack([w_g[:, jb * 256:(jb + 1) * 256], w_u[:, jb * 256:(jb + 1) * 256]], axis=1)
        blocks.append(_kc(gu.reshape(1024, 512), 8))
    for half in range(2):
        for kb in range(3):
            nk = 8 if kb < 2 else 6
            blocks.append(_pad(_kc(w_d[kb * 1024: kb * 1024 + nk * 128, half * 512:(half + 1) * 512], nk)))
    blocks.append(_kc(w_pg[:, 0:512], 8))
    blocks.append(_pad(_kc(w_pp, 2)))
    blocks.append(_kc(w_pg[:, 512:1024], 8))
    assert len(blocks) == NBLK
    return np.ascontiguousarray(np.stack(blocks, axis=0).astype(np.float32))


def build_consts():
    c = {}
    c["ident"] = np.eye(128, dtype=np.float32)
    s = np.arange(128)[:, None]
    t = np.arange(128)[None, :]
    c["maskP"] = ((s // 64 == t // 64) & (s <= t)).astype(np.float32)
    c["maskS"] = ((s // 8 == t // 8) & (s <= t)).astype(np.float32)
    c["seqm"] = (s // 8 == np.arange(16)[None, :]).astype(np.float32)
    r = np.ones(640, np.float32)
    r[0:512:64] = 0.0
    r[512:640:8] = 0.0
    c["resetm"] = np.broadcast_to(r, (128, 640)).copy()
    rc = np.zeros((4, 16), np.float32)
    for g, w in enumerate((2, 4, 8, 16)):
        rc[g] = 1.0 / np.minimum(np.arange(16) + 1, w)
    c["rc"] = np.broadcast_to(rc.reshape(1, 64), (128, 64)).copy()
    order = ["ident", "seqm", "rc", "maskP", "maskS", "resetm"]
    offs = {}
    o = 0
    for k in order:
        offs[k] = (o, c[k].shape[1])
        o += c[k].shape[1]
    return np.ascontiguousarray(np.concatenate([c[k] for k in order], axis=1)), offs


CONST_ARR, COFF = build_consts()
CW = CONST_ARR.shape[1]
CW_KEEP = COFF["maskP"][0]


class _Stop(Exception):
    pass


STOP = None


def ck(name):
    if STOP == name:
        raise _Stop()


def build_program(ntiles=NTILES):
    nc = bass.Bass("TRN2", target_bir_lowering=False)

    def din(name, shape):
        return nc.dram_tensor(name, shape, F32, kind="ExternalInput").ap()

    def dout(name, shape):
        return nc.dram_tensor(name, shape, F32, kind="ExternalOutput").ap()

    xp = din("xp", [2048, 1024])
    xs = din("xs", [128, 1024])
    ppd = din("pp", [2048, 256])
    psd = din("ps", [128, 256])
    spool = din("spool", [240, 512])
    shg = din("shg", [16, 4, 128, 128])
    wall = din("wall", [NBLK, 128, 4096])
    cst = din("cst", [128, CW])
    gvec = din("gvec", [4, 1024])
    smalld = din("small", [128, 16])
    pmixd = din("pmix", [128, 512])
    yp = dout("yp", [2048, 1024])
    ys = dout("ys", [128, 1024])
    npp = dout("npp", [15, 512])
    nhp = dout("nhp", [4, 128, 128])
    nps = dout("nps", [240, 512])
    nhs = dout("nhs", [16, 4, 128, 128])

    with ExitStack() as st:
        P = Prog(nc, st)

        def sb(name, shape, dt):
            return st.enter_context(nc.sbuf_tensor("sb_" + name, shape, dt))

        xres = sb("xres", [128, 5, 1024], F32)
        hT = sb("hT", [128, 8, 640], BF16)
        hn = [sb(f"hn{i}", [128, 1024], BF16) for i in range(2)]
        junk = sb("junk", [128, 512], BF16)
        ystage = [sb(f"ystage{i}", [128, 512], F32) for i in range(4)]
        resetmb = sb("resetmb", [128, 640], BF16)
        ring = [sb(f"ring{i}", [128, 4096], BF16) for i in range(NRING)]
        gB = sb("gB", [128, 4, 1024], F32)
        cs = sb("cs", [128, CW_KEEP], F32)
        identb = sb("identb", [128, 128], BF16)
        onesd = sb("onesd", [128, 128], BF16)
        maskPb = sb("maskPb", [128, 128], BF16)
        maskSb = sb("maskSb", [128, 128], BF16)
        pmixf = sb("pmixf", [128, 512], F32)
        pmix = sb("pmix", [128, 4, 128], BF16)
        small = sb("small", [128, 16], F32)
        lbv = sb("lbv", [128, 8], F32)
        carry = sb("carry", [128, 4, 128], F32)
        ucarry = sb("ucarry", [128, 4, 16], F32)
        ssq = sb("ssq", [128, 16], F32)
        ssq1 = sb("ssq1", [128, 8], F32)
        dmy = sb("dmy", [128, 2], F32)
        rst1 = sb("rst1", [128, 4], F32)
        rst = sb("rst", [128, 8], F32)
        identf = cs[:, COFF["ident"][0]:COFF["ident"][0] + 128]
        seqm = cs[:, COFF["seqm"][0]:COFF["seqm"][0] + 16]
        rcv = cs[:, COFF["rc"][0]:COFF["rc"][0] + 64]
        resetm = resetmb

        MIX_BYTES = 0
        scr_plan = {}

        def plan(phase, name, nelem, dt):
            nonlocal MIX_BYTES
            nbytes = nelem * (4 if dt == F32 else 2)
            nbytes = (nbytes + 31) // 32 * 32
            off = scr_plan.setdefault(("off", phase), 0)
            scr_plan[name] = (off, nelem, dt)
            scr_plan[("off", phase)] = off + nbytes

        U_NAMES = ["ug", "ue", "pa", "pb_", "sa", "sb_", "tmp16", "pooled", "npsb", "stg", "npp_sb"]
        for name, n, dt in [
            ("ug", 528, F32), ("ue", 16 * 24, F32), ("pa", 528, F32), ("pb_", 528, F32),
            ("sa", 16 * 24, F32), ("sb_", 16 * 24, F32), ("tmp16", 16, F32),
            ("pooled", 4 * 640, BF16), ("npsb", 4 * 240, F32),
            ("stg", 2 * 512, F32), ("npp_sb", 512, F32), ("pool_out", 4 * 640, BF16),
            ("qf", 640, F32), ("t1", 640, F32), ("t2", 640, F32), ("t3", 640, F32), ("t4", 640, F32),
            ("qe", 640, BF16), ("qe2", 640, BF16), ("keb", 640, BF16), ("kdb", 640, BF16), ("vb", 640, BF16), ("gbf", 640, BF16), ("gbf2", 640, BF16),
            ("kdT", 5 * 128, BF16), ("vT", 5 * 128, BF16), ("Sall", 9 * 128, F32), ("Sbf", 8 * 128, BF16),
            ("dec", 24, F32), ("dec2", 24, F32), ("Am", 5 * 128, BF16), ("osq", 640, BF16), ("o_fin", 4 * 640, BF16),
            ("S0", 16 * 128, F32), ("S0bf", 16 * 128, BF16), ("Vm", 16 * 128, BF16),
            ("merged", 8 * 640, BF16), ("s1", 640, F32), ("s2", 640, F32),
        ]:
            plan("mix", name, n, dt)
        for name, n, dt in [
            ("hidden", 22 * 640, BF16), ("ftmp", 640, F32), ("sgt0", 512, F32), ("sgt1", 512, F32),
            ("pT", 2 * 640, BF16), ("pstage", 5 * 256, F32), ("pbf", 256, BF16),
        ]:
            plan("ffn", name, n, dt)
        u_end = scr_plan["pool_out"][0]
        assert 2 * 8 * 640 * 2 <= u_end, u_end
        scr_plan["sgA"] = (0, 8 * 640, BF16)
        scr_plan["sgB"] = (8 * 640 * 2, 8 * 640, BF16)
        scr_bytes = max(scr_plan[("off", "mix")], scr_plan[("off", "ffn")])
        scr = sb("scr", [128, scr_bytes // 4], F32)

        def sv(name):
            off, n, dt = scr_plan[name]
            if dt == F32:
                return scr[:, off // 4: off // 4 + n]
            return scr[:, off // 4: off // 4 + n // 2].bitcast(BF16)

        ug = sv("ug"); ue = sv("ue").rearrange("p (i r) -> p i r", r=24)
        pa = sv("pa"); pb_ = sv("pb_")
        sa = sv("sa").rearrange("p (i r) -> p i r", r=24); sb_ = sv("sb_").rearrange("p (i r) -> p i r", r=24)
        tmp16 = sv("tmp16")
        pooled = sv("pooled").rearrange("p (g t) -> p g t", g=4)
        pool_out = sv("pool_out").rearrange("p (g t) -> p g t", g=4)
        npsb = sv("npsb").rearrange("p (g r) -> p g r", g=4)
        stg = sv("stg").rearrange("p (h c) -> p h c", h=2)
        nps_sb = stg
        npp_sb = sv("npp_sb")
        qf = sv("qf"); t1 = sv("t1"); t2 = sv("t2"); t3 = sv("t3"); t4 = sv("t4")
        qe = sv("qe"); qe2 = sv("qe2"); keb = sv("keb"); kdb = sv("kdb"); vb = sv("vb"); gbf = sv("gbf"); gbf2 = sv("gbf2")
        kdT = sv("kdT").rearrange("p (b k) -> p b k", b=5); vT = sv("vT").rearrange("p (b k) -> p b k", b=5)
        Sall = sv("Sall").rearrange("p (c v) -> p c v", c=9); Sbf = sv("Sbf").rearrange("p (c v) -> p c v", c=8)
        dec = sv("dec"); dec2 = sv("dec2"); Am = sv("Am").rearrange("p (b k) -> p b k", b=5); osq = sv("osq")
        o_fin = sv("o_fin").rearrange("p (h t) -> p h t", h=4)
        S0 = sv("S0").rearrange("p (i v) -> p i v", i=16); S0bf = sv("S0bf").rearrange("p (i v) -> p i v", i=16)
        Vm = sv("Vm").rearrange("p (i v) -> p i v", i=16)
        merged = sv("merged").rearrange("p (k t) -> p k t", k=8); s1 = sv("s1"); s2 = sv("s2")
        sgA = sv("sgA").rearrange("p (k t) -> p k t", k=8); sgB = sv("sgB").rearrange("p (k t) -> p k t", k=8)
        hidden = sv("hidden").rearrange("p (f t) -> p f t", f=22); ftmp = sv("ftmp")
        assert scr_plan["S0bf"][0] == scr_plan["S0"][0] + 8192 and scr_plan["Vm"][0] == scr_plan["S0"][0] + 12288
        assert scr_plan["S0"][0] >= scr_plan[("off", "ffn")]
        xo = scr_plan["S0"][0] // 4
        xalt = scr[:, xo:xo + 4096].rearrange("p (b d) -> p b d", b=4)
        sgt = [sv("sgt0"), sv("sgt1")]
        pT = sv("pT").rearrange("p (k t) -> p k t", k=2); pstage = sv("pstage").rearrange("p (b c) -> p b c", b=5); pbf = sv("pbf")

        pbs = [st.enter_context(nc.psum_tensor(f"pb{i}", [128, 512], F32)) for i in range(8)]
        bank_ctr = [0]

        def nb():
            b = bank_ctr[0] % 8
            bank_ctr[0] += 1
            return b

        def PB(b):
            return ("pb", b)

        wstate = {"tile": 0, "issued": set(), "released": set(), "total": NBLK * ntiles}

        def w_issue_n(n, extra_reads=()):
            if n >= wstate["total"] or n in wstate["issued"]:
                return
            slot = n % NRING
            blk = n % NBLK
            P.dma("pool", f"ring{slot}",
                  lambda e, slot=slot, blk=blk: e.dma_start(out=ring[slot][:], in_=wall[blk]),
                  reads=list(extra_reads), writes=[("ring", slot)])
            wstate["issued"].add(n)

        def w_issue(extra_reads=()):
            w_issue_n(len(wstate["issued"]), extra_reads)

        def w_get(expect):
            n = wstate["tile"] * NBLK + expect
            assert n in wstate["issued"], ("ring too small / block not prefetched", n)
            assert n - NRING < 0 or (n - NRING) in wstate["released"]
            slot = n % NRING
            return ring[slot], ("ring", slot)

        def w_release(expect):
            n = wstate["tile"] * NBLK + expect
            assert n in wstate["issued"] and n not in wstate["released"]
            wstate["released"].add(n)
            w_issue_n(n + NRING)

        def A(eng, fn, reads, writes):
            return P.op(eng, fn, reads=reads, writes=writes)

        def mm(out, lhsT, rhs, start, stop, reads, bank):
            A("pe", lambda e: e.matmul(out, lhsT=lhsT, rhs=rhs, start=start, stop=stop), reads, [PB(bank)])

        def tp(out, in_, ident, reads, bank):
            A("pe", lambda e: e.transpose(out=out, in_=in_, identity=ident), reads, [PB(bank)])

        def act(out, in_, func, reads, writes, **kw):
            A("act", lambda e: e.activation(out=out, in_=in_, func=func, **kw), reads, writes)

        def tt(out, in0, in1, op, reads, writes, eng="dve"):
            A(eng, lambda e: e.tensor_tensor(out=out, in0=in0, in1=in1, op=op), reads, writes)

        def ts(out, in0, s1_, s2_, op0, op1, reads, writes):
            A("dve", lambda e: e.tensor_scalar(out=out, in0=in0, scalar1=s1_, scalar2=s2_, op0=op0, op1=op1), reads, writes)

        def stt(out, in0, scalar, in1, op0, op1, reads, writes):
            A("dve", lambda e: e.scalar_tensor_tensor(out=out, in0=in0, scalar=scalar, in1=in1, op0=op0, op1=op1), reads, writes)

        def cp(out, in_, reads, writes, eng="dve"):
            if eng == "act":
                act(out, in_, AF.Copy, reads, writes)
            else:
                A(eng, lambda e: e.tensor_copy(out=out, in_=in_), reads, writes)

        def warm(func):
            act(dmy[:, 1:2], dmy[:, 0:1], func, ["dmy0"], ["dmy1"])

        A("dve", lambda e: e.memset(dmy[:], 1.0), [], ["dmy0", "dmy1"])
        P.dma("sp", "d_cs", lambda e: e.dma_start(out=cs[:], in_=cst[:, 0:CW_KEEP]), writes=["cs"])
        cstage = scr[:, 0:CW - CW_KEEP]
        P.dma("sp", "d_cs2", lambda e: e.dma_start(out=cstage, in_=cst[:, CW_KEEP:CW]), writes=["cstage"])
        maskPf = cstage[:, 0:128]
        maskSf = cstage[:, 128:256]
        resetmf = cstage[:, 256:896]
        P.dma("sp", "d_small", lambda e: e.dma_start(out=small[:], in_=smalld), writes=["small"])
        P.dma("sp", "d_pmix", lambda e: e.dma_start(out=pmixf[:], in_=pmixd), writes=["pmixf"])
        for gi in range(4):
            P.dma("sp", f"d_g{gi}",
                  lambda e, gi=gi: e.dma_start(out=gB[:, gi, :], in_=gvec[gi:gi + 1, :].broadcast_to([128, 1024])),
                  writes=[("gB", gi)])
        cp(identb[:], identf, ["cs"], ["identb"])
        cp(maskPb[:], maskPf, ["cstage"], ["maskPb"])
        cp(maskSb[:], maskSf, ["cstage"], ["maskSb"])
        cp(resetmb[:], resetmf, ["cstage"], ["resetmb"])
        cp(pmix[:].rearrange("p g c -> p (g c)"), pmixf[:], ["pmixf"], ["pmix"])
        A("dve", lambda e: e.memset(onesd[:], 1.0 / 128.0), [], ["onesd"])
        A("dve", lambda e: e.memset(carry[:].rearrange("p h v -> p (h v)"), 0.0), [], ["carry"])
        A("dve", lambda e: e.memset(ucarry[:].rearrange("p g t -> p (g t)"), 0.0), [], ["ucarry"])
        tt(lbv[:, 0:4], small[:, 8:12], small[:, 12:16], ALU.subtract, ["small"], ["lbv"])
        act(lbv[:, 0:4], lbv[:, 0:4], AF.Sigmoid, ["lbv"], ["lbv"])
        ts(lbv[:, 4:8], lbv[:, 0:4], -1.0, 1.0, ALU.mult, ALU.add, ["lbv"], ["lbv"])

        def run_tile(ti):
            wstate["tile"] = ti
            has_s = ti == 0
            last = ti == 3
            NT = 640 if has_s else 512
            blocks = [0, 1, 2, 3] + ([4] if has_s else [])
            subs = [(0, 512, "p")] + ([(512, 128, "s")] if has_s else [])
            hTk_p = [("hT", b) for b in range(4)]
            hTk = {"p": hTk_p, "s": [("hT", 4)]}

            def K(name):
                return [(name, "p")] + ([(name, "s")] if has_s else [])

            def xbuf(tj):
                return (xres, "xres") if tj % 2 == 0 else (xalt, "xalt")

            X, xk = xbuf(ti)

            def load_x(tj, b):
                src = xp[tj * 512 + b * 128: tj * 512 + (b + 1) * 128, :] if b < 4 else xs
                Xn, xkn = xbuf(tj)
                P.dma("sp", f"d_x{tj % 2}_{b}", lambda e, b=b, src=src, Xn=Xn: e.dma_start(out=Xn[:, b, :], in_=src),
                      writes=[(xkn, b)])

            if ti == 0:
                for b in blocks:
                    load_x(0, b)
            if ti == 0:
                for _ in range(NRING):
                    w_issue(extra_reads=[(xk, b) for b in blocks])
            if has_s:
                for half in range(2):
                    P.dma("sp", f"d_stg{half}",
                          lambda e, half=half: e.dma_start(out=stg[0:120, half, :], in_=spool[half * 120:(half + 1) * 120, :]),
                          writes=[("stg", half)])

            def half_stats(b, half, X_=None, xk_=None, ssq_=None, tag="ssq"):
                X_ = X if X_ is None else X_
                xk_ = xk if xk_ is None else xk_
                ssq_ = ssq if ssq_ is None else ssq_
                act(junk[:, 0:512], X_[:, b, half * 512:(half + 1) * 512], AF.Square, [(xk_, b)], ["junk", (tag, b, half)],
                    accum_out=ssq_[:, 2 * b + half:2 * b + half + 1])

            def norm_stats(have_partials, blks=None, X_=None, xk_=None, ssq_=None, rst_=None, tag="ssq", rtag="rst"):
                blks = blocks if blks is None else blks
                ssq_ = ssq if ssq_ is None else ssq_
                rst_ = rst if rst_ is None else rst_
                if not have_partials:
                    for b in blks:
                        for half in range(2):
                            half_stats(b, half, X_, xk_, ssq_, tag)
                nbk = len(blks)
                tt(rst_[:, 0:nbk], ssq_[:, 0:2 * nbk:2], ssq_[:, 1:2 * nbk:2], ALU.add,
                   [(tag, b, hf) for b in blks for hf in range(2)], [rtag])
                act(rst_[:, 0:nbk], rst_[:, 0:nbk], AF.Ln, [rtag], [rtag], scale=1.0 / 1024.0, bias=EPS)
                act(rst_[:, 0:nbk], rst_[:, 0:nbk], AF.Exp, [rtag], [rtag], scale=-0.5)

            def norm_apply(gi, blks=None, X_=None, xk_=None, rst_=None, rtag="rst"):
                blks = blocks if blks is None else blks
                X_ = X if X_ is None else X_
                xk_ = xk if xk_ is None else xk_
                rst_ = rst if rst_ is None else rst_
                for b in blks:
                    hb = hn[b % 2]
                    hk = f"hn{b % 2}"
                    stt(hb[:], X_[:, b, :], rst_[:, b:b + 1], gB[:, gi, :], ALU.mult, ALU.mult,
                        [(xk_, b), rtag, ("gB", gi)], [hk])
                    bank = nb()
                    bv = pbs[bank][:].bitcast(BF16)
                    for k in range(8):
                        tp(bv[:, k * 128:(k + 1) * 128], hb[:, k * 128:(k + 1) * 128], identb[:], [hk, "identb"], bank)
                    c0 = b * 128
                    cp(hT[:, :, c0:c0 + 128], bv.rearrange("p (k t) -> p k t", k=8), [PB(bank)], [("hT", b)], eng="act")

            def norm_T(gi, have_partials):
                norm_stats(have_partials)
                norm_apply(gi)

            ck("load")
            if ti == 0:
                norm_T(0, False)
            ck("norm1")

            wv, wk = w_get(0)
            wu = wv[:].rearrange("p (k c) -> p k c", k=8)
            ubanks = []
            for g in range(4):
                bp = nb()
                bs = nb() if has_s else None
                for k in range(8):
                    mm(pbs[bp][:, 0:512], wu[:, k, g * 128:(g + 1) * 128], hT[:, k, 0:512], k == 0, k == 7, [wk] + hTk_p, bp)
                    if has_s:
                        mm(pbs[bs][:, 0:128], wu[:, k, g * 128:(g + 1) * 128], hT[:, k, 512:640], k == 0, k == 7, [wk, ("hT", 4)], bs)
                ubanks.append((bp, bs))
                if g == 3:
                    w_release(0)
                w = 2 << g
                cp(ug[:, 0:16], ucarry[:, g, :], ["ucarry"], ["ug"])
                cp(ug[:, 16:528], pbs[bp][:, 0:512], [PB(bp)], ["ug"], eng="act")
                tt(pa[:, 1:528], ug[:, 1:528], ug[:, 0:527], ALU.add, ["ug"], ["pa"])
                sw = pa
                swk = "pa"
                if w >= 4:
                    tt(pb_[:, 3:528], pa[:, 3:528], pa[:, 1:526], ALU.add, ["pa"], ["pb_"])
                    sw, swk = pb_, "pb_"
                if w >= 8:
                    tt(pa[:, 7:528], pb_[:, 7:528], pb_[:, 3:524], ALU.add, ["pb_"], ["pa"])
                    sw, swk = pa, "pa"
                if w >= 16:
                    tt(pb_[:, 15:528], pa[:, 15:528], pa[:, 7:520], ALU.add, ["pa"], ["pb_"])
                    sw, swk = pb_, "pb_"
                stt(pooled[:, g, 0:512], sw[:, 16:528], 1.0 / w, ug[:, 16:528], ALU.mult, ALU.subtract,
                    [swk, "ug"], [("pooled", g, "p")])
                if ti == 0:
                    tt(tmp16[:], sw[:, 16:32], rcv[:, g * 16:(g + 1) * 16], ALU.mult, [swk, "cs"], ["tmp16"])
                    tt(pooled[:, g, 0:16], tmp16[:], ug[:, 16:32], ALU.subtract, ["tmp16", "ug"], [("pooled", g, "p")])
                cp(ucarry[:, g, :], ug[:, 512:528], ["ug"], ["ucarry"])
                if last:
                    bt = nb()
                    tp(pbs[bt][0:15, 0:128], ug[:, 513:528], identf, ["ug", "cs"], bt)
                    cp(npp_sb[0:15, g * 128:(g + 1) * 128], pbs[bt][0:15, 0:128], [PB(bt)], ["npp_sb"], eng="act")
                if has_s:
                    bt = nb()
                    for half in range(2):
                        tp(pbs[bt][:, half * 120:(half + 1) * 120], stg[0:120, half, g * 128:(g + 1) * 128],
                           identf[0:120, 0:120], [("stg", half), "cs"], bt)
                    cp(ue[:, :, 0:15], pbs[bt][:, 0:240].rearrange("p (i r) -> p i r", r=15), [PB(bt)], ["ue"], eng="act")
                    cp(ue[:, :, 15:23], pbs[bs][:, 0:128].rearrange("p (i t) -> p i t", t=8), [PB(bs)], ["ue"], eng="act")
                    tt(sa[:, :, 1:23], ue[:, :, 1:23], ue[:, :, 0:22], ALU.add, ["ue"], ["sa"])
                    ssw, sswk = sa, "sa"
                    if w >= 4:
                        tt(sb_[:, :, 3:23], sa[:, :, 3:23], sa[:, :, 1:21], ALU.add, ["sa"], ["sb_"])
                        ssw, sswk = sb_, "sb_"
                    if w >= 8:
                        tt(sa[:, :, 7:23], sb_[:, :, 7:23], sb_[:, :, 3:19], ALU.add, ["sb_"], ["sa"])
                        ssw, sswk = sa, "sa"
                    if w >= 16:
                        tt(sb_[:, :, 15:23], sa[:, :, 15:23], sa[:, :, 7:15], ALU.add, ["sa"], ["sb_"])
                        ssw, sswk = sb_, "sb_"
                    stt(pooled[:, g, 512:640].rearrange("p (i t) -> p i t", t=8), ssw[:, :, 15:23], 1.0 / w,
                        ue[:, :, 15:23], ALU.mult, ALU.subtract, [sswk, "ue"], [("pooled", g, "s")])
                    cp(npsb[:, g, :].rearrange("p (i r) -> p i r", r=15), ue[:, :, 8:23], ["ue"], [("npsb", g)])
            if last:
                P.dma("sp", "d_npp", lambda e: e.dma_start(out=npp, in_=npp_sb[0:15, :]), reads=["npp_sb"])
            if has_s:
                for half in range(2):
                    bt = nb()
                    for g in range(4):
                        tp(pbs[bt][0:120, g * 128:(g + 1) * 128], npsb[:, g, half * 120:(half + 1) * 120], identf,
                           [("npsb", g), "cs"], bt)
                    cp(nps_sb[0:120, half, :], pbs[bt][0:120, 0:512], [PB(bt)], [("stg", half)], eng="act")
                    P.dma("sp", f"d_nps{half}",
                          lambda e, half=half: e.dma_start(out=nps[half * 120:(half + 1) * 120, :], in_=nps_sb[0:120, half, :]),
                          reads=[("stg", half)])
            ck("u")
            PIPE = not has_s
            if PIPE:
                zc, rcn = [0], [0]

                def nbZ():
                    b = zc[0] % 4
                    zc[0] += 1
                    return b

                def nbR():
                    b = 6 + rcn[0] % 2
                    rcn[0] += 1
                    return b

                gcn = [0]

                def nbG():
                    b = 4 + gcn[0] % 2
                    gcn[0] += 1
                    return b
            else:
                nbZ = nbR = nbG = nb
            gb2 = [gbf, gbf2]
            QE = [qe, qe2]
            DEC = [dec, dec2]
            HS = {h: {} for h in range(4)}

            def poolmix():
                for g in range(4):
                    for (c0, n, sk) in subs:
                        bk = nbR()
                        mm(pbs[bk][:, 0:n], pmix[:, g, :], pooled[:, g, c0:c0 + n], True, True, ["pmix", ("pooled", g, sk)], bk)
                        act(pool_out[:, g, c0:c0 + n], pbs[bk][:, 0:n], AF.Copy, [PB(bk), "small"], [("pool_out", g, sk)],
                            scale=small[:, g:g + 1])

            def z_group(h, j):
                S_ = HS[h]
                if j == 0:
                    wv, wk = w_get(1 + 2 * h)
                    S_["wh"] = wv[:].rearrange("p (k j c) -> p k j c", k=8, j=4)
                    S_["wk"] = wk
                    S_["bs"] = nbZ() if has_s else None
                    S_["zb"] = []
                wh, wk, bs = S_["wh"], S_["wk"], S_["bs"]
                bp = nbZ()
                for k in range(8):
                    mm(pbs[bp][:, 0:512], wh[:, k, j, :], hT[:, k, 0:512], k == 0, k == 7, [wk] + hTk_p, bp)
                    if has_s:
                        mm(pbs[bs][:, j * 128:(j + 1) * 128], wh[:, k, j, :], hT[:, k, 512:640], k == 0, k == 7,
                           [wk, ("hT", 4)], bs)
                S_["zb"].append(bp)
                if j == 3:
                    w_release(1 + 2 * h)

            def z_evac(h):
                S_ = HS[h]
                zb, bs = S_["zb"], S_["bs"]
                gbh = gb2[h % 2]
                gbk = f"gbf{h % 2}"

                def zsrc(j, sk):
                    return (pbs[zb[j]][:, 0:512], PB(zb[j])) if sk == "p" else (pbs[bs][:, j * 128:(j + 1) * 128], PB(bs))

                for (c0, n, sk) in subs:
                    src, key = zsrc(1, sk)
                    act(t1[:, c0:c0 + n], src, AF.Sigmoid, [key], [("t1", sk)])
                    src, key = zsrc(0, sk)
                    act(qf[:, c0:c0 + n], src, AF.Sigmoid, [key], [("qf", sk)])
                    tt(qf[:, c0:c0 + n], qf[:, c0:c0 + n], src, ALU.mult, [("qf", sk), key], [("qf", sk)])
                    src, key = zsrc(3, sk)
                    act(t4[:, c0:c0 + n], src, AF.Sigmoid, [key], [("t4", sk)])
                    tt(gbh[:, c0:c0 + n], t4[:, c0:c0 + n], src, ALU.mult, [("t4", sk), key], [(gbk, sk)])
                    src, key = zsrc(2, sk)
                    cp(vb[:, c0:c0 + n], src, [key], [("vb", sk)])
                warm(AF.Ln)

            def g_mm(h, jj):
                S_ = HS[h]
                if jj == 0:
                    wgv_, wgk_ = w_get(2 + 2 * h)
                    S_["wgv"] = wgv_[:].rearrange("p (k t c) -> p k t c", k=8, t=4)
                    S_["wgk"] = wgk_
                    S_["gate_ev"] = []
                wgv, wgk_ = S_["wgv"], S_["wgk"]
                j = 2 * h + jj
                bsg = nbG() if has_s else None
                for t_ in range(2):
                    bpg = nbG()
                    for k in range(8):
                        mm(pbs[bpg][:, 0:512], wgv[:, k, 2 * jj + t_, :], hT[:, k, 0:512], k == 0, k == 7, [wgk_] + hTk_p, bpg)
                        if has_s:
                            mm(pbs[bsg][:, t_ * 128:(t_ + 1) * 128], wgv[:, k, 2 * jj + t_, :], hT[:, k, 512:640], k == 0, k == 7,
                               [wgk_, ("hT", 4)], bsg)
                    S_["gate_ev"].append((j, t_, bpg, bsg))
                if jj == 1:
                    w_release(2 + 2 * h)

            def g_evac(h, jj):
                evs = HS[h]["gate_ev"][2 * jj:2 * jj + 2]
                if has_s:
                    for (j, t_, bpg, bsg) in evs:
                        dst = sgA if t_ == 0 else sgB
                        dk = "sgA" if t_ == 0 else "sgB"
                        cp(dst[:, j, 512:640], pbs[bsg][:, t_ * 128:(t_ + 1) * 128], [PB(bsg)], [(dk, j, "s")], eng="act")
                for (j, t_, bpg, bsg) in evs:
                    dst = sgA if t_ == 0 else sgB
                    dk = "sgA" if t_ == 0 else "sgB"
                    cp(dst[:, j, 0:512], pbs[bpg][:, 0:512], [PB(bpg)], [(dk, j, "p")], eng="act")

            def chain_a(h):
                ts(t1[:, 0:NT], t1[:, 0:NT], lbv[:, 4 + h:5 + h], lbv[:, h:h + 1], ALU.mult, ALU.add, K("t1") + ["lbv"], K("t1"))
                ts(t2[:, 0:NT], t1[:, 0:NT], -1.0, 1.0, ALU.mult, ALU.add, K("t1"), K("t2"))
                act(t1[:, 0:NT], t1[:, 0:NT], AF.Ln, K("t1"), K("t1"))

            def cb_scan(h):
                A("dve", lambda e: e.tensor_tensor_scan(out=t3[:, 0:NT], data0=resetm[:, 0:NT], data1=t1[:, 0:NT],
                                                        initial=0.0, op0=ALU.mult, op1=ALU.add),
                  K("t1") + ["resetmb"], K("t3"))

            def cb_exp(h):
                act(t1[:, 0:NT], t3[:, 0:NT], AF.Exp, K("t3"), K("t1"))
                act(t4[:, 0:NT], t3[:, 0:NT], AF.Exp, K("t3"), K("t4"), scale=-1.0)

            def cb_mul(h):
                qe_ = QE[h % 2]
                qk = f"qe{h % 2}"
                tt(qe_[:, 0:NT], qf[:, 0:NT], t1[:, 0:NT], ALU.mult, K("qf") + K("t1"), [(qk, "p")] + ([(qk, "s")] if has_s else []))
                tt(t2[:, 0:NT], t2[:, 0:NT], t4[:, 0:NT], ALU.mult, K("t2") + K("t4"), K("t2"))

            def cb_keb(h):
                cp(keb[:, 0:NT], t2[:, 0:NT], K("t2"), K("keb"), eng="act")

            def cb_dec(h):
                dec_ = DEC[h % 2]
                dk_ = f"dec{h % 2}"
                cp(dec_[:, 0:8], t1[:, 63:512:64], [("t1", "p")], [(dk_, "p")])
                tt(kdb[:, 0:512].rearrange("p (c t) -> p c t", t=64), t2[:, 0:512].rearrange("p (c t) -> p c t", t=64),
                   dec_[:, 0:8].unsqueeze(2).broadcast_to([128, 8, 64]), ALU.mult, [("t2", "p"), (dk_, "p")], [("kdb", "p")])
                if has_s:
                    cp(dec_[:, 8:24], t1[:, 519:640:8], [("t1", "s")], [(dk_, "s")])
                    tt(kdb[:, 512:640].rearrange("p (c t) -> p c t", t=8), t2[:, 512:640].rearrange("p (c t) -> p c t", t=8),
                       dec_[:, 8:24].unsqueeze(2).broadcast_to([128, 16, 8]), ALU.mult, [("t2", "s"), (dk_, "s")], [("kdb", "s")])

            def chain_b(h):
                cb_scan(h); cb_exp(h); cb_mul(h); cb_keb(h); cb_dec(h)

            def R1(h):
                qe = QE[h % 2]
                qk = f"qe{h % 2}"
                for (src_, srck, dst, dstk) in ((kdb, "kdb", kdT, "kdT"), (vb, "vb", vT, "vT")):
                    bt = nbR()
                    bv = pbs[bt][:].bitcast(BF16)
                    for b in blocks:
                        sk = "p" if b < 4 else "s"
                        tp(bv[:, b * 128:(b + 1) * 128], src_[:, b * 128:(b + 1) * 128], identb[:], [(srck, sk), "identb"], bt)
                    nbk = len(blocks)
                    cp(dst[:, 0:nbk, :], bv[:, 0:nbk * 128].rearrange("p (b k) -> p b k", k=128), [PB(bt)], [dstk], eng="act")
                ba = nbR()
                for b in range(4):
                    mm(pbs[ba][:, b * 128:(b + 1) * 128], keb[:, b * 128:(b + 1) * 128], qe[:, b * 128:(b + 1) * 128],
                       True, True, [("keb", "p"), (qk, "p")], ba)
                tt(Am[:, 0:4, :], pbs[ba][:, 0:512].rearrange("p (b t) -> p b t", b=4),
                   maskPb[:].unsqueeze(1).broadcast_to([128, 4, 128]), ALU.mult, [PB(ba), "maskPb"], [("Am", "p")])
                if has_s:
                    bas = nbR()
                    mm(pbs[bas][:, 0:128], keb[:, 512:640], qe[:, 512:640], True, True, [("keb", "s"), (qk, "s")], bas)
                    tt(Am[:, 4, :], pbs[bas][:, 0:128], maskSb[:], ALU.mult, [PB(bas), "maskSb"], [("Am", "s")])

            def build_vm():
                tt(Vm[:, :, :], vT[:, 4, :].unsqueeze(1).broadcast_to([128, 16, 128]),
                   seqm.unsqueeze(2).broadcast_to([128, 16, 128]), ALU.mult, ["vT", "cs"], ["Vm"])

            def sample_prefetch(h):
                P.dma("sp", "d_S0", lambda e, h=h: e.dma_start(out=S0[:, :, :], in_=shg[:, h, :, :].rearrange("i k v -> k i v")),
                      writes=["S0"])

            def sample_bf(h):
                cp(S0bf[:, :, :].rearrange("p i v -> p (i v)"), S0[:, :, :].rearrange("p i v -> p (i v)"), ["S0"], ["S0bf"], eng="act")

            def sample_state_update(h):
                dec = DEC[h % 2]
                dk_ = f"dec{h % 2}"
                tt(S0[:, :, :], S0[:, :, :], dec[:, 8:24].unsqueeze(2).broadcast_to([128, 16, 128]), ALU.mult,
                   ["S0", (dk_, "s")], ["S0"])
                for q4 in range(4):
                    bd = nbR()
                    mm(pbs[bd][:, 0:512], kdT[:, 4, :], Vm[:, 4 * q4:4 * q4 + 4, :].rearrange("p i v -> p (i v)"), True, True,
                       ["kdT", "Vm"], bd)
                    tt(S0[:, 4 * q4:4 * q4 + 4, :].rearrange("p i v -> p (i v)"),
                       S0[:, 4 * q4:4 * q4 + 4, :].rearrange("p i v -> p (i v)"), pbs[bd][:, 0:512], ALU.add,
                       ["S0", PB(bd)], ["S0"])
                P.dma("sp", "d_nhs", lambda e, h=h: e.dma_start(out=nhs[:, h, :, :].rearrange("i k v -> k i v"), in_=S0[:, :, :]),
                      reads=["S0"])

            def R2_mm(h):
                dsb = [nbR(), nbR()]
                HS[h]["dsb"] = dsb
                for c in range(8):
                    blk, half = c // 2, c % 2
                    bk = dsb[half]
                    mm(pbs[bk][:, blk * 128:(blk + 1) * 128], kdT[half * 64:(half + 1) * 64, blk, :],
                       vT[half * 64:(half + 1) * 64, blk, :], True, True, ["kdT", "vT"], bk)
                cp(Sall[:, 0, :], carry[:, h, :], ["carry"], [("Sall", 0)])

            def R2_steps(h, c0, c1):
                dec_ = DEC[h % 2]
                dk_ = f"dec{h % 2}"
                dsb = HS[h]["dsb"]
                for c in range(c0, c1):
                    blk, half = c // 2, c % 2
                    bk = dsb[half]
                    stt(Sall[:, c + 1, :], Sall[:, c, :], dec_[:, c:c + 1], pbs[bk][:, blk * 128:(blk + 1) * 128],
                        ALU.mult, ALU.add, [("Sall", c), (dk_, "p"), PB(bk)], [("Sall", c + 1)])

            def R2_half(h):
                cp(Sbf[:, 0:4, :].rearrange("p c v -> p (c v)"), Sall[:, 0:4, :].rearrange("p c v -> p (c v)"),
                   [("Sall", c) for c in range(4)], [("Sbf", 0)], eng="act")

            def R2_fin(h):
                cp(Sbf[:, 4:8, :].rearrange("p c v -> p (c v)"), Sall[:, 4:8, :].rearrange("p c v -> p (c v)"),
                   [("Sall", c) for c in range(4, 8)], [("Sbf", 1)], eng="act")
                cp(carry[:, h, :], Sall[:, 8, :], [("Sall", 8)], ["carry"])
                if last:
                    P.dma("sp", f"d_nhp{h}", lambda e, h=h: e.dma_start(out=nhp[h], in_=Sall[:, 8, :]), reads=[("Sall", 8)])

            def R2(h):
                R2_mm(h); R2_steps(h, 0, 4); R2_half(h); R2_steps(h, 4, 8); R2_fin(h)

            def R3(h):
                qe = QE[h % 2]
                qk = f"qe{h % 2}"
                S_ = HS[h]
                bo = nbR()
                for b in range(4):
                    mm(pbs[bo][:, b * 128:(b + 1) * 128], vT[:, b, :], Am[:, b, :], True, False, ["vT", ("Am", "p")], bo)
                    for half in range(2):
                        c = 2 * b + half
                        mm(pbs[bo][:, c * 64:(c + 1) * 64], Sbf[:, c, :], qe[:, c * 64:(c + 1) * 64], False, half == 1,
                           [("Sbf", c // 4), (qk, "p")], bo)
                bos = None
                if has_s:
                    bos = nbR()
                    mm(pbs[bos][:, 0:128], vT[:, 4, :], Am[:, 4, :], True, False, ["vT", ("Am", "s")], bos)
                    for i in range(16):
                        mm(pbs[bos][:, i * 8:(i + 1) * 8], S0bf[:, i, :], qe[:, 512 + i * 8:512 + (i + 1) * 8], False, i == 15,
                           ["S0bf", (qk, "s")], bos)
                S_["obanks"] = [(0, 512, "p", bo)] + ([(512, 128, "s", bos)] if has_s else [])
                for (c0, n, sk, bk) in S_["obanks"]:
                    act(osq[:, c0:c0 + n], pbs[bk][:, 0:n], AF.Square, [PB(bk)], [("osq", sk)])

            def R4(h):
                gbh = gb2[h % 2]
                gbk = f"gbf{h % 2}"
                for (c0, n, sk, bk) in HS[h]["obanks"]:
                    bn = nbR()
                    mm(pbs[bn][:, 0:n], onesd[:], osq[:, c0:c0 + n], True, True, ["onesd", ("osq", sk)], bn)
                    act(s1[:, c0:c0 + n], pbs[bn][:, 0:n], AF.Ln, [PB(bn)], [("s1", sk)], bias=EPS)
                    act(s1[:, c0:c0 + n], s1[:, c0:c0 + n], AF.Exp, [("s1", sk)], [("s1", sk)], scale=-0.5)
                    stt(s2[:, c0:c0 + n], pbs[bk][:, 0:n], small[:, 4 + h:5 + h], s1[:, c0:c0 + n], ALU.mult, ALU.mult,
                        [PB(bk), "small", ("s1", sk)], [("s2", sk)])
                tt(o_fin[:, h, 0:NT], s2[:, 0:NT], gbh[:, 0:NT], ALU.mult, K("s2") + [(gbk, "p"), (gbk, "s")],
                   [("o_fin", h, s_) for s_ in ("p", "s")])

            if PIPE:
                for j in range(4):
                    z_group(0, j)
            poolmix()
            ck("poolmix")
            P.barrier()
            ck("bar")
            if PIPE:
                for h in range(4):
                    p = h - 1
                    if h > 0:
                        R1(p)
                    z_evac(h)
                    g_mm(h, 0)
                    if h > 0:
                        R2_mm(p)
                    chain_a(h)
                    if h > 0:
                        R2_steps(p, 0, 4)
                        R2_half(p)
                    g_evac(h, 0)
                    if h < 3:
                        for j in range(4):
                            z_group(h + 1, j)
                    cb_scan(h)
                    if h > 0:
                        R2_steps(p, 4, 7)
                    cb_exp(h)
                    cb_mul(h)
                    cb_dec(h)
                    cb_keb(h)
                    g_mm(h, 1)
                    if h > 0:
                        R2_steps(p, 7, 8)
                        R2_fin(p)
                        R3(p)
                    g_evac(h, 1)
                    if h > 0:
                        R4(p)
                    warm(AF.Sigmoid)
                R1(3); R2(3); R3(3); R4(3)
                warm(AF.Sigmoid)
            else:
                for h in range(4):
                    sample_prefetch(h)
                    for j in range(4):
                        z_group(h, j)
                    z_evac(h)
                    g_mm(h, 0)
                    g_mm(h, 1)
                    chain_a(h)
                    g_evac(h, 0)
                    chain_b(h)
                    R1(h)
                    g_evac(h, 1)
                    sample_bf(h)
                    R2_mm(h)
                    R2_steps(h, 0, 4)
                    R2_half(h)
                    R2_steps(h, 4, 8)
                    build_vm()
                    R2_fin(h)
                    R3(h); R4(h)
                    warm(AF.Sigmoid)
                    sample_state_update(h)

            ck("hgrn")
            for jh in range(2):
                wy, wyk = w_get(9 + jh)
                wyv = wy[:].rearrange("p (k t c) -> p k t c", k=4, t=2)
                for jj in range(4):
                    j = jh * 4 + jj
                    bs = nb() if has_s else None
                    b_ya, b_yb = nb(), nb()
                    for t_, bk, srcb, srck in ((0, b_ya, pool_out, "pool_out"), (1, b_yb, o_fin, "o_fin")):
                        for kk in range(4):
                            for (c0, n, sk) in subs:
                                o = pbs[bk][:, 0:512] if sk == "p" else pbs[bs][:, t_ * 128:(t_ + 1) * 128]
                                mm(o, wyv[:, kk, t_, jj * 128:(jj + 1) * 128], srcb[:, kk, c0:c0 + n], kk == 0, kk == 3,
                                   [wyk, (srck, kk, sk)], bk if sk == "p" else bs)
                    if jj == 3:
                        w_release(9 + jh)
                    for (c0, n, sk) in subs:
                        oa = pbs[b_ya][:, 0:512] if sk == "p" else pbs[bs][:, 0:128]
                        ob = pbs[b_yb][:, 0:512] if sk == "p" else pbs[bs][:, 128:256]
                        ka = PB(b_ya) if sk == "p" else PB(bs)
                        kb_ = PB(b_yb) if sk == "p" else PB(bs)
                        act(s1[:, c0:c0 + n], sgA[:, j, c0:c0 + n], AF.Sigmoid, [("sgA", j, sk)], [("s1", sk)])
                        act(s2[:, c0:c0 + n], sgB[:, j, c0:c0 + n], AF.Sigmoid, [("sgB", j, sk)], [("s2", sk)])
                        tt(s1[:, c0:c0 + n], s1[:, c0:c0 + n], oa, ALU.mult, [("s1", sk), ka], [("s1", sk)])
                        tt(s2[:, c0:c0 + n], s2[:, c0:c0 + n], ob, ALU.mult, [("s2", sk), kb_], [("s2", sk)])
                    tt(merged[:, j, 0:NT], s1[:, 0:NT], s2[:, 0:NT], ALU.add, K("s1") + K("s2"),
                       [("merged", j, s_) for s_ in ("p", "s")])

            ck("merge")
            warm(AF.Ln)
            for half in range(2):
                wo, wok = w_get(11 + half)
                wov = wo[:].rearrange("p (k c) -> p k c", k=8)
                bks = {b: nb() for b in blocks}
                for k in range(8):
                    for b in blocks:
                        sk = "p" if b < 4 else "s"
                        mm(pbs[bks[b]][:, 0:512], merged[:, k, b * 128:(b + 1) * 128], wov[:, k, :], k == 0, k == 7,
                           [wok, ("merged", k, sk)], bks[b])
                w_release(11 + half)
                for b in blocks:
                    tt(X[:, b, half * 512:(half + 1) * 512], X[:, b, half * 512:(half + 1) * 512], pbs[bks[b]][:, 0:512],
                       ALU.add, [(xk, b), PB(bks[b])], [(xk, b)])
                    half_stats(b, half)

            ck("wout")
            norm_T(1, True)
            warm(AF.Sigmoid)
            P.barrier()
            if ti + 1 < ntiles:
                for b in range(4):
                    load_x(ti + 1, b)
            for b in blocks:
                src = ppd[ti * 512 + b * 128: ti * 512 + (b + 1) * 128, :] if b < 4 else psd
                P.dma("sp", f"d_p{b}", lambda e, src=src, b=b: e.dma_start(out=pstage[:, b, :], in_=src), writes=[("pstage", b)])
            for jb in range(11):
                wf, wfk = w_get(13 + jb)
                wfv = wf[:].rearrange("p (k j c) -> p k j c", k=8, j=2)
                for cc in range(2):
                    f = 2 * jb + cc
                    bs = nb() if has_s else None
                    b_g, b_u = nb(), nb()
                    for jx, bk in ((0, b_g), (1, b_u)):
                        for k in range(8):
                            for (c0, n, sk) in subs:
                                o = pbs[bk][:, 0:512] if sk == "p" else pbs[bs][:, jx * 128:(jx + 1) * 128]
                                mm(o, wfv[:, k, jx, cc * 128:(cc + 1) * 128], hT[:, k, c0:c0 + n], k == 0, k == 7,
                                   [wfk] + hTk[sk], bk if sk == "p" else bs)
                    if cc == 1:
                        w_release(13 + jb)
                    for (c0, n, sk) in subs:
                        og = pbs[b_g][:, 0:512] if sk == "p" else pbs[bs][:, 0:128]
                        ou = pbs[b_u][:, 0:512] if sk == "p" else pbs[bs][:, 128:256]
                        kg = PB(b_g) if sk == "p" else PB(bs)
                        ku = PB(b_u) if sk == "p" else PB(bs)
                        act(ftmp[:, c0:c0 + n], og, AF.Sigmoid, [kg], [("ftmp", sk)])
                        tt(ftmp[:, c0:c0 + n], ftmp[:, c0:c0 + n], og, ALU.mult, [("ftmp", sk), kg], [("ftmp", sk)])
                        tt(hidden[:, f, c0:c0 + n], ftmp[:, c0:c0 + n], ou, ALU.mult, [("ftmp", sk), ku], [("hidden", f, sk)])
            for b in blocks:
                cp(pbf[:], pstage[:, b, :], [("pstage", b)], ["pbf"], eng="act")
                bt = nb()
                bv = pbs[bt][:].bitcast(BF16)
                for k in range(2):
                    tp(bv[:, k * 128:(k + 1) * 128], pbf[:, k * 128:(k + 1) * 128], identb[:], ["pbf", "identb"], bt)
                cp(pT[:, :, b * 128:(b + 1) * 128], bv[:, 0:256].rearrange("p (k t) -> p k t", k=2), [PB(bt)], [("pT", b)])
            warm(AF.Ln)
            for half in range(2):
                bks = {b: nb() for b in blocks}
                for kb in range(3):
                    wd, wdk = w_get(24 + half * 3 + kb)
                    wdv = wd[:].rearrange("p (k c) -> p k c", k=8)
                    nk = 8 if kb < 2 else 6
                    for kk in range(nk):
                        f = kb * 8 + kk
                        for b in blocks:
                            sk = "p" if b < 4 else "s"
                            mm(pbs[bks[b]][:, 0:512], hidden[:, f, b * 128:(b + 1) * 128], wdv[:, kk, :], f == 0, f == 21,
                               [wdk, ("hidden", f, sk)], bks[b])
                    w_release(24 + half * 3 + kb)
                for b in blocks:
                    tt(X[:, b, half * 512:(half + 1) * 512], X[:, b, half * 512:(half + 1) * 512], pbs[bks[b]][:, 0:512],
                       ALU.add, [(xk, b), PB(bks[b])], [(xk, b)])
                    half_stats(b, half)

            ck("ffn")
            norm_T(2, True)
            if ti + 1 < ntiles:
                Xn, xkn = xbuf(ti + 1)
                norm_stats(False, blks=[0, 1, 2, 3], X_=Xn, xk_=xkn, ssq_=ssq1, rst_=rst1, tag="ssq1", rtag="rst1")
            warm(AF.Sigmoid)
            wg0, wg0k = w_get(30)
            wpp_, wppk = w_get(31)
            wg1, wg1k = w_get(32)
            wppv = wpp_[:, 0:2048].rearrange("p (k c) -> p k c", k=2)
            for half in range(2):
                wg_, wgk = (wg0, wg0k) if half == 0 else (wg1, wg1k)
                wgv = wg_[:].rearrange("p (k c) -> p k c", k=8)
                bks = {b: nb() for b in blocks}
                for k in range(8):
                    for b in blocks:
                        mm(pbs[bks[b]][:, 0:512], hT[:, k, b * 128:(b + 1) * 128], wgv[:, k, :], k == 0, k == 7,
                           [wgk, ("hT", b)], bks[b])
                if half == 0:
                    w_release(30)
                for b in blocks:
                    sg = sgt[b % 2]
                    sgk = f"sgt{b % 2}"
                    act(sg[:], pbs[bks[b]][:, 0:512], AF.Sigmoid, [PB(bks[b])], [sgk])
                    be = nb()
                    for k in range(2):
                        mm(pbs[be][:, 0:512], pT[:, k, b * 128:(b + 1) * 128], wppv[:, k, half * 512:(half + 1) * 512], k == 0, k == 1,
                           [wppk, ("pT", b)], be)
                    tt(sg[:], sg[:], pbs[be][:, 0:512], ALU.mult, [sgk, PB(be)], [sgk])
                    tt(X[:, b, half * 512:(half + 1) * 512], X[:, b, half * 512:(half + 1) * 512], sg[:], ALU.add,
                       [(xk, b), sgk], [(xk, b)])
                for b in blocks:
                    half_stats(b, half)
                if half == 1:
                    w_release(31); w_release(32)

            ck("ple")
            if ti + 1 < ntiles:
                Xn, xkn = xbuf(ti + 1)
                norm_apply(0, blks=[0, 1, 2, 3], X_=Xn, xk_=xkn, rst_=rst1, rtag="rst1")
            P.barrier()
            norm_stats(True)
            for b in blocks:
                dst = yp[ti * 512 + b * 128: ti * 512 + (b + 1) * 128, :] if b < 4 else ys
                for half in range(2):
                    q_ = (2 * b + half) % 4
                    yst = ystage[q_]
                    ysk = f"ystage{q_}"
                    hs_ = slice(half * 512, (half + 1) * 512)
                    stt(yst[:], X[:, b, hs_], rst[:, b:b + 1], gB[:, 3, hs_], ALU.mult, ALU.mult,
                        [(xk, b), "rst", ("gB", 3)], [ysk])
                    P.dma("sp", f"d_y{q_}", lambda e, yst=yst, dst=dst, hs_=hs_: e.dma_start(out=dst[:, hs_], in_=yst[:]),
                          reads=[ysk])

        P.barrier()
        try:
            ck("setup")
            for ti in range(ntiles):
                run_tile(ti)
        except _Stop:
            pass
        P.finish("sp")
        P.emit()
    return nc


_PROG = {}


def _prep_inputs(inp):
    f = lambda a: np.ascontiguousarray(np.asarray(a, dtype=np.float32))
    w_in = f(inp["w_in"][0])
    wallv = build_wall(w_in, f(inp["w_pool_up"][0]), f(inp["w_hgrn_up"][0]), f(inp["w_out"][0]),
                       f(inp["w_ffn_gate"][0]), f(inp["w_ffn_up"][0]), f(inp["w_ffn_down"][0]),
                       f(inp["w_ple_gate"][0]), f(inp["w_ple_proj"][0]))
    gvec = np.ascontiguousarray(np.stack([f(inp["g_mix"][0]), f(inp["g_ffn"][0]), f(inp["g_ple"][0]), f(inp["g_final"])], 0))
    small = np.zeros((128, 16), np.float32)
    small[:, 0:4] = f(inp["pool_scale"][0]).reshape(4, 128).T
    small[:, 4:8] = f(inp["hgrn_norm"][0]).reshape(4, 128).T
    small[:, 8:12] = f(inp["hgrn_lb"][0]).reshape(4, 128).T
    small[:, 12:16] = f(inp["hgrn_lb"][1]).reshape(4, 128).T
    pmix = np.ascontiguousarray(f(inp["w_pool_mix"][0]).transpose(1, 0, 2)).reshape(128, 512)
    xp = f(inp["x_prompt"]); xsm = f(inp["x_sample"])
    ppr = f(inp["p_prompt"][0]); psm = f(inp["p_sample"][0])
    spl = f(inp["state_pool"][0]); shg = f(inp["state_hgrn"][0])
    maps = []
    for c in range(NCORES):
        maps.append({
            "xp": xp[c], "xs": xsm[16 * c:16 * c + 16].reshape(128, 1024),
            "pp": ppr[c], "ps": psm[16 * c:16 * c + 16].reshape(128, 256),
            "spool": spl[16 * c:16 * c + 16].reshape(240, 512),
            "shg": shg[16 * c:16 * c + 16],
            "wall": wallv, "cst": CONST_ARR, "gvec": gvec, "small": small, "pmix": pmix,
        })
    return maps


def kernel(**inputs):
    if "nc" not in _PROG:
        _PROG["nc"] = build_program()
    nc = _PROG["nc"]
    maps = _prep_inputs(inputs)
    res = run_bass_kernel_spmd(nc, maps, core_ids=list(range(NCORES)))
    R = res.results
    y_p = np.stack([R[c]["yp"] for c in range(NCORES)], 0).astype(np.float32)
    y_s = np.concatenate([R[c]["ys"].reshape(16, 8, 1024) for c in range(NCORES)], 0).astype(np.float32)
    npp = np.stack([R[c]["npp"] for c in range(NCORES)], 0)[None].astype(np.float32)
    nhp = np.stack([R[c]["nhp"] for c in range(NCORES)], 0)[None].astype(np.float32)
    nps = np.concatenate([R[c]["nps"].reshape(16, 15, 512) for c in range(NCORES)], 0)[None].astype(np.float32)
    nhs = np.concatenate([R[c]["nhs"] for c in range(NCORES)], 0)[None].astype(np.float32)
    return (y_p, y_s, npp, nhp, nps, nhs)
```

```python
import numpy as np
from contextlib import ExitStack
import concourse.bass as bass
import concourse.mybir as mybir
from concourse.bass_utils import run_bass_kernel_spmd

F32 = mybir.dt.float32
BF16 = mybir.dt.bfloat16
AF = mybir.ActivationFunctionType
ALU = mybir.AluOpType

ENGS = ("pe", "act", "dve", "pool", "sp")
EPS = 1e-6
NCORES = 8
NRING = 5
NTILES = 4
import os as _os
SAME_DIST = int(_os.environ.get("SAME_DIST", str(1 << 30)))


class Prog:
    def __init__(self, nc, stack, same_engine_wait=True):
        self.nc = nc
        self.stack = stack
        self.streams = {e: [] for e in ENGS}
        self.count = {e: 0 for e in ENGS}
        self.known = {e: {} for e in ENGS}
        self.sems = {}
        self.dma_count = {}
        self.lastw = {}
        self.readers = {}
        self.same_engine_wait = same_engine_wait

    def _collect(self, eng, reads, writes):
        deps = []
        for k in reads:
            ev = self.lastw.get(k)
            if ev is not None:
                deps.append(ev)
        for k in writes:
            ev = self.lastw.get(k)
            if ev is not None:
                deps.append(ev)
            rd = self.readers.get(k)
            if rd:
                deps.extend(rd.values())
        kn = self.known[eng]
        need = {}
        for (s, v, vc) in deps:
            if s == eng and (eng == "pe" or not self.same_engine_wait or self.count[eng] - v >= SAME_DIST):
                continue
            if kn.get(s, 0) >= v:
                continue
            if need.get(s, 0) < v:
                need[s] = v
            for s2, v2 in vc.items():
                if s2 == eng:
                    continue
                if kn.get(s2, 0) < v2:
                    kn[s2] = v2
        waits = []
        for s, v in need.items():
            waits.append((s, v))
            if kn.get(s, 0) < v:
                kn[s] = v
        return waits

    def _record(self, ev, reads, writes):
        s = ev[0]
        for k in reads:
            self.readers.setdefault(k, {})[s] = ev
        for k in writes:
            self.lastw[k] = ev
            self.readers[k] = {}

    def op(self, eng, fn, reads=(), writes=()):
        waits = self._collect(eng, reads, writes)
        self.count[eng] += 1
        n = self.count[eng]
        vc = dict(self.known[eng])
        vc[eng] = n
        ev = (eng, n, vc)
        self.streams[eng].append((waits, fn, (eng, 1)))
        self._record(ev, reads, writes)
        return ev

    def dma(self, q, semname, fn, reads=(), writes=()):
        waits = self._collect(q, reads, writes)
        self.dma_count[semname] = self.dma_count.get(semname, 0) + 1
        v = 16 * self.dma_count[semname]
        vc = dict(self.known[q])
        vc[semname] = v
        ev = (semname, v, vc)
        self.streams[q].append((waits, fn, (semname, 16)))
        self._record(ev, reads, writes)
        return ev

    def barrier(self, engines=("act", "dve", "sp")):
        for e in engines:
            kn = self.known[e]
            waits = []
            for e2 in ("pe", "act", "dve"):
                c = self.count[e2]
                if c and kn.get(e2, 0) < c:
                    waits.append((e2, c))
                    kn[e2] = c
            for s, c in self.dma_count.items():
                if s.startswith("ring") or s.startswith("d_y") or s.startswith("d_x"):
                    continue
                if kn.get(s, 0) < 16 * c:
                    waits.append((s, 16 * c))
                    kn[s] = 16 * c
            if waits:
                self.streams[e].append((waits, None, None))

    def finish(self, eng="sp"):
        kn = self.known[eng]
        waits = []
        for s, c in self.dma_count.items():
            if kn.get(s, 0) < 16 * c:
                waits.append((s, 16 * c))
                kn[s] = 16 * c
        for e in ("pe", "act", "dve"):
            if self.count[e] and kn.get(e, 0) < self.count[e]:
                waits.append((e, self.count[e]))
        self.streams[eng].append((waits, None, None))

    def emit(self):
        nc = self.nc
        for s in list(ENGS) + list(self.dma_count):
            if s not in self.sems:
                self.sems[s] = self.stack.enter_context(nc.semaphore(s))
        with nc.Block() as block:
            def run(engine, items):
                for waits, fn, inc in items:
                    for s, v in waits:
                        engine.wait_ge(self.sems[s], v)
                    if fn is not None:
                        ins = fn(engine)
                        ins.then_inc(self.sems[inc[0]], inc[1])

            @block.tensor
            def _(eng):
                run(eng, self.streams["pe"])

            @block.scalar
            def _(eng):
                run(eng, self.streams["act"])

            @block.vector
            def _(eng):
                run(eng, self.streams["dve"])

            @block.gpsimd
            def _(eng):
                run(eng, self.streams["pool"])

            @block.sync
            def _(eng):
                run(eng, self.streams["sp"])


NBLK = 33


def _kc(W, nk):
    C = W.shape[1]
    return np.ascontiguousarray(W.reshape(nk, 128, C).transpose(1, 0, 2)).reshape(128, nk * C)


def _pad(a):
    out = np.zeros((128, 4096), np.float32)
    out[:, : a.shape[1]] = a
    return out


def build_wall(w_in, w_pool_up, w_hgrn_up, w_out, w_g, w_u, w_d, w_pg, w_pp):
    blocks = []
    blocks.append(_kc(w_in[:, 0:512], 8))
    zz = w_in[:, 512:2560].reshape(8, 128, 4, 4, 128)
    ga = w_in[:, 2560:3584].reshape(8, 128, 8, 128)
    gb = w_in[:, 3584:4608].reshape(8, 128, 8, 128)
    for h in range(4):
        blocks.append(np.ascontiguousarray(zz[:, :, :, h, :].transpose(1, 0, 2, 3)).reshape(128, 4096))
        gg = np.stack([ga[:, :, 2 * h], gb[:, :, 2 * h], ga[:, :, 2 * h + 1], gb[:, :, 2 * h + 1]], axis=2)
        blocks.append(np.ascontiguousarray(gg.transpose(1, 0, 2, 3)).reshape(128, 4096))
    for half in range(2):
        yy = np.stack([w_pool_up[:, half * 512:(half + 1) * 512].reshape(4, 128, 512),
                       w_hgrn_up[:, half * 512:(half + 1) * 512].reshape(4, 128, 512)], axis=2)
        blocks.append(np.ascontiguousarray(yy.transpose(1, 0, 2, 3)).reshape(128, 4096))
    blocks.append(_kc(w_out[:, 0:512], 8))
    blocks.append(_kc(w_out[:, 512:1024], 8))
    for jb in range(11):
        gu = np.stack([w_g[:, jb * 256:(jb + 1) * 256], w_u[:, jb * 256:(jb + 1) * 256]], axis=1)
        blocks.append(_kc(gu.reshape(1024, 512), 8))
    for half in range(2):
        for kb in range(3):
            nk = 8 if kb < 2 else 6
            blocks.append(_pad(_kc(w_d[kb * 1024: kb * 1024 + nk * 128, half * 512:(half + 1) * 512], nk)))
    blocks.append(_kc(w_pg[:, 0:512], 8))
    blocks.append(_pad(_kc(w_pp, 2)))
    blocks.append(_kc(w_pg[:, 512:1024], 8))
    assert len(blocks) == NBLK
    return np.ascontiguousarray(np.stack(blocks, axis=0).astype(np.float32))


def build_consts():
    c = {}
    c["ident"] = np.eye(128, dtype=np.float32)
    s = np.arange(128)[:, None]
    t = np.arange(128)[None, :]
    c["maskP"] = ((s // 64 == t // 64) & (s <= t)).astype(np.float32)
    c["maskS"] = ((s // 8 == t // 8) & (s <= t)).astype(np.float32)
    c["seqm"] = (s // 8 == np.arange(16)[None, :]).astype(np.float32)
    r = np.ones(640, np.float32)
    r[0:512:64] = 0.0
    r[512:640:8] = 0.0
    c["resetm"] = np.broadcast_to(r, (128, 640)).copy()
    rc = np.zeros((4, 16), np.float32)
    for g, w in enumerate((2, 4, 8, 16)):
        rc[g] = 1.0 / np.minimum(np.arange(16) + 1, w)
    c["rc"] = np.broadcast_to(rc.reshape(1, 64), (128, 64)).copy()
    order = ["ident", "seqm", "rc", "maskP", "maskS", "resetm"]
    offs = {}
    o = 0
    for k in order:
        offs[k] = (o, c[k].shape[1])
        o += c[k].shape[1]
    return np.ascontiguousarray(np.concatenate([c[k] for k in order], axis=1)), offs


CONST_ARR, COFF = build_consts()
CW = CONST_ARR.shape[1]
CW_KEEP = COFF["maskP"][0]


class _Stop(Exception):
    pass


STOP = None


def ck(name):
    if STOP == name:
        raise _Stop()


def build_program(ntiles=NTILES):
    nc = bass.Bass("TRN2", target_bir_lowering=False)

    def din(name, shape):
        return nc.dram_tensor(name, shape, F32, kind="ExternalInput").ap()

    def dout(name, shape):
        return nc.dram_tensor(name, shape, F32, kind="ExternalOutput").ap()

    xp = din("xp", [2048, 1024])
    xs = din("xs", [128, 1024])
    ppd = din("pp", [2048, 256])
    psd = din("ps", [128, 256])
    spool = din("spool", [240, 512])
    shg = din("shg", [16, 4, 128, 128])
    wall = din("wall", [NBLK, 128, 4096])
    cst = din("cst", [128, CW])
    gvec = din("gvec", [4, 1024])
    smalld = din("small", [128, 16])
    pmixd = din("pmix", [128, 512])
    yp = dout("yp", [2048, 1024])
    ys = dout("ys", [128, 1024])
    npp = dout("npp", [15, 512])
    nhp = dout("nhp", [4, 128, 128])
    nps = dout("nps", [240, 512])
    nhs = dout("nhs", [16, 4, 128, 128])

    with ExitStack() as st:
        P = Prog(nc, st)

        def sb(name, shape, dt):
            return st.enter_context(nc.sbuf_tensor("sb_" + name, shape, dt))

        xres = sb("xres", [128, 5, 1024], F32)
        hT = sb("hT", [128, 8, 640], BF16)
        hn = [sb(f"hn{i}", [128, 1024], BF16) for i in range(2)]
        junk = sb("junk", [128, 512], BF16)
        ystage = [sb(f"ystage{i}", [128, 512], F32) for i in range(4)]
        resetmb = sb("resetmb", [128, 640], BF16)
        ring = [sb(f"ring{i}", [128, 4096], BF16) for i in range(NRING)]
        gB = sb("gB", [128, 4, 1024], F32)
        cs = sb("cs", [128, CW_KEEP], F32)
        identb = sb("identb", [128, 128], BF16)
        onesd = sb("onesd", [128, 128], BF16)
        maskPb = sb("maskPb", [128, 128], BF16)
        maskSb = sb("maskSb", [128, 128], BF16)
        pmixf = sb("pmixf", [128, 512], F32)
        pmix = sb("pmix", [128, 4, 128], BF16)
        small = sb("small", [128, 16], F32)
        lbv = sb("lbv", [128, 8], F32)
        carry = sb("carry", [128, 4, 128], F32)
        ucarry = sb("ucarry", [128, 4, 16], F32)
        ssq = sb("ssq", [128, 16], F32)
        ssq1 = sb("ssq1", [128, 8], F32)
        dmy = sb("dmy", [128, 2], F32)
        rst1 = sb("rst1", [128, 4], F32)
        rst = sb("rst", [128, 8], F32)
        identf = cs[:, COFF["ident"][0]:COFF["ident"][0] + 128]
        seqm = cs[:, COFF["seqm"][0]:COFF["seqm"][0] + 16]
        rcv = cs[:, COFF["rc"][0]:COFF["rc"][0] + 64]
        resetm = resetmb

        MIX_BYTES = 0
        scr_plan = {}

        def plan(phase, name, nelem, dt):
            nonlocal MIX_BYTES
            nbytes = nelem * (4 if dt == F32 else 2)
            nbytes = (nbytes + 31) // 32 * 32
            off = scr_plan.setdefault(("off", phase), 0)
            scr_plan[name] = (off, nelem, dt)
            scr_plan[("off", phase)] = off + nbytes

        U_NAMES = ["ug", "ue", "pa", "pb_", "sa", "sb_", "tmp16", "pooled", "npsb", "stg", "npp_sb"]
        for name, n, dt in [
            ("ug", 528, F32), ("ue", 16 * 24, F32), ("pa", 528, F32), ("pb_", 528, F32),
            ("sa", 16 * 24, F32), ("sb_", 16 * 24, F32), ("tmp16", 16, F32),
            ("pooled", 4 * 640, BF16), ("npsb", 4 * 240, F32),
            ("stg", 2 * 512, F32), ("npp_sb", 512, F32), ("pool_out", 4 * 640, BF16),
            ("qf", 640, F32), ("t1", 640, F32), ("t2", 640, F32), ("t3", 640, F32), ("t4", 640, F32),
            ("qe", 640, BF16), ("qe2", 640, BF16), ("keb", 640, BF16), ("kdb", 640, BF16), ("vb", 640, BF16), ("gbf", 640, BF16), ("gbf2", 640, BF16),
            ("kdT", 5 * 128, BF16), ("vT", 5 * 128, BF16), ("Sall", 9 * 128, F32), ("Sbf", 8 * 128, BF16),
            ("dec", 24, F32), ("dec2", 24, F32), ("Am", 5 * 128, BF16), ("osq", 640, BF16), ("o_fin", 4 * 640, BF16),
            ("S0", 16 * 128, F32), ("S0bf", 16 * 128, BF16), ("Vm", 16 * 128, BF16),
            ("merged", 8 * 640, BF16), ("s1", 640, F32), ("s2", 640, F32),
        ]:
            plan("mix", name, n, dt)
        for name, n, dt in [
            ("hidden", 22 * 640, BF16), ("ftmp", 640, F32), ("sgt0", 512, F32), ("sgt1", 512, F32),
            ("pT", 2 * 640, BF16), ("pstage", 5 * 256, F32), ("pbf", 256, BF16),
        ]:
            plan("ffn", name, n, dt)
        u_end = scr_plan["pool_out"][0]
        assert 2 * 8 * 640 * 2 <= u_end, u_end
        scr_plan["sgA"] = (0, 8 * 640, BF16)
        scr_plan["sgB"] = (8 * 640 * 2, 8 * 640, BF16)
        scr_bytes = max(scr_plan[("off", "mix")], scr_plan[("off", "ffn")])
        scr = sb("scr", [128, scr_bytes // 4], F32)

        def sv(name):
            off, n, dt = scr_plan[name]
            if dt == F32:
                return scr[:, off // 4: off // 4 + n]
            return scr[:, off // 4: off // 4 + n // 2].bitcast(BF16)

        ug = sv("ug"); ue = sv("ue").rearrange("p (i r) -> p i r", r=24)
        pa = sv("pa"); pb_ = sv("pb_")
        sa = sv("sa").rearrange("p (i r) -> p i r", r=24); sb_ = sv("sb_").rearrange("p (i r) -> p i r", r=24)
        tmp16 = sv("tmp16")
        pooled = sv("pooled").rearrange("p (g t) -> p g t", g=4)
        pool_out = sv("pool_out").rearrange("p (g t) -> p g t", g=4)
        npsb = sv("npsb").rearrange("p (g r) -> p g r", g=4)
        stg = sv("stg").rearrange("p (h c) -> p h c", h=2)
        nps_sb = stg
        npp_sb = sv("npp_sb")
        qf = sv("qf"); t1 = sv("t1"); t2 = sv("t2"); t3 = sv("t3"); t4 = sv("t4")
        qe = sv("qe"); qe2 = sv("qe2"); keb = sv("keb"); kdb = sv("kdb"); vb = sv("vb"); gbf = sv("gbf"); gbf2 = sv("gbf2")
        kdT = sv("kdT").rearrange("p (b k) -> p b k", b=5); vT = sv("vT").rearrange("p (b k) -> p b k", b=5)
        Sall = sv("Sall").rearrange("p (c v) -> p c v", c=9); Sbf = sv("Sbf").rearrange("p (c v) -> p c v", c=8)
        dec = sv("dec"); dec2 = sv("dec2"); Am = sv("Am").rearrange("p (b k) -> p b k", b=5); osq = sv("osq")
        o_fin = sv("o_fin").rearrange("p (h t) -> p h t", h=4)
        S0 = sv("S0").rearrange("p (i v) -> p i v", i=16); S0bf = sv("S0bf").rearrange("p (i v) -> p i v", i=16)
        Vm = sv("Vm").rearrange("p (i v) -> p i v", i=16)
        merged = sv("merged").rearrange("p (k t) -> p k t", k=8); s1 = sv("s1"); s2 = sv("s2")
        sgA = sv("sgA").rearrange("p (k t) -> p k t", k=8); sgB = sv("sgB").rearrange("p (k t) -> p k t", k=8)
        hidden = sv("hidden").rearrange("p (f t) -> p f t", f=22); ftmp = sv("ftmp")
        assert scr_plan["S0bf"][0] == scr_plan["S0"][0] + 8192 and scr_plan["Vm"][0] == scr_plan["S0"][0] + 12288
        assert scr_plan["S0"][0] >= scr_plan[("off", "ffn")]
        xo = scr_plan["S0"][0] // 4
        xalt = scr[:, xo:xo + 4096].rearrange("p (b d) -> p b d", b=4)
        sgt = [sv("sgt0"), sv("sgt1")]
        pT = sv("pT").rearrange("p (k t) -> p k t", k=2); pstage = sv("pstage").rearrange("p (b c) -> p b c", b=5); pbf = sv("pbf")

        pbs = [st.enter_context(nc.psum_tensor(f"pb{i}", [128, 512], F32)) for i in range(8)]
        bank_ctr = [0]

        def nb():
            b = bank_ctr[0] % 8
            bank_ctr[0] += 1
            return b

        def PB(b):
            return ("pb", b)

        wstate = {"tile": 0, "issued": set(), "released": set(), "total": NBLK * ntiles}

        def w_issue_n(n, extra_reads=()):
            if n >= wstate["total"] or n in wstate["issued"]:
                return
            slot = n % NRING
            blk = n % NBLK
            P.dma("pool", f"ring{slot}",
                  lambda e, slot=slot, blk=blk: e.dma_start(out=ring[slot][:], in_=wall[blk]),
                  reads=list(extra_reads), writes=[("ring", slot)])
            wstate["issued"].add(n)

        def w_issue(extra_reads=()):
            w_issue_n(len(wstate["issued"]), extra_reads)

        def w_get(expect):
            n = wstate["tile"] * NBLK + expect
            assert n in wstate["issued"], ("ring too small / block not prefetched", n)
            assert n - NRING < 0 or (n - NRING) in wstate["released"]
            slot = n % NRING
            return ring[slot], ("ring", slot)

        def w_release(expect):
            n = wstate["tile"] * NBLK + expect
            assert n in wstate["issued"] and n not in wstate["released"]
            wstate["released"].add(n)
            w_issue_n(n + NRING)

        def A(eng, fn, reads, writes):
            return P.op(eng, fn, reads=reads, writes=writes)

        def mm(out, lhsT, rhs, start, stop, reads, bank):
            A("pe", lambda e: e.matmul(out, lhsT=lhsT, rhs=rhs, start=start, stop=stop), reads, [PB(bank)])

        def tp(out, in_, ident, reads, bank):
            A("pe", lambda e: e.transpose(out=out, in_=in_, identity=ident), reads, [PB(bank)])

        def act(out, in_, func, reads, writes, **kw):
            A("act", lambda e: e.activation(out=out, in_=in_, func=func, **kw), reads, writes)

        def tt(out, in0, in1, op, reads, writes, eng="dve"):
            A(eng, lambda e: e.tensor_tensor(out=out, in0=in0, in1=in1, op=op), reads, writes)

        def ts(out, in0, s1_, s2_, op0, op1, reads, writes):
            A("dve", lambda e: e.tensor_scalar(out=out, in0=in0, scalar1=s1_, scalar2=s2_, op0=op0, op1=op1), reads, writes)

        def stt(out, in0, scalar, in1, op0, op1, reads, writes):
            A("dve", lambda e: e.scalar_tensor_tensor(out=out, in0=in0, scalar=scalar, in1=in1, op0=op0, op1=op1), reads, writes)

        def cp(out, in_, reads, writes, eng="dve"):
            if eng == "act":
                act(out, in_, AF.Copy, reads, writes)
            else:
                A(eng, lambda e: e.tensor_copy(out=out, in_=in_), reads, writes)

        def warm(func):
            act(dmy[:, 1:2], dmy[:, 0:1], func, ["dmy0"], ["dmy1"])

        A("dve", lambda e: e.memset(dmy[:], 1.0), [], ["dmy0", "dmy1"])
        P.dma("sp", "d_cs", lambda e: e.dma_start(out=cs[:], in_=cst[:, 0:CW_KEEP]), writes=["cs"])
        cstage = scr[:, 0:CW - CW_KEEP]
        P.dma("sp", "d_cs2", lambda e: e.dma_start(out=cstage, in_=cst[:, CW_KEEP:CW]), writes=["cstage"])
        maskPf = cstage[:, 0:128]
        maskSf = cstage[:, 128:256]
        resetmf = cstage[:, 256:896]
        P.dma("sp", "d_small", lambda e: e.dma_start(out=small[:], in_=smalld), writes=["small"])
        P.dma("sp", "d_pmix", lambda e: e.dma_start(out=pmixf[:], in_=pmixd), writes=["pmixf"])
        for gi in range(4):
            P.dma("sp", f"d_g{gi}",
                  lambda e, gi=gi: e.dma_start(out=gB[:, gi, :], in_=gvec[gi:gi + 1, :].broadcast_to([128, 1024])),
                  writes=[("gB", gi)])
        cp(identb[:], identf, ["cs"], ["identb"])
        cp(maskPb[:], maskPf, ["cstage"], ["maskPb"])
        cp(maskSb[:], maskSf, ["cstage"], ["maskSb"])
        cp(resetmb[:], resetmf, ["cstage"], ["resetmb"])
        cp(pmix[:].rearrange("p g c -> p (g c)"), pmixf[:], ["pmixf"], ["pmix"])
        A("dve", lambda e: e.memset(onesd[:], 1.0 / 128.0), [], ["onesd"])
        A("dve", lambda e: e.memset(carry[:].rearrange("p h v -> p (h v)"), 0.0), [], ["carry"])
        A("dve", lambda e: e.memset(ucarry[:].rearrange("p g t -> p (g t)"), 0.0), [], ["ucarry"])
        tt(lbv[:, 0:4], small[:, 8:12], small[:, 12:16], ALU.subtract, ["small"], ["lbv"])
        act(lbv[:, 0:4], lbv[:, 0:4], AF.Sigmoid, ["lbv"], ["lbv"])
        ts(lbv[:, 4:8], lbv[:, 0:4], -1.0, 1.0, ALU.mult, ALU.add, ["lbv"], ["lbv"])

        def run_tile(ti):
            wstate["tile"] = ti
            has_s = ti == 0
            last = ti == 3
            NT = 640 if has_s else 512
            blocks = [0, 1, 2, 3] + ([4] if has_s else [])
            subs = [(0, 512, "p")] + ([(512, 128, "s")] if has_s else [])
            hTk_p = [("hT", b) for b in range(4)]
            hTk = {"p": hTk_p, "s": [("hT", 4)]}

            def K(name):
                return [(name, "p")] + ([(name, "s")] if has_s else [])

            def xbuf(tj):
                return (xres, "xres") if tj % 2 == 0 else (xalt, "xalt")

            X, xk = xbuf(ti)

            def load_x(tj, b):
                src = xp[tj * 512 + b * 128: tj * 512 + (b + 1) * 128, :] if b < 4 else xs
                Xn, xkn = xbuf(tj)
                P.dma("sp", f"d_x{tj % 2}_{b}", lambda e, b=b, src=src, Xn=Xn: e.dma_start(out=Xn[:, b, :], in_=src),
                      writes=[(xkn, b)])

            if ti == 0:
                for b in blocks:
                    load_x(0, b)
            if ti == 0:
                for _ in range(NRING):
                    w_issue(extra_reads=[(xk, b) for b in blocks])
            if has_s:
                for half in range(2):
                    P.dma("sp", f"d_stg{half}",
                          lambda e, half=half: e.dma_start(out=stg[0:120, half, :], in_=spool[half * 120:(half + 1) * 120, :]),
                          writes=[("stg", half)])

            def half_stats(b, half, X_=None, xk_=None, ssq_=None, tag="ssq"):
                X_ = X if X_ is None else X_
                xk_ = xk if xk_ is None else xk_
                ssq_ = ssq if ssq_ is None else ssq_
                act(junk[:, 0:512], X_[:, b, half * 512:(half + 1) * 512], AF.Square, [(xk_, b)], ["junk", (tag, b, half)],
                    accum_out=ssq_[:, 2 * b + half:2 * b + half + 1])

            def norm_stats(have_partials, blks=None, X_=None, xk_=None, ssq_=None, rst_=None, tag="ssq", rtag="rst"):
                blks = blocks if blks is None else blks
                ssq_ = ssq if ssq_ is None else ssq_
                rst_ = rst if rst_ is None else rst_
                if not have_partials:
                    for b in blks:
                        for half in range(2):
                            half_stats(b, half, X_, xk_, ssq_, tag)
                nbk = len(blks)
                tt(rst_[:, 0:nbk], ssq_[:, 0:2 * nbk:2], ssq_[:, 1:2 * nbk:2], ALU.add,
                   [(tag, b, hf) for b in blks for hf in range(2)], [rtag])
                act(rst_[:, 0:nbk], rst_[:, 0:nbk], AF.Ln, [rtag], [rtag], scale=1.0 / 1024.0, bias=EPS)
                act(rst_[:, 0:nbk], rst_[:, 0:nbk], AF.Exp, [rtag], [rtag], scale=-0.5)

            def norm_apply(gi, blks=None, X_=None, xk_=None, rst_=None, rtag="rst"):
                blks = blocks if blks is None else blks
                X_ = X if X_ is None else X_
                xk_ = xk if xk_ is None else xk_
                rst_ = rst if rst_ is None else rst_
                for b in blks:
                    hb = hn[b % 2]
                    hk = f"hn{b % 2}"
                    stt(hb[:], X_[:, b, :], rst_[:, b:b + 1], gB[:, gi, :], ALU.mult, ALU.mult,
                        [(xk_, b), rtag, ("gB", gi)], [hk])
                    bank = nb()
                    bv = pbs[bank][:].bitcast(BF16)
                    for k in range(8):
                        tp(bv[:, k * 128:(k + 1) * 128], hb[:, k * 128:(k + 1) * 128], identb[:], [hk, "identb"], bank)
                    c0 = b * 128
                    cp(hT[:, :, c0:c0 + 128], bv.rearrange("p (k t) -> p k t", k=8), [PB(bank)], [("hT", b)], eng="act")

            def norm_T(gi, have_partials):
                norm_stats(have_partials)
                norm_apply(gi)

            ck("load")
            if ti == 0:
                norm_T(0, False)
            ck("norm1")

            wv, wk = w_get(0)
            wu = wv[:].rearrange("p (k c) -> p k c", k=8)
            ubanks = []
            for g in range(4):
                bp = nb()
                bs = nb() if has_s else None
                for k in range(8):
                    mm(pbs[bp][:, 0:512], wu[:, k, g * 128:(g + 1) * 128], hT[:, k, 0:512], k == 0, k == 7, [wk] + hTk_p, bp)
                    if has_s:
                        mm(pbs[bs][:, 0:128], wu[:, k, g * 128:(g + 1) * 128], hT[:, k, 512:640], k == 0, k == 7, [wk, ("hT", 4)], bs)
                ubanks.append((bp, bs))
                if g == 3:
                    w_release(0)
                w = 2 << g
                cp(ug[:, 0:16], ucarry[:, g, :], ["ucarry"], ["ug"])
                cp(ug[:, 16:528], pbs[bp][:, 0:512], [PB(bp)], ["ug"], eng="act")
                tt(pa[:, 1:528], ug[:, 1:528], ug[:, 0:527], ALU.add, ["ug"], ["pa"])
                sw = pa
                swk = "pa"
                if w >= 4:
                    tt(pb_[:, 3:528], pa[:, 3:528], pa[:, 1:526], ALU.add, ["pa"], ["pb_"])
                    sw, swk = pb_, "pb_"
                if w >= 8:
                    tt(pa[:, 7:528], pb_[:, 7:528], pb_[:, 3:524], ALU.add, ["pb_"], ["pa"])
                    sw, swk = pa, "pa"
                if w >= 16:
                    tt(pb_[:, 15:528], pa[:, 15:528], pa[:, 7:520], ALU.add, ["pa"], ["pb_"])
                    sw, swk = pb_, "pb_"
                stt(pooled[:, g, 0:512], sw[:, 16:528], 1.0 / w, ug[:, 16:528], ALU.mult, ALU.subtract,
                    [swk, "ug"], [("pooled", g, "p")])
                if ti == 0:
                    tt(tmp16[:], sw[:, 16:32], rcv[:, g * 16:(g + 1) * 16], ALU.mult, [swk, "cs"], ["tmp16"])
                    tt(pooled[:, g, 0:16], tmp16[:], ug[:, 16:32], ALU.subtract, ["tmp16", "ug"], [("pooled", g, "p")])
                cp(ucarry[:, g, :], ug[:, 512:528], ["ug"], ["ucarry"])
                if last:
                    bt = nb()
                    tp(pbs[bt][0:15, 0:128], ug[:, 513:528], identf, ["ug", "cs"], bt)
                    cp(npp_sb[0:15, g * 128:(g + 1) * 128], pbs[bt][0:15, 0:128], [PB(bt)], ["npp_sb"], eng="act")
                if has_s:
                    bt = nb()
                    for half in range(2):
                        tp(pbs[bt][:, half * 120:(half + 1) * 120], stg[0:120, half, g * 128:(g + 1) * 128],
                           identf[0:120, 0:120], [("stg", half), "cs"], bt)
                    cp(ue[:, :, 0:15], pbs[bt][:, 0:240].rearrange("p (i r) -> p i r", r=15), [PB(bt)], ["ue"], eng="act")
                    cp(ue[:, :, 15:23], pbs[bs][:, 0:128].rearrange("p (i t) -> p i t", t=8), [PB(bs)], ["ue"], eng="act")
                    tt(sa[:, :, 1:23], ue[:, :, 1:23], ue[:, :, 0:22], ALU.add, ["ue"], ["sa"])
                    ssw, sswk = sa, "sa"
                    if w >= 4:
                        tt(sb_[:, :, 3:23], sa[:, :, 3:23], sa[:, :, 1:21], ALU.add, ["sa"], ["sb_"])
                        ssw, sswk = sb_, "sb_"
                    if w >= 8:
                        tt(sa[:, :, 7:23], sb_[:, :, 7:23], sb_[:, :, 3:19], ALU.add, ["sb_"], ["sa"])
                        ssw, sswk = sa, "sa"
                    if w >= 16:
                        tt(sb_[:, :, 15:23], sa[:, :, 15:23], sa[:, :, 7:15], ALU.add, ["sa"], ["sb_"])
                        ssw, sswk = sb_, "sb_"
                    stt(pooled[:, g, 512:640].rearrange("p (i t) -> p i t", t=8), ssw[:, :, 15:23], 1.0 / w,
                        ue[:, :, 15:23], ALU.mult, ALU.subtract, [sswk, "ue"], [("pooled", g, "s")])
                    cp(npsb[:, g, :].rearrange("p (i r) -> p i r", r=15), ue[:, :, 8:23], ["ue"], [("npsb", g)])
            if last:
                P.dma("sp", "d_npp", lambda e: e.dma_start(out=npp, in_=npp_sb[0:15, :]), reads=["npp_sb"])
            if has_s:
                for half in range(2):
                    bt = nb()
                    for g in range(4):
                        tp(pbs[bt][0:120, g * 128:(g + 1) * 128], npsb[:, g, half * 120:(half + 1) * 120], identf,
                           [("npsb", g), "cs"], bt)
                    cp(nps_sb[0:120, half, :], pbs[bt][0:120, 0:512], [PB(bt)], [("stg", half)], eng="act")
                    P.dma("sp", f"d_nps{half}",
                          lambda e, half=half: e.dma_start(out=nps[half * 120:(half + 1) * 120, :], in_=nps_sb[0:120, half, :]),
                          reads=[("stg", half)])
            ck("u")
            PIPE = not has_s
            if PIPE:
                zc, rcn = [0], [0]

                def nbZ():
                    b = zc[0] % 4
                    zc[0] += 1
                    return b

                def nbR():
                    b = 6 + rcn[0] % 2
                    rcn[0] += 1
                    return b

                gcn = [0]

                def nbG():
                    b = 4 + gcn[0] % 2
                    gcn[0] += 1
                    return b
            else:
                nbZ = nbR = nbG = nb
            gb2 = [gbf, gbf2]
            QE = [qe, qe2]
            DEC = [dec, dec2]
            HS = {h: {} for h in range(4)}

            def poolmix():
                for g in range(4):
                    for (c0, n, sk) in subs:
                        bk = nbR()
                        mm(pbs[bk][:, 0:n], pmix[:, g, :], pooled[:, g, c0:c0 + n], True, True, ["pmix", ("pooled", g, sk)], bk)
                        act(pool_out[:, g, c0:c0 + n], pbs[bk][:, 0:n], AF.Copy, [PB(bk), "small"], [("pool_out", g, sk)],
                            scale=small[:, g:g + 1])

            def z_group(h, j):
                S_ = HS[h]
                if j == 0:
                    wv, wk = w_get(1 + 2 * h)
                    S_["wh"] = wv[:].rearrange("p (k j c) -> p k j c", k=8, j=4)
                    S_["wk"] = wk
                    S_["bs"] = nbZ() if has_s else None
                    S_["zb"] = []
                wh, wk, bs = S_["wh"], S_["wk"], S_["bs"]
                bp = nbZ()
                for k in range(8):
                    mm(pbs[bp][:, 0:512], wh[:, k, j, :], hT[:, k, 0:512], k == 0, k == 7, [wk] + hTk_p, bp)
                    if has_s:
                        mm(pbs[bs][:, j * 128:(j + 1) * 128], wh[:, k, j, :], hT[:, k, 512:640], k == 0, k == 7,
                           [wk, ("hT", 4)], bs)
                S_["zb"].append(bp)
                if j == 3:
                    w_release(1 + 2 * h)

            def z_evac(h):
                S_ = HS[h]
                zb, bs = S_["zb"], S_["bs"]
                gbh = gb2[h % 2]
                gbk = f"gbf{h % 2}"

                def zsrc(j, sk):
                    return (pbs[zb[j]][:, 0:512], PB(zb[j])) if sk == "p" else (pbs[bs][:, j * 128:(j + 1) * 128], PB(bs))

                for (c0, n, sk) in subs:
                    src, key = zsrc(1, sk)
                    act(t1[:, c0:c0 + n], src, AF.Sigmoid, [key], [("t1", sk)])
                    src, key = zsrc(0, sk)
                    act(qf[:, c0:c0 + n], src, AF.Sigmoid, [key], [("qf", sk)])
                    tt(qf[:, c0:c0 + n], qf[:, c0:c0 + n], src, ALU.mult, [("qf", sk), key], [("qf", sk)])
                    src, key = zsrc(3, sk)
                    act(t4[:, c0:c0 + n], src, AF.Sigmoid, [key], [("t4", sk)])
                    tt(gbh[:, c0:c0 + n], t4[:, c0:c0 + n], src, ALU.mult, [("t4", sk), key], [(gbk, sk)])
                    src, key = zsrc(2, sk)
                    cp(vb[:, c0:c0 + n], src, [key], [("vb", sk)])
                warm(AF.Ln)

            def g_mm(h, jj, tsel=(0, 1)):
                S_ = HS[h]
                if jj == 0 and tsel[0] == 0:
                    wgv_, wgk_ = w_get(2 + 2 * h)
                    S_["wgv"] = wgv_[:].rearrange("p (k t c) -> p k t c", k=8, t=4)
                    S_["wgk"] = wgk_
                    S_["gate_ev"] = []
                wgv, wgk_ = S_["wgv"], S_["wgk"]
                j = 2 * h + jj
                if tsel[0] == 0:
                    S_["bsg"] = nbG() if has_s else None
                bsg = S_["bsg"]
                for t_ in tsel:
                    bpg = nbG()
                    for k in range(8):
                        mm(pbs[bpg][:, 0:512], wgv[:, k, 2 * jj + t_, :], hT[:, k, 0:512], k == 0, k == 7, [wgk_] + hTk_p, bpg)
                        if has_s:
                            mm(pbs[bsg][:, t_ * 128:(t_ + 1) * 128], wgv[:, k, 2 * jj + t_, :], hT[:, k, 512:640], k == 0, k == 7,
                               [wgk_, ("hT", 4)], bsg)
                    S_["gate_ev"].append((j, t_, bpg, bsg))
                if jj == 1 and tsel[-1] == 1:
                    w_release(2 + 2 * h)

            def g_evac(h, jj):
                evs = HS[h]["gate_ev"][2 * jj:2 * jj + 2]
                if has_s:
                    for (j, t_, bpg, bsg) in evs:
                        dst = sgA if t_ == 0 else sgB
                        dk = "sgA" if t_ == 0 else "sgB"
                        cp(dst[:, j, 512:640], pbs[bsg][:, t_ * 128:(t_ + 1) * 128], [PB(bsg)], [(dk, j, "s")], eng="act")
                for (j, t_, bpg, bsg) in evs:
                    dst = sgA if t_ == 0 else sgB
                    dk = "sgA" if t_ == 0 else "sgB"
                    cp(dst[:, j, 0:512], pbs[bpg][:, 0:512], [PB(bpg)], [(dk, j, "p")], eng="act")

            def chain_a(h):
                ts(t1[:, 0:NT], t1[:, 0:NT], lbv[:, 4 + h:5 + h], lbv[:, h:h + 1], ALU.mult, ALU.add, K("t1") + ["lbv"], K("t1"))
                ts(t2[:, 0:NT], t1[:, 0:NT], -1.0, 1.0, ALU.mult, ALU.add, K("t1"), K("t2"))
                act(t1[:, 0:NT], t1[:, 0:NT], AF.Ln, K("t1"), K("t1"))

            def cb_scan(h):
                A("dve", lambda e: e.tensor_tensor_scan(out=t3[:, 0:NT], data0=resetm[:, 0:NT], data1=t1[:, 0:NT],
                                                        initial=0.0, op0=ALU.mult, op1=ALU.add),
                  K("t1") + ["resetmb"], K("t3"))

            def cb_exp(h):
                act(t1[:, 0:NT], t3[:, 0:NT], AF.Exp, K("t3"), K("t1"))
                act(t4[:, 0:NT], t3[:, 0:NT], AF.Exp, K("t3"), K("t4"), scale=-1.0)

            def cb_mul(h):
                qe_ = QE[h % 2]
                qk = f"qe{h % 2}"
                tt(qe_[:, 0:NT], qf[:, 0:NT], t1[:, 0:NT], ALU.mult, K("qf") + K("t1"), [(qk, "p")] + ([(qk, "s")] if has_s else []))
                tt(t2[:, 0:NT], t2[:, 0:NT], t4[:, 0:NT], ALU.mult, K("t2") + K("t4"), K("t2"))

            def cb_keb(h):
                cp(keb[:, 0:NT], t2[:, 0:NT], K("t2"), K("keb"), eng="act")

            def cb_dec(h):
                dec_ = DEC[h % 2]
                dk_ = f"dec{h % 2}"
                cp(dec_[:, 0:8], t1[:, 63:512:64], [("t1", "p")], [(dk_, "p")])
                tt(kdb[:, 0:512].rearrange("p (c t) -> p c t", t=64), t2[:, 0:512].rearrange("p (c t) -> p c t", t=64),
                   dec_[:, 0:8].unsqueeze(2).broadcast_to([128, 8, 64]), ALU.mult, [("t2", "p"), (dk_, "p")], [("kdb", "p")])
                if has_s:
                    cp(dec_[:, 8:24], t1[:, 519:640:8], [("t1", "s")], [(dk_, "s")])
                    tt(kdb[:, 512:640].rearrange("p (c t) -> p c t", t=8), t2[:, 512:640].rearrange("p (c t) -> p c t", t=8),
                       dec_[:, 8:24].unsqueeze(2).broadcast_to([128, 16, 8]), ALU.mult, [("t2", "s"), (dk_, "s")], [("kdb", "s")])

            def chain_b(h):
                cb_scan(h); cb_exp(h); cb_mul(h); cb_keb(h); cb_dec(h)

            def R1(h):
                qe = QE[h % 2]
                qk = f"qe{h % 2}"
                for (src_, srck, dst, dstk) in ((kdb, "kdb", kdT, "kdT"), (vb, "vb", vT, "vT")):
                    bt = nbR()
                    bv = pbs[bt][:].bitcast(BF16)
                    for b in blocks:
                        sk = "p" if b < 4 else "s"
                        tp(bv[:, b * 128:(b + 1) * 128], src_[:, b * 128:(b + 1) * 128], identb[:], [(srck, sk), "identb"], bt)
                    nbk = len(blocks)
                    cp(dst[:, 0:nbk, :], bv[:, 0:nbk * 128].rearrange("p (b k) -> p b k", k=128), [PB(bt)], [dstk], eng="act")
                ba = nbR()
                for b in range(4):
                    mm(pbs[ba][:, b * 128:(b + 1) * 128], keb[:, b * 128:(b + 1) * 128], qe[:, b * 128:(b + 1) * 128],
                       True, True, [("keb", "p"), (qk, "p")], ba)
                tt(Am[:, 0:4, :], pbs[ba][:, 0:512].rearrange("p (b t) -> p b t", b=4),
                   maskPb[:].unsqueeze(1).broadcast_to([128, 4, 128]), ALU.mult, [PB(ba), "maskPb"], [("Am", "p")])
                if has_s:
                    bas = nbR()
                    mm(pbs[bas][:, 0:128], keb[:, 512:640], qe[:, 512:640], True, True, [("keb", "s"), (qk, "s")], bas)
                    tt(Am[:, 4, :], pbs[bas][:, 0:128], maskSb[:], ALU.mult, [PB(bas), "maskSb"], [("Am", "s")])

            def build_vm():
                tt(Vm[:, :, :], vT[:, 4, :].unsqueeze(1).broadcast_to([128, 16, 128]),
                   seqm.unsqueeze(2).broadcast_to([128, 16, 128]), ALU.mult, ["vT", "cs"], ["Vm"])

            def sample_prefetch(h):
                P.dma("sp", "d_S0", lambda e, h=h: e.dma_start(out=S0[:, :, :], in_=shg[:, h, :, :].rearrange("i k v -> k i v")),
                      writes=["S0"])

            def sample_bf(h):
                cp(S0bf[:, :, :].rearrange("p i v -> p (i v)"), S0[:, :, :].rearrange("p i v -> p (i v)"), ["S0"], ["S0bf"], eng="act")

            def sample_state_update(h):
                dec = DEC[h % 2]
                dk_ = f"dec{h % 2}"
                tt(S0[:, :, :], S0[:, :, :], dec[:, 8:24].unsqueeze(2).broadcast_to([128, 16, 128]), ALU.mult,
                   ["S0", (dk_, "s")], ["S0"])
                for q4 in range(4):
                    bd = nbR()
                    mm(pbs[bd][:, 0:512], kdT[:, 4, :], Vm[:, 4 * q4:4 * q4 + 4, :].rearrange("p i v -> p (i v)"), True, True,
                       ["kdT", "Vm"], bd)
                    tt(S0[:, 4 * q4:4 * q4 + 4, :].rearrange("p i v -> p (i v)"),
                       S0[:, 4 * q4:4 * q4 + 4, :].rearrange("p i v -> p (i v)"), pbs[bd][:, 0:512], ALU.add,
                       ["S0", PB(bd)], ["S0"])
                P.dma("sp", "d_nhs", lambda e, h=h: e.dma_start(out=nhs[:, h, :, :].rearrange("i k v -> k i v"), in_=S0[:, :, :]),
                      reads=["S0"])

            def R2_mm(h):
                dsb = [nbR(), nbR()]
                HS[h]["dsb"] = dsb
                for c in range(8):
                    blk, half = c // 2, c % 2
                    bk = dsb[half]
                    mm(pbs[bk][:, blk * 128:(blk + 1) * 128], kdT[half * 64:(half + 1) * 64, blk, :],
                       vT[half * 64:(half + 1) * 64, blk, :], True, True, ["kdT", "vT"], bk)
                cp(Sall[:, 0, :], carry[:, h, :], ["carry"], [("Sall", 0)])

            def R2_steps(h, c0, c1):
                dec_ = DEC[h % 2]
                dk_ = f"dec{h % 2}"
                dsb = HS[h]["dsb"]
                for c in range(c0, c1):
                    blk, half = c // 2, c % 2
                    bk = dsb[half]
                    stt(Sall[:, c + 1, :], Sall[:, c, :], dec_[:, c:c + 1], pbs[bk][:, blk * 128:(blk + 1) * 128],
                        ALU.mult, ALU.add, [("Sall", c), (dk_, "p"), PB(bk)], [("Sall", c + 1)])

            def R2_half(h):
                cp(Sbf[:, 0:4, :].rearrange("p c v -> p (c v)"), Sall[:, 0:4, :].rearrange("p c v -> p (c v)"),
                   [("Sall", c) for c in range(4)], [("Sbf", 0)], eng="act")

            def R2_fin(h):
                cp(Sbf[:, 4:8, :].rearrange("p c v -> p (c v)"), Sall[:, 4:8, :].rearrange("p c v -> p (c v)"),
                   [("Sall", c) for c in range(4, 8)], [("Sbf", 1)], eng="act")
                cp(carry[:, h, :], Sall[:, 8, :], [("Sall", 8)], ["carry"])
                if last:
                    P.dma("sp", f"d_nhp{h}", lambda e, h=h: e.dma_start(out=nhp[h], in_=Sall[:, 8, :]), reads=[("Sall", 8)])

            def R2(h):
                R2_mm(h); R2_steps(h, 0, 4); R2_half(h); R2_steps(h, 4, 8); R2_fin(h)

            def R3(h):
                qe = QE[h % 2]
                qk = f"qe{h % 2}"
                S_ = HS[h]
                bo = nbR()
                for b in range(4):
                    mm(pbs[bo][:, b * 128:(b + 1) * 128], vT[:, b, :], Am[:, b, :], True, False, ["vT", ("Am", "p")], bo)
                    for half in range(2):
                        c = 2 * b + half
                        mm(pbs[bo][:, c * 64:(c + 1) * 64], Sbf[:, c, :], qe[:, c * 64:(c + 1) * 64], False, half == 1,
                           [("Sbf", c // 4), (qk, "p")], bo)
                bos = None
                if has_s:
                    bos = nbR()
                    mm(pbs[bos][:, 0:128], vT[:, 4, :], Am[:, 4, :], True, False, ["vT", ("Am", "s")], bos)
                    for i in range(16):
                        mm(pbs[bos][:, i * 8:(i + 1) * 8], S0bf[:, i, :], qe[:, 512 + i * 8:512 + (i + 1) * 8], False, i == 15,
                           ["S0bf", (qk, "s")], bos)
                S_["obanks"] = [(0, 512, "p", bo)] + ([(512, 128, "s", bos)] if has_s else [])
                for (c0, n, sk, bk) in S_["obanks"]:
                    act(osq[:, c0:c0 + n], pbs[bk][:, 0:n], AF.Square, [PB(bk)], [("osq", sk)])

            def R4(h):
                gbh = gb2[h % 2]
                gbk = f"gbf{h % 2}"
                for (c0, n, sk, bk) in HS[h]["obanks"]:
                    bn = nbR()
                    mm(pbs[bn][:, 0:n], onesd[:], osq[:, c0:c0 + n], True, True, ["onesd", ("osq", sk)], bn)
                    act(s1[:, c0:c0 + n], pbs[bn][:, 0:n], AF.Ln, [PB(bn)], [("s1", sk)], bias=EPS)
                    act(s1[:, c0:c0 + n], s1[:, c0:c0 + n], AF.Exp, [("s1", sk)], [("s1", sk)], scale=-0.5)
                    stt(s2[:, c0:c0 + n], pbs[bk][:, 0:n], small[:, 4 + h:5 + h], s1[:, c0:c0 + n], ALU.mult, ALU.mult,
                        [PB(bk), "small", ("s1", sk)], [("s2", sk)])
                tt(o_fin[:, h, 0:NT], s2[:, 0:NT], gbh[:, 0:NT], ALU.mult, K("s2") + [(gbk, "p"), (gbk, "s")],
                   [("o_fin", h, s_) for s_ in ("p", "s")])

            if PIPE:
                for j in range(4):
                    z_group(0, j)
            poolmix()
            ck("poolmix")
            P.barrier()
            ck("bar")
            if PIPE:
                for h in range(4):
                    p = h - 1
                    if h > 0:
                        R1(p)
                    z_evac(h)
                    g_mm(h, 0)
                    if h > 0:
                        R2_mm(p)
                    chain_a(h)
                    if h > 0:
                        R2_steps(p, 0, 4)
                        R2_half(p)
                    g_evac(h, 0)
                    if h < 3:
                        for j in range(4):
                            z_group(h + 1, j)
                    cb_scan(h)
                    if h > 0:
                        R2_steps(p, 4, 7)
                    cb_exp(h)
                    cb_mul(h)
                    cb_dec(h)
                    cb_keb(h)
                    if h > 0:
                        R2_steps(p, 7, 8)
                        R2_fin(p)
                        R3(p)
                    g_mm(h, 1, (0,))
                    if h > 0:
                        R4(p)
                    g_mm(h, 1, (1,))
                    g_evac(h, 1)
                    warm(AF.Sigmoid)
                R1(3); R2(3); R3(3); R4(3)
                warm(AF.Sigmoid)
            else:
                for h in range(4):
                    sample_prefetch(h)
                    for j in range(4):
                        z_group(h, j)
                    z_evac(h)
                    g_mm(h, 0)
                    g_mm(h, 1)
                    chain_a(h)
                    g_evac(h, 0)
                    chain_b(h)
                    R1(h)
                    g_evac(h, 1)
                    sample_bf(h)
                    R2_mm(h)
                    R2_steps(h, 0, 4)
                    R2_half(h)
                    R2_steps(h, 4, 8)
                    build_vm()
                    R2_fin(h)
                    R3(h); R4(h)
                    warm(AF.Sigmoid)
                    sample_state_update(h)

            ck("hgrn")
            for jh in range(2):
                wy, wyk = w_get(9 + jh)
                wyv = wy[:].rearrange("p (k t c) -> p k t c", k=4, t=2)
                for jj in range(4):
                    j = jh * 4 + jj
                    bs = nb() if has_s else None
                    b_ya, b_yb = nb(), nb()
                    for t_, bk, srcb, srck in ((0, b_ya, pool_out, "pool_out"), (1, b_yb, o_fin, "o_fin")):
                        for kk in range(4):
                            for (c0, n, sk) in subs:
                                o = pbs[bk][:, 0:512] if sk == "p" else pbs[bs][:, t_ * 128:(t_ + 1) * 128]
                                mm(o, wyv[:, kk, t_, jj * 128:(jj + 1) * 128], srcb[:, kk, c0:c0 + n], kk == 0, kk == 3,
                                   [wyk, (srck, kk, sk)], bk if sk == "p" else bs)
                    if jj == 3:
                        w_release(9 + jh)
                    for (c0, n, sk) in subs:
                        oa = pbs[b_ya][:, 0:512] if sk == "p" else pbs[bs][:, 0:128]
                        ob = pbs[b_yb][:, 0:512] if sk == "p" else pbs[bs][:, 128:256]
                        ka = PB(b_ya) if sk == "p" else PB(bs)
                        kb_ = PB(b_yb) if sk == "p" else PB(bs)
                        act(s1[:, c0:c0 + n], sgA[:, j, c0:c0 + n], AF.Sigmoid, [("sgA", j, sk)], [("s1", sk)])
                        act(s2[:, c0:c0 + n], sgB[:, j, c0:c0 + n], AF.Sigmoid, [("sgB", j, sk)], [("s2", sk)])
                        tt(s1[:, c0:c0 + n], s1[:, c0:c0 + n], oa, ALU.mult, [("s1", sk), ka], [("s1", sk)])
                        tt(s2[:, c0:c0 + n], s2[:, c0:c0 + n], ob, ALU.mult, [("s2", sk), kb_], [("s2", sk)])
                    tt(merged[:, j, 0:NT], s1[:, 0:NT], s2[:, 0:NT], ALU.add, K("s1") + K("s2"),
                       [("merged", j, s_) for s_ in ("p", "s")])

            ck("merge")
            warm(AF.Ln)
            for half in range(2):
                wo, wok = w_get(11 + half)
                wov = wo[:].rearrange("p (k c) -> p k c", k=8)
                bks = {b: nb() for b in blocks}
                for k in range(8):
                    for b in blocks:
                        sk = "p" if b < 4 else "s"
                        mm(pbs[bks[b]][:, 0:512], merged[:, k, b * 128:(b + 1) * 128], wov[:, k, :], k == 0, k == 7,
                           [wok, ("merged", k, sk)], bks[b])
                w_release(11 + half)
                for b in blocks:
                    tt(X[:, b, half * 512:(half + 1) * 512], X[:, b, half * 512:(half + 1) * 512], pbs[bks[b]][:, 0:512],
                       ALU.add, [(xk, b), PB(bks[b])], [(xk, b)])
                    half_stats(b, half)

            ck("wout")
            norm_T(1, True)
            warm(AF.Sigmoid)
            P.barrier()
            if ti + 1 < ntiles:
                for b in range(4):
                    load_x(ti + 1, b)
            for b in blocks:
                src = ppd[ti * 512 + b * 128: ti * 512 + (b + 1) * 128, :] if b < 4 else psd
                P.dma("sp", f"d_p{b}", lambda e, src=src, b=b: e.dma_start(out=pstage[:, b, :], in_=src), writes=[("pstage", b)])
            for jb in range(11):
                wf, wfk = w_get(13 + jb)
                wfv = wf[:].rearrange("p (k j c) -> p k j c", k=8, j=2)
                for cc in range(2):
                    f = 2 * jb + cc
                    bs = nb() if has_s else None
                    b_g, b_u = nb(), nb()
                    for jx, bk in ((0, b_g), (1, b_u)):
                        for k in range(8):
                            for (c0, n, sk) in subs:
                                o = pbs[bk][:, 0:512] if sk == "p" else pbs[bs][:, jx * 128:(jx + 1) * 128]
                                mm(o, wfv[:, k, jx, cc * 128:(cc + 1) * 128], hT[:, k, c0:c0 + n], k == 0, k == 7,
                                   [wfk] + hTk[sk], bk if sk == "p" else bs)
                    if cc == 1:
                        w_release(13 + jb)
                    for (c0, n, sk) in subs:
                        og = pbs[b_g][:, 0:512] if sk == "p" else pbs[bs][:, 0:128]
                        ou = pbs[b_u][:, 0:512] if sk == "p" else pbs[bs][:, 128:256]
                        kg = PB(b_g) if sk == "p" else PB(bs)
                        ku = PB(b_u) if sk == "p" else PB(bs)
                        act(ftmp[:, c0:c0 + n], og, AF.Sigmoid, [kg], [("ftmp", sk)])
                        tt(ftmp[:, c0:c0 + n], ftmp[:, c0:c0 + n], og, ALU.mult, [("ftmp", sk), kg], [("ftmp", sk)])
                        tt(hidden[:, f, c0:c0 + n], ftmp[:, c0:c0 + n], ou, ALU.mult, [("ftmp", sk), ku], [("hidden", f, sk)])
            for b in blocks:
                cp(pbf[:], pstage[:, b, :], [("pstage", b)], ["pbf"], eng="act")
                bt = nb()
                bv = pbs[bt][:].bitcast(BF16)
                for k in range(2):
                    tp(bv[:, k * 128:(k + 1) * 128], pbf[:, k * 128:(k + 1) * 128], identb[:], ["pbf", "identb"], bt)
                cp(pT[:, :, b * 128:(b + 1) * 128], bv[:, 0:256].rearrange("p (k t) -> p k t", k=2), [PB(bt)], [("pT", b)])
            warm(AF.Ln)
            for half in range(2):
                bks = {b: nb() for b in blocks}
                for kb in range(3):
                    wd, wdk = w_get(24 + half * 3 + kb)
                    wdv = wd[:].rearrange("p (k c) -> p k c", k=8)
                    nk = 8 if kb < 2 else 6
                    for kk in range(nk):
                        f = kb * 8 + kk
                        for b in blocks:
                            sk = "p" if b < 4 else "s"
                            mm(pbs[bks[b]][:, 0:512], hidden[:, f, b * 128:(b + 1) * 128], wdv[:, kk, :], f == 0, f == 21,
                               [wdk, ("hidden", f, sk)], bks[b])
                    w_release(24 + half * 3 + kb)
                for b in blocks:
                    tt(X[:, b, half * 512:(half + 1) * 512], X[:, b, half * 512:(half + 1) * 512], pbs[bks[b]][:, 0:512],
                       ALU.add, [(xk, b), PB(bks[b])], [(xk, b)])
                    half_stats(b, half)

            ck("ffn")
            norm_T(2, True)
            if ti + 1 < ntiles:
                Xn, xkn = xbuf(ti + 1)
                norm_stats(False, blks=[0, 1, 2, 3], X_=Xn, xk_=xkn, ssq_=ssq1, rst_=rst1, tag="ssq1", rtag="rst1")
            warm(AF.Sigmoid)
            wg0, wg0k = w_get(30)
            wpp_, wppk = w_get(31)
            wg1, wg1k = w_get(32)
            wppv = wpp_[:, 0:2048].rearrange("p (k c) -> p k c", k=2)
            for half in range(2):
                wg_, wgk = (wg0, wg0k) if half == 0 else (wg1, wg1k)
                wgv = wg_[:].rearrange("p (k c) -> p k c", k=8)
                bks = {b: nb() for b in blocks}
                for k in range(8):
                    for b in blocks:
                        mm(pbs[bks[b]][:, 0:512], hT[:, k, b * 128:(b + 1) * 128], wgv[:, k, :], k == 0, k == 7,
                           [wgk, ("hT", b)], bks[b])
                if half == 0:
                    w_release(30)
                for b in blocks:
                    sg = sgt[b % 2]
                    sgk = f"sgt{b % 2}"
                    act(sg[:], pbs[bks[b]][:, 0:512], AF.Sigmoid, [PB(bks[b])], [sgk])
                    be = nb()
                    for k in range(2):
                        mm(pbs[be][:, 0:512], pT[:, k, b * 128:(b + 1) * 128], wppv[:, k, half * 512:(half + 1) * 512], k == 0, k == 1,
                           [wppk, ("pT", b)], be)
                    tt(sg[:], sg[:], pbs[be][:, 0:512], ALU.mult, [sgk, PB(be)], [sgk])
                    tt(X[:, b, half * 512:(half + 1) * 512], X[:, b, half * 512:(half + 1) * 512], sg[:], ALU.add,
                       [(xk, b), sgk], [(xk, b)])
                for b in blocks:
                    half_stats(b, half)
                if half == 1:
                    w_release(31); w_release(32)

            ck("ple")
            if ti + 1 < ntiles:
                Xn, xkn = xbuf(ti + 1)
                norm_apply(0, blks=[0, 1, 2, 3], X_=Xn, xk_=xkn, rst_=rst1, rtag="rst1")
            P.barrier()
            norm_stats(True)
            for b in blocks:
                dst = yp[ti * 512 + b * 128: ti * 512 + (b + 1) * 128, :] if b < 4 else ys
                for half in range(2):
                    q_ = (2 * b + half) % 4
                    yst = ystage[q_]
                    ysk = f"ystage{q_}"
                    hs_ = slice(half * 512, (half + 1) * 512)
                    stt(yst[:], X[:, b, hs_], rst[:, b:b + 1], gB[:, 3, hs_], ALU.mult, ALU.mult,
                        [(xk, b), "rst", ("gB", 3)], [ysk])
                    P.dma("sp", f"d_y{q_}", lambda e, yst=yst, dst=dst, hs_=hs_: e.dma_start(out=dst[:, hs_], in_=yst[:]),
                          reads=[ysk])

        P.barrier()
        try:
            ck("setup")
            for ti in range(ntiles):
                run_tile(ti)
        except _Stop:
            pass
        P.finish("sp")
        P.emit()
    return nc


_PROG = {}


def _prep_inputs(inp):
    f = lambda a: np.ascontiguousarray(np.asarray(a, dtype=np.float32))
    w_in = f(inp["w_in"][0])
    wallv = build_wall(w_in, f(inp["w_pool_up"][0]), f(inp["w_hgrn_up"][0]), f(inp["w_out"][0]),
                       f(inp["w_ffn_gate"][0]), f(inp["w_ffn_up"][0]), f(inp["w_ffn_down"][0]),
                       f(inp["w_ple_gate"][0]), f(inp["w_ple_proj"][0]))
    gvec = np.ascontiguousarray(np.stack([f(inp["g_mix"][0]), f(inp["g_ffn"][0]), f(inp["g_ple"][0]), f(inp["g_final"])], 0))
    small = np.zeros((128, 16), np.float32)
    small[:, 0:4] = f(inp["pool_scale"][0]).reshape(4, 128).T
    small[:, 4:8] = f(inp["hgrn_norm"][0]).reshape(4, 128).T
    small[:, 8:12] = f(inp["hgrn_lb"][0]).reshape(4, 128).T
    small[:, 12:16] = f(inp["hgrn_lb"][1]).reshape(4, 128).T
    pmix = np.ascontiguousarray(f(inp["w_pool_mix"][0]).transpose(1, 0, 2)).reshape(128, 512)
    xp = f(inp["x_prompt"]); xsm = f(inp["x_sample"])
    ppr = f(inp["p_prompt"][0]); psm = f(inp["p_sample"][0])
    spl = f(inp["state_pool"][0]); shg = f(inp["state_hgrn"][0])
    maps = []
    for c in range(NCORES):
        maps.append({
            "xp": xp[c], "xs": xsm[16 * c:16 * c + 16].reshape(128, 1024),
            "pp": ppr[c], "ps": psm[16 * c:16 * c + 16].reshape(128, 256),
            "spool": spl[16 * c:16 * c + 16].reshape(240, 512),
            "shg": shg[16 * c:16 * c + 16],
            "wall": wallv, "cst": CONST_ARR, "gvec": gvec, "small": small, "pmix": pmix,
        })
    return maps


def kernel(**inputs):
    if "nc" not in _PROG:
        _PROG["nc"] = build_program()
    nc = _PROG["nc"]
    maps = _prep_inputs(inputs)
    res = run_bass_kernel_spmd(nc, maps, core_ids=list(range(NCORES)))
    R = res.results
    y_p = np.stack([R[c]["yp"] for c in range(NCORES)], 0).astype(np.float32)
    y_s = np.concatenate([R[c]["ys"].reshape(16, 8, 1024) for c in range(NCORES)], 0).astype(np.float32)
    npp = np.stack([R[c]["npp"] for c in range(NCORES)], 0)[None].astype(np.float32)
    nhp = np.stack([R[c]["nhp"] for c in range(NCORES)], 0)[None].astype(np.float32)
    nps = np.concatenate([R[c]["nps"].reshape(16, 15, 512) for c in range(NCORES)], 0)[None].astype(np.float32)
    nhs = np.concatenate([R[c]["nhs"] for c in range(NCORES)], 0)[None].astype(np.float32)
    return (y_p, y_s, npp, nhp, nps, nhs)
```

```python
import numpy as np
from contextlib import ExitStack
import concourse.bass as bass
import concourse.mybir as mybir
from concourse.bass_utils import run_bass_kernel_spmd

F32 = mybir.dt.float32
BF16 = mybir.dt.bfloat16
AF = mybir.ActivationFunctionType
ALU = mybir.AluOpType

ENGS = ("pe", "act", "dve", "pool", "sp")
EPS = 1e-6
NCORES = 8
NRING = 5
NTILES = 4
import os as _os
SAME_DIST = int(_os.environ.get("SAME_DIST", str(1 << 30)))


class Prog:
    def __init__(self, nc, stack, same_engine_wait=True):
        self.nc = nc
        self.stack = stack
        self.streams = {e: [] for e in ENGS}
        self.count = {e: 0 for e in ENGS}
        self.known = {e: {} for e in ENGS}
        self.sems = {}
        self.dma_count = {}
        self.lastw = {}
        self.readers = {}
        self.same_engine_wait = same_engine_wait

    def _collect(self, eng, reads, writes):
        deps = []
        for k in reads:
            ev = self.lastw.get(k)
            if ev is not None:
                deps.append(ev)
        for k in writes:
            ev = self.lastw.get(k)
            if ev is not None:
                deps.append(ev)
            rd = self.readers.get(k)
            if rd:
                deps.extend(rd.values())
        kn = self.known[eng]
        need = {}
        for (s, v, vc) in deps:
            if s == eng and (eng == "pe" or not self.same_engine_wait or self.count[eng] - v >= SAME_DIST):
                continue
            if kn.get(s, 0) >= v:
                continue
            if need.get(s, 0) < v:
                need[s] = v
            for s2, v2 in vc.items():
                if s2 == eng:
                    continue
                if kn.get(s2, 0) < v2:
                    kn[s2] = v2
        waits = []
        for s, v in need.items():
            waits.append((s, v))
            if kn.get(s, 0) < v:
                kn[s] = v
        return waits

    def _record(self, ev, reads, writes):
        s = ev[0]
        for k in reads:
            self.readers.setdefault(k, {})[s] = ev
        for k in writes:
            self.lastw[k] = ev
            self.readers[k] = {}

    def op(self, eng, fn, reads=(), writes=()):
        waits = self._collect(eng, reads, writes)
        self.count[eng] += 1
        n = self.count[eng]
        vc = dict(self.known[eng])
        vc[eng] = n
        ev = (eng, n, vc)
        self.streams[eng].append((waits, fn, (eng, 1)))
        self._record(ev, reads, writes)
        return ev

    def dma(self, q, semname, fn, reads=(), writes=()):
        waits = self._collect(q, reads, writes)
        self.dma_count[semname] = self.dma_count.get(semname, 0) + 1
        v = 16 * self.dma_count[semname]
        vc = dict(self.known[q])
        vc[semname] = v
        ev = (semname, v, vc)
        self.streams[q].append((waits, fn, (semname, 16)))
        self._record(ev, reads, writes)
        return ev

    def barrier(self, engines=("act", "dve", "sp")):
        for e in engines:
            kn = self.known[e]
            waits = []
            for e2 in ("pe", "act", "dve"):
                c = self.count[e2]
                if c and kn.get(e2, 0) < c:
                    waits.append((e2, c))
                    kn[e2] = c
            for s, c in self.dma_count.items():
                if s.startswith("ring") or s.startswith("d_y") or s.startswith("d_x"):
                    continue
                if kn.get(s, 0) < 16 * c:
                    waits.append((s, 16 * c))
                    kn[s] = 16 * c
            if waits:
                self.streams[e].append((waits, None, None))

    def finish(self, eng="sp"):
        kn = self.known[eng]
        waits = []
        for s, c in self.dma_count.items():
            if kn.get(s, 0) < 16 * c:
                waits.append((s, 16 * c))
                kn[s] = 16 * c
        for e in ("pe", "act", "dve"):
            if self.count[e] and kn.get(e, 0) < self.count[e]:
                waits.append((e, self.count[e]))
        self.streams[eng].append((waits, None, None))

    def emit(self):
        nc = self.nc
        for s in list(ENGS) + list(self.dma_count):
            if s not in self.sems:
                self.sems[s] = self.stack.enter_context(nc.semaphore(s))
        with nc.Block() as block:
            def run(engine, items):
                for waits, fn, inc in items:
                    for s, v in waits:
                        engine.wait_ge(self.sems[s], v)
                    if fn is not None:
                        ins = fn(engine)
                        ins.then_inc(self.sems[inc[0]], inc[1])

            @block.tensor
            def _(eng):
                run(eng, self.streams["pe"])

            @block.scalar
            def _(eng):
                run(eng, self.streams["act"])

            @block.vector
            def _(eng):
                run(eng, self.streams["dve"])

            @block.gpsimd
            def _(eng):
                run(eng, self.streams["pool"])

            @block.sync
            def _(eng):
                run(eng, self.streams["sp"])


NBLK = 33


def _kc(W, nk):
    C = W.shape[1]
    return np.ascontiguousarray(W.reshape(nk, 128, C).transpose(1, 0, 2)).reshape(128, nk * C)


def _pad(a):
    out = np.zeros((128, 4096), np.float32)
    out[:, : a.shape[1]] = a
    return out


def build_wall(w_in, w_pool_up, w_hgrn_up, w_out, w_g, w_u, w_d, w_pg, w_pp):
    blocks = []
    blocks.append(_kc(w_in[:, 0:512], 8))
    zz = w_in[:, 512:2560].reshape(8, 128, 4, 4, 128)
    ga = w_in[:, 2560:3584].reshape(8, 128, 8, 128)
    gb = w_in[:, 3584:4608].reshape(8, 128, 8, 128)
    for h in range(4):
        blocks.append(np.ascontiguousarray(zz[:, :, :, h, :].transpose(1, 0, 2, 3)).reshape(128, 4096))
        gg = np.stack([ga[:, :, 2 * h], gb[:, :, 2 * h], ga[:, :, 2 * h + 1], gb[:, :, 2 * h + 1]], axis=2)
        blocks.append(np.ascontiguousarray(gg.transpose(1, 0, 2, 3)).reshape(128, 4096))
    for half in range(2):
        yy = np.stack([w_pool_up[:, half * 512:(half + 1) * 512].reshape(4, 128, 512),
                       w_hgrn_up[:, half * 512:(half + 1) * 512].reshape(4, 128, 512)], axis=2)
        blocks.append(np.ascontiguousarray(yy.transpose(1, 0, 2, 3)).reshape(128, 4096))
    blocks.append(_kc(w_out[:, 0:512], 8))
    blocks.append(_kc(w_out[:, 512:1024], 8))
    for jb in range(11):
        gu = np.stack([w_g[:, jb * 256:(jb + 1) * 256], w_u[:, jb * 256:(jb + 1) * 256]], axis=1)
        blocks.append(_kc(gu.reshape(1024, 512), 8))
    for half in range(2):
        for kb in range(3):
            nk = 8 if kb < 2 else 6
            blocks.append(_pad(_kc(w_d[kb * 1024: kb * 1024 + nk * 128, half * 512:(half + 1) * 512], nk)))
    blocks.append(_kc(w_pg[:, 0:512], 8))
    blocks.append(_pad(_kc(w_pp, 2)))
    blocks.append(_kc(w_pg[:, 512:1024], 8))
    assert len(blocks) == NBLK
    return np.ascontiguousarray(np.stack(blocks, axis=0).astype(np.float32))


def build_consts():
    c = {}
    c["ident"] = np.eye(128, dtype=np.float32)
    s = np.arange(128)[:, None]
    t = np.arange(128)[None, :]
    c["maskP"] = ((s // 64 == t // 64) & (s <= t)).astype(np.float32)
    c["maskS"] = ((s // 8 == t // 8) & (s <= t)).astype(np.float32)
    c["seqm"] = (s // 8 == np.arange(16)[None, :]).astype(np.float32)
    r = np.ones(640, np.float32)
    r[0:512:64] = 0.0
    r[512:640:8] = 0.0
    c["resetm"] = np.broadcast_to(r, (128, 640)).copy()
    rc = np.zeros((4, 16), np.float32)
    for g, w in enumerate((2, 4, 8, 16)):
        rc[g] = 1.0 / np.minimum(np.arange(16) + 1, w)
    c["rc"] = np.broadcast_to(rc.reshape(1, 64), (128, 64)).copy()
    order = ["ident", "seqm", "rc", "maskP", "maskS", "resetm"]
    offs = {}
    o = 0
    for k in order:
        offs[k] = (o, c[k].shape[1])
        o += c[k].shape[1]
    return np.ascontiguousarray(np.concatenate([c[k] for k in order], axis=1)), offs


CONST_ARR, COFF = build_consts()
CW = CONST_ARR.shape[1]
CW_KEEP = COFF["maskP"][0]


class _Stop(Exception):
    pass


STOP = None


def ck(name):
    if STOP == name:
        raise _Stop()


def build_program(ntiles=NTILES):
    nc = bass.Bass("TRN2", target_bir_lowering=False)

    def din(name, shape):
        return nc.dram_tensor(name, shape, F32, kind="ExternalInput").ap()

    def dout(name, shape):
        return nc.dram_tensor(name, shape, F32, kind="ExternalOutput").ap()

    xp = din("xp", [2048, 1024])
    xs = din("xs", [128, 1024])
    ppd = din("pp", [2048, 256])
    psd = din("ps", [128, 256])
    spool = din("spool", [240, 512])
    shg = din("shg", [16, 4, 128, 128])
    wall = din("wall", [NBLK, 128, 4096])
    cst = din("cst", [128, CW])
    gvec = din("gvec", [4, 1024])
    smalld = din("small", [128, 16])
    pmixd = din("pmix", [128, 512])
    yp = dout("yp", [2048, 1024])
    ys = dout("ys", [128, 1024])
    npp = dout("npp", [15, 512])
    nhp = dout("nhp", [4, 128, 128])
    nps = dout("nps", [240, 512])
    nhs = dout("nhs", [16, 4, 128, 128])

    with ExitStack() as st:
        P = Prog(nc, st)

        def sb(name, shape, dt):
            return st.enter_context(nc.sbuf_tensor("sb_" + name, shape, dt))

        xres = sb("xres", [128, 5, 1024], F32)
        hT = sb("hT", [128, 8, 640], BF16)
        hn = [sb(f"hn{i}", [128, 1024], BF16) for i in range(2)]
        junk = sb("junk", [128, 512], BF16)
        ystage = [sb(f"ystage{i}", [128, 512], F32) for i in range(4)]
        resetmb = sb("resetmb", [128, 640], BF16)
        ring = [sb(f"ring{i}", [128, 4096], BF16) for i in range(NRING)]
        gB = sb("gB", [128, 4, 1024], F32)
        cs = sb("cs", [128, CW_KEEP], F32)
        identb = sb("identb", [128, 128], BF16)
        onesd = sb("onesd", [128, 128], BF16)
        maskPb = sb("maskPb", [128, 128], BF16)
        maskSb = sb("maskSb", [128, 128], BF16)
        pmixf = sb("pmixf", [128, 512], F32)
        pmix = sb("pmix", [128, 4, 128], BF16)
        small = sb("small", [128, 16], F32)
        lbv = sb("lbv", [128, 8], F32)
        carry = sb("carry", [128, 4, 128], F32)
        ucarry = sb("ucarry", [128, 4, 16], F32)
        ssq = sb("ssq", [128, 16], F32)
        ssq1 = sb("ssq1", [128, 8], F32)
        dmy = sb("dmy", [128, 2], F32)
        rst1 = sb("rst1", [128, 4], F32)
        rst = sb("rst", [128, 8], F32)
        identf = cs[:, COFF["ident"][0]:COFF["ident"][0] + 128]
        seqm = cs[:, COFF["seqm"][0]:COFF["seqm"][0] + 16]
        rcv = cs[:, COFF["rc"][0]:COFF["rc"][0] + 64]
        resetm = resetmb

        MIX_BYTES = 0
        scr_plan = {}

        def plan(phase, name, nelem, dt):
            nonlocal MIX_BYTES
            nbytes = nelem * (4 if dt == F32 else 2)
            nbytes = (nbytes + 31) // 32 * 32
            off = scr_plan.setdefault(("off", phase), 0)
            scr_plan[name] = (off, nelem, dt)
            scr_plan[("off", phase)] = off + nbytes

        U_NAMES = ["ug", "ue", "pa", "pb_", "sa", "sb_", "tmp16", "pooled", "npsb", "stg", "npp_sb"]
        for name, n, dt in [
            ("ug", 528, F32), ("ue", 16 * 24, F32), ("pa", 528, F32), ("pb_", 528, F32),
            ("sa", 16 * 24, F32), ("sb_", 16 * 24, F32), ("tmp16", 16, F32),
            ("pooled", 4 * 640, BF16), ("npsb", 4 * 240, F32),
            ("stg", 2 * 512, F32), ("npp_sb", 512, F32), ("pool_out", 4 * 640, BF16),
            ("qf", 640, F32), ("t1", 640, F32), ("t2", 640, F32), ("t3", 640, F32), ("t4", 640, F32),
            ("qe", 640, BF16), ("qe2", 640, BF16), ("keb", 640, BF16), ("kdb", 640, BF16), ("vb", 640, BF16), ("gbf", 640, BF16), ("gbf2", 640, BF16),
            ("kdT", 5 * 128, BF16), ("vT", 5 * 128, BF16), ("Sall", 9 * 128, F32), ("Sbf", 8 * 128, BF16),
            ("dec", 24, F32), ("dec2", 24, F32), ("Am", 5 * 128, BF16), ("osq", 640, BF16), ("o_fin", 4 * 640, BF16),
            ("S0", 16 * 128, F32), ("S0bf", 16 * 128, BF16), ("Vm", 16 * 128, BF16),
            ("merged", 8 * 640, BF16), ("s1", 640, F32), ("s2", 640, F32),
        ]:
            plan("mix", name, n, dt)
        for name, n, dt in [
            ("hidden", 22 * 640, BF16), ("ftmp", 640, F32), ("sgt0", 512, F32), ("sgt1", 512, F32),
            ("pT", 2 * 640, BF16), ("pstage", 5 * 256, F32), ("pbf", 256, BF16),
        ]:
            plan("ffn", name, n, dt)
        u_end = scr_plan["pool_out"][0]
        assert 2 * 8 * 640 * 2 <= u_end, u_end
        scr_plan["sgA"] = (0, 8 * 640, BF16)
        scr_plan["sgB"] = (8 * 640 * 2, 8 * 640, BF16)
        scr_bytes = max(scr_plan[("off", "mix")], scr_plan[("off", "ffn")])
        scr = sb("scr", [128, scr_bytes // 4], F32)

        def sv(name):
            off, n, dt = scr_plan[name]
            if dt == F32:
                return scr[:, off // 4: off // 4 + n]
            return scr[:, off // 4: off // 4 + n // 2].bitcast(BF16)

        ug = sv("ug"); ue = sv("ue").rearrange("p (i r) -> p i r", r=24)
        pa = sv("pa"); pb_ = sv("pb_")
        sa = sv("sa").rearrange("p (i r) -> p i r", r=24); sb_ = sv("sb_").rearrange("p (i r) -> p i r", r=24)
        tmp16 = sv("tmp16")
        pooled = sv("pooled").rearrange("p (g t) -> p g t", g=4)
        pool_out = sv("pool_out").rearrange("p (g t) -> p g t", g=4)
        npsb = sv("npsb").rearrange("p (g r) -> p g r", g=4)
        stg = sv("stg").rearrange("p (h c) -> p h c", h=2)
        nps_sb = stg
        npp_sb = sv("npp_sb")
        qf = sv("qf"); t1 = sv("t1"); t2 = sv("t2"); t3 = sv("t3"); t4 = sv("t4")
        qe = sv("qe"); qe2 = sv("qe2"); keb = sv("keb"); kdb = sv("kdb"); vb = sv("vb"); gbf = sv("gbf"); gbf2 = sv("gbf2")
        kdT = sv("kdT").rearrange("p (b k) -> p b k", b=5); vT = sv("vT").rearrange("p (b k) -> p b k", b=5)
        Sall = sv("Sall").rearrange("p (c v) -> p c v", c=9); Sbf = sv("Sbf").rearrange("p (c v) -> p c v", c=8)
        dec = sv("dec"); dec2 = sv("dec2"); Am = sv("Am").rearrange("p (b k) -> p b k", b=5); osq = sv("osq")
        o_fin = sv("o_fin").rearrange("p (h t) -> p h t", h=4)
        S0 = sv("S0").rearrange("p (i v) -> p i v", i=16); S0bf = sv("S0bf").rearrange("p (i v) -> p i v", i=16)
        Vm = sv("Vm").rearrange("p (i v) -> p i v", i=16)
        merged = sv("merged").rearrange("p (k t) -> p k t", k=8); s1 = sv("s1"); s2 = sv("s2")
        sgA = sv("sgA").rearrange("p (k t) -> p k t", k=8); sgB = sv("sgB").rearrange("p (k t) -> p k t", k=8)
        hidden = sv("hidden").rearrange("p (f t) -> p f t", f=22); ftmp = sv("ftmp")
        assert scr_plan["S0bf"][0] == scr_plan["S0"][0] + 8192 and scr_plan["Vm"][0] == scr_plan["S0"][0] + 12288
        assert scr_plan["S0"][0] >= scr_plan[("off", "ffn")]
        xo = scr_plan["S0"][0] // 4
        xalt = scr[:, xo:xo + 4096].rearrange("p (b d) -> p b d", b=4)
        sgt = [sv("sgt0"), sv("sgt1")]
        pT = sv("pT").rearrange("p (k t) -> p k t", k=2); pstage = sv("pstage").rearrange("p (b c) -> p b c", b=5); pbf = sv("pbf")

        pbs = [st.enter_context(nc.psum_tensor(f"pb{i}", [128, 512], F32)) for i in range(8)]
        bank_ctr = [0]

        def nb():
            b = bank_ctr[0] % 8
            bank_ctr[0] += 1
            return b

        def PB(b):
            return ("pb", b)

        wstate = {"tile": 0, "issued": set(), "released": set(), "total": NBLK * ntiles}

        def w_issue_n(n, extra_reads=()):
            if n >= wstate["total"] or n in wstate["issued"]:
                return
            slot = n % NRING
            blk = n % NBLK
            P.dma("pool", f"ring{slot}",
                  lambda e, slot=slot, blk=blk: e.dma_start(out=ring[slot][:], in_=wall[blk]),
                  reads=list(extra_reads), writes=[("ring", slot)])
            wstate["issued"].add(n)

        def w_issue(extra_reads=()):
            w_issue_n(len(wstate["issued"]), extra_reads)

        def w_get(expect):
            n = wstate["tile"] * NBLK + expect
            assert n in wstate["issued"], ("ring too small / block not prefetched", n)
            assert n - NRING < 0 or (n - NRING) in wstate["released"]
            slot = n % NRING
            return ring[slot], ("ring", slot)

        def w_release(expect):
            n = wstate["tile"] * NBLK + expect
            assert n in wstate["issued"] and n not in wstate["released"]
            wstate["released"].add(n)
            w_issue_n(n + NRING)

        def A(eng, fn, reads, writes):
            return P.op(eng, fn, reads=reads, writes=writes)

        def mm(out, lhsT, rhs, start, stop, reads, bank):
            A("pe", lambda e: e.matmul(out, lhsT=lhsT, rhs=rhs, start=start, stop=stop), reads, [PB(bank)])

        def tp(out, in_, ident, reads, bank):
            A("pe", lambda e: e.transpose(out=out, in_=in_, identity=ident), reads, [PB(bank)])

        def act(out, in_, func, reads, writes, **kw):
            A("act", lambda e: e.activation(out=out, in_=in_, func=func, **kw), reads, writes)

        def tt(out, in0, in1, op, reads, writes, eng="dve"):
            A(eng, lambda e: e.tensor_tensor(out=out, in0=in0, in1=in1, op=op), reads, writes)

        def ts(out, in0, s1_, s2_, op0, op1, reads, writes):
            A("dve", lambda e: e.tensor_scalar(out=out, in0=in0, scalar1=s1_, scalar2=s2_, op0=op0, op1=op1), reads, writes)

        def stt(out, in0, scalar, in1, op0, op1, reads, writes):
            A("dve", lambda e: e.scalar_tensor_tensor(out=out, in0=in0, scalar=scalar, in1=in1, op0=op0, op1=op1), reads, writes)

        def cp(out, in_, reads, writes, eng="dve"):
            if eng == "act":
                act(out, in_, AF.Copy, reads, writes)
            else:
                A(eng, lambda e: e.tensor_copy(out=out, in_=in_), reads, writes)

        def warm(func):
            act(dmy[:, 1:2], dmy[:, 0:1], func, ["dmy0"], ["dmy1"])

        A("dve", lambda e: e.memset(dmy[:], 1.0), [], ["dmy0", "dmy1"])
        P.dma("sp", "d_cs", lambda e: e.dma_start(out=cs[:], in_=cst[:, 0:CW_KEEP]), writes=["cs"])
        cstage = scr[:, 0:CW - CW_KEEP]
        P.dma("sp", "d_cs2", lambda e: e.dma_start(out=cstage, in_=cst[:, CW_KEEP:CW]), writes=["cstage"])
        maskPf = cstage[:, 0:128]
        maskSf = cstage[:, 128:256]
        resetmf = cstage[:, 256:896]
        P.dma("sp", "d_small", lambda e: e.dma_start(out=small[:], in_=smalld), writes=["small"])
        P.dma("sp", "d_pmix", lambda e: e.dma_start(out=pmixf[:], in_=pmixd), writes=["pmixf"])
        for gi in range(4):
            P.dma("sp", f"d_g{gi}",
                  lambda e, gi=gi: e.dma_start(out=gB[:, gi, :], in_=gvec[gi:gi + 1, :].broadcast_to([128, 1024])),
                  writes=[("gB", gi)])
        cp(identb[:], identf, ["cs"], ["identb"])
        cp(maskPb[:], maskPf, ["cstage"], ["maskPb"])
        cp(maskSb[:], maskSf, ["cstage"], ["maskSb"])
        cp(resetmb[:], resetmf, ["cstage"], ["resetmb"])
        cp(pmix[:].rearrange("p g c -> p (g c)"), pmixf[:], ["pmixf"], ["pmix"])
        A("dve", lambda e: e.memset(onesd[:], 1.0 / 128.0), [], ["onesd"])
        A("dve", lambda e: e.memset(carry[:].rearrange("p h v -> p (h v)"), 0.0), [], ["carry"])
        A("dve", lambda e: e.memset(ucarry[:].rearrange("p g t -> p (g t)"), 0.0), [], ["ucarry"])
        tt(lbv[:, 0:4], small[:, 8:12], small[:, 12:16], ALU.subtract, ["small"], ["lbv"])
        act(lbv[:, 0:4], lbv[:, 0:4], AF.Sigmoid, ["lbv"], ["lbv"])
        ts(lbv[:, 4:8], lbv[:, 0:4], -1.0, 1.0, ALU.mult, ALU.add, ["lbv"], ["lbv"])

        def run_tile(ti):
            wstate["tile"] = ti
            has_s = ti == 0
            last = ti == 3
            NT = 640 if has_s else 512
            blocks = [0, 1, 2, 3] + ([4] if has_s else [])
            subs = [(0, 512, "p")] + ([(512, 128, "s")] if has_s else [])
            hTk_p = [("hT", b) for b in range(4)]
            hTk = {"p": hTk_p, "s": [("hT", 4)]}

            def K(name):
                return [(name, "p")] + ([(name, "s")] if has_s else [])

            def xbuf(tj):
                return (xres, "xres") if tj % 2 == 0 else (xalt, "xalt")

            X, xk = xbuf(ti)

            def load_x(tj, b):
                src = xp[tj * 512 + b * 128: tj * 512 + (b + 1) * 128, :] if b < 4 else xs
                Xn, xkn = xbuf(tj)
                P.dma("sp", f"d_x{tj % 2}_{b}", lambda e, b=b, src=src, Xn=Xn: e.dma_start(out=Xn[:, b, :], in_=src),
                      writes=[(xkn, b)])

            if ti == 0:
                for b in blocks:
                    load_x(0, b)
            if ti == 0:
                for _ in range(NRING):
                    w_issue(extra_reads=[(xk, b) for b in blocks])
            if has_s:
                for half in range(2):
                    P.dma("sp", f"d_stg{half}",
                          lambda e, half=half: e.dma_start(out=stg[0:120, half, :], in_=spool[half * 120:(half + 1) * 120, :]),
                          writes=[("stg", half)])

            def half_stats(b, half, X_=None, xk_=None, ssq_=None, tag="ssq"):
                X_ = X if X_ is None else X_
                xk_ = xk if xk_ is None else xk_
                ssq_ = ssq if ssq_ is None else ssq_
                act(junk[:, 0:512], X_[:, b, half * 512:(half + 1) * 512], AF.Square, [(xk_, b)], ["junk", (tag, b, half)],
                    accum_out=ssq_[:, 2 * b + half:2 * b + half + 1])

            def norm_stats(have_partials, blks=None, X_=None, xk_=None, ssq_=None, rst_=None, tag="ssq", rtag="rst"):
                blks = blocks if blks is None else blks
                ssq_ = ssq if ssq_ is None else ssq_
                rst_ = rst if rst_ is None else rst_
                if not have_partials:
                    for b in blks:
                        for half in range(2):
                            half_stats(b, half, X_, xk_, ssq_, tag)
                nbk = len(blks)
                tt(rst_[:, 0:nbk], ssq_[:, 0:2 * nbk:2], ssq_[:, 1:2 * nbk:2], ALU.add,
                   [(tag, b, hf) for b in blks for hf in range(2)], [rtag])
                act(rst_[:, 0:nbk], rst_[:, 0:nbk], AF.Ln, [rtag], [rtag], scale=1.0 / 1024.0, bias=EPS)
                act(rst_[:, 0:nbk], rst_[:, 0:nbk], AF.Exp, [rtag], [rtag], scale=-0.5)

            def norm_apply(gi, blks=None, X_=None, xk_=None, rst_=None, rtag="rst"):
                blks = blocks if blks is None else blks
                X_ = X if X_ is None else X_
                xk_ = xk if xk_ is None else xk_
                rst_ = rst if rst_ is None else rst_
                for b in blks:
                    hb = hn[b % 2]
                    hk = f"hn{b % 2}"
                    stt(hb[:], X_[:, b, :], rst_[:, b:b + 1], gB[:, gi, :], ALU.mult, ALU.mult,
                        [(xk_, b), rtag, ("gB", gi)], [hk])
                    bank = nb()
                    bv = pbs[bank][:].bitcast(BF16)
                    for k in range(8):
                        tp(bv[:, k * 128:(k + 1) * 128], hb[:, k * 128:(k + 1) * 128], identb[:], [hk, "identb"], bank)
                    c0 = b * 128
                    cp(hT[:, :, c0:c0 + 128], bv.rearrange("p (k t) -> p k t", k=8), [PB(bank)], [("hT", b)], eng="act")

            def norm_T(gi, have_partials):
                norm_stats(have_partials)
                norm_apply(gi)

            ck("load")
            if ti == 0:
                norm_T(0, False)
            ck("norm1")

            wv, wk = w_get(0)
            wu = wv[:].rearrange("p (k c) -> p k c", k=8)
            ubanks = []
            for g in range(4):
                bp = nb()
                bs = nb() if has_s else None
                for k in range(8):
                    mm(pbs[bp][:, 0:512], wu[:, k, g * 128:(g + 1) * 128], hT[:, k, 0:512], k == 0, k == 7, [wk] + hTk_p, bp)
                    if has_s:
                        mm(pbs[bs][:, 0:128], wu[:, k, g * 128:(g + 1) * 128], hT[:, k, 512:640], k == 0, k == 7, [wk, ("hT", 4)], bs)
                ubanks.append((bp, bs))
                if g == 3:
                    w_release(0)
                w = 2 << g
                cp(ug[:, 0:16], ucarry[:, g, :], ["ucarry"], ["ug"])
                cp(ug[:, 16:528], pbs[bp][:, 0:512], [PB(bp)], ["ug"], eng="act")
                tt(pa[:, 1:528], ug[:, 1:528], ug[:, 0:527], ALU.add, ["ug"], ["pa"])
                sw = pa
                swk = "pa"
                if w >= 4:
                    tt(pb_[:, 3:528], pa[:, 3:528], pa[:, 1:526], ALU.add, ["pa"], ["pb_"])
                    sw, swk = pb_, "pb_"
                if w >= 8:
                    tt(pa[:, 7:528], pb_[:, 7:528], pb_[:, 3:524], ALU.add, ["pb_"], ["pa"])
                    sw, swk = pa, "pa"
                if w >= 16:
                    tt(pb_[:, 15:528], pa[:, 15:528], pa[:, 7:520], ALU.add, ["pa"], ["pb_"])
                    sw, swk = pb_, "pb_"
                stt(pooled[:, g, 0:512], sw[:, 16:528], 1.0 / w, ug[:, 16:528], ALU.mult, ALU.subtract,
                    [swk, "ug"], [("pooled", g, "p")])
                if ti == 0:
                    tt(tmp16[:], sw[:, 16:32], rcv[:, g * 16:(g + 1) * 16], ALU.mult, [swk, "cs"], ["tmp16"])
                    tt(pooled[:, g, 0:16], tmp16[:], ug[:, 16:32], ALU.subtract, ["tmp16", "ug"], [("pooled", g, "p")])
                cp(ucarry[:, g, :], ug[:, 512:528], ["ug"], ["ucarry"])
                if last:
                    bt = nb()
                    tp(pbs[bt][0:15, 0:128], ug[:, 513:528], identf, ["ug", "cs"], bt)
                    cp(npp_sb[0:15, g * 128:(g + 1) * 128], pbs[bt][0:15, 0:128], [PB(bt)], ["npp_sb"], eng="act")
                if has_s:
                    bt = nb()
                    for half in range(2):
                        tp(pbs[bt][:, half * 120:(half + 1) * 120], stg[0:120, half, g * 128:(g + 1) * 128],
                           identf[0:120, 0:120], [("stg", half), "cs"], bt)
                    cp(ue[:, :, 0:15], pbs[bt][:, 0:240].rearrange("p (i r) -> p i r", r=15), [PB(bt)], ["ue"], eng="act")
                    cp(ue[:, :, 15:23], pbs[bs][:, 0:128].rearrange("p (i t) -> p i t", t=8), [PB(bs)], ["ue"], eng="act")
                    tt(sa[:, :, 1:23], ue[:, :, 1:23], ue[:, :, 0:22], ALU.add, ["ue"], ["sa"])
                    ssw, sswk = sa, "sa"
                    if w >= 4:
                        tt(sb_[:, :, 3:23], sa[:, :, 3:23], sa[:, :, 1:21], ALU.add, ["sa"], ["sb_"])
                        ssw, sswk = sb_, "sb_"
                    if w >= 8:
                        tt(sa[:, :, 7:23], sb_[:, :, 7:23], sb_[:, :, 3:19], ALU.add, ["sb_"], ["sa"])
                        ssw, sswk = sa, "sa"
                    if w >= 16:
                        tt(sb_[:, :, 15:23], sa[:, :, 15:23], sa[:, :, 7:15], ALU.add, ["sa"], ["sb_"])
                        ssw, sswk = sb_, "sb_"
                    stt(pooled[:, g, 512:640].rearrange("p (i t) -> p i t", t=8), ssw[:, :, 15:23], 1.0 / w,
                        ue[:, :, 15:23], ALU.mult, ALU.subtract, [sswk, "ue"], [("pooled", g, "s")])
                    cp(npsb[:, g, :].rearrange("p (i r) -> p i r", r=15), ue[:, :, 8:23], ["ue"], [("npsb", g)])
            if last:
                P.dma("sp", "d_npp", lambda e: e.dma_start(out=npp, in_=npp_sb[0:15, :]), reads=["npp_sb"])
            if has_s:
                for half in range(2):
                    bt = nb()
                    for g in range(4):
                        tp(pbs[bt][0:120, g * 128:(g + 1) * 128], npsb[:, g, half * 120:(half + 1) * 120], identf,
                           [("npsb", g), "cs"], bt)
                    cp(nps_sb[0:120, half, :], pbs[bt][0:120, 0:512], [PB(bt)], [("stg", half)], eng="act")
                    P.dma("sp", f"d_nps{half}",
                          lambda e, half=half: e.dma_start(out=nps[half * 120:(half + 1) * 120, :], in_=nps_sb[0:120, half, :]),
                          reads=[("stg", half)])
            ck("u")
            PIPE = not has_s
            if PIPE:
                zc, rcn = [0], [0]

                def nbZ():
                    b = zc[0] % 4
                    zc[0] += 1
                    return b

                def nbR():
                    b = 6 + rcn[0] % 2
                    rcn[0] += 1
                    return b

                gcn = [0]

                def nbG():
                    b = 4 + gcn[0] % 2
                    gcn[0] += 1
                    return b
            else:
                nbZ = nbR = nbG = nb
            gb2 = [gbf, gbf2]
            QE = [qe, qe2]
            DEC = [dec, dec2]
            HS = {h: {} for h in range(4)}

            def poolmix():
                for g in range(4):
                    for (c0, n, sk) in subs:
                        bk = nbR()
                        mm(pbs[bk][:, 0:n], pmix[:, g, :], pooled[:, g, c0:c0 + n], True, True, ["pmix", ("pooled", g, sk)], bk)
                        act(pool_out[:, g, c0:c0 + n], pbs[bk][:, 0:n], AF.Copy, [PB(bk), "small"], [("pool_out", g, sk)],
                            scale=small[:, g:g + 1])

            def z_group(h, j):
                S_ = HS[h]
                if j == 0:
                    wv, wk = w_get(1 + 2 * h)
                    S_["wh"] = wv[:].rearrange("p (k j c) -> p k j c", k=8, j=4)
                    S_["wk"] = wk
                    S_["bs"] = nbZ() if has_s else None
                    S_["zb"] = []
                wh, wk, bs = S_["wh"], S_["wk"], S_["bs"]
                bp = nbZ()
                for k in range(8):
                    mm(pbs[bp][:, 0:512], wh[:, k, j, :], hT[:, k, 0:512], k == 0, k == 7, [wk] + hTk_p, bp)
                    if has_s:
                        mm(pbs[bs][:, j * 128:(j + 1) * 128], wh[:, k, j, :], hT[:, k, 512:640], k == 0, k == 7,
                           [wk, ("hT", 4)], bs)
                S_["zb"].append(bp)
                if j == 3:
                    w_release(1 + 2 * h)

            def z_evac(h):
                S_ = HS[h]
                zb, bs = S_["zb"], S_["bs"]
                gbh = gb2[h % 2]
                gbk = f"gbf{h % 2}"

                def zsrc(j, sk):
                    return (pbs[zb[j]][:, 0:512], PB(zb[j])) if sk == "p" else (pbs[bs][:, j * 128:(j + 1) * 128], PB(bs))

                for (c0, n, sk) in subs:
                    src, key = zsrc(1, sk)
                    act(t1[:, c0:c0 + n], src, AF.Sigmoid, [key], [("t1", sk)])
                    src, key = zsrc(0, sk)
                    act(qf[:, c0:c0 + n], src, AF.Sigmoid, [key], [("qf", sk)])
                    tt(qf[:, c0:c0 + n], qf[:, c0:c0 + n], src, ALU.mult, [("qf", sk), key], [("qf", sk)])
                    src, key = zsrc(3, sk)
                    act(t4[:, c0:c0 + n], src, AF.Sigmoid, [key], [("t4", sk)])
                    tt(gbh[:, c0:c0 + n], t4[:, c0:c0 + n], src, ALU.mult, [("t4", sk), key], [(gbk, sk)])
                    src, key = zsrc(2, sk)
                    cp(vb[:, c0:c0 + n], src, [key], [("vb", sk)])
                warm(AF.Ln)

            def g_mm(h, jj, tsel=(0, 1)):
                S_ = HS[h]
                if jj == 0 and tsel[0] == 0:
                    wgv_, wgk_ = w_get(2 + 2 * h)
                    S_["wgv"] = wgv_[:].rearrange("p (k t c) -> p k t c", k=8, t=4)
                    S_["wgk"] = wgk_
                    S_["gate_ev"] = []
                wgv, wgk_ = S_["wgv"], S_["wgk"]
                j = 2 * h + jj
                if tsel[0] == 0:
                    S_["bsg"] = nbG() if has_s else None
                bsg = S_["bsg"]
                for t_ in tsel:
                    bpg = nbG()
                    for k in range(8):
                        mm(pbs[bpg][:, 0:512], wgv[:, k, 2 * jj + t_, :], hT[:, k, 0:512], k == 0, k == 7, [wgk_] + hTk_p, bpg)
                        if has_s:
                            mm(pbs[bsg][:, t_ * 128:(t_ + 1) * 128], wgv[:, k, 2 * jj + t_, :], hT[:, k, 512:640], k == 0, k == 7,
                               [wgk_, ("hT", 4)], bsg)
                    S_["gate_ev"].append((j, t_, bpg, bsg))
                if jj == 1 and tsel[-1] == 1:
                    w_release(2 + 2 * h)

            def g_evac(h, jj):
                evs = HS[h]["gate_ev"][2 * jj:2 * jj + 2]
                if has_s:
                    for (j, t_, bpg, bsg) in evs:
                        dst = sgA if t_ == 0 else sgB
                        dk = "sgA" if t_ == 0 else "sgB"
                        cp(dst[:, j, 512:640], pbs[bsg][:, t_ * 128:(t_ + 1) * 128], [PB(bsg)], [(dk, j, "s")], eng="act")
                for (j, t_, bpg, bsg) in evs:
                    dst = sgA if t_ == 0 else sgB
                    dk = "sgA" if t_ == 0 else "sgB"
                    cp(dst[:, j, 0:512], pbs[bpg][:, 0:512], [PB(bpg)], [(dk, j, "p")], eng="act")

            def chain_a(h):
                ts(t1[:, 0:NT], t1[:, 0:NT], lbv[:, 4 + h:5 + h], lbv[:, h:h + 1], ALU.mult, ALU.add, K("t1") + ["lbv"], K("t1"))
                ts(t2[:, 0:NT], t1[:, 0:NT], -1.0, 1.0, ALU.mult, ALU.add, K("t1"), K("t2"))
                act(t1[:, 0:NT], t1[:, 0:NT], AF.Ln, K("t1"), K("t1"))

            def cb_scan(h):
                A("dve", lambda e: e.tensor_tensor_scan(out=t3[:, 0:NT], data0=resetm[:, 0:NT], data1=t1[:, 0:NT],
                                                        initial=0.0, op0=ALU.mult, op1=ALU.add),
                  K("t1") + ["resetmb"], K("t3"))

            def cb_exp(h):
                act(t1[:, 0:NT], t3[:, 0:NT], AF.Exp, K("t3"), K("t1"))
                act(t4[:, 0:NT], t3[:, 0:NT], AF.Exp, K("t3"), K("t4"), scale=-1.0)

            def cb_mul(h):
                qe_ = QE[h % 2]
                qk = f"qe{h % 2}"
                tt(qe_[:, 0:NT], qf[:, 0:NT], t1[:, 0:NT], ALU.mult, K("qf") + K("t1"), [(qk, "p")] + ([(qk, "s")] if has_s else []))
                tt(t2[:, 0:NT], t2[:, 0:NT], t4[:, 0:NT], ALU.mult, K("t2") + K("t4"), K("t2"))

            def cb_keb(h):
                cp(keb[:, 0:NT], t2[:, 0:NT], K("t2"), K("keb"), eng="act")

            def cb_dec(h):
                dec_ = DEC[h % 2]
                dk_ = f"dec{h % 2}"
                cp(dec_[:, 0:8], t1[:, 63:512:64], [("t1", "p")], [(dk_, "p")])
                tt(kdb[:, 0:512].rearrange("p (c t) -> p c t", t=64), t2[:, 0:512].rearrange("p (c t) -> p c t", t=64),
                   dec_[:, 0:8].unsqueeze(2).broadcast_to([128, 8, 64]), ALU.mult, [("t2", "p"), (dk_, "p")], [("kdb", "p")])
                if has_s:
                    cp(dec_[:, 8:24], t1[:, 519:640:8], [("t1", "s")], [(dk_, "s")])
                    tt(kdb[:, 512:640].rearrange("p (c t) -> p c t", t=8), t2[:, 512:640].rearrange("p (c t) -> p c t", t=8),
                       dec_[:, 8:24].unsqueeze(2).broadcast_to([128, 16, 8]), ALU.mult, [("t2", "s"), (dk_, "s")], [("kdb", "s")])

            def chain_b(h):
                cb_scan(h); cb_exp(h); cb_mul(h); cb_keb(h); cb_dec(h)

            def R1(h):
                qe = QE[h % 2]
                qk = f"qe{h % 2}"
                for (src_, srck, dst, dstk) in ((kdb, "kdb", kdT, "kdT"), (vb, "vb", vT, "vT")):
                    bt = nbR()
                    bv = pbs[bt][:].bitcast(BF16)
                    for b in blocks:
                        sk = "p" if b < 4 else "s"
                        tp(bv[:, b * 128:(b + 1) * 128], src_[:, b * 128:(b + 1) * 128], identb[:], [(srck, sk), "identb"], bt)
                    nbk = len(blocks)
                    cp(dst[:, 0:nbk, :], bv[:, 0:nbk * 128].rearrange("p (b k) -> p b k", k=128), [PB(bt)], [dstk], eng="act")
                ba = nbR()
                for b in range(4):
                    mm(pbs[ba][:, b * 128:(b + 1) * 128], keb[:, b * 128:(b + 1) * 128], qe[:, b * 128:(b + 1) * 128],
                       True, True, [("keb", "p"), (qk, "p")], ba)
                tt(Am[:, 0:4, :], pbs[ba][:, 0:512].rearrange("p (b t) -> p b t", b=4),
                   maskPb[:].unsqueeze(1).broadcast_to([128, 4, 128]), ALU.mult, [PB(ba), "maskPb"], [("Am", "p")])
                if has_s:
                    bas = nbR()
                    mm(pbs[bas][:, 0:128], keb[:, 512:640], qe[:, 512:640], True, True, [("keb", "s"), (qk, "s")], bas)
                    tt(Am[:, 4, :], pbs[bas][:, 0:128], maskSb[:], ALU.mult, [PB(bas), "maskSb"], [("Am", "s")])

            def build_vm():
                tt(Vm[:, :, :], vT[:, 4, :].unsqueeze(1).broadcast_to([128, 16, 128]),
                   seqm.unsqueeze(2).broadcast_to([128, 16, 128]), ALU.mult, ["vT", "cs"], ["Vm"])

            def sample_prefetch(h):
                P.dma("sp", "d_S0", lambda e, h=h: e.dma_start(out=S0[:, :, :], in_=shg[:, h, :, :].rearrange("i k v -> k i v")),
                      writes=["S0"])

            def sample_bf(h):
                cp(S0bf[:, :, :].rearrange("p i v -> p (i v)"), S0[:, :, :].rearrange("p i v -> p (i v)"), ["S0"], ["S0bf"], eng="act")

            def sample_state_update(h):
                dec = DEC[h % 2]
                dk_ = f"dec{h % 2}"
                tt(S0[:, :, :], S0[:, :, :], dec[:, 8:24].unsqueeze(2).broadcast_to([128, 16, 128]), ALU.mult,
                   ["S0", (dk_, "s")], ["S0"])
                for q4 in range(4):
                    bd = nbR()
                    mm(pbs[bd][:, 0:512], kdT[:, 4, :], Vm[:, 4 * q4:4 * q4 + 4, :].rearrange("p i v -> p (i v)"), True, True,
                       ["kdT", "Vm"], bd)
                    tt(S0[:, 4 * q4:4 * q4 + 4, :].rearrange("p i v -> p (i v)"),
                       S0[:, 4 * q4:4 * q4 + 4, :].rearrange("p i v -> p (i v)"), pbs[bd][:, 0:512], ALU.add,
                       ["S0", PB(bd)], ["S0"])
                P.dma("sp", "d_nhs", lambda e, h=h: e.dma_start(out=nhs[:, h, :, :].rearrange("i k v -> k i v"), in_=S0[:, :, :]),
                      reads=["S0"])

            def R2_mm(h):
                dsb = [nbR(), nbR()]
                HS[h]["dsb"] = dsb
                for c in range(8):
                    blk, half = c // 2, c % 2
                    bk = dsb[half]
                    mm(pbs[bk][:, blk * 128:(blk + 1) * 128], kdT[half * 64:(half + 1) * 64, blk, :],
                       vT[half * 64:(half + 1) * 64, blk, :], True, True, ["kdT", "vT"], bk)
                cp(Sall[:, 0, :], carry[:, h, :], ["carry"], [("Sall", 0)])

            def R2_steps(h, c0, c1):
                dec_ = DEC[h % 2]
                dk_ = f"dec{h % 2}"
                dsb = HS[h]["dsb"]
                for c in range(c0, c1):
                    blk, half = c // 2, c % 2
                    bk = dsb[half]
                    stt(Sall[:, c + 1, :], Sall[:, c, :], dec_[:, c:c + 1], pbs[bk][:, blk * 128:(blk + 1) * 128],
                        ALU.mult, ALU.add, [("Sall", c), (dk_, "p"), PB(bk)], [("Sall", c + 1)])

            def R2_half(h):
                cp(Sbf[:, 0:4, :].rearrange("p c v -> p (c v)"), Sall[:, 0:4, :].rearrange("p c v -> p (c v)"),
                   [("Sall", c) for c in range(4)], [("Sbf", 0)], eng="act")

            def R2_fin(h):
                cp(Sbf[:, 4:8, :].rearrange("p c v -> p (c v)"), Sall[:, 4:8, :].rearrange("p c v -> p (c v)"),
                   [("Sall", c) for c in range(4, 8)], [("Sbf", 1)], eng="act")
                cp(carry[:, h, :], Sall[:, 8, :], [("Sall", 8)], ["carry"])
                if last:
                    P.dma("sp", f"d_nhp{h}", lambda e, h=h: e.dma_start(out=nhp[h], in_=Sall[:, 8, :]), reads=[("Sall", 8)])

            def R2(h):
                R2_mm(h); R2_steps(h, 0, 4); R2_half(h); R2_steps(h, 4, 8); R2_fin(h)

            def R3(h):
                qe = QE[h % 2]
                qk = f"qe{h % 2}"
                S_ = HS[h]
                bo = nbR()
                for b in range(4):
                    mm(pbs[bo][:, b * 128:(b + 1) * 128], vT[:, b, :], Am[:, b, :], True, False, ["vT", ("Am", "p")], bo)
                    for half in range(2):
                        c = 2 * b + half
                        mm(pbs[bo][:, c * 64:(c + 1) * 64], Sbf[:, c, :], qe[:, c * 64:(c + 1) * 64], False, half == 1,
                           [("Sbf", c // 4), (qk, "p")], bo)
                bos = None
                if has_s:
                    bos = nbR()
                    mm(pbs[bos][:, 0:128], vT[:, 4, :], Am[:, 4, :], True, False, ["vT", ("Am", "s")], bos)
                    for i in range(16):
                        mm(pbs[bos][:, i * 8:(i + 1) * 8], S0bf[:, i, :], qe[:, 512 + i * 8:512 + (i + 1) * 8], False, i == 15,
                           ["S0bf", (qk, "s")], bos)
                S_["obanks"] = [(0, 512, "p", bo)] + ([(512, 128, "s", bos)] if has_s else [])
                for (c0, n, sk, bk) in S_["obanks"]:
                    act(osq[:, c0:c0 + n], pbs[bk][:, 0:n], AF.Square, [PB(bk)], [("osq", sk)])

            def R4(h):
                gbh = gb2[h % 2]
                gbk = f"gbf{h % 2}"
                for (c0, n, sk, bk) in HS[h]["obanks"]:
                    bn = nbR()
                    mm(pbs[bn][:, 0:n], onesd[:], osq[:, c0:c0 + n], True, True, ["onesd", ("osq", sk)], bn)
                    act(s1[:, c0:c0 + n], pbs[bn][:, 0:n], AF.Ln, [PB(bn)], [("s1", sk)], bias=EPS)
                    act(s1[:, c0:c0 + n], s1[:, c0:c0 + n], AF.Exp, [("s1", sk)], [("s1", sk)], scale=-0.5)
                    stt(s2[:, c0:c0 + n], pbs[bk][:, 0:n], small[:, 4 + h:5 + h], s1[:, c0:c0 + n], ALU.mult, ALU.mult,
                        [PB(bk), "small", ("s1", sk)], [("s2", sk)])
                tt(o_fin[:, h, 0:NT], s2[:, 0:NT], gbh[:, 0:NT], ALU.mult, K("s2") + [(gbk, "p"), (gbk, "s")],
                   [("o_fin", h, s_) for s_ in ("p", "s")])

            if PIPE:
                for j in range(4):
                    z_group(0, j)
            poolmix()
            ck("poolmix")
            P.barrier()
            ck("bar")
            if PIPE:
                for h in range(4):
                    p = h - 1
                    if h > 0:
                        R1(p)
                    z_evac(h)
                    g_mm(h, 0)
                    if h > 0:
                        R2_mm(p)
                    chain_a(h)
                    if h > 0:
                        R2_steps(p, 0, 4)
                        R2_half(p)
                    g_evac(h, 0)
                    if h < 3:
                        for j in range(4):
                            z_group(h + 1, j)
                    cb_scan(h)
                    if h > 0:
                        R2_steps(p, 4, 7)
                    cb_exp(h)
                    cb_mul(h)
                    cb_dec(h)
                    cb_keb(h)
                    if h < 3:
                        if h > 0:
                            R2_steps(p, 7, 8)
                            R2_fin(p)
                            R3(p)
                        g_mm(h, 1, (0,))
                        if h > 0:
                            R4(p)
                        g_mm(h, 1, (1,))
                        g_evac(h, 1)
                    else:
                        g_mm(h, 1)
                        R2_steps(p, 7, 8)
                        R2_fin(p)
                        R3(p)
                        g_evac(h, 1)
                        R4(p)
                    warm(AF.Sigmoid)
                R1(3); R2(3); R3(3); R4(3)
                warm(AF.Sigmoid)
            else:
                for h in range(4):
                    sample_prefetch(h)
                    for j in range(4):
                        z_group(h, j)
                    z_evac(h)
                    g_mm(h, 0)
                    g_mm(h, 1)
                    chain_a(h)
                    g_evac(h, 0)
                    chain_b(h)
                    R1(h)
                    g_evac(h, 1)
                    sample_bf(h)
                    R2_mm(h)
                    R2_steps(h, 0, 4)
                    R2_half(h)
                    R2_steps(h, 4, 8)
                    build_vm()
                    R2_fin(h)
                    R3(h); R4(h)
                    warm(AF.Sigmoid)
                    sample_state_update(h)

            ck("hgrn")
            for jh in range(2):
                wy, wyk = w_get(9 + jh)
                wyv = wy[:].rearrange("p (k t c) -> p k t c", k=4, t=2)
                for jj in range(4):
                    j = jh * 4 + jj
                    bs = nb() if has_s else None
                    b_ya, b_yb = nb(), nb()
                    for t_, bk, srcb, srck in ((0, b_ya, pool_out, "pool_out"), (1, b_yb, o_fin, "o_fin")):
                        for kk in range(4):
                            for (c0, n, sk) in subs:
                                o = pbs[bk][:, 0:512] if sk == "p" else pbs[bs][:, t_ * 128:(t_ + 1) * 128]
                                mm(o, wyv[:, kk, t_, jj * 128:(jj + 1) * 128], srcb[:, kk, c0:c0 + n], kk == 0, kk == 3,
                                   [wyk, (srck, kk, sk)], bk if sk == "p" else bs)
                    if jj == 3:
                        w_release(9 + jh)
                    for (c0, n, sk) in subs:
                        oa = pbs[b_ya][:, 0:512] if sk == "p" else pbs[bs][:, 0:128]
                        ob = pbs[b_yb][:, 0:512] if sk == "p" else pbs[bs][:, 128:256]
                        ka = PB(b_ya) if sk == "p" else PB(bs)
                        kb_ = PB(b_yb) if sk == "p" else PB(bs)
                        act(s1[:, c0:c0 + n], sgA[:, j, c0:c0 + n], AF.Sigmoid, [("sgA", j, sk)], [("s1", sk)])
                        act(s2[:, c0:c0 + n], sgB[:, j, c0:c0 + n], AF.Sigmoid, [("sgB", j, sk)], [("s2", sk)])
                        tt(s1[:, c0:c0 + n], s1[:, c0:c0 + n], oa, ALU.mult, [("s1", sk), ka], [("s1", sk)])
                        tt(s2[:, c0:c0 + n], s2[:, c0:c0 + n], ob, ALU.mult, [("s2", sk), kb_], [("s2", sk)])
                    tt(merged[:, j, 0:NT], s1[:, 0:NT], s2[:, 0:NT], ALU.add, K("s1") + K("s2"),
                       [("merged", j, s_) for s_ in ("p", "s")])

            ck("merge")
            warm(AF.Ln)
            for half in range(2):
                wo, wok = w_get(11 + half)
                wov = wo[:].rearrange("p (k c) -> p k c", k=8)
                bks = {b: nb() for b in blocks}
                for k in range(8):
                    for b in blocks:
                        sk = "p" if b < 4 else "s"
                        mm(pbs[bks[b]][:, 0:512], merged[:, k, b * 128:(b + 1) * 128], wov[:, k, :], k == 0, k == 7,
                           [wok, ("merged", k, sk)], bks[b])
                w_release(11 + half)
                for b in blocks:
                    tt(X[:, b, half * 512:(half + 1) * 512], X[:, b, half * 512:(half + 1) * 512], pbs[bks[b]][:, 0:512],
                       ALU.add, [(xk, b), PB(bks[b])], [(xk, b)])
                    half_stats(b, half)

            ck("wout")
            norm_T(1, True)
            warm(AF.Sigmoid)
            P.barrier()
            if ti + 1 < ntiles:
                for b in range(4):
                    load_x(ti + 1, b)
            for b in blocks:
                src = ppd[ti * 512 + b * 128: ti * 512 + (b + 1) * 128, :] if b < 4 else psd
                P.dma("sp", f"d_p{b}", lambda e, src=src, b=b: e.dma_start(out=pstage[:, b, :], in_=src), writes=[("pstage", b)])
            for jb in range(11):
                wf, wfk = w_get(13 + jb)
                wfv = wf[:].rearrange("p (k j c) -> p k j c", k=8, j=2)
                for cc in range(2):
                    f = 2 * jb + cc
                    bs = nb() if has_s else None
                    b_g, b_u = nb(), nb()
                    for jx, bk in ((0, b_g), (1, b_u)):
                        for k in range(8):
                            for (c0, n, sk) in subs:
                                o = pbs[bk][:, 0:512] if sk == "p" else pbs[bs][:, jx * 128:(jx + 1) * 128]
                                mm(o, wfv[:, k, jx, cc * 128:(cc + 1) * 128], hT[:, k, c0:c0 + n], k == 0, k == 7,
                                   [wfk] + hTk[sk], bk if sk == "p" else bs)
                    if cc == 1:
                        w_release(13 + jb)
                    for (c0, n, sk) in subs:
                        og = pbs[b_g][:, 0:512] if sk == "p" else pbs[bs][:, 0:128]
                        ou = pbs[b_u][:, 0:512] if sk == "p" else pbs[bs][:, 128:256]
                        kg = PB(b_g) if sk == "p" else PB(bs)
                        ku = PB(b_u) if sk == "p" else PB(bs)
                        act(ftmp[:, c0:c0 + n], og, AF.Sigmoid, [kg], [("ftmp", sk)])
                        tt(ftmp[:, c0:c0 + n], ftmp[:, c0:c0 + n], og, ALU.mult, [("ftmp", sk), kg], [("ftmp", sk)])
                        tt(hidden[:, f, c0:c0 + n], ftmp[:, c0:c0 + n], ou, ALU.mult, [("ftmp", sk), ku], [("hidden", f, sk)])
            for b in blocks:
                cp(pbf[:], pstage[:, b, :], [("pstage", b)], ["pbf"], eng="act")
                bt = nb()
                bv = pbs[bt][:].bitcast(BF16)
                for k in range(2):
                    tp(bv[:, k * 128:(k + 1) * 128], pbf[:, k * 128:(k + 1) * 128], identb[:], ["pbf", "identb"], bt)
                cp(pT[:, :, b * 128:(b + 1) * 128], bv[:, 0:256].rearrange("p (k t) -> p k t", k=2), [PB(bt)], [("pT", b)])
            warm(AF.Ln)
            for half in range(2):
                bks = {b: nb() for b in blocks}
                for kb in range(3):
                    wd, wdk = w_get(24 + half * 3 + kb)
                    wdv = wd[:].rearrange("p (k c) -> p k c", k=8)
                    nk = 8 if kb < 2 else 6
                    for kk in range(nk):
                        f = kb * 8 + kk
                        for b in blocks:
                            sk = "p" if b < 4 else "s"
                            mm(pbs[bks[b]][:, 0:512], hidden[:, f, b * 128:(b + 1) * 128], wdv[:, kk, :], f == 0, f == 21,
                               [wdk, ("hidden", f, sk)], bks[b])
                    w_release(24 + half * 3 + kb)
                for b in blocks:
                    tt(X[:, b, half * 512:(half + 1) * 512], X[:, b, half * 512:(half + 1) * 512], pbs[bks[b]][:, 0:512],
                       ALU.add, [(xk, b), PB(bks[b])], [(xk, b)])
                    half_stats(b, half)

            ck("ffn")
            norm_T(2, True)
            if ti + 1 < ntiles:
                Xn, xkn = xbuf(ti + 1)
                norm_stats(False, blks=[0, 1, 2, 3], X_=Xn, xk_=xkn, ssq_=ssq1, rst_=rst1, tag="ssq1", rtag="rst1")
            warm(AF.Sigmoid)
            wg0, wg0k = w_get(30)
            wpp_, wppk = w_get(31)
            wg1, wg1k = w_get(32)
            wppv = wpp_[:, 0:2048].rearrange("p (k c) -> p k c", k=2)
            for half in range(2):
                wg_, wgk = (wg0, wg0k) if half == 0 else (wg1, wg1k)
                wgv = wg_[:].rearrange("p (k c) -> p k c", k=8)
                bks = {b: nb() for b in blocks}
                for k in range(8):
                    for b in blocks:
                        mm(pbs[bks[b]][:, 0:512], hT[:, k, b * 128:(b + 1) * 128], wgv[:, k, :], k == 0, k == 7,
                           [wgk, ("hT", b)], bks[b])
                if half == 0:
                    w_release(30)
                for b in blocks:
                    sg = sgt[b % 2]
                    sgk = f"sgt{b % 2}"
                    act(sg[:], pbs[bks[b]][:, 0:512], AF.Sigmoid, [PB(bks[b])], [sgk])
                    be = nb()
                    for k in range(2):
                        mm(pbs[be][:, 0:512], pT[:, k, b * 128:(b + 1) * 128], wppv[:, k, half * 512:(half + 1) * 512], k == 0, k == 1,
                           [wppk, ("pT", b)], be)
                    tt(sg[:], sg[:], pbs[be][:, 0:512], ALU.mult, [sgk, PB(be)], [sgk])
                    tt(X[:, b, half * 512:(half + 1) * 512], X[:, b, half * 512:(half + 1) * 512], sg[:], ALU.add,
                       [(xk, b), sgk], [(xk, b)])
                for b in blocks:
                    half_stats(b, half)
                if half == 1:
                    w_release(31); w_release(32)

            ck("ple")
            if ti + 1 < ntiles:
                Xn, xkn = xbuf(ti + 1)
                norm_apply(0, blks=[0, 1, 2, 3], X_=Xn, xk_=xkn, rst_=rst1, rtag="rst1")
            P.barrier()
            norm_stats(True)
            for b in blocks:
                dst = yp[ti * 512 + b * 128: ti * 512 + (b + 1) * 128, :] if b < 4 else ys
                for half in range(2):
                    q_ = (2 * b + half) % 4
                    yst = ystage[q_]
                    ysk = f"ystage{q_}"
                    hs_ = slice(half * 512, (half + 1) * 512)
                    stt(yst[:], X[:, b, hs_], rst[:, b:b + 1], gB[:, 3, hs_], ALU.mult, ALU.mult,
                        [(xk, b), "rst", ("gB", 3)], [ysk])
                    P.dma("sp", f"d_y{q_}", lambda e, yst=yst, dst=dst, hs_=hs_: e.dma_start(out=dst[:, hs_], in_=yst[:]),
                          reads=[ysk])

        P.barrier()
        try:
            ck("setup")
            for ti in range(ntiles):
                run_tile(ti)
        except _Stop:
            pass
        P.finish("sp")
        P.emit()
    return nc


_PROG = {}


def _prep_inputs(inp):
    f = lambda a: np.ascontiguousarray(np.asarray(a, dtype=np.float32))
    w_in = f(inp["w_in"][0])
    wallv = build_wall(w_in, f(inp["w_pool_up"][0]), f(inp["w_hgrn_up"][0]), f(inp["w_out"][0]),
                       f(inp["w_ffn_gate"][0]), f(inp["w_ffn_up"][0]), f(inp["w_ffn_down"][0]),
                       f(inp["w_ple_gate"][0]), f(inp["w_ple_proj"][0]))
    gvec = np.ascontiguousarray(np.stack([f(inp["g_mix"][0]), f(inp["g_ffn"][0]), f(inp["g_ple"][0]), f(inp["g_final"])], 0))
    small = np.zeros((128, 16), np.float32)
    small[:, 0:4] = f(inp["pool_scale"][0]).reshape(4, 128).T
    small[:, 4:8] = f(inp["hgrn_norm"][0]).reshape(4, 128).T
    small[:, 8:12] = f(inp["hgrn_lb"][0]).reshape(4, 128).T
    small[:, 12:16] = f(inp["hgrn_lb"][1]).reshape(4, 128).T
    pmix = np.ascontiguousarray(f(inp["w_pool_mix"][0]).transpose(1, 0, 2)).reshape(128, 512)
    xp = f(inp["x_prompt"]); xsm = f(inp["x_sample"])
    ppr = f(inp["p_prompt"][0]); psm = f(inp["p_sample"][0])
    spl = f(inp["state_pool"][0]); shg = f(inp["state_hgrn"][0])
    maps = []
    for c in range(NCORES):
        maps.append({
            "xp": xp[c], "xs": xsm[16 * c:16 * c + 16].reshape(128, 1024),
            "pp": ppr[c], "ps": psm[16 * c:16 * c + 16].reshape(128, 256),
            "spool": spl[16 * c:16 * c + 16].reshape(240, 512),
            "shg": shg[16 * c:16 * c + 16],
            "wall": wallv, "cst": CONST_ARR, "gvec": gvec, "small": small, "pmix": pmix,
        })
    return maps


def kernel(**inputs):
    if "nc" not in _PROG:
        _PROG["nc"] = build_program()
    nc = _PROG["nc"]
    maps = _prep_inputs(inputs)
    res = run_bass_kernel_spmd(nc, maps, core_ids=list(range(NCORES)))
    R = res.results
    y_p = np.stack([R[c]["yp"] for c in range(NCORES)], 0).astype(np.float32)
    y_s = np.concatenate([R[c]["ys"].reshape(16, 8, 1024) for c in range(NCORES)], 0).astype(np.float32)
    npp = np.stack([R[c]["npp"] for c in range(NCORES)], 0)[None].astype(np.float32)
    nhp = np.stack([R[c]["nhp"] for c in range(NCORES)], 0)[None].astype(np.float32)
    nps = np.concatenate([R[c]["nps"].reshape(16, 15, 512) for c in range(NCORES)], 0)[None].astype(np.float32)
    nhs = np.concatenate([R[c]["nhs"] for c in range(NCORES)], 0)[None].astype(np.float32)
    return (y_p, y_s, npp, nhp, nps, nhs)
```

```python
import numpy as np
from contextlib import ExitStack
import concourse.bass as bass
import concourse.mybir as mybir
from concourse.bass_utils import run_bass_kernel_spmd

F32 = mybir.dt.float32
BF16 = mybir.dt.bfloat16
AF = mybir.ActivationFunctionType
ALU = mybir.AluOpType

ENGS = ("pe", "act", "dve", "pool", "sp")
EPS = 1e-6
NCORES = 8
NRING = 5
NTILES = 4
import os as _os
SAME_DIST = int(_os.environ.get("SAME_DIST", str(1 << 30)))


class Prog:
    def __init__(self, nc, stack, same_engine_wait=True):
        self.nc = nc
        self.stack = stack
        self.streams = {e: [] for e in ENGS}
        self.count = {e: 0 for e in ENGS}
        self.known = {e: {} for e in ENGS}
        self.sems = {}
        self.dma_count = {}
        self.lastw = {}
        self.readers = {}
        self.same_engine_wait = same_engine_wait

    def _collect(self, eng, reads, writes):
        deps = []
        for k in reads:
            ev = self.lastw.get(k)
            if ev is not None:
                deps.append(ev)
        for k in writes:
            ev = self.lastw.get(k)
            if ev is not None:
                deps.append(ev)
            rd = self.readers.get(k)
            if rd:
                deps.extend(rd.values())
        kn = self.known[eng]
        need = {}
        for (s, v, vc) in deps:
            if s == eng and (eng == "pe" or not self.same_engine_wait or self.count[eng] - v >= SAME_DIST):
                continue
            if kn.get(s, 0) >= v:
                continue
            if need.get(s, 0) < v:
                need[s] = v
            for s2, v2 in vc.items():
                if s2 == eng:
                    continue
                if kn.get(s2, 0) < v2:
                    kn[s2] = v2
        waits = []
        for s, v in need.items():
            waits.append((s, v))
            if kn.get(s, 0) < v:
                kn[s] = v
        return waits

    def _record(self, ev, reads, writes):
        s = ev[0]
        for k in reads:
            self.readers.setdefault(k, {})[s] = ev
        for k in writes:
            self.lastw[k] = ev
            self.readers[k] = {}

    def op(self, eng, fn, reads=(), writes=()):
        waits = self._collect(eng, reads, writes)
        self.count[eng] += 1
        n = self.count[eng]
        vc = dict(self.known[eng])
        vc[eng] = n
        ev = (eng, n, vc)
        self.streams[eng].append((waits, fn, (eng, 1)))
        self._record(ev, reads, writes)
        return ev

    def dma(self, q, semname, fn, reads=(), writes=()):
        waits = self._collect(q, reads, writes)
        self.dma_count[semname] = self.dma_count.get(semname, 0) + 1
        v = 16 * self.dma_count[semname]
        vc = dict(self.known[q])
        vc[semname] = v
        ev = (semname, v, vc)
        self.streams[q].append((waits, fn, (semname, 16)))
        self._record(ev, reads, writes)
        return ev

    def barrier(self, engines=("act", "dve", "sp")):
        for e in engines:
            kn = self.known[e]
            waits = []
            for e2 in ("pe", "act", "dve"):
                c = self.count[e2]
                if c and kn.get(e2, 0) < c:
                    waits.append((e2, c))
                    kn[e2] = c
            for s, c in self.dma_count.items():
                if s.startswith("ring") or s.startswith("d_y") or s.startswith("d_x"):
                    continue
                if kn.get(s, 0) < 16 * c:
                    waits.append((s, 16 * c))
                    kn[s] = 16 * c
            if waits:
                self.streams[e].append((waits, None, None))

    def finish(self, eng="sp"):
        kn = self.known[eng]
        waits = []
        for s, c in self.dma_count.items():
            if kn.get(s, 0) < 16 * c:
                waits.append((s, 16 * c))
                kn[s] = 16 * c
        for e in ("pe", "act", "dve"):
            if self.count[e] and kn.get(e, 0) < self.count[e]:
                waits.append((e, self.count[e]))
        self.streams[eng].append((waits, None, None))

    def emit(self):
        nc = self.nc
        for s in list(ENGS) + list(self.dma_count):
            if s not in self.sems:
                self.sems[s] = self.stack.enter_context(nc.semaphore(s))
        with nc.Block() as block:
            def run(engine, items):
                for waits, fn, inc in items:
                    for s, v in waits:
                        engine.wait_ge(self.sems[s], v)
                    if fn is not None:
                        ins = fn(engine)
                        ins.then_inc(self.sems[inc[0]], inc[1])

            @block.tensor
            def _(eng):
                run(eng, self.streams["pe"])

            @block.scalar
            def _(eng):
                run(eng, self.streams["act"])

            @block.vector
            def _(eng):
                run(eng, self.streams["dve"])

            @block.gpsimd
            def _(eng):
                run(eng, self.streams["pool"])

            @block.sync
            def _(eng):
                run(eng, self.streams["sp"])


NBLK = 33


def _kc(W, nk):
    C = W.shape[1]
    return np.ascontiguousarray(W.reshape(nk, 128, C).transpose(1, 0, 2)).reshape(128, nk * C)


def _pad(a):
    out = np.zeros((128, 4096), np.float32)
    out[:, : a.shape[1]] = a
    return out


def build_wall(w_in, w_pool_up, w_hgrn_up, w_out, w_g, w_u, w_d, w_pg, w_pp):
    blocks = []
    blocks.append(_kc(w_in[:, 0:512], 8))
    zz = w_in[:, 512:2560].reshape(8, 128, 4, 4, 128)
    ga = w_in[:, 2560:3584].reshape(8, 128, 8, 128)
    gb = w_in[:, 3584:4608].reshape(8, 128, 8, 128)
    for h in range(4):
        blocks.append(np.ascontiguousarray(zz[:, :, :, h, :].transpose(1, 0, 2, 3)).reshape(128, 4096))
        gg = np.stack([ga[:, :, 2 * h], gb[:, :, 2 * h], ga[:, :, 2 * h + 1], gb[:, :, 2 * h + 1]], axis=2)
        blocks.append(np.ascontiguousarray(gg.transpose(1, 0, 2, 3)).reshape(128, 4096))
    for half in range(2):
        yy = np.stack([w_pool_up[:, half * 512:(half + 1) * 512].reshape(4, 128, 512),
                       w_hgrn_up[:, half * 512:(half + 1) * 512].reshape(4, 128, 512)], axis=2)
        blocks.append(np.ascontiguousarray(yy.transpose(1, 0, 2, 3)).reshape(128, 4096))
    blocks.append(_kc(w_out[:, 0:512], 8))
    blocks.append(_kc(w_out[:, 512:1024], 8))
    for jb in range(11):
        gu = np.stack([w_g[:, jb * 256:(jb + 1) * 256], w_u[:, jb * 256:(jb + 1) * 256]], axis=1)
        blocks.append(_kc(gu.reshape(1024, 512), 8))
    for half in range(2):
        for kb in range(3):
            nk = 8 if kb < 2 else 6
            blocks.append(_pad(_kc(w_d[kb * 1024: kb * 1024 + nk * 128, half * 512:(half + 1) * 512], nk)))
    blocks.append(_kc(w_pg[:, 0:512], 8))
    blocks.append(_pad(_kc(w_pp, 2)))
    blocks.append(_kc(w_pg[:, 512:1024], 8))
    assert len(blocks) == NBLK
    return np.ascontiguousarray(np.stack(blocks, axis=0).astype(np.float32))


def build_consts():
    c = {}
    c["ident"] = np.eye(128, dtype=np.float32)
    s = np.arange(128)[:, None]
    t = np.arange(128)[None, :]
    c["maskP"] = ((s // 64 == t // 64) & (s <= t)).astype(np.float32)
    c["maskS"] = ((s // 8 == t // 8) & (s <= t)).astype(np.float32)
    c["seqm"] = (s // 8 == np.arange(16)[None, :]).astype(np.float32)
    r = np.ones(640, np.float32)
    r[0:512:64] = 0.0
    r[512:640:8] = 0.0
    c["resetm"] = np.broadcast_to(r, (128, 640)).copy()
    rc = np.zeros((4, 16), np.float32)
    for g, w in enumerate((2, 4, 8, 16)):
        rc[g] = 1.0 / np.minimum(np.arange(16) + 1, w)
    c["rc"] = np.broadcast_to(rc.reshape(1, 64), (128, 64)).copy()
    order = ["ident", "seqm", "rc", "maskP", "maskS", "resetm"]
    offs = {}
    o = 0
    for k in order:
        offs[k] = (o, c[k].shape[1])
        o += c[k].shape[1]
    return np.ascontiguousarray(np.concatenate([c[k] for k in order], axis=1)), offs


CONST_ARR, COFF = build_consts()
CW = CONST_ARR.shape[1]
CW_KEEP = COFF["maskP"][0]


class _Stop(Exception):
    pass


STOP = None


def ck(name):
    if STOP == name:
        raise _Stop()


def build_program(ntiles=NTILES):
    nc = bass.Bass("TRN2", target_bir_lowering=False)

    def din(name, shape):
        return nc.dram_tensor(name, shape, F32, kind="ExternalInput").ap()

    def dout(name, shape):
        return nc.dram_tensor(name, shape, F32, kind="ExternalOutput").ap()

    xp = din("xp", [2048, 1024])
    xs = din("xs", [128, 1024])
    ppd = din("pp", [2048, 256])
    psd = din("ps", [128, 256])
    spool = din("spool", [240, 512])
    shg = din("shg", [16, 4, 128, 128])
    wall = din("wall", [NBLK, 128, 4096])
    cst = din("cst", [128, CW])
    gvec = din("gvec", [4, 1024])
    smalld = din("small", [128, 16])
    pmixd = din("pmix", [128, 512])
    yp = dout("yp", [2048, 1024])
    ys = dout("ys", [128, 1024])
    npp = dout("npp", [15, 512])
    nhp = dout("nhp", [4, 128, 128])
    nps = dout("nps", [240, 512])
    nhs = dout("nhs", [16, 4, 128, 128])

    with ExitStack() as st:
        P = Prog(nc, st)

        def sb(name, shape, dt):
            return st.enter_context(nc.sbuf_tensor("sb_" + name, shape, dt))

        xres = sb("xres", [128, 5, 1024], F32)
        hT = sb("hT", [128, 8, 640], BF16)
        hn = [sb(f"hn{i}", [128, 1024], BF16) for i in range(2)]
        junk = sb("junk", [128, 512], BF16)
        ystage = [sb(f"ystage{i}", [128, 512], F32) for i in range(4)]
        resetmb = sb("resetmb", [128, 640], BF16)
        ring = [sb(f"ring{i}", [128, 4096], BF16) for i in range(NRING)]
        gB = sb("gB", [128, 4, 1024], F32)
        cs = sb("cs", [128, CW_KEEP], F32)
        identb = sb("identb", [128, 128], BF16)
        onesd = sb("onesd", [128, 128], BF16)
        maskPb = sb("maskPb", [128, 128], BF16)
        maskSb = sb("maskSb", [128, 128], BF16)
        pmixf = sb("pmixf", [128, 512], F32)
        pmix = sb("pmix", [128, 4, 128], BF16)
        small = sb("small", [128, 16], F32)
        lbv = sb("lbv", [128, 8], F32)
        carry = sb("carry", [128, 4, 128], F32)
        ucarry = sb("ucarry", [128, 4, 16], F32)
        ssq = sb("ssq", [128, 16], F32)
        ssq1 = sb("ssq1", [128, 8], F32)
        dmy = sb("dmy", [128, 2], F32)
        rst1 = sb("rst1", [128, 4], F32)
        rst = sb("rst", [128, 8], F32)
        identf = cs[:, COFF["ident"][0]:COFF["ident"][0] + 128]
        seqm = cs[:, COFF["seqm"][0]:COFF["seqm"][0] + 16]
        rcv = cs[:, COFF["rc"][0]:COFF["rc"][0] + 64]
        resetm = resetmb

        MIX_BYTES = 0
        scr_plan = {}

        def plan(phase, name, nelem, dt):
            nonlocal MIX_BYTES
            nbytes = nelem * (4 if dt == F32 else 2)
            nbytes = (nbytes + 31) // 32 * 32
            off = scr_plan.setdefault(("off", phase), 0)
            scr_plan[name] = (off, nelem, dt)
            scr_plan[("off", phase)] = off + nbytes

        U_NAMES = ["ug", "ue", "pa", "pb_", "sa", "sb_", "tmp16", "pooled", "npsb", "stg", "npp_sb"]
        for name, n, dt in [
            ("ug", 528, F32), ("ue", 16 * 24, F32), ("pa", 528, F32), ("pb_", 528, F32),
            ("sa", 16 * 24, F32), ("sb_", 16 * 24, F32), ("tmp16", 16, F32),
            ("pooled", 4 * 640, BF16), ("npsb", 4 * 240, F32),
            ("stg", 2 * 512, F32), ("npp_sb", 512, F32), ("pool_out", 4 * 640, BF16),
            ("qf", 640, F32), ("t1", 640, F32), ("t2", 640, F32), ("t3", 640, F32), ("t4", 640, F32),
            ("qe", 640, BF16), ("qe2", 640, BF16), ("keb", 640, BF16), ("kdb", 640, BF16), ("vb", 640, BF16), ("gbf", 640, BF16), ("gbf2", 640, BF16),
            ("kdT", 5 * 128, BF16), ("vT", 5 * 128, BF16), ("Sall", 9 * 128, F32), ("Sbf", 8 * 128, BF16),
            ("dec", 24, F32), ("dec2", 24, F32), ("Am", 5 * 128, BF16), ("osq", 640, BF16), ("o_fin", 4 * 640, BF16),
            ("S0", 16 * 128, F32), ("S0bf", 16 * 128, BF16), ("Vm", 16 * 128, BF16),
            ("merged", 8 * 640, BF16), ("s1", 640, F32), ("s2", 640, F32),
        ]:
            plan("mix", name, n, dt)
        for name, n, dt in [
            ("hidden", 22 * 640, BF16), ("ftmp", 640, F32), ("sgt0", 512, F32), ("sgt1", 512, F32),
            ("pT", 2 * 640, BF16), ("pstage", 5 * 256, F32), ("pbf", 256, BF16),
        ]:
            plan("ffn", name, n, dt)
        u_end = scr_plan["pool_out"][0]
        assert 2 * 8 * 640 * 2 <= u_end, u_end
        scr_plan["sgA"] = (0, 8 * 640, BF16)
        scr_plan["sgB"] = (8 * 640 * 2, 8 * 640, BF16)
        scr_bytes = max(scr_plan[("off", "mix")], scr_plan[("off", "ffn")])
        scr = sb("scr", [128, scr_bytes // 4], F32)

        def sv(name):
            off, n, dt = scr_plan[name]
            if dt == F32:
                return scr[:, off // 4: off // 4 + n]
            return scr[:, off // 4: off // 4 + n // 2].bitcast(BF16)

        ug = sv("ug"); ue = sv("ue").rearrange("p (i r) -> p i r", r=24)
        pa = sv("pa"); pb_ = sv("pb_")
        sa = sv("sa").rearrange("p (i r) -> p i r", r=24); sb_ = sv("sb_").rearrange("p (i r) -> p i r", r=24)
        tmp16 = sv("tmp16")
        pooled = sv("pooled").rearrange("p (g t) -> p g t", g=4)
        pool_out = sv("pool_out").rearrange("p (g t) -> p g t", g=4)
        npsb = sv("npsb").rearrange("p (g r) -> p g r", g=4)
        stg = sv("stg").rearrange("p (h c) -> p h c", h=2)
        nps_sb = stg
        npp_sb = sv("npp_sb")
        qf = sv("qf"); t1 = sv("t1"); t2 = sv("t2"); t3 = sv("t3"); t4 = sv("t4")
        qe = sv("qe"); qe2 = sv("qe2"); keb = sv("keb"); kdb = sv("kdb"); vb = sv("vb"); gbf = sv("gbf"); gbf2 = sv("gbf2")
        kdT = sv("kdT").rearrange("p (b k) -> p b k", b=5); vT = sv("vT").rearrange("p (b k) -> p b k", b=5)
        Sall = sv("Sall").rearrange("p (c v) -> p c v", c=9); Sbf = sv("Sbf").rearrange("p (c v) -> p c v", c=8)
        dec = sv("dec"); dec2 = sv("dec2"); Am = sv("Am").rearrange("p (b k) -> p b k", b=5); osq = sv("osq")
        o_fin = sv("o_fin").rearrange("p (h t) -> p h t", h=4)
        S0 = sv("S0").rearrange("p (i v) -> p i v", i=16); S0bf = sv("S0bf").rearrange("p (i v) -> p i v", i=16)
        Vm = sv("Vm").rearrange("p (i v) -> p i v", i=16)
        merged = sv("merged").rearrange("p (k t) -> p k t", k=8); s1 = sv("s1"); s2 = sv("s2")
        sgA = sv("sgA").rearrange("p (k t) -> p k t", k=8); sgB = sv("sgB").rearrange("p (k t) -> p k t", k=8)
        hidden = sv("hidden").rearrange("p (f t) -> p f t", f=22); ftmp = sv("ftmp")
        assert scr_plan["S0bf"][0] == scr_plan["S0"][0] + 8192 and scr_plan["Vm"][0] == scr_plan["S0"][0] + 12288
        assert scr_plan["S0"][0] >= scr_plan[("off", "ffn")]
        xo = scr_plan["S0"][0] // 4
        xalt = scr[:, xo:xo + 4096].rearrange("p (b d) -> p b d", b=4)
        sgt = [sv("sgt0"), sv("sgt1")]
        pT = sv("pT").rearrange("p (k t) -> p k t", k=2); pstage = sv("pstage").rearrange("p (b c) -> p b c", b=5); pbf = sv("pbf")

        pbs = [st.enter_context(nc.psum_tensor(f"pb{i}", [128, 512], F32)) for i in range(8)]
        bank_ctr = [0]

        def nb():
            b = bank_ctr[0] % 8
            bank_ctr[0] += 1
            return b

        def PB(b):
            return ("pb", b)

        wstate = {"tile": 0, "issued": set(), "released": set(), "total": NBLK * ntiles}

        def w_issue_n(n, extra_reads=()):
            if n >= wstate["total"] or n in wstate["issued"]:
                return
            slot = n % NRING
            blk = n % NBLK
            P.dma("pool", f"ring{slot}",
                  lambda e, slot=slot, blk=blk: e.dma_start(out=ring[slot][:], in_=wall[blk]),
                  reads=list(extra_reads), writes=[("ring", slot)])
            wstate["issued"].add(n)

        def w_issue(extra_reads=()):
            w_issue_n(len(wstate["issued"]), extra_reads)

        def w_get(expect):
            n = wstate["tile"] * NBLK + expect
            assert n in wstate["issued"], ("ring too small / block not prefetched", n)
            assert n - NRING < 0 or (n - NRING) in wstate["released"]
            slot = n % NRING
            return ring[slot], ("ring", slot)

        def w_release(expect):
            n = wstate["tile"] * NBLK + expect
            assert n in wstate["issued"] and n not in wstate["released"]
            wstate["released"].add(n)
            w_issue_n(n + NRING)

        def A(eng, fn, reads, writes):
            return P.op(eng, fn, reads=reads, writes=writes)

        def mm(out, lhsT, rhs, start, stop, reads, bank):
            A("pe", lambda e: e.matmul(out, lhsT=lhsT, rhs=rhs, start=start, stop=stop), reads, [PB(bank)])

        def tp(out, in_, ident, reads, bank):
            A("pe", lambda e: e.transpose(out=out, in_=in_, identity=ident), reads, [PB(bank)])

        def act(out, in_, func, reads, writes, **kw):
            A("act", lambda e: e.activation(out=out, in_=in_, func=func, **kw), reads, writes)

        def tt(out, in0, in1, op, reads, writes, eng="dve"):
            A(eng, lambda e: e.tensor_tensor(out=out, in0=in0, in1=in1, op=op), reads, writes)

        def ts(out, in0, s1_, s2_, op0, op1, reads, writes):
            A("dve", lambda e: e.tensor_scalar(out=out, in0=in0, scalar1=s1_, scalar2=s2_, op0=op0, op1=op1), reads, writes)

        def stt(out, in0, scalar, in1, op0, op1, reads, writes):
            A("dve", lambda e: e.scalar_tensor_tensor(out=out, in0=in0, scalar=scalar, in1=in1, op0=op0, op1=op1), reads, writes)

        def cp(out, in_, reads, writes, eng="dve"):
            if eng == "act":
                act(out, in_, AF.Copy, reads, writes)
            else:
                A(eng, lambda e: e.tensor_copy(out=out, in_=in_), reads, writes)

        def warm(func):
            act(dmy[:, 1:2], dmy[:, 0:1], func, ["dmy0"], ["dmy1"])

        A("dve", lambda e: e.memset(dmy[:], 1.0), [], ["dmy0", "dmy1"])
        P.dma("sp", "d_cs", lambda e: e.dma_start(out=cs[:], in_=cst[:, 0:CW_KEEP]), writes=["cs"])
        cstage = scr[:, 0:CW - CW_KEEP]
        P.dma("sp", "d_cs2", lambda e: e.dma_start(out=cstage, in_=cst[:, CW_KEEP:CW]), writes=["cstage"])
        maskPf = cstage[:, 0:128]
        maskSf = cstage[:, 128:256]
        resetmf = cstage[:, 256:896]
        P.dma("sp", "d_small", lambda e: e.dma_start(out=small[:], in_=smalld), writes=["small"])
        P.dma("sp", "d_pmix", lambda e: e.dma_start(out=pmixf[:], in_=pmixd), writes=["pmixf"])
        for gi in range(4):
            P.dma("sp", f"d_g{gi}",
                  lambda e, gi=gi: e.dma_start(out=gB[:, gi, :], in_=gvec[gi:gi + 1, :].broadcast_to([128, 1024])),
                  writes=[("gB", gi)])
        cp(identb[:], identf, ["cs"], ["identb"])
        cp(maskPb[:], maskPf, ["cstage"], ["maskPb"])
        cp(maskSb[:], maskSf, ["cstage"], ["maskSb"])
        cp(resetmb[:], resetmf, ["cstage"], ["resetmb"])
        cp(pmix[:].rearrange("p g c -> p (g c)"), pmixf[:], ["pmixf"], ["pmix"])
        A("dve", lambda e: e.memset(onesd[:], 1.0 / 128.0), [], ["onesd"])
        A("dve", lambda e: e.memset(carry[:].rearrange("p h v -> p (h v)"), 0.0), [], ["carry"])
        A("dve", lambda e: e.memset(ucarry[:].rearrange("p g t -> p (g t)"), 0.0), [], ["ucarry"])
        tt(lbv[:, 0:4], small[:, 8:12], small[:, 12:16], ALU.subtract, ["small"], ["lbv"])
        act(lbv[:, 0:4], lbv[:, 0:4], AF.Sigmoid, ["lbv"], ["lbv"])
        ts(lbv[:, 4:8], lbv[:, 0:4], -1.0, 1.0, ALU.mult, ALU.add, ["lbv"], ["lbv"])

        def run_tile(ti):
            wstate["tile"] = ti
            has_s = ti == 0
            last = ti == 3
            NT = 640 if has_s else 512
            blocks = [0, 1, 2, 3] + ([4] if has_s else [])
            subs = [(0, 512, "p")] + ([(512, 128, "s")] if has_s else [])
            hTk_p = [("hT", b) for b in range(4)]
            hTk = {"p": hTk_p, "s": [("hT", 4)]}

            def K(name):
                return [(name, "p")] + ([(name, "s")] if has_s else [])

            def xbuf(tj):
                return (xres, "xres") if tj % 2 == 0 else (xalt, "xalt")

            X, xk = xbuf(ti)

            def load_x(tj, b):
                src = xp[tj * 512 + b * 128: tj * 512 + (b + 1) * 128, :] if b < 4 else xs
                Xn, xkn = xbuf(tj)
                P.dma("sp", f"d_x{tj % 2}_{b}", lambda e, b=b, src=src, Xn=Xn: e.dma_start(out=Xn[:, b, :], in_=src),
                      writes=[(xkn, b)])

            if ti == 0:
                for b in blocks:
                    load_x(0, b)
            if ti == 0:
                for _ in range(NRING):
                    w_issue(extra_reads=[(xk, b) for b in blocks])
            if has_s:
                for half in range(2):
                    P.dma("sp", f"d_stg{half}",
                          lambda e, half=half: e.dma_start(out=stg[0:120, half, :], in_=spool[half * 120:(half + 1) * 120, :]),
                          writes=[("stg", half)])

            def half_stats(b, half, X_=None, xk_=None, ssq_=None, tag="ssq"):
                X_ = X if X_ is None else X_
                xk_ = xk if xk_ is None else xk_
                ssq_ = ssq if ssq_ is None else ssq_
                act(junk[:, 0:512], X_[:, b, half * 512:(half + 1) * 512], AF.Square, [(xk_, b)], ["junk", (tag, b, half)],
                    accum_out=ssq_[:, 2 * b + half:2 * b + half + 1])

            def norm_stats(have_partials, blks=None, X_=None, xk_=None, ssq_=None, rst_=None, tag="ssq", rtag="rst"):
                blks = blocks if blks is None else blks
                ssq_ = ssq if ssq_ is None else ssq_
                rst_ = rst if rst_ is None else rst_
                if not have_partials:
                    for b in blks:
                        for half in range(2):
                            half_stats(b, half, X_, xk_, ssq_, tag)
                nbk = len(blks)
                tt(rst_[:, 0:nbk], ssq_[:, 0:2 * nbk:2], ssq_[:, 1:2 * nbk:2], ALU.add,
                   [(tag, b, hf) for b in blks for hf in range(2)], [rtag])
                act(rst_[:, 0:nbk], rst_[:, 0:nbk], AF.Ln, [rtag], [rtag], scale=1.0 / 1024.0, bias=EPS)
                act(rst_[:, 0:nbk], rst_[:, 0:nbk], AF.Exp, [rtag], [rtag], scale=-0.5)

            def norm_apply(gi, blks=None, X_=None, xk_=None, rst_=None, rtag="rst"):
                blks = blocks if blks is None else blks
                X_ = X if X_ is None else X_
                xk_ = xk if xk_ is None else xk_
                rst_ = rst if rst_ is None else rst_
                for b in blks:
                    hb = hn[b % 2]
                    hk = f"hn{b % 2}"
                    stt(hb[:], X_[:, b, :], rst_[:, b:b + 1], gB[:, gi, :], ALU.mult, ALU.mult,
                        [(xk_, b), rtag, ("gB", gi)], [hk])
                    bank = nb()
                    bv = pbs[bank][:].bitcast(BF16)
                    for k in range(8):
                        tp(bv[:, k * 128:(k + 1) * 128], hb[:, k * 128:(k + 1) * 128], identb[:], [hk, "identb"], bank)
                    c0 = b * 128
                    cp(hT[:, :, c0:c0 + 128], bv.rearrange("p (k t) -> p k t", k=8), [PB(bank)], [("hT", b)], eng="act")

            def norm_T(gi, have_partials):
                norm_stats(have_partials)
                norm_apply(gi)

            ck("load")
            if ti == 0:
                norm_T(0, False)
            ck("norm1")

            wv, wk = w_get(0)
            wu = wv[:].rearrange("p (k c) -> p k c", k=8)
            ubanks = []
            for g in range(4):
                bp = nb()
                bs = nb() if has_s else None
                for k in range(8):
                    mm(pbs[bp][:, 0:512], wu[:, k, g * 128:(g + 1) * 128], hT[:, k, 0:512], k == 0, k == 7, [wk] + hTk_p, bp)
                    if has_s:
                        mm(pbs[bs][:, 0:128], wu[:, k, g * 128:(g + 1) * 128], hT[:, k, 512:640], k == 0, k == 7, [wk, ("hT", 4)], bs)
                ubanks.append((bp, bs))
                if g == 3:
                    w_release(0)
                w = 2 << g
                cp(ug[:, 0:16], ucarry[:, g, :], ["ucarry"], ["ug"])
                cp(ug[:, 16:528], pbs[bp][:, 0:512], [PB(bp)], ["ug"], eng="act")
                tt(pa[:, 1:528], ug[:, 1:528], ug[:, 0:527], ALU.add, ["ug"], ["pa"])
                sw = pa
                swk = "pa"
                if w >= 4:
                    tt(pb_[:, 3:528], pa[:, 3:528], pa[:, 1:526], ALU.add, ["pa"], ["pb_"])
                    sw, swk = pb_, "pb_"
                if w >= 8:
                    tt(pa[:, 7:528], pb_[:, 7:528], pb_[:, 3:524], ALU.add, ["pb_"], ["pa"])
                    sw, swk = pa, "pa"
                if w >= 16:
                    tt(pb_[:, 15:528], pa[:, 15:528], pa[:, 7:520], ALU.add, ["pa"], ["pb_"])
                    sw, swk = pb_, "pb_"
                stt(pooled[:, g, 0:512], sw[:, 16:528], 1.0 / w, ug[:, 16:528], ALU.mult, ALU.subtract,
                    [swk, "ug"], [("pooled", g, "p")])
                if ti == 0:
                    tt(tmp16[:], sw[:, 16:32], rcv[:, g * 16:(g + 1) * 16], ALU.mult, [swk, "cs"], ["tmp16"])
                    tt(pooled[:, g, 0:16], tmp16[:], ug[:, 16:32], ALU.subtract, ["tmp16", "ug"], [("pooled", g, "p")])
                cp(ucarry[:, g, :], ug[:, 512:528], ["ug"], ["ucarry"])
                if last:
                    bt = nb()
                    tp(pbs[bt][0:15, 0:128], ug[:, 513:528], identf, ["ug", "cs"], bt)
                    cp(npp_sb[0:15, g * 128:(g + 1) * 128], pbs[bt][0:15, 0:128], [PB(bt)], ["npp_sb"], eng="act")
                if has_s:
                    bt = nb()
                    for half in range(2):
                        tp(pbs[bt][:, half * 120:(half + 1) * 120], stg[0:120, half, g * 128:(g + 1) * 128],
                           identf[0:120, 0:120], [("stg", half), "cs"], bt)
                    cp(ue[:, :, 0:15], pbs[bt][:, 0:240].rearrange("p (i r) -> p i r", r=15), [PB(bt)], ["ue"], eng="act")
                    cp(ue[:, :, 15:23], pbs[bs][:, 0:128].rearrange("p (i t) -> p i t", t=8), [PB(bs)], ["ue"], eng="act")
                    tt(sa[:, :, 1:23], ue[:, :, 1:23], ue[:, :, 0:22], ALU.add, ["ue"], ["sa"])
                    ssw, sswk = sa, "sa"
                    if w >= 4:
                        tt(sb_[:, :, 3:23], sa[:, :, 3:23], sa[:, :, 1:21], ALU.add, ["sa"], ["sb_"])
                        ssw, sswk = sb_, "sb_"
                    if w >= 8:
                        tt(sa[:, :, 7:23], sb_[:, :, 7:23], sb_[:, :, 3:19], ALU.add, ["sb_"], ["sa"])
                        ssw, sswk = sa, "sa"
                    if w >= 16:
                        tt(sb_[:, :, 15:23], sa[:, :, 15:23], sa[:, :, 7:15], ALU.add, ["sa"], ["sb_"])
                        ssw, sswk = sb_, "sb_"
                    stt(pooled[:, g, 512:640].rearrange("p (i t) -> p i t", t=8), ssw[:, :, 15:23], 1.0 / w,
                        ue[:, :, 15:23], ALU.mult, ALU.subtract, [sswk, "ue"], [("pooled", g, "s")])
                    cp(npsb[:, g, :].rearrange("p (i r) -> p i r", r=15), ue[:, :, 8:23], ["ue"], [("npsb", g)])
            if last:
                P.dma("sp", "d_npp", lambda e: e.dma_start(out=npp, in_=npp_sb[0:15, :]), reads=["npp_sb"])
            if has_s:
                for half in range(2):
                    bt = nb()
                    for g in range(4):
                        tp(pbs[bt][0:120, g * 128:(g + 1) * 128], npsb[:, g, half * 120:(half + 1) * 120], identf,
                           [("npsb", g), "cs"], bt)
                    cp(nps_sb[0:120, half, :], pbs[bt][0:120, 0:512], [PB(bt)], [("stg", half)], eng="act")
                    P.dma("sp", f"d_nps{half}",
                          lambda e, half=half: e.dma_start(out=nps[half * 120:(half + 1) * 120, :], in_=nps_sb[0:120, half, :]),
                          reads=[("stg", half)])
            ck("u")
            PIPE = not has_s
            if PIPE:
                zc, rcn = [0], [0]

                def nbZ():
                    b = zc[0] % 4
                    zc[0] += 1
                    return b

                def nbR():
                    b = 6 + rcn[0] % 2
                    rcn[0] += 1
                    return b

                gcn = [0]

                def nbG():
                    b = 4 + gcn[0] % 2
                    gcn[0] += 1
                    return b
            else:
                nbZ = nbR = nbG = nb
            gb2 = [gbf, gbf2]
            QE = [qe, qe2]
            DEC = [dec, dec2]
            HS = {h: {} for h in range(4)}

            def poolmix():
                for g in range(4):
                    for (c0, n, sk) in subs:
                        bk = nbR()
                        mm(pbs[bk][:, 0:n], pmix[:, g, :], pooled[:, g, c0:c0 + n], True, True, ["pmix", ("pooled", g, sk)], bk)
                        act(pool_out[:, g, c0:c0 + n], pbs[bk][:, 0:n], AF.Copy, [PB(bk), "small"], [("pool_out", g, sk)],
                            scale=small[:, g:g + 1])

            def z_group(h, j):
                S_ = HS[h]
                if j == 0:
                    wv, wk = w_get(1 + 2 * h)
                    S_["wh"] = wv[:].rearrange("p (k j c) -> p k j c", k=8, j=4)
                    S_["wk"] = wk
                    S_["bs"] = nbZ() if has_s else None
                    S_["zb"] = []
                wh, wk, bs = S_["wh"], S_["wk"], S_["bs"]
                bp = nbZ()
                for k in range(8):
                    mm(pbs[bp][:, 0:512], wh[:, k, j, :], hT[:, k, 0:512], k == 0, k == 7, [wk] + hTk_p, bp)
                    if has_s:
                        mm(pbs[bs][:, j * 128:(j + 1) * 128], wh[:, k, j, :], hT[:, k, 512:640], k == 0, k == 7,
                           [wk, ("hT", 4)], bs)
                S_["zb"].append(bp)
                if j == 3:
                    w_release(1 + 2 * h)

            def z_evac(h):
                S_ = HS[h]
                zb, bs = S_["zb"], S_["bs"]
                gbh = gb2[h % 2]
                gbk = f"gbf{h % 2}"

                def zsrc(j, sk):
                    return (pbs[zb[j]][:, 0:512], PB(zb[j])) if sk == "p" else (pbs[bs][:, j * 128:(j + 1) * 128], PB(bs))

                for (c0, n, sk) in subs:
                    src, key = zsrc(1, sk)
                    act(t1[:, c0:c0 + n], src, AF.Sigmoid, [key], [("t1", sk)])
                    src, key = zsrc(0, sk)
                    act(qf[:, c0:c0 + n], src, AF.Sigmoid, [key], [("qf", sk)])
                    tt(qf[:, c0:c0 + n], qf[:, c0:c0 + n], src, ALU.mult, [("qf", sk), key], [("qf", sk)])
                    src, key = zsrc(3, sk)
                    act(t4[:, c0:c0 + n], src, AF.Sigmoid, [key], [("t4", sk)])
                    tt(gbh[:, c0:c0 + n], t4[:, c0:c0 + n], src, ALU.mult, [("t4", sk), key], [(gbk, sk)])
                    src, key = zsrc(2, sk)
                    cp(vb[:, c0:c0 + n], src, [key], [("vb", sk)])
                warm(AF.Ln)

            def g_mm(h, jj, tsel=(0, 1)):
                S_ = HS[h]
                if jj == 0 and tsel[0] == 0:
                    wgv_, wgk_ = w_get(2 + 2 * h)
                    S_["wgv"] = wgv_[:].rearrange("p (k t c) -> p k t c", k=8, t=4)
                    S_["wgk"] = wgk_
                    S_["gate_ev"] = []
                wgv, wgk_ = S_["wgv"], S_["wgk"]
                j = 2 * h + jj
                if tsel[0] == 0:
                    S_["bsg"] = nbG() if has_s else None
                bsg = S_["bsg"]
                for t_ in tsel:
                    bpg = nbG()
                    for k in range(8):
                        mm(pbs[bpg][:, 0:512], wgv[:, k, 2 * jj + t_, :], hT[:, k, 0:512], k == 0, k == 7, [wgk_] + hTk_p, bpg)
                        if has_s:
                            mm(pbs[bsg][:, t_ * 128:(t_ + 1) * 128], wgv[:, k, 2 * jj + t_, :], hT[:, k, 512:640], k == 0, k == 7,
                               [wgk_, ("hT", 4)], bsg)
                    S_["gate_ev"].append((j, t_, bpg, bsg))
                if jj == 1 and tsel[-1] == 1:
                    w_release(2 + 2 * h)

            def g_evac(h, jj):
                evs = HS[h]["gate_ev"][2 * jj:2 * jj + 2]
                if has_s:
                    for (j, t_, bpg, bsg) in evs:
                        dst = sgA if t_ == 0 else sgB
                        dk = "sgA" if t_ == 0 else "sgB"
                        cp(dst[:, j, 512:640], pbs[bsg][:, t_ * 128:(t_ + 1) * 128], [PB(bsg)], [(dk, j, "s")], eng="act")
                for (j, t_, bpg, bsg) in evs:
                    dst = sgA if t_ == 0 else sgB
                    dk = "sgA" if t_ == 0 else "sgB"
                    cp(dst[:, j, 0:512], pbs[bpg][:, 0:512], [PB(bpg)], [(dk, j, "p")], eng="act")

            def chain_a(h):
                ts(t1[:, 0:NT], t1[:, 0:NT], lbv[:, 4 + h:5 + h], lbv[:, h:h + 1], ALU.mult, ALU.add, K("t1") + ["lbv"], K("t1"))
                ts(t2[:, 0:NT], t1[:, 0:NT], -1.0, 1.0, ALU.mult, ALU.add, K("t1"), K("t2"))
                act(t1[:, 0:NT], t1[:, 0:NT], AF.Ln, K("t1"), K("t1"))

            def cb_scan(h):
                A("dve", lambda e: e.tensor_tensor_scan(out=t3[:, 0:NT], data0=resetm[:, 0:NT], data1=t1[:, 0:NT],
                                                        initial=0.0, op0=ALU.mult, op1=ALU.add),
                  K("t1") + ["resetmb"], K("t3"))

            def cb_exp(h):
                act(t1[:, 0:NT], t3[:, 0:NT], AF.Exp, K("t3"), K("t1"))
                act(t4[:, 0:NT], t3[:, 0:NT], AF.Exp, K("t3"), K("t4"), scale=-1.0)

            def cb_mul(h):
                qe_ = QE[h % 2]
                qk = f"qe{h % 2}"
                tt(qe_[:, 0:NT], qf[:, 0:NT], t1[:, 0:NT], ALU.mult, K("qf") + K("t1"), [(qk, "p")] + ([(qk, "s")] if has_s else []))
                tt(t2[:, 0:NT], t2[:, 0:NT], t4[:, 0:NT], ALU.mult, K("t2") + K("t4"), K("t2"))

            def cb_keb(h):
                cp(keb[:, 0:NT], t2[:, 0:NT], K("t2"), K("keb"), eng="act")

            def cb_dec(h):
                dec_ = DEC[h % 2]
                dk_ = f"dec{h % 2}"
                cp(dec_[:, 0:8], t1[:, 63:512:64], [("t1", "p")], [(dk_, "p")])
                tt(kdb[:, 0:512].rearrange("p (c t) -> p c t", t=64), t2[:, 0:512].rearrange("p (c t) -> p c t", t=64),
                   dec_[:, 0:8].unsqueeze(2).broadcast_to([128, 8, 64]), ALU.mult, [("t2", "p"), (dk_, "p")], [("kdb", "p")])
                if has_s:
                    cp(dec_[:, 8:24], t1[:, 519:640:8], [("t1", "s")], [(dk_, "s")])
                    tt(kdb[:, 512:640].rearrange("p (c t) -> p c t", t=8), t2[:, 512:640].rearrange("p (c t) -> p c t", t=8),
                       dec_[:, 8:24].unsqueeze(2).broadcast_to([128, 16, 8]), ALU.mult, [("t2", "s"), (dk_, "s")], [("kdb", "s")])

            def chain_b(h):
                cb_scan(h); cb_exp(h); cb_mul(h); cb_keb(h); cb_dec(h)

            def R1(h):
                qe = QE[h % 2]
                qk = f"qe{h % 2}"
                for (src_, srck, dst, dstk) in ((kdb, "kdb", kdT, "kdT"), (vb, "vb", vT, "vT")):
                    bt = nbR()
                    bv = pbs[bt][:].bitcast(BF16)
                    for b in blocks:
                        sk = "p" if b < 4 else "s"
                        tp(bv[:, b * 128:(b + 1) * 128], src_[:, b * 128:(b + 1) * 128], identb[:], [(srck, sk), "identb"], bt)
                    nbk = len(blocks)
                    cp(dst[:, 0:nbk, :], bv[:, 0:nbk * 128].rearrange("p (b k) -> p b k", k=128), [PB(bt)], [dstk], eng="act")
                ba = nbR()
                for b in range(4):
                    mm(pbs[ba][:, b * 128:(b + 1) * 128], keb[:, b * 128:(b + 1) * 128], qe[:, b * 128:(b + 1) * 128],
                       True, True, [("keb", "p"), (qk, "p")], ba)
                tt(Am[:, 0:4, :], pbs[ba][:, 0:512].rearrange("p (b t) -> p b t", b=4),
                   maskPb[:].unsqueeze(1).broadcast_to([128, 4, 128]), ALU.mult, [PB(ba), "maskPb"], [("Am", "p")])
                if has_s:
                    bas = nbR()
                    mm(pbs[bas][:, 0:128], keb[:, 512:640], qe[:, 512:640], True, True, [("keb", "s"), (qk, "s")], bas)
                    tt(Am[:, 4, :], pbs[bas][:, 0:128], maskSb[:], ALU.mult, [PB(bas), "maskSb"], [("Am", "s")])

            def build_vm():
                tt(Vm[:, :, :], vT[:, 4, :].unsqueeze(1).broadcast_to([128, 16, 128]),
                   seqm.unsqueeze(2).broadcast_to([128, 16, 128]), ALU.mult, ["vT", "cs"], ["Vm"])

            def sample_prefetch(h):
                P.dma("sp", "d_S0", lambda e, h=h: e.dma_start(out=S0[:, :, :], in_=shg[:, h, :, :].rearrange("i k v -> k i v")),
                      writes=["S0"])

            def sample_bf(h):
                cp(S0bf[:, :, :].rearrange("p i v -> p (i v)"), S0[:, :, :].rearrange("p i v -> p (i v)"), ["S0"], ["S0bf"], eng="act")

            def sample_state_update(h):
                dec = DEC[h % 2]
                dk_ = f"dec{h % 2}"
                tt(S0[:, :, :], S0[:, :, :], dec[:, 8:24].unsqueeze(2).broadcast_to([128, 16, 128]), ALU.mult,
                   ["S0", (dk_, "s")], ["S0"])
                for q4 in range(4):
                    bd = nbR()
                    mm(pbs[bd][:, 0:512], kdT[:, 4, :], Vm[:, 4 * q4:4 * q4 + 4, :].rearrange("p i v -> p (i v)"), True, True,
                       ["kdT", "Vm"], bd)
                    tt(S0[:, 4 * q4:4 * q4 + 4, :].rearrange("p i v -> p (i v)"),
                       S0[:, 4 * q4:4 * q4 + 4, :].rearrange("p i v -> p (i v)"), pbs[bd][:, 0:512], ALU.add,
                       ["S0", PB(bd)], ["S0"])
                P.dma("sp", "d_nhs", lambda e, h=h: e.dma_start(out=nhs[:, h, :, :].rearrange("i k v -> k i v"), in_=S0[:, :, :]),
                      reads=["S0"])

            def R2_mm(h):
                dsb = [nbR(), nbR()]
                HS[h]["dsb"] = dsb
                for c in range(8):
                    blk, half = c // 2, c % 2
                    bk = dsb[half]
                    mm(pbs[bk][:, blk * 128:(blk + 1) * 128], kdT[half * 64:(half + 1) * 64, blk, :],
                       vT[half * 64:(half + 1) * 64, blk, :], True, True, ["kdT", "vT"], bk)
                cp(Sall[:, 0, :], carry[:, h, :], ["carry"], [("Sall", 0)])

            def R2_steps(h, c0, c1):
                dec_ = DEC[h % 2]
                dk_ = f"dec{h % 2}"
                dsb = HS[h]["dsb"]
                for c in range(c0, c1):
                    blk, half = c // 2, c % 2
                    bk = dsb[half]
                    stt(Sall[:, c + 1, :], Sall[:, c, :], dec_[:, c:c + 1], pbs[bk][:, blk * 128:(blk + 1) * 128],
                        ALU.mult, ALU.add, [("Sall", c), (dk_, "p"), PB(bk)], [("Sall", c + 1)])

            def R2_half(h):
                cp(Sbf[:, 0:4, :].rearrange("p c v -> p (c v)"), Sall[:, 0:4, :].rearrange("p c v -> p (c v)"),
                   [("Sall", c) for c in range(4)], [("Sbf", 0)], eng="act")

            def R2_fin(h):
                cp(Sbf[:, 4:8, :].rearrange("p c v -> p (c v)"), Sall[:, 4:8, :].rearrange("p c v -> p (c v)"),
                   [("Sall", c) for c in range(4, 8)], [("Sbf", 1)], eng="act")
                cp(carry[:, h, :], Sall[:, 8, :], [("Sall", 8)], ["carry"])
                if last:
                    P.dma("sp", f"d_nhp{h}", lambda e, h=h: e.dma_start(out=nhp[h], in_=Sall[:, 8, :]), reads=[("Sall", 8)])

            def R2(h):
                R2_mm(h); R2_steps(h, 0, 4); R2_half(h); R2_steps(h, 4, 8); R2_fin(h)

            def R3(h):
                qe = QE[h % 2]
                qk = f"qe{h % 2}"
                S_ = HS[h]
                bo = nbR()
                for b in range(4):
                    mm(pbs[bo][:, b * 128:(b + 1) * 128], vT[:, b, :], Am[:, b, :], True, False, ["vT", ("Am", "p")], bo)
                    for half in range(2):
                        c = 2 * b + half
                        mm(pbs[bo][:, c * 64:(c + 1) * 64], Sbf[:, c, :], qe[:, c * 64:(c + 1) * 64], False, half == 1,
                           [("Sbf", c // 4), (qk, "p")], bo)
                bos = None
                if has_s:
                    bos = nbR()
                    mm(pbs[bos][:, 0:128], vT[:, 4, :], Am[:, 4, :], True, False, ["vT", ("Am", "s")], bos)
                    for i in range(16):
                        mm(pbs[bos][:, i * 8:(i + 1) * 8], S0bf[:, i, :], qe[:, 512 + i * 8:512 + (i + 1) * 8], False, i == 15,
                           ["S0bf", (qk, "s")], bos)
                S_["obanks"] = [(0, 512, "p", bo)] + ([(512, 128, "s", bos)] if has_s else [])
                for (c0, n, sk, bk) in S_["obanks"]:
                    act(osq[:, c0:c0 + n], pbs[bk][:, 0:n], AF.Square, [PB(bk)], [("osq", sk)])

            def R4(h):
                gbh = gb2[h % 2]
                gbk = f"gbf{h % 2}"
                for (c0, n, sk, bk) in HS[h]["obanks"]:
                    bn = nbR()
                    mm(pbs[bn][:, 0:n], onesd[:], osq[:, c0:c0 + n], True, True, ["onesd", ("osq", sk)], bn)
                    act(s1[:, c0:c0 + n], pbs[bn][:, 0:n], AF.Ln, [PB(bn)], [("s1", sk)], bias=EPS)
                    act(s1[:, c0:c0 + n], s1[:, c0:c0 + n], AF.Exp, [("s1", sk)], [("s1", sk)], scale=-0.5)
                    stt(s2[:, c0:c0 + n], pbs[bk][:, 0:n], small[:, 4 + h:5 + h], s1[:, c0:c0 + n], ALU.mult, ALU.mult,
                        [PB(bk), "small", ("s1", sk)], [("s2", sk)])
                tt(o_fin[:, h, 0:NT], s2[:, 0:NT], gbh[:, 0:NT], ALU.mult, K("s2") + [(gbk, "p"), (gbk, "s")],
                   [("o_fin", h, s_) for s_ in ("p", "s")])

            if PIPE:
                for j in range(4):
                    z_group(0, j)
            poolmix()
            ck("poolmix")
            P.barrier()
            ck("bar")
            if PIPE:
                for h in range(4):
                    p = h - 1
                    if h > 0:
                        R1(p)
                    z_evac(h)
                    g_mm(h, 0)
                    if h > 0:
                        R2_mm(p)
                    chain_a(h)
                    if h > 0:
                        R2_steps(p, 0, 4)
                        R2_half(p)
                    g_evac(h, 0)
                    if h < 3:
                        for j in range(4):
                            z_group(h + 1, j)
                    cb_scan(h)
                    if h > 0:
                        R2_steps(p, 4, 7)
                    cb_exp(h)
                    cb_mul(h)
                    cb_dec(h)
                    cb_keb(h)
                    if h < 3:
                        if h > 0:
                            R2_steps(p, 7, 8)
                            R2_fin(p)
                            R3(p)
                        g_mm(h, 1, (0,))
                        if h > 0:
                            R4(p)
                        g_mm(h, 1, (1,))
                        g_evac(h, 1)
                    else:
                        g_mm(h, 1)
                        R2_steps(p, 7, 8)
                        R2_fin(p)
                        R3(p)
                        g_evac(h, 1)
                        R4(p)
                    if h < 3:
                        warm(AF.Sigmoid)
                R1(3); R2(3); R3(3); R4(3)
                warm(AF.Sigmoid)
            else:
                for h in range(4):
                    sample_prefetch(h)
                    for j in range(4):
                        z_group(h, j)
                    z_evac(h)
                    g_mm(h, 0)
                    g_mm(h, 1)
                    chain_a(h)
                    g_evac(h, 0)
                    chain_b(h)
                    R1(h)
                    g_evac(h, 1)
                    sample_bf(h)
                    R2_mm(h)
                    R2_steps(h, 0, 4)
                    R2_half(h)
                    R2_steps(h, 4, 8)
                    build_vm()
                    R2_fin(h)
                    R3(h); R4(h)
                    warm(AF.Sigmoid)
                    sample_state_update(h)

            ck("hgrn")
            for jh in range(2):
                wy, wyk = w_get(9 + jh)
                wyv = wy[:].rearrange("p (k t c) -> p k t c", k=4, t=2)
                for jj in range(4):
                    j = jh * 4 + jj
                    bs = nb() if has_s else None
                    b_ya, b_yb = nb(), nb()
                    for t_, bk, srcb, srck in ((0, b_ya, pool_out, "pool_out"), (1, b_yb, o_fin, "o_fin")):
                        for kk in range(4):
                            for (c0, n, sk) in subs:
                                o = pbs[bk][:, 0:512] if sk == "p" else pbs[bs][:, t_ * 128:(t_ + 1) * 128]
                                mm(o, wyv[:, kk, t_, jj * 128:(jj + 1) * 128], srcb[:, kk, c0:c0 + n], kk == 0, kk == 3,
                                   [wyk, (srck, kk, sk)], bk if sk == "p" else bs)
                    if jj == 3:
                        w_release(9 + jh)
                    m1, m1k = (s1, "s1") if j % 2 == 0 else (t3, "t3")
                    m2, m2k = (s2, "s2") if j % 2 == 0 else (t4, "t4")
                    for (c0, n, sk) in subs:
                        oa = pbs[b_ya][:, 0:512] if sk == "p" else pbs[bs][:, 0:128]
                        ob = pbs[b_yb][:, 0:512] if sk == "p" else pbs[bs][:, 128:256]
                        ka = PB(b_ya) if sk == "p" else PB(bs)
                        kb_ = PB(b_yb) if sk == "p" else PB(bs)
                        act(m1[:, c0:c0 + n], sgA[:, j, c0:c0 + n], AF.Sigmoid, [("sgA", j, sk)], [(m1k, sk)])
                        act(m2[:, c0:c0 + n], sgB[:, j, c0:c0 + n], AF.Sigmoid, [("sgB", j, sk)], [(m2k, sk)])
                        tt(m1[:, c0:c0 + n], m1[:, c0:c0 + n], oa, ALU.mult, [(m1k, sk), ka], [(m1k, sk)])
                        tt(m2[:, c0:c0 + n], m2[:, c0:c0 + n], ob, ALU.mult, [(m2k, sk), kb_], [(m2k, sk)])
                    tt(merged[:, j, 0:NT], m1[:, 0:NT], m2[:, 0:NT], ALU.add, K(m1k) + K(m2k),
                       [("merged", j, s_) for s_ in ("p", "s")])

            ck("merge")
            warm(AF.Ln)
            for half in range(2):
                wo, wok = w_get(11 + half)
                wov = wo[:].rearrange("p (k c) -> p k c", k=8)
                bks = {b: nb() for b in blocks}
                for k in range(8):
                    for b in blocks:
                        sk = "p" if b < 4 else "s"
                        mm(pbs[bks[b]][:, 0:512], merged[:, k, b * 128:(b + 1) * 128], wov[:, k, :], k == 0, k == 7,
                           [wok, ("merged", k, sk)], bks[b])
                w_release(11 + half)
                for b in blocks:
                    tt(X[:, b, half * 512:(half + 1) * 512], X[:, b, half * 512:(half + 1) * 512], pbs[bks[b]][:, 0:512],
                       ALU.add, [(xk, b), PB(bks[b])], [(xk, b)])
                    half_stats(b, half)

            ck("wout")
            norm_T(1, True)
            warm(AF.Sigmoid)
            P.barrier()
            if ti + 1 < ntiles:
                for b in range(4):
                    load_x(ti + 1, b)
            for b in blocks:
                src = ppd[ti * 512 + b * 128: ti * 512 + (b + 1) * 128, :] if b < 4 else psd
                P.dma("sp", f"d_p{b}", lambda e, src=src, b=b: e.dma_start(out=pstage[:, b, :], in_=src), writes=[("pstage", b)])
            for jb in range(11):
                wf, wfk = w_get(13 + jb)
                wfv = wf[:].rearrange("p (k j c) -> p k j c", k=8, j=2)
                for cc in range(2):
                    f = 2 * jb + cc
                    bs = nb() if has_s else None
                    b_g, b_u = nb(), nb()
                    for jx, bk in ((0, b_g), (1, b_u)):
                        for k in range(8):
                            for (c0, n, sk) in subs:
                                o = pbs[bk][:, 0:512] if sk == "p" else pbs[bs][:, jx * 128:(jx + 1) * 128]
                                mm(o, wfv[:, k, jx, cc * 128:(cc + 1) * 128], hT[:, k, c0:c0 + n], k == 0, k == 7,
                                   [wfk] + hTk[sk], bk if sk == "p" else bs)
                    if cc == 1:
                        w_release(13 + jb)
                    for (c0, n, sk) in subs:
                        og = pbs[b_g][:, 0:512] if sk == "p" else pbs[bs][:, 0:128]
                        ou = pbs[b_u][:, 0:512] if sk == "p" else pbs[bs][:, 128:256]
                        kg = PB(b_g) if sk == "p" else PB(bs)
                        ku = PB(b_u) if sk == "p" else PB(bs)
                        act(ftmp[:, c0:c0 + n], og, AF.Sigmoid, [kg], [("ftmp", sk)])
                        tt(ftmp[:, c0:c0 + n], ftmp[:, c0:c0 + n], og, ALU.mult, [("ftmp", sk), kg], [("ftmp", sk)])
                        tt(hidden[:, f, c0:c0 + n], ftmp[:, c0:c0 + n], ou, ALU.mult, [("ftmp", sk), ku], [("hidden", f, sk)])
            for b in blocks:
                cp(pbf[:], pstage[:, b, :], [("pstage", b)], ["pbf"], eng="act")
                bt = nb()
                bv = pbs[bt][:].bitcast(BF16)
                for k in range(2):
                    tp(bv[:, k * 128:(k + 1) * 128], pbf[:, k * 128:(k + 1) * 128], identb[:], ["pbf", "identb"], bt)
                cp(pT[:, :, b * 128:(b + 1) * 128], bv[:, 0:256].rearrange("p (k t) -> p k t", k=2), [PB(bt)], [("pT", b)])
            warm(AF.Ln)
            for half in range(2):
                bks = {b: nb() for b in blocks}
                for kb in range(3):
                    wd, wdk = w_get(24 + half * 3 + kb)
                    wdv = wd[:].rearrange("p (k c) -> p k c", k=8)
                    nk = 8 if kb < 2 else 6
                    for kk in range(nk):
                        f = kb * 8 + kk
                        for b in blocks:
                            sk = "p" if b < 4 else "s"
                            mm(pbs[bks[b]][:, 0:512], hidden[:, f, b * 128:(b + 1) * 128], wdv[:, kk, :], f == 0, f == 21,
                               [wdk, ("hidden", f, sk)], bks[b])
                    w_release(24 + half * 3 + kb)
                for b in blocks:
                    tt(X[:, b, half * 512:(half + 1) * 512], X[:, b, half * 512:(half + 1) * 512], pbs[bks[b]][:, 0:512],
                       ALU.add, [(xk, b), PB(bks[b])], [(xk, b)])
                    half_stats(b, half)

            ck("ffn")
            norm_T(2, True)
            if ti + 1 < ntiles:
                Xn, xkn = xbuf(ti + 1)
                norm_stats(False, blks=[0, 1, 2, 3], X_=Xn, xk_=xkn, ssq_=ssq1, rst_=rst1, tag="ssq1", rtag="rst1")
            warm(AF.Sigmoid)
            wg0, wg0k = w_get(30)
            wpp_, wppk = w_get(31)
            wg1, wg1k = w_get(32)
            wppv = wpp_[:, 0:2048].rearrange("p (k c) -> p k c", k=2)
            for half in range(2):
                wg_, wgk = (wg0, wg0k) if half == 0 else (wg1, wg1k)
                wgv = wg_[:].rearrange("p (k c) -> p k c", k=8)
                bks = {b: nb() for b in blocks}
                for k in range(8):
                    for b in blocks:
                        mm(pbs[bks[b]][:, 0:512], hT[:, k, b * 128:(b + 1) * 128], wgv[:, k, :], k == 0, k == 7,
                           [wgk, ("hT", b)], bks[b])
                if half == 0:
                    w_release(30)
                for b in blocks:
                    sg = sgt[b % 2]
                    sgk = f"sgt{b % 2}"
                    act(sg[:], pbs[bks[b]][:, 0:512], AF.Sigmoid, [PB(bks[b])], [sgk])
                    be = nb()
                    for k in range(2):
                        mm(pbs[be][:, 0:512], pT[:, k, b * 128:(b + 1) * 128], wppv[:, k, half * 512:(half + 1) * 512], k == 0, k == 1,
                           [wppk, ("pT", b)], be)
                    tt(sg[:], sg[:], pbs[be][:, 0:512], ALU.mult, [sgk, PB(be)], [sgk])
                    tt(X[:, b, half * 512:(half + 1) * 512], X[:, b, half * 512:(half + 1) * 512], sg[:], ALU.add,
                       [(xk, b), sgk], [(xk, b)])
                for b in blocks:
                    half_stats(b, half)
                if half == 1:
                    w_release(31); w_release(32)

            ck("ple")
            if ti + 1 < ntiles:
                Xn, xkn = xbuf(ti + 1)
                norm_apply(0, blks=[0, 1, 2, 3], X_=Xn, xk_=xkn, rst_=rst1, rtag="rst1")
            P.barrier()
            norm_stats(True)
            for b in blocks:
                dst = yp[ti * 512 + b * 128: ti * 512 + (b + 1) * 128, :] if b < 4 else ys
                for half in range(2):
                    q_ = (2 * b + half) % 4
                    yst = ystage[q_]
                    ysk = f"ystage{q_}"
                    hs_ = slice(half * 512, (half + 1) * 512)
                    stt(yst[:], X[:, b, hs_], rst[:, b:b + 1], gB[:, 3, hs_], ALU.mult, ALU.mult,
                        [(xk, b), "rst", ("gB", 3)], [ysk])
                    P.dma("sp", f"d_y{q_}", lambda e, yst=yst, dst=dst, hs_=hs_: e.dma_start(out=dst[:, hs_], in_=yst[:]),
                          reads=[ysk])

        P.barrier()
        try:
            ck("setup")
            for ti in range(ntiles):
                run_tile(ti)
        except _Stop:
            pass
        P.finish("sp")
        P.emit()
    return nc


_PROG = {}


def _prep_inputs(inp):
    f = lambda a: np.ascontiguousarray(np.asarray(a, dtype=np.float32))
    w_in = f(inp["w_in"][0])
    wallv = build_wall(w_in, f(inp["w_pool_up"][0]), f(inp["w_hgrn_up"][0]), f(inp["w_out"][0]),
                       f(inp["w_ffn_gate"][0]), f(inp["w_ffn_up"][0]), f(inp["w_ffn_down"][0]),
                       f(inp["w_ple_gate"][0]), f(inp["w_ple_proj"][0]))
    gvec = np.ascontiguousarray(np.stack([f(inp["g_mix"][0]), f(inp["g_ffn"][0]), f(inp["g_ple"][0]), f(inp["g_final"])], 0))
    small = np.zeros((128, 16), np.float32)
    small[:, 0:4] = f(inp["pool_scale"][0]).reshape(4, 128).T
    small[:, 4:8] = f(inp["hgrn_norm"][0]).reshape(4, 128).T
    small[:, 8:12] = f(inp["hgrn_lb"][0]).reshape(4, 128).T
    small[:, 12:16] = f(inp["hgrn_lb"][1]).reshape(4, 128).T
    pmix = np.ascontiguousarray(f(inp["w_pool_mix"][0]).transpose(1, 0, 2)).reshape(128, 512)
    xp = f(inp["x_prompt"]); xsm = f(inp["x_sample"])
    ppr = f(inp["p_prompt"][0]); psm = f(inp["p_sample"][0])
    spl = f(inp["state_pool"][0]); shg = f(inp["state_hgrn"][0])
    maps = []
    for c in range(NCORES):
        maps.append({
            "xp": xp[c], "xs": xsm[16 * c:16 * c + 16].reshape(128, 1024),
            "pp": ppr[c], "ps": psm[16 * c:16 * c + 16].reshape(128, 256),
            "spool": spl[16 * c:16 * c + 16].reshape(240, 512),
            "shg": shg[16 * c:16 * c + 16],
            "wall": wallv, "cst": CONST_ARR, "gvec": gvec, "small": small, "pmix": pmix,
        })
    return maps


def kernel(**inputs):
    if "nc" not in _PROG:
        _PROG["nc"] = build_program()
    nc = _PROG["nc"]
    maps = _prep_inputs(inputs)
    res = run_bass_kernel_spmd(nc, maps, core_ids=list(range(NCORES)))
    R = res.results
    y_p = np.stack([R[c]["yp"] for c in range(NCORES)], 0).astype(np.float32)
    y_s = np.concatenate([R[c]["ys"].reshape(16, 8, 1024) for c in range(NCORES)], 0).astype(np.float32)
    npp = np.stack([R[c]["npp"] for c in range(NCORES)], 0)[None].astype(np.float32)
    nhp = np.stack([R[c]["nhp"] for c in range(NCORES)], 0)[None].astype(np.float32)
    nps = np.concatenate([R[c]["nps"].reshape(16, 15, 512) for c in range(NCORES)], 0)[None].astype(np.float32)
    nhs = np.concatenate([R[c]["nhs"] for c in range(NCORES)], 0)[None].astype(np.float32)
    return (y_p, y_s, npp, nhp, nps, nhs)
```

```python
import numpy as np
from contextlib import ExitStack
import concourse.bass as bass
import concourse.mybir as mybir
from concourse.bass_utils import run_bass_kernel_spmd

F32 = mybir.dt.float32
BF16 = mybir.dt.bfloat16
AF = mybir.ActivationFunctionType
ALU = mybir.AluOpType

ENGS = ("pe", "act", "dve", "pool", "sp")
EPS = 1e-6
NCORES = 8
NRING = 5
NTILES = 4
import os as _os
SAME_DIST = int(_os.environ.get("SAME_DIST", str(1 << 30)))


class Prog:
    def __init__(self, nc, stack, same_engine_wait=True):
        self.nc = nc
        self.stack = stack
        self.streams = {e: [] for e in ENGS}
        self.count = {e: 0 for e in ENGS}
        self.known = {e: {} for e in ENGS}
        self.sems = {}
        self.dma_count = {}
        self.lastw = {}
        self.readers = {}
        self.same_engine_wait = same_engine_wait

    def _collect(self, eng, reads, writes):
        deps = []
        for k in reads:
            ev = self.lastw.get(k)
            if ev is not None:
                deps.append(ev)
        for k in writes:
            ev = self.lastw.get(k)
            if ev is not None:
                deps.append(ev)
            rd = self.readers.get(k)
            if rd:
                deps.extend(rd.values())
        kn = self.known[eng]
        need = {}
        for (s, v, vc) in deps:
            if s == eng and (eng == "pe" or not self.same_engine_wait or self.count[eng] - v >= SAME_DIST):
                continue
            if kn.get(s, 0) >= v:
                continue
            if need.get(s, 0) < v:
                need[s] = v
            for s2, v2 in vc.items():
                if s2 == eng:
                    continue
                if kn.get(s2, 0) < v2:
                    kn[s2] = v2
        waits = []
        for s, v in need.items():
            waits.append((s, v))
            if kn.get(s, 0) < v:
                kn[s] = v
        return waits

    def _record(self, ev, reads, writes):
        s = ev[0]
        for k in reads:
            self.readers.setdefault(k, {})[s] = ev
        for k in writes:
            self.lastw[k] = ev
            self.readers[k] = {}

    def op(self, eng, fn, reads=(), writes=()):
        waits = self._collect(eng, reads, writes)
        self.count[eng] += 1
        n = self.count[eng]
        vc = dict(self.known[eng])
        vc[eng] = n
        ev = (eng, n, vc)
        self.streams[eng].append((waits, fn, (eng, 1)))
        self._record(ev, reads, writes)
        return ev

    def dma(self, q, semname, fn, reads=(), writes=()):
        waits = self._collect(q, reads, writes)
        self.dma_count[semname] = self.dma_count.get(semname, 0) + 1
        v = 16 * self.dma_count[semname]
        vc = dict(self.known[q])
        vc[semname] = v
        ev = (semname, v, vc)
        self.streams[q].append((waits, fn, (semname, 16)))
        self._record(ev, reads, writes)
        return ev

    def barrier(self, engines=("act", "dve", "sp")):
        for e in engines:
            kn = self.known[e]
            waits = []
            for e2 in ("pe", "act", "dve"):
                c = self.count[e2]
                if c and kn.get(e2, 0) < c:
                    waits.append((e2, c))
                    kn[e2] = c
            for s, c in self.dma_count.items():
                if s.startswith("ring") or s.startswith("d_y") or s.startswith("d_x"):
                    continue
                if kn.get(s, 0) < 16 * c:
                    waits.append((s, 16 * c))
                    kn[s] = 16 * c
            if waits:
                self.streams[e].append((waits, None, None))

    def finish(self, eng="sp"):
        kn = self.known[eng]
        waits = []
        for s, c in self.dma_count.items():
            if kn.get(s, 0) < 16 * c:
                waits.append((s, 16 * c))
                kn[s] = 16 * c
        for e in ("pe", "act", "dve"):
            if self.count[e] and kn.get(e, 0) < self.count[e]:
                waits.append((e, self.count[e]))
        self.streams[eng].append((waits, None, None))

    def emit(self):
        nc = self.nc
        for s in list(ENGS) + list(self.dma_count):
            if s not in self.sems:
                self.sems[s] = self.stack.enter_context(nc.semaphore(s))
        with nc.Block() as block:
            def run(engine, items):
                for waits, fn, inc in items:
                    for s, v in waits:
                        engine.wait_ge(self.sems[s], v)
                    if fn is not None:
                        ins = fn(engine)
                        ins.then_inc(self.sems[inc[0]], inc[1])

            @block.tensor
            def _(eng):
                run(eng, self.streams["pe"])

            @block.scalar
            def _(eng):
                run(eng, self.streams["act"])

            @block.vector
            def _(eng):
                run(eng, self.streams["dve"])

            @block.gpsimd
            def _(eng):
                run(eng, self.streams["pool"])

            @block.sync
            def _(eng):
                run(eng, self.streams["sp"])


NBLK = 33


def _kc(W, nk):
    C = W.shape[1]
    return np.ascontiguousarray(W.reshape(nk, 128, C).transpose(1, 0, 2)).reshape(128, nk * C)


def _pad(a):
    out = np.zeros((128, 4096), np.float32)
    out[:, : a.shape[1]] = a
    return out


def build_wall(w_in, w_pool_up, w_hgrn_up, w_out, w_g, w_u, w_d, w_pg, w_pp):
    blocks = []
    blocks.append(_kc(w_in[:, 0:512], 8))
    zz = w_in[:, 512:2560].reshape(8, 128, 4, 4, 128)
    ga = w_in[:, 2560:3584].reshape(8, 128, 8, 128)
    gb = w_in[:, 3584:4608].reshape(8, 128, 8, 128)
    for h in range(4):
        blocks.append(np.ascontiguousarray(zz[:, :, :, h, :].transpose(1, 0, 2, 3)).reshape(128, 4096))
        gg = np.stack([ga[:, :, 2 * h], gb[:, :, 2 * h], ga[:, :, 2 * h + 1], gb[:, :, 2 * h + 1]], axis=2)
        blocks.append(np.ascontiguousarray(gg.transpose(1, 0, 2, 3)).reshape(128, 4096))
    for half in range(2):
        yy = np.stack([w_pool_up[:, half * 512:(half + 1) * 512].reshape(4, 128, 512),
                       w_hgrn_up[:, half * 512:(half + 1) * 512].reshape(4, 128, 512)], axis=2)
        blocks.append(np.ascontiguousarray(yy.transpose(1, 0, 2, 3)).reshape(128, 4096))
    blocks.append(_kc(w_out[:, 0:512], 8))
    blocks.append(_kc(w_out[:, 512:1024], 8))
    for jb in range(11):
        gu = np.stack([w_g[:, jb * 256:(jb + 1) * 256], w_u[:, jb * 256:(jb + 1) * 256]], axis=1)
        blocks.append(_kc(gu.reshape(1024, 512), 8))
    for half in range(2):
        for kb in range(3):
            nk = 8 if kb < 2 else 6
            blocks.append(_pad(_kc(w_d[kb * 1024: kb * 1024 + nk * 128, half * 512:(half + 1) * 512], nk)))
    blocks.append(_kc(w_pg[:, 0:512], 8))
    blocks.append(_pad(_kc(w_pp, 2)))
    blocks.append(_kc(w_pg[:, 512:1024], 8))
    assert len(blocks) == NBLK
    return np.ascontiguousarray(np.stack(blocks, axis=0).astype(np.float32))


def build_consts():
    c = {}
    c["ident"] = np.eye(128, dtype=np.float32)
    s = np.arange(128)[:, None]
    t = np.arange(128)[None, :]
    c["maskP"] = ((s // 64 == t // 64) & (s <= t)).astype(np.float32)
    c["maskS"] = ((s // 8 == t // 8) & (s <= t)).astype(np.float32)
    c["seqm"] = (s // 8 == np.arange(16)[None, :]).astype(np.float32)
    r = np.ones(640, np.float32)
    r[0:512:64] = 0.0
    r[512:640:8] = 0.0
    c["resetm"] = np.broadcast_to(r, (128, 640)).copy()
    rc = np.zeros((4, 16), np.float32)
    for g, w in enumerate((2, 4, 8, 16)):
        rc[g] = 1.0 / np.minimum(np.arange(16) + 1, w)
    c["rc"] = np.broadcast_to(rc.reshape(1, 64), (128, 64)).copy()
    order = ["ident", "seqm", "rc", "maskP", "maskS", "resetm"]
    offs = {}
    o = 0
    for k in order:
        offs[k] = (o, c[k].shape[1])
        o += c[k].shape[1]
    return np.ascontiguousarray(np.concatenate([c[k] for k in order], axis=1)), offs


CONST_ARR, COFF = build_consts()
CW = CONST_ARR.shape[1]
CW_KEEP = COFF["maskP"][0]


class _Stop(Exception):
    pass


STOP = None


def ck(name):
    if STOP == name:
        raise _Stop()


def build_program(ntiles=NTILES):
    nc = bass.Bass("TRN2", target_bir_lowering=False)

    def din(name, shape):
        return nc.dram_tensor(name, shape, F32, kind="ExternalInput").ap()

    def dout(name, shape):
        return nc.dram_tensor(name, shape, F32, kind="ExternalOutput").ap()

    xp = din("xp", [2048, 1024])
    xs = din("xs", [128, 1024])
    ppd = din("pp", [2048, 256])
    psd = din("ps", [128, 256])
    spool = din("spool", [240, 512])
    shg = din("shg", [16, 4, 128, 128])
    wall = din("wall", [NBLK, 128, 4096])
    cst = din("cst", [128, CW])
    gvec = din("gvec", [4, 1024])
    smalld = din("small", [128, 16])
    pmixd = din("pmix", [128, 512])
    yp = dout("yp", [2048, 1024])
    ys = dout("ys", [128, 1024])
    npp = dout("npp", [15, 512])
    nhp = dout("nhp", [4, 128, 128])
    nps = dout("nps", [240, 512])
    nhs = dout("nhs", [16, 4, 128, 128])

    with ExitStack() as st:
        P = Prog(nc, st)

        def sb(name, shape, dt):
            return st.enter_context(nc.sbuf_tensor("sb_" + name, shape, dt))

        xres = sb("xres", [128, 5, 1024], F32)
        hT = sb("hT", [128, 8, 640], BF16)
        hn = [sb(f"hn{i}", [128, 1024], BF16) for i in range(2)]
        junk = sb("junk", [128, 512], BF16)
        ystage = [sb(f"ystage{i}", [128, 512], F32) for i in range(4)]
        resetmb = sb("resetmb", [128, 640], BF16)
        ring = [sb(f"ring{i}", [128, 4096], BF16) for i in range(NRING)]
        gB = sb("gB", [128, 4, 1024], F32)
        cs = sb("cs", [128, CW_KEEP], F32)
        identb = sb("identb", [128, 128], BF16)
        onesd = sb("onesd", [128, 128], BF16)
        maskPb = sb("maskPb", [128, 128], BF16)
        maskSb = sb("maskSb", [128, 128], BF16)
        pmixf = sb("pmixf", [128, 512], F32)
        pmix = sb("pmix", [128, 4, 128], BF16)
        small = sb("small", [128, 16], F32)
        lbv = sb("lbv", [128, 8], F32)
        carry = sb("carry", [128, 4, 128], F32)
        ucarry = sb("ucarry", [128, 4, 16], F32)
        ssq = sb("ssq", [128, 16], F32)
        ssq1 = sb("ssq1", [128, 8], F32)
        dmy = sb("dmy", [128, 2], F32)
        rst1 = sb("rst1", [128, 4], F32)
        rst = sb("rst", [128, 8], F32)
        identf = cs[:, COFF["ident"][0]:COFF["ident"][0] + 128]
        seqm = cs[:, COFF["seqm"][0]:COFF["seqm"][0] + 16]
        rcv = cs[:, COFF["rc"][0]:COFF["rc"][0] + 64]
        resetm = resetmb

        MIX_BYTES = 0
        scr_plan = {}

        def plan(phase, name, nelem, dt):
            nonlocal MIX_BYTES
            nbytes = nelem * (4 if dt == F32 else 2)
            nbytes = (nbytes + 31) // 32 * 32
            off = scr_plan.setdefault(("off", phase), 0)
            scr_plan[name] = (off, nelem, dt)
            scr_plan[("off", phase)] = off + nbytes

        U_NAMES = ["ug", "ue", "pa", "pb_", "sa", "sb_", "tmp16", "pooled", "npsb", "stg", "npp_sb"]
        for name, n, dt in [
            ("ug", 528, F32), ("ue", 16 * 24, F32), ("pa", 528, F32), ("pb_", 528, F32),
            ("sa", 16 * 24, F32), ("sb_", 16 * 24, F32), ("tmp16", 16, F32),
            ("pooled", 4 * 640, BF16), ("npsb", 4 * 240, F32),
            ("stg", 2 * 512, F32), ("npp_sb", 512, F32), ("pool_out", 4 * 640, BF16),
            ("qf", 640, F32), ("t1", 640, F32), ("t2", 640, F32), ("t3", 640, F32), ("t4", 640, F32),
            ("qe", 640, BF16), ("qe2", 640, BF16), ("keb", 640, BF16), ("kdb", 640, BF16), ("vb", 640, BF16), ("gbf", 640, BF16), ("gbf2", 640, BF16),
            ("kdT", 5 * 128, BF16), ("vT", 5 * 128, BF16), ("Sall", 9 * 128, F32), ("Sbf", 8 * 128, BF16),
            ("dec", 24, F32), ("dec2", 24, F32), ("Am", 5 * 128, BF16), ("osq", 640, BF16), ("o_fin", 4 * 640, BF16),
            ("S0", 16 * 128, F32), ("S0bf", 16 * 128, BF16), ("Vm", 16 * 128, BF16),
            ("merged", 8 * 640, BF16), ("s1", 640, F32), ("s2", 640, F32),
        ]:
            plan("mix", name, n, dt)
        for name, n, dt in [
            ("hidden", 22 * 640, BF16), ("ftmp", 640, F32), ("sgt0", 512, F32), ("sgt1", 512, F32),
            ("pT", 2 * 640, BF16), ("pstage", 5 * 256, F32), ("pbf", 256, BF16),
        ]:
            plan("ffn", name, n, dt)
        u_end = scr_plan["pool_out"][0]
        assert 2 * 8 * 640 * 2 <= u_end, u_end
        scr_plan["sgA"] = (0, 8 * 640, BF16)
        scr_plan["sgB"] = (8 * 640 * 2, 8 * 640, BF16)
        scr_bytes = max(scr_plan[("off", "mix")], scr_plan[("off", "ffn")])
        scr = sb("scr", [128, scr_bytes // 4], F32)

        def sv(name):
            off, n, dt = scr_plan[name]
            if dt == F32:
                return scr[:, off // 4: off // 4 + n]
            return scr[:, off // 4: off // 4 + n // 2].bitcast(BF16)

        ug = sv("ug"); ue = sv("ue").rearrange("p (i r) -> p i r", r=24)
        pa = sv("pa"); pb_ = sv("pb_")
        sa = sv("sa").rearrange("p (i r) -> p i r", r=24); sb_ = sv("sb_").rearrange("p (i r) -> p i r", r=24)
        tmp16 = sv("tmp16")
        pooled = sv("pooled").rearrange("p (g t) -> p g t", g=4)
        pool_out = sv("pool_out").rearrange("p (g t) -> p g t", g=4)
        npsb = sv("npsb").rearrange("p (g r) -> p g r", g=4)
        stg = sv("stg").rearrange("p (h c) -> p h c", h=2)
        nps_sb = stg
        npp_sb = sv("npp_sb")
        qf = sv("qf"); t1 = sv("t1"); t2 = sv("t2"); t3 = sv("t3"); t4 = sv("t4")
        qe = sv("qe"); qe2 = sv("qe2"); keb = sv("keb"); kdb = sv("kdb"); vb = sv("vb"); gbf = sv("gbf"); gbf2 = sv("gbf2")
        kdT = sv("kdT").rearrange("p (b k) -> p b k", b=5); vT = sv("vT").rearrange("p (b k) -> p b k", b=5)
        Sall = sv("Sall").rearrange("p (c v) -> p c v", c=9); Sbf = sv("Sbf").rearrange("p (c v) -> p c v", c=8)
        dec = sv("dec"); dec2 = sv("dec2"); Am = sv("Am").rearrange("p (b k) -> p b k", b=5); osq = sv("osq")
        o_fin = sv("o_fin").rearrange("p (h t) -> p h t", h=4)
        S0 = sv("S0").rearrange("p (i v) -> p i v", i=16); S0bf = sv("S0bf").rearrange("p (i v) -> p i v", i=16)
        Vm = sv("Vm").rearrange("p (i v) -> p i v", i=16)
        merged = sv("merged").rearrange("p (k t) -> p k t", k=8); s1 = sv("s1"); s2 = sv("s2")
        sgA = sv("sgA").rearrange("p (k t) -> p k t", k=8); sgB = sv("sgB").rearrange("p (k t) -> p k t", k=8)
        hidden = sv("hidden").rearrange("p (f t) -> p f t", f=22); ftmp = sv("ftmp")
        assert scr_plan["S0bf"][0] == scr_plan["S0"][0] + 8192 and scr_plan["Vm"][0] == scr_plan["S0"][0] + 12288
        assert scr_plan["S0"][0] >= scr_plan[("off", "ffn")]
        xo = scr_plan["S0"][0] // 4
        xalt = scr[:, xo:xo + 4096].rearrange("p (b d) -> p b d", b=4)
        sgt = [sv("sgt0"), sv("sgt1")]
        pT = sv("pT").rearrange("p (k t) -> p k t", k=2); pstage = sv("pstage").rearrange("p (b c) -> p b c", b=5); pbf = sv("pbf")

        pbs = [st.enter_context(nc.psum_tensor(f"pb{i}", [128, 512], F32)) for i in range(8)]
        bank_ctr = [0]

        def nb():
            b = bank_ctr[0] % 8
            bank_ctr[0] += 1
            return b

        def PB(b):
            return ("pb", b)

        wstate = {"tile": 0, "issued": set(), "released": set(), "total": NBLK * ntiles}

        def w_issue_n(n, extra_reads=()):
            if n >= wstate["total"] or n in wstate["issued"]:
                return
            slot = n % NRING
            blk = n % NBLK
            P.dma("pool", f"ring{slot}",
                  lambda e, slot=slot, blk=blk: e.dma_start(out=ring[slot][:], in_=wall[blk]),
                  reads=list(extra_reads), writes=[("ring", slot)])
            wstate["issued"].add(n)

        def w_issue(extra_reads=()):
            w_issue_n(len(wstate["issued"]), extra_reads)

        def w_get(expect):
            n = wstate["tile"] * NBLK + expect
            assert n in wstate["issued"], ("ring too small / block not prefetched", n)
            assert n - NRING < 0 or (n - NRING) in wstate["released"]
            slot = n % NRING
            return ring[slot], ("ring", slot)

        def w_release(expect):
            n = wstate["tile"] * NBLK + expect
            assert n in wstate["issued"] and n not in wstate["released"]
            wstate["released"].add(n)
            w_issue_n(n + NRING)

        def A(eng, fn, reads, writes):
            return P.op(eng, fn, reads=reads, writes=writes)

        def mm(out, lhsT, rhs, start, stop, reads, bank):
            A("pe", lambda e: e.matmul(out, lhsT=lhsT, rhs=rhs, start=start, stop=stop), reads, [PB(bank)])

        def tp(out, in_, ident, reads, bank):
            A("pe", lambda e: e.transpose(out=out, in_=in_, identity=ident), reads, [PB(bank)])

        def act(out, in_, func, reads, writes, **kw):
            A("act", lambda e: e.activation(out=out, in_=in_, func=func, **kw), reads, writes)

        def tt(out, in0, in1, op, reads, writes, eng="dve"):
            A(eng, lambda e: e.tensor_tensor(out=out, in0=in0, in1=in1, op=op), reads, writes)

        def ts(out, in0, s1_, s2_, op0, op1, reads, writes):
            A("dve", lambda e: e.tensor_scalar(out=out, in0=in0, scalar1=s1_, scalar2=s2_, op0=op0, op1=op1), reads, writes)

        def stt(out, in0, scalar, in1, op0, op1, reads, writes):
            A("dve", lambda e: e.scalar_tensor_tensor(out=out, in0=in0, scalar=scalar, in1=in1, op0=op0, op1=op1), reads, writes)

        def cp(out, in_, reads, writes, eng="dve"):
            if eng == "act":
                act(out, in_, AF.Copy, reads, writes)
            else:
                A(eng, lambda e: e.tensor_copy(out=out, in_=in_), reads, writes)

        def warm(func):
            act(dmy[:, 1:2], dmy[:, 0:1], func, ["dmy0"], ["dmy1"])

        A("dve", lambda e: e.memset(dmy[:], 1.0), [], ["dmy0", "dmy1"])
        P.dma("sp", "d_cs", lambda e: e.dma_start(out=cs[:], in_=cst[:, 0:CW_KEEP]), writes=["cs"])
        cstage = scr[:, 0:CW - CW_KEEP]
        P.dma("sp", "d_cs2", lambda e: e.dma_start(out=cstage, in_=cst[:, CW_KEEP:CW]), writes=["cstage"])
        maskPf = cstage[:, 0:128]
        maskSf = cstage[:, 128:256]
        resetmf = cstage[:, 256:896]
        P.dma("sp", "d_small", lambda e: e.dma_start(out=small[:], in_=smalld), writes=["small"])
        P.dma("sp", "d_pmix", lambda e: e.dma_start(out=pmixf[:], in_=pmixd), writes=["pmixf"])
        for gi in range(4):
            P.dma("sp", f"d_g{gi}",
                  lambda e, gi=gi: e.dma_start(out=gB[:, gi, :], in_=gvec[gi:gi + 1, :].broadcast_to([128, 1024])),
                  writes=[("gB", gi)])
        cp(identb[:], identf, ["cs"], ["identb"])
        cp(maskPb[:], maskPf, ["cstage"], ["maskPb"])
        cp(maskSb[:], maskSf, ["cstage"], ["maskSb"])
        cp(resetmb[:], resetmf, ["cstage"], ["resetmb"])
        cp(pmix[:].rearrange("p g c -> p (g c)"), pmixf[:], ["pmixf"], ["pmix"])
        A("dve", lambda e: e.memset(onesd[:], 1.0 / 128.0), [], ["onesd"])
        A("dve", lambda e: e.memset(carry[:].rearrange("p h v -> p (h v)"), 0.0), [], ["carry"])
        A("dve", lambda e: e.memset(ucarry[:].rearrange("p g t -> p (g t)"), 0.0), [], ["ucarry"])
        tt(lbv[:, 0:4], small[:, 8:12], small[:, 12:16], ALU.subtract, ["small"], ["lbv"])
        act(lbv[:, 0:4], lbv[:, 0:4], AF.Sigmoid, ["lbv"], ["lbv"])
        ts(lbv[:, 4:8], lbv[:, 0:4], -1.0, 1.0, ALU.mult, ALU.add, ["lbv"], ["lbv"])

        def run_tile(ti):
            wstate["tile"] = ti
            has_s = ti == 0
            last = ti == 3
            NT = 640 if has_s else 512
            blocks = [0, 1, 2, 3] + ([4] if has_s else [])
            subs = [(0, 512, "p")] + ([(512, 128, "s")] if has_s else [])
            hTk_p = [("hT", b) for b in range(4)]
            hTk = {"p": hTk_p, "s": [("hT", 4)]}

            def K(name):
                return [(name, "p")] + ([(name, "s")] if has_s else [])

            def xbuf(tj):
                return (xres, "xres") if tj % 2 == 0 else (xalt, "xalt")

            X, xk = xbuf(ti)

            def load_x(tj, b):
                src = xp[tj * 512 + b * 128: tj * 512 + (b + 1) * 128, :] if b < 4 else xs
                Xn, xkn = xbuf(tj)
                P.dma("sp", f"d_x{tj % 2}_{b}", lambda e, b=b, src=src, Xn=Xn: e.dma_start(out=Xn[:, b, :], in_=src),
                      writes=[(xkn, b)])

            if ti == 0:
                for b in blocks:
                    load_x(0, b)
            if ti == 0:
                for _ in range(NRING):
                    w_issue(extra_reads=[(xk, b) for b in blocks])
            if has_s:
                for half in range(2):
                    P.dma("sp", f"d_stg{half}",
                          lambda e, half=half: e.dma_start(out=stg[0:120, half, :], in_=spool[half * 120:(half + 1) * 120, :]),
                          writes=[("stg", half)])

            def half_stats(b, half, X_=None, xk_=None, ssq_=None, tag="ssq"):
                X_ = X if X_ is None else X_
                xk_ = xk if xk_ is None else xk_
                ssq_ = ssq if ssq_ is None else ssq_
                act(junk[:, 0:512], X_[:, b, half * 512:(half + 1) * 512], AF.Square, [(xk_, b)], ["junk", (tag, b, half)],
                    accum_out=ssq_[:, 2 * b + half:2 * b + half + 1])

            def norm_stats(have_partials, blks=None, X_=None, xk_=None, ssq_=None, rst_=None, tag="ssq", rtag="rst"):
                blks = blocks if blks is None else blks
                ssq_ = ssq if ssq_ is None else ssq_
                rst_ = rst if rst_ is None else rst_
                if not have_partials:
                    for b in blks:
                        for half in range(2):
                            half_stats(b, half, X_, xk_, ssq_, tag)
                nbk = len(blks)
                tt(rst_[:, 0:nbk], ssq_[:, 0:2 * nbk:2], ssq_[:, 1:2 * nbk:2], ALU.add,
                   [(tag, b, hf) for b in blks for hf in range(2)], [rtag])
                act(rst_[:, 0:nbk], rst_[:, 0:nbk], AF.Ln, [rtag], [rtag], scale=1.0 / 1024.0, bias=EPS)
                act(rst_[:, 0:nbk], rst_[:, 0:nbk], AF.Exp, [rtag], [rtag], scale=-0.5)

            def norm_apply(gi, blks=None, X_=None, xk_=None, rst_=None, rtag="rst"):
                blks = blocks if blks is None else blks
                X_ = X if X_ is None else X_
                xk_ = xk if xk_ is None else xk_
                rst_ = rst if rst_ is None else rst_
                for b in blks:
                    hb = hn[b % 2]
                    hk = f"hn{b % 2}"
                    stt(hb[:], X_[:, b, :], rst_[:, b:b + 1], gB[:, gi, :], ALU.mult, ALU.mult,
                        [(xk_, b), rtag, ("gB", gi)], [hk])
                    bank = nb()
                    bv = pbs[bank][:].bitcast(BF16)
                    for k in range(8):
                        tp(bv[:, k * 128:(k + 1) * 128], hb[:, k * 128:(k + 1) * 128], identb[:], [hk, "identb"], bank)
                    c0 = b * 128
                    cp(hT[:, :, c0:c0 + 128], bv.rearrange("p (k t) -> p k t", k=8), [PB(bank)], [("hT", b)], eng="act")

            def norm_T(gi, have_partials):
                norm_stats(have_partials)
                norm_apply(gi)

            ck("load")
            if ti == 0:
                norm_T(0, False)
            ck("norm1")

            wv, wk = w_get(0)
            wu = wv[:].rearrange("p (k c) -> p k c", k=8)
            ubanks = []
            for g in range(4):
                bp = nb()
                bs = nb() if has_s else None
                for k in range(8):
                    mm(pbs[bp][:, 0:512], wu[:, k, g * 128:(g + 1) * 128], hT[:, k, 0:512], k == 0, k == 7, [wk] + hTk_p, bp)
                    if has_s:
                        mm(pbs[bs][:, 0:128], wu[:, k, g * 128:(g + 1) * 128], hT[:, k, 512:640], k == 0, k == 7, [wk, ("hT", 4)], bs)
                ubanks.append((bp, bs))
                if g == 3:
                    w_release(0)
                w = 2 << g
                cp(ug[:, 0:16], ucarry[:, g, :], ["ucarry"], ["ug"])
                cp(ug[:, 16:528], pbs[bp][:, 0:512], [PB(bp)], ["ug"], eng="act")
                tt(pa[:, 1:528], ug[:, 1:528], ug[:, 0:527], ALU.add, ["ug"], ["pa"])
                sw = pa
                swk = "pa"
                if w >= 4:
                    tt(pb_[:, 3:528], pa[:, 3:528], pa[:, 1:526], ALU.add, ["pa"], ["pb_"])
                    sw, swk = pb_, "pb_"
                if w >= 8:
                    tt(pa[:, 7:528], pb_[:, 7:528], pb_[:, 3:524], ALU.add, ["pb_"], ["pa"])
                    sw, swk = pa, "pa"
                if w >= 16:
                    tt(pb_[:, 15:528], pa[:, 15:528], pa[:, 7:520], ALU.add, ["pa"], ["pb_"])
                    sw, swk = pb_, "pb_"
                stt(pooled[:, g, 0:512], sw[:, 16:528], 1.0 / w, ug[:, 16:528], ALU.mult, ALU.subtract,
                    [swk, "ug"], [("pooled", g, "p")])
                if ti == 0:
                    tt(tmp16[:], sw[:, 16:32], rcv[:, g * 16:(g + 1) * 16], ALU.mult, [swk, "cs"], ["tmp16"])
                    tt(pooled[:, g, 0:16], tmp16[:], ug[:, 16:32], ALU.subtract, ["tmp16", "ug"], [("pooled", g, "p")])
                cp(ucarry[:, g, :], ug[:, 512:528], ["ug"], ["ucarry"])
                if last:
                    bt = nb()
                    tp(pbs[bt][0:15, 0:128], ug[:, 513:528], identf, ["ug", "cs"], bt)
                    cp(npp_sb[0:15, g * 128:(g + 1) * 128], pbs[bt][0:15, 0:128], [PB(bt)], ["npp_sb"], eng="act")
                if has_s:
                    bt = nb()
                    for half in range(2):
                        tp(pbs[bt][:, half * 120:(half + 1) * 120], stg[0:120, half, g * 128:(g + 1) * 128],
                           identf[0:120, 0:120], [("stg", half), "cs"], bt)
                    cp(ue[:, :, 0:15], pbs[bt][:, 0:240].rearrange("p (i r) -> p i r", r=15), [PB(bt)], ["ue"], eng="act")
                    cp(ue[:, :, 15:23], pbs[bs][:, 0:128].rearrange("p (i t) -> p i t", t=8), [PB(bs)], ["ue"], eng="act")
                    tt(sa[:, :, 1:23], ue[:, :, 1:23], ue[:, :, 0:22], ALU.add, ["ue"], ["sa"])
                    ssw, sswk = sa, "sa"
                    if w >= 4:
                        tt(sb_[:, :, 3:23], sa[:, :, 3:23], sa[:, :, 1:21], ALU.add, ["sa"], ["sb_"])
                        ssw, sswk = sb_, "sb_"
                    if w >= 8:
                        tt(sa[:, :, 7:23], sb_[:, :, 7:23], sb_[:, :, 3:19], ALU.add, ["sb_"], ["sa"])
                        ssw, sswk = sa, "sa"
                    if w >= 16:
                        tt(sb_[:, :, 15:23], sa[:, :, 15:23], sa[:, :, 7:15], ALU.add, ["sa"], ["sb_"])
                        ssw, sswk = sb_, "sb_"
                    stt(pooled[:, g, 512:640].rearrange("p (i t) -> p i t", t=8), ssw[:, :, 15:23], 1.0 / w,
                        ue[:, :, 15:23], ALU.mult, ALU.subtract, [sswk, "ue"], [("pooled", g, "s")])
                    cp(npsb[:, g, :].rearrange("p (i r) -> p i r", r=15), ue[:, :, 8:23], ["ue"], [("npsb", g)])
            if last:
                P.dma("sp", "d_npp", lambda e: e.dma_start(out=npp, in_=npp_sb[0:15, :]), reads=["npp_sb"])
            if has_s:
                for half in range(2):
                    bt = nb()
                    for g in range(4):
                        tp(pbs[bt][0:120, g * 128:(g + 1) * 128], npsb[:, g, half * 120:(half + 1) * 120], identf,
                           [("npsb", g), "cs"], bt)
                    cp(nps_sb[0:120, half, :], pbs[bt][0:120, 0:512], [PB(bt)], [("stg", half)], eng="act")
                    P.dma("sp", f"d_nps{half}",
                          lambda e, half=half: e.dma_start(out=nps[half * 120:(half + 1) * 120, :], in_=nps_sb[0:120, half, :]),
                          reads=[("stg", half)])
            ck("u")
            PIPE = not has_s
            if PIPE:
                zc, rcn = [0], [0]

                def nbZ():
                    b = zc[0] % 4
                    zc[0] += 1
                    return b

                def nbR():
                    b = 6 + rcn[0] % 2
                    rcn[0] += 1
                    return b

                gcn = [0]

                def nbG():
                    b = 4 + gcn[0] % 2
                    gcn[0] += 1
                    return b
            else:
                nbZ = nbR = nbG = nb
            gb2 = [gbf, gbf2]
            QE = [qe, qe2]
            DEC = [dec, dec2]
            HS = {h: {} for h in range(4)}

            def poolmix():
                for g in range(4):
                    for (c0, n, sk) in subs:
                        bk = nbR()
                        mm(pbs[bk][:, 0:n], pmix[:, g, :], pooled[:, g, c0:c0 + n], True, True, ["pmix", ("pooled", g, sk)], bk)
                        act(pool_out[:, g, c0:c0 + n], pbs[bk][:, 0:n], AF.Copy, [PB(bk), "small"], [("pool_out", g, sk)],
                            scale=small[:, g:g + 1])

            def z_group(h, j):
                S_ = HS[h]
                if j == 0:
                    wv, wk = w_get(1 + 2 * h)
                    S_["wh"] = wv[:].rearrange("p (k j c) -> p k j c", k=8, j=4)
                    S_["wk"] = wk
                    S_["bs"] = nbZ() if has_s else None
                    S_["zb"] = []
                wh, wk, bs = S_["wh"], S_["wk"], S_["bs"]
                bp = nbZ()
                for k in range(8):
                    mm(pbs[bp][:, 0:512], wh[:, k, j, :], hT[:, k, 0:512], k == 0, k == 7, [wk] + hTk_p, bp)
                    if has_s:
                        mm(pbs[bs][:, j * 128:(j + 1) * 128], wh[:, k, j, :], hT[:, k, 512:640], k == 0, k == 7,
                           [wk, ("hT", 4)], bs)
                S_["zb"].append(bp)
                if j == 3:
                    w_release(1 + 2 * h)

            def z_evac(h):
                S_ = HS[h]
                zb, bs = S_["zb"], S_["bs"]
                gbh = gb2[h % 2]
                gbk = f"gbf{h % 2}"

                def zsrc(j, sk):
                    return (pbs[zb[j]][:, 0:512], PB(zb[j])) if sk == "p" else (pbs[bs][:, j * 128:(j + 1) * 128], PB(bs))

                for (c0, n, sk) in subs:
                    src, key = zsrc(1, sk)
                    act(t1[:, c0:c0 + n], src, AF.Sigmoid, [key], [("t1", sk)])
                    src, key = zsrc(0, sk)
                    act(qf[:, c0:c0 + n], src, AF.Sigmoid, [key], [("qf", sk)])
                    tt(qf[:, c0:c0 + n], qf[:, c0:c0 + n], src, ALU.mult, [("qf", sk), key], [("qf", sk)])
                    src, key = zsrc(3, sk)
                    act(t4[:, c0:c0 + n], src, AF.Sigmoid, [key], [("t4", sk)])
                    tt(gbh[:, c0:c0 + n], t4[:, c0:c0 + n], src, ALU.mult, [("t4", sk), key], [(gbk, sk)])
                    src, key = zsrc(2, sk)
                    cp(vb[:, c0:c0 + n], src, [key], [("vb", sk)])
                warm(AF.Ln)

            def g_mm(h, jj, tsel=(0, 1)):
                S_ = HS[h]
                if jj == 0 and tsel[0] == 0:
                    wgv_, wgk_ = w_get(2 + 2 * h)
                    S_["wgv"] = wgv_[:].rearrange("p (k t c) -> p k t c", k=8, t=4)
                    S_["wgk"] = wgk_
                    S_["gate_ev"] = []
                wgv, wgk_ = S_["wgv"], S_["wgk"]
                j = 2 * h + jj
                if tsel[0] == 0:
                    S_["bsg"] = nbG() if has_s else None
                bsg = S_["bsg"]
                for t_ in tsel:
                    bpg = nbG()
                    for k in range(8):
                        mm(pbs[bpg][:, 0:512], wgv[:, k, 2 * jj + t_, :], hT[:, k, 0:512], k == 0, k == 7, [wgk_] + hTk_p, bpg)
                        if has_s:
                            mm(pbs[bsg][:, t_ * 128:(t_ + 1) * 128], wgv[:, k, 2 * jj + t_, :], hT[:, k, 512:640], k == 0, k == 7,
                               [wgk_, ("hT", 4)], bsg)
                    S_["gate_ev"].append((j, t_, bpg, bsg))
                if jj == 1 and tsel[-1] == 1:
                    w_release(2 + 2 * h)

            def g_evac(h, jj):
                evs = HS[h]["gate_ev"][2 * jj:2 * jj + 2]
                if has_s:
                    for (j, t_, bpg, bsg) in evs:
                        dst = sgA if t_ == 0 else sgB
                        dk = "sgA" if t_ == 0 else "sgB"
                        cp(dst[:, j, 512:640], pbs[bsg][:, t_ * 128:(t_ + 1) * 128], [PB(bsg)], [(dk, j, "s")], eng="act")
                for (j, t_, bpg, bsg) in evs:
                    dst = sgA if t_ == 0 else sgB
                    dk = "sgA" if t_ == 0 else "sgB"
                    cp(dst[:, j, 0:512], pbs[bpg][:, 0:512], [PB(bpg)], [(dk, j, "p")], eng="act")

            def chain_a(h):
                ts(t1[:, 0:NT], t1[:, 0:NT], lbv[:, 4 + h:5 + h], lbv[:, h:h + 1], ALU.mult, ALU.add, K("t1") + ["lbv"], K("t1"))
                ts(t2[:, 0:NT], t1[:, 0:NT], -1.0, 1.0, ALU.mult, ALU.add, K("t1"), K("t2"))
                act(t1[:, 0:NT], t1[:, 0:NT], AF.Ln, K("t1"), K("t1"))

            def cb_scan(h):
                A("dve", lambda e: e.tensor_tensor_scan(out=t3[:, 0:NT], data0=resetm[:, 0:NT], data1=t1[:, 0:NT],
                                                        initial=0.0, op0=ALU.mult, op1=ALU.add),
                  K("t1") + ["resetmb"], K("t3"))

            def cb_exp(h):
                act(t1[:, 0:NT], t3[:, 0:NT], AF.Exp, K("t3"), K("t1"))
                act(t4[:, 0:NT], t3[:, 0:NT], AF.Exp, K("t3"), K("t4"), scale=-1.0)

            def cb_mul(h):
                qe_ = QE[h % 2]
                qk = f"qe{h % 2}"
                tt(qe_[:, 0:NT], qf[:, 0:NT], t1[:, 0:NT], ALU.mult, K("qf") + K("t1"), [(qk, "p")] + ([(qk, "s")] if has_s else []))
                tt(t2[:, 0:NT], t2[:, 0:NT], t4[:, 0:NT], ALU.mult, K("t2") + K("t4"), K("t2"))

            def cb_keb(h):
                cp(keb[:, 0:NT], t2[:, 0:NT], K("t2"), K("keb"), eng="act")

            def cb_dec(h):
                dec_ = DEC[h % 2]
                dk_ = f"dec{h % 2}"
                cp(dec_[:, 0:8], t1[:, 63:512:64], [("t1", "p")], [(dk_, "p")])
                tt(kdb[:, 0:512].rearrange("p (c t) -> p c t", t=64), t2[:, 0:512].rearrange("p (c t) -> p c t", t=64),
                   dec_[:, 0:8].unsqueeze(2).broadcast_to([128, 8, 64]), ALU.mult, [("t2", "p"), (dk_, "p")], [("kdb", "p")])
                if has_s:
                    cp(dec_[:, 8:24], t1[:, 519:640:8], [("t1", "s")], [(dk_, "s")])
                    tt(kdb[:, 512:640].rearrange("p (c t) -> p c t", t=8), t2[:, 512:640].rearrange("p (c t) -> p c t", t=8),
                       dec_[:, 8:24].unsqueeze(2).broadcast_to([128, 16, 8]), ALU.mult, [("t2", "s"), (dk_, "s")], [("kdb", "s")])

            def chain_b(h):
                cb_scan(h); cb_exp(h); cb_mul(h); cb_keb(h); cb_dec(h)

            def R1(h):
                qe = QE[h % 2]
                qk = f"qe{h % 2}"
                for (src_, srck, dst, dstk) in ((kdb, "kdb", kdT, "kdT"), (vb, "vb", vT, "vT")):
                    bt = nbR()
                    bv = pbs[bt][:].bitcast(BF16)
                    for b in blocks:
                        sk = "p" if b < 4 else "s"
                        tp(bv[:, b * 128:(b + 1) * 128], src_[:, b * 128:(b + 1) * 128], identb[:], [(srck, sk), "identb"], bt)
                    nbk = len(blocks)
                    cp(dst[:, 0:nbk, :], bv[:, 0:nbk * 128].rearrange("p (b k) -> p b k", k=128), [PB(bt)], [dstk], eng="act")
                ba = nbR()
                for b in range(4):
                    mm(pbs[ba][:, b * 128:(b + 1) * 128], keb[:, b * 128:(b + 1) * 128], qe[:, b * 128:(b + 1) * 128],
                       True, True, [("keb", "p"), (qk, "p")], ba)
                tt(Am[:, 0:4, :], pbs[ba][:, 0:512].rearrange("p (b t) -> p b t", b=4),
                   maskPb[:].unsqueeze(1).broadcast_to([128, 4, 128]), ALU.mult, [PB(ba), "maskPb"], [("Am", "p")])
                if has_s:
                    bas = nbR()
                    mm(pbs[bas][:, 0:128], keb[:, 512:640], qe[:, 512:640], True, True, [("keb", "s"), (qk, "s")], bas)
                    tt(Am[:, 4, :], pbs[bas][:, 0:128], maskSb[:], ALU.mult, [PB(bas), "maskSb"], [("Am", "s")])

            def build_vm():
                tt(Vm[:, :, :], vT[:, 4, :].unsqueeze(1).broadcast_to([128, 16, 128]),
                   seqm.unsqueeze(2).broadcast_to([128, 16, 128]), ALU.mult, ["vT", "cs"], ["Vm"])

            def sample_prefetch(h):
                P.dma("sp", "d_S0", lambda e, h=h: e.dma_start(out=S0[:, :, :], in_=shg[:, h, :, :].rearrange("i k v -> k i v")),
                      writes=["S0"])

            def sample_bf(h):
                cp(S0bf[:, :, :].rearrange("p i v -> p (i v)"), S0[:, :, :].rearrange("p i v -> p (i v)"), ["S0"], ["S0bf"], eng="act")

            def sample_state_update(h):
                dec = DEC[h % 2]
                dk_ = f"dec{h % 2}"
                tt(S0[:, :, :], S0[:, :, :], dec[:, 8:24].unsqueeze(2).broadcast_to([128, 16, 128]), ALU.mult,
                   ["S0", (dk_, "s")], ["S0"])
                for q4 in range(4):
                    bd = nbR()
                    mm(pbs[bd][:, 0:512], kdT[:, 4, :], Vm[:, 4 * q4:4 * q4 + 4, :].rearrange("p i v -> p (i v)"), True, True,
                       ["kdT", "Vm"], bd)
                    tt(S0[:, 4 * q4:4 * q4 + 4, :].rearrange("p i v -> p (i v)"),
                       S0[:, 4 * q4:4 * q4 + 4, :].rearrange("p i v -> p (i v)"), pbs[bd][:, 0:512], ALU.add,
                       ["S0", PB(bd)], ["S0"])
                P.dma("sp", "d_nhs", lambda e, h=h: e.dma_start(out=nhs[:, h, :, :].rearrange("i k v -> k i v"), in_=S0[:, :, :]),
                      reads=["S0"])

            def R2_mm(h):
                dsb = [nbR(), nbR()]
                HS[h]["dsb"] = dsb
                for c in range(8):
                    blk, half = c // 2, c % 2
                    bk = dsb[half]
                    mm(pbs[bk][:, blk * 128:(blk + 1) * 128], kdT[half * 64:(half + 1) * 64, blk, :],
                       vT[half * 64:(half + 1) * 64, blk, :], True, True, ["kdT", "vT"], bk)
                cp(Sall[:, 0, :], carry[:, h, :], ["carry"], [("Sall", 0)])

            def R2_steps(h, c0, c1):
                dec_ = DEC[h % 2]
                dk_ = f"dec{h % 2}"
                dsb = HS[h]["dsb"]
                for c in range(c0, c1):
                    blk, half = c // 2, c % 2
                    bk = dsb[half]
                    stt(Sall[:, c + 1, :], Sall[:, c, :], dec_[:, c:c + 1], pbs[bk][:, blk * 128:(blk + 1) * 128],
                        ALU.mult, ALU.add, [("Sall", c), (dk_, "p"), PB(bk)], [("Sall", c + 1)])

            def R2_half(h):
                cp(Sbf[:, 0:4, :].rearrange("p c v -> p (c v)"), Sall[:, 0:4, :].rearrange("p c v -> p (c v)"),
                   [("Sall", c) for c in range(4)], [("Sbf", 0)], eng="act")

            def R2_fin(h):
                cp(Sbf[:, 4:8, :].rearrange("p c v -> p (c v)"), Sall[:, 4:8, :].rearrange("p c v -> p (c v)"),
                   [("Sall", c) for c in range(4, 8)], [("Sbf", 1)], eng="act")
                cp(carry[:, h, :], Sall[:, 8, :], [("Sall", 8)], ["carry"])
                if last:
                    P.dma("sp", f"d_nhp{h}", lambda e, h=h: e.dma_start(out=nhp[h], in_=Sall[:, 8, :]), reads=[("Sall", 8)])

            def R2(h):
                R2_mm(h); R2_steps(h, 0, 4); R2_half(h); R2_steps(h, 4, 8); R2_fin(h)

            def R3(h):
                qe = QE[h % 2]
                qk = f"qe{h % 2}"
                S_ = HS[h]
                bo = nbR()
                for b in range(4):
                    mm(pbs[bo][:, b * 128:(b + 1) * 128], vT[:, b, :], Am[:, b, :], True, False, ["vT", ("Am", "p")], bo)
                    for half in range(2):
                        c = 2 * b + half
                        mm(pbs[bo][:, c * 64:(c + 1) * 64], Sbf[:, c, :], qe[:, c * 64:(c + 1) * 64], False, half == 1,
                           [("Sbf", c // 4), (qk, "p")], bo)
                bos = None
                if has_s:
                    bos = nbR()
                    mm(pbs[bos][:, 0:128], vT[:, 4, :], Am[:, 4, :], True, False, ["vT", ("Am", "s")], bos)
                    for i in range(16):
                        mm(pbs[bos][:, i * 8:(i + 1) * 8], S0bf[:, i, :], qe[:, 512 + i * 8:512 + (i + 1) * 8], False, i == 15,
                           ["S0bf", (qk, "s")], bos)
                S_["obanks"] = [(0, 512, "p", bo)] + ([(512, 128, "s", bos)] if has_s else [])
                for (c0, n, sk, bk) in S_["obanks"]:
                    act(osq[:, c0:c0 + n], pbs[bk][:, 0:n], AF.Square, [PB(bk)], [("osq", sk)])

            def R4(h):
                gbh = gb2[h % 2]
                gbk = f"gbf{h % 2}"
                for (c0, n, sk, bk) in HS[h]["obanks"]:
                    bn = nbR()
                    mm(pbs[bn][:, 0:n], onesd[:], osq[:, c0:c0 + n], True, True, ["onesd", ("osq", sk)], bn)
                    act(s1[:, c0:c0 + n], pbs[bn][:, 0:n], AF.Ln, [PB(bn)], [("s1", sk)], bias=EPS)
                    act(s1[:, c0:c0 + n], s1[:, c0:c0 + n], AF.Exp, [("s1", sk)], [("s1", sk)], scale=-0.5)
                    stt(s2[:, c0:c0 + n], pbs[bk][:, 0:n], small[:, 4 + h:5 + h], s1[:, c0:c0 + n], ALU.mult, ALU.mult,
                        [PB(bk), "small", ("s1", sk)], [("s2", sk)])
                tt(o_fin[:, h, 0:NT], s2[:, 0:NT], gbh[:, 0:NT], ALU.mult, K("s2") + [(gbk, "p"), (gbk, "s")],
                   [("o_fin", h, s_) for s_ in ("p", "s")])

            if PIPE:
                for j in range(4):
                    z_group(0, j)
            poolmix()
            ck("poolmix")
            P.barrier()
            ck("bar")
            if PIPE:
                for h in range(4):
                    p = h - 1
                    if h > 0:
                        R1(p)
                    z_evac(h)
                    g_mm(h, 0)
                    if h > 0:
                        R2_mm(p)
                    chain_a(h)
                    if h > 0:
                        R2_steps(p, 0, 4)
                        R2_half(p)
                    g_evac(h, 0)
                    if h < 3:
                        for j in range(4):
                            z_group(h + 1, j)
                    cb_scan(h)
                    if h > 0:
                        R2_steps(p, 4, 7)
                    cb_exp(h)
                    cb_mul(h)
                    cb_dec(h)
                    cb_keb(h)
                    if h < 3:
                        if h > 0:
                            R2_steps(p, 7, 8)
                            R2_fin(p)
                            R3(p)
                        g_mm(h, 1, (0,))
                        if h > 0:
                            R4(p)
                        g_mm(h, 1, (1,))
                        g_evac(h, 1)
                    else:
                        g_mm(h, 1)
                        R2_steps(p, 7, 8)
                        R2_fin(p)
                        R3(p)
                        g_evac(h, 1)
                        R4(p)
                    if h < 3:
                        warm(AF.Sigmoid)
                R1(3); R2(3); R3(3); R4(3)
                warm(AF.Sigmoid)
            else:
                for h in range(4):
                    sample_prefetch(h)
                    for j in range(4):
                        z_group(h, j)
                    z_evac(h)
                    g_mm(h, 0)
                    g_mm(h, 1)
                    chain_a(h)
                    g_evac(h, 0)
                    chain_b(h)
                    R1(h)
                    g_evac(h, 1)
                    sample_bf(h)
                    R2_mm(h)
                    R2_steps(h, 0, 4)
                    R2_half(h)
                    R2_steps(h, 4, 8)
                    build_vm()
                    R2_fin(h)
                    R3(h); R4(h)
                    warm(AF.Sigmoid)
                    sample_state_update(h)

            ck("hgrn")
            for jh in range(2):
                wy, wyk = w_get(9 + jh)
                wyv = wy[:].rearrange("p (k t c) -> p k t c", k=4, t=2)
                for jj in range(4):
                    j = jh * 4 + jj
                    bs = nb() if has_s else None
                    b_ya, b_yb = nb(), nb()
                    for t_, bk, srcb, srck in ((0, b_ya, pool_out, "pool_out"), (1, b_yb, o_fin, "o_fin")):
                        for kk in range(4):
                            for (c0, n, sk) in subs:
                                o = pbs[bk][:, 0:512] if sk == "p" else pbs[bs][:, t_ * 128:(t_ + 1) * 128]
                                mm(o, wyv[:, kk, t_, jj * 128:(jj + 1) * 128], srcb[:, kk, c0:c0 + n], kk == 0, kk == 3,
                                   [wyk, (srck, kk, sk)], bk if sk == "p" else bs)
                    if jj == 3:
                        w_release(9 + jh)
                    m1, m1k = (s1, "s1") if j % 2 == 0 else (t3, "t3")
                    m2, m2k = (s2, "s2") if j % 2 == 0 else (t4, "t4")
                    for (c0, n, sk) in subs:
                        oa = pbs[b_ya][:, 0:512] if sk == "p" else pbs[bs][:, 0:128]
                        ob = pbs[b_yb][:, 0:512] if sk == "p" else pbs[bs][:, 128:256]
                        ka = PB(b_ya) if sk == "p" else PB(bs)
                        kb_ = PB(b_yb) if sk == "p" else PB(bs)
                        act(m1[:, c0:c0 + n], sgA[:, j, c0:c0 + n], AF.Sigmoid, [("sgA", j, sk)], [(m1k, sk)])
                        act(m2[:, c0:c0 + n], sgB[:, j, c0:c0 + n], AF.Sigmoid, [("sgB", j, sk)], [(m2k, sk)])
                        tt(m1[:, c0:c0 + n], m1[:, c0:c0 + n], oa, ALU.mult, [(m1k, sk), ka], [(m1k, sk)])
                        tt(m2[:, c0:c0 + n], m2[:, c0:c0 + n], ob, ALU.mult, [(m2k, sk), kb_], [(m2k, sk)])
                    tt(merged[:, j, 0:NT], m1[:, 0:NT], m2[:, 0:NT], ALU.add, K(m1k) + K(m2k),
                       [("merged", j, s_) for s_ in ("p", "s")])

            ck("merge")
            warm(AF.Ln)
            for half in range(2):
                wo, wok = w_get(11 + half)
                wov = wo[:].rearrange("p (k c) -> p k c", k=8)
                bks = {b: nb() for b in blocks}
                for k in range(8):
                    for b in blocks:
                        sk = "p" if b < 4 else "s"
                        mm(pbs[bks[b]][:, 0:512], merged[:, k, b * 128:(b + 1) * 128], wov[:, k, :], k == 0, k == 7,
                           [wok, ("merged", k, sk)], bks[b])
                w_release(11 + half)
                for b in blocks:
                    tt(X[:, b, half * 512:(half + 1) * 512], X[:, b, half * 512:(half + 1) * 512], pbs[bks[b]][:, 0:512],
                       ALU.add, [(xk, b), PB(bks[b])], [(xk, b)])
                    half_stats(b, half)

            ck("wout")
            norm_T(1, True)
            warm(AF.Sigmoid)
            P.barrier()
            if ti + 1 < ntiles:
                for b in range(4):
                    load_x(ti + 1, b)
            for b in blocks:
                src = ppd[ti * 512 + b * 128: ti * 512 + (b + 1) * 128, :] if b < 4 else psd
                P.dma("sp", f"d_p{b}", lambda e, src=src, b=b: e.dma_start(out=pstage[:, b, :], in_=src), writes=[("pstage", b)])
            for jb in range(11):
                wf, wfk = w_get(13 + jb)
                wfv = wf[:].rearrange("p (k j c) -> p k j c", k=8, j=2)
                for cc in range(2):
                    f = 2 * jb + cc
                    bs = nb() if has_s else None
                    b_g, b_u = nb(), nb()
                    for jx, bk in ((0, b_g), (1, b_u)):
                        for k in range(8):
                            for (c0, n, sk) in subs:
                                o = pbs[bk][:, 0:512] if sk == "p" else pbs[bs][:, jx * 128:(jx + 1) * 128]
                                mm(o, wfv[:, k, jx, cc * 128:(cc + 1) * 128], hT[:, k, c0:c0 + n], k == 0, k == 7,
                                   [wfk] + hTk[sk], bk if sk == "p" else bs)
                    if cc == 1:
                        w_release(13 + jb)
                    for (c0, n, sk) in subs:
                        og = pbs[b_g][:, 0:512] if sk == "p" else pbs[bs][:, 0:128]
                        ou = pbs[b_u][:, 0:512] if sk == "p" else pbs[bs][:, 128:256]
                        kg = PB(b_g) if sk == "p" else PB(bs)
                        ku = PB(b_u) if sk == "p" else PB(bs)
                        act(ftmp[:, c0:c0 + n], og, AF.Sigmoid, [kg], [("ftmp", sk)])
                        tt(ftmp[:, c0:c0 + n], ftmp[:, c0:c0 + n], og, ALU.mult, [("ftmp", sk), kg], [("ftmp", sk)])
                        tt(hidden[:, f, c0:c0 + n], ftmp[:, c0:c0 + n], ou, ALU.mult, [("ftmp", sk), ku], [("hidden", f, sk)])
            for b in blocks:
                cp(pbf[:], pstage[:, b, :], [("pstage", b)], ["pbf"], eng="act")
                bt = nb()
                bv = pbs[bt][:].bitcast(BF16)
                for k in range(2):
                    tp(bv[:, k * 128:(k + 1) * 128], pbf[:, k * 128:(k + 1) * 128], identb[:], ["pbf", "identb"], bt)
                cp(pT[:, :, b * 128:(b + 1) * 128], bv[:, 0:256].rearrange("p (k t) -> p k t", k=2), [PB(bt)], [("pT", b)])
            warm(AF.Ln)
            for half in range(2):
                bks = {b: nb() for b in blocks}
                for kb in range(3):
                    wd, wdk = w_get(24 + half * 3 + kb)
                    wdv = wd[:].rearrange("p (k c) -> p k c", k=8)
                    nk = 8 if kb < 2 else 6
                    for kk in range(nk):
                        f = kb * 8 + kk
                        for b in blocks:
                            sk = "p" if b < 4 else "s"
                            mm(pbs[bks[b]][:, 0:512], hidden[:, f, b * 128:(b + 1) * 128], wdv[:, kk, :], f == 0, f == 21,
                               [wdk, ("hidden", f, sk)], bks[b])
                    w_release(24 + half * 3 + kb)
                for b in blocks:
                    tt(X[:, b, half * 512:(half + 1) * 512], X[:, b, half * 512:(half + 1) * 512], pbs[bks[b]][:, 0:512],
                       ALU.add, [(xk, b), PB(bks[b])], [(xk, b)])
                    half_stats(b, half)

            ck("ffn")
            norm_T(2, True)
            if ti + 1 < ntiles:
                Xn, xkn = xbuf(ti + 1)
                norm_stats(False, blks=[0, 1, 2, 3], X_=Xn, xk_=xkn, ssq_=ssq1, rst_=rst1, tag="ssq1", rtag="rst1")
            warm(AF.Sigmoid)
            wg0, wg0k = w_get(30)
            wpp_, wppk = w_get(31)
            wg1, wg1k = w_get(32)
            wppv = wpp_[:, 0:2048].rearrange("p (k c) -> p k c", k=2)
            for half in range(2):
                wg_, wgk = (wg0, wg0k) if half == 0 else (wg1, wg1k)
                wgv = wg_[:].rearrange("p (k c) -> p k c", k=8)
                bks = {b: nb() for b in blocks}
                for k in range(8):
                    for b in blocks:
                        mm(pbs[bks[b]][:, 0:512], hT[:, k, b * 128:(b + 1) * 128], wgv[:, k, :], k == 0, k == 7,
                           [wgk, ("hT", b)], bks[b])
                if half == 0:
                    w_release(30)
                sgl = [(sgt[0][:], ["sgt0"]), (sgt[1][:], ["sgt1"]), (ftmp[:, 0:512], [("ftmp", "p")]),
                       (pstage[:, 0:2, :].rearrange("p b c -> p (b c)"), [("pstage", 0), ("pstage", 1)])]
                for b in blocks[:4]:
                    sg, sgk = sgl[b % 4]
                    act(sg, pbs[bks[b]][:, 0:512], AF.Sigmoid, [PB(bks[b])], sgk)
                for b in blocks:
                    sg, sgk = sgl[b % 4]
                    if b >= 4:
                        act(sg, pbs[bks[b]][:, 0:512], AF.Sigmoid, [PB(bks[b])], sgk)
                    be = nb()
                    for k in range(2):
                        mm(pbs[be][:, 0:512], pT[:, k, b * 128:(b + 1) * 128], wppv[:, k, half * 512:(half + 1) * 512], k == 0, k == 1,
                           [wppk, ("pT", b)], be)
                    tt(sg, sg, pbs[be][:, 0:512], ALU.mult, sgk + [PB(be)], sgk)
                    tt(X[:, b, half * 512:(half + 1) * 512], X[:, b, half * 512:(half + 1) * 512], sg, ALU.add,
                       [(xk, b)] + sgk, [(xk, b)])
                for b in blocks:
                    half_stats(b, half)
                if half == 1:
                    w_release(31); w_release(32)

            ck("ple")
            if ti + 1 < ntiles:
                Xn, xkn = xbuf(ti + 1)
                norm_apply(0, blks=[0, 1, 2, 3], X_=Xn, xk_=xkn, rst_=rst1, rtag="rst1")
            P.barrier()
            norm_stats(True)
            for b in blocks:
                dst = yp[ti * 512 + b * 128: ti * 512 + (b + 1) * 128, :] if b < 4 else ys
                for half in range(2):
                    q_ = (2 * b + half) % 4
                    yst = ystage[q_]
                    ysk = f"ystage{q_}"
                    hs_ = slice(half * 512, (half + 1) * 512)
                    stt(yst[:], X[:, b, hs_], rst[:, b:b + 1], gB[:, 3, hs_], ALU.mult, ALU.mult,
                        [(xk, b), "rst", ("gB", 3)], [ysk])
                    P.dma("sp", f"d_y{q_}", lambda e, yst=yst, dst=dst, hs_=hs_: e.dma_start(out=dst[:, hs_], in_=yst[:]),
                          reads=[ysk])

        P.barrier()
        try:
            ck("setup")
            for ti in range(ntiles):
                run_tile(ti)
        except _Stop:
            pass
        P.finish("sp")
        P.emit()
    return nc


_PROG = {}


def _prep_inputs(inp):
    f = lambda a: np.ascontiguousarray(np.asarray(a, dtype=np.float32))
    w_in = f(inp["w_in"][0])
    wallv = build_wall(w_in, f(inp["w_pool_up"][0]), f(inp["w_hgrn_up"][0]), f(inp["w_out"][0]),
                       f(inp["w_ffn_gate"][0]), f(inp["w_ffn_up"][0]), f(inp["w_ffn_down"][0]),
                       f(inp["w_ple_gate"][0]), f(inp["w_ple_proj"][0]))
    gvec = np.ascontiguousarray(np.stack([f(inp["g_mix"][0]), f(inp["g_ffn"][0]), f(inp["g_ple"][0]), f(inp["g_final"])], 0))
    small = np.zeros((128, 16), np.float32)
    small[:, 0:4] = f(inp["pool_scale"][0]).reshape(4, 128).T
    small[:, 4:8] = f(inp["hgrn_norm"][0]).reshape(4, 128).T
    small[:, 8:12] = f(inp["hgrn_lb"][0]).reshape(4, 128).T
    small[:, 12:16] = f(inp["hgrn_lb"][1]).reshape(4, 128).T
    pmix = np.ascontiguousarray(f(inp["w_pool_mix"][0]).transpose(1, 0, 2)).reshape(128, 512)
    xp = f(inp["x_prompt"]); xsm = f(inp["x_sample"])
    ppr = f(inp["p_prompt"][0]); psm = f(inp["p_sample"][0])
    spl = f(inp["state_pool"][0]); shg = f(inp["state_hgrn"][0])
    maps = []
    for c in range(NCORES):
        maps.append({
            "xp": xp[c], "xs": xsm[16 * c:16 * c + 16].reshape(128, 1024),
            "pp": ppr[c], "ps": psm[16 * c:16 * c + 16].reshape(128, 256),
            "spool": spl[16 * c:16 * c + 16].reshape(240, 512),
            "shg": shg[16 * c:16 * c + 16],
            "wall": wallv, "cst": CONST_ARR, "gvec": gvec, "small": small, "pmix": pmix,
        })
    return maps


def kernel(**inputs):
    if "nc" not in _PROG:
        _PROG["nc"] = build_program()
    nc = _PROG["nc"]
    maps = _prep_inputs(inputs)
    res = run_bass_kernel_spmd(nc, maps, core_ids=list(range(NCORES)))
    R = res.results
    y_p = np.stack([R[c]["yp"] for c in range(NCORES)], 0).astype(np.float32)
    y_s = np.concatenate([R[c]["ys"].reshape(16, 8, 1024) for c in range(NCORES)], 0).astype(np.float32)
    npp = np.stack([R[c]["npp"] for c in range(NCORES)], 0)[None].astype(np.float32)
    nhp = np.stack([R[c]["nhp"] for c in range(NCORES)], 0)[None].astype(np.float32)
    nps = np.concatenate([R[c]["nps"].reshape(16, 15, 512) for c in range(NCORES)], 0)[None].astype(np.float32)
    nhs = np.concatenate([R[c]["nhs"] for c in range(NCORES)], 0)[None].astype(np.float32)
    return (y_p, y_s, npp, nhp, nps, nhs)
```

```python
import numpy as np
from contextlib import ExitStack
import concourse.bass as bass
import concourse.mybir as mybir
from concourse.bass_utils import run_bass_kernel_spmd

F32 = mybir.dt.float32
BF16 = mybir.dt.bfloat16
AF = mybir.ActivationFunctionType
ALU = mybir.AluOpType

ENGS = ("pe", "act", "dve", "pool", "sp")
EPS = 1e-6
NCORES = 8
NRING = 5
NTILES = 4
import os as _os
SAME_DIST = int(_os.environ.get("SAME_DIST", str(1 << 30)))


class Prog:
    def __init__(self, nc, stack, same_engine_wait=True):
        self.nc = nc
        self.stack = stack
        self.streams = {e: [] for e in ENGS}
        self.count = {e: 0 for e in ENGS}
        self.known = {e: {} for e in ENGS}
        self.sems = {}
        self.dma_count = {}
        self.lastw = {}
        self.readers = {}
        self.same_engine_wait = same_engine_wait

    def _collect(self, eng, reads, writes):
        deps = []
        for k in reads:
            ev = self.lastw.get(k)
            if ev is not None:
                deps.append(ev)
        for k in writes:
            ev = self.lastw.get(k)
            if ev is not None:
                deps.append(ev)
            rd = self.readers.get(k)
            if rd:
                deps.extend(rd.values())
        kn = self.known[eng]
        need = {}
        for (s, v, vc) in deps:
            if s == eng and (eng == "pe" or not self.same_engine_wait or self.count[eng] - v >= SAME_DIST):
                continue
            if kn.get(s, 0) >= v:
                continue
            if need.get(s, 0) < v:
                need[s] = v
            for s2, v2 in vc.items():
                if s2 == eng:
                    continue
                if kn.get(s2, 0) < v2:
                    kn[s2] = v2
        waits = []
        for s, v in need.items():
            waits.append((s, v))
            if kn.get(s, 0) < v:
                kn[s] = v
        return waits

    def _record(self, ev, reads, writes):
        s = ev[0]
        for k in reads:
            self.readers.setdefault(k, {})[s] = ev
        for k in writes:
            self.lastw[k] = ev
            self.readers[k] = {}

    def op(self, eng, fn, reads=(), writes=()):
        waits = self._collect(eng, reads, writes)
        self.count[eng] += 1
        n = self.count[eng]
        vc = dict(self.known[eng])
        vc[eng] = n
        ev = (eng, n, vc)
        self.streams[eng].append((waits, fn, (eng, 1)))
        self._record(ev, reads, writes)
        return ev

    def dma(self, q, semname, fn, reads=(), writes=()):
        waits = self._collect(q, reads, writes)
        self.dma_count[semname] = self.dma_count.get(semname, 0) + 1
        v = 16 * self.dma_count[semname]
        vc = dict(self.known[q])
        vc[semname] = v
        ev = (semname, v, vc)
        self.streams[q].append((waits, fn, (semname, 16)))
        self._record(ev, reads, writes)
        return ev

    def barrier(self, engines=("act", "dve", "sp")):
        for e in engines:
            kn = self.known[e]
            waits = []
            for e2 in ("pe", "act", "dve"):
                c = self.count[e2]
                if c and kn.get(e2, 0) < c:
                    waits.append((e2, c))
                    kn[e2] = c
            for s, c in self.dma_count.items():
                if s.startswith("ring") or s.startswith("d_y") or s.startswith("d_x"):
                    continue
                if kn.get(s, 0) < 16 * c:
                    waits.append((s, 16 * c))
                    kn[s] = 16 * c
            if waits:
                self.streams[e].append((waits, None, None))

    def finish(self, eng="sp"):
        kn = self.known[eng]
        waits = []
        for s, c in self.dma_count.items():
            if kn.get(s, 0) < 16 * c:
                waits.append((s, 16 * c))
                kn[s] = 16 * c
        for e in ("pe", "act", "dve"):
            if self.count[e] and kn.get(e, 0) < self.count[e]:
                waits.append((e, self.count[e]))
        self.streams[eng].append((waits, None, None))

    def emit(self):
        nc = self.nc
        for s in list(ENGS) + list(self.dma_count):
            if s not in self.sems:
                self.sems[s] = self.stack.enter_context(nc.semaphore(s))
        with nc.Block() as block:
            def run(engine, items):
                for waits, fn, inc in items:
                    for s, v in waits:
                        engine.wait_ge(self.sems[s], v)
                    if fn is not None:
                        ins = fn(engine)
                        ins.then_inc(self.sems[inc[0]], inc[1])

            @block.tensor
            def _(eng):
                run(eng, self.streams["pe"])

            @block.scalar
            def _(eng):
                run(eng, self.streams["act"])

            @block.vector
            def _(eng):
                run(eng, self.streams["dve"])

            @block.gpsimd
            def _(eng):
                run(eng, self.streams["pool"])

            @block.sync
            def _(eng):
                run(eng, self.streams["sp"])


NBLK = 33


def _kc(W, nk):
    C = W.shape[1]
    return np.ascontiguousarray(W.reshape(nk, 128, C).transpose(1, 0, 2)).reshape(128, nk * C)


def _pad(a):
    out = np.zeros((128, 4096), np.float32)
    out[:, : a.shape[1]] = a
    return out


def build_wall(w_in, w_pool_up, w_hgrn_up, w_out, w_g, w_u, w_d, w_pg, w_pp):
    blocks = []
    blocks.append(_kc(w_in[:, 0:512], 8))
    zz = w_in[:, 512:2560].reshape(8, 128, 4, 4, 128)
    ga = w_in[:, 2560:3584].reshape(8, 128, 8, 128)
    gb = w_in[:, 3584:4608].reshape(8, 128, 8, 128)
    for h in range(4):
        blocks.append(np.ascontiguousarray(zz[:, :, :, h, :].transpose(1, 0, 2, 3)).reshape(128, 4096))
        gg = np.stack([ga[:, :, 2 * h], gb[:, :, 2 * h], ga[:, :, 2 * h + 1], gb[:, :, 2 * h + 1]], axis=2)
        blocks.append(np.ascontiguousarray(gg.transpose(1, 0, 2, 3)).reshape(128, 4096))
    for half in range(2):
        yy = np.stack([w_pool_up[:, half * 512:(half + 1) * 512].reshape(4, 128, 512),
                       w_hgrn_up[:, half * 512:(half + 1) * 512].reshape(4, 128, 512)], axis=2)
        blocks.append(np.ascontiguousarray(yy.transpose(1, 0, 2, 3)).reshape(128, 4096))
    blocks.append(_kc(w_out[:, 0:512], 8))
    blocks.append(_kc(w_out[:, 512:1024], 8))
    for jb in range(11):
        gu = np.stack([w_g[:, jb * 256:(jb + 1) * 256], w_u[:, jb * 256:(jb + 1) * 256]], axis=1)
        blocks.append(_kc(gu.reshape(1024, 512), 8))
    for half in range(2):
        for kb in range(3):
            nk = 8 if kb < 2 else 6
            blocks.append(_pad(_kc(w_d[kb * 1024: kb * 1024 + nk * 128, half * 512:(half + 1) * 512], nk)))
    blocks.append(_kc(w_pg[:, 0:512], 8))
    blocks.append(_pad(_kc(w_pp, 2)))
    blocks.append(_kc(w_pg[:, 512:1024], 8))
    assert len(blocks) == NBLK
    return np.ascontiguousarray(np.stack(blocks, axis=0).astype(np.float32))


def build_consts():
    c = {}
    c["ident"] = np.eye(128, dtype=np.float32)
    s = np.arange(128)[:, None]
    t = np.arange(128)[None, :]
    c["maskP"] = ((s // 64 == t // 64) & (s <= t)).astype(np.float32)
    c["maskS"] = ((s // 8 == t // 8) & (s <= t)).astype(np.float32)
    c["seqm"] = (s // 8 == np.arange(16)[None, :]).astype(np.float32)
    r = np.ones(640, np.float32)
    r[0:512:64] = 0.0
    r[512:640:8] = 0.0
    c["resetm"] = np.broadcast_to(r, (128, 640)).copy()
    rc = np.zeros((4, 16), np.float32)
    for g, w in enumerate((2, 4, 8, 16)):
        rc[g] = 1.0 / np.minimum(np.arange(16) + 1, w)
    c["rc"] = np.broadcast_to(rc.reshape(1, 64), (128, 64)).copy()
    order = ["ident", "seqm", "rc", "maskP", "maskS", "resetm"]
    offs = {}
    o = 0
    for k in order:
        offs[k] = (o, c[k].shape[1])
        o += c[k].shape[1]
    return np.ascontiguousarray(np.concatenate([c[k] for k in order], axis=1)), offs


CONST_ARR, COFF = build_consts()
CW = CONST_ARR.shape[1]
CW_KEEP = COFF["maskP"][0]


class _Stop(Exception):
    pass


STOP = None


def ck(name):
    if STOP == name:
        raise _Stop()


def build_program(ntiles=NTILES):
    nc = bass.Bass("TRN2", target_bir_lowering=False)

    def din(name, shape):
        return nc.dram_tensor(name, shape, F32, kind="ExternalInput").ap()

    def dout(name, shape):
        return nc.dram_tensor(name, shape, F32, kind="ExternalOutput").ap()

    xp = din("xp", [2048, 1024])
    xs = din("xs", [128, 1024])
    ppd = din("pp", [2048, 256])
    psd = din("ps", [128, 256])
    spool = din("spool", [240, 512])
    shg = din("shg", [16, 4, 128, 128])
    wall = din("wall", [NBLK, 128, 4096])
    cst = din("cst", [128, CW])
    gvec = din("gvec", [4, 1024])
    smalld = din("small", [128, 16])
    pmixd = din("pmix", [128, 512])
    yp = dout("yp", [2048, 1024])
    ys = dout("ys", [128, 1024])
    npp = dout("npp", [15, 512])
    nhp = dout("nhp", [4, 128, 128])
    nps = dout("nps", [240, 512])
    nhs = dout("nhs", [16, 4, 128, 128])

    with ExitStack() as st:
        P = Prog(nc, st)

        def sb(name, shape, dt):
            return st.enter_context(nc.sbuf_tensor("sb_" + name, shape, dt))

        xres = sb("xres", [128, 5, 1024], F32)
        hT = sb("hT", [128, 8, 640], BF16)
        hn = [sb(f"hn{i}", [128, 1024], BF16) for i in range(2)]
        junk = sb("junk", [128, 512], BF16)
        ystage = [sb(f"ystage{i}", [128, 512], F32) for i in range(4)]
        resetmb = sb("resetmb", [128, 640], BF16)
        ring = [sb(f"ring{i}", [128, 4096], BF16) for i in range(NRING)]
        gB = sb("gB", [128, 4, 1024], F32)
        cs = sb("cs", [128, CW_KEEP], F32)
        identb = sb("identb", [128, 128], BF16)
        onesd = sb("onesd", [128, 128], BF16)
        maskPb = sb("maskPb", [128, 128], BF16)
        maskSb = sb("maskSb", [128, 128], BF16)
        pmixf = sb("pmixf", [128, 512], F32)
        pmix = sb("pmix", [128, 4, 128], BF16)
        small = sb("small", [128, 16], F32)
        lbv = sb("lbv", [128, 8], F32)
        carry = sb("carry", [128, 4, 128], F32)
        ucarry = sb("ucarry", [128, 4, 16], F32)
        ssq = sb("ssq", [128, 16], F32)
        ssq1 = sb("ssq1", [128, 8], F32)
        dmy = sb("dmy", [128, 2], F32)
        rst1 = sb("rst1", [128, 4], F32)
        rst = sb("rst", [128, 8], F32)
        identf = cs[:, COFF["ident"][0]:COFF["ident"][0] + 128]
        seqm = cs[:, COFF["seqm"][0]:COFF["seqm"][0] + 16]
        rcv = cs[:, COFF["rc"][0]:COFF["rc"][0] + 64]
        resetm = resetmb

        MIX_BYTES = 0
        scr_plan = {}

        def plan(phase, name, nelem, dt):
            nonlocal MIX_BYTES
            nbytes = nelem * (4 if dt == F32 else 2)
            nbytes = (nbytes + 31) // 32 * 32
            off = scr_plan.setdefault(("off", phase), 0)
            scr_plan[name] = (off, nelem, dt)
            scr_plan[("off", phase)] = off + nbytes

        U_NAMES = ["ug", "ue", "pa", "pb_", "sa", "sb_", "tmp16", "pooled", "npsb", "stg", "npp_sb"]
        for name, n, dt in [
            ("ug", 528, F32), ("ue", 16 * 24, F32), ("pa", 528, F32), ("pb_", 528, F32),
            ("sa", 16 * 24, F32), ("sb_", 16 * 24, F32), ("tmp16", 16, F32),
            ("pooled", 4 * 640, BF16), ("npsb", 4 * 240, F32),
            ("stg", 2 * 512, F32), ("npp_sb", 512, F32), ("pool_out", 4 * 640, BF16),
            ("qf", 640, F32), ("t1", 640, F32), ("t2", 640, F32), ("t3", 640, F32), ("t4", 640, F32),
            ("qe", 640, BF16), ("qe2", 640, BF16), ("keb", 640, BF16), ("kdb", 640, BF16), ("vb", 640, BF16), ("gbf", 640, BF16), ("gbf2", 640, BF16),
            ("kdT", 5 * 128, BF16), ("vT", 5 * 128, BF16), ("Sall", 9 * 128, F32), ("Sbf", 8 * 128, BF16),
            ("dec", 24, F32), ("dec2", 24, F32), ("Am", 5 * 128, BF16), ("osq", 640, BF16), ("o_fin", 4 * 640, BF16),
            ("S0", 16 * 128, F32), ("S0bf", 16 * 128, BF16), ("Vm", 16 * 128, BF16),
            ("merged", 8 * 640, BF16), ("s1", 640, F32), ("s2", 640, F32),
        ]:
            plan("mix", name, n, dt)
        for name, n, dt in [
            ("hidden", 22 * 640, BF16), ("ftmp", 640, F32), ("sgt0", 512, F32), ("sgt1", 512, F32),
            ("pT", 2 * 640, BF16), ("pstage", 5 * 256, F32), ("pbf", 256, BF16),
        ]:
            plan("ffn", name, n, dt)
        u_end = scr_plan["pool_out"][0]
        assert 2 * 8 * 640 * 2 <= u_end, u_end
        scr_plan["sgA"] = (0, 8 * 640, BF16)
        scr_plan["sgB"] = (8 * 640 * 2, 8 * 640, BF16)
        scr_bytes = max(scr_plan[("off", "mix")], scr_plan[("off", "ffn")])
        scr = sb("scr", [128, scr_bytes // 4], F32)

        def sv(name):
            off, n, dt = scr_plan[name]
            if dt == F32:
                return scr[:, off // 4: off // 4 + n]
            return scr[:, off // 4: off // 4 + n // 2].bitcast(BF16)

        ug = sv("ug"); ue = sv("ue").rearrange("p (i r) -> p i r", r=24)
        pa = sv("pa"); pb_ = sv("pb_")
        sa = sv("sa").rearrange("p (i r) -> p i r", r=24); sb_ = sv("sb_").rearrange("p (i r) -> p i r", r=24)
        tmp16 = sv("tmp16")
        pooled = sv("pooled").rearrange("p (g t) -> p g t", g=4)
        pool_out = sv("pool_out").rearrange("p (g t) -> p g t", g=4)
        npsb = sv("npsb").rearrange("p (g r) -> p g r", g=4)
        stg = sv("stg").rearrange("p (h c) -> p h c", h=2)
        nps_sb = stg
        npp_sb = sv("npp_sb")
        qf = sv("qf"); t1 = sv("t1"); t2 = sv("t2"); t3 = sv("t3"); t4 = sv("t4")
        qe = sv("qe"); qe2 = sv("qe2"); keb = sv("keb"); kdb = sv("kdb"); vb = sv("vb"); gbf = sv("gbf"); gbf2 = sv("gbf2")
        kdT = sv("kdT").rearrange("p (b k) -> p b k", b=5); vT = sv("vT").rearrange("p (b k) -> p b k", b=5)
        Sall = sv("Sall").rearrange("p (c v) -> p c v", c=9); Sbf = sv("Sbf").rearrange("p (c v) -> p c v", c=8)
        dec = sv("dec"); dec2 = sv("dec2"); Am = sv("Am").rearrange("p (b k) -> p b k", b=5); osq = sv("osq")
        o_fin = sv("o_fin").rearrange("p (h t) -> p h t", h=4)
        S0 = sv("S0").rearrange("p (i v) -> p i v", i=16); S0bf = sv("S0bf").rearrange("p (i v) -> p i v", i=16)
        Vm = sv("Vm").rearrange("p (i v) -> p i v", i=16)
        merged = sv("merged").rearrange("p (k t) -> p k t", k=8); s1 = sv("s1"); s2 = sv("s2")
        sgA = sv("sgA").rearrange("p (k t) -> p k t", k=8); sgB = sv("sgB").rearrange("p (k t) -> p k t", k=8)
        hidden = sv("hidden").rearrange("p (f t) -> p f t", f=22); ftmp = sv("ftmp")
        assert scr_plan["S0bf"][0] == scr_plan["S0"][0] + 8192 and scr_plan["Vm"][0] == scr_plan["S0"][0] + 12288
        assert scr_plan["S0"][0] >= scr_plan[("off", "ffn")]
        xo = scr_plan["S0"][0] // 4
        xalt = scr[:, xo:xo + 4096].rearrange("p (b d) -> p b d", b=4)
        sgt = [sv("sgt0"), sv("sgt1")]
        pT = sv("pT").rearrange("p (k t) -> p k t", k=2); pstage = sv("pstage").rearrange("p (b c) -> p b c", b=5); pbf = sv("pbf")

        pbs = [st.enter_context(nc.psum_tensor(f"pb{i}", [128, 512], F32)) for i in range(8)]
        bank_ctr = [0]

        def nb():
            b = bank_ctr[0] % 8
            bank_ctr[0] += 1
            return b

        def PB(b):
            return ("pb", b)

        wstate = {"tile": 0, "issued": set(), "released": set(), "total": NBLK * ntiles}

        def w_issue_n(n, extra_reads=()):
            if n >= wstate["total"] or n in wstate["issued"]:
                return
            slot = n % NRING
            blk = n % NBLK
            P.dma("pool", f"ring{slot}",
                  lambda e, slot=slot, blk=blk: e.dma_start(out=ring[slot][:], in_=wall[blk]),
                  reads=list(extra_reads), writes=[("ring", slot)])
            wstate["issued"].add(n)

        def w_issue(extra_reads=()):
            w_issue_n(len(wstate["issued"]), extra_reads)

        def w_get(expect):
            n = wstate["tile"] * NBLK + expect
            assert n in wstate["issued"], ("ring too small / block not prefetched", n)
            assert n - NRING < 0 or (n - NRING) in wstate["released"]
            slot = n % NRING
            return ring[slot], ("ring", slot)

        def w_release(expect):
            n = wstate["tile"] * NBLK + expect
            assert n in wstate["issued"] and n not in wstate["released"]
            wstate["released"].add(n)
            w_issue_n(n + NRING)

        def A(eng, fn, reads, writes):
            return P.op(eng, fn, reads=reads, writes=writes)

        def mm(out, lhsT, rhs, start, stop, reads, bank):
            A("pe", lambda e: e.matmul(out, lhsT=lhsT, rhs=rhs, start=start, stop=stop), reads, [PB(bank)])

        def tp(out, in_, ident, reads, bank):
            A("pe", lambda e: e.transpose(out=out, in_=in_, identity=ident), reads, [PB(bank)])

        def act(out, in_, func, reads, writes, **kw):
            A("act", lambda e: e.activation(out=out, in_=in_, func=func, **kw), reads, writes)

        def tt(out, in0, in1, op, reads, writes, eng="dve"):
            A(eng, lambda e: e.tensor_tensor(out=out, in0=in0, in1=in1, op=op), reads, writes)

        def ts(out, in0, s1_, s2_, op0, op1, reads, writes):
            A("dve", lambda e: e.tensor_scalar(out=out, in0=in0, scalar1=s1_, scalar2=s2_, op0=op0, op1=op1), reads, writes)

        def stt(out, in0, scalar, in1, op0, op1, reads, writes):
            A("dve", lambda e: e.scalar_tensor_tensor(out=out, in0=in0, scalar=scalar, in1=in1, op0=op0, op1=op1), reads, writes)

        def cp(out, in_, reads, writes, eng="dve"):
            if eng == "act":
                act(out, in_, AF.Copy, reads, writes)
            else:
                A(eng, lambda e: e.tensor_copy(out=out, in_=in_), reads, writes)

        def warm(func):
            act(dmy[:, 1:2], dmy[:, 0:1], func, ["dmy0"], ["dmy1"])

        A("dve", lambda e: e.memset(dmy[:], 1.0), [], ["dmy0", "dmy1"])
        P.dma("sp", "d_cs", lambda e: e.dma_start(out=cs[:], in_=cst[:, 0:CW_KEEP]), writes=["cs"])
        cstage = scr[:, 0:CW - CW_KEEP]
        P.dma("sp", "d_cs2", lambda e: e.dma_start(out=cstage, in_=cst[:, CW_KEEP:CW]), writes=["cstage"])
        maskPf = cstage[:, 0:128]
        maskSf = cstage[:, 128:256]
        resetmf = cstage[:, 256:896]
        P.dma("sp", "d_small", lambda e: e.dma_start(out=small[:], in_=smalld), writes=["small"])
        P.dma("sp", "d_pmix", lambda e: e.dma_start(out=pmixf[:], in_=pmixd), writes=["pmixf"])
        for gi in range(4):
            P.dma("sp", f"d_g{gi}",
                  lambda e, gi=gi: e.dma_start(out=gB[:, gi, :], in_=gvec[gi:gi + 1, :].broadcast_to([128, 1024])),
                  writes=[("gB", gi)])
        cp(identb[:], identf, ["cs"], ["identb"])
        cp(maskPb[:], maskPf, ["cstage"], ["maskPb"])
        cp(maskSb[:], maskSf, ["cstage"], ["maskSb"])
        cp(resetmb[:], resetmf, ["cstage"], ["resetmb"])
        cp(pmix[:].rearrange("p g c -> p (g c)"), pmixf[:], ["pmixf"], ["pmix"])
        A("dve", lambda e: e.memset(onesd[:], 1.0 / 128.0), [], ["onesd"])
        A("dve", lambda e: e.memset(carry[:].rearrange("p h v -> p (h v)"), 0.0), [], ["carry"])
        A("dve", lambda e: e.memset(ucarry[:].rearrange("p g t -> p (g t)"), 0.0), [], ["ucarry"])
        tt(lbv[:, 0:4], small[:, 8:12], small[:, 12:16], ALU.subtract, ["small"], ["lbv"])
        act(lbv[:, 0:4], lbv[:, 0:4], AF.Sigmoid, ["lbv"], ["lbv"])
        ts(lbv[:, 4:8], lbv[:, 0:4], -1.0, 1.0, ALU.mult, ALU.add, ["lbv"], ["lbv"])

        def run_tile(ti):
            wstate["tile"] = ti
            has_s = ti == 0
            last = ti == 3
            NT = 640 if has_s else 512
            blocks = [0, 1, 2, 3] + ([4] if has_s else [])
            subs = [(0, 512, "p")] + ([(512, 128, "s")] if has_s else [])
            hTk_p = [("hT", b) for b in range(4)]
            hTk = {"p": hTk_p, "s": [("hT", 4)]}

            def K(name):
                return [(name, "p")] + ([(name, "s")] if has_s else [])

            def xbuf(tj):
                return (xres, "xres") if tj % 2 == 0 else (xalt, "xalt")

            X, xk = xbuf(ti)

            def load_x(tj, b):
                src = xp[tj * 512 + b * 128: tj * 512 + (b + 1) * 128, :] if b < 4 else xs
                Xn, xkn = xbuf(tj)
                P.dma("sp", f"d_x{tj % 2}_{b}", lambda e, b=b, src=src, Xn=Xn: e.dma_start(out=Xn[:, b, :], in_=src),
                      writes=[(xkn, b)])

            if ti == 0:
                for b in blocks:
                    load_x(0, b)
            if ti == 0:
                for _ in range(NRING):
                    w_issue(extra_reads=[(xk, b) for b in blocks])
            if has_s:
                for half in range(2):
                    P.dma("sp", f"d_stg{half}",
                          lambda e, half=half: e.dma_start(out=stg[0:120, half, :], in_=spool[half * 120:(half + 1) * 120, :]),
                          writes=[("stg", half)])

            def half_stats(b, half, X_=None, xk_=None, ssq_=None, tag="ssq"):
                X_ = X if X_ is None else X_
                xk_ = xk if xk_ is None else xk_
                ssq_ = ssq if ssq_ is None else ssq_
                act(junk[:, 0:512], X_[:, b, half * 512:(half + 1) * 512], AF.Square, [(xk_, b)], ["junk", (tag, b, half)],
                    accum_out=ssq_[:, 2 * b + half:2 * b + half + 1])

            def norm_stats(have_partials, blks=None, X_=None, xk_=None, ssq_=None, rst_=None, tag="ssq", rtag="rst"):
                blks = blocks if blks is None else blks
                ssq_ = ssq if ssq_ is None else ssq_
                rst_ = rst if rst_ is None else rst_
                if not have_partials:
                    for b in blks:
                        for half in range(2):
                            half_stats(b, half, X_, xk_, ssq_, tag)
                nbk = len(blks)
                tt(rst_[:, 0:nbk], ssq_[:, 0:2 * nbk:2], ssq_[:, 1:2 * nbk:2], ALU.add,
                   [(tag, b, hf) for b in blks for hf in range(2)], [rtag])
                act(rst_[:, 0:nbk], rst_[:, 0:nbk], AF.Ln, [rtag], [rtag], scale=1.0 / 1024.0, bias=EPS)
                act(rst_[:, 0:nbk], rst_[:, 0:nbk], AF.Exp, [rtag], [rtag], scale=-0.5)

            def norm_apply(gi, blks=None, X_=None, xk_=None, rst_=None, rtag="rst"):
                blks = blocks if blks is None else blks
                X_ = X if X_ is None else X_
                xk_ = xk if xk_ is None else xk_
                rst_ = rst if rst_ is None else rst_
                for b in blks:
                    hb = hn[b % 2]
                    hk = f"hn{b % 2}"
                    stt(hb[:], X_[:, b, :], rst_[:, b:b + 1], gB[:, gi, :], ALU.mult, ALU.mult,
                        [(xk_, b), rtag, ("gB", gi)], [hk])
                    bank = nb()
                    bv = pbs[bank][:].bitcast(BF16)
                    for k in range(8):
                        tp(bv[:, k * 128:(k + 1) * 128], hb[:, k * 128:(k + 1) * 128], identb[:], [hk, "identb"], bank)
                    c0 = b * 128
                    cp(hT[:, :, c0:c0 + 128], bv.rearrange("p (k t) -> p k t", k=8), [PB(bank)], [("hT", b)], eng="act")

            def norm_T(gi, have_partials):
                norm_stats(have_partials)
                norm_apply(gi)

            ck("load")
            if ti == 0:
                norm_T(0, False)
            ck("norm1")

            wv, wk = w_get(0)
            wu = wv[:].rearrange("p (k c) -> p k c", k=8)
            ubanks = []
            for g in range(4):
                bp = nb()
                bs = nb() if has_s else None
                for k in range(8):
                    mm(pbs[bp][:, 0:512], wu[:, k, g * 128:(g + 1) * 128], hT[:, k, 0:512], k == 0, k == 7, [wk] + hTk_p, bp)
                    if has_s:
                        mm(pbs[bs][:, 0:128], wu[:, k, g * 128:(g + 1) * 128], hT[:, k, 512:640], k == 0, k == 7, [wk, ("hT", 4)], bs)
                ubanks.append((bp, bs))
                if g == 3:
                    w_release(0)
                w = 2 << g
                cp(ug[:, 0:16], ucarry[:, g, :], ["ucarry"], ["ug"])
                cp(ug[:, 16:528], pbs[bp][:, 0:512], [PB(bp)], ["ug"], eng="act")
                tt(pa[:, 1:528], ug[:, 1:528], ug[:, 0:527], ALU.add, ["ug"], ["pa"])
                sw = pa
                swk = "pa"
                if w >= 4:
                    tt(pb_[:, 3:528], pa[:, 3:528], pa[:, 1:526], ALU.add, ["pa"], ["pb_"])
                    sw, swk = pb_, "pb_"
                if w >= 8:
                    tt(pa[:, 7:528], pb_[:, 7:528], pb_[:, 3:524], ALU.add, ["pb_"], ["pa"])
                    sw, swk = pa, "pa"
                if w >= 16:
                    tt(pb_[:, 15:528], pa[:, 15:528], pa[:, 7:520], ALU.add, ["pa"], ["pb_"])
                    sw, swk = pb_, "pb_"
                stt(pooled[:, g, 0:512], sw[:, 16:528], 1.0 / w, ug[:, 16:528], ALU.mult, ALU.subtract,
                    [swk, "ug"], [("pooled", g, "p")])
                if ti == 0:
                    tt(tmp16[:], sw[:, 16:32], rcv[:, g * 16:(g + 1) * 16], ALU.mult, [swk, "cs"], ["tmp16"])
                    tt(pooled[:, g, 0:16], tmp16[:], ug[:, 16:32], ALU.subtract, ["tmp16", "ug"], [("pooled", g, "p")])
                cp(ucarry[:, g, :], ug[:, 512:528], ["ug"], ["ucarry"])
                if last:
                    bt = nb()
                    tp(pbs[bt][0:15, 0:128], ug[:, 513:528], identf, ["ug", "cs"], bt)
                    cp(npp_sb[0:15, g * 128:(g + 1) * 128], pbs[bt][0:15, 0:128], [PB(bt)], ["npp_sb"], eng="act")
                if has_s:
                    bt = nb()
                    for half in range(2):
                        tp(pbs[bt][:, half * 120:(half + 1) * 120], stg[0:120, half, g * 128:(g + 1) * 128],
                           identf[0:120, 0:120], [("stg", half), "cs"], bt)
                    cp(ue[:, :, 0:15], pbs[bt][:, 0:240].rearrange("p (i r) -> p i r", r=15), [PB(bt)], ["ue"], eng="act")
                    cp(ue[:, :, 15:23], pbs[bs][:, 0:128].rearrange("p (i t) -> p i t", t=8), [PB(bs)], ["ue"], eng="act")
                    tt(sa[:, :, 1:23], ue[:, :, 1:23], ue[:, :, 0:22], ALU.add, ["ue"], ["sa"])
                    ssw, sswk = sa, "sa"
                    if w >= 4:
                        tt(sb_[:, :, 3:23], sa[:, :, 3:23], sa[:, :, 1:21], ALU.add, ["sa"], ["sb_"])
                        ssw, sswk = sb_, "sb_"
                    if w >= 8:
                        tt(sa[:, :, 7:23], sb_[:, :, 7:23], sb_[:, :, 3:19], ALU.add, ["sb_"], ["sa"])
                        ssw, sswk = sa, "sa"
                    if w >= 16:
                        tt(sb_[:, :, 15:23], sa[:, :, 15:23], sa[:, :, 7:15], ALU.add, ["sa"], ["sb_"])
                        ssw, sswk = sb_, "sb_"
                    stt(pooled[:, g, 512:640].rearrange("p (i t) -> p i t", t=8), ssw[:, :, 15:23], 1.0 / w,
                        ue[:, :, 15:23], ALU.mult, ALU.subtract, [sswk, "ue"], [("pooled", g, "s")])
                    cp(npsb[:, g, :].rearrange("p (i r) -> p i r", r=15), ue[:, :, 8:23], ["ue"], [("npsb", g)])
            if last:
                P.dma("sp", "d_npp", lambda e: e.dma_start(out=npp, in_=npp_sb[0:15, :]), reads=["npp_sb"])
            if has_s:
                for half in range(2):
                    bt = nb()
                    for g in range(4):
                        tp(pbs[bt][0:120, g * 128:(g + 1) * 128], npsb[:, g, half * 120:(half + 1) * 120], identf,
                           [("npsb", g), "cs"], bt)
                    cp(nps_sb[0:120, half, :], pbs[bt][0:120, 0:512], [PB(bt)], [("stg", half)], eng="act")
                    P.dma("sp", f"d_nps{half}",
                          lambda e, half=half: e.dma_start(out=nps[half * 120:(half + 1) * 120, :], in_=nps_sb[0:120, half, :]),
                          reads=[("stg", half)])
            ck("u")
            PIPE = not has_s
            if PIPE:
                zc, rcn = [0], [0]

                def nbZ():
                    b = zc[0] % 4
                    zc[0] += 1
                    return b

                def nbR():
                    b = 6 + rcn[0] % 2
                    rcn[0] += 1
                    return b

                gcn = [0]

                def nbG():
                    b = 4 + gcn[0] % 2
                    gcn[0] += 1
                    return b
            else:
                nbZ = nbR = nbG = nb
            gb2 = [gbf, gbf2]
            QE = [qe, qe2]
            DEC = [dec, dec2]
            HS = {h: {} for h in range(4)}

            def poolmix():
                for g in range(4):
                    for (c0, n, sk) in subs:
                        bk = nbR()
                        mm(pbs[bk][:, 0:n], pmix[:, g, :], pooled[:, g, c0:c0 + n], True, True, ["pmix", ("pooled", g, sk)], bk)
                        act(pool_out[:, g, c0:c0 + n], pbs[bk][:, 0:n], AF.Copy, [PB(bk), "small"], [("pool_out", g, sk)],
                            scale=small[:, g:g + 1])

            def z_group(h, j):
                S_ = HS[h]
                if j == 0:
                    wv, wk = w_get(1 + 2 * h)
                    S_["wh"] = wv[:].rearrange("p (k j c) -> p k j c", k=8, j=4)
                    S_["wk"] = wk
                    S_["bs"] = nbZ() if has_s else None
                    S_["zb"] = []
                wh, wk, bs = S_["wh"], S_["wk"], S_["bs"]
                bp = nbZ()
                for k in range(8):
                    mm(pbs[bp][:, 0:512], wh[:, k, j, :], hT[:, k, 0:512], k == 0, k == 7, [wk] + hTk_p, bp)
                    if has_s:
                        mm(pbs[bs][:, j * 128:(j + 1) * 128], wh[:, k, j, :], hT[:, k, 512:640], k == 0, k == 7,
                           [wk, ("hT", 4)], bs)
                S_["zb"].append(bp)
                if j == 3:
                    w_release(1 + 2 * h)

            def z_evac(h):
                S_ = HS[h]
                zb, bs = S_["zb"], S_["bs"]
                gbh = gb2[h % 2]
                gbk = f"gbf{h % 2}"

                def zsrc(j, sk):
                    return (pbs[zb[j]][:, 0:512], PB(zb[j])) if sk == "p" else (pbs[bs][:, j * 128:(j + 1) * 128], PB(bs))

                for (c0, n, sk) in subs:
                    src, key = zsrc(1, sk)
                    act(t1[:, c0:c0 + n], src, AF.Sigmoid, [key], [("t1", sk)])
                    src, key = zsrc(0, sk)
                    act(qf[:, c0:c0 + n], src, AF.Sigmoid, [key], [("qf", sk)])
                    tt(qf[:, c0:c0 + n], qf[:, c0:c0 + n], src, ALU.mult, [("qf", sk), key], [("qf", sk)])
                    src, key = zsrc(3, sk)
                    act(t4[:, c0:c0 + n], src, AF.Sigmoid, [key], [("t4", sk)])
                    tt(gbh[:, c0:c0 + n], t4[:, c0:c0 + n], src, ALU.mult, [("t4", sk), key], [(gbk, sk)])
                    src, key = zsrc(2, sk)
                    cp(vb[:, c0:c0 + n], src, [key], [("vb", sk)])
                warm(AF.Ln)

            def g_mm(h, jj, tsel=(0, 1)):
                S_ = HS[h]
                if jj == 0 and tsel[0] == 0:
                    wgv_, wgk_ = w_get(2 + 2 * h)
                    S_["wgv"] = wgv_[:].rearrange("p (k t c) -> p k t c", k=8, t=4)
                    S_["wgk"] = wgk_
                    S_["gate_ev"] = []
                wgv, wgk_ = S_["wgv"], S_["wgk"]
                j = 2 * h + jj
                if tsel[0] == 0:
                    S_["bsg"] = nbG() if has_s else None
                bsg = S_["bsg"]
                for t_ in tsel:
                    bpg = nbG()
                    for k in range(8):
                        mm(pbs[bpg][:, 0:512], wgv[:, k, 2 * jj + t_, :], hT[:, k, 0:512], k == 0, k == 7, [wgk_] + hTk_p, bpg)
                        if has_s:
                            mm(pbs[bsg][:, t_ * 128:(t_ + 1) * 128], wgv[:, k, 2 * jj + t_, :], hT[:, k, 512:640], k == 0, k == 7,
                               [wgk_, ("hT", 4)], bsg)
                    S_["gate_ev"].append((j, t_, bpg, bsg))
                if jj == 1 and tsel[-1] == 1:
                    w_release(2 + 2 * h)

            def g_evac(h, jj):
                evs = HS[h]["gate_ev"][2 * jj:2 * jj + 2]
                if has_s:
                    for (j, t_, bpg, bsg) in evs:
                        dst = sgA if t_ == 0 else sgB
                        dk = "sgA" if t_ == 0 else "sgB"
                        cp(dst[:, j, 512:640], pbs[bsg][:, t_ * 128:(t_ + 1) * 128], [PB(bsg)], [(dk, j, "s")], eng="act")
                for (j, t_, bpg, bsg) in evs:
                    dst = sgA if t_ == 0 else sgB
                    dk = "sgA" if t_ == 0 else "sgB"
                    cp(dst[:, j, 0:512], pbs[bpg][:, 0:512], [PB(bpg)], [(dk, j, "p")], eng="act")

            def chain_a(h):
                ts(t1[:, 0:NT], t1[:, 0:NT], lbv[:, 4 + h:5 + h], lbv[:, h:h + 1], ALU.mult, ALU.add, K("t1") + ["lbv"], K("t1"))
                ts(t2[:, 0:NT], t1[:, 0:NT], -1.0, 1.0, ALU.mult, ALU.add, K("t1"), K("t2"))
                act(t1[:, 0:NT], t1[:, 0:NT], AF.Ln, K("t1"), K("t1"))

            def cb_scan(h):
                A("dve", lambda e: e.tensor_tensor_scan(out=t3[:, 0:NT], data0=resetm[:, 0:NT], data1=t1[:, 0:NT],
                                                        initial=0.0, op0=ALU.mult, op1=ALU.add),
                  K("t1") + ["resetmb"], K("t3"))

            def cb_exp(h):
                act(t1[:, 0:NT], t3[:, 0:NT], AF.Exp, K("t3"), K("t1"))
                act(t4[:, 0:NT], t3[:, 0:NT], AF.Exp, K("t3"), K("t4"), scale=-1.0)

            def cb_mul(h):
                qe_ = QE[h % 2]
                qk = f"qe{h % 2}"
                tt(qe_[:, 0:NT], qf[:, 0:NT], t1[:, 0:NT], ALU.mult, K("qf") + K("t1"), [(qk, "p")] + ([(qk, "s")] if has_s else []))
                tt(t2[:, 0:NT], t2[:, 0:NT], t4[:, 0:NT], ALU.mult, K("t2") + K("t4"), K("t2"))

            def cb_keb(h):
                cp(keb[:, 0:NT], t2[:, 0:NT], K("t2"), K("keb"), eng="act")

            def cb_dec(h):
                dec_ = DEC[h % 2]
                dk_ = f"dec{h % 2}"
                cp(dec_[:, 0:8], t1[:, 63:512:64], [("t1", "p")], [(dk_, "p")])
                tt(kdb[:, 0:512].rearrange("p (c t) -> p c t", t=64), t2[:, 0:512].rearrange("p (c t) -> p c t", t=64),
                   dec_[:, 0:8].unsqueeze(2).broadcast_to([128, 8, 64]), ALU.mult, [("t2", "p"), (dk_, "p")], [("kdb", "p")])
                if has_s:
                    cp(dec_[:, 8:24], t1[:, 519:640:8], [("t1", "s")], [(dk_, "s")])
                    tt(kdb[:, 512:640].rearrange("p (c t) -> p c t", t=8), t2[:, 512:640].rearrange("p (c t) -> p c t", t=8),
                       dec_[:, 8:24].unsqueeze(2).broadcast_to([128, 16, 8]), ALU.mult, [("t2", "s"), (dk_, "s")], [("kdb", "s")])

            def chain_b(h):
                cb_scan(h); cb_exp(h); cb_mul(h); cb_keb(h); cb_dec(h)

            def R1(h):
                qe = QE[h % 2]
                qk = f"qe{h % 2}"
                for (src_, srck, dst, dstk) in ((kdb, "kdb", kdT, "kdT"), (vb, "vb", vT, "vT")):
                    bt = nbR()
                    bv = pbs[bt][:].bitcast(BF16)
                    for b in blocks:
                        sk = "p" if b < 4 else "s"
                        tp(bv[:, b * 128:(b + 1) * 128], src_[:, b * 128:(b + 1) * 128], identb[:], [(srck, sk), "identb"], bt)
                    nbk = len(blocks)
                    cp(dst[:, 0:nbk, :], bv[:, 0:nbk * 128].rearrange("p (b k) -> p b k", k=128), [PB(bt)], [dstk], eng="act")
                ba = nbR()
                for b in range(4):
                    mm(pbs[ba][:, b * 128:(b + 1) * 128], keb[:, b * 128:(b + 1) * 128], qe[:, b * 128:(b + 1) * 128],
                       True, True, [("keb", "p"), (qk, "p")], ba)
                tt(Am[:, 0:4, :], pbs[ba][:, 0:512].rearrange("p (b t) -> p b t", b=4),
                   maskPb[:].unsqueeze(1).broadcast_to([128, 4, 128]), ALU.mult, [PB(ba), "maskPb"], [("Am", "p")])
                if has_s:
                    bas = nbR()
                    mm(pbs[bas][:, 0:128], keb[:, 512:640], qe[:, 512:640], True, True, [("keb", "s"), (qk, "s")], bas)
                    tt(Am[:, 4, :], pbs[bas][:, 0:128], maskSb[:], ALU.mult, [PB(bas), "maskSb"], [("Am", "s")])

            def build_vm():
                tt(Vm[:, :, :], vT[:, 4, :].unsqueeze(1).broadcast_to([128, 16, 128]),
                   seqm.unsqueeze(2).broadcast_to([128, 16, 128]), ALU.mult, ["vT", "cs"], ["Vm"])

            def sample_prefetch(h):
                P.dma("sp", "d_S0", lambda e, h=h: e.dma_start(out=S0[:, :, :], in_=shg[:, h, :, :].rearrange("i k v -> k i v")),
                      writes=["S0"])

            def sample_bf(h):
                cp(S0bf[:, :, :].rearrange("p i v -> p (i v)"), S0[:, :, :].rearrange("p i v -> p (i v)"), ["S0"], ["S0bf"], eng="act")

            def sample_state_update(h):
                dec = DEC[h % 2]
                dk_ = f"dec{h % 2}"
                tt(S0[:, :, :], S0[:, :, :], dec[:, 8:24].unsqueeze(2).broadcast_to([128, 16, 128]), ALU.mult,
                   ["S0", (dk_, "s")], ["S0"])
                for q4 in range(4):
                    bd = nbR()
                    mm(pbs[bd][:, 0:512], kdT[:, 4, :], Vm[:, 4 * q4:4 * q4 + 4, :].rearrange("p i v -> p (i v)"), True, True,
                       ["kdT", "Vm"], bd)
                    tt(S0[:, 4 * q4:4 * q4 + 4, :].rearrange("p i v -> p (i v)"),
                       S0[:, 4 * q4:4 * q4 + 4, :].rearrange("p i v -> p (i v)"), pbs[bd][:, 0:512], ALU.add,
                       ["S0", PB(bd)], ["S0"])
                P.dma("sp", "d_nhs", lambda e, h=h: e.dma_start(out=nhs[:, h, :, :].rearrange("i k v -> k i v"), in_=S0[:, :, :]),
                      reads=["S0"])

            def R2_mm(h):
                dsb = [nbR(), nbR()]
                HS[h]["dsb"] = dsb
                for c in range(8):
                    blk, half = c // 2, c % 2
                    bk = dsb[half]
                    mm(pbs[bk][:, blk * 128:(blk + 1) * 128], kdT[half * 64:(half + 1) * 64, blk, :],
                       vT[half * 64:(half + 1) * 64, blk, :], True, True, ["kdT", "vT"], bk)
                cp(Sall[:, 0, :], carry[:, h, :], ["carry"], [("Sall", 0)])

            def R2_steps(h, c0, c1):
                dec_ = DEC[h % 2]
                dk_ = f"dec{h % 2}"
                dsb = HS[h]["dsb"]
                for c in range(c0, c1):
                    blk, half = c // 2, c % 2
                    bk = dsb[half]
                    stt(Sall[:, c + 1, :], Sall[:, c, :], dec_[:, c:c + 1], pbs[bk][:, blk * 128:(blk + 1) * 128],
                        ALU.mult, ALU.add, [("Sall", c), (dk_, "p"), PB(bk)], [("Sall", c + 1)])

            def R2_half(h):
                cp(Sbf[:, 0:4, :].rearrange("p c v -> p (c v)"), Sall[:, 0:4, :].rearrange("p c v -> p (c v)"),
                   [("Sall", c) for c in range(4)], [("Sbf", 0)], eng="act")

            def R2_fin(h):
                cp(Sbf[:, 4:8, :].rearrange("p c v -> p (c v)"), Sall[:, 4:8, :].rearrange("p c v -> p (c v)"),
                   [("Sall", c) for c in range(4, 8)], [("Sbf", 1)], eng="act")
                cp(carry[:, h, :], Sall[:, 8, :], [("Sall", 8)], ["carry"])
                if last:
                    P.dma("sp", f"d_nhp{h}", lambda e, h=h: e.dma_start(out=nhp[h], in_=Sall[:, 8, :]), reads=[("Sall", 8)])

            def R2(h):
                R2_mm(h); R2_steps(h, 0, 4); R2_half(h); R2_steps(h, 4, 8); R2_fin(h)

            def R3(h):
                qe = QE[h % 2]
                qk = f"qe{h % 2}"
                S_ = HS[h]
                bo = nbR()
                for b in range(4):
                    mm(pbs[bo][:, b * 128:(b + 1) * 128], vT[:, b, :], Am[:, b, :], True, False, ["vT", ("Am", "p")], bo)
                    for half in range(2):
                        c = 2 * b + half
                        mm(pbs[bo][:, c * 64:(c + 1) * 64], Sbf[:, c, :], qe[:, c * 64:(c + 1) * 64], False, half == 1,
                           [("Sbf", c // 4), (qk, "p")], bo)
                bos = None
                if has_s:
                    bos = nbR()
                    mm(pbs[bos][:, 0:128], vT[:, 4, :], Am[:, 4, :], True, False, ["vT", ("Am", "s")], bos)
                    for i in range(16):
                        mm(pbs[bos][:, i * 8:(i + 1) * 8], S0bf[:, i, :], qe[:, 512 + i * 8:512 + (i + 1) * 8], False, i == 15,
                           ["S0bf", (qk, "s")], bos)
                S_["obanks"] = [(0, 512, "p", bo)] + ([(512, 128, "s", bos)] if has_s else [])
                for (c0, n, sk, bk) in S_["obanks"]:
                    act(osq[:, c0:c0 + n], pbs[bk][:, 0:n], AF.Square, [PB(bk)], [("osq", sk)])

            def R4(h):
                gbh = gb2[h % 2]
                gbk = f"gbf{h % 2}"
                for (c0, n, sk, bk) in HS[h]["obanks"]:
                    bn = nbR()
                    mm(pbs[bn][:, 0:n], onesd[:], osq[:, c0:c0 + n], True, True, ["onesd", ("osq", sk)], bn)
                    act(s1[:, c0:c0 + n], pbs[bn][:, 0:n], AF.Ln, [PB(bn)], [("s1", sk)], bias=EPS)
                    act(s1[:, c0:c0 + n], s1[:, c0:c0 + n], AF.Exp, [("s1", sk)], [("s1", sk)], scale=-0.5)
                    stt(s2[:, c0:c0 + n], pbs[bk][:, 0:n], small[:, 4 + h:5 + h], s1[:, c0:c0 + n], ALU.mult, ALU.mult,
                        [PB(bk), "small", ("s1", sk)], [("s2", sk)])
                tt(o_fin[:, h, 0:NT], s2[:, 0:NT], gbh[:, 0:NT], ALU.mult, K("s2") + [(gbk, "p"), (gbk, "s")],
                   [("o_fin", h, s_) for s_ in ("p", "s")])

            if PIPE:
                for j in range(4):
                    z_group(0, j)
            poolmix()
            ck("poolmix")
            P.barrier()
            ck("bar")
            if PIPE:
                for h in range(4):
                    p = h - 1
                    if h > 0:
                        R1(p)
                    z_evac(h)
                    g_mm(h, 0)
                    if h > 0:
                        R2_mm(p)
                    chain_a(h)
                    if h > 0:
                        R2_steps(p, 0, 4)
                        R2_half(p)
                    g_evac(h, 0)
                    if h < 3:
                        for j in range(4):
                            z_group(h + 1, j)
                    cb_scan(h)
                    if h > 0:
                        R2_steps(p, 4, 7)
                    cb_exp(h)
                    cb_mul(h)
                    cb_dec(h)
                    cb_keb(h)
                    if h < 3:
                        if h > 0:
                            R2_steps(p, 7, 8)
                            R2_fin(p)
                            R3(p)
                        g_mm(h, 1, (0,))
                        if h > 0:
                            R4(p)
                        g_mm(h, 1, (1,))
                        g_evac(h, 1)
                    else:
                        g_mm(h, 1)
                        R2_steps(p, 7, 8)
                        R2_fin(p)
                        R3(p)
                        g_evac(h, 1)
                        R4(p)
                    if h < 3:
                        warm(AF.Sigmoid)
                R1(3); R2(3); R3(3); R4(3)
                warm(AF.Sigmoid)
            else:
                for h in range(4):
                    sample_prefetch(h)
                    for j in range(4):
                        z_group(h, j)
                    z_evac(h)
                    g_mm(h, 0)
                    g_mm(h, 1)
                    chain_a(h)
                    g_evac(h, 0)
                    chain_b(h)
                    R1(h)
                    g_evac(h, 1)
                    sample_bf(h)
                    R2_mm(h)
                    R2_steps(h, 0, 4)
                    R2_half(h)
                    R2_steps(h, 4, 8)
                    build_vm()
                    R2_fin(h)
                    R3(h); R4(h)
                    warm(AF.Sigmoid)
                    sample_state_update(h)

            ck("hgrn")
            for jh in range(2):
                wy, wyk = w_get(9 + jh)
                wyv = wy[:].rearrange("p (k t c) -> p k t c", k=4, t=2)
                for jj in range(4):
                    j = jh * 4 + jj
                    bs = nb() if has_s else None
                    b_ya, b_yb = nb(), nb()
                    for t_, bk, srcb, srck in ((0, b_ya, pool_out, "pool_out"), (1, b_yb, o_fin, "o_fin")):
                        for kk in range(4):
                            for (c0, n, sk) in subs:
                                o = pbs[bk][:, 0:512] if sk == "p" else pbs[bs][:, t_ * 128:(t_ + 1) * 128]
                                mm(o, wyv[:, kk, t_, jj * 128:(jj + 1) * 128], srcb[:, kk, c0:c0 + n], kk == 0, kk == 3,
                                   [wyk, (srck, kk, sk)], bk if sk == "p" else bs)
                    if jj == 3:
                        w_release(9 + jh)
                    m1, m1k = (s1, "s1") if j % 2 == 0 else (t3, "t3")
                    m2, m2k = (s2, "s2") if j % 2 == 0 else (t4, "t4")
                    for (c0, n, sk) in subs:
                        oa = pbs[b_ya][:, 0:512] if sk == "p" else pbs[bs][:, 0:128]
                        ob = pbs[b_yb][:, 0:512] if sk == "p" else pbs[bs][:, 128:256]
                        ka = PB(b_ya) if sk == "p" else PB(bs)
                        kb_ = PB(b_yb) if sk == "p" else PB(bs)
                        act(m1[:, c0:c0 + n], sgA[:, j, c0:c0 + n], AF.Sigmoid, [("sgA", j, sk)], [(m1k, sk)])
                        act(m2[:, c0:c0 + n], sgB[:, j, c0:c0 + n], AF.Sigmoid, [("sgB", j, sk)], [(m2k, sk)])
                        tt(m1[:, c0:c0 + n], m1[:, c0:c0 + n], oa, ALU.mult, [(m1k, sk), ka], [(m1k, sk)])
                        tt(m2[:, c0:c0 + n], m2[:, c0:c0 + n], ob, ALU.mult, [(m2k, sk), kb_], [(m2k, sk)])
                    tt(merged[:, j, 0:NT], m1[:, 0:NT], m2[:, 0:NT], ALU.add, K(m1k) + K(m2k),
                       [("merged", j, s_) for s_ in ("p", "s")])

            ck("merge")
            warm(AF.Ln)
            for half in range(2):
                wo, wok = w_get(11 + half)
                wov = wo[:].rearrange("p (k c) -> p k c", k=8)
                bks = {b: nb() for b in blocks}
                for k in range(8):
                    for b in blocks:
                        sk = "p" if b < 4 else "s"
                        mm(pbs[bks[b]][:, 0:512], merged[:, k, b * 128:(b + 1) * 128], wov[:, k, :], k == 0, k == 7,
                           [wok, ("merged", k, sk)], bks[b])
                w_release(11 + half)
                for b in blocks:
                    tt(X[:, b, half * 512:(half + 1) * 512], X[:, b, half * 512:(half + 1) * 512], pbs[bks[b]][:, 0:512],
                       ALU.add, [(xk, b), PB(bks[b])], [(xk, b)])
                    half_stats(b, half)

            ck("wout")
            norm_T(1, True)
            warm(AF.Sigmoid)
            P.barrier()
            if ti + 1 < ntiles:
                for b in range(4):
                    load_x(ti + 1, b)
            for b in blocks:
                src = ppd[ti * 512 + b * 128: ti * 512 + (b + 1) * 128, :] if b < 4 else psd
                P.dma("sp", f"d_p{b}", lambda e, src=src, b=b: e.dma_start(out=pstage[:, b, :], in_=src), writes=[("pstage", b)])
            for jb in range(11):
                wf, wfk = w_get(13 + jb)
                wfv = wf[:].rearrange("p (k j c) -> p k j c", k=8, j=2)
                for cc in range(2):
                    f = 2 * jb + cc
                    bs = nb() if has_s else None
                    b_g, b_u = nb(), nb()
                    if f == 0:
                        for hc in range(2):
                            for jx, bk in ((0, b_g), (1, b_u)):
                                for k in range(8):
                                    mm(pbs[bk][:, hc * 256:(hc + 1) * 256], wfv[:, k, jx, cc * 128:(cc + 1) * 128],
                                       hT[:, k, hc * 256:(hc + 1) * 256], k == 0, k == 7,
                                       [wfk, ("hT", 2 * hc), ("hT", 2 * hc + 1)], bk)
                        if has_s:
                            for jx, bk in ((0, b_g), (1, b_u)):
                                for k in range(8):
                                    mm(pbs[bs][:, jx * 128:(jx + 1) * 128], wfv[:, k, jx, cc * 128:(cc + 1) * 128], hT[:, k, 512:640],
                                       k == 0, k == 7, [wfk] + hTk["s"], bs)
                    else:
                        for jx, bk in ((0, b_g), (1, b_u)):
                            for k in range(8):
                                for (c0, n, sk) in subs:
                                    o = pbs[bk][:, 0:512] if sk == "p" else pbs[bs][:, jx * 128:(jx + 1) * 128]
                                    mm(o, wfv[:, k, jx, cc * 128:(cc + 1) * 128], hT[:, k, c0:c0 + n], k == 0, k == 7,
                                       [wfk] + hTk[sk], bk if sk == "p" else bs)
                    if cc == 1:
                        w_release(13 + jb)
                    for (c0, n, sk) in subs:
                        og = pbs[b_g][:, 0:512] if sk == "p" else pbs[bs][:, 0:128]
                        ou = pbs[b_u][:, 0:512] if sk == "p" else pbs[bs][:, 128:256]
                        kg = PB(b_g) if sk == "p" else PB(bs)
                        ku = PB(b_u) if sk == "p" else PB(bs)
                        act(ftmp[:, c0:c0 + n], og, AF.Sigmoid, [kg], [("ftmp", sk)])
                        tt(ftmp[:, c0:c0 + n], ftmp[:, c0:c0 + n], og, ALU.mult, [("ftmp", sk), kg], [("ftmp", sk)])
                        tt(hidden[:, f, c0:c0 + n], ftmp[:, c0:c0 + n], ou, ALU.mult, [("ftmp", sk), ku], [("hidden", f, sk)])
            for b in blocks:
                cp(pbf[:], pstage[:, b, :], [("pstage", b)], ["pbf"], eng="act")
                bt = nb()
                bv = pbs[bt][:].bitcast(BF16)
                for k in range(2):
                    tp(bv[:, k * 128:(k + 1) * 128], pbf[:, k * 128:(k + 1) * 128], identb[:], ["pbf", "identb"], bt)
                cp(pT[:, :, b * 128:(b + 1) * 128], bv[:, 0:256].rearrange("p (k t) -> p k t", k=2), [PB(bt)], [("pT", b)])
            warm(AF.Ln)
            for half in range(2):
                bks = {b: nb() for b in blocks}
                for kb in range(3):
                    wd, wdk = w_get(24 + half * 3 + kb)
                    wdv = wd[:].rearrange("p (k c) -> p k c", k=8)
                    nk = 8 if kb < 2 else 6
                    for kk in range(nk):
                        f = kb * 8 + kk
                        for b in blocks:
                            sk = "p" if b < 4 else "s"
                            mm(pbs[bks[b]][:, 0:512], hidden[:, f, b * 128:(b + 1) * 128], wdv[:, kk, :], f == 0, f == 21,
                               [wdk, ("hidden", f, sk)], bks[b])
                    w_release(24 + half * 3 + kb)
                for b in blocks:
                    tt(X[:, b, half * 512:(half + 1) * 512], X[:, b, half * 512:(half + 1) * 512], pbs[bks[b]][:, 0:512],
                       ALU.add, [(xk, b), PB(bks[b])], [(xk, b)])
                    half_stats(b, half)

            ck("ffn")
            norm_T(2, True)
            if ti + 1 < ntiles:
                Xn, xkn = xbuf(ti + 1)
                norm_stats(False, blks=[0, 1, 2, 3], X_=Xn, xk_=xkn, ssq_=ssq1, rst_=rst1, tag="ssq1", rtag="rst1")
            warm(AF.Sigmoid)
            wg0, wg0k = w_get(30)
            wpp_, wppk = w_get(31)
            wg1, wg1k = w_get(32)
            wppv = wpp_[:, 0:2048].rearrange("p (k c) -> p k c", k=2)
            for half in range(2):
                wg_, wgk = (wg0, wg0k) if half == 0 else (wg1, wg1k)
                wgv = wg_[:].rearrange("p (k c) -> p k c", k=8)
                bks = {b: nb() for b in blocks}
                for k in range(8):
                    for b in blocks:
                        mm(pbs[bks[b]][:, 0:512], hT[:, k, b * 128:(b + 1) * 128], wgv[:, k, :], k == 0, k == 7,
                           [wgk, ("hT", b)], bks[b])
                if half == 0:
                    w_release(30)
                sgl = [(sgt[0][:], ["sgt0"]), (sgt[1][:], ["sgt1"]), (ftmp[:, 0:512], [("ftmp", "p")]),
                       (pstage[:, 0:2, :].rearrange("p b c -> p (b c)"), [("pstage", 0), ("pstage", 1)])]
                for b in blocks[:4]:
                    sg, sgk = sgl[b % 4]
                    act(sg, pbs[bks[b]][:, 0:512], AF.Sigmoid, [PB(bks[b])], sgk)
                for b in blocks:
                    sg, sgk = sgl[b % 4]
                    if b >= 4:
                        act(sg, pbs[bks[b]][:, 0:512], AF.Sigmoid, [PB(bks[b])], sgk)
                    be = nb()
                    for k in range(2):
                        mm(pbs[be][:, 0:512], pT[:, k, b * 128:(b + 1) * 128], wppv[:, k, half * 512:(half + 1) * 512], k == 0, k == 1,
                           [wppk, ("pT", b)], be)
                    tt(sg, sg, pbs[be][:, 0:512], ALU.mult, sgk + [PB(be)], sgk)
                    tt(X[:, b, half * 512:(half + 1) * 512], X[:, b, half * 512:(half + 1) * 512], sg, ALU.add,
                       [(xk, b)] + sgk, [(xk, b)])
                for b in blocks:
                    half_stats(b, half)
                if half == 1:
                    w_release(31); w_release(32)

            ck("ple")
            if ti + 1 < ntiles:
                Xn, xkn = xbuf(ti + 1)
                norm_apply(0, blks=[0, 1, 2, 3], X_=Xn, xk_=xkn, rst_=rst1, rtag="rst1")
            P.barrier()
            norm_stats(True)
            for b in blocks:
                dst = yp[ti * 512 + b * 128: ti * 512 + (b + 1) * 128, :] if b < 4 else ys
                for half in range(2):
                    q_ = (2 * b + half) % 4
                    yst = ystage[q_]
                    ysk = f"ystage{q_}"
                    hs_ = slice(half * 512, (half + 1) * 512)
                    stt(yst[:], X[:, b, hs_], rst[:, b:b + 1], gB[:, 3, hs_], ALU.mult, ALU.mult,
                        [(xk, b), "rst", ("gB", 3)], [ysk])
                    P.dma("sp", f"d_y{q_}", lambda e, yst=yst, dst=dst, hs_=hs_: e.dma_start(out=dst[:, hs_], in_=yst[:]),
                          reads=[ysk])

        P.barrier()
        try:
            ck("setup")
            for ti in range(ntiles):
                run_tile(ti)
        except _Stop:
            pass
        P.finish("sp")
        P.emit()
    return nc


_PROG = {}


def _prep_inputs(inp):
    f = lambda a: np.ascontiguousarray(np.asarray(a, dtype=np.float32))
    w_in = f(inp["w_in"][0])
    wallv = build_wall(w_in, f(inp["w_pool_up"][0]), f(inp["w_hgrn_up"][0]), f(inp["w_out"][0]),
                       f(inp["w_ffn_gate"][0]), f(inp["w_ffn_up"][0]), f(inp["w_ffn_down"][0]),
                       f(inp["w_ple_gate"][0]), f(inp["w_ple_proj"][0]))
    gvec = np.ascontiguousarray(np.stack([f(inp["g_mix"][0]), f(inp["g_ffn"][0]), f(inp["g_ple"][0]), f(inp["g_final"])], 0))
    small = np.zeros((128, 16), np.float32)
    small[:, 0:4] = f(inp["pool_scale"][0]).reshape(4, 128).T
    small[:, 4:8] = f(inp["hgrn_norm"][0]).reshape(4, 128).T
    small[:, 8:12] = f(inp["hgrn_lb"][0]).reshape(4, 128).T
    small[:, 12:16] = f(inp["hgrn_lb"][1]).reshape(4, 128).T
    pmix = np.ascontiguousarray(f(inp["w_pool_mix"][0]).transpose(1, 0, 2)).reshape(128, 512)
    xp = f(inp["x_prompt"]); xsm = f(inp["x_sample"])
    ppr = f(inp["p_prompt"][0]); psm = f(inp["p_sample"][0])
    spl = f(inp["state_pool"][0]); shg = f(inp["state_hgrn"][0])
    maps = []
    for c in range(NCORES):
        maps.append({
            "xp": xp[c], "xs": xsm[16 * c:16 * c + 16].reshape(128, 1024),
            "pp": ppr[c], "ps": psm[16 * c:16 * c + 16].reshape(128, 256),
            "spool": spl[16 * c:16 * c + 16].reshape(240, 512),
            "shg": shg[16 * c:16 * c + 16],
            "wall": wallv, "cst": CONST_ARR, "gvec": gvec, "small": small, "pmix": pmix,
        })
    return maps


def kernel(**inputs):
    if "nc" not in _PROG:
        _PROG["nc"] = build_program()
    nc = _PROG["nc"]
    maps = _prep_inputs(inputs)
    res = run_bass_kernel_spmd(nc, maps, core_ids=list(range(NCORES)))
    R = res.results
    y_p = np.stack([R[c]["yp"] for c in range(NCORES)], 0).astype(np.float32)
    y_s = np.concatenate([R[c]["ys"].reshape(16, 8, 1024) for c in range(NCORES)], 0).astype(np.float32)
    npp = np.stack([R[c]["npp"] for c in range(NCORES)], 0)[None].astype(np.float32)
    nhp = np.stack([R[c]["nhp"] for c in range(NCORES)], 0)[None].astype(np.float32)
    nps = np.concatenate([R[c]["nps"].reshape(16, 15, 512) for c in range(NCORES)], 0)[None].astype(np.float32)
    nhs = np.concatenate([R[c]["nhs"] for c in range(NCORES)], 0)[None].astype(np.float32)
    return (y_p, y_s, npp, nhp, nps, nhs)
```
